# Optimizing a Trainium2 kernel written in Bass

```python
import math
import jax, jax.numpy as jnp
from jax import lax
import numpy as np

D_MODEL = 1024
BATCH = 2
SEQ = 8192
DEPTH = 2

D_MIX = D_MODEL
HGRN_WIDTH = D_MIX // 4
HGRN_HEAD_DIM = 64
HGRN_HEADS = HGRN_WIDTH // HGRN_HEAD_DIM
HGRN_CHUNK = 16
LB_FLOOR = 1e-30
S5_WIDTH = D_MIX // 4
S5_GROUP = 16
S5_GROUPS = S5_WIDTH // S5_GROUP
S5_STATE = 64
S5_DT_MIN = 1e-3
S5_DT_MAX = 1e-1
ATTN_WIDTH = D_MIX - HGRN_WIDTH - S5_WIDTH
DIFF_HEAD_DIM = 64
DIFF_V_DIM = 2 * DIFF_HEAD_DIM
DIFF_HEADS = ATTN_WIDTH // DIFF_V_DIM
ROPE_DIM = DIFF_HEAD_DIM // 4
ROPE_THETA = 500000.0
Q_BLOCK = 128
MASK_VALUE = -1e30
EPS = 1e-6
D_IN = 4 * HGRN_WIDTH + 2 * S5_WIDTH + 4 * ATTN_WIDTH

kernel_name = "hymba_style_hgrn2_s5_diffattn"


def _in_splits():
    sizes = [HGRN_WIDTH] * 4 + [S5_WIDTH] * 2 + [ATTN_WIDTH] * 4
    return [int(v) for v in np.cumsum(sizes)[:-1]]


def rmsnorm(x, w):
    xf = x.astype(jnp.float32)
    return xf * lax.rsqrt(jnp.mean(xf * xf, axis=-1, keepdims=True) + EPS) * w.astype(jnp.float32)


def hgrn2_mixer(q, f_raw, v, lb, g_norm_w):
    b, s, _ = q.shape
    n = s // HGRN_CHUNK
    q = jax.nn.silu(q)
    log_f = jnp.logaddexp(jnp.log(jnp.maximum(lb, LB_FLOOR)), jnp.log1p(-lb) + jax.nn.log_sigmoid(f_raw))
    k = -jnp.expm1(log_f)

    def to_chunks(t):
        return t.reshape(b, n, HGRN_CHUNK, HGRN_HEADS, HGRN_HEAD_DIM).transpose(0, 3, 1, 2, 4)

    qc, kc, vc, lfc = to_chunks(q), to_chunks(k), to_chunks(v), to_chunks(log_f)
    cum = jnp.cumsum(lfc, axis=3)
    causal = jnp.tril(jnp.ones((HGRN_CHUNK, HGRN_CHUNK), dtype=bool))[:, :, None]
    rel = cum[..., :, None, :] - cum[..., None, :, :]
    decay = jnp.where(causal, jnp.exp(jnp.where(causal, rel, 0.0)), 0.0)
    scores = jnp.einsum('bhntd,bhnsd,bhntsd->bhnts', qc, kc, decay)
    o_intra = jnp.einsum('bhnts,bhnsv->bhntv', scores, vc)
    last = cum[..., -1:, :]
    chunk_upd = jnp.einsum('bhnsd,bhnsv->bhndv', kc * jnp.exp(last - cum), vc)
    chunk_decay = jnp.exp(last[..., 0, :])

    def step(state, inp):
        dec, upd = inp
        return dec[..., None] * state + upd, state

    init = jnp.zeros((b, HGRN_HEADS, HGRN_HEAD_DIM, HGRN_HEAD_DIM), jnp.float32)
    _, prev = lax.scan(step, init, (jnp.moveaxis(chunk_decay, 2, 0), jnp.moveaxis(chunk_upd, 2, 0)))
    prev = jnp.moveaxis(prev, 0, 2)
    o_inter = jnp.einsum('bhntd,bhndv->bhntv', qc * jnp.exp(cum), prev)
    o = (o_intra + o_inter).transpose(0, 2, 3, 1, 4).reshape(b, s, HGRN_HEADS, HGRN_HEAD_DIM)
    o = rmsnorm(o, g_norm_w)
    return o.reshape(b, s, HGRN_WIDTH)


def s5_mixer(u, a_re, a_im, b_re, b_im, c_re, c_im, log_dt, d_skip, glu_w, glu_b):
    bsz, s, _ = u.shape
    ug = u.reshape(bsz, s, S5_GROUPS, S5_GROUP)
    dt = jnp.exp(log_dt)[:, None]
    mag = jnp.exp(dt * a_re)
    ab_re = mag * jnp.cos(dt * a_im)
    ab_im = mag * jnp.sin(dt * a_im)
    den = a_re * a_re + a_im * a_im
    num_re = ab_re - 1.0
    num_im = ab_im
    z_re = (num_re * a_re + num_im * a_im) / den
    z_im = (num_im * a_re - num_re * a_im) / den
    bb_re = z_re[..., None] * b_re - z_im[..., None] * b_im
    bb_im = z_re[..., None] * b_im + z_im[..., None] * b_re
    bu_re = jnp.einsum('gpc,bsgc->bsgp', bb_re, ug)
    bu_im = jnp.einsum('gpc,bsgc->bsgp', bb_im, ug)
    at_re = jnp.broadcast_to(ab_re, bu_re.shape)
    at_im = jnp.broadcast_to(ab_im, bu_im.shape)

    def combine(left, right):
        a1r, a1i, b1r, b1i = left
        a2r, a2i, b2r, b2i = right
        return (a1r * a2r - a1i * a2i,
                a1r * a2i + a1i * a2r,
                a2r * b1r - a2i * b1i + b2r,
                a2r * b1i + a2i * b1r + b2i)

    _, _, x_re, x_im = lax.associative_scan(combine, (at_re, at_im, bu_re, bu_im), axis=1)
    y = jnp.einsum('gcp,bsgp->bsgc', c_re, x_re) - jnp.einsum('gcp,bsgp->bsgc', c_im, x_im)
    y = y.reshape(bsz, s, S5_WIDTH) + d_skip * u
    y = jax.nn.gelu(y)
    return y * jax.nn.sigmoid(y @ glu_w + glu_b)


def partial_rope(t, cos, sin):
    half = ROPE_DIM // 2
    r1 = t[..., :half]
    r2 = t[..., half:ROPE_DIM]
    rot = jnp.concatenate([r1 * cos - r2 * sin, r2 * cos + r1 * sin], axis=-1)
    return jnp.concatenate([rot, t[..., ROPE_DIM:]], axis=-1)


def diff_attention(q, k, v, lam, lambda_init, subln_w):
    b, s, _ = q.shape
    pos = jnp.arange(s, dtype=jnp.float32)
    inv_freq = ROPE_THETA ** (-jnp.arange(0, ROPE_DIM, 2, dtype=jnp.float32) / ROPE_DIM)
    ang = pos[:, None] * inv_freq[None, :]
    cos, sin = jnp.cos(ang), jnp.sin(ang)
    q = q.reshape(b, s, DIFF_HEADS, 2, DIFF_HEAD_DIM).transpose(0, 2, 3, 1, 4)
    k = k.reshape(b, s, DIFF_HEADS, 2, DIFF_HEAD_DIM).transpose(0, 2, 3, 1, 4)
    v = v.reshape(b, s, DIFF_HEADS, DIFF_V_DIM).transpose(0, 2, 1, 3)
    q = partial_rope(q, cos, sin) * (DIFF_HEAD_DIM ** -0.5)
    k = partial_rope(k, cos, sin)
    outs = []
    for blk in range(s // Q_BLOCK):
        q0 = blk * Q_BLOCK
        kv_len = q0 + Q_BLOCK
        qb = q[:, :, :, q0:kv_len]
        sc = jnp.einsum('bhmqd,bhmkd->bhmqk', qb, k[:, :, :, :kv_len])
        mask = (q0 + jnp.arange(Q_BLOCK))[:, None] >= jnp.arange(kv_len)[None, :]
        p = jax.nn.softmax(jnp.where(mask, sc, MASK_VALUE), axis=-1)
        w = p[:, :, 0] - lam * p[:, :, 1]
        outs.append(jnp.einsum('bhqk,bhkv->bhqv', w, v[:, :, :kv_len]))
    o = jnp.concatenate(outs, axis=2)
    o = rmsnorm(o, subln_w) * (1.0 - lambda_init)
    return o.transpose(0, 2, 1, 3).reshape(b, s, ATTN_WIDTH)


def setup_inputs(seed: int = 0) -> dict:
    key = jax.random.key(seed)
    ks = jax.random.split(key, 22)
    nrm = jax.random.normal
    f32 = jnp.float32
    return {
        "x": nrm(ks[0], (BATCH, SEQ, D_MODEL), f32),
        "norm_w": 1.0 + 0.02 * nrm(ks[1], (DEPTH, D_MODEL), f32),
        "w_in": nrm(ks[2], (DEPTH, D_MODEL, D_IN), f32) * D_MODEL ** -0.5,
        "w_out": nrm(ks[3], (DEPTH, D_MIX, D_MODEL), f32) * D_MIX ** -0.5,
        "hgrn_lb_logits": nrm(ks[4], (DEPTH, HGRN_WIDTH), f32),
        "hgrn_norm_w": 1.0 + 0.02 * nrm(ks[5], (DEPTH, HGRN_HEAD_DIM), f32),
        "s5_a_re": -0.5 + 0.01 * nrm(ks[6], (DEPTH, S5_GROUPS, S5_STATE), f32),
        "s5_a_im": math.pi * jnp.arange(S5_STATE, dtype=f32) + 0.01 * nrm(ks[7], (DEPTH, S5_GROUPS, S5_STATE), f32),
        "s5_b_re": nrm(ks[8], (DEPTH, S5_GROUPS, S5_STATE, S5_GROUP), f32) * (2 * S5_GROUP) ** -0.5,
        "s5_b_im": nrm(ks[9], (DEPTH, S5_GROUPS, S5_STATE, S5_GROUP), f32) * (2 * S5_GROUP) ** -0.5,
        "s5_c_re": nrm(ks[10], (DEPTH, S5_GROUPS, S5_GROUP, S5_STATE), f32) * S5_STATE ** -0.5,
        "s5_c_im": nrm(ks[11], (DEPTH, S5_GROUPS, S5_GROUP, S5_STATE), f32) * S5_STATE ** -0.5,
        "s5_log_dt": jax.random.uniform(ks[12], (DEPTH, S5_GROUPS), f32, math.log(S5_DT_MIN), math.log(S5_DT_MAX)),
        "s5_d": nrm(ks[13], (DEPTH, S5_WIDTH), f32),
        "s5_glu_w": nrm(ks[14], (DEPTH, S5_WIDTH, S5_WIDTH), f32) * S5_WIDTH ** -0.5,
        "s5_glu_b": 0.01 * nrm(ks[15], (DEPTH, S5_WIDTH), f32),
        "diff_lq1": 0.1 * nrm(ks[16], (DEPTH, DIFF_HEAD_DIM), f32),
        "diff_lk1": 0.1 * nrm(ks[17], (DEPTH, DIFF_HEAD_DIM), f32),
        "diff_lq2": 0.1 * nrm(ks[18], (DEPTH, DIFF_HEAD_DIM), f32),
        "diff_lk2": 0.1 * nrm(ks[19], (DEPTH, DIFF_HEAD_DIM), f32),
        "diff_subln_w": 1.0 + 0.02 * nrm(ks[20], (DEPTH, DIFF_V_DIM), f32),
        "final_norm_w": 1.0 + 0.02 * nrm(ks[21], (D_MODEL,), f32),
    }


def reference(x, norm_w, w_in, w_out, hgrn_lb_logits, hgrn_norm_w, s5_a_re, s5_a_im, s5_b_re, s5_b_im,
              s5_c_re, s5_c_im, s5_log_dt, s5_d, s5_glu_w, s5_glu_b, diff_lq1, diff_lk1, diff_lq2, diff_lk2,
              diff_subln_w, final_norm_w):
    f32 = jnp.float32
    in_dtype = x.dtype
    h_res = x.astype(f32)
    lb_p = jax.nn.softmax(hgrn_lb_logits.astype(f32), axis=0)
    lb_all = jnp.cumsum(lb_p, axis=0) - lb_p[0:1]
    splits = _in_splits()
    for l in range(DEPTH):
        h = rmsnorm(h_res, norm_w[l])
        proj = h @ w_in[l].astype(f32)
        hq, hf, hi, hg, su, sg, aq, ak, av, ag = jnp.split(proj, splits, axis=-1)
        o_h = hgrn2_mixer(hq, hf, hi, lb_all[l], hgrn_norm_w[l]) * jax.nn.silu(hg)
        o_s = s5_mixer(su, s5_a_re[l].astype(f32), s5_a_im[l].astype(f32), s5_b_re[l].astype(f32),
                       s5_b_im[l].astype(f32), s5_c_re[l].astype(f32), s5_c_im[l].astype(f32),
                       s5_log_dt[l].astype(f32), s5_d[l].astype(f32), s5_glu_w[l].astype(f32),
                       s5_glu_b[l].astype(f32)) * jax.nn.silu(sg)
        lambda_init = 0.8 - 0.6 * math.exp(-0.3 * l)
        lam = (jnp.exp(jnp.sum(diff_lq1[l].astype(f32) * diff_lk1[l].astype(f32)))
               - jnp.exp(jnp.sum(diff_lq2[l].astype(f32) * diff_lk2[l].astype(f32))) + lambda_init)
        o_a = diff_attention(aq, ak, av, lam, lambda_init, diff_subln_w[l]) * jax.nn.silu(ag)
        mix = jnp.concatenate([o_h, o_s, o_a], axis=-1)
        h_res = h_res + mix @ w_out[l].astype(f32)
    return rmsnorm(h_res, final_norm_w).astype(in_dtype)
```

```python
import contextlib
import math
import numpy as np
import ml_dtypes
import concourse.bass as bass
import concourse.mybir as mybir
from concourse.bass_utils import run_bass_kernel_spmd

F32 = mybir.dt.float32
BF16 = mybir.dt.bfloat16
I32 = mybir.dt.int32
AF = mybir.ActivationFunctionType
ALU = mybir.AluOpType
AX = mybir.AxisListType

D_MODEL = 1024
SEQ = 8192
BATCH = 2
DEPTH = 2
EPS = 1e-6
NCORES = 8
TQ = SEQ // 4
ROPE_THETA = 500000.0
import os
DBG = set(os.environ.get("KDBG", "").split(","))


class Sched:
    ENGS = ["pe", "act", "dve", "pool", "sp"]

    def __init__(self, nc):
        self.nc = nc
        self.ops = []
        self.last_w = {}
        self.readers = {}
        self.cnt = {e: 0 for e in self.ENGS}
        self.dma_cnt = {}
        self.base = set()

    def op(self, eng, fn, r=(), w=(), dma=False):
        deps = set(self.base)
        for x in r:
            if x in self.last_w:
                deps.add(self.last_w[x])
        for x in w:
            if x in self.last_w:
                deps.add(self.last_w[x])
            for d in self.readers.get(x, ()):
                deps.add(d)
        if dma:
            q = self.dma_cnt.get(eng, 0)
            self.dma_cnt[eng] = q + 1
            tok = ("dma", eng, q)
        else:
            self.cnt[eng] += 1
            tok = ("eng", eng, self.cnt[eng])
        self.ops.append((eng, fn, deps, tok))
        for x in w:
            self.last_w[x] = tok
            self.readers[x] = []
        for x in r:
            self.readers.setdefault(x, []).append(tok)
        return tok

    def barrier(self):
        b = set()
        for e in self.ENGS:
            if self.cnt[e] > 0:
                b.add(("eng", e, self.cnt[e]))
        for e, n in self.dma_cnt.items():
            for q in range(max(0, n - self.NSLOT), n):
                b.add(("dma", e, q))
        self.base = b
        self.last_w = {}
        self.readers = {}

    NSLOT = 8

    def emit(self):
        nc = self.nc
        NSLOT = self.NSLOT
        with contextlib.ExitStack() as st:
            esem = {e: st.enter_context(nc.semaphore("s_" + e)) for e in self.ENGS}
            dsem = {}
            for e in self.dma_cnt:
                dsem[e] = [st.enter_context(nc.semaphore(f"d_{e}_{i}")) for i in range(NSLOT)]
            block = st.enter_context(nc.Block())
            per = {e: [o for o in self.ops if o[0] == e] for e in self.ENGS}

            def mk(ename):
                def body(eng):
                    seen = {}

                    def wait(tok):
                        if tok[0] == "eng":
                            _, e2, n = tok
                            key = ("eng", e2)
                            if seen.get(key, 0) >= n:
                                return
                            seen[key] = n
                            eng.wait_ge(esem[e2], n)
                        else:
                            _, e2, q = tok
                            slot = q % NSLOT
                            val = 16 * (q // NSLOT + 1)
                            key = ("dma", e2, slot)
                            if seen.get(key, 0) >= val:
                                return
                            seen[key] = val
                            eng.wait_ge(dsem[e2][slot], val)
                    for (_, fn, deps, tok) in per[ename]:
                        for d in sorted(deps):
                            wait(d)
                        if tok[0] == "dma":
                            q = tok[2]
                            if q >= NSLOT:
                                wait(("dma", ename, q - NSLOT))
                            ins = fn(eng)
                            ins.then_inc(dsem[ename][q % NSLOT], 16)
                        else:
                            ins = fn(eng)
                            ins.then_inc(esem[ename], 1)
                    n = self.dma_cnt.get(ename, 0)
                    for q in range(max(0, n - NSLOT), n):
                        wait(("dma", ename, q))
                return body
            block.tensor(mk("pe"))
            block.scalar(mk("act"))
            block.vector(mk("dve"))
            block.gpsimd(mk("pool"))
            block.sync(mk("sp"))


NFM = 704
NTM = 256
FM_CH = [(0, 128), (128, 128), (256, 64), (320, 128), (448, 128), (576, 128)]


def phase_inproj(nc, S, st, hT, wcat, nw, fm, tm_sf, tm_v, T=SEQ):
    TS = lambda n, s, d: st.enter_context(nc.sbuf_tensor(n, s, d))
    PS = lambda n: st.enter_context(nc.psum_tensor(n, [128, 512], F32))
    nw_sb = TS("ip_nw", [128, 8], F32)
    wst = [TS(f"ip_wst{i}", [128, NFM + NTM], F32) for i in range(2)]
    wall = TS("ip_wall", [128, 8, NFM + NTM], BF16)
    ones = TS("ip_ones", [128, 128], BF16)
    xin = [TS(f"ip_xin{i}", [128, 8, 512], F32) for i in range(2)]
    xsq = TS("ip_xsq", [128, 8, 512], BF16)
    sq = TS("ip_sq", [128, 512], F32)
    rstd = TS("ip_rstd", [128, 512], F32)
    xn = [TS(f"ip_xn{i}", [128, 8, 512], BF16) for i in range(2)]
    fmo = [TS(f"ip_fmo{i}", [128, 512], BF16) for i in range(3)]
    tsf = [TS(f"ip_tsf{i}", [128, 4, 64], F32) for i in range(2)]
    tv = [TS(f"ip_tv{i}", [128, 4, 192], BF16) for i in range(2)]
    ps_ss = PS("ip_ps_ss")
    ps_fm = [PS(f"ip_ps_fm{i}") for i in range(2)]
    ps_tm = [PS(f"ip_ps_tm{i}") for i in range(2)]

    S.op("sp", lambda e: e.dma_start(out=nw_sb[:], in_=nw[:, :]), w=["nw"], dma=True)
    S.op("pool", lambda e: e.memset(ones[:], 1.0), w=["ones"])
    for k in range(8):
        S.op("sp", lambda e, k=k: e.dma_start(out=wst[k % 2][:], in_=wcat[k * 128:(k + 1) * 128, :]),
             w=[("wst", k % 2)], dma=True)
        S.op("dve", lambda e, k=k: e.tensor_scalar(out=wall[:, k, :], in0=wst[k % 2][:], scalar1=nw_sb[:, k:k + 1],
                                                  scalar2=None, op0=ALU.mult),
             r=[("wst", k % 2), "nw"], w=[("wall", k)])
    wall_r = [("wall", k) for k in range(8)]
    hT_v = hT.rearrange("(k p) t -> p k t", p=128)
    fmi = 0
    for ti in range(T // 512):
        b = ti % 2
        t0 = ti * 512
        S.op("sp", lambda e, b=b, t0=t0: e.dma_start(out=xin[b][:, 0:4, :], in_=hT_v[:, 0:4, t0:t0 + 512]),
             w=[("xin", b, 0)], dma=True)
        S.op("sp" if "NOPOOLDMA" in DBG else "pool", lambda e, b=b, t0=t0: e.dma_start(out=xin[b][:, 4:8, :], in_=hT_v[:, 4:8, t0:t0 + 512]),
             w=[("xin", b, 1)], dma=True)
        xr = [("xin", b, 0), ("xin", b, 1)]
        S.op("act", lambda e, b=b: e.activation(out=xsq[:], in_=xin[b][:], func=AF.Square), r=xr, w=["xsq"])
        for k in range(8):
            S.op("pe", lambda e, k=k: e.matmul(ps_ss[:], ones[:], xsq[:, k, :], start=(k == 0), stop=(k == 7)),
                 r=["xsq", "ones"], w=["ps_ss"])
        S.op("act", lambda e: e.activation(out=sq[:], in_=ps_ss[:], func=AF.Sqrt, scale=1.0 / D_MODEL, bias=EPS),
             r=["ps_ss"], w=["sq"])
        S.op("dve", lambda e: e.reciprocal(out=rstd[:], in_=sq[:]), r=["sq"], w=["rstd"])
        S.op("dve", lambda e, b=b: e.tensor_tensor(out=xn[b][:, 0:4, :], in0=xin[b][:, 0:4, :],
                                                  in1=rstd[:].unsqueeze(1).broadcast_to([128, 4, 512]), op=ALU.mult),
             r=[("xin", b, 0), "rstd"], w=[("xn", b, 0)])
        S.op("dve" if "NOPOOLTT" in DBG else "pool", lambda e, b=b: e.tensor_tensor(out=xn[b][:, 4:8, :], in0=xin[b][:, 4:8, :],
                                                   in1=rstd[:].unsqueeze(1).broadcast_to([128, 4, 512]), op=ALU.mult),
             r=[("xin", b, 1), "rstd"], w=[("xn", b, 1)])
        xnr = [("xn", b, 0), ("xn", b, 1)]
        for ci, (c0, cw) in enumerate(FM_CH):
            if "NOFM" in DBG or ("FM%d" % ci) in DBG:
                continue
            pb = ci % 2
            for k in range(8):
                S.op("pe", lambda e, k=k, c0=c0, cw=cw, pb=pb, b=b: e.matmul(
                    ps_fm[pb][0:cw, :], wall[:, k, c0:c0 + cw], xn[b][:, k, :], start=(k == 0), stop=(k == 7)),
                    r=xnr + wall_r, w=[("ps_fm", pb)])
            fb = fmi % 3
            fmi += 1
            if ci == 0:
                S.op("act", lambda e, pb=pb, fb=fb: e.activation(out=fmo[fb][0:64, :], in_=ps_fm[pb][0:64, :], func=AF.Silu),
                     r=[("ps_fm", pb)], w=[("fmo", fb, 0)])
                S.op("act", lambda e, pb=pb, fb=fb: e.activation(out=fmo[fb][64:128, :], in_=ps_fm[pb][64:128, :],
                                                               func=AF.Sigmoid, scale=-1.0),
                     r=[("ps_fm", pb)], w=[("fmo", fb, 1)])
                wl = [("fmo", fb, 0), ("fmo", fb, 1)]
            elif ci in (1, 5):
                S.op("act", lambda e, pb=pb, fb=fb: e.activation(out=fmo[fb][:], in_=ps_fm[pb][:], func=AF.Silu),
                     r=[("ps_fm", pb)], w=[("fmo", fb, 0), ("fmo", fb, 1)])
                wl = [("fmo", fb, 0), ("fmo", fb, 1)]
            else:
                S.op("dve", lambda e, pb=pb, fb=fb, cw=cw: e.tensor_copy(out=fmo[fb][0:cw, :], in_=ps_fm[pb][0:cw, :]),
                     r=[("ps_fm", pb)], w=[("fmo", fb, 0), ("fmo", fb, 1)])
                wl = [("fmo", fb, 0), ("fmo", fb, 1)]
            S.op("sp", lambda e, fb=fb, c0=c0, cw=cw, t0=t0: e.dma_start(out=fm[c0:c0 + cw, t0:t0 + 512], in_=fmo[fb][0:cw, :]),
                 r=wl, dma=True)
        for pb in range(0 if "NOTM" in DBG else 2):
            for t4 in (2 * pb, 2 * pb + 1):
                off = (t4 % 2) * 256
                for k in range(8):
                    S.op("pe", lambda e, k=k, t4=t4, pb=pb, off=off, b=b: e.matmul(
                        ps_tm[pb][:, off:off + 256], xn[b][:, k, t4 * 128:(t4 + 1) * 128], wall[:, k, NFM:NFM + NTM],
                        start=(k == 0), stop=(k == 7)),
                        r=xnr + wall_r, w=[("ps_tm", pb)])
            if "TMNOEVAC" in DBG:
                continue
            for t4 in (2 * pb, 2 * pb + 1):
                off = (t4 % 2) * 256
                if "TMNOACT" not in DBG:
                  S.op("act", lambda e, pb=pb, b=b, t4=t4, off=off: e.activation(
                    out=tsf[b][:, t4, :], in_=ps_tm[pb][:, off:off + 64], func=AF.Sigmoid),
                    w=[("ps_tm", pb), ("tsf", b, t4)])
                if "TMNODVE" not in DBG:
                  S.op("dve", lambda e, pb=pb, b=b, t4=t4, off=off: e.tensor_copy(
                    out=tv[b][:, t4, :], in_=ps_tm[pb][:, off + 64:off + 256]),
                    w=[("ps_tm", pb), ("tv", b, t4)])
        if "TMNODMA" in DBG:
            continue
        S.op("sp", lambda e, b=b, t0=t0: e.dma_start(
            out=tm_sf[t0:t0 + 512, :].rearrange("(a p) c -> p a c", p=128), in_=tsf[b][:]),
            r=[("tsf", b, t4) for t4 in range(4)], dma=True)
        S.op("sp", lambda e, b=b, t0=t0: e.dma_start(
            out=tm_v[t0:t0 + 512, :].rearrange("(a p) c -> p a c", p=128), in_=tv[b][:]),
            r=[("tv", b, t4) for t4 in range(4)], dma=True)


def core_cols(j):
    r = lambda s, n: list(range(s, s + n))
    fmc = (r(0 + 64 * j, 64) + r(256 + 64 * j, 64) + r(768 + 64 * j, 64) + r(1280 + 64 * j, 64) + r(1024 + 64 * j, 64)
           + r(1536 + 128 * j, 128) + r(2048 + 128 * j, 128) + r(3072 + 128 * j, 128))
    tmc = r(256 + 64 * j, 64) + r(512 + 64 * j, 64) + r(2560 + 128 * j, 128)
    return np.array(fmc + tmc)


def build_inproj(T=SEQ):
    nc = bass.Bass("TRN2", target_bir_lowering=False)
    hT = nc.dram_tensor("hT", [D_MODEL, T], F32, kind="ExternalInput").ap()
    wcat = nc.dram_tensor("wcat", [D_MODEL, NFM + NTM], F32, kind="ExternalInput").ap()
    nw = nc.dram_tensor("nw", [128, 8], F32, kind="ExternalInput").ap()
    fm = nc.dram_tensor("fm", [NFM, T], BF16, kind="ExternalOutput").ap()
    tm_sf = nc.dram_tensor("tm_sf", [T, 64], F32, kind="ExternalOutput").ap()
    tm_v = nc.dram_tensor("tm_v", [T, 192], BF16, kind="ExternalOutput").ap()
    with contextlib.ExitStack() as st:
        S = Sched(nc)
        phase_inproj(nc, S, st, hT, wcat, nw, fm, tm_sf, tm_v, T)
        S.emit()
    return nc


C1_2PI = 6.28125
C2_2PI = 2.0 * math.pi - 6.28125


def rope_tables(nc, S, st, ropef, sinT, cosT, T, tag):
    TS = lambda n, s, d: st.enter_context(nc.sbuf_tensor(n, s, d))
    CH = min(2048, T)
    pi_ = TS(tag + "_pi", [128, CH], I32)
    ang = TS(tag + "_ang", [128, CH], F32)
    kf = TS(tag + "_kf", [128, CH], F32)
    ki = TS(tag + "_ki", [128, CH], I32)
    hs = TS(tag + "_hs", [128, CH], F32)
    for c in range(T // CH):
        S.op("pool", lambda e, c=c: e.iota(pi_[:], pattern=[[1, CH]], base=c * CH, channel_multiplier=0), w=[tag + "pi"])
        S.op("dve", lambda e: e.tensor_copy(out=ang[:], in_=pi_[:]), r=[tag + "pi"], w=[tag + "ang"])
        S.op("dve", lambda e: e.tensor_scalar(out=ang[:], in0=ang[:], scalar1=ropef[:, 0:1], scalar2=None, op0=ALU.mult),
             r=["ropef"], w=[tag + "ang"])
        S.op("dve", lambda e: e.tensor_scalar(out=kf[:], in0=ang[:], scalar1=1.0 / (2.0 * math.pi), scalar2=None, op0=ALU.mult),
             r=[tag + "ang"], w=[tag + "kf"])
        S.op("dve", lambda e: e.tensor_copy(out=ki[:], in_=kf[:]), r=[tag + "kf"], w=[tag + "ki"])
        S.op("dve", lambda e: e.tensor_copy(out=kf[:], in_=ki[:]), r=[tag + "ki"], w=[tag + "kf"])
        S.op("dve", lambda e: e.scalar_tensor_tensor(out=ang[:], in0=kf[:], scalar=-C1_2PI, in1=ang[:], op0=ALU.mult, op1=ALU.add),
             r=[tag + "kf"], w=[tag + "ang"])
        S.op("dve", lambda e: e.scalar_tensor_tensor(out=ang[:], in0=kf[:], scalar=-C2_2PI, in1=ang[:], op0=ALU.mult, op1=ALU.add),
             r=[tag + "kf"], w=[tag + "ang"])
        S.op("dve", lambda e: e.tensor_scalar(out=ang[:], in0=ang[:], scalar1=math.pi, scalar2=-math.pi, op0=ALU.min, op1=ALU.max),
             w=[tag + "ang"])
        S.op("act", lambda e, c=c: e.activation(out=sinT[:, c * CH:(c + 1) * CH], in_=ang[:], func=AF.Sin),
             r=[tag + "ang"], w=[(tag + "sin", c)])
        S.op("act", lambda e: e.activation(out=hs[:], in_=ang[:], func=AF.Sin, scale=0.5), r=[tag + "ang"], w=[tag + "hs"])
        S.op("pool", lambda e: e.tensor_tensor(out=hs[:], in0=hs[:], in1=hs[:], op=ALU.mult), w=[tag + "hs"])
        S.op("pool", lambda e, c=c: e.tensor_scalar(out=cosT[:, c * CH:(c + 1) * CH], in0=hs[:], scalar1=-2.0, scalar2=1.0,
                                                   op0=ALU.mult, op1=ALU.add), r=[tag + "hs"], w=[(tag + "cos", c)])
    return [(tag + "sin", c) for c in range(T // CH)] + [(tag + "cos", c) for c in range(T // CH)]


def phase_attn(nc, S, st, fm, tm_v, lqk, subln, ropef_d, rmat_d, cmask_d, oa, lambda_init, T=SEQ):
    TS = lambda n, s, d: st.enter_context(nc.sbuf_tensor(n, s, d))
    PS = lambda n: st.enter_context(nc.psum_tensor(n, [128, 512], F32))
    NQ = T // 512
    NK = T // 128
    ropef = TS("at_ropef", [128, 1], F32)
    rm32 = TS("at_rm32", [128, 128], F32)
    rm = TS("at_rm", [128, 128], BF16)
    cmask = TS("at_cmask", [128, 4, 512], BF16)
    ones = TS("at_ones", [128, 128], BF16)
    sinT = TS("at_sin", [128, T], BF16)
    cosT = TS("at_cos", [128, T], BF16)
    qraw = TS("at_qraw", [128, T], BF16)
    kraw = TS("at_kraw", [128, T], BF16)
    qr = TS("at_qr", [128, T], BF16)
    kr = TS("at_kr", [128, T], BF16)
    sag = TS("at_sag", [128, T], BF16)
    vsb = TS("at_v", [128, NK, 128], BF16)
    lq = TS("at_lq", [128, 256], F32)
    lp = TS("at_lp", [128, 128], F32)
    le = TS("at_le", [128, 2], F32)
    neglam = TS("at_neglam", [128, 1], F32)
    sw = TS("at_sw", [128, 1], F32)
    t1 = TS("at_t1", [128, 512], F32)
    t2 = TS("at_t2", [128, 512], F32)
    P = [[TS(f"at_P{i}_{m}", [128, 512], BF16) for m in range(2)] for i in range(3)]
    r0 = TS("at_r0", [128, 512], F32)
    r1 = TS("at_r1", [128, 512], F32)
    o0 = TS("at_o0", [128, 512], F32)
    o1 = TS("at_o1", [128, 512], F32)
    osq = TS("at_osq", [128, 512], BF16)
    nsq = TS("at_nsq", [128, 512], F32)
    ob = [TS(f"at_ob{i}", [128, 512], BF16) for i in range(2)]
    ps_s = [[PS(f"at_ps_s{i}_{m}") for m in range(2)] for i in range(2)]
    ps_o = [PS(f"at_ps_o{m}") for m in range(2)]
    ps_l = [PS(f"at_ps_l{m}") for m in range(2)]

    S.op("sp", lambda e: e.dma_start(out=ropef[:], in_=ropef_d[:, :]), w=["ropef"], dma=True)
    S.op("sp", lambda e: e.dma_start(out=rm32[:], in_=rmat_d[:, :]), w=["rm32"], dma=True)
    S.op("sp", lambda e: e.dma_start(out=cmask[:], in_=cmask_d.rearrange("d p q -> p d q")), w=["cmask"], dma=True)
    S.op("sp", lambda e: e.dma_start(out=lq[:], in_=lqk.partition_broadcast(128)), w=["lq"], dma=True)
    S.op("sp", lambda e: e.dma_start(out=sw[:], in_=subln[:, :]), w=["sw"], dma=True)
    S.op("sp", lambda e: e.dma_start(out=qraw[:], in_=fm[320:448, :]), w=["qraw"], dma=True)
    S.op("sp", lambda e: e.dma_start(out=kraw[:], in_=fm[448:576, :]), w=["kraw"], dma=True)
    S.op("sp", lambda e: e.dma_start(out=sag[:], in_=fm[576:704, :]), w=["sag"], dma=True)
    S.op("pool", lambda e: e.dma_start(out=vsb[:], in_=tm_v[:, 64:192].rearrange("(a p) c -> p a c", p=128)), w=["vsb"], dma=True)
    S.op("pool", lambda e: e.memset(ones[:], 1.0), w=["ones"])
    S.op("dve", lambda e: e.tensor_copy(out=rm[:], in_=rm32[:]), r=["rm32"], w=["rm"])
    S.op("dve", lambda e: e.tensor_tensor(out=lp[:], in0=lq[:, 0:128], in1=lq[:, 128:256], op=ALU.mult), r=["lq"], w=["lp"])
    S.op("dve", lambda e: e.tensor_reduce(out=le[:], in_=lp[:].rearrange("p (a c) -> p a c", a=2), axis=AX.X, op=ALU.add),
         r=["lp"], w=["le"])
    S.op("act", lambda e: e.activation(out=le[:], in_=le[:], func=AF.Exp), w=["le"])
    S.op("dve", lambda e: e.tensor_tensor(out=neglam[:], in0=le[:, 1:2], in1=le[:, 0:1], op=ALU.subtract), r=["le"], w=["neglam"])
    S.op("dve", lambda e: e.tensor_scalar(out=neglam[:], in0=neglam[:], scalar1=-lambda_init, scalar2=None, op0=ALU.add), w=["neglam"])
    S.op("dve", lambda e: e.tensor_scalar(out=sw[:], in0=sw[:], scalar1=1.0 - lambda_init, scalar2=None, op0=ALU.mult), w=["sw"])
    tabs = rope_tables(nc, S, st, ropef, sinT, cosT, T, "at_rp")
    for src, dst, sn, dn in ((qraw, qr, "qraw", "qr"), (kraw, kr, "kraw", "kr")):
        for ti in range(NQ):
            sl = slice(ti * 512, (ti + 1) * 512)
            S.op("pe", lambda e, src=src, sl=sl: e.matmul(ps_s[0][0][:], rm[:], src[:, sl], start=True, stop=True),
                 r=[sn, "rm"], w=["ps_s00"])
            S.op("dve", lambda e, src=src, sl=sl: e.tensor_tensor(out=t1[:], in0=src[:, sl], in1=cosT[:, sl], op=ALU.mult),
                 r=[sn] + tabs, w=["t1"])
            S.op("dve", lambda e, sl=sl: e.tensor_tensor(out=t2[:], in0=ps_s[0][0][:], in1=sinT[:, sl], op=ALU.mult),
                 r=tabs, w=["t2", "ps_s00"])
            S.op("pool", lambda e, dst=dst, sl=sl: e.tensor_tensor(out=dst[:, sl], in0=t1[:], in1=t2[:], op=ALU.add),
                 r=["t1", "t2"], w=[(dn, ti)])
    qr_all = [("qr", ti) for ti in range(NQ)]
    kr_all = [("kr", ti) for ti in range(NQ)]
    psn = lambda i, m: f"ps_s{i}{m}"
    for qi in range(NQ):
        qs = slice(qi * 512, (qi + 1) * 512)
        nk = 4 * (qi + 1)

        def QK(n):
            i = n % 2
            for m in range(2):
                S.op("pe", lambda e, n=n, i=i, m=m, qs=qs: e.matmul(ps_s[i][m][:], kr[64 * m:64 * m + 64, n * 128:(n + 1) * 128],
                                                         qr[64 * m:64 * m + 64, qs], start=True, stop=True),
                     r=[("qr", qi), ("kr", n // 4)], w=[psn(i, m)])

        def EXP(n):
            i = n % 2
            j = n % 3
            for m in range(2):
                S.op("act", lambda e, i=i, j=j, m=m: e.activation(out=P[j][m][:], in_=ps_s[i][m][:], func=AF.Exp, scale=0.125),
                     w=[psn(i, m), ("P", j, m)])
                d = n - 4 * qi
                if d >= 0:
                    S.op("dve", lambda e, j=j, m=m, d=d: e.tensor_tensor(out=P[j][m][:], in0=P[j][m][:], in1=cmask[:, d, :], op=ALU.mult),
                         r=["cmask"], w=[("P", j, m)])

        def PV(n):
            j = n % 3
            for m in range(2):
                S.op("pe", lambda e, n=n, j=j, m=m, nk=nk: e.matmul(ps_o[m][:], vsb[:, n, :], P[j][m][:], start=(n == 0), stop=(n == nk - 1)),
                     r=[("P", j, m), "vsb"], w=[f"ps_o{m}"])
                S.op("pe", lambda e, n=n, j=j, m=m, nk=nk: e.matmul(ps_l[m][:], ones[:], P[j][m][:], start=(n == 0), stop=(n == nk - 1)),
                     r=[("P", j, m), "ones"], w=[f"ps_l{m}"])
        QK(0)
        for n in range(nk):
            if n + 1 < nk:
                QK(n + 1)
            EXP(n)
            PV(n)
        S.op("dve", lambda e: e.reciprocal(out=r0[:], in_=ps_l[0][:]), w=["r0", "ps_l0"])
        S.op("dve", lambda e: e.reciprocal(out=r1[:], in_=ps_l[1][:]), w=["r1", "ps_l1"])
        S.op("dve", lambda e: e.tensor_tensor(out=o0[:], in0=ps_o[0][:], in1=r0[:], op=ALU.mult), r=["r0"], w=["o0", "ps_o0"])
        S.op("dve", lambda e: e.tensor_tensor(out=o1[:], in0=ps_o[1][:], in1=r1[:], op=ALU.mult), r=["r1"], w=["o1", "ps_o1"])
        S.op("dve", lambda e: e.scalar_tensor_tensor(out=o0[:], in0=o1[:], scalar=neglam[:, 0:1], in1=o0[:], op0=ALU.mult, op1=ALU.add),
             r=["o1", "neglam"], w=["o0"])
        S.op("act", lambda e: e.activation(out=osq[:], in_=o0[:], func=AF.Square), r=["o0"], w=["osq"])
        S.op("pe", lambda e: e.matmul(ps_s[0][0][:], ones[:], osq[:], start=True, stop=True), r=["osq", "ones"], w=[psn(0, 0)])
        S.op("act", lambda e: e.activation(out=nsq[:], in_=ps_s[0][0][:], func=AF.Sqrt, scale=1.0 / 128.0, bias=EPS),
             w=["nsq", psn(0, 0)])
        S.op("dve", lambda e: e.reciprocal(out=nsq[:], in_=nsq[:]), w=["nsq"])
        S.op("dve", lambda e: e.scalar_tensor_tensor(out=o0[:], in0=o0[:], scalar=sw[:, 0:1], in1=nsq[:], op0=ALU.mult, op1=ALU.mult),
             r=["nsq", "sw"], w=["o0"])
        S.op("pool", lambda e, qi=qi, qs=qs: e.tensor_tensor(out=ob[qi % 2][:], in0=o0[:], in1=sag[:, qs], op=ALU.mult),
             r=["o0", "sag"], w=[("ob", qi % 2)])
        S.op("sp", lambda e, qi=qi, qs=qs: e.dma_start(out=oa[:, qs], in_=ob[qi % 2][:]), r=[("ob", qi % 2)], dma=True)


def attn_consts():
    ropef = np.zeros((128, 1), np.float32)
    inv = (ROPE_THETA ** (-np.arange(0, 16, 2, dtype=np.float32) / 16.0)).astype(np.float32)
    rmat = np.zeros((128, 128), np.float32)
    for base in (0, 64):
        for i in range(8):
            ropef[base + i, 0] = -inv[i]
            ropef[base + 8 + i, 0] = inv[i]
            rmat[base + 8 + i, base + i] = 1.0
            rmat[base + i, base + 8 + i] = 1.0
    k = np.arange(128)[:, None]
    q = np.arange(512)[None, :]
    cmask = np.stack([(128 * d + k <= q) for d in range(4)]).astype(ml_dtypes.bfloat16)
    return ropef, rmat, cmask


def build_attn(lambda_init, T=SEQ):
    nc = bass.Bass("TRN2", target_bir_lowering=False)
    fm = nc.dram_tensor("fm", [NFM, T], BF16, kind="ExternalInput").ap()
    tm_v = nc.dram_tensor("tm_v", [T, 192], BF16, kind="ExternalInput").ap()
    lqk = nc.dram_tensor("lqk", [1, 256], F32, kind="ExternalInput").ap()
    subln = nc.dram_tensor("subln", [128, 1], F32, kind="ExternalInput").ap()
    ropef = nc.dram_tensor("ropef", [128, 1], F32, kind="ExternalInput").ap()
    rmat = nc.dram_tensor("rmat", [128, 128], F32, kind="ExternalInput").ap()
    cmask = nc.dram_tensor("cmask", [4, 128, 512], BF16, kind="ExternalInput").ap()
    oa = nc.dram_tensor("oa", [128, T], BF16, kind="ExternalOutput").ap()
    with contextlib.ExitStack() as st:
        S = Sched(nc)
        phase_attn(nc, S, st, fm, tm_v, lqk, subln, ropef, rmat, cmask, oa, lambda_init, T)
        S.emit()
    return nc


def hgrn_consts():
    s = np.arange(128)[:, None]
    t = np.arange(128)[None, :]
    same = (s // 16) == (t // 16)
    m_incl = (same & (s <= t)).astype(np.float32)
    m_rev = (same & (s > t)).astype(np.float32)
    m_tot8 = ((s // 16) == np.arange(8)[None, :]).astype(np.float32)
    mcat = np.concatenate([m_incl, m_tot8], axis=1)
    return mcat, m_rev


def phase_hgrn(nc, S, st, fm, tm_sf, tm_v, lbl_bc_d, lbl_col_d, gw_d, mcat_d, mrev_d, oh, lb_coef, T=SEQ):
    TS = lambda n, s, d: st.enter_context(nc.sbuf_tensor(n, s, d))
    PS = lambda n: st.enter_context(nc.psum_tensor(n, [128, 512], F32))
    NT = T // 128
    sq = TS("hg_sq", [64, T], BF16)
    snf = TS("hg_snf", [64, T], BF16)
    shg = TS("hg_shg", [64, T], BF16)
    sf = TS("hg_sf", [128, NT, 64], F32)
    omf = TS("hg_omf", [128, NT, 64], F32)
    logf = TS("hg_logf", [128, NT, 64], F32)
    vall = TS("hg_v", [128, NT, 64], BF16)
    lbl = TS("hg_lbl", [128, 128], F32)
    lb_bc = TS("hg_lb_bc", [128, 64], F32)
    oml_bc = TS("hg_oml_bc", [128, 64], F32)
    lblc = TS("hg_lblc", [64, 2], F32)
    oml_col = TS("hg_oml_col", [64, 1], F32)
    gw = TS("hg_gw", [64, 1], F32)
    mcat = TS("hg_mcat", [128, 136], F32)
    mrev = TS("hg_mrev", [128, 128], F32)
    mincl_bf = TS("hg_mincl", [128, 128], BF16)
    mtot_bf = TS("hg_mtot", [128, 8], BF16)
    ones64 = TS("hg_ones", [64, 64], BF16)
    eq = TS("hg_eq", [64, 128], F32)
    ekn = TS("hg_ekn", [64, 128], F32)
    dec = [TS(f"hg_dec{i}", [64, 8], F32) for i in range(2)]
    ehat = TS("hg_ehat", [128, 64], F32)
    qt = [TS(f"hg_qt{i}", [64, 128], BF16) for i in range(2)]
    kt = TS("hg_kt", [64, 128], BF16)
    khat = TS("hg_khat", [128, 64], BF16)
    vblk = TS("hg_vblk", [128, 8, 64], BF16)
    scm = TS("hg_scm", [128, 128], BF16)
    Sall = [TS(f"hg_S{i}", [64, 9, 64], F32) for i in range(2)]
    Sbf = [TS(f"hg_Sbf{i}", [64, 8, 64], BF16) for i in range(2)]
    osq = TS("hg_osq", [64, 512], BF16)
    o32 = TS("hg_o32", [64, 512], F32)
    nsq = TS("hg_nsq", [64, 512], F32)
    ohb = [TS(f"hg_ohb{i}", [64, 512], BF16) for i in range(2)]
    ps_c = PS("hg_ps_c")
    ps_r = PS("hg_ps_r")
    ps_sc = PS("hg_ps_sc")
    ps_u = [PS(f"hg_ps_u{i}") for i in range(2)]
    ps_oh = [PS(f"hg_ps_oh{i}") for i in range(2)]
    ps_n = PS("hg_ps_n")

    S.op("sp", lambda e: e.dma_start(out=sq[:], in_=fm[0:64, :]), w=["sq"], dma=True)
    S.op("sp", lambda e: e.dma_start(out=snf[:], in_=fm[64:128, :]), w=["snf"], dma=True)
    S.op("sp", lambda e: e.dma_start(out=shg[:], in_=fm[128:192, :]), w=["shg"], dma=True)
    S.op("sp", lambda e: e.dma_start(out=sf[:], in_=tm_sf.rearrange("(a p) c -> p a c", p=128)), w=["sf"], dma=True)
    S.op("pool", lambda e: e.dma_start(out=vall[:], in_=tm_v[:, 0:64].rearrange("(a p) c -> p a c", p=128)), w=["vall"], dma=True)
    S.op("sp", lambda e: e.dma_start(out=lbl[:], in_=lbl_bc_d.partition_broadcast(128)), w=["lbl"], dma=True)
    S.op("sp", lambda e: e.dma_start(out=lblc[:], in_=lbl_col_d[:, :]), w=["lblc"], dma=True)
    S.op("sp", lambda e: e.dma_start(out=gw[:], in_=gw_d[:, :]), w=["gw"], dma=True)
    S.op("sp", lambda e: e.dma_start(out=mcat[:], in_=mcat_d[:, :]), w=["mcat"], dma=True)
    S.op("sp", lambda e: e.dma_start(out=mrev[:], in_=mrev_d[:, :]), w=["mrev"], dma=True)
    S.op("pool", lambda e: e.memset(ones64[:], 1.0), w=["ones64"])
    S.op("pool", lambda e: e.memset(Sall[0][:, 0, :], 0.0), w=[("S", 0)])
    S.op("dve", lambda e: e.tensor_copy(out=mincl_bf[:], in_=mcat[:, 0:128]), r=["mcat"], w=["mincl_bf"])
    S.op("dve", lambda e: e.tensor_copy(out=mtot_bf[:], in_=mcat[:, 128:136]), r=["mcat"], w=["mtot_bf"])
    S.op("dve", lambda e: e.tensor_tensor(out=lb_bc[:], in0=lbl[:, 64:128], in1=lbl[:, 0:64], op=ALU.subtract), r=["lbl"], w=["lb_bc"])
    S.op("act", lambda e: e.activation(out=lb_bc[:], in_=lb_bc[:], func=AF.Sigmoid), w=["lb_bc"])
    S.op("dve", lambda e: e.tensor_scalar(out=lb_bc[:], in0=lb_bc[:], scalar1=float(lb_coef), scalar2=None, op0=ALU.mult), w=["lb_bc"])
    S.op("dve", lambda e: e.tensor_scalar(out=oml_bc[:], in0=lb_bc[:], scalar1=-1.0, scalar2=1.0, op0=ALU.mult, op1=ALU.add),
         r=["lb_bc"], w=["oml_bc"])
    S.op("dve", lambda e: e.tensor_tensor(out=oml_col[:], in0=lblc[:, 1:2], in1=lblc[:, 0:1], op=ALU.subtract), r=["lblc"], w=["oml_col"])
    S.op("act", lambda e: e.activation(out=oml_col[:], in_=oml_col[:], func=AF.Sigmoid), w=["oml_col"])
    S.op("dve", lambda e: e.tensor_scalar(out=oml_col[:], in0=oml_col[:], scalar1=-float(lb_coef), scalar2=1.0, op0=ALU.mult, op1=ALU.add),
         w=["oml_col"])
    for g in range(NT // 8 if NT >= 8 else 1):
        nt = min(8, NT)
        sl = slice(g * 8, g * 8 + nt)
        S.op("dve", lambda e, sl=sl, nt=nt: e.tensor_tensor(out=sf[:, sl, :], in0=sf[:, sl, :],
                                                        in1=oml_bc[:].unsqueeze(1).broadcast_to([128, nt, 64]), op=ALU.mult),
             r=["oml_bc"], w=["sf"])
        S.op("dve", lambda e, sl=sl, nt=nt: e.tensor_tensor(out=sf[:, sl, :], in0=sf[:, sl, :],
                                                        in1=lb_bc[:].unsqueeze(1).broadcast_to([128, nt, 64]), op=ALU.add),
             r=["lb_bc"], w=["sf"])
        S.op("act", lambda e, sl=sl: e.activation(out=logf[:, sl, :], in_=sf[:, sl, :], func=AF.Ln), r=["sf"], w=[("logf", g)])
        S.op("pool", lambda e, sl=sl: e.tensor_scalar(out=omf[:, sl, :], in0=sf[:, sl, :], scalar1=-1.0, scalar2=1.0,
                                                     op0=ALU.mult, op1=ALU.add), r=["sf"], w=[("omf", g)])
    for i in range(NT):
        g = i // 8
        b = i % 2
        ts = slice(i * 128, (i + 1) * 128)
        ob = (i // 4) % 2
        S.op("pe", lambda e, i=i: e.matmul(ps_c[0:64, 0:136], logf[:, i, :], mcat[:], start=True, stop=True),
             r=[("logf", g), "mcat"], w=["ps_c"])
        S.op("pe", lambda e, i=i: e.matmul(ps_r[:, 0:64], mrev[:], logf[:, i, :], start=True, stop=True),
             r=[("logf", g), "mrev"], w=["ps_r"])
        S.op("act", lambda e: e.activation(out=eq[:], in_=ps_c[0:64, 0:128], func=AF.Exp), w=["eq", "ps_c"])
        S.op("act", lambda e: e.activation(out=ekn[:], in_=ps_c[0:64, 0:128], func=AF.Exp, scale=-1.0), w=["ekn", "ps_c"])
        S.op("act", lambda e, b=b: e.activation(out=dec[b][:], in_=ps_c[0:64, 128:136], func=AF.Exp), w=[("dec", b), "ps_c"])
        S.op("act", lambda e: e.activation(out=ehat[:], in_=ps_r[:, 0:64], func=AF.Exp), w=["ehat", "ps_r"])
        S.op("dve", lambda e, b=b, ts=ts: e.tensor_tensor(out=qt[b][:], in0=sq[:, ts], in1=eq[:], op=ALU.mult),
             r=["sq", "eq"], w=[("qt", b)])
        S.op("dve", lambda e, ts=ts: e.scalar_tensor_tensor(out=kt[:], in0=snf[:, ts], scalar=oml_col[:, 0:1], in1=ekn[:],
                                                           op0=ALU.mult, op1=ALU.mult), r=["snf", "ekn", "oml_col"], w=["kt"])
        S.op("dve", lambda e, i=i: e.tensor_tensor(out=khat[:], in0=omf[:, i, :], in1=ehat[:], op=ALU.mult),
             r=[("omf", g), "ehat"], w=["khat"])
        S.op("pool", lambda e, i=i: e.tensor_tensor(out=vblk[:], in0=vall[:, i, :].unsqueeze(1).broadcast_to([128, 8, 64]),
                                                   in1=mtot_bf[:].unsqueeze(2).broadcast_to([128, 8, 64]), op=ALU.mult),
             r=["vall", "mtot_bf"], w=["vblk"])
        S.op("pe", lambda e, b=b: e.matmul(ps_sc[:, 0:128], kt[:], qt[b][:], start=True, stop=True), r=["kt", ("qt", b)], w=["ps_sc"])
        S.op("dve", lambda e: e.tensor_tensor(out=scm[:], in0=ps_sc[:, 0:128], in1=mincl_bf[:], op=ALU.mult),
             r=["mincl_bf"], w=["scm", "ps_sc"])
        S.op("pe", lambda e, b=b: e.matmul(ps_u[b][0:64, :], khat[:], vblk[:].rearrange("p a c -> p (a c)"), start=True, stop=True),
             r=["khat", "vblk"], w=[("ps_u", b)])
        if i > 0:
            S.op("dve", lambda e, b=b: e.tensor_copy(out=Sall[b][:, 0, :], in_=Sall[1 - b][:, 8, :]), r=[("S", 1 - b)], w=[("S", b)])
        for n in range(8):
            S.op("dve", lambda e, b=b, n=n: e.scalar_tensor_tensor(
                out=Sall[b][:, n + 1, :], in0=Sall[b][:, n, :], scalar=dec[b][:, n:n + 1], in1=ps_u[b][0:64, n * 64:(n + 1) * 64],
                op0=ALU.mult, op1=ALU.add), r=[("dec", b)], w=[("S", b), ("ps_u", b)])
        S.op("act", lambda e, b=b: e.activation(out=Sbf[b][:], in_=Sall[b][:, 0:8, :], func=AF.Copy), r=[("S", b)], w=[("Sbf", b)])
        c0 = (i % 4) * 128
        S.op("pe", lambda e, i=i, ob=ob, c0=c0: e.matmul(ps_oh[ob][0:64, c0:c0 + 128], vall[:, i, :], scm[:], start=True, stop=False),
             r=["vall", "scm"], w=[("ps_oh", ob)])
        for n in range(8):
            S.op("pe", lambda e, b=b, ob=ob, c0=c0, n=n: e.matmul(
                ps_oh[ob][0:64, c0 + 16 * n:c0 + 16 * n + 16], Sbf[b][:, n, :], qt[b][:, 16 * n:16 * n + 16],
                start=False, stop=(n == 7)), r=[("Sbf", b), ("qt", b)], w=[("ps_oh", ob)])
        if i % 4 == 3 or i == NT - 1:
            qs = slice((i // 4) * 512, (i // 4) * 512 + 512)
            S.op("act", lambda e, ob=ob: e.activation(out=osq[:], in_=ps_oh[ob][0:64, :], func=AF.Square), w=["osq", ("ps_oh", ob)])
            S.op("act", lambda e, ob=ob: e.activation(out=o32[:], in_=ps_oh[ob][0:64, :], func=AF.Copy), w=["o32", ("ps_oh", ob)])
            S.op("pe", lambda e: e.matmul(ps_n[0:64, :], ones64[:], osq[:], start=True, stop=True), r=["osq", "ones64"], w=["ps_n"])
            S.op("act", lambda e: e.activation(out=nsq[:], in_=ps_n[0:64, :], func=AF.Sqrt, scale=1.0 / 64.0, bias=EPS), w=["nsq", "ps_n"])
            S.op("dve", lambda e: e.reciprocal(out=nsq[:], in_=nsq[:]), w=["nsq"])
            S.op("dve", lambda e: e.scalar_tensor_tensor(out=o32[:], in0=o32[:], scalar=gw[:, 0:1], in1=nsq[:], op0=ALU.mult, op1=ALU.mult),
                 r=["nsq", "gw"], w=["o32"])
            S.op("pool", lambda e, ob=ob, qs=qs: e.tensor_tensor(out=ohb[ob][:], in0=o32[:], in1=shg[:, qs], op=ALU.mult),
                 r=["o32", "shg"], w=[("ohb", ob)])
            S.op("sp", lambda e, ob=ob, qs=qs: e.dma_start(out=oh[:, qs], in_=ohb[ob][:]), r=[("ohb", ob)], dma=True)


def build_hgrn(lb_coef, T=SEQ):
    nc = bass.Bass("TRN2", target_bir_lowering=False)
    fm = nc.dram_tensor("fm", [NFM, T], BF16, kind="ExternalInput").ap()
    tm_sf = nc.dram_tensor("tm_sf", [T, 64], F32, kind="ExternalInput").ap()
    tm_v = nc.dram_tensor("tm_v", [T, 192], BF16, kind="ExternalInput").ap()
    lbl_bc = nc.dram_tensor("lbl_bc", [1, 128], F32, kind="ExternalInput").ap()
    lbl_col = nc.dram_tensor("lbl_col", [64, 2], F32, kind="ExternalInput").ap()
    gw = nc.dram_tensor("gw", [64, 1], F32, kind="ExternalInput").ap()
    mcat = nc.dram_tensor("mcat", [128, 136], F32, kind="ExternalInput").ap()
    mrev = nc.dram_tensor("mrev", [128, 128], F32, kind="ExternalInput").ap()
    oh = nc.dram_tensor("oh", [64, T], BF16, kind="ExternalOutput").ap()
    with contextlib.ExitStack() as st:
        S = Sched(nc)
        phase_hgrn(nc, S, st, fm, tm_sf, tm_v, lbl_bc, lbl_col, gw, mcat, mrev, oh, lb_coef, T)
        S.emit()
    return nc


def sincos(S, ang, kf, ki, hs, sin_out, cos_out, tag, eng="dve"):
    a, k, h = tag + "ang", tag + "kf", tag + "hs"
    S.op(eng, lambda e: e.tensor_scalar(out=kf, in0=ang, scalar1=1.0 / (2.0 * math.pi), scalar2=None, op0=ALU.mult), r=[a], w=[k])
    S.op(eng, lambda e: e.tensor_copy(out=ki, in_=kf), r=[k], w=[tag + "ki"])
    S.op(eng, lambda e: e.tensor_copy(out=kf, in_=ki), r=[tag + "ki"], w=[k])
    S.op("dve", lambda e: e.scalar_tensor_tensor(out=ang, in0=kf, scalar=-C1_2PI, in1=ang, op0=ALU.mult, op1=ALU.add), r=[k], w=[a])
    S.op("dve", lambda e: e.scalar_tensor_tensor(out=ang, in0=kf, scalar=-C2_2PI, in1=ang, op0=ALU.mult, op1=ALU.add), r=[k], w=[a])
    S.op(eng, lambda e: e.tensor_scalar(out=ang, in0=ang, scalar1=math.pi, scalar2=-math.pi, op0=ALU.min, op1=ALU.max), w=[a])
    S.op("act", lambda e: e.activation(out=sin_out, in_=ang, func=AF.Sin), r=[a], w=[tag + "sin"])
    S.op("act", lambda e: e.activation(out=hs, in_=ang, func=AF.Sin, scale=0.5), r=[a], w=[h])
    S.op(eng, lambda e: e.tensor_tensor(out=hs, in0=hs, in1=hs, op=ALU.mult), w=[h])
    S.op(eng, lambda e: e.tensor_scalar(out=cos_out, in0=hs, scalar1=-2.0, scalar2=1.0, op0=ALU.mult, op1=ALU.add), r=[h], w=[tag + "cos"])


def cmul(S, eng, o_re, o_im, a_re, a_im, b_re, b_im, t0, t1, rd, wr, conj_a=False):
    sg = -1.0 if conj_a else 1.0
    S.op(eng, lambda e: e.tensor_tensor(out=t0, in0=a_im, in1=b_im, op=ALU.mult), r=rd, w=[wr + "t0"])
    S.op(eng, lambda e: e.tensor_tensor(out=t1, in0=a_re, in1=b_re, op=ALU.mult), r=rd, w=[wr + "t1"])
    S.op("dve", lambda e: e.scalar_tensor_tensor(out=o_re, in0=t0, scalar=-sg, in1=t1, op0=ALU.mult, op1=ALU.add),
         r=[wr + "t0", wr + "t1"], w=[wr + "re"])
    S.op(eng, lambda e: e.tensor_tensor(out=t0, in0=a_im, in1=b_re, op=ALU.mult), r=rd + [wr + "re"], w=[wr + "t0"])
    S.op(eng, lambda e: e.tensor_tensor(out=t1, in0=a_re, in1=b_im, op=ALU.mult), r=rd + [wr + "re"], w=[wr + "t1"])
    S.op("dve", lambda e: e.scalar_tensor_tensor(out=o_im, in0=t0, scalar=sg, in1=t1, op0=ALU.mult, op1=ALU.add),
         r=[wr + "t0", wr + "t1"], w=[wr + "im"])


def s5_consts():
    negsig = np.repeat(-np.arange(16, dtype=np.float32), 64)[None, :]
    kidx = np.arange(32, dtype=np.float32)[None, :]
    midx = np.arange(1, 513, dtype=np.float32)[None, :]
    rowmask = (np.arange(64)[:, None] // 16 == np.arange(4)[None, :]).astype(np.float32)
    return negsig, kidx, midx, rowmask


def s5_params(z, l, j):
    gs = [4 * j + gl for gl in range(4)]
    f = np.float32
    pA_are = np.concatenate([np.repeat(z["s5_a_re"][l][g][None, :], 16, 0) for g in gs]).astype(f)
    pA_aim = np.concatenate([np.repeat(z["s5_a_im"][l][g][None, :], 16, 0) for g in gs]).astype(f)
    pA_ldt = np.concatenate([np.full((16, 1), z["s5_log_dt"][l][g]) for g in gs]).astype(f)
    pA_bre = np.concatenate([z["s5_b_re"][l][g].T for g in gs]).astype(f)
    pA_bim = np.concatenate([z["s5_b_im"][l][g].T for g in gs]).astype(f)
    pB = np.zeros((2, 128, 3), f)
    pB_cre = np.zeros((2, 128, 64), f)
    pB_cim = np.zeros((2, 128, 64), f)
    for q in range(2):
        for h in range(2):
            gl = 2 * q + h
            g = gs[gl]
            rows = slice(64 * h, 64 * h + 64)
            pB[q, rows, 0] = z["s5_a_re"][l][g]
            pB[q, rows, 1] = z["s5_a_im"][l][g]
            pB[q, rows, 2] = z["s5_log_dt"][l][g]
            pB_cre[q, rows, 16 * gl:16 * gl + 16] = z["s5_c_re"][l][g].T
            pB_cim[q, rows, 16 * gl:16 * gl + 16] = z["s5_c_im"][l][g].T
    dcol = z["s5_d"][l][64 * j:64 * j + 64][:, None].astype(f)
    pA = np.concatenate([pA_are, pA_aim, pA_bre, pA_bim, pA_ldt], axis=1)
    return {"s5p_pA": np.ascontiguousarray(pA), "s5p_pB": pB, "s5p_cre": pB_cre, "s5p_cim": pB_cim, "s5p_d": dcol}


def phase_s5(nc, S, st, fm, pA_d, pB_d, cre_d, cim_d, dcol_d, negsig_d, kidx_d, midx_d, rowmask_d, yg, T=SEQ):
    TS = lambda n, s, d: st.enter_context(nc.sbuf_tensor(n, s, d))
    PS = lambda n: st.enter_context(nc.psum_tensor(n, [128, 512], F32))
    NB = T // 16
    su = TS("s5_su", [64, T], BF16)
    outsb = TS("s5_out", [64, T], BF16)
    pA = TS("s5_pA", [64, 257], F32)
    dcol = TS("s5_dcol", [64, 1], F32)
    rowmask = TS("s5_rowmask", [64, 4], F32)
    negsig = TS("s5_negsig", [64, 1024], F32)
    SCR = TS("s5_scr", [128, 8192], F32)
    tA = [SCR[0:64, 1024 * i:1024 * (i + 1)] for i in range(8)]
    tAi = TS("s5_tAi", [64, 1024], I32)
    sA = [TS(f"s5_sA{i}", [64, 64], F32) for i in range(10)]
    sAi = TS("s5_sAi", [64, 64], I32)
    dtA = TS("s5_dtA", [64, 1], F32)
    W1tab = [[TS(f"s5_W1tab{q}{ri}", [64, 16, 128], BF16) for ri in range(2)] for q in range(2)]
    pB = [TS(f"s5_pB{q}", [128, 3], F32) for q in range(2)]
    crep = [TS(f"s5_crep{q}", [128, 64], F32) for q in range(2)]
    cimp = [TS(f"s5_cimp{q}", [128, 64], F32) for q in range(2)]
    kidx = TS("s5_kidx", [128, 32], F32)
    midx = TS("s5_midx", [128, 512], F32)
    tB = [TS(f"s5_tB{i}", [128, 32], F32) for i in range(7)]
    tBi = TS("s5_tBi", [128, 32], I32)
    cB = [TS(f"s5_cB{i}", [128, 1], F32) for i in range(6)]
    cBi = TS("s5_cBi", [128, 1], I32)
    gt = [SCR[:, 2048 * i:2048 * (i + 1)].rearrange("p (k c) -> p k c", k=32) for i in range(2)]
    Gpad = [[TS(f"s5_G{q}{ri}", [128, 32, 64], BF16) for ri in range(2)] for q in range(2)]
    Tc = [TS(f"s5_Tc{q}", [128, 512], F32) for q in range(2)]
    Tsn = [TS(f"s5_Ts{q}", [128, 512], F32) for q in range(2)]
    rho = [TS(f"s5_rho{q}", [128, 1], F32) for q in range(2)]
    l2 = [SCR[:, 4096 + 512 * i:4096 + 512 * (i + 1)] for i in range(6)]
    l2i = TS("s5_l2i", [128, 512], I32)
    roll = [TS(f"s5_roll{i}", [128, 512], F32) for i in range(2)]
    W15 = [[TS(f"s5_W15{q}{ri}", [128, 512], F32) for ri in range(2)] for q in range(2)]
    W1bf = [[TS(f"s5_W1bf{q}{ri}", [128, 16, 512], BF16) for ri in range(2)] for q in range(2)]
    Xbf = [[TS(f"s5_Xbf{q}{ri}", [128, 512], BF16) for ri in range(2)] for q in range(2)]
    ytmp = [TS(f"s5_ytmp{i}", [64, 512], F32) for i in range(2)]
    ps_z = [PS(f"s5_ps_z{i}") for i in range(2)]
    ps_y = [PS(f"s5_ps_y{i}") for i in range(2)]

    ld = lambda eng, dst, src, name: S.op(eng, lambda e: e.dma_start(out=dst, in_=src), w=[name], dma=True)
    ld("sp", su[:], fm[256:320, :], "su")
    ld("sp", pA[:], pA_d[:, :], "pA")
    ld("sp", dcol[:], dcol_d[:, :], "dcol")
    ld("sp", rowmask[:], rowmask_d[:, :], "rowmask")
    ld("sp", negsig[:], negsig_d.partition_broadcast(64), "negsig")
    ld("sp", kidx[:], kidx_d.partition_broadcast(128), "kidx")
    ld("sp", midx[:], midx_d.partition_broadcast(128), "midx")
    for q in range(2):
        ld("sp", pB[q][:], pB_d[q], ("pB", q))
        ld("sp", crep[q][:], cre_d[q], ("crep", q))
        ld("sp", cimp[q][:], cim_d[q], ("cimp", q))
    are, aim, bre, bim, ldt = pA[:, 0:64], pA[:, 64:128], pA[:, 128:192], pA[:, 192:256], pA[:, 256:257]
    lam, th, abr, abi, mg, zr, zi, den, u0, u1 = [t[:] for t in sA]
    S.op("act", lambda e: e.activation(out=dtA[:], in_=ldt, func=AF.Exp), r=["pA"], w=["dtA"])
    S.op("dve", lambda e: e.tensor_scalar(out=lam, in0=are, scalar1=dtA[:, 0:1], scalar2=None, op0=ALU.mult), r=["pA", "dtA"], w=["lamA"])
    S.op("dve", lambda e: e.tensor_scalar(out=th, in0=aim, scalar1=dtA[:, 0:1], scalar2=None, op0=ALU.mult), r=["pA", "dtA"], w=["thA"])
    S.op("dve", lambda e: e.tensor_copy(out=u0, in_=th), r=["thA"], w=["sAang"])
    sincos(S, u0, u1, sAi[:], den, abi, abr, "sA")
    S.op("act", lambda e: e.activation(out=mg, in_=lam, func=AF.Exp), r=["lamA"], w=["mgA"])
    S.op("dve", lambda e: e.tensor_tensor(out=abr, in0=abr, in1=mg, op=ALU.mult), r=["mgA", "sAcos"], w=["abr"])
    S.op("dve", lambda e: e.tensor_tensor(out=abi, in0=abi, in1=mg, op=ALU.mult), r=["mgA", "sAsin"], w=["abi"])
    S.op("dve", lambda e: e.tensor_scalar(out=abr, in0=abr, scalar1=-1.0, scalar2=None, op0=ALU.add), w=["abr"])
    S.op("dve", lambda e: e.tensor_tensor(out=den, in0=are, in1=are, op=ALU.mult), r=["pA", "sAcos", "sAsin"], w=["den"])
    S.op("dve", lambda e: e.tensor_tensor(out=u0, in0=aim, in1=aim, op=ALU.mult), r=["pA", "sAsin"], w=["u0"])
    S.op("dve", lambda e: e.tensor_tensor(out=den, in0=den, in1=u0, op=ALU.add), r=["u0"], w=["den"])
    S.op("dve", lambda e: e.reciprocal(out=den, in_=den), w=["den"])
    S.op("dve", lambda e: e.tensor_tensor(out=u0, in0=abr, in1=are, op=ALU.mult), r=["abr"], w=["u0"])
    S.op("dve", lambda e: e.tensor_tensor(out=u1, in0=abi, in1=aim, op=ALU.mult), r=["abi"], w=["u1"])
    S.op("dve", lambda e: e.tensor_tensor(out=zr, in0=u0, in1=u1, op=ALU.add), r=["u0", "u1"], w=["zr"])
    S.op("dve", lambda e: e.tensor_tensor(out=zr, in0=zr, in1=den, op=ALU.mult), r=["den"], w=["zr"])
    S.op("dve", lambda e: e.tensor_tensor(out=u0, in0=abi, in1=are, op=ALU.mult), r=["abi", "zr"], w=["u0"])
    S.op("dve", lambda e: e.tensor_tensor(out=u1, in0=abr, in1=aim, op=ALU.mult), r=["abr", "zr"], w=["u1"])
    S.op("dve", lambda e: e.tensor_tensor(out=zi, in0=u0, in1=u1, op=ALU.subtract), r=["u0", "u1"], w=["zi"])
    S.op("dve", lambda e: e.tensor_tensor(out=zi, in0=zi, in1=den, op=ALU.mult), r=["den"], w=["zi"])
    A3 = lambda t: t[:].rearrange("p (s m) -> p s m", s=16)
    bc3 = lambda ap: ap.unsqueeze(1).broadcast_to([64, 16, 64])
    ang3, kf3, hs3, sn3, cs3, mg3, w_r, w_i = tA
    S.op("dve", lambda e: e.tensor_tensor(out=A3(ang3), in0=A3(negsig), in1=bc3(th), op=ALU.mult), r=["negsig", "thA"], w=["tAang"])
    sincos(S, ang3[:], kf3[:], tAi[:], hs3[:], sn3[:], cs3[:], "tA")
    S.op("dve", lambda e: e.tensor_tensor(out=A3(mg3), in0=A3(negsig), in1=bc3(lam), op=ALU.mult), r=["negsig", "lamA"], w=["mg3"])
    S.op("act", lambda e: e.activation(out=mg3[:], in_=mg3[:], func=AF.Exp), w=["mg3"])
    S.op("dve", lambda e: e.tensor_tensor(out=cs3[:], in0=cs3[:], in1=mg3[:], op=ALU.mult), r=["mg3"], w=["tAcos"])
    S.op("dve", lambda e: e.tensor_tensor(out=sn3[:], in0=sn3[:], in1=mg3[:], op=ALU.mult), r=["mg3"], w=["tAsin"])
    cmul(S, "dve", A3(w_r), A3(w_i), A3(cs3), A3(sn3), bc3(zr), bc3(zi), A3(ang3), A3(kf3),
         ["tAcos", "tAsin", "zr", "zi", "tAang", "tAkf"], "wz")
    cmul(S, "dve", A3(cs3), A3(sn3), A3(w_r), A3(w_i), bc3(bre), bc3(bim), A3(ang3), A3(kf3),
         ["wzre", "wzim", "pA", "tAcos", "tAsin"], "Bs")
    for q in range(2):
        for ri, src in ((0, cs3), (1, sn3)):
            for h in range(2):
                gl = 2 * q + h
                S.op("dve", lambda e, q=q, ri=ri, h=h, gl=gl, src=src: e.tensor_scalar(
                    out=W1tab[q][ri][:, :, 64 * h:64 * h + 64], in0=A3(src), scalar1=rowmask[:, gl:gl + 1], scalar2=None, op0=ALU.mult),
                    r=["Bsre", "Bsim", "rowmask"], w=[("W1tab", q, ri, h)])
    S.barrier()
    for q in range(2):
        lamB, thB, dtB, phi, th15, junk = [t[:] for t in cB]
        angk, kfk, hsk, snk, csk, mgk, nsk = [t[:] for t in tB]
        pq = [("pB", q)]
        tg = f"B{q}"
        S.op("act", lambda e, q=q: e.activation(out=dtB, in_=pB[q][:, 2:3], func=AF.Exp), r=pq, w=[tg + "dt"])
        S.op("dve", lambda e, q=q: e.tensor_tensor(out=lamB, in0=pB[q][:, 0:1], in1=dtB, op=ALU.mult), r=pq + [tg + "dt"], w=[tg + "lam"])
        S.op("dve", lambda e, q=q: e.tensor_tensor(out=thB, in0=pB[q][:, 1:2], in1=dtB, op=ALU.mult), r=pq + [tg + "dt"], w=[tg + "th"])
        S.op("dve", lambda e: e.tensor_scalar(out=angk, in0=kidx[:], scalar1=thB[:, 0:1], scalar2=None, op0=ALU.mult),
             r=["kidx", tg + "th"], w=[tg + "kang"])
        sincos(S, angk, kfk, tBi[:], hsk, snk, csk, tg + "k")
        S.op("dve", lambda e: e.tensor_scalar(out=mgk, in0=kidx[:], scalar1=lamB[:, 0:1], scalar2=None, op0=ALU.mult),
             r=["kidx", tg + "lam"], w=[tg + "mgk"])
        S.op("act", lambda e: e.activation(out=mgk, in_=mgk, func=AF.Exp), w=[tg + "mgk"])
        S.op("dve", lambda e: e.tensor_tensor(out=csk, in0=csk, in1=mgk, op=ALU.mult), r=[tg + "mgk"], w=[tg + "kcos"])
        S.op("dve", lambda e: e.tensor_tensor(out=snk, in0=snk, in1=mgk, op=ALU.mult), r=[tg + "mgk"], w=[tg + "ksin"])
        S.op("dve", lambda e: e.tensor_scalar(out=nsk, in0=snk, scalar1=-1.0, scalar2=None, op0=ALU.mult), r=[tg + "ksin"], w=[tg + "nsk"])
        S.op("dve", lambda e: e.tensor_scalar(out=kfk, in0=csk, scalar1=-1.0, scalar2=None, op0=ALU.mult), r=[tg + "kcos"], w=[tg + "kkf"])
        kb = lambda ap: ap.unsqueeze(2).broadcast_to([128, 32, 64])
        cb = lambda t: t[:].unsqueeze(1).broadcast_to([128, 32, 64])
        for ri, (f1, f2) in enumerate(((csk, nsk), (nsk, kfk))):
            S.op("dve", lambda e, q=q, f1=f1: e.tensor_tensor(out=gt[0][:], in0=cb(crep[q]), in1=kb(f1), op=ALU.mult),
                 r=[("crep", q), tg + "kcos", tg + "nsk", tg + "kkf"], w=["gt0"])
            S.op("dve", lambda e, q=q, f2=f2: e.tensor_tensor(out=gt[1][:], in0=cb(cimp[q]), in1=kb(f2), op=ALU.mult),
                 r=[("cimp", q), tg + "kcos", tg + "nsk", tg + "kkf"], w=["gt1"])
            S.op("dve", lambda e, q=q, ri=ri: e.tensor_tensor(out=Gpad[q][ri][:], in0=gt[0][:], in1=gt[1][:], op=ALU.add),
                 r=["gt0", "gt1"], w=[("Gpad", q, ri)])
        S.op("dve", lambda e: e.tensor_scalar(out=phi, in0=thB, scalar1=16.0, scalar2=None, op0=ALU.mult), r=[tg + "th"], w=[tg + "phi"])
        S.op("dve", lambda e: e.tensor_scalar(out=th15, in0=phi, scalar1=1.0 / (2.0 * math.pi), scalar2=None, op0=ALU.mult),
             r=[tg + "phi"], w=[tg + "th15"])
        S.op("dve", lambda e: e.tensor_copy(out=cBi[:], in_=th15), r=[tg + "th15"], w=[tg + "cBi"])
        S.op("dve", lambda e: e.tensor_copy(out=th15, in_=cBi[:]), r=[tg + "cBi"], w=[tg + "th15"])
        S.op("dve", lambda e: e.scalar_tensor_tensor(out=phi, in0=th15, scalar=-C1_2PI, in1=phi, op0=ALU.mult, op1=ALU.add),
             r=[tg + "th15"], w=[tg + "phi"])
        S.op("dve", lambda e: e.scalar_tensor_tensor(out=phi, in0=th15, scalar=-C2_2PI, in1=phi, op0=ALU.mult, op1=ALU.add),
             r=[tg + "th15"], w=[tg + "phi"])
        S.op("dve", lambda e: e.tensor_scalar(out=l2[0][:], in0=midx[:], scalar1=phi[:, 0:1], scalar2=None, op0=ALU.mult),
             r=["midx", tg + "phi"], w=["l2ang"])
        sincos(S, l2[0][:], l2[1][:], l2i[:], l2[2][:], Tsn[q][:], Tc[q][:], "l2")
        S.op("dve", lambda e, q=q: e.tensor_copy(out=Tsn[q][:], in_=Tsn[q][:]), r=["l2sin"], w=[("Ts", q)])
        S.op("dve", lambda e, q=q: e.tensor_copy(out=Tc[q][:], in_=Tc[q][:]), r=["l2cos"], w=[("Tc", q)])
        S.op("act", lambda e, q=q: e.activation(out=rho[q][:], in_=lamB, func=AF.Exp, scale=16.0), r=[tg + "lam"], w=[("rho", q)])
    suv = su[:].rearrange("p (m s) -> p s m", s=16)
    zi_ = 0
    for q in range(2):
        for ri in range(2):
            for s in range(16):
                pb = zi_ % 2
                zi_ += 1
                S.op("pe", lambda e, q=q, ri=ri, s=s, pb=pb: e.matmul(ps_z[pb][:, 0:NB], W1tab[q][ri][:, s, :], suv[:, s, :], start=True, stop=True),
                     r=["su", ("W1tab", q, ri, 0), ("W1tab", q, ri, 1)], w=[("ps_z", pb)])
                dst = W15[q][ri] if s == 15 else roll[s % 2]
                dn = ("W15", q, ri) if s == 15 else ("roll", s % 2)
                if s == 0:
                    S.op("dve", lambda e, pb=pb, dst=dst: e.tensor_copy(out=dst[:, 0:NB], in_=ps_z[pb][:, 0:NB]), w=[dn, ("ps_z", pb)])
                else:
                    S.op("dve", lambda e, pb=pb, dst=dst, s=s: e.tensor_tensor(out=dst[:, 0:NB], in0=ps_z[pb][:, 0:NB],
                                                                          in1=roll[(s - 1) % 2][:, 0:NB], op=ALU.add),
                         r=[("roll", (s - 1) % 2)], w=[dn, ("ps_z", pb)])
                S.op("act", lambda e, q=q, ri=ri, s=s, dst=dst: e.activation(out=W1bf[q][ri][:, s, 0:NB], in_=dst[:, 0:NB], func=AF.Copy),
                     r=[dn], w=[("W1bf", q, ri, s)])
    for q in range(2):
        ur, ui, t0, t1, vr, vi = [t[:, 0:NB] for t in l2]
        tc, tsn = Tc[q][:, 0:NB], Tsn[q][:, 0:NB]
        wre, wim = W15[q][0][:, 0:NB], W15[q][1][:, 0:NB]
        cmul(S, "dve", ur, ui, tc, tsn, wre, wim, t0, t1, [("Tc", q), ("Ts", q), ("W15", q, 0), ("W15", q, 1), "l2v"], "l2u", conj_a=True)
        rb = rho[q][:, 0:1].broadcast_to([128, NB])
        S.op("dve", lambda e, rb=rb: e.tensor_tensor_scan(out=vr, data0=rb, data1=ur, initial=0.0, op0=ALU.mult, op1=ALU.add),
             r=["l2ure", ("rho", q)], w=["l2vr"])
        S.op("dve", lambda e, rb=rb: e.tensor_tensor_scan(out=vi, data0=rb, data1=ui, initial=0.0, op0=ALU.mult, op1=ALU.add),
             r=["l2uim", ("rho", q)], w=["l2vi"])
        cmul(S, "dve", ur, ui, tc, tsn, vr, vi, t0, t1, [("Tc", q), ("Ts", q), "l2vr", "l2vi"], "l2x")
        for ri, src in ((0, ur), (1, ui)):
            S.op("pool", lambda e, q=q, ri=ri: e.memset(Xbf[q][ri][:, 0:1], 0.0), w=[("Xbf", q, ri)])
            if NB > 1:
                S.op("act", lambda e, q=q, ri=ri, src=src: e.activation(out=Xbf[q][ri][:, 1:NB], in_=src[:, 0:NB - 1], func=AF.Copy),
                     r=["l2xre", "l2xim"], w=[("Xbf", q, ri)])
        S.op("dve", lambda e: e.tensor_copy(out=l2[0][:, 0:1], in_=l2[0][:, 0:1]), r=[("Xbf", q, 0), ("Xbf", q, 1)], w=["l2v", "l2ure", "l2uim"])
    outv = outsb[:].rearrange("p (m s) -> p s m", s=16)
    for s in range(16):
        pb = s % 2
        k = 0
        for q in range(2):
            for ri in range(2):
                S.op("pe", lambda e, q=q, ri=ri, s=s, pb=pb, k=k: e.matmul(ps_y[pb][0:64, 0:NB], Gpad[q][ri][:, s, :], W1bf[q][ri][:, s, 0:NB],
                                                                     start=(k == 0), stop=False),
                     r=[("Gpad", q, ri), ("W1bf", q, ri, s)], w=[("ps_y", pb)])
                k += 1
        for q in range(2):
            for ri in range(2):
                S.op("pe", lambda e, q=q, ri=ri, s=s, pb=pb, k=k: e.matmul(ps_y[pb][0:64, 0:NB], Gpad[q][ri][:, s + 16, :], Xbf[q][ri][:, 0:NB],
                                                                     start=False, stop=(k == 7)),
                     r=[("Gpad", q, ri), ("Xbf", q, ri)], w=[("ps_y", pb)])
                k += 1
        S.op("dve", lambda e, s=s, pb=pb: e.scalar_tensor_tensor(out=ytmp[pb][:, 0:NB], in0=suv[:, s, :], scalar=dcol[:, 0:1],
                                                            in1=ps_y[pb][0:64, 0:NB], op0=ALU.mult, op1=ALU.add),
             r=["su", "dcol"], w=[("ytmp", pb), ("ps_y", pb)])
        S.op("act", lambda e, s=s, pb=pb: e.activation(out=outv[:, s, :], in_=ytmp[pb][:, 0:NB], func=AF.Gelu),
             r=[("ytmp", pb)], w=[("outsb", s)])
    S.op("sp", lambda e: e.dma_start(out=yg[:, :], in_=outsb[:]), r=[("outsb", s) for s in range(16)], dma=True)


def build_s5(T=SEQ):
    nc = bass.Bass("TRN2", target_bir_lowering=False)
    D = lambda n, s, d=F32, k="ExternalInput": nc.dram_tensor(n, s, d, kind=k).ap()
    fm = D("fm", [NFM, T], BF16)
    pA = D("s5p_pA", [64, 257]); pB = D("s5p_pB", [2, 128, 3]); cre = D("s5p_cre", [2, 128, 64]); cim = D("s5p_cim", [2, 128, 64])
    dcol = D("s5p_d", [64, 1]); negsig = D("negsig", [1, 1024]); kidx = D("kidx", [1, 32]); midx = D("midx", [1, 512])
    rowmask = D("rowmask", [64, 4])
    yg = D("yg", [64, T], BF16, "ExternalOutput")
    with contextlib.ExitStack() as st:
        S = Sched(nc)
        phase_s5(nc, S, st, fm, pA, pB, cre, cim, dcol, negsig, kidx, midx, rowmask, yg, T)
        S.emit()
    return nc


def phase_out(nc, S, st, mixin, ssg_d, hT, wout_d, gluw_d, glub_d, fnw_d, hout, final, NTOK=TQ):
    TS = lambda n, s, d: st.enter_context(nc.sbuf_tensor(n, s, d))
    PS = lambda n: st.enter_context(nc.psum_tensor(n, [128, 512], F32))
    wst = [TS(f"po_wst{i}", [128, 1024], F32) for i in range(2)]
    wout = TS("po_wout", [128, 8, 1024], BF16)
    gst = TS("po_gst", [128, 2, 256], F32)
    gluw = TS("po_gluw", [128, 2, 256], BF16)
    glub = TS("po_glub", [128, 2], F32)
    fnw = TS("po_fnw", [128, 8], F32)
    ones = TS("po_ones", [128, 128], BF16)
    mix = [TS(f"po_mix{i}", [128, 8, 512], BF16) for i in range(2)]
    ssg = [TS(f"po_ssg{i}", [128, 2, 512], BF16) for i in range(2)]
    hin = [TS(f"po_hin{i}", [128, 8, 512], F32) for i in range(2)]
    sg = TS("po_sg", [128, 512], F32)
    osb = TS("po_osb", [128, 2, 512], BF16)
    hn = TS("po_hn", [128, 8, 512], F32)
    hsq = TS("po_hsq", [128, 8, 512], BF16)
    nsq = TS("po_nsq", [128, 512], F32)
    ps_g = PS("po_ps_g")
    ps_o = [PS(f"po_ps_o{i}") for i in range(3)]
    ps_n = PS("po_ps_n")

    S.op("pool", lambda e: e.memset(ones[:], 1.0), w=["ones"])
    S.op("sp", lambda e: e.dma_start(out=gst[:], in_=gluw_d.rearrange("(k p) o -> p k o", p=128)), w=["gst"], dma=True)
    S.op("sp", lambda e: e.dma_start(out=glub[:], in_=glub_d[:, :]), w=["glub"], dma=True)
    S.op("sp", lambda e: e.dma_start(out=fnw[:], in_=fnw_d[:, :]), w=["fnw"], dma=True)
    S.op("dve", lambda e: e.tensor_copy(out=gluw[:], in_=gst[:]), r=["gst"], w=["gluw"])
    for k in range(8):
        S.op("sp", lambda e, k=k: e.dma_start(out=wst[k % 2][:], in_=wout_d[k * 128:(k + 1) * 128, :]), w=[("wst", k % 2)], dma=True)
        S.op("pool" if k % 2 else "dve", lambda e, k=k: e.tensor_copy(out=wout[:, k, :], in_=wst[k % 2][:]), r=[("wst", k % 2)], w=[("wout", k)])
    wr = [("wout", k) for k in range(8)]
    mv = mixin.rearrange("(k p) t -> p k t", p=128)
    sv = ssg_d.rearrange("(k p) t -> p k t", p=128)
    hv = hT.rearrange("(k p) t -> p k t", p=128)
    ov = hout.rearrange("(k p) t -> p k t", p=128)
    oi = 0
    for ti in range(NTOK // 512):
        b = ti % 2
        ts = slice(ti * 512, (ti + 1) * 512)
        S.op("sp", lambda e, b=b, ts=ts: e.dma_start(out=mix[b][:], in_=mv[:, :, ts]), w=[("mix", b)], dma=True)
        S.op("sp", lambda e, b=b, ts=ts: e.dma_start(out=ssg[b][:], in_=sv[:, :, ts]), w=[("ssg", b)], dma=True)
        S.op("pool", lambda e, b=b, ts=ts: e.dma_start(out=hin[b][:], in_=hv[:, :, ts]), w=[("hin", b)], dma=True)
        for oc in range(2):
            for kc in range(2):
                S.op("pe", lambda e, b=b, oc=oc, kc=kc: e.matmul(ps_g[:], gluw[:, kc, oc * 128:(oc + 1) * 128], mix[b][:, 2 + kc, :],
                                                             start=(kc == 0), stop=(kc == 1)), r=[("mix", b), "gluw"], w=["ps_g"])
            S.op("act", lambda e, oc=oc: e.activation(out=sg[:], in_=ps_g[:], func=AF.Sigmoid, bias=glub[:, oc:oc + 1]),
                 r=["glub"], w=["sg", "ps_g"])
            S.op("dve", lambda e, b=b, oc=oc: e.tensor_tensor(out=sg[:], in0=sg[:], in1=mix[b][:, 2 + oc, :], op=ALU.mult),
                 r=[("mix", b)], w=["sg"])
            S.op("dve", lambda e, b=b, oc=oc: e.tensor_tensor(out=osb[:, oc, :], in0=sg[:], in1=ssg[b][:, oc, :], op=ALU.mult),
                 r=[("ssg", b), "sg"], w=[("osb", oc)])
        for dc in range(8):
            pb = oi % 3
            oi += 1
            for kc in range(8):
                rhs = (lambda b=b, kc=kc: osb[:, kc - 2, :]) if kc in (2, 3) else (lambda b=b, kc=kc: mix[b][:, kc, :])
                S.op("pe", lambda e, dc=dc, kc=kc, pb=pb, rhs=rhs: e.matmul(ps_o[pb][:], wout[:, kc, dc * 128:(dc + 1) * 128], rhs(),
                                                                      start=(kc == 0), stop=(kc == 7)),
                     r=wr + [("mix", b), ("osb", 0), ("osb", 1)], w=[("ps_o", pb)])
            S.op("dve", lambda e, b=b, dc=dc, pb=pb: e.tensor_tensor(out=hn[:, dc, :], in0=ps_o[pb][:], in1=hin[b][:, dc, :], op=ALU.add),
                 r=[("hin", b)], w=[("hn", dc), ("ps_o", pb)])
            if not final:
                S.op("sp", lambda e, dc=dc, ts=ts: e.dma_start(out=ov[:, dc, ts], in_=hn[:, dc, :]), r=[("hn", dc)], dma=True)
        if final:
            hr = [("hn", dc) for dc in range(8)]
            S.op("act", lambda e: e.activation(out=hsq[:], in_=hn[:], func=AF.Square), r=hr, w=["hsq"])
            for k in range(8):
                S.op("pe", lambda e, k=k: e.matmul(ps_n[:], ones[:], hsq[:, k, :], start=(k == 0), stop=(k == 7)), r=["hsq", "ones"], w=["ps_n"])
            S.op("act", lambda e: e.activation(out=nsq[:], in_=ps_n[:], func=AF.Sqrt, scale=1.0 / D_MODEL, bias=EPS), w=["nsq", "ps_n"])
            S.op("dve", lambda e: e.reciprocal(out=nsq[:], in_=nsq[:]), w=["nsq"])
            for dc in range(8):
                S.op("pool" if dc % 2 else "dve", lambda e, dc=dc: e.scalar_tensor_tensor(
                    out=hn[:, dc, :], in0=hn[:, dc, :], scalar=fnw[:, dc:dc + 1], in1=nsq[:], op0=ALU.mult, op1=ALU.mult) if dc % 2 == 0 else
                    e.tensor_tensor(out=hn[:, dc, :], in0=hn[:, dc, :], in1=nsq[:], op=ALU.mult),
                    r=["nsq", "fnw"], w=[("hn", dc)])
                if dc % 2:
                    S.op("pool", lambda e, dc=dc: e.tensor_scalar(out=hn[:, dc, :], in0=hn[:, dc, :], scalar1=fnw[:, dc:dc + 1], scalar2=None,
                                                                  op0=ALU.mult), r=["fnw"], w=[("hn", dc)])
                S.op("sp", lambda e, dc=dc, ts=ts: e.dma_start(out=ov[:, dc, ts], in_=hn[:, dc, :]), r=[("hn", dc)], dma=True)


def build_out(final, NTOK=TQ):
    nc = bass.Bass("TRN2", target_bir_lowering=False)
    D = lambda n, s, d=F32, k="ExternalInput": nc.dram_tensor(n, s, d, kind=k).ap()
    mixin = D("mixin", [1024, NTOK], BF16)
    ssg = D("ssg", [256, NTOK], BF16)
    hT = D("hT", [D_MODEL, NTOK])
    wout = D("wout", [1024, 1024]); gluw = D("gluw", [256, 256]); glub = D("glub", [128, 2]); fnw = D("fnw", [128, 8])
    hout = D("hout", [D_MODEL, NTOK], F32, "ExternalOutput")
    with contextlib.ExitStack() as st:
        S = Sched(nc)
        phase_out(nc, S, st, mixin, ssg, hT, wout, gluw, glub, fnw, hout, final, NTOK)
        S.emit()
    return nc


_CACHE = {}


def _prog(key, fn):
    if key not in _CACHE:
        _CACHE[key] = fn()
    return _CACHE[key]


def kernel(**inp):
    f = np.float32
    bf = ml_dtypes.bfloat16
    x = np.asarray(inp["x"], f)
    cores = list(range(NCORES))
    ropef, rmat, cmask = attn_consts()
    mcat, mrev = hgrn_consts()
    negsig, kidx, midx, rowmask = s5_consts()
    hT = [np.ascontiguousarray(x[b].T) for b in range(BATCH)]
    for l in range(DEPTH):
        lambda_init = 0.8 - 0.6 * math.exp(-0.3 * l)
        w_in = np.asarray(inp["w_in"][l], f)
        nw = np.ascontiguousarray(np.asarray(inp["norm_w"][l], f).reshape(8, 128).T)
        nc = _prog("inproj", build_inproj)
        ims = [{"hT": hT[c // 4], "wcat": np.ascontiguousarray(w_in[:, core_cols(c % 4)]), "nw": nw} for c in cores]
        r1 = run_bass_kernel_spmd(nc, ims, core_ids=cores).results
        nc = _prog(("attn", l), lambda: build_attn(lambda_init))
        lqk = np.concatenate([inp["diff_lq1"][l], inp["diff_lq2"][l], inp["diff_lk1"][l], inp["diff_lk2"][l]])[None, :].astype(f)
        subln = np.asarray(inp["diff_subln_w"][l], f)[:, None]
        ims = [{"fm": r1[c]["fm"], "tm_v": r1[c]["tm_v"], "lqk": lqk, "subln": subln, "ropef": ropef, "rmat": rmat, "cmask": cmask}
               for c in cores]
        ra = run_bass_kernel_spmd(nc, ims, core_ids=cores).results
        nc = _prog(("hgrn", l), lambda: build_hgrn(float(l)))
        ims = []
        for c in cores:
            j = c % 4
            lbl = np.asarray(inp["hgrn_lb_logits"], f)[:, 64 * j:64 * j + 64]
            ims.append({"fm": r1[c]["fm"], "tm_sf": r1[c]["tm_sf"], "tm_v": r1[c]["tm_v"],
                        "lbl_bc": np.ascontiguousarray(lbl.reshape(1, 128)), "lbl_col": np.ascontiguousarray(lbl.T),
                        "gw": np.asarray(inp["hgrn_norm_w"][l], f)[:, None], "mcat": mcat, "mrev": mrev})
        rh = run_bass_kernel_spmd(nc, ims, core_ids=cores).results
        nc = _prog("s5", build_s5)
        ims = []
        for c in cores:
            d = {"fm": r1[c]["fm"], "negsig": negsig, "kidx": kidx, "midx": midx, "rowmask": rowmask}
            d.update(s5_params(inp, l, c % 4))
            ims.append(d)
        rs = run_bass_kernel_spmd(nc, ims, core_ids=cores).results
        final = (l == DEPTH - 1)
        nc = _prog(("out", final), lambda: build_out(final))
        ims = []
        for c in cores:
            b, tq = c // 4, c % 4
            ts = slice(tq * TQ, (tq + 1) * TQ)
            src = [4 * b + j for j in range(4)]
            mixin = np.concatenate([rh[s]["oh"][:, ts] for s in src] + [rs[s]["yg"][:, ts] for s in src] + [ra[s]["oa"][:, ts] for s in src])
            ssg = np.concatenate([r1[s]["fm"][192:256, ts] for s in src])
            ims.append({"mixin": np.ascontiguousarray(mixin), "ssg": np.ascontiguousarray(ssg), "hT": np.ascontiguousarray(hT[b][:, ts]),
                        "wout": np.asarray(inp["w_out"][l], f), "gluw": np.asarray(inp["s5_glu_w"][l], f),
                        "glub": np.ascontiguousarray(np.asarray(inp["s5_glu_b"][l], f).reshape(2, 128).T),
                        "fnw": np.ascontiguousarray(np.asarray(inp["final_norm_w"], f).reshape(8, 128).T)})
        ro = run_bass_kernel_spmd(nc, ims, core_ids=cores).results
        hT = [np.concatenate([ro[4 * b + tq]["hout"] for tq in range(4)], axis=1) for b in range(BATCH)]
    out = np.stack([hT[b].T for b in range(BATCH)]).astype(f)
    return np.ascontiguousarray(out)
```

```python
import contextlib
import math
import numpy as np
import ml_dtypes
import concourse.bass as bass
import concourse.mybir as mybir
from concourse.bass_utils import run_bass_kernel_spmd

F32 = mybir.dt.float32
BF16 = mybir.dt.bfloat16
I32 = mybir.dt.int32
AF = mybir.ActivationFunctionType
ALU = mybir.AluOpType
AX = mybir.AxisListType

D_MODEL = 1024
SEQ = 8192
BATCH = 2
DEPTH = 2
EPS = 1e-6
NCORES = 8
TQ = SEQ // 4
ROPE_THETA = 500000.0
import os
DBG = set(os.environ.get("KDBG", "").split(","))


class Sched:
    ENGS = ["pe", "act", "dve", "pool", "sp"]

    def __init__(self, nc):
        self.nc = nc
        self.ops = []
        self.last_w = {}
        self.readers = {}
        self.cnt = {e: 0 for e in self.ENGS}
        self.dma_cnt = {}
        self.base = set()

    def op(self, eng, fn, r=(), w=(), dma=False):
        deps = set(self.base)
        for x in r:
            if x in self.last_w:
                deps.add(self.last_w[x])
        for x in w:
            if x in self.last_w:
                deps.add(self.last_w[x])
            for d in self.readers.get(x, ()):
                deps.add(d)
        if dma:
            q = self.dma_cnt.get(eng, 0)
            self.dma_cnt[eng] = q + 1
            tok = ("dma", eng, q)
        else:
            self.cnt[eng] += 1
            tok = ("eng", eng, self.cnt[eng])
        self.ops.append((eng, fn, deps, tok))
        for x in w:
            self.last_w[x] = tok
            self.readers[x] = []
        for x in r:
            self.readers.setdefault(x, []).append(tok)
        return tok

    def barrier(self):
        b = set()
        for e in self.ENGS:
            if self.cnt[e] > 0:
                b.add(("eng", e, self.cnt[e]))
        for e, n in self.dma_cnt.items():
            for q in range(max(0, n - self.NSLOT), n):
                b.add(("dma", e, q))
        self.base = b
        self.last_w = {}
        self.readers = {}

    NSLOT = 8

    def emit(self):
        nc = self.nc
        NSLOT = self.NSLOT
        with contextlib.ExitStack() as st:
            esem = {e: st.enter_context(nc.semaphore("s_" + e)) for e in self.ENGS}
            dsem = {}
            for e in self.dma_cnt:
                dsem[e] = [st.enter_context(nc.semaphore(f"d_{e}_{i}")) for i in range(NSLOT)]
            block = st.enter_context(nc.Block())
            per = {e: [o for o in self.ops if o[0] == e] for e in self.ENGS}

            def mk(ename):
                def body(eng):
                    seen = {}

                    def wait(tok):
                        if tok[0] == "eng":
                            _, e2, n = tok
                            key = ("eng", e2)
                            if seen.get(key, 0) >= n:
                                return
                            seen[key] = n
                            eng.wait_ge(esem[e2], n)
                        else:
                            _, e2, q = tok
                            slot = q % NSLOT
                            val = 16 * (q // NSLOT + 1)
                            key = ("dma", e2, slot)
                            if seen.get(key, 0) >= val:
                                return
                            seen[key] = val
                            eng.wait_ge(dsem[e2][slot], val)
                    for (_, fn, deps, tok) in per[ename]:
                        for d in sorted(deps):
                            wait(d)
                        if tok[0] == "dma":
                            q = tok[2]
                            if q >= NSLOT:
                                wait(("dma", ename, q - NSLOT))
                            ins = fn(eng)
                            ins.then_inc(dsem[ename][q % NSLOT], 16)
                        else:
                            ins = fn(eng)
                            ins.then_inc(esem[ename], 1)
                    n = self.dma_cnt.get(ename, 0)
                    for q in range(max(0, n - NSLOT), n):
                        wait(("dma", ename, q))
                return body
            block.tensor(mk("pe"))
            block.scalar(mk("act"))
            block.vector(mk("dve"))
            block.gpsimd(mk("pool"))
            block.sync(mk("sp"))


NFM = 704
NTM = 256
FM_CH = [(0, 128), (128, 128), (256, 64), (320, 128), (448, 128), (576, 128)]


def phase_inproj(nc, S, st, hT, wcat, nw, fm, tm_sf, tm_v, T=SEQ):
    TS = lambda n, s, d: st.enter_context(nc.sbuf_tensor(n, s, d))
    PS = lambda n: st.enter_context(nc.psum_tensor(n, [128, 512], F32))
    nw_sb = TS("ip_nw", [128, 8], F32)
    wst = [TS(f"ip_wst{i}", [128, NFM + NTM], F32) for i in range(2)]
    wall = TS("ip_wall", [128, 8, NFM + NTM], BF16)
    ones = TS("ip_ones", [128, 128], BF16)
    xin = [TS(f"ip_xin{i}", [128, 8, 512], F32) for i in range(2)]
    xsq = TS("ip_xsq", [128, 8, 512], BF16)
    sq = TS("ip_sq", [128, 512], F32)
    rstd = TS("ip_rstd", [128, 512], F32)
    xn = [TS(f"ip_xn{i}", [128, 8, 512], BF16) for i in range(2)]
    fmo = [TS(f"ip_fmo{i}", [128, 512], BF16) for i in range(3)]
    tsf = [TS(f"ip_tsf{i}", [128, 4, 64], F32) for i in range(2)]
    tv = [TS(f"ip_tv{i}", [128, 4, 192], BF16) for i in range(2)]
    ps_ss = PS("ip_ps_ss")
    ps_fm = [PS(f"ip_ps_fm{i}") for i in range(2)]
    ps_tm = [PS(f"ip_ps_tm{i}") for i in range(2)]

    S.op("sp", lambda e: e.dma_start(out=nw_sb[:], in_=nw[:, :]), w=["nw"], dma=True)
    S.op("pool", lambda e: e.memset(ones[:], 1.0), w=["ones"])
    for k in range(8):
        S.op("sp", lambda e, k=k: e.dma_start(out=wst[k % 2][:], in_=wcat[k * 128:(k + 1) * 128, :]),
             w=[("wst", k % 2)], dma=True)
        S.op("dve", lambda e, k=k: e.tensor_scalar(out=wall[:, k, :], in0=wst[k % 2][:], scalar1=nw_sb[:, k:k + 1],
                                                  scalar2=None, op0=ALU.mult),
             r=[("wst", k % 2), "nw"], w=[("wall", k)])
    wall_r = [("wall", k) for k in range(8)]
    hT_v = hT.rearrange("(k p) t -> p k t", p=128)
    fmi = 0
    for ti in range(T // 512):
        b = ti % 2
        t0 = ti * 512
        S.op("sp", lambda e, b=b, t0=t0: e.dma_start(out=xin[b][:, 0:4, :], in_=hT_v[:, 0:4, t0:t0 + 512]),
             w=[("xin", b, 0)], dma=True)
        S.op("sp" if "NOPOOLDMA" in DBG else "pool", lambda e, b=b, t0=t0: e.dma_start(out=xin[b][:, 4:8, :], in_=hT_v[:, 4:8, t0:t0 + 512]),
             w=[("xin", b, 1)], dma=True)
        xr = [("xin", b, 0), ("xin", b, 1)]
        S.op("act", lambda e, b=b: e.activation(out=xsq[:], in_=xin[b][:], func=AF.Square), r=xr, w=["xsq"])
        for k in range(8):
            S.op("pe", lambda e, k=k: e.matmul(ps_ss[:], ones[:], xsq[:, k, :], start=(k == 0), stop=(k == 7)),
                 r=["xsq", "ones"], w=["ps_ss"])
        S.op("act", lambda e: e.activation(out=sq[:], in_=ps_ss[:], func=AF.Sqrt, scale=1.0 / D_MODEL, bias=EPS),
             r=["ps_ss"], w=["sq"])
        S.op("dve", lambda e: e.reciprocal(out=rstd[:], in_=sq[:]), r=["sq"], w=["rstd"])
        S.op("dve", lambda e, b=b: e.tensor_tensor(out=xn[b][:, 0:4, :], in0=xin[b][:, 0:4, :],
                                                  in1=rstd[:].unsqueeze(1).broadcast_to([128, 4, 512]), op=ALU.mult),
             r=[("xin", b, 0), "rstd"], w=[("xn", b, 0)])
        S.op("dve" if "NOPOOLTT" in DBG else "pool", lambda e, b=b: e.tensor_tensor(out=xn[b][:, 4:8, :], in0=xin[b][:, 4:8, :],
                                                   in1=rstd[:].unsqueeze(1).broadcast_to([128, 4, 512]), op=ALU.mult),
             r=[("xin", b, 1), "rstd"], w=[("xn", b, 1)])
        xnr = [("xn", b, 0), ("xn", b, 1)]
        for ci, (c0, cw) in enumerate(FM_CH):
            if "NOFM" in DBG or ("FM%d" % ci) in DBG:
                continue
            pb = ci % 2
            for k in range(8):
                S.op("pe", lambda e, k=k, c0=c0, cw=cw, pb=pb, b=b: e.matmul(
                    ps_fm[pb][0:cw, :], wall[:, k, c0:c0 + cw], xn[b][:, k, :], start=(k == 0), stop=(k == 7)),
                    r=xnr + wall_r, w=[("ps_fm", pb)])
            fb = fmi % 3
            fmi += 1
            if ci == 0:
                S.op("act", lambda e, pb=pb, fb=fb: e.activation(out=fmo[fb][0:64, :], in_=ps_fm[pb][0:64, :], func=AF.Silu),
                     r=[("ps_fm", pb)], w=[("fmo", fb, 0)])
                S.op("act", lambda e, pb=pb, fb=fb: e.activation(out=fmo[fb][64:128, :], in_=ps_fm[pb][64:128, :],
                                                               func=AF.Sigmoid, scale=-1.0),
                     r=[("ps_fm", pb)], w=[("fmo", fb, 1)])
                wl = [("fmo", fb, 0), ("fmo", fb, 1)]
            elif ci in (1, 5):
                S.op("act", lambda e, pb=pb, fb=fb: e.activation(out=fmo[fb][:], in_=ps_fm[pb][:], func=AF.Silu),
                     r=[("ps_fm", pb)], w=[("fmo", fb, 0), ("fmo", fb, 1)])
                wl = [("fmo", fb, 0), ("fmo", fb, 1)]
            else:
                S.op("dve", lambda e, pb=pb, fb=fb, cw=cw: e.tensor_copy(out=fmo[fb][0:cw, :], in_=ps_fm[pb][0:cw, :]),
                     r=[("ps_fm", pb)], w=[("fmo", fb, 0), ("fmo", fb, 1)])
                wl = [("fmo", fb, 0), ("fmo", fb, 1)]
            S.op("sp", lambda e, fb=fb, c0=c0, cw=cw, t0=t0: e.dma_start(out=fm[c0:c0 + cw, t0:t0 + 512], in_=fmo[fb][0:cw, :]),
                 r=wl, dma=True)
        for pb in range(0 if "NOTM" in DBG else 2):
            for t4 in (2 * pb, 2 * pb + 1):
                off = (t4 % 2) * 256
                for k in range(8):
                    S.op("pe", lambda e, k=k, t4=t4, pb=pb, off=off, b=b: e.matmul(
                        ps_tm[pb][:, off:off + 256], xn[b][:, k, t4 * 128:(t4 + 1) * 128], wall[:, k, NFM:NFM + NTM],
                        start=(k == 0), stop=(k == 7)),
                        r=xnr + wall_r, w=[("ps_tm", pb)])
            if "TMNOEVAC" in DBG:
                continue
            for t4 in (2 * pb, 2 * pb + 1):
                off = (t4 % 2) * 256
                if "TMNOACT" not in DBG:
                  S.op("act", lambda e, pb=pb, b=b, t4=t4, off=off: e.activation(
                    out=tsf[b][:, t4, :], in_=ps_tm[pb][:, off:off + 64], func=AF.Sigmoid),
                    w=[("ps_tm", pb), ("tsf", b, t4)])
                if "TMNODVE" not in DBG:
                  S.op("dve", lambda e, pb=pb, b=b, t4=t4, off=off: e.tensor_copy(
                    out=tv[b][:, t4, :], in_=ps_tm[pb][:, off + 64:off + 256]),
                    w=[("ps_tm", pb), ("tv", b, t4)])
        if "TMNODMA" in DBG:
            continue
        S.op("sp", lambda e, b=b, t0=t0: e.dma_start(
            out=tm_sf[t0:t0 + 512, :].rearrange("(a p) c -> p a c", p=128), in_=tsf[b][:]),
            r=[("tsf", b, t4) for t4 in range(4)], dma=True)
        S.op("sp", lambda e, b=b, t0=t0: e.dma_start(
            out=tm_v[t0:t0 + 512, :].rearrange("(a p) c -> p a c", p=128), in_=tv[b][:]),
            r=[("tv", b, t4) for t4 in range(4)], dma=True)


def core_cols(j):
    r = lambda s, n: list(range(s, s + n))
    fmc = (r(0 + 64 * j, 64) + r(256 + 64 * j, 64) + r(768 + 64 * j, 64) + r(1280 + 64 * j, 64) + r(1024 + 64 * j, 64)
           + r(1536 + 128 * j, 128) + r(2048 + 128 * j, 128) + r(3072 + 128 * j, 128))
    tmc = r(256 + 64 * j, 64) + r(512 + 64 * j, 64) + r(2560 + 128 * j, 128)
    return np.array(fmc + tmc)


def build_inproj(T=SEQ):
    nc = bass.Bass("TRN2", target_bir_lowering=False)
    hT = nc.dram_tensor("hT", [D_MODEL, T], F32, kind="ExternalInput").ap()
    wcat = nc.dram_tensor("wcat", [D_MODEL, NFM + NTM], F32, kind="ExternalInput").ap()
    nw = nc.dram_tensor("nw", [128, 8], F32, kind="ExternalInput").ap()
    fm = nc.dram_tensor("fm", [NFM, T], BF16, kind="ExternalOutput").ap()
    tm_sf = nc.dram_tensor("tm_sf", [T, 64], F32, kind="ExternalOutput").ap()
    tm_v = nc.dram_tensor("tm_v", [T, 192], BF16, kind="ExternalOutput").ap()
    with contextlib.ExitStack() as st:
        S = Sched(nc)
        phase_inproj(nc, S, st, hT, wcat, nw, fm, tm_sf, tm_v, T)
        S.emit()
    return nc


C1_2PI = 6.28125
C2_2PI = 2.0 * math.pi - 6.28125


def rope_tables(nc, S, st, ropef, sinT, cosT, T, tag):
    TS = lambda n, s, d: st.enter_context(nc.sbuf_tensor(n, s, d))
    CH = min(2048, T)
    pi_ = TS(tag + "_pi", [128, CH], I32)
    ang = TS(tag + "_ang", [128, CH], F32)
    kf = TS(tag + "_kf", [128, CH], F32)
    ki = TS(tag + "_ki", [128, CH], I32)
    hs = TS(tag + "_hs", [128, CH], F32)
    for c in range(T // CH):
        S.op("pool", lambda e, c=c: e.iota(pi_[:], pattern=[[1, CH]], base=c * CH, channel_multiplier=0), w=[tag + "pi"])
        S.op("dve", lambda e: e.tensor_copy(out=ang[:], in_=pi_[:]), r=[tag + "pi"], w=[tag + "ang"])
        S.op("dve", lambda e: e.tensor_scalar(out=ang[:], in0=ang[:], scalar1=ropef[:, 0:1], scalar2=None, op0=ALU.mult),
             r=["ropef"], w=[tag + "ang"])
        S.op("dve", lambda e: e.tensor_scalar(out=kf[:], in0=ang[:], scalar1=1.0 / (2.0 * math.pi), scalar2=None, op0=ALU.mult),
             r=[tag + "ang"], w=[tag + "kf"])
        S.op("dve", lambda e: e.tensor_copy(out=ki[:], in_=kf[:]), r=[tag + "kf"], w=[tag + "ki"])
        S.op("dve", lambda e: e.tensor_copy(out=kf[:], in_=ki[:]), r=[tag + "ki"], w=[tag + "kf"])
        S.op("dve", lambda e: e.scalar_tensor_tensor(out=ang[:], in0=kf[:], scalar=-C1_2PI, in1=ang[:], op0=ALU.mult, op1=ALU.add),
             r=[tag + "kf"], w=[tag + "ang"])
        S.op("dve", lambda e: e.scalar_tensor_tensor(out=ang[:], in0=kf[:], scalar=-C2_2PI, in1=ang[:], op0=ALU.mult, op1=ALU.add),
             r=[tag + "kf"], w=[tag + "ang"])
        S.op("dve", lambda e: e.tensor_scalar(out=ang[:], in0=ang[:], scalar1=math.pi, scalar2=-math.pi, op0=ALU.min, op1=ALU.max),
             w=[tag + "ang"])
        S.op("act", lambda e, c=c: e.activation(out=sinT[:, c * CH:(c + 1) * CH], in_=ang[:], func=AF.Sin),
             r=[tag + "ang"], w=[(tag + "sin", c)])
        S.op("act", lambda e: e.activation(out=hs[:], in_=ang[:], func=AF.Sin, scale=0.5), r=[tag + "ang"], w=[tag + "hs"])
        S.op("pool", lambda e: e.tensor_tensor(out=hs[:], in0=hs[:], in1=hs[:], op=ALU.mult), w=[tag + "hs"])
        S.op("pool", lambda e, c=c: e.tensor_scalar(out=cosT[:, c * CH:(c + 1) * CH], in0=hs[:], scalar1=-2.0, scalar2=1.0,
                                                   op0=ALU.mult, op1=ALU.add), r=[tag + "hs"], w=[(tag + "cos", c)])
    return [(tag + "sin", c) for c in range(T // CH)] + [(tag + "cos", c) for c in range(T // CH)]


def phase_attn(nc, S, st, fm, tm_v, lqk, subln, ropef_d, rmat_d, cmask_d, oa, lambda_init, T=SEQ):
    TS = lambda n, s, d: st.enter_context(nc.sbuf_tensor(n, s, d))
    PS = lambda n: st.enter_context(nc.psum_tensor(n, [128, 512], F32))
    NQ = T // 512
    NK = T // 128
    ropef = TS("at_ropef", [128, 1], F32)
    rm32 = TS("at_rm32", [128, 128], F32)
    rm = TS("at_rm", [128, 128], BF16)
    cmask = TS("at_cmask", [128, 4, 512], BF16)
    ones = TS("at_ones", [128, 128], BF16)
    sinT = TS("at_sin", [128, T], BF16)
    cosT = TS("at_cos", [128, T], BF16)
    qraw = TS("at_qraw", [128, T], BF16)
    kraw = TS("at_kraw", [128, T], BF16)
    qr = TS("at_qr", [128, T], BF16)
    kr = TS("at_kr", [128, T], BF16)
    sag = TS("at_sag", [128, T], BF16)
    vsb = TS("at_v", [128, NK, 128], BF16)
    lq = TS("at_lq", [128, 256], F32)
    lp = TS("at_lp", [128, 128], F32)
    le = TS("at_le", [128, 2], F32)
    neglam = TS("at_neglam", [128, 1], F32)
    sw = TS("at_sw", [128, 1], F32)
    t1 = TS("at_t1", [128, 512], F32)
    t2 = TS("at_t2", [128, 512], F32)
    P = [[TS(f"at_P{i}_{m}", [128, 512], BF16) for m in range(2)] for i in range(3)]
    r0 = TS("at_r0", [128, 512], F32)
    r1 = TS("at_r1", [128, 512], F32)
    o0 = TS("at_o0", [128, 512], F32)
    o1 = TS("at_o1", [128, 512], F32)
    osq = TS("at_osq", [128, 512], BF16)
    nsq = TS("at_nsq", [128, 512], F32)
    ob = [TS(f"at_ob{i}", [128, 512], BF16) for i in range(2)]
    ps_s = [[PS(f"at_ps_s{i}_{m}") for m in range(2)] for i in range(2)]
    ps_o = [PS(f"at_ps_o{m}") for m in range(2)]
    ps_l = [PS(f"at_ps_l{m}") for m in range(2)]

    S.op("sp", lambda e: e.dma_start(out=ropef[:], in_=ropef_d[:, :]), w=["ropef"], dma=True)
    S.op("sp", lambda e: e.dma_start(out=rm32[:], in_=rmat_d[:, :]), w=["rm32"], dma=True)
    S.op("sp", lambda e: e.dma_start(out=cmask[:], in_=cmask_d.rearrange("d p q -> p d q")), w=["cmask"], dma=True)
    S.op("sp", lambda e: e.dma_start(out=lq[:], in_=lqk.partition_broadcast(128)), w=["lq"], dma=True)
    S.op("sp", lambda e: e.dma_start(out=sw[:], in_=subln[:, :]), w=["sw"], dma=True)
    S.op("sp", lambda e: e.dma_start(out=qraw[:], in_=fm[320:448, :]), w=["qraw"], dma=True)
    S.op("sp", lambda e: e.dma_start(out=kraw[:], in_=fm[448:576, :]), w=["kraw"], dma=True)
    S.op("sp", lambda e: e.dma_start(out=sag[:], in_=fm[576:704, :]), w=["sag"], dma=True)
    S.op("pool", lambda e: e.dma_start(out=vsb[:], in_=tm_v[:, 64:192].rearrange("(a p) c -> p a c", p=128)), w=["vsb"], dma=True)
    S.op("pool", lambda e: e.memset(ones[:], 1.0), w=["ones"])
    S.op("dve", lambda e: e.tensor_copy(out=rm[:], in_=rm32[:]), r=["rm32"], w=["rm"])
    S.op("dve", lambda e: e.tensor_tensor(out=lp[:], in0=lq[:, 0:128], in1=lq[:, 128:256], op=ALU.mult), r=["lq"], w=["lp"])
    S.op("dve", lambda e: e.tensor_reduce(out=le[:], in_=lp[:].rearrange("p (a c) -> p a c", a=2), axis=AX.X, op=ALU.add),
         r=["lp"], w=["le"])
    S.op("act", lambda e: e.activation(out=le[:], in_=le[:], func=AF.Exp), w=["le"])
    S.op("dve", lambda e: e.tensor_tensor(out=neglam[:], in0=le[:, 1:2], in1=le[:, 0:1], op=ALU.subtract), r=["le"], w=["neglam"])
    S.op("dve", lambda e: e.tensor_scalar(out=neglam[:], in0=neglam[:], scalar1=-lambda_init, scalar2=None, op0=ALU.add), w=["neglam"])
    S.op("dve", lambda e: e.tensor_scalar(out=sw[:], in0=sw[:], scalar1=1.0 - lambda_init, scalar2=None, op0=ALU.mult), w=["sw"])
    tabs = rope_tables(nc, S, st, ropef, sinT, cosT, T, "at_rp")
    for src, dst, sn, dn in ((qraw, qr, "qraw", "qr"), (kraw, kr, "kraw", "kr")):
        for ti in range(NQ):
            sl = slice(ti * 512, (ti + 1) * 512)
            S.op("pe", lambda e, src=src, sl=sl: e.matmul(ps_s[0][0][:], rm[:], src[:, sl], start=True, stop=True),
                 r=[sn, "rm"], w=["ps_s00"])
            S.op("dve", lambda e, src=src, sl=sl: e.tensor_tensor(out=t1[:], in0=src[:, sl], in1=cosT[:, sl], op=ALU.mult),
                 r=[sn] + tabs, w=["t1"])
            S.op("dve", lambda e, sl=sl: e.tensor_tensor(out=t2[:], in0=ps_s[0][0][:], in1=sinT[:, sl], op=ALU.mult),
                 r=tabs, w=["t2", "ps_s00"])
            S.op("pool", lambda e, dst=dst, sl=sl: e.tensor_tensor(out=dst[:, sl], in0=t1[:], in1=t2[:], op=ALU.add),
                 r=["t1", "t2"], w=[(dn, ti)])
    qr_all = [("qr", ti) for ti in range(NQ)]
    kr_all = [("kr", ti) for ti in range(NQ)]
    psn = lambda i, m: f"ps_s{i}{m}"
    for qi in range(NQ):
        qs = slice(qi * 512, (qi + 1) * 512)
        nk = 4 * (qi + 1)

        def QK(n):
            i = n % 2
            for m in range(2):
                S.op("pe", lambda e, n=n, i=i, m=m, qs=qs: e.matmul(ps_s[i][m][:], kr[64 * m:64 * m + 64, n * 128:(n + 1) * 128],
                                                         qr[64 * m:64 * m + 64, qs], start=True, stop=True),
                     r=[("qr", qi), ("kr", n // 4)], w=[psn(i, m)])

        def EXP(n):
            i = n % 2
            j = n % 3
            for m in range(2):
                S.op("act", lambda e, i=i, j=j, m=m: e.activation(out=P[j][m][:], in_=ps_s[i][m][:], func=AF.Exp, scale=0.125),
                     w=[psn(i, m), ("P", j, m)])
                d = n - 4 * qi
                if d >= 0:
                    S.op("dve", lambda e, j=j, m=m, d=d: e.tensor_tensor(out=P[j][m][:], in0=P[j][m][:], in1=cmask[:, d, :], op=ALU.mult),
                         r=["cmask"], w=[("P", j, m)])

        def PV(n):
            j = n % 3
            for m in range(2):
                S.op("pe", lambda e, n=n, j=j, m=m, nk=nk: e.matmul(ps_o[m][:], vsb[:, n, :], P[j][m][:], start=(n == 0), stop=(n == nk - 1)),
                     r=[("P", j, m), "vsb"], w=[f"ps_o{m}"])
                S.op("pe", lambda e, n=n, j=j, m=m, nk=nk: e.matmul(ps_l[m][:], ones[:], P[j][m][:], start=(n == 0), stop=(n == nk - 1)),
                     r=[("P", j, m), "ones"], w=[f"ps_l{m}"])
        QK(0)
        for n in range(nk):
            if n + 1 < nk:
                QK(n + 1)
            EXP(n)
            PV(n)
        S.op("dve", lambda e: e.reciprocal(out=r0[:], in_=ps_l[0][:]), w=["r0", "ps_l0"])
        S.op("dve", lambda e: e.reciprocal(out=r1[:], in_=ps_l[1][:]), w=["r1", "ps_l1"])
        S.op("dve", lambda e: e.tensor_tensor(out=o0[:], in0=ps_o[0][:], in1=r0[:], op=ALU.mult), r=["r0"], w=["o0", "ps_o0"])
        S.op("dve", lambda e: e.tensor_tensor(out=o1[:], in0=ps_o[1][:], in1=r1[:], op=ALU.mult), r=["r1"], w=["o1", "ps_o1"])
        S.op("dve", lambda e: e.scalar_tensor_tensor(out=o0[:], in0=o1[:], scalar=neglam[:, 0:1], in1=o0[:], op0=ALU.mult, op1=ALU.add),
             r=["o1", "neglam"], w=["o0"])
        S.op("act", lambda e: e.activation(out=osq[:], in_=o0[:], func=AF.Square), r=["o0"], w=["osq"])
        S.op("pe", lambda e: e.matmul(ps_s[0][0][:], ones[:], osq[:], start=True, stop=True), r=["osq", "ones"], w=[psn(0, 0)])
        S.op("act", lambda e: e.activation(out=nsq[:], in_=ps_s[0][0][:], func=AF.Sqrt, scale=1.0 / 128.0, bias=EPS),
             w=["nsq", psn(0, 0)])
        S.op("dve", lambda e: e.reciprocal(out=nsq[:], in_=nsq[:]), w=["nsq"])
        S.op("dve", lambda e: e.scalar_tensor_tensor(out=o0[:], in0=o0[:], scalar=sw[:, 0:1], in1=nsq[:], op0=ALU.mult, op1=ALU.mult),
             r=["nsq", "sw"], w=["o0"])
        S.op("pool", lambda e, qi=qi, qs=qs: e.tensor_tensor(out=ob[qi % 2][:], in0=o0[:], in1=sag[:, qs], op=ALU.mult),
             r=["o0", "sag"], w=[("ob", qi % 2)])
        S.op("sp", lambda e, qi=qi, qs=qs: e.dma_start(out=oa[:, qs], in_=ob[qi % 2][:]), r=[("ob", qi % 2)], dma=True)


def attn_consts():
    ropef = np.zeros((128, 1), np.float32)
    inv = (ROPE_THETA ** (-np.arange(0, 16, 2, dtype=np.float32) / 16.0)).astype(np.float32)
    rmat = np.zeros((128, 128), np.float32)
    for base in (0, 64):
        for i in range(8):
            ropef[base + i, 0] = -inv[i]
            ropef[base + 8 + i, 0] = inv[i]
            rmat[base + 8 + i, base + i] = 1.0
            rmat[base + i, base + 8 + i] = 1.0
    k = np.arange(128)[:, None]
    q = np.arange(512)[None, :]
    cmask = np.stack([(128 * d + k <= q) for d in range(4)]).astype(ml_dtypes.bfloat16)
    return ropef, rmat, cmask


def build_attn(lambda_init, T=SEQ):
    nc = bass.Bass("TRN2", target_bir_lowering=False)
    fm = nc.dram_tensor("fm", [NFM, T], BF16, kind="ExternalInput").ap()
    tm_v = nc.dram_tensor("tm_v", [T, 192], BF16, kind="ExternalInput").ap()
    lqk = nc.dram_tensor("lqk", [1, 256], F32, kind="ExternalInput").ap()
    subln = nc.dram_tensor("subln", [128, 1], F32, kind="ExternalInput").ap()
    ropef = nc.dram_tensor("ropef", [128, 1], F32, kind="ExternalInput").ap()
    rmat = nc.dram_tensor("rmat", [128, 128], F32, kind="ExternalInput").ap()
    cmask = nc.dram_tensor("cmask", [4, 128, 512], BF16, kind="ExternalInput").ap()
    oa = nc.dram_tensor("oa", [128, T], BF16, kind="ExternalOutput").ap()
    with contextlib.ExitStack() as st:
        S = Sched(nc)
        phase_attn(nc, S, st, fm, tm_v, lqk, subln, ropef, rmat, cmask, oa, lambda_init, T)
        S.emit()
    return nc


def hgrn_consts():
    s = np.arange(128)[:, None]
    t = np.arange(128)[None, :]
    same = (s // 16) == (t // 16)
    m_incl = (same & (s <= t)).astype(np.float32)
    m_rev = (same & (s > t)).astype(np.float32)
    m_tot8 = ((s // 16) == np.arange(8)[None, :]).astype(np.float32)
    mcat = np.concatenate([m_incl, m_tot8], axis=1)
    return mcat, m_rev


def phase_hgrn(nc, S, st, fm, tm_sf, tm_v, lbl_bc_d, lbl_col_d, gw_d, mcat_d, mrev_d, oh, lb_coef, T=SEQ):
    TS = lambda n, s, d: st.enter_context(nc.sbuf_tensor(n, s, d))
    PS = lambda n: st.enter_context(nc.psum_tensor(n, [128, 512], F32))
    NT = T // 128
    sq = TS("hg_sq", [64, T], BF16)
    snf = TS("hg_snf", [64, T], BF16)
    shg = TS("hg_shg", [64, T], BF16)
    sf = TS("hg_sf", [128, NT, 64], F32)
    omf = TS("hg_omf", [128, NT, 64], F32)
    logf = TS("hg_logf", [128, NT, 64], F32)
    vall = TS("hg_v", [128, NT, 64], BF16)
    lbl = TS("hg_lbl", [128, 128], F32)
    lb_bc = TS("hg_lb_bc", [128, 64], F32)
    oml_bc = TS("hg_oml_bc", [128, 64], F32)
    lblc = TS("hg_lblc", [64, 2], F32)
    oml_col = TS("hg_oml_col", [64, 1], F32)
    gw = TS("hg_gw", [64, 1], F32)
    mcat = TS("hg_mcat", [128, 136], F32)
    mrev = TS("hg_mrev", [128, 128], F32)
    mincl_bf = TS("hg_mincl", [128, 128], BF16)
    mtot_bf = TS("hg_mtot", [128, 8], BF16)
    ones64 = TS("hg_ones", [64, 64], BF16)
    eq = TS("hg_eq", [64, 128], F32)
    ekn = TS("hg_ekn", [64, 128], F32)
    dec = [TS(f"hg_dec{i}", [64, 8], F32) for i in range(2)]
    ehat = TS("hg_ehat", [128, 64], F32)
    qt = [TS(f"hg_qt{i}", [64, 128], BF16) for i in range(2)]
    kt = TS("hg_kt", [64, 128], BF16)
    khat = TS("hg_khat", [128, 64], BF16)
    vblk = TS("hg_vblk", [128, 8, 64], BF16)
    scm = TS("hg_scm", [128, 128], BF16)
    Sall = [TS(f"hg_S{i}", [64, 9, 64], F32) for i in range(2)]
    Sbf = [TS(f"hg_Sbf{i}", [64, 8, 64], BF16) for i in range(2)]
    osq = TS("hg_osq", [64, 512], BF16)
    o32 = TS("hg_o32", [64, 512], F32)
    nsq = TS("hg_nsq", [64, 512], F32)
    ohb = [TS(f"hg_ohb{i}", [64, 512], BF16) for i in range(2)]
    ps_c = PS("hg_ps_c")
    ps_r = PS("hg_ps_r")
    ps_sc = PS("hg_ps_sc")
    ps_u = [PS(f"hg_ps_u{i}") for i in range(2)]
    ps_oh = [PS(f"hg_ps_oh{i}") for i in range(2)]
    ps_n = PS("hg_ps_n")

    S.op("sp", lambda e: e.dma_start(out=sq[:], in_=fm[0:64, :]), w=["sq"], dma=True)
    S.op("sp", lambda e: e.dma_start(out=snf[:], in_=fm[64:128, :]), w=["snf"], dma=True)
    S.op("sp", lambda e: e.dma_start(out=shg[:], in_=fm[128:192, :]), w=["shg"], dma=True)
    S.op("sp", lambda e: e.dma_start(out=sf[:], in_=tm_sf.rearrange("(a p) c -> p a c", p=128)), w=["sf"], dma=True)
    S.op("pool", lambda e: e.dma_start(out=vall[:], in_=tm_v[:, 0:64].rearrange("(a p) c -> p a c", p=128)), w=["vall"], dma=True)
    S.op("sp", lambda e: e.dma_start(out=lbl[:], in_=lbl_bc_d.partition_broadcast(128)), w=["lbl"], dma=True)
    S.op("sp", lambda e: e.dma_start(out=lblc[:], in_=lbl_col_d[:, :]), w=["lblc"], dma=True)
    S.op("sp", lambda e: e.dma_start(out=gw[:], in_=gw_d[:, :]), w=["gw"], dma=True)
    S.op("sp", lambda e: e.dma_start(out=mcat[:], in_=mcat_d[:, :]), w=["mcat"], dma=True)
    S.op("sp", lambda e: e.dma_start(out=mrev[:], in_=mrev_d[:, :]), w=["mrev"], dma=True)
    S.op("pool", lambda e: e.memset(ones64[:], 1.0), w=["ones64"])
    S.op("pool", lambda e: e.memset(Sall[0][:, 0, :], 0.0), w=[("S", 0)])
    S.op("dve", lambda e: e.tensor_copy(out=mincl_bf[:], in_=mcat[:, 0:128]), r=["mcat"], w=["mincl_bf"])
    S.op("dve", lambda e: e.tensor_copy(out=mtot_bf[:], in_=mcat[:, 128:136]), r=["mcat"], w=["mtot_bf"])
    S.op("dve", lambda e: e.tensor_tensor(out=lb_bc[:], in0=lbl[:, 64:128], in1=lbl[:, 0:64], op=ALU.subtract), r=["lbl"], w=["lb_bc"])
    S.op("act", lambda e: e.activation(out=lb_bc[:], in_=lb_bc[:], func=AF.Sigmoid), w=["lb_bc"])
    S.op("dve", lambda e: e.tensor_scalar(out=lb_bc[:], in0=lb_bc[:], scalar1=float(lb_coef), scalar2=None, op0=ALU.mult), w=["lb_bc"])
    S.op("dve", lambda e: e.tensor_scalar(out=oml_bc[:], in0=lb_bc[:], scalar1=-1.0, scalar2=1.0, op0=ALU.mult, op1=ALU.add),
         r=["lb_bc"], w=["oml_bc"])
    S.op("dve", lambda e: e.tensor_tensor(out=oml_col[:], in0=lblc[:, 1:2], in1=lblc[:, 0:1], op=ALU.subtract), r=["lblc"], w=["oml_col"])
    S.op("act", lambda e: e.activation(out=oml_col[:], in_=oml_col[:], func=AF.Sigmoid), w=["oml_col"])
    S.op("dve", lambda e: e.tensor_scalar(out=oml_col[:], in0=oml_col[:], scalar1=-float(lb_coef), scalar2=1.0, op0=ALU.mult, op1=ALU.add),
         w=["oml_col"])
    for g in range(NT // 8 if NT >= 8 else 1):
        nt = min(8, NT)
        sl = slice(g * 8, g * 8 + nt)
        S.op("dve", lambda e, sl=sl, nt=nt: e.tensor_tensor(out=sf[:, sl, :], in0=sf[:, sl, :],
                                                        in1=oml_bc[:].unsqueeze(1).broadcast_to([128, nt, 64]), op=ALU.mult),
             r=["oml_bc"], w=["sf"])
        S.op("dve", lambda e, sl=sl, nt=nt: e.tensor_tensor(out=sf[:, sl, :], in0=sf[:, sl, :],
                                                        in1=lb_bc[:].unsqueeze(1).broadcast_to([128, nt, 64]), op=ALU.add),
             r=["lb_bc"], w=["sf"])
        S.op("act", lambda e, sl=sl: e.activation(out=logf[:, sl, :], in_=sf[:, sl, :], func=AF.Ln), r=["sf"], w=[("logf", g)])
        S.op("pool", lambda e, sl=sl: e.tensor_scalar(out=omf[:, sl, :], in0=sf[:, sl, :], scalar1=-1.0, scalar2=1.0,
                                                     op0=ALU.mult, op1=ALU.add), r=["sf"], w=[("omf", g)])
    for i in range(NT):
        g = i // 8
        b = i % 2
        ts = slice(i * 128, (i + 1) * 128)
        ob = (i // 4) % 2
        S.op("pe", lambda e, i=i: e.matmul(ps_c[0:64, 0:136], logf[:, i, :], mcat[:], start=True, stop=True),
             r=[("logf", g), "mcat"], w=["ps_c"])
        S.op("pe", lambda e, i=i: e.matmul(ps_r[:, 0:64], mrev[:], logf[:, i, :], start=True, stop=True),
             r=[("logf", g), "mrev"], w=["ps_r"])
        S.op("act", lambda e: e.activation(out=eq[:], in_=ps_c[0:64, 0:128], func=AF.Exp), w=["eq", "ps_c"])
        S.op("act", lambda e: e.activation(out=ekn[:], in_=ps_c[0:64, 0:128], func=AF.Exp, scale=-1.0), w=["ekn", "ps_c"])
        S.op("act", lambda e, b=b: e.activation(out=dec[b][:], in_=ps_c[0:64, 128:136], func=AF.Exp), w=[("dec", b), "ps_c"])
        S.op("act", lambda e: e.activation(out=ehat[:], in_=ps_r[:, 0:64], func=AF.Exp), w=["ehat", "ps_r"])
        S.op("dve", lambda e, b=b, ts=ts: e.tensor_tensor(out=qt[b][:], in0=sq[:, ts], in1=eq[:], op=ALU.mult),
             r=["sq", "eq"], w=[("qt", b)])
        S.op("dve", lambda e, ts=ts: e.scalar_tensor_tensor(out=kt[:], in0=snf[:, ts], scalar=oml_col[:, 0:1], in1=ekn[:],
                                                           op0=ALU.mult, op1=ALU.mult), r=["snf", "ekn", "oml_col"], w=["kt"])
        S.op("dve", lambda e, i=i: e.tensor_tensor(out=khat[:], in0=omf[:, i, :], in1=ehat[:], op=ALU.mult),
             r=[("omf", g), "ehat"], w=["khat"])
        S.op("pool", lambda e, i=i: e.tensor_tensor(out=vblk[:], in0=vall[:, i, :].unsqueeze(1).broadcast_to([128, 8, 64]),
                                                   in1=mtot_bf[:].unsqueeze(2).broadcast_to([128, 8, 64]), op=ALU.mult),
             r=["vall", "mtot_bf"], w=["vblk"])
        S.op("pe", lambda e, b=b: e.matmul(ps_sc[:, 0:128], kt[:], qt[b][:], start=True, stop=True), r=["kt", ("qt", b)], w=["ps_sc"])
        S.op("dve", lambda e: e.tensor_tensor(out=scm[:], in0=ps_sc[:, 0:128], in1=mincl_bf[:], op=ALU.mult),
             r=["mincl_bf"], w=["scm", "ps_sc"])
        S.op("pe", lambda e, b=b: e.matmul(ps_u[b][0:64, :], khat[:], vblk[:].rearrange("p a c -> p (a c)"), start=True, stop=True),
             r=["khat", "vblk"], w=[("ps_u", b)])
        if i > 0:
            S.op("dve", lambda e, b=b: e.tensor_copy(out=Sall[b][:, 0, :], in_=Sall[1 - b][:, 8, :]), r=[("S", 1 - b)], w=[("S", b)])
        for n in range(8):
            S.op("dve", lambda e, b=b, n=n: e.scalar_tensor_tensor(
                out=Sall[b][:, n + 1, :], in0=Sall[b][:, n, :], scalar=dec[b][:, n:n + 1], in1=ps_u[b][0:64, n * 64:(n + 1) * 64],
                op0=ALU.mult, op1=ALU.add), r=[("dec", b)], w=[("S", b), ("ps_u", b)])
        S.op("act", lambda e, b=b: e.activation(out=Sbf[b][:], in_=Sall[b][:, 0:8, :], func=AF.Copy), r=[("S", b)], w=[("Sbf", b)])
        c0 = (i % 4) * 128
        S.op("pe", lambda e, i=i, ob=ob, c0=c0: e.matmul(ps_oh[ob][0:64, c0:c0 + 128], vall[:, i, :], scm[:], start=True, stop=False),
             r=["vall", "scm"], w=[("ps_oh", ob)])
        for n in range(8):
            S.op("pe", lambda e, b=b, ob=ob, c0=c0, n=n: e.matmul(
                ps_oh[ob][0:64, c0 + 16 * n:c0 + 16 * n + 16], Sbf[b][:, n, :], qt[b][:, 16 * n:16 * n + 16],
                start=False, stop=(n == 7)), r=[("Sbf", b), ("qt", b)], w=[("ps_oh", ob)])
        if i % 4 == 3 or i == NT - 1:
            qs = slice((i // 4) * 512, (i // 4) * 512 + 512)
            S.op("act", lambda e, ob=ob: e.activation(out=osq[:], in_=ps_oh[ob][0:64, :], func=AF.Square), w=["osq", ("ps_oh", ob)])
            S.op("act", lambda e, ob=ob: e.activation(out=o32[:], in_=ps_oh[ob][0:64, :], func=AF.Copy), w=["o32", ("ps_oh", ob)])
            S.op("pe", lambda e: e.matmul(ps_n[0:64, :], ones64[:], osq[:], start=True, stop=True), r=["osq", "ones64"], w=["ps_n"])
            S.op("act", lambda e: e.activation(out=nsq[:], in_=ps_n[0:64, :], func=AF.Sqrt, scale=1.0 / 64.0, bias=EPS), w=["nsq", "ps_n"])
            S.op("dve", lambda e: e.reciprocal(out=nsq[:], in_=nsq[:]), w=["nsq"])
            S.op("dve", lambda e: e.scalar_tensor_tensor(out=o32[:], in0=o32[:], scalar=gw[:, 0:1], in1=nsq[:], op0=ALU.mult, op1=ALU.mult),
                 r=["nsq", "gw"], w=["o32"])
            S.op("pool", lambda e, ob=ob, qs=qs: e.tensor_tensor(out=ohb[ob][:], in0=o32[:], in1=shg[:, qs], op=ALU.mult),
                 r=["o32", "shg"], w=[("ohb", ob)])
            S.op("sp", lambda e, ob=ob, qs=qs: e.dma_start(out=oh[:, qs], in_=ohb[ob][:]), r=[("ohb", ob)], dma=True)


def build_hgrn(lb_coef, T=SEQ):
    nc = bass.Bass("TRN2", target_bir_lowering=False)
    fm = nc.dram_tensor("fm", [NFM, T], BF16, kind="ExternalInput").ap()
    tm_sf = nc.dram_tensor("tm_sf", [T, 64], F32, kind="ExternalInput").ap()
    tm_v = nc.dram_tensor("tm_v", [T, 192], BF16, kind="ExternalInput").ap()
    lbl_bc = nc.dram_tensor("lbl_bc", [1, 128], F32, kind="ExternalInput").ap()
    lbl_col = nc.dram_tensor("lbl_col", [64, 2], F32, kind="ExternalInput").ap()
    gw = nc.dram_tensor("gw", [64, 1], F32, kind="ExternalInput").ap()
    mcat = nc.dram_tensor("mcat", [128, 136], F32, kind="ExternalInput").ap()
    mrev = nc.dram_tensor("mrev", [128, 128], F32, kind="ExternalInput").ap()
    oh = nc.dram_tensor("oh", [64, T], BF16, kind="ExternalOutput").ap()
    with contextlib.ExitStack() as st:
        S = Sched(nc)
        phase_hgrn(nc, S, st, fm, tm_sf, tm_v, lbl_bc, lbl_col, gw, mcat, mrev, oh, lb_coef, T)
        S.emit()
    return nc


def sincos(S, ang, kf, ki, hs, sin_out, cos_out, tag, eng="dve"):
    a, k, h = tag + "ang", tag + "kf", tag + "hs"
    S.op(eng, lambda e: e.tensor_scalar(out=kf, in0=ang, scalar1=1.0 / (2.0 * math.pi), scalar2=None, op0=ALU.mult), r=[a], w=[k])
    S.op(eng, lambda e: e.tensor_copy(out=ki, in_=kf), r=[k], w=[tag + "ki"])
    S.op(eng, lambda e: e.tensor_copy(out=kf, in_=ki), r=[tag + "ki"], w=[k])
    S.op("dve", lambda e: e.scalar_tensor_tensor(out=ang, in0=kf, scalar=-C1_2PI, in1=ang, op0=ALU.mult, op1=ALU.add), r=[k], w=[a])
    S.op("dve", lambda e: e.scalar_tensor_tensor(out=ang, in0=kf, scalar=-C2_2PI, in1=ang, op0=ALU.mult, op1=ALU.add), r=[k], w=[a])
    S.op(eng, lambda e: e.tensor_scalar(out=ang, in0=ang, scalar1=math.pi, scalar2=-math.pi, op0=ALU.min, op1=ALU.max), w=[a])
    S.op("act", lambda e: e.activation(out=sin_out, in_=ang, func=AF.Sin), r=[a], w=[tag + "sin"])
    S.op("act", lambda e: e.activation(out=hs, in_=ang, func=AF.Sin, scale=0.5), r=[a], w=[h])
    S.op(eng, lambda e: e.tensor_tensor(out=hs, in0=hs, in1=hs, op=ALU.mult), w=[h])
    S.op(eng, lambda e: e.tensor_scalar(out=cos_out, in0=hs, scalar1=-2.0, scalar2=1.0, op0=ALU.mult, op1=ALU.add), r=[h], w=[tag + "cos"])


def cmul(S, eng, o_re, o_im, a_re, a_im, b_re, b_im, t0, t1, rd, wr, conj_a=False):
    sg = -1.0 if conj_a else 1.0
    S.op(eng, lambda e: e.tensor_tensor(out=t0, in0=a_im, in1=b_im, op=ALU.mult), r=rd, w=[wr + "t0"])
    S.op(eng, lambda e: e.tensor_tensor(out=t1, in0=a_re, in1=b_re, op=ALU.mult), r=rd, w=[wr + "t1"])
    S.op("dve", lambda e: e.scalar_tensor_tensor(out=o_re, in0=t0, scalar=-sg, in1=t1, op0=ALU.mult, op1=ALU.add),
         r=[wr + "t0", wr + "t1"], w=[wr + "re"])
    S.op(eng, lambda e: e.tensor_tensor(out=t0, in0=a_im, in1=b_re, op=ALU.mult), r=rd + [wr + "re"], w=[wr + "t0"])
    S.op(eng, lambda e: e.tensor_tensor(out=t1, in0=a_re, in1=b_im, op=ALU.mult), r=rd + [wr + "re"], w=[wr + "t1"])
    S.op("dve", lambda e: e.scalar_tensor_tensor(out=o_im, in0=t0, scalar=sg, in1=t1, op0=ALU.mult, op1=ALU.add),
         r=[wr + "t0", wr + "t1"], w=[wr + "im"])


def s5_consts():
    negsig = np.repeat(-np.arange(16, dtype=np.float32), 64)[None, :]
    kidx = np.arange(32, dtype=np.float32)[None, :]
    midx = np.arange(1, 513, dtype=np.float32)[None, :]
    rowmask = (np.arange(64)[:, None] // 16 == np.arange(4)[None, :]).astype(np.float32)
    return negsig, kidx, midx, rowmask


def s5_params(z, l, j):
    gs = [4 * j + gl for gl in range(4)]
    f = np.float32
    pA_are = np.concatenate([np.repeat(z["s5_a_re"][l][g][None, :], 16, 0) for g in gs]).astype(f)
    pA_aim = np.concatenate([np.repeat(z["s5_a_im"][l][g][None, :], 16, 0) for g in gs]).astype(f)
    pA_ldt = np.concatenate([np.full((16, 1), z["s5_log_dt"][l][g]) for g in gs]).astype(f)
    pA_bre = np.concatenate([z["s5_b_re"][l][g].T for g in gs]).astype(f)
    pA_bim = np.concatenate([z["s5_b_im"][l][g].T for g in gs]).astype(f)
    pB = np.zeros((2, 128, 3), f)
    pB_cre = np.zeros((2, 128, 64), f)
    pB_cim = np.zeros((2, 128, 64), f)
    for q in range(2):
        for h in range(2):
            gl = 2 * q + h
            g = gs[gl]
            rows = slice(64 * h, 64 * h + 64)
            pB[q, rows, 0] = z["s5_a_re"][l][g]
            pB[q, rows, 1] = z["s5_a_im"][l][g]
            pB[q, rows, 2] = z["s5_log_dt"][l][g]
            pB_cre[q, rows, 16 * gl:16 * gl + 16] = z["s5_c_re"][l][g].T
            pB_cim[q, rows, 16 * gl:16 * gl + 16] = z["s5_c_im"][l][g].T
    dcol = z["s5_d"][l][64 * j:64 * j + 64][:, None].astype(f)
    pA = np.concatenate([pA_are, pA_aim, pA_bre, pA_bim, pA_ldt], axis=1)
    return {"s5p_pA": np.ascontiguousarray(pA), "s5p_pB": pB, "s5p_cre": pB_cre, "s5p_cim": pB_cim, "s5p_d": dcol}


def phase_s5(nc, S, st, fm, pA_d, pB_d, cre_d, cim_d, dcol_d, negsig_d, kidx_d, midx_d, rowmask_d, yg, T=SEQ):
    TS = lambda n, s, d: st.enter_context(nc.sbuf_tensor(n, s, d))
    PS = lambda n: st.enter_context(nc.psum_tensor(n, [128, 512], F32))
    NB = T // 16
    su = TS("s5_su", [64, T], BF16)
    outsb = TS("s5_out", [64, T], BF16)
    pA = TS("s5_pA", [64, 257], F32)
    dcol = TS("s5_dcol", [64, 1], F32)
    rowmask = TS("s5_rowmask", [64, 4], F32)
    negsig = TS("s5_negsig", [64, 1024], F32)
    SCR = TS("s5_scr", [128, 8192], F32)
    tA = [SCR[0:64, 1024 * i:1024 * (i + 1)] for i in range(8)]
    tAi = TS("s5_tAi", [64, 1024], I32)
    sA = [TS(f"s5_sA{i}", [64, 64], F32) for i in range(10)]
    sAi = TS("s5_sAi", [64, 64], I32)
    dtA = TS("s5_dtA", [64, 1], F32)
    W1tab = [[TS(f"s5_W1tab{q}{ri}", [64, 16, 128], BF16) for ri in range(2)] for q in range(2)]
    pB = [TS(f"s5_pB{q}", [128, 3], F32) for q in range(2)]
    crep = [TS(f"s5_crep{q}", [128, 64], F32) for q in range(2)]
    cimp = [TS(f"s5_cimp{q}", [128, 64], F32) for q in range(2)]
    kidx = TS("s5_kidx", [128, 32], F32)
    midx = TS("s5_midx", [128, 512], F32)
    tB = [TS(f"s5_tB{i}", [128, 32], F32) for i in range(7)]
    tBi = TS("s5_tBi", [128, 32], I32)
    cB = [TS(f"s5_cB{i}", [128, 1], F32) for i in range(6)]
    cBi = TS("s5_cBi", [128, 1], I32)
    gt = [SCR[:, 2048 * i:2048 * (i + 1)].rearrange("p (k c) -> p k c", k=32) for i in range(2)]
    Gpad = [[TS(f"s5_G{q}{ri}", [128, 32, 64], BF16) for ri in range(2)] for q in range(2)]
    Tc = [TS(f"s5_Tc{q}", [128, 512], F32) for q in range(2)]
    Tsn = [TS(f"s5_Ts{q}", [128, 512], F32) for q in range(2)]
    rho = [TS(f"s5_rho{q}", [128, 1], F32) for q in range(2)]
    l2 = [SCR[:, 4096 + 512 * i:4096 + 512 * (i + 1)] for i in range(6)]
    l2i = TS("s5_l2i", [128, 512], I32)
    roll = [TS(f"s5_roll{i}", [128, 512], F32) for i in range(2)]
    W15 = [[TS(f"s5_W15{q}{ri}", [128, 512], F32) for ri in range(2)] for q in range(2)]
    W1bf = [[TS(f"s5_W1bf{q}{ri}", [128, 16, 512], BF16) for ri in range(2)] for q in range(2)]
    Xbf = [[TS(f"s5_Xbf{q}{ri}", [128, 512], BF16) for ri in range(2)] for q in range(2)]
    ytmp = [TS(f"s5_ytmp{i}", [64, 512], F32) for i in range(2)]
    ps_z = [PS(f"s5_ps_z{i}") for i in range(2)]
    ps_y = [PS(f"s5_ps_y{i}") for i in range(2)]

    ld = lambda eng, dst, src, name: S.op(eng, lambda e: e.dma_start(out=dst, in_=src), w=[name], dma=True)
    ld("sp", su[:], fm[256:320, :], "su")
    ld("sp", pA[:], pA_d[:, :], "pA")
    ld("sp", dcol[:], dcol_d[:, :], "dcol")
    ld("sp", rowmask[:], rowmask_d[:, :], "rowmask")
    ld("sp", negsig[:], negsig_d.partition_broadcast(64), "negsig")
    ld("sp", kidx[:], kidx_d.partition_broadcast(128), "kidx")
    ld("sp", midx[:], midx_d.partition_broadcast(128), "midx")
    for q in range(2):
        ld("sp", pB[q][:], pB_d[q], ("pB", q))
        ld("sp", crep[q][:], cre_d[q], ("crep", q))
        ld("sp", cimp[q][:], cim_d[q], ("cimp", q))
    are, aim, bre, bim, ldt = pA[:, 0:64], pA[:, 64:128], pA[:, 128:192], pA[:, 192:256], pA[:, 256:257]
    lam, th, abr, abi, mg, zr, zi, den, u0, u1 = [t[:] for t in sA]
    S.op("act", lambda e: e.activation(out=dtA[:], in_=ldt, func=AF.Exp), r=["pA"], w=["dtA"])
    S.op("dve", lambda e: e.tensor_scalar(out=lam, in0=are, scalar1=dtA[:, 0:1], scalar2=None, op0=ALU.mult), r=["pA", "dtA"], w=["lamA"])
    S.op("dve", lambda e: e.tensor_scalar(out=th, in0=aim, scalar1=dtA[:, 0:1], scalar2=None, op0=ALU.mult), r=["pA", "dtA"], w=["thA"])
    S.op("dve", lambda e: e.tensor_copy(out=u0, in_=th), r=["thA"], w=["sAang"])
    sincos(S, u0, u1, sAi[:], den, abi, abr, "sA")
    S.op("act", lambda e: e.activation(out=mg, in_=lam, func=AF.Exp), r=["lamA"], w=["mgA"])
    S.op("dve", lambda e: e.tensor_tensor(out=abr, in0=abr, in1=mg, op=ALU.mult), r=["mgA", "sAcos"], w=["abr"])
    S.op("dve", lambda e: e.tensor_tensor(out=abi, in0=abi, in1=mg, op=ALU.mult), r=["mgA", "sAsin"], w=["abi"])
    S.op("dve", lambda e: e.tensor_scalar(out=abr, in0=abr, scalar1=-1.0, scalar2=None, op0=ALU.add), w=["abr"])
    S.op("dve", lambda e: e.tensor_tensor(out=den, in0=are, in1=are, op=ALU.mult), r=["pA", "sAcos", "sAsin"], w=["den"])
    S.op("dve", lambda e: e.tensor_tensor(out=u0, in0=aim, in1=aim, op=ALU.mult), r=["pA", "sAsin"], w=["u0"])
    S.op("dve", lambda e: e.tensor_tensor(out=den, in0=den, in1=u0, op=ALU.add), r=["u0"], w=["den"])
    S.op("dve", lambda e: e.reciprocal(out=den, in_=den), w=["den"])
    S.op("dve", lambda e: e.tensor_tensor(out=u0, in0=abr, in1=are, op=ALU.mult), r=["abr"], w=["u0"])
    S.op("dve", lambda e: e.tensor_tensor(out=u1, in0=abi, in1=aim, op=ALU.mult), r=["abi"], w=["u1"])
    S.op("dve", lambda e: e.tensor_tensor(out=zr, in0=u0, in1=u1, op=ALU.add), r=["u0", "u1"], w=["zr"])
    S.op("dve", lambda e: e.tensor_tensor(out=zr, in0=zr, in1=den, op=ALU.mult), r=["den"], w=["zr"])
    S.op("dve", lambda e: e.tensor_tensor(out=u0, in0=abi, in1=are, op=ALU.mult), r=["abi", "zr"], w=["u0"])
    S.op("dve", lambda e: e.tensor_tensor(out=u1, in0=abr, in1=aim, op=ALU.mult), r=["abr", "zr"], w=["u1"])
    S.op("dve", lambda e: e.tensor_tensor(out=zi, in0=u0, in1=u1, op=ALU.subtract), r=["u0", "u1"], w=["zi"])
    S.op("dve", lambda e: e.tensor_tensor(out=zi, in0=zi, in1=den, op=ALU.mult), r=["den"], w=["zi"])
    A3 = lambda t: t[:].rearrange("p (s m) -> p s m", s=16)
    bc3 = lambda ap: ap.unsqueeze(1).broadcast_to([64, 16, 64])
    ang3, kf3, hs3, sn3, cs3, mg3, w_r, w_i = tA
    S.op("dve", lambda e: e.tensor_tensor(out=A3(ang3), in0=A3(negsig), in1=bc3(th), op=ALU.mult), r=["negsig", "thA"], w=["tAang"])
    sincos(S, ang3[:], kf3[:], tAi[:], hs3[:], sn3[:], cs3[:], "tA")
    S.op("dve", lambda e: e.tensor_tensor(out=A3(mg3), in0=A3(negsig), in1=bc3(lam), op=ALU.mult), r=["negsig", "lamA"], w=["mg3"])
    S.op("act", lambda e: e.activation(out=mg3[:], in_=mg3[:], func=AF.Exp), w=["mg3"])
    S.op("dve", lambda e: e.tensor_tensor(out=cs3[:], in0=cs3[:], in1=mg3[:], op=ALU.mult), r=["mg3"], w=["tAcos"])
    S.op("dve", lambda e: e.tensor_tensor(out=sn3[:], in0=sn3[:], in1=mg3[:], op=ALU.mult), r=["mg3"], w=["tAsin"])
    cmul(S, "dve", A3(w_r), A3(w_i), A3(cs3), A3(sn3), bc3(zr), bc3(zi), A3(ang3), A3(kf3),
         ["tAcos", "tAsin", "zr", "zi", "tAang", "tAkf"], "wz")
    cmul(S, "dve", A3(cs3), A3(sn3), A3(w_r), A3(w_i), bc3(bre), bc3(bim), A3(ang3), A3(kf3),
         ["wzre", "wzim", "pA", "tAcos", "tAsin"], "Bs")
    for q in range(2):
        for ri, src in ((0, cs3), (1, sn3)):
            for h in range(2):
                gl = 2 * q + h
                S.op("dve", lambda e, q=q, ri=ri, h=h, gl=gl, src=src: e.tensor_scalar(
                    out=W1tab[q][ri][:, :, 64 * h:64 * h + 64], in0=A3(src), scalar1=rowmask[:, gl:gl + 1], scalar2=None, op0=ALU.mult),
                    r=["Bsre", "Bsim", "rowmask"], w=[("W1tab", q, ri, h)])
    S.barrier()
    for q in range(2):
        lamB, thB, dtB, phi, th15, junk = [t[:] for t in cB]
        angk, kfk, hsk, snk, csk, mgk, nsk = [t[:] for t in tB]
        pq = [("pB", q)]
        tg = f"B{q}"
        S.op("act", lambda e, q=q: e.activation(out=dtB, in_=pB[q][:, 2:3], func=AF.Exp), r=pq, w=[tg + "dt"])
        S.op("dve", lambda e, q=q: e.tensor_tensor(out=lamB, in0=pB[q][:, 0:1], in1=dtB, op=ALU.mult), r=pq + [tg + "dt"], w=[tg + "lam"])
        S.op("dve", lambda e, q=q: e.tensor_tensor(out=thB, in0=pB[q][:, 1:2], in1=dtB, op=ALU.mult), r=pq + [tg + "dt"], w=[tg + "th"])
        S.op("dve", lambda e: e.tensor_scalar(out=angk, in0=kidx[:], scalar1=thB[:, 0:1], scalar2=None, op0=ALU.mult),
             r=["kidx", tg + "th"], w=[tg + "kang"])
        sincos(S, angk, kfk, tBi[:], hsk, snk, csk, tg + "k")
        S.op("dve", lambda e: e.tensor_scalar(out=mgk, in0=kidx[:], scalar1=lamB[:, 0:1], scalar2=None, op0=ALU.mult),
             r=["kidx", tg + "lam"], w=[tg + "mgk"])
        S.op("act", lambda e: e.activation(out=mgk, in_=mgk, func=AF.Exp), w=[tg + "mgk"])
        S.op("dve", lambda e: e.tensor_tensor(out=csk, in0=csk, in1=mgk, op=ALU.mult), r=[tg + "mgk"], w=[tg + "kcos"])
        S.op("dve", lambda e: e.tensor_tensor(out=snk, in0=snk, in1=mgk, op=ALU.mult), r=[tg + "mgk"], w=[tg + "ksin"])
        S.op("dve", lambda e: e.tensor_scalar(out=nsk, in0=snk, scalar1=-1.0, scalar2=None, op0=ALU.mult), r=[tg + "ksin"], w=[tg + "nsk"])
        S.op("dve", lambda e: e.tensor_scalar(out=kfk, in0=csk, scalar1=-1.0, scalar2=None, op0=ALU.mult), r=[tg + "kcos"], w=[tg + "kkf"])
        kb = lambda ap: ap.unsqueeze(2).broadcast_to([128, 32, 64])
        cb = lambda t: t[:].unsqueeze(1).broadcast_to([128, 32, 64])
        for ri, (f1, f2) in enumerate(((csk, nsk), (nsk, kfk))):
            S.op("dve", lambda e, q=q, f1=f1: e.tensor_tensor(out=gt[0][:], in0=cb(crep[q]), in1=kb(f1), op=ALU.mult),
                 r=[("crep", q), tg + "kcos", tg + "nsk", tg + "kkf"], w=["gt0"])
            S.op("dve", lambda e, q=q, f2=f2: e.tensor_tensor(out=gt[1][:], in0=cb(cimp[q]), in1=kb(f2), op=ALU.mult),
                 r=[("cimp", q), tg + "kcos", tg + "nsk", tg + "kkf"], w=["gt1"])
            S.op("dve", lambda e, q=q, ri=ri: e.tensor_tensor(out=Gpad[q][ri][:], in0=gt[0][:], in1=gt[1][:], op=ALU.add),
                 r=["gt0", "gt1"], w=[("Gpad", q, ri)])
        S.op("dve", lambda e: e.tensor_scalar(out=phi, in0=thB, scalar1=16.0, scalar2=None, op0=ALU.mult), r=[tg + "th"], w=[tg + "phi"])
        S.op("dve", lambda e: e.tensor_scalar(out=th15, in0=phi, scalar1=1.0 / (2.0 * math.pi), scalar2=None, op0=ALU.mult),
             r=[tg + "phi"], w=[tg + "th15"])
        S.op("dve", lambda e: e.tensor_copy(out=cBi[:], in_=th15), r=[tg + "th15"], w=[tg + "cBi"])
        S.op("dve", lambda e: e.tensor_copy(out=th15, in_=cBi[:]), r=[tg + "cBi"], w=[tg + "th15"])
        S.op("dve", lambda e: e.scalar_tensor_tensor(out=phi, in0=th15, scalar=-C1_2PI, in1=phi, op0=ALU.mult, op1=ALU.add),
             r=[tg + "th15"], w=[tg + "phi"])
        S.op("dve", lambda e: e.scalar_tensor_tensor(out=phi, in0=th15, scalar=-C2_2PI, in1=phi, op0=ALU.mult, op1=ALU.add),
             r=[tg + "th15"], w=[tg + "phi"])
        S.op("dve", lambda e: e.tensor_scalar(out=l2[0][:], in0=midx[:], scalar1=phi[:, 0:1], scalar2=None, op0=ALU.mult),
             r=["midx", tg + "phi"], w=["l2ang"])
        sincos(S, l2[0][:], l2[1][:], l2i[:], l2[2][:], Tsn[q][:], Tc[q][:], "l2")
        S.op("dve", lambda e, q=q: e.tensor_copy(out=Tsn[q][:], in_=Tsn[q][:]), r=["l2sin"], w=[("Ts", q)])
        S.op("dve", lambda e, q=q: e.tensor_copy(out=Tc[q][:], in_=Tc[q][:]), r=["l2cos"], w=[("Tc", q)])
        S.op("act", lambda e, q=q: e.activation(out=rho[q][:], in_=lamB, func=AF.Exp, scale=16.0), r=[tg + "lam"], w=[("rho", q)])
    suv = su[:].rearrange("p (m s) -> p s m", s=16)
    zi_ = 0
    for q in range(2):
        for ri in range(2):
            for s in range(16):
                pb = zi_ % 2
                zi_ += 1
                S.op("pe", lambda e, q=q, ri=ri, s=s, pb=pb: e.matmul(ps_z[pb][:, 0:NB], W1tab[q][ri][:, s, :], suv[:, s, :], start=True, stop=True),
                     r=["su", ("W1tab", q, ri, 0), ("W1tab", q, ri, 1)], w=[("ps_z", pb)])
                dst = W15[q][ri] if s == 15 else roll[s % 2]
                dn = ("W15", q, ri) if s == 15 else ("roll", s % 2)
                if s == 0:
                    S.op("dve", lambda e, pb=pb, dst=dst: e.tensor_copy(out=dst[:, 0:NB], in_=ps_z[pb][:, 0:NB]), w=[dn, ("ps_z", pb)])
                else:
                    S.op("dve", lambda e, pb=pb, dst=dst, s=s: e.tensor_tensor(out=dst[:, 0:NB], in0=ps_z[pb][:, 0:NB],
                                                                          in1=roll[(s - 1) % 2][:, 0:NB], op=ALU.add),
                         r=[("roll", (s - 1) % 2)], w=[dn, ("ps_z", pb)])
                S.op("act", lambda e, q=q, ri=ri, s=s, dst=dst: e.activation(out=W1bf[q][ri][:, s, 0:NB], in_=dst[:, 0:NB], func=AF.Copy),
                     r=[dn], w=[("W1bf", q, ri, s)])
    for q in range(2):
        ur, ui, t0, t1, vr, vi = [t[:, 0:NB] for t in l2]
        tc, tsn = Tc[q][:, 0:NB], Tsn[q][:, 0:NB]
        wre, wim = W15[q][0][:, 0:NB], W15[q][1][:, 0:NB]
        cmul(S, "dve", ur, ui, tc, tsn, wre, wim, t0, t1, [("Tc", q), ("Ts", q), ("W15", q, 0), ("W15", q, 1), "l2v"], "l2u", conj_a=True)
        rb = rho[q][:, 0:1].broadcast_to([128, NB])
        S.op("dve", lambda e, rb=rb: e.tensor_tensor_scan(out=vr, data0=rb, data1=ur, initial=0.0, op0=ALU.mult, op1=ALU.add),
             r=["l2ure", ("rho", q)], w=["l2vr"])
        S.op("dve", lambda e, rb=rb: e.tensor_tensor_scan(out=vi, data0=rb, data1=ui, initial=0.0, op0=ALU.mult, op1=ALU.add),
             r=["l2uim", ("rho", q)], w=["l2vi"])
        cmul(S, "dve", ur, ui, tc, tsn, vr, vi, t0, t1, [("Tc", q), ("Ts", q), "l2vr", "l2vi"], "l2x")
        for ri, src in ((0, ur), (1, ui)):
            S.op("pool", lambda e, q=q, ri=ri: e.memset(Xbf[q][ri][:, 0:1], 0.0), w=[("Xbf", q, ri)])
            if NB > 1:
                S.op("act", lambda e, q=q, ri=ri, src=src: e.activation(out=Xbf[q][ri][:, 1:NB], in_=src[:, 0:NB - 1], func=AF.Copy),
                     r=["l2xre", "l2xim"], w=[("Xbf", q, ri)])
        S.op("dve", lambda e: e.tensor_copy(out=l2[0][:, 0:1], in_=l2[0][:, 0:1]), r=[("Xbf", q, 0), ("Xbf", q, 1)], w=["l2v", "l2ure", "l2uim"])
    outv = outsb[:].rearrange("p (m s) -> p s m", s=16)
    for s in range(16):
        pb = s % 2
        k = 0
        for q in range(2):
            for ri in range(2):
                S.op("pe", lambda e, q=q, ri=ri, s=s, pb=pb, k=k: e.matmul(ps_y[pb][0:64, 0:NB], Gpad[q][ri][:, s, :], W1bf[q][ri][:, s, 0:NB],
                                                                     start=(k == 0), stop=False),
                     r=[("Gpad", q, ri), ("W1bf", q, ri, s)], w=[("ps_y", pb)])
                k += 1
        for q in range(2):
            for ri in range(2):
                S.op("pe", lambda e, q=q, ri=ri, s=s, pb=pb, k=k: e.matmul(ps_y[pb][0:64, 0:NB], Gpad[q][ri][:, s + 16, :], Xbf[q][ri][:, 0:NB],
                                                                     start=False, stop=(k == 7)),
                     r=[("Gpad", q, ri), ("Xbf", q, ri)], w=[("ps_y", pb)])
                k += 1
        S.op("dve", lambda e, s=s, pb=pb: e.scalar_tensor_tensor(out=ytmp[pb][:, 0:NB], in0=suv[:, s, :], scalar=dcol[:, 0:1],
                                                            in1=ps_y[pb][0:64, 0:NB], op0=ALU.mult, op1=ALU.add),
             r=["su", "dcol"], w=[("ytmp", pb), ("ps_y", pb)])
        S.op("act", lambda e, s=s, pb=pb: e.activation(out=outv[:, s, :], in_=ytmp[pb][:, 0:NB], func=AF.Gelu),
             r=[("ytmp", pb)], w=[("outsb", s)])
    S.op("sp", lambda e: e.dma_start(out=yg[:, :], in_=outsb[:]), r=[("outsb", s) for s in range(16)], dma=True)


def build_s5(T=SEQ):
    nc = bass.Bass("TRN2", target_bir_lowering=False)
    D = lambda n, s, d=F32, k="ExternalInput": nc.dram_tensor(n, s, d, kind=k).ap()
    fm = D("fm", [NFM, T], BF16)
    pA = D("s5p_pA", [64, 257]); pB = D("s5p_pB", [2, 128, 3]); cre = D("s5p_cre", [2, 128, 64]); cim = D("s5p_cim", [2, 128, 64])
    dcol = D("s5p_d", [64, 1]); negsig = D("negsig", [1, 1024]); kidx = D("kidx", [1, 32]); midx = D("midx", [1, 512])
    rowmask = D("rowmask", [64, 4])
    yg = D("yg", [64, T], BF16, "ExternalOutput")
    with contextlib.ExitStack() as st:
        S = Sched(nc)
        phase_s5(nc, S, st, fm, pA, pB, cre, cim, dcol, negsig, kidx, midx, rowmask, yg, T)
        S.emit()
    return nc


def phase_out(nc, S, st, mixin, ssg_d, hT, wout_d, gluw_d, glub_d, fnw_d, hout, final, NTOK=TQ):
    TS = lambda n, s, d: st.enter_context(nc.sbuf_tensor(n, s, d))
    PS = lambda n: st.enter_context(nc.psum_tensor(n, [128, 512], F32))
    wst = [TS(f"po_wst{i}", [128, 1024], F32) for i in range(2)]
    wout = TS("po_wout", [128, 8, 1024], BF16)
    gst = TS("po_gst", [128, 2, 256], F32)
    gluw = TS("po_gluw", [128, 2, 256], BF16)
    glub = TS("po_glub", [128, 2], F32)
    fnw = TS("po_fnw", [128, 8], F32)
    ones = TS("po_ones", [128, 128], BF16)
    mix = [TS(f"po_mix{i}", [128, 8, 512], BF16) for i in range(2)]
    ssg = [TS(f"po_ssg{i}", [128, 2, 512], BF16) for i in range(2)]
    hin = [TS(f"po_hin{i}", [128, 8, 512], F32) for i in range(2)]
    sg = TS("po_sg", [128, 512], F32)
    osb = TS("po_osb", [128, 2, 512], BF16)
    hn = TS("po_hn", [128, 8, 512], F32)
    hsq = TS("po_hsq", [128, 8, 512], BF16)
    nsq = TS("po_nsq", [128, 512], F32)
    ps_g = PS("po_ps_g")
    ps_o = [PS(f"po_ps_o{i}") for i in range(3)]
    ps_n = PS("po_ps_n")

    S.op("pool", lambda e: e.memset(ones[:], 1.0), w=["ones"])
    S.op("sp", lambda e: e.dma_start(out=gst[:], in_=gluw_d.rearrange("(k p) o -> p k o", p=128)), w=["gst"], dma=True)
    S.op("sp", lambda e: e.dma_start(out=glub[:], in_=glub_d[:, :]), w=["glub"], dma=True)
    S.op("sp", lambda e: e.dma_start(out=fnw[:], in_=fnw_d[:, :]), w=["fnw"], dma=True)
    S.op("dve", lambda e: e.tensor_copy(out=gluw[:], in_=gst[:]), r=["gst"], w=["gluw"])
    for k in range(8):
        S.op("sp", lambda e, k=k: e.dma_start(out=wst[k % 2][:], in_=wout_d[k * 128:(k + 1) * 128, :]), w=[("wst", k % 2)], dma=True)
        S.op("pool" if k % 2 else "dve", lambda e, k=k: e.tensor_copy(out=wout[:, k, :], in_=wst[k % 2][:]), r=[("wst", k % 2)], w=[("wout", k)])
    wr = [("wout", k) for k in range(8)]
    mv = mixin.rearrange("(k p) t -> p k t", p=128)
    sv = ssg_d.rearrange("(k p) t -> p k t", p=128)
    hv = hT.rearrange("(k p) t -> p k t", p=128)
    ov = hout.rearrange("(k p) t -> p k t", p=128)
    oi = 0
    for ti in range(NTOK // 512):
        b = ti % 2
        ts = slice(ti * 512, (ti + 1) * 512)
        S.op("sp", lambda e, b=b, ts=ts: e.dma_start(out=mix[b][:], in_=mv[:, :, ts]), w=[("mix", b)], dma=True)
        S.op("sp", lambda e, b=b, ts=ts: e.dma_start(out=ssg[b][:], in_=sv[:, :, ts]), w=[("ssg", b)], dma=True)
        S.op("pool", lambda e, b=b, ts=ts: e.dma_start(out=hin[b][:], in_=hv[:, :, ts]), w=[("hin", b)], dma=True)
        for oc in range(2):
            for kc in range(2):
                S.op("pe", lambda e, b=b, oc=oc, kc=kc: e.matmul(ps_g[:], gluw[:, kc, oc * 128:(oc + 1) * 128], mix[b][:, 2 + kc, :],
                                                             start=(kc == 0), stop=(kc == 1)), r=[("mix", b), "gluw"], w=["ps_g"])
            S.op("act", lambda e, oc=oc: e.activation(out=sg[:], in_=ps_g[:], func=AF.Sigmoid, bias=glub[:, oc:oc + 1]),
                 r=["glub"], w=["sg", "ps_g"])
            S.op("dve", lambda e, b=b, oc=oc: e.tensor_tensor(out=sg[:], in0=sg[:], in1=mix[b][:, 2 + oc, :], op=ALU.mult),
                 r=[("mix", b)], w=["sg"])
            S.op("dve", lambda e, b=b, oc=oc: e.tensor_tensor(out=osb[:, oc, :], in0=sg[:], in1=ssg[b][:, oc, :], op=ALU.mult),
                 r=[("ssg", b), "sg"], w=[("osb", oc)])
        for dc in range(8):
            pb = oi % 3
            oi += 1
            for kc in range(8):
                rhs = (lambda b=b, kc=kc: osb[:, kc - 2, :]) if kc in (2, 3) else (lambda b=b, kc=kc: mix[b][:, kc, :])
                S.op("pe", lambda e, dc=dc, kc=kc, pb=pb, rhs=rhs: e.matmul(ps_o[pb][:], wout[:, kc, dc * 128:(dc + 1) * 128], rhs(),
                                                                      start=(kc == 0), stop=(kc == 7)),
                     r=wr + [("mix", b), ("osb", 0), ("osb", 1)], w=[("ps_o", pb)])
            S.op("dve", lambda e, b=b, dc=dc, pb=pb: e.tensor_tensor(out=hn[:, dc, :], in0=ps_o[pb][:], in1=hin[b][:, dc, :], op=ALU.add),
                 r=[("hin", b)], w=[("hn", dc), ("ps_o", pb)])
            if not final:
                S.op("sp", lambda e, dc=dc, ts=ts: e.dma_start(out=ov[:, dc, ts], in_=hn[:, dc, :]), r=[("hn", dc)], dma=True)
        if final:
            hr = [("hn", dc) for dc in range(8)]
            S.op("act", lambda e: e.activation(out=hsq[:], in_=hn[:], func=AF.Square), r=hr, w=["hsq"])
            for k in range(8):
                S.op("pe", lambda e, k=k: e.matmul(ps_n[:], ones[:], hsq[:, k, :], start=(k == 0), stop=(k == 7)), r=["hsq", "ones"], w=["ps_n"])
            S.op("act", lambda e: e.activation(out=nsq[:], in_=ps_n[:], func=AF.Sqrt, scale=1.0 / D_MODEL, bias=EPS), w=["nsq", "ps_n"])
            S.op("dve", lambda e: e.reciprocal(out=nsq[:], in_=nsq[:]), w=["nsq"])
            for dc in range(8):
                S.op("pool" if dc % 2 else "dve", lambda e, dc=dc: e.scalar_tensor_tensor(
                    out=hn[:, dc, :], in0=hn[:, dc, :], scalar=fnw[:, dc:dc + 1], in1=nsq[:], op0=ALU.mult, op1=ALU.mult) if dc % 2 == 0 else
                    e.tensor_tensor(out=hn[:, dc, :], in0=hn[:, dc, :], in1=nsq[:], op=ALU.mult),
                    r=["nsq", "fnw"], w=[("hn", dc)])
                if dc % 2:
                    S.op("pool", lambda e, dc=dc: e.tensor_scalar(out=hn[:, dc, :], in0=hn[:, dc, :], scalar1=fnw[:, dc:dc + 1], scalar2=None,
                                                                  op0=ALU.mult), r=["fnw"], w=[("hn", dc)])
                S.op("sp", lambda e, dc=dc, ts=ts: e.dma_start(out=ov[:, dc, ts], in_=hn[:, dc, :]), r=[("hn", dc)], dma=True)


def build_out(final, NTOK=TQ):
    nc = bass.Bass("TRN2", target_bir_lowering=False)
    D = lambda n, s, d=F32, k="ExternalInput": nc.dram_tensor(n, s, d, kind=k).ap()
    mixin = D("mixin", [1024, NTOK], BF16)
    ssg = D("ssg", [256, NTOK], BF16)
    hT = D("hT", [D_MODEL, NTOK])
    wout = D("wout", [1024, 1024]); gluw = D("gluw", [256, 256]); glub = D("glub", [128, 2]); fnw = D("fnw", [128, 8])
    hout = D("hout", [D_MODEL, NTOK], F32, "ExternalOutput")
    with contextlib.ExitStack() as st:
        S = Sched(nc)
        phase_out(nc, S, st, mixin, ssg, hT, wout, gluw, glub, fnw, hout, final, NTOK)
        S.emit()
    return nc


_CACHE = {}


def _prog(key, fn):
    if key not in _CACHE:
        _CACHE[key] = fn()
    return _CACHE[key]


def build_mixers(l, T=SEQ, which=("ip", "at", "hg", "s5")):
    lambda_init = 0.8 - 0.6 * math.exp(-0.3 * l)
    nc = bass.Bass("TRN2", target_bir_lowering=False)
    D = lambda n, s, d=F32, k="ExternalInput": nc.dram_tensor(n, s, d, kind=k).ap()
    hT = D("hT", [D_MODEL, T]); wcat = D("wcat", [D_MODEL, NFM + NTM]); nw = D("nw", [128, 8])
    lqk = D("lqk", [1, 256]); subln = D("subln", [128, 1]); ropef = D("ropef", [128, 1]); rmat = D("rmat", [128, 128])
    cmask = D("cmask", [4, 128, 512], BF16)
    lbl_bc = D("lbl_bc", [1, 128]); lbl_col = D("lbl_col", [64, 2]); gw = D("gw", [64, 1]); mcat = D("mcat", [128, 136]); mrev = D("mrev", [128, 128])
    pA = D("s5p_pA", [64, 257]); pB = D("s5p_pB", [2, 128, 3]); cre = D("s5p_cre", [2, 128, 64]); cim = D("s5p_cim", [2, 128, 64])
    dcol = D("s5p_d", [64, 1]); negsig = D("negsig", [1, 1024]); kidx = D("kidx", [1, 32]); midx = D("midx", [1, 512]); rowmask = D("rowmask", [64, 4])
    fm = D("fm", [NFM, T], BF16, "Internal")
    tm_sf = D("tm_sf", [T, 64], F32, "Internal")
    tm_v = D("tm_v", [T, 192], BF16, "Internal")
    mo = D("mo", [320, T], BF16, "ExternalOutput")
    if "ip" in which:
        with contextlib.ExitStack() as st:
            S = Sched(nc)
            phase_inproj(nc, S, st, hT, wcat, nw, fm, tm_sf, tm_v, T)
            S.op("sp", lambda e: e.dma_start(out=mo[128:192, :], in_=fm[192:256, :]), r=[], dma=True)
            S.emit()
    if "at" in which:
        with contextlib.ExitStack() as st:
            S = Sched(nc)
            phase_attn(nc, S, st, fm, tm_v, lqk, subln, ropef, rmat, cmask, mo[192:320, :], lambda_init, T)
            S.emit()
    if "hg" in which:
        with contextlib.ExitStack() as st:
            S = Sched(nc)
            phase_hgrn(nc, S, st, fm, tm_sf, tm_v, lbl_bc, lbl_col, gw, mcat, mrev, mo[0:64, :], float(l), T)
            S.emit()
    if "s5" in which:
        with contextlib.ExitStack() as st:
            S = Sched(nc)
            phase_s5(nc, S, st, fm, pA, pB, cre, cim, dcol, negsig, kidx, midx, rowmask, mo[64:128, :], T)
            S.emit()
    return nc


def mixer_inputs(inp, l, c, hT_b):
    f = np.float32
    j = c % 4
    ropef, rmat, cmask = attn_consts()
    mcat, mrev = hgrn_consts()
    negsig, kidx, midx, rowmask = s5_consts()
    lbl = np.asarray(inp["hgrn_lb_logits"], f)[:, 64 * j:64 * j + 64]
    d = {"hT": hT_b, "wcat": np.ascontiguousarray(np.asarray(inp["w_in"][l], f)[:, core_cols(j)]),
         "nw": np.ascontiguousarray(np.asarray(inp["norm_w"][l], f).reshape(8, 128).T),
         "lqk": np.concatenate([inp["diff_lq1"][l], inp["diff_lq2"][l], inp["diff_lk1"][l], inp["diff_lk2"][l]])[None, :].astype(f),
         "subln": np.asarray(inp["diff_subln_w"][l], f)[:, None], "ropef": ropef, "rmat": rmat, "cmask": cmask,
         "lbl_bc": np.ascontiguousarray(lbl.reshape(1, 128)), "lbl_col": np.ascontiguousarray(lbl.T),
         "gw": np.asarray(inp["hgrn_norm_w"][l], f)[:, None], "mcat": mcat, "mrev": mrev,
         "negsig": negsig, "kidx": kidx, "midx": midx, "rowmask": rowmask}
    d.update(s5_params(inp, l, j))
    return d


def kernel(**inp):
    f = np.float32
    x = np.asarray(inp["x"], f)
    cores = list(range(NCORES))
    hT = [np.ascontiguousarray(x[b].T) for b in range(BATCH)]
    for l in range(DEPTH):
        nc = _prog(("mix", l), lambda: build_mixers(l))
        ims = [mixer_inputs(inp, l, c, hT[c // 4]) for c in cores]
        rm = run_bass_kernel_spmd(nc, ims, core_ids=cores).results
        final = (l == DEPTH - 1)
        nc = _prog(("out", final), lambda: build_out(final))
        ims = []
        for c in cores:
            b, tq = c // 4, c % 4
            ts = slice(tq * TQ, (tq + 1) * TQ)
            src = [4 * b + j for j in range(4)]
            mixin = np.concatenate([rm[s]["mo"][0:64, ts] for s in src] + [rm[s]["mo"][64:128, ts] for s in src]
                                   + [rm[s]["mo"][192:320, ts] for s in src])
            ssg = np.concatenate([rm[s]["mo"][128:192, ts] for s in src])
            ims.append({"mixin": np.ascontiguousarray(mixin), "ssg": np.ascontiguousarray(ssg), "hT": np.ascontiguousarray(hT[b][:, ts]),
                        "wout": np.asarray(inp["w_out"][l], f), "gluw": np.asarray(inp["s5_glu_w"][l], f),
                        "glub": np.ascontiguousarray(np.asarray(inp["s5_glu_b"][l], f).reshape(2, 128).T),
                        "fnw": np.ascontiguousarray(np.asarray(inp["final_norm_w"], f).reshape(8, 128).T)})
        ro = run_bass_kernel_spmd(nc, ims, core_ids=cores).results
        hT = [np.concatenate([ro[4 * b + tq]["hout"] for tq in range(4)], axis=1) for b in range(BATCH)]
    out = np.stack([hT[b].T for b in range(BATCH)]).astype(f)
    return np.ascontiguousarray(out)
```

```python
import contextlib
import math
import numpy as np
import ml_dtypes
import concourse.bass as bass
import concourse.mybir as mybir
from concourse.bass_utils import run_bass_kernel_spmd

F32 = mybir.dt.float32
BF16 = mybir.dt.bfloat16
I32 = mybir.dt.int32
AF = mybir.ActivationFunctionType
ALU = mybir.AluOpType
AX = mybir.AxisListType

D_MODEL = 1024
SEQ = 8192
BATCH = 2
DEPTH = 2
EPS = 1e-6
NCORES = 8
TQ = SEQ // 4
ROPE_THETA = 500000.0
import os
DBG = set(os.environ.get("KDBG", "").split(","))


class Sched:
    ENGS = ["pe", "act", "dve", "pool", "sp"]

    def __init__(self, nc):
        self.nc = nc
        self.ops = []
        self.last_w = {}
        self.readers = {}
        self.cnt = {e: 0 for e in self.ENGS}
        self.dma_cnt = {}
        self.base = set()

    def op(self, eng, fn, r=(), w=(), dma=False):
        deps = set(self.base)
        for x in r:
            if x in self.last_w:
                deps.add(self.last_w[x])
        for x in w:
            if x in self.last_w:
                deps.add(self.last_w[x])
            for d in self.readers.get(x, ()):
                deps.add(d)
        if dma:
            q = self.dma_cnt.get(eng, 0)
            self.dma_cnt[eng] = q + 1
            tok = ("dma", eng, q)
        else:
            self.cnt[eng] += 1
            tok = ("eng", eng, self.cnt[eng])
        self.ops.append((eng, fn, deps, tok))
        for x in w:
            self.last_w[x] = tok
            self.readers[x] = []
        for x in r:
            self.readers.setdefault(x, []).append(tok)
        return tok

    def barrier(self):
        b = set()
        for e in self.ENGS:
            if self.cnt[e] > 0:
                b.add(("eng", e, self.cnt[e]))
        for e, n in self.dma_cnt.items():
            for q in range(max(0, n - self.NSLOT), n):
                b.add(("dma", e, q))
        self.base = b
        self.last_w = {}
        self.readers = {}

    NSLOT = 8

    def emit(self):
        nc = self.nc
        NSLOT = self.NSLOT
        needed = set()
        for (eng, fn, deps, tok) in self.ops:
            for d in deps:
                if d[0] == "eng" and not (d[1] == "pe" and eng == "pe"):
                    needed.add(d)
        sig = {}
        run = {e: 0 for e in self.ENGS}
        for (eng, fn, deps, tok) in self.ops:
            if tok[0] == "eng":
                if tok in needed:
                    run[eng] += 1
                sig[tok] = run[eng]
        with contextlib.ExitStack() as st:
            esem = {e: st.enter_context(nc.semaphore("s_" + e)) for e in self.ENGS}
            dsem = {}
            for e in self.dma_cnt:
                dsem[e] = [st.enter_context(nc.semaphore(f"d_{e}_{i}")) for i in range(NSLOT)]
            block = st.enter_context(nc.Block())
            per = {e: [o for o in self.ops if o[0] == e] for e in self.ENGS}

            def mk(ename):
                def body(eng):
                    seen = {}

                    def wait(tok):
                        if tok[0] == "eng":
                            _, e2, n = tok
                            if e2 == "pe" and ename == "pe":
                                return
                            v = sig[tok]
                            key = ("eng", e2)
                            if seen.get(key, 0) >= v:
                                return
                            seen[key] = v
                            eng.wait_ge(esem[e2], v)
                        else:
                            _, e2, q = tok
                            slot = q % NSLOT
                            val = 16 * (q // NSLOT + 1)
                            key = ("dma", e2, slot)
                            if seen.get(key, 0) >= val:
                                return
                            seen[key] = val
                            eng.wait_ge(dsem[e2][slot], val)
                    for (_, fn, deps, tok) in per[ename]:
                        for d in sorted(deps):
                            wait(d)
                        if tok[0] == "dma":
                            q = tok[2]
                            if q >= NSLOT:
                                wait(("dma", ename, q - NSLOT))
                            ins = fn(eng)
                            ins.then_inc(dsem[ename][q % NSLOT], 16)
                        else:
                            ins = fn(eng)
                            if tok in needed:
                                ins.then_inc(esem[ename], 1)
                    n = self.dma_cnt.get(ename, 0)
                    for q in range(max(0, n - NSLOT), n):
                        wait(("dma", ename, q))
                return body
            block.tensor(mk("pe"))
            block.scalar(mk("act"))
            block.vector(mk("dve"))
            block.gpsimd(mk("pool"))
            block.sync(mk("sp"))


NFM = 704
NTM = 256
FM_CH = [(0, 128), (128, 128), (256, 64), (320, 128), (448, 128), (576, 128)]


def phase_inproj(nc, S, st, hT, wcat, nw, fm, tm_sf, tm_v, T=SEQ):
    TS = lambda n, s, d: st.enter_context(nc.sbuf_tensor(n, s, d))
    PS = lambda n: st.enter_context(nc.psum_tensor(n, [128, 512], F32))
    nw_sb = TS("ip_nw", [128, 8], F32)
    wst = [TS(f"ip_wst{i}", [128, NFM + NTM], F32) for i in range(2)]
    wall = TS("ip_wall", [128, 8, NFM + NTM], BF16)
    ones = TS("ip_ones", [128, 128], BF16)
    xin = [TS(f"ip_xin{i}", [128, 8, 512], F32) for i in range(2)]
    xsq = TS("ip_xsq", [128, 8, 512], BF16)
    sq = TS("ip_sq", [128, 512], F32)
    rstd = TS("ip_rstd", [128, 512], F32)
    xn = [TS(f"ip_xn{i}", [128, 8, 512], BF16) for i in range(2)]
    fmo = [TS(f"ip_fmo{i}", [128, 512], BF16) for i in range(6)]
    tsf = [TS(f"ip_tsf{i}", [128, 4, 64], F32) for i in range(2)]
    tv = [TS(f"ip_tv{i}", [128, 4, 192], BF16) for i in range(2)]
    ps_ss = PS("ip_ps_ss")
    ps_fm = [PS(f"ip_ps_fm{i}") for i in range(4)]
    ps_tm = [PS(f"ip_ps_tm{i}") for i in range(2)]

    S.op("sp", lambda e: e.dma_start(out=nw_sb[:], in_=nw[:, :]), w=["nw"], dma=True)
    S.op("pool", lambda e: e.memset(ones[:], 1.0), w=["ones"])
    for k in range(8):
        S.op("sp", lambda e, k=k: e.dma_start(out=wst[k % 2][:], in_=wcat[k * 128:(k + 1) * 128, :]),
             w=[("wst", k % 2)], dma=True)
        S.op("dve", lambda e, k=k: e.tensor_scalar(out=wall[:, k, :], in0=wst[k % 2][:], scalar1=nw_sb[:, k:k + 1],
                                                  scalar2=None, op0=ALU.mult),
             r=[("wst", k % 2), "nw"], w=[("wall", k)])
    wall_r = [("wall", k) for k in range(8)]
    hT_v = hT.rearrange("(k p) t -> p k t", p=128)
    fmi = 0
    NTI = T // 512

    def load(ti):
        b = ti % 2
        t0 = ti * 512
        S.op("pool", lambda e, b=b, t0=t0: e.dma_start(out=xin[b][:, 0:4, :], in_=hT_v[:, 0:4, t0:t0 + 512]),
             w=[("xin", b, 0)], dma=True)
        S.op("pool", lambda e, b=b, t0=t0: e.dma_start(out=xin[b][:, 4:8, :], in_=hT_v[:, 4:8, t0:t0 + 512]),
             w=[("xin", b, 1)], dma=True)
    def front_sq(ti):
        b = ti % 2
        xr = [("xin", b, 0), ("xin", b, 1)]
        S.op("act", lambda e, b=b: e.activation(out=xsq[:], in_=xin[b][:], func=AF.Square), r=xr, w=["xsq"])

    def front_ss(ti):
        b = ti % 2
        for k in range(8):
            S.op("pe", lambda e, k=k: e.matmul(ps_ss[:], ones[:], xsq[:, k, :], start=(k == 0), stop=(k == 7)),
                 r=["xsq", "ones"], w=["ps_ss"])
        S.op("act", lambda e: e.activation(out=sq[:], in_=ps_ss[:], func=AF.Sqrt, scale=1.0 / D_MODEL, bias=EPS),
             w=["ps_ss", "sq"])
        S.op("dve", lambda e: e.reciprocal(out=rstd[:], in_=sq[:]), r=["sq"], w=["rstd"])
        for hh in range(2):
            S.op("dve", lambda e, b=b, hh=hh: e.tensor_tensor(out=xn[b][:, 4 * hh:4 * hh + 4, :], in0=xin[b][:, 4 * hh:4 * hh + 4, :],
                                                         in1=rstd[:].unsqueeze(1).broadcast_to([128, 4, 512]), op=ALU.mult),
                 r=[("xin", b, hh), "rstd"], w=[("xn", b, hh)])
    load(0)
    if NTI > 1:
        load(1)
    front_sq(0)
    front_ss(0)
    for ti in range(NTI):
        b = ti % 2
        t0 = ti * 512
        if ti + 1 < NTI:
            front_sq(ti + 1)
        xnr = [("xn", b, 0), ("xn", b, 1)]
        for ci, (c0, cw) in enumerate(FM_CH):
            if "NOFM" in DBG or ("FM%d" % ci) in DBG:
                continue
            if ci == 3:
                if ti + 1 < NTI:
                    front_ss(ti + 1)
                if ti + 2 < NTI:
                    load(ti + 2)
            pb = fmi % 4
            for k in range(8):
                S.op("pe", lambda e, k=k, c0=c0, cw=cw, pb=pb, b=b: e.matmul(
                    ps_fm[pb][0:cw, :], wall[:, k, c0:c0 + cw], xn[b][:, k, :], start=(k == 0), stop=(k == 7)),
                    r=xnr + wall_r, w=[("ps_fm", pb)])
            fb = fmi % 6
            fmi += 1
            if ci == 0:
                S.op("act", lambda e, pb=pb, fb=fb: e.activation(out=fmo[fb][0:64, :], in_=ps_fm[pb][0:64, :], func=AF.Silu),
                     r=[("ps_fm", pb)], w=[("fmo", fb, 0)])
                S.op("act", lambda e, pb=pb, fb=fb: e.activation(out=fmo[fb][64:128, :], in_=ps_fm[pb][64:128, :],
                                                               func=AF.Sigmoid, scale=-1.0),
                     r=[("ps_fm", pb)], w=[("fmo", fb, 1)])
                wl = [("fmo", fb, 0), ("fmo", fb, 1)]
            elif ci in (1, 5):
                S.op("act", lambda e, pb=pb, fb=fb: e.activation(out=fmo[fb][:], in_=ps_fm[pb][:], func=AF.Silu),
                     r=[("ps_fm", pb)], w=[("fmo", fb, 0), ("fmo", fb, 1)])
                wl = [("fmo", fb, 0), ("fmo", fb, 1)]
            else:
                S.op("dve", lambda e, pb=pb, fb=fb, cw=cw: e.tensor_copy(out=fmo[fb][0:cw, :], in_=ps_fm[pb][0:cw, :]),
                     r=[("ps_fm", pb)], w=[("fmo", fb, 0), ("fmo", fb, 1)])
                wl = [("fmo", fb, 0), ("fmo", fb, 1)]
            S.op("sp", lambda e, fb=fb, c0=c0, cw=cw, t0=t0: e.dma_start(out=fm[c0:c0 + cw, t0:t0 + 512], in_=fmo[fb][0:cw, :]),
                 r=wl, dma=True)
        for pb in range(0 if "NOTM" in DBG else 2):
            for t4 in (2 * pb, 2 * pb + 1):
                off = (t4 % 2) * 256
                for k in range(8):
                    S.op("pe", lambda e, k=k, t4=t4, pb=pb, off=off, b=b: e.matmul(
                        ps_tm[pb][:, off:off + 256], xn[b][:, k, t4 * 128:(t4 + 1) * 128], wall[:, k, NFM:NFM + NTM],
                        start=(k == 0), stop=(k == 7)),
                        r=xnr + wall_r, w=[("ps_tm", pb)])
            if "TMNOEVAC" in DBG:
                continue
            for t4 in (2 * pb, 2 * pb + 1):
                off = (t4 % 2) * 256
                if "TMNOACT" not in DBG:
                  S.op("act", lambda e, pb=pb, b=b, t4=t4, off=off: e.activation(
                    out=tsf[b][:, t4, :], in_=ps_tm[pb][:, off:off + 64], func=AF.Sigmoid),
                    w=[("ps_tm", pb), ("tsf", b, t4)])
                if "TMNODVE" not in DBG:
                  S.op("dve", lambda e, pb=pb, b=b, t4=t4, off=off: e.tensor_copy(
                    out=tv[b][:, t4, :], in_=ps_tm[pb][:, off + 64:off + 256]),
                    w=[("ps_tm", pb), ("tv", b, t4)])
        if "TMNODMA" in DBG:
            continue
        S.op("sp", lambda e, b=b, t0=t0: e.dma_start(
            out=tm_sf[t0:t0 + 512, :].rearrange("(a p) c -> p a c", p=128), in_=tsf[b][:]),
            r=[("tsf", b, t4) for t4 in range(4)], dma=True)
        S.op("sp", lambda e, b=b, t0=t0: e.dma_start(
            out=tm_v[t0:t0 + 512, :].rearrange("(a p) c -> p a c", p=128), in_=tv[b][:]),
            r=[("tv", b, t4) for t4 in range(4)], dma=True)


def core_cols(j):
    r = lambda s, n: list(range(s, s + n))
    fmc = (r(0 + 64 * j, 64) + r(256 + 64 * j, 64) + r(768 + 64 * j, 64) + r(1280 + 64 * j, 64) + r(1024 + 64 * j, 64)
           + r(1536 + 128 * j, 128) + r(2048 + 128 * j, 128) + r(3072 + 128 * j, 128))
    tmc = r(256 + 64 * j, 64) + r(512 + 64 * j, 64) + r(2560 + 128 * j, 128)
    return np.array(fmc + tmc)


def build_inproj(T=SEQ):
    nc = bass.Bass("TRN2", target_bir_lowering=False)
    hT = nc.dram_tensor("hT", [D_MODEL, T], F32, kind="ExternalInput").ap()
    wcat = nc.dram_tensor("wcat", [D_MODEL, NFM + NTM], F32, kind="ExternalInput").ap()
    nw = nc.dram_tensor("nw", [128, 8], F32, kind="ExternalInput").ap()
    fm = nc.dram_tensor("fm", [NFM, T], BF16, kind="ExternalOutput").ap()
    tm_sf = nc.dram_tensor("tm_sf", [T, 64], F32, kind="ExternalOutput").ap()
    tm_v = nc.dram_tensor("tm_v", [T, 192], BF16, kind="ExternalOutput").ap()
    with contextlib.ExitStack() as st:
        S = Sched(nc)
        phase_inproj(nc, S, st, hT, wcat, nw, fm, tm_sf, tm_v, T)
        S.emit()
    return nc


C1_2PI = 6.28125
C2_2PI = 2.0 * math.pi - 6.28125


def rope_tables(nc, S, st, ropef, sinT, cosT, T, tag):
    TS = lambda n, s, d: st.enter_context(nc.sbuf_tensor(n, s, d))
    CH = min(2048, T)
    pi_ = TS(tag + "_pi", [128, CH], I32)
    ang = TS(tag + "_ang", [128, CH], F32)
    kf = TS(tag + "_kf", [128, CH], F32)
    ki = TS(tag + "_ki", [128, CH], I32)
    hs = TS(tag + "_hs", [128, CH], F32)
    for c in range(T // CH):
        S.op("pool", lambda e, c=c: e.iota(pi_[:], pattern=[[1, CH]], base=c * CH, channel_multiplier=0), w=[tag + "pi"])
        S.op("dve", lambda e: e.tensor_copy(out=ang[:], in_=pi_[:]), r=[tag + "pi"], w=[tag + "ang"])
        S.op("dve", lambda e: e.tensor_scalar(out=ang[:], in0=ang[:], scalar1=ropef[:, 0:1], scalar2=None, op0=ALU.mult),
             r=["ropef"], w=[tag + "ang"])
        S.op("dve", lambda e: e.tensor_scalar(out=kf[:], in0=ang[:], scalar1=1.0 / (2.0 * math.pi), scalar2=None, op0=ALU.mult),
             r=[tag + "ang"], w=[tag + "kf"])
        S.op("dve", lambda e: e.tensor_copy(out=ki[:], in_=kf[:]), r=[tag + "kf"], w=[tag + "ki"])
        S.op("dve", lambda e: e.tensor_copy(out=kf[:], in_=ki[:]), r=[tag + "ki"], w=[tag + "kf"])
        S.op("dve", lambda e: e.scalar_tensor_tensor(out=ang[:], in0=kf[:], scalar=-C1_2PI, in1=ang[:], op0=ALU.mult, op1=ALU.add),
             r=[tag + "kf"], w=[tag + "ang"])
        S.op("dve", lambda e: e.scalar_tensor_tensor(out=ang[:], in0=kf[:], scalar=-C2_2PI, in1=ang[:], op0=ALU.mult, op1=ALU.add),
             r=[tag + "kf"], w=[tag + "ang"])
        S.op("dve", lambda e: e.tensor_scalar(out=ang[:], in0=ang[:], scalar1=math.pi, scalar2=-math.pi, op0=ALU.min, op1=ALU.max),
             w=[tag + "ang"])
        S.op("act", lambda e, c=c: e.activation(out=sinT[:, c * CH:(c + 1) * CH], in_=ang[:], func=AF.Sin),
             r=[tag + "ang"], w=[(tag + "sin", c)])
        S.op("act", lambda e: e.activation(out=hs[:], in_=ang[:], func=AF.Sin, scale=0.5), r=[tag + "ang"], w=[tag + "hs"])
        S.op("pool", lambda e: e.tensor_tensor(out=hs[:], in0=hs[:], in1=hs[:], op=ALU.mult), w=[tag + "hs"])
        S.op("pool", lambda e, c=c: e.tensor_scalar(out=cosT[:, c * CH:(c + 1) * CH], in0=hs[:], scalar1=-2.0, scalar2=1.0,
                                                   op0=ALU.mult, op1=ALU.add), r=[tag + "hs"], w=[(tag + "cos", c)])
    return [(tag + "sin", c) for c in range(T // CH)] + [(tag + "cos", c) for c in range(T // CH)]


def phase_attn(nc, S, st, fm, tm_v, lqk, subln, ropef_d, rmat_d, cmask_d, oa, lambda_init, T=SEQ):
    TS = lambda n, s, d: st.enter_context(nc.sbuf_tensor(n, s, d))
    PS = lambda n: st.enter_context(nc.psum_tensor(n, [128, 512], F32))
    NQ = T // 512
    NK = T // 128
    ropef = TS("at_ropef", [128, 1], F32)
    rm32 = TS("at_rm32", [128, 128], F32)
    rm = TS("at_rm", [128, 128], BF16)
    cmask = TS("at_cmask", [128, 4, 512], BF16)
    ones = TS("at_ones", [128, 128], BF16)
    sinT = TS("at_sin", [128, T], BF16)
    cosT = TS("at_cos", [128, T], BF16)
    qraw = TS("at_qraw", [128, T], BF16)
    kraw = TS("at_kraw", [128, T], BF16)
    qr = TS("at_qr", [128, T], BF16)
    kr = TS("at_kr", [128, T], BF16)
    sag = TS("at_sag", [128, T], BF16)
    vsb = TS("at_v", [128, NK, 128], BF16)
    lq = TS("at_lq", [128, 256], F32)
    lp = TS("at_lp", [128, 128], F32)
    le = TS("at_le", [128, 2], F32)
    neglam = TS("at_neglam", [128, 1], F32)
    sw = TS("at_sw", [128, 1], F32)
    t1 = TS("at_t1", [128, 512], F32)
    t2 = TS("at_t2", [128, 512], F32)
    P = [[TS(f"at_P{i}_{m}", [128, 512], BF16) for m in range(2)] for i in range(3)]
    r0 = TS("at_r0", [128, 512], F32)
    r1 = TS("at_r1", [128, 512], F32)
    o0 = TS("at_o0", [128, 512], F32)
    o1 = TS("at_o1", [128, 512], F32)
    osq = TS("at_osq", [128, 512], BF16)
    nsq = TS("at_nsq", [128, 512], F32)
    ob = [TS(f"at_ob{i}", [128, 512], BF16) for i in range(2)]
    ps_s = [[PS(f"at_ps_s{i}_{m}") for m in range(2)] for i in range(2)]
    ps_o = [PS(f"at_ps_o{m}") for m in range(2)]
    ps_l = [PS(f"at_ps_l{m}") for m in range(2)]

    S.op("sp", lambda e: e.dma_start(out=ropef[:], in_=ropef_d[:, :]), w=["ropef"], dma=True)
    S.op("sp", lambda e: e.dma_start(out=rm32[:], in_=rmat_d[:, :]), w=["rm32"], dma=True)
    S.op("sp", lambda e: e.dma_start(out=cmask[:], in_=cmask_d.rearrange("d p q -> p d q")), w=["cmask"], dma=True)
    S.op("sp", lambda e: e.dma_start(out=lq[:], in_=lqk.partition_broadcast(128)), w=["lq"], dma=True)
    S.op("sp", lambda e: e.dma_start(out=sw[:], in_=subln[:, :]), w=["sw"], dma=True)
    S.op("sp", lambda e: e.dma_start(out=qraw[:], in_=fm[320:448, :]), w=["qraw"], dma=True)
    S.op("sp", lambda e: e.dma_start(out=kraw[:], in_=fm[448:576, :]), w=["kraw"], dma=True)
    S.op("sp", lambda e: e.dma_start(out=sag[:], in_=fm[576:704, :]), w=["sag"], dma=True)
    S.op("pool", lambda e: e.dma_start(out=vsb[:], in_=tm_v[:, 64:192].rearrange("(a p) c -> p a c", p=128)), w=["vsb"], dma=True)
    S.op("pool", lambda e: e.memset(ones[:], 1.0), w=["ones"])
    S.op("dve", lambda e: e.tensor_copy(out=rm[:], in_=rm32[:]), r=["rm32"], w=["rm"])
    S.op("dve", lambda e: e.tensor_tensor(out=lp[:], in0=lq[:, 0:128], in1=lq[:, 128:256], op=ALU.mult), r=["lq"], w=["lp"])
    S.op("dve", lambda e: e.tensor_reduce(out=le[:], in_=lp[:].rearrange("p (a c) -> p a c", a=2), axis=AX.X, op=ALU.add),
         r=["lp"], w=["le"])
    S.op("act", lambda e: e.activation(out=le[:], in_=le[:], func=AF.Exp), w=["le"])
    S.op("dve", lambda e: e.tensor_tensor(out=neglam[:], in0=le[:, 1:2], in1=le[:, 0:1], op=ALU.subtract), r=["le"], w=["neglam"])
    S.op("dve", lambda e: e.tensor_scalar(out=neglam[:], in0=neglam[:], scalar1=-lambda_init, scalar2=None, op0=ALU.add), w=["neglam"])
    S.op("dve", lambda e: e.tensor_scalar(out=sw[:], in0=sw[:], scalar1=1.0 - lambda_init, scalar2=None, op0=ALU.mult), w=["sw"])
    tabs = rope_tables(nc, S, st, ropef, sinT, cosT, T, "at_rp")
    for src, dst, sn, dn in ((qraw, qr, "qraw", "qr"), (kraw, kr, "kraw", "kr")):
        for ti in range(NQ):
            sl = slice(ti * 512, (ti + 1) * 512)
            S.op("pe", lambda e, src=src, sl=sl: e.matmul(ps_s[0][0][:], rm[:], src[:, sl], start=True, stop=True),
                 r=[sn, "rm"], w=["ps_s00"])
            S.op("dve", lambda e, src=src, sl=sl: e.tensor_tensor(out=t1[:], in0=src[:, sl], in1=cosT[:, sl], op=ALU.mult),
                 r=[sn] + tabs, w=["t1"])
            S.op("dve", lambda e, sl=sl: e.tensor_tensor(out=t2[:], in0=ps_s[0][0][:], in1=sinT[:, sl], op=ALU.mult),
                 r=tabs, w=["t2", "ps_s00"])
            S.op("pool", lambda e, dst=dst, sl=sl: e.tensor_tensor(out=dst[:, sl], in0=t1[:], in1=t2[:], op=ALU.add),
                 r=["t1", "t2"], w=[(dn, ti)])
    qr_all = [("qr", ti) for ti in range(NQ)]
    kr_all = [("kr", ti) for ti in range(NQ)]
    psn = lambda i, m: f"ps_s{i}{m}"
    for qi in range(NQ):
        qs = slice(qi * 512, (qi + 1) * 512)
        nk = 4 * (qi + 1)

        def QK(n):
            i = n % 2
            for m in range(2):
                S.op("pe", lambda e, n=n, i=i, m=m, qs=qs: e.matmul(ps_s[i][m][:], kr[64 * m:64 * m + 64, n * 128:(n + 1) * 128],
                                                         qr[64 * m:64 * m + 64, qs], start=True, stop=True),
                     r=[("qr", qi), ("kr", n // 4)], w=[psn(i, m)])

        def EXP(n):
            i = n % 2
            j = n % 3
            for m in range(2):
                S.op("act", lambda e, i=i, j=j, m=m: e.activation(out=P[j][m][:], in_=ps_s[i][m][:], func=AF.Exp, scale=0.125),
                     w=[psn(i, m), ("P", j, m)])
                d = n - 4 * qi
                if d >= 0:
                    S.op("dve", lambda e, j=j, m=m, d=d: e.tensor_tensor(out=P[j][m][:], in0=P[j][m][:], in1=cmask[:, d, :], op=ALU.mult),
                         r=["cmask"], w=[("P", j, m)])

        def PV(n):
            j = n % 3
            for m in range(2):
                S.op("pe", lambda e, n=n, j=j, m=m, nk=nk: e.matmul(ps_o[m][:], vsb[:, n, :], P[j][m][:], start=(n == 0), stop=(n == nk - 1)),
                     r=[("P", j, m), "vsb"], w=[f"ps_o{m}"])
                S.op("pe", lambda e, n=n, j=j, m=m, nk=nk: e.matmul(ps_l[m][:], ones[:], P[j][m][:], start=(n == 0), stop=(n == nk - 1)),
                     r=[("P", j, m), "ones"], w=[f"ps_l{m}"])
        QK(0)
        for n in range(nk):
            if n + 1 < nk:
                QK(n + 1)
            EXP(n)
            PV(n)
        S.op("dve", lambda e: e.reciprocal(out=r0[:], in_=ps_l[0][:]), w=["r0", "ps_l0"])
        S.op("dve", lambda e: e.reciprocal(out=r1[:], in_=ps_l[1][:]), w=["r1", "ps_l1"])
        S.op("dve", lambda e: e.tensor_tensor(out=o0[:], in0=ps_o[0][:], in1=r0[:], op=ALU.mult), r=["r0"], w=["o0", "ps_o0"])
        S.op("dve", lambda e: e.tensor_tensor(out=o1[:], in0=ps_o[1][:], in1=r1[:], op=ALU.mult), r=["r1"], w=["o1", "ps_o1"])
        S.op("dve", lambda e: e.scalar_tensor_tensor(out=o0[:], in0=o1[:], scalar=neglam[:, 0:1], in1=o0[:], op0=ALU.mult, op1=ALU.add),
             r=["o1", "neglam"], w=["o0"])
        S.op("act", lambda e: e.activation(out=osq[:], in_=o0[:], func=AF.Square), r=["o0"], w=["osq"])
        S.op("pe", lambda e: e.matmul(ps_s[0][0][:], ones[:], osq[:], start=True, stop=True), r=["osq", "ones"], w=[psn(0, 0)])
        S.op("act", lambda e: e.activation(out=nsq[:], in_=ps_s[0][0][:], func=AF.Sqrt, scale=1.0 / 128.0, bias=EPS),
             w=["nsq", psn(0, 0)])
        S.op("dve", lambda e: e.reciprocal(out=nsq[:], in_=nsq[:]), w=["nsq"])
        S.op("dve", lambda e: e.scalar_tensor_tensor(out=o0[:], in0=o0[:], scalar=sw[:, 0:1], in1=nsq[:], op0=ALU.mult, op1=ALU.mult),
             r=["nsq", "sw"], w=["o0"])
        S.op("pool", lambda e, qi=qi, qs=qs: e.tensor_tensor(out=ob[qi % 2][:], in0=o0[:], in1=sag[:, qs], op=ALU.mult),
             r=["o0", "sag"], w=[("ob", qi % 2)])
        S.op("sp", lambda e, qi=qi, qs=qs: e.dma_start(out=oa[:, qs], in_=ob[qi % 2][:]), r=[("ob", qi % 2)], dma=True)


def attn_consts():
    ropef = np.zeros((128, 1), np.float32)
    inv = (ROPE_THETA ** (-np.arange(0, 16, 2, dtype=np.float32) / 16.0)).astype(np.float32)
    rmat = np.zeros((128, 128), np.float32)
    for base in (0, 64):
        for i in range(8):
            ropef[base + i, 0] = -inv[i]
            ropef[base + 8 + i, 0] = inv[i]
            rmat[base + 8 + i, base + i] = 1.0
            rmat[base + i, base + 8 + i] = 1.0
    k = np.arange(128)[:, None]
    q = np.arange(512)[None, :]
    cmask = np.stack([(128 * d + k <= q) for d in range(4)]).astype(ml_dtypes.bfloat16)
    return ropef, rmat, cmask


def build_attn(lambda_init, T=SEQ):
    nc = bass.Bass("TRN2", target_bir_lowering=False)
    fm = nc.dram_tensor("fm", [NFM, T], BF16, kind="ExternalInput").ap()
    tm_v = nc.dram_tensor("tm_v", [T, 192], BF16, kind="ExternalInput").ap()
    lqk = nc.dram_tensor("lqk", [1, 256], F32, kind="ExternalInput").ap()
    subln = nc.dram_tensor("subln", [128, 1], F32, kind="ExternalInput").ap()
    ropef = nc.dram_tensor("ropef", [128, 1], F32, kind="ExternalInput").ap()
    rmat = nc.dram_tensor("rmat", [128, 128], F32, kind="ExternalInput").ap()
    cmask = nc.dram_tensor("cmask", [4, 128, 512], BF16, kind="ExternalInput").ap()
    oa = nc.dram_tensor("oa", [128, T], BF16, kind="ExternalOutput").ap()
    with contextlib.ExitStack() as st:
        S = Sched(nc)
        phase_attn(nc, S, st, fm, tm_v, lqk, subln, ropef, rmat, cmask, oa, lambda_init, T)
        S.emit()
    return nc


def hgrn_consts():
    s = np.arange(128)[:, None]
    t = np.arange(128)[None, :]
    same = (s // 16) == (t // 16)
    m_incl = (same & (s <= t)).astype(np.float32)
    m_rev = (same & (s > t)).astype(np.float32)
    m_tot8 = ((s // 16) == np.arange(8)[None, :]).astype(np.float32)
    mcat = np.concatenate([m_incl, m_tot8], axis=1)
    return mcat, m_rev


def phase_hgrn(nc, S, st, fm, tm_sf, tm_v, lbl_bc_d, lbl_col_d, gw_d, mcat_d, mrev_d, oh, lb_coef, T=SEQ):
    TS = lambda n, s, d: st.enter_context(nc.sbuf_tensor(n, s, d))
    PS = lambda n: st.enter_context(nc.psum_tensor(n, [128, 512], F32))
    NT = T // 128
    sq = TS("hg_sq", [64, T], BF16)
    snf = TS("hg_snf", [64, T], BF16)
    shg = TS("hg_shg", [64, T], BF16)
    sf = TS("hg_sf", [128, NT, 64], F32)
    omf = TS("hg_omf", [128, NT, 64], F32)
    logf = TS("hg_logf", [128, NT, 64], F32)
    vall = TS("hg_v", [128, NT, 64], BF16)
    lbl = TS("hg_lbl", [128, 128], F32)
    lb_bc = TS("hg_lb_bc", [128, 64], F32)
    oml_bc = TS("hg_oml_bc", [128, 64], F32)
    lblc = TS("hg_lblc", [64, 2], F32)
    oml_col = TS("hg_oml_col", [64, 1], F32)
    gw = TS("hg_gw", [64, 1], F32)
    mcat = TS("hg_mcat", [128, 136], F32)
    mrev = TS("hg_mrev", [128, 128], F32)
    mincl_bf = TS("hg_mincl", [128, 128], BF16)
    mtot_bf = TS("hg_mtot", [128, 8], BF16)
    ones64 = TS("hg_ones", [64, 64], BF16)
    eq = TS("hg_eq", [64, 128], F32)
    ekn = TS("hg_ekn", [64, 128], F32)
    dec = [TS(f"hg_dec{i}", [64, 8], F32) for i in range(2)]
    ehat = TS("hg_ehat", [128, 64], F32)
    qt = [TS(f"hg_qt{i}", [64, 128], BF16) for i in range(2)]
    kt = TS("hg_kt", [64, 128], BF16)
    khat = TS("hg_khat", [128, 64], BF16)
    vblk = TS("hg_vblk", [128, 8, 64], BF16)
    scm = TS("hg_scm", [128, 128], BF16)
    Sall = [TS(f"hg_S{i}", [64, 9, 64], F32) for i in range(2)]
    Sbf = [TS(f"hg_Sbf{i}", [64, 8, 64], BF16) for i in range(2)]
    osq = TS("hg_osq", [64, 512], BF16)
    o32 = TS("hg_o32", [64, 512], F32)
    nsq = TS("hg_nsq", [64, 512], F32)
    ohb = [TS(f"hg_ohb{i}", [64, 512], BF16) for i in range(2)]
    ps_c = PS("hg_ps_c")
    ps_r = PS("hg_ps_r")
    ps_sc = PS("hg_ps_sc")
    ps_u = [PS(f"hg_ps_u{i}") for i in range(2)]
    ps_oh = [PS(f"hg_ps_oh{i}") for i in range(2)]
    ps_n = PS("hg_ps_n")

    S.op("sp", lambda e: e.dma_start(out=sq[:], in_=fm[0:64, :]), w=["sq"], dma=True)
    S.op("sp", lambda e: e.dma_start(out=snf[:], in_=fm[64:128, :]), w=["snf"], dma=True)
    S.op("sp", lambda e: e.dma_start(out=shg[:], in_=fm[128:192, :]), w=["shg"], dma=True)
    S.op("sp", lambda e: e.dma_start(out=sf[:], in_=tm_sf.rearrange("(a p) c -> p a c", p=128)), w=["sf"], dma=True)
    S.op("pool", lambda e: e.dma_start(out=vall[:], in_=tm_v[:, 0:64].rearrange("(a p) c -> p a c", p=128)), w=["vall"], dma=True)
    S.op("sp", lambda e: e.dma_start(out=lbl[:], in_=lbl_bc_d.partition_broadcast(128)), w=["lbl"], dma=True)
    S.op("sp", lambda e: e.dma_start(out=lblc[:], in_=lbl_col_d[:, :]), w=["lblc"], dma=True)
    S.op("sp", lambda e: e.dma_start(out=gw[:], in_=gw_d[:, :]), w=["gw"], dma=True)
    S.op("sp", lambda e: e.dma_start(out=mcat[:], in_=mcat_d[:, :]), w=["mcat"], dma=True)
    S.op("sp", lambda e: e.dma_start(out=mrev[:], in_=mrev_d[:, :]), w=["mrev"], dma=True)
    S.op("pool", lambda e: e.memset(ones64[:], 1.0), w=["ones64"])
    S.op("pool", lambda e: e.memset(Sall[0][:, 0, :], 0.0), w=[("S", 0)])
    S.op("dve", lambda e: e.tensor_copy(out=mincl_bf[:], in_=mcat[:, 0:128]), r=["mcat"], w=["mincl_bf"])
    S.op("dve", lambda e: e.tensor_copy(out=mtot_bf[:], in_=mcat[:, 128:136]), r=["mcat"], w=["mtot_bf"])
    S.op("dve", lambda e: e.tensor_tensor(out=lb_bc[:], in0=lbl[:, 64:128], in1=lbl[:, 0:64], op=ALU.subtract), r=["lbl"], w=["lb_bc"])
    S.op("act", lambda e: e.activation(out=lb_bc[:], in_=lb_bc[:], func=AF.Sigmoid), w=["lb_bc"])
    S.op("dve", lambda e: e.tensor_scalar(out=lb_bc[:], in0=lb_bc[:], scalar1=float(lb_coef), scalar2=None, op0=ALU.mult), w=["lb_bc"])
    S.op("dve", lambda e: e.tensor_scalar(out=oml_bc[:], in0=lb_bc[:], scalar1=-1.0, scalar2=1.0, op0=ALU.mult, op1=ALU.add),
         r=["lb_bc"], w=["oml_bc"])
    S.op("dve", lambda e: e.tensor_tensor(out=oml_col[:], in0=lblc[:, 1:2], in1=lblc[:, 0:1], op=ALU.subtract), r=["lblc"], w=["oml_col"])
    S.op("act", lambda e: e.activation(out=oml_col[:], in_=oml_col[:], func=AF.Sigmoid), w=["oml_col"])
    S.op("dve", lambda e: e.tensor_scalar(out=oml_col[:], in0=oml_col[:], scalar1=-float(lb_coef), scalar2=1.0, op0=ALU.mult, op1=ALU.add),
         w=["oml_col"])
    for g in range(NT // 8 if NT >= 8 else 1):
        nt = min(8, NT)
        sl = slice(g * 8, g * 8 + nt)
        S.op("dve", lambda e, sl=sl, nt=nt: e.tensor_tensor(out=sf[:, sl, :], in0=sf[:, sl, :],
                                                        in1=oml_bc[:].unsqueeze(1).broadcast_to([128, nt, 64]), op=ALU.mult),
             r=["oml_bc"], w=["sf"])
        S.op("dve", lambda e, sl=sl, nt=nt: e.tensor_tensor(out=sf[:, sl, :], in0=sf[:, sl, :],
                                                        in1=lb_bc[:].unsqueeze(1).broadcast_to([128, nt, 64]), op=ALU.add),
             r=["lb_bc"], w=["sf"])
        S.op("act", lambda e, sl=sl: e.activation(out=logf[:, sl, :], in_=sf[:, sl, :], func=AF.Ln), r=["sf"], w=[("logf", g)])
        S.op("pool", lambda e, sl=sl: e.tensor_scalar(out=omf[:, sl, :], in0=sf[:, sl, :], scalar1=-1.0, scalar2=1.0,
                                                     op0=ALU.mult, op1=ALU.add), r=["sf"], w=[("omf", g)])
    for i in range(NT):
        g = i // 8
        b = i % 2
        ts = slice(i * 128, (i + 1) * 128)
        ob = (i // 4) % 2
        S.op("pe", lambda e, i=i: e.matmul(ps_c[0:64, 0:136], logf[:, i, :], mcat[:], start=True, stop=True),
             r=[("logf", g), "mcat"], w=["ps_c"])
        S.op("pe", lambda e, i=i: e.matmul(ps_r[:, 0:64], mrev[:], logf[:, i, :], start=True, stop=True),
             r=[("logf", g), "mrev"], w=["ps_r"])
        S.op("act", lambda e: e.activation(out=eq[:], in_=ps_c[0:64, 0:128], func=AF.Exp), w=["eq", "ps_c"])
        S.op("act", lambda e: e.activation(out=ekn[:], in_=ps_c[0:64, 0:128], func=AF.Exp, scale=-1.0), w=["ekn", "ps_c"])
        S.op("act", lambda e, b=b: e.activation(out=dec[b][:], in_=ps_c[0:64, 128:136], func=AF.Exp), w=[("dec", b), "ps_c"])
        S.op("act", lambda e: e.activation(out=ehat[:], in_=ps_r[:, 0:64], func=AF.Exp), w=["ehat", "ps_r"])
        S.op("dve", lambda e, b=b, ts=ts: e.tensor_tensor(out=qt[b][:], in0=sq[:, ts], in1=eq[:], op=ALU.mult),
             r=["sq", "eq"], w=[("qt", b)])
        S.op("dve", lambda e, ts=ts: e.scalar_tensor_tensor(out=kt[:], in0=snf[:, ts], scalar=oml_col[:, 0:1], in1=ekn[:],
                                                           op0=ALU.mult, op1=ALU.mult), r=["snf", "ekn", "oml_col"], w=["kt"])
        S.op("dve", lambda e, i=i: e.tensor_tensor(out=khat[:], in0=omf[:, i, :], in1=ehat[:], op=ALU.mult),
             r=[("omf", g), "ehat"], w=["khat"])
        S.op("pool", lambda e, i=i: e.tensor_tensor(out=vblk[:], in0=vall[:, i, :].unsqueeze(1).broadcast_to([128, 8, 64]),
                                                   in1=mtot_bf[:].unsqueeze(2).broadcast_to([128, 8, 64]), op=ALU.mult),
             r=["vall", "mtot_bf"], w=["vblk"])
        S.op("pe", lambda e, b=b: e.matmul(ps_sc[:, 0:128], kt[:], qt[b][:], start=True, stop=True), r=["kt", ("qt", b)], w=["ps_sc"])
        S.op("dve", lambda e: e.tensor_tensor(out=scm[:], in0=ps_sc[:, 0:128], in1=mincl_bf[:], op=ALU.mult),
             r=["mincl_bf"], w=["scm", "ps_sc"])
        S.op("pe", lambda e, b=b: e.matmul(ps_u[b][0:64, :], khat[:], vblk[:].rearrange("p a c -> p (a c)"), start=True, stop=True),
             r=["khat", "vblk"], w=[("ps_u", b)])
        if i > 0:
            S.op("dve", lambda e, b=b: e.tensor_copy(out=Sall[b][:, 0, :], in_=Sall[1 - b][:, 8, :]), r=[("S", 1 - b)], w=[("S", b)])
        for n in range(8):
            S.op("dve", lambda e, b=b, n=n: e.scalar_tensor_tensor(
                out=Sall[b][:, n + 1, :], in0=Sall[b][:, n, :], scalar=dec[b][:, n:n + 1], in1=ps_u[b][0:64, n * 64:(n + 1) * 64],
                op0=ALU.mult, op1=ALU.add), r=[("dec", b)], w=[("S", b), ("ps_u", b)])
        S.op("act", lambda e, b=b: e.activation(out=Sbf[b][:], in_=Sall[b][:, 0:8, :], func=AF.Copy), r=[("S", b)], w=[("Sbf", b)])
        c0 = (i % 4) * 128
        S.op("pe", lambda e, i=i, ob=ob, c0=c0: e.matmul(ps_oh[ob][0:64, c0:c0 + 128], vall[:, i, :], scm[:], start=True, stop=False),
             r=["vall", "scm"], w=[("ps_oh", ob)])
        for n in range(8):
            S.op("pe", lambda e, b=b, ob=ob, c0=c0, n=n: e.matmul(
                ps_oh[ob][0:64, c0 + 16 * n:c0 + 16 * n + 16], Sbf[b][:, n, :], qt[b][:, 16 * n:16 * n + 16],
                start=False, stop=(n == 7)), r=[("Sbf", b), ("qt", b)], w=[("ps_oh", ob)])
        if i % 4 == 3 or i == NT - 1:
            qs = slice((i // 4) * 512, (i // 4) * 512 + 512)
            S.op("act", lambda e, ob=ob: e.activation(out=osq[:], in_=ps_oh[ob][0:64, :], func=AF.Square), w=["osq", ("ps_oh", ob)])
            S.op("act", lambda e, ob=ob: e.activation(out=o32[:], in_=ps_oh[ob][0:64, :], func=AF.Copy), w=["o32", ("ps_oh", ob)])
            S.op("pe", lambda e: e.matmul(ps_n[0:64, :], ones64[:], osq[:], start=True, stop=True), r=["osq", "ones64"], w=["ps_n"])
            S.op("act", lambda e: e.activation(out=nsq[:], in_=ps_n[0:64, :], func=AF.Sqrt, scale=1.0 / 64.0, bias=EPS), w=["nsq", "ps_n"])
            S.op("dve", lambda e: e.reciprocal(out=nsq[:], in_=nsq[:]), w=["nsq"])
            S.op("dve", lambda e: e.scalar_tensor_tensor(out=o32[:], in0=o32[:], scalar=gw[:, 0:1], in1=nsq[:], op0=ALU.mult, op1=ALU.mult),
                 r=["nsq", "gw"], w=["o32"])
            S.op("pool", lambda e, ob=ob, qs=qs: e.tensor_tensor(out=ohb[ob][:], in0=o32[:], in1=shg[:, qs], op=ALU.mult),
                 r=["o32", "shg"], w=[("ohb", ob)])
            S.op("sp", lambda e, ob=ob, qs=qs: e.dma_start(out=oh[:, qs], in_=ohb[ob][:]), r=[("ohb", ob)], dma=True)


def build_hgrn(lb_coef, T=SEQ):
    nc = bass.Bass("TRN2", target_bir_lowering=False)
    fm = nc.dram_tensor("fm", [NFM, T], BF16, kind="ExternalInput").ap()
    tm_sf = nc.dram_tensor("tm_sf", [T, 64], F32, kind="ExternalInput").ap()
    tm_v = nc.dram_tensor("tm_v", [T, 192], BF16, kind="ExternalInput").ap()
    lbl_bc = nc.dram_tensor("lbl_bc", [1, 128], F32, kind="ExternalInput").ap()
    lbl_col = nc.dram_tensor("lbl_col", [64, 2], F32, kind="ExternalInput").ap()
    gw = nc.dram_tensor("gw", [64, 1], F32, kind="ExternalInput").ap()
    mcat = nc.dram_tensor("mcat", [128, 136], F32, kind="ExternalInput").ap()
    mrev = nc.dram_tensor("mrev", [128, 128], F32, kind="ExternalInput").ap()
    oh = nc.dram_tensor("oh", [64, T], BF16, kind="ExternalOutput").ap()
    with contextlib.ExitStack() as st:
        S = Sched(nc)
        phase_hgrn(nc, S, st, fm, tm_sf, tm_v, lbl_bc, lbl_col, gw, mcat, mrev, oh, lb_coef, T)
        S.emit()
    return nc


def sincos(S, ang, kf, ki, hs, sin_out, cos_out, tag, eng="dve"):
    a, k, h = tag + "ang", tag + "kf", tag + "hs"
    S.op(eng, lambda e: e.tensor_scalar(out=kf, in0=ang, scalar1=1.0 / (2.0 * math.pi), scalar2=None, op0=ALU.mult), r=[a], w=[k])
    S.op(eng, lambda e: e.tensor_copy(out=ki, in_=kf), r=[k], w=[tag + "ki"])
    S.op(eng, lambda e: e.tensor_copy(out=kf, in_=ki), r=[tag + "ki"], w=[k])
    S.op("dve", lambda e: e.scalar_tensor_tensor(out=ang, in0=kf, scalar=-C1_2PI, in1=ang, op0=ALU.mult, op1=ALU.add), r=[k], w=[a])
    S.op("dve", lambda e: e.scalar_tensor_tensor(out=ang, in0=kf, scalar=-C2_2PI, in1=ang, op0=ALU.mult, op1=ALU.add), r=[k], w=[a])
    S.op(eng, lambda e: e.tensor_scalar(out=ang, in0=ang, scalar1=math.pi, scalar2=-math.pi, op0=ALU.min, op1=ALU.max), w=[a])
    S.op("act", lambda e: e.activation(out=sin_out, in_=ang, func=AF.Sin), r=[a], w=[tag + "sin"])
    S.op("act", lambda e: e.activation(out=hs, in_=ang, func=AF.Sin, scale=0.5), r=[a], w=[h])
    S.op(eng, lambda e: e.tensor_tensor(out=hs, in0=hs, in1=hs, op=ALU.mult), w=[h])
    S.op(eng, lambda e: e.tensor_scalar(out=cos_out, in0=hs, scalar1=-2.0, scalar2=1.0, op0=ALU.mult, op1=ALU.add), r=[h], w=[tag + "cos"])


def cmul(S, eng, o_re, o_im, a_re, a_im, b_re, b_im, t0, t1, rd, wr, conj_a=False):
    sg = -1.0 if conj_a else 1.0
    S.op(eng, lambda e: e.tensor_tensor(out=t0, in0=a_im, in1=b_im, op=ALU.mult), r=rd, w=[wr + "t0"])
    S.op(eng, lambda e: e.tensor_tensor(out=t1, in0=a_re, in1=b_re, op=ALU.mult), r=rd, w=[wr + "t1"])
    S.op("dve", lambda e: e.scalar_tensor_tensor(out=o_re, in0=t0, scalar=-sg, in1=t1, op0=ALU.mult, op1=ALU.add),
         r=[wr + "t0", wr + "t1"], w=[wr + "re"])
    S.op(eng, lambda e: e.tensor_tensor(out=t0, in0=a_im, in1=b_re, op=ALU.mult), r=rd + [wr + "re"], w=[wr + "t0"])
    S.op(eng, lambda e: e.tensor_tensor(out=t1, in0=a_re, in1=b_im, op=ALU.mult), r=rd + [wr + "re"], w=[wr + "t1"])
    S.op("dve", lambda e: e.scalar_tensor_tensor(out=o_im, in0=t0, scalar=sg, in1=t1, op0=ALU.mult, op1=ALU.add),
         r=[wr + "t0", wr + "t1"], w=[wr + "im"])


def s5_consts():
    negsig = np.repeat(-np.arange(16, dtype=np.float32), 64)[None, :]
    kidx = np.arange(32, dtype=np.float32)[None, :]
    midx = np.arange(1, 513, dtype=np.float32)[None, :]
    rowmask = (np.arange(64)[:, None] // 16 == np.arange(4)[None, :]).astype(np.float32)
    return negsig, kidx, midx, rowmask


def s5_params(z, l, j):
    gs = [4 * j + gl for gl in range(4)]
    f = np.float32
    pA_are = np.concatenate([np.repeat(z["s5_a_re"][l][g][None, :], 16, 0) for g in gs]).astype(f)
    pA_aim = np.concatenate([np.repeat(z["s5_a_im"][l][g][None, :], 16, 0) for g in gs]).astype(f)
    pA_ldt = np.concatenate([np.full((16, 1), z["s5_log_dt"][l][g]) for g in gs]).astype(f)
    pA_bre = np.concatenate([z["s5_b_re"][l][g].T for g in gs]).astype(f)
    pA_bim = np.concatenate([z["s5_b_im"][l][g].T for g in gs]).astype(f)
    pB = np.zeros((2, 128, 3), f)
    pB_cre = np.zeros((2, 128, 64), f)
    pB_cim = np.zeros((2, 128, 64), f)
    for q in range(2):
        for h in range(2):
            gl = 2 * q + h
            g = gs[gl]
            rows = slice(64 * h, 64 * h + 64)
            pB[q, rows, 0] = z["s5_a_re"][l][g]
            pB[q, rows, 1] = z["s5_a_im"][l][g]
            pB[q, rows, 2] = z["s5_log_dt"][l][g]
            pB_cre[q, rows, 16 * gl:16 * gl + 16] = z["s5_c_re"][l][g].T
            pB_cim[q, rows, 16 * gl:16 * gl + 16] = z["s5_c_im"][l][g].T
    dcol = z["s5_d"][l][64 * j:64 * j + 64][:, None].astype(f)
    pA = np.concatenate([pA_are, pA_aim, pA_bre, pA_bim, pA_ldt], axis=1)
    return {"s5p_pA": np.ascontiguousarray(pA), "s5p_pB": pB, "s5p_cre": pB_cre, "s5p_cim": pB_cim, "s5p_d": dcol}


def phase_s5(nc, S, st, fm, pA_d, pB_d, cre_d, cim_d, dcol_d, negsig_d, kidx_d, midx_d, rowmask_d, yg, T=SEQ):
    TS = lambda n, s, d: st.enter_context(nc.sbuf_tensor(n, s, d))
    PS = lambda n: st.enter_context(nc.psum_tensor(n, [128, 512], F32))
    NB = T // 16
    su = TS("s5_su", [64, T], BF16)
    outsb = TS("s5_out", [64, T], BF16)
    pA = TS("s5_pA", [64, 257], F32)
    dcol = TS("s5_dcol", [64, 1], F32)
    rowmask = TS("s5_rowmask", [64, 4], F32)
    negsig = TS("s5_negsig", [64, 1024], F32)
    SCR = TS("s5_scr", [128, 8192], F32)
    tA = [SCR[0:64, 1024 * i:1024 * (i + 1)] for i in range(8)]
    tAi = TS("s5_tAi", [64, 1024], I32)
    sA = [TS(f"s5_sA{i}", [64, 64], F32) for i in range(10)]
    sAi = TS("s5_sAi", [64, 64], I32)
    dtA = TS("s5_dtA", [64, 1], F32)
    W1tab = [[TS(f"s5_W1tab{q}{ri}", [64, 16, 128], BF16) for ri in range(2)] for q in range(2)]
    pB = [TS(f"s5_pB{q}", [128, 3], F32) for q in range(2)]
    crep = [TS(f"s5_crep{q}", [128, 64], F32) for q in range(2)]
    cimp = [TS(f"s5_cimp{q}", [128, 64], F32) for q in range(2)]
    kidx = TS("s5_kidx", [128, 32], F32)
    midx = TS("s5_midx", [128, 512], F32)
    tB = [TS(f"s5_tB{i}", [128, 32], F32) for i in range(7)]
    tBi = TS("s5_tBi", [128, 32], I32)
    cB = [TS(f"s5_cB{i}", [128, 1], F32) for i in range(6)]
    cBi = TS("s5_cBi", [128, 1], I32)
    gt = [SCR[:, 2048 * i:2048 * (i + 1)].rearrange("p (k c) -> p k c", k=32) for i in range(2)]
    Gpad = [[TS(f"s5_G{q}{ri}", [128, 32, 64], BF16) for ri in range(2)] for q in range(2)]
    Tc = [TS(f"s5_Tc{q}", [128, 512], F32) for q in range(2)]
    Tsn = [TS(f"s5_Ts{q}", [128, 512], F32) for q in range(2)]
    rho = [TS(f"s5_rho{q}", [128, 1], F32) for q in range(2)]
    l2 = [SCR[:, 4096 + 512 * i:4096 + 512 * (i + 1)] for i in range(6)]
    l2i = TS("s5_l2i", [128, 512], I32)
    roll = [TS(f"s5_roll{i}", [128, 512], F32) for i in range(2)]
    W15 = [[TS(f"s5_W15{q}{ri}", [128, 512], F32) for ri in range(2)] for q in range(2)]
    W1bf = [[TS(f"s5_W1bf{q}{ri}", [128, 16, 512], BF16) for ri in range(2)] for q in range(2)]
    Xbf = [[TS(f"s5_Xbf{q}{ri}", [128, 512], BF16) for ri in range(2)] for q in range(2)]
    ytmp = [TS(f"s5_ytmp{i}", [64, 512], F32) for i in range(2)]
    ps_z = [PS(f"s5_ps_z{i}") for i in range(2)]
    ps_y = [PS(f"s5_ps_y{i}") for i in range(2)]

    ld = lambda eng, dst, src, name: S.op(eng, lambda e: e.dma_start(out=dst, in_=src), w=[name], dma=True)
    ld("sp", su[:], fm[256:320, :], "su")
    ld("sp", pA[:], pA_d[:, :], "pA")
    ld("sp", dcol[:], dcol_d[:, :], "dcol")
    ld("sp", rowmask[:], rowmask_d[:, :], "rowmask")
    ld("sp", negsig[:], negsig_d.partition_broadcast(64), "negsig")
    ld("sp", kidx[:], kidx_d.partition_broadcast(128), "kidx")
    ld("sp", midx[:], midx_d.partition_broadcast(128), "midx")
    for q in range(2):
        ld("sp", pB[q][:], pB_d[q], ("pB", q))
        ld("sp", crep[q][:], cre_d[q], ("crep", q))
        ld("sp", cimp[q][:], cim_d[q], ("cimp", q))
    are, aim, bre, bim, ldt = pA[:, 0:64], pA[:, 64:128], pA[:, 128:192], pA[:, 192:256], pA[:, 256:257]
    lam, th, abr, abi, mg, zr, zi, den, u0, u1 = [t[:] for t in sA]
    S.op("act", lambda e: e.activation(out=dtA[:], in_=ldt, func=AF.Exp), r=["pA"], w=["dtA"])
    S.op("dve", lambda e: e.tensor_scalar(out=lam, in0=are, scalar1=dtA[:, 0:1], scalar2=None, op0=ALU.mult), r=["pA", "dtA"], w=["lamA"])
    S.op("dve", lambda e: e.tensor_scalar(out=th, in0=aim, scalar1=dtA[:, 0:1], scalar2=None, op0=ALU.mult), r=["pA", "dtA"], w=["thA"])
    S.op("dve", lambda e: e.tensor_copy(out=u0, in_=th), r=["thA"], w=["sAang"])
    sincos(S, u0, u1, sAi[:], den, abi, abr, "sA")
    S.op("act", lambda e: e.activation(out=mg, in_=lam, func=AF.Exp), r=["lamA"], w=["mgA"])
    S.op("dve", lambda e: e.tensor_tensor(out=abr, in0=abr, in1=mg, op=ALU.mult), r=["mgA", "sAcos"], w=["abr"])
    S.op("dve", lambda e: e.tensor_tensor(out=abi, in0=abi, in1=mg, op=ALU.mult), r=["mgA", "sAsin"], w=["abi"])
    S.op("dve", lambda e: e.tensor_scalar(out=abr, in0=abr, scalar1=-1.0, scalar2=None, op0=ALU.add), w=["abr"])
    S.op("dve", lambda e: e.tensor_tensor(out=den, in0=are, in1=are, op=ALU.mult), r=["pA", "sAcos", "sAsin"], w=["den"])
    S.op("dve", lambda e: e.tensor_tensor(out=u0, in0=aim, in1=aim, op=ALU.mult), r=["pA", "sAsin"], w=["u0"])
    S.op("dve", lambda e: e.tensor_tensor(out=den, in0=den, in1=u0, op=ALU.add), r=["u0"], w=["den"])
    S.op("dve", lambda e: e.reciprocal(out=den, in_=den), w=["den"])
    S.op("dve", lambda e: e.tensor_tensor(out=u0, in0=abr, in1=are, op=ALU.mult), r=["abr"], w=["u0"])
    S.op("dve", lambda e: e.tensor_tensor(out=u1, in0=abi, in1=aim, op=ALU.mult), r=["abi"], w=["u1"])
    S.op("dve", lambda e: e.tensor_tensor(out=zr, in0=u0, in1=u1, op=ALU.add), r=["u0", "u1"], w=["zr"])
    S.op("dve", lambda e: e.tensor_tensor(out=zr, in0=zr, in1=den, op=ALU.mult), r=["den"], w=["zr"])
    S.op("dve", lambda e: e.tensor_tensor(out=u0, in0=abi, in1=are, op=ALU.mult), r=["abi", "zr"], w=["u0"])
    S.op("dve", lambda e: e.tensor_tensor(out=u1, in0=abr, in1=aim, op=ALU.mult), r=["abr", "zr"], w=["u1"])
    S.op("dve", lambda e: e.tensor_tensor(out=zi, in0=u0, in1=u1, op=ALU.subtract), r=["u0", "u1"], w=["zi"])
    S.op("dve", lambda e: e.tensor_tensor(out=zi, in0=zi, in1=den, op=ALU.mult), r=["den"], w=["zi"])
    A3 = lambda t: t[:].rearrange("p (s m) -> p s m", s=16)
    bc3 = lambda ap: ap.unsqueeze(1).broadcast_to([64, 16, 64])
    ang3, kf3, hs3, sn3, cs3, mg3, w_r, w_i = tA
    S.op("dve", lambda e: e.tensor_tensor(out=A3(ang3), in0=A3(negsig), in1=bc3(th), op=ALU.mult), r=["negsig", "thA"], w=["tAang"])
    sincos(S, ang3[:], kf3[:], tAi[:], hs3[:], sn3[:], cs3[:], "tA")
    S.op("dve", lambda e: e.tensor_tensor(out=A3(mg3), in0=A3(negsig), in1=bc3(lam), op=ALU.mult), r=["negsig", "lamA"], w=["mg3"])
    S.op("act", lambda e: e.activation(out=mg3[:], in_=mg3[:], func=AF.Exp), w=["mg3"])
    S.op("dve", lambda e: e.tensor_tensor(out=cs3[:], in0=cs3[:], in1=mg3[:], op=ALU.mult), r=["mg3"], w=["tAcos"])
    S.op("dve", lambda e: e.tensor_tensor(out=sn3[:], in0=sn3[:], in1=mg3[:], op=ALU.mult), r=["mg3"], w=["tAsin"])
    cmul(S, "dve", A3(w_r), A3(w_i), A3(cs3), A3(sn3), bc3(zr), bc3(zi), A3(ang3), A3(kf3),
         ["tAcos", "tAsin", "zr", "zi", "tAang", "tAkf"], "wz")
    cmul(S, "dve", A3(cs3), A3(sn3), A3(w_r), A3(w_i), bc3(bre), bc3(bim), A3(ang3), A3(kf3),
         ["wzre", "wzim", "pA", "tAcos", "tAsin"], "Bs")
    for q in range(2):
        for ri, src in ((0, cs3), (1, sn3)):
            for h in range(2):
                gl = 2 * q + h
                S.op("dve", lambda e, q=q, ri=ri, h=h, gl=gl, src=src: e.tensor_scalar(
                    out=W1tab[q][ri][:, :, 64 * h:64 * h + 64], in0=A3(src), scalar1=rowmask[:, gl:gl + 1], scalar2=None, op0=ALU.mult),
                    r=["Bsre", "Bsim", "rowmask"], w=[("W1tab", q, ri, h)])
    S.barrier()
    for q in range(2):
        lamB, thB, dtB, phi, th15, junk = [t[:] for t in cB]
        angk, kfk, hsk, snk, csk, mgk, nsk = [t[:] for t in tB]
        pq = [("pB", q)]
        tg = f"B{q}"
        S.op("act", lambda e, q=q: e.activation(out=dtB, in_=pB[q][:, 2:3], func=AF.Exp), r=pq, w=[tg + "dt"])
        S.op("dve", lambda e, q=q: e.tensor_tensor(out=lamB, in0=pB[q][:, 0:1], in1=dtB, op=ALU.mult), r=pq + [tg + "dt"], w=[tg + "lam"])
        S.op("dve", lambda e, q=q: e.tensor_tensor(out=thB, in0=pB[q][:, 1:2], in1=dtB, op=ALU.mult), r=pq + [tg + "dt"], w=[tg + "th"])
        S.op("dve", lambda e: e.tensor_scalar(out=angk, in0=kidx[:], scalar1=thB[:, 0:1], scalar2=None, op0=ALU.mult),
             r=["kidx", tg + "th"], w=[tg + "kang"])
        sincos(S, angk, kfk, tBi[:], hsk, snk, csk, tg + "k")
        S.op("dve", lambda e: e.tensor_scalar(out=mgk, in0=kidx[:], scalar1=lamB[:, 0:1], scalar2=None, op0=ALU.mult),
             r=["kidx", tg + "lam"], w=[tg + "mgk"])
        S.op("act", lambda e: e.activation(out=mgk, in_=mgk, func=AF.Exp), w=[tg + "mgk"])
        S.op("dve", lambda e: e.tensor_tensor(out=csk, in0=csk, in1=mgk, op=ALU.mult), r=[tg + "mgk"], w=[tg + "kcos"])
        S.op("dve", lambda e: e.tensor_tensor(out=snk, in0=snk, in1=mgk, op=ALU.mult), r=[tg + "mgk"], w=[tg + "ksin"])
        S.op("dve", lambda e: e.tensor_scalar(out=nsk, in0=snk, scalar1=-1.0, scalar2=None, op0=ALU.mult), r=[tg + "ksin"], w=[tg + "nsk"])
        S.op("dve", lambda e: e.tensor_scalar(out=kfk, in0=csk, scalar1=-1.0, scalar2=None, op0=ALU.mult), r=[tg + "kcos"], w=[tg + "kkf"])
        kb = lambda ap: ap.unsqueeze(2).broadcast_to([128, 32, 64])
        cb = lambda t: t[:].unsqueeze(1).broadcast_to([128, 32, 64])
        for ri, (f1, f2) in enumerate(((csk, nsk), (nsk, kfk))):
            S.op("dve", lambda e, q=q, f1=f1: e.tensor_tensor(out=gt[0][:], in0=cb(crep[q]), in1=kb(f1), op=ALU.mult),
                 r=[("crep", q), tg + "kcos", tg + "nsk", tg + "kkf"], w=["gt0"])
            S.op("dve", lambda e, q=q, f2=f2: e.tensor_tensor(out=gt[1][:], in0=cb(cimp[q]), in1=kb(f2), op=ALU.mult),
                 r=[("cimp", q), tg + "kcos", tg + "nsk", tg + "kkf"], w=["gt1"])
            S.op("dve", lambda e, q=q, ri=ri: e.tensor_tensor(out=Gpad[q][ri][:], in0=gt[0][:], in1=gt[1][:], op=ALU.add),
                 r=["gt0", "gt1"], w=[("Gpad", q, ri)])
        S.op("dve", lambda e: e.tensor_scalar(out=phi, in0=thB, scalar1=16.0, scalar2=None, op0=ALU.mult), r=[tg + "th"], w=[tg + "phi"])
        S.op("dve", lambda e: e.tensor_scalar(out=th15, in0=phi, scalar1=1.0 / (2.0 * math.pi), scalar2=None, op0=ALU.mult),
             r=[tg + "phi"], w=[tg + "th15"])
        S.op("dve", lambda e: e.tensor_copy(out=cBi[:], in_=th15), r=[tg + "th15"], w=[tg + "cBi"])
        S.op("dve", lambda e: e.tensor_copy(out=th15, in_=cBi[:]), r=[tg + "cBi"], w=[tg + "th15"])
        S.op("dve", lambda e: e.scalar_tensor_tensor(out=phi, in0=th15, scalar=-C1_2PI, in1=phi, op0=ALU.mult, op1=ALU.add),
             r=[tg + "th15"], w=[tg + "phi"])
        S.op("dve", lambda e: e.scalar_tensor_tensor(out=phi, in0=th15, scalar=-C2_2PI, in1=phi, op0=ALU.mult, op1=ALU.add),
             r=[tg + "th15"], w=[tg + "phi"])
        S.op("dve", lambda e: e.tensor_scalar(out=l2[0][:], in0=midx[:], scalar1=phi[:, 0:1], scalar2=None, op0=ALU.mult),
             r=["midx", tg + "phi"], w=["l2ang"])
        sincos(S, l2[0][:], l2[1][:], l2i[:], l2[2][:], Tsn[q][:], Tc[q][:], "l2")
        S.op("dve", lambda e, q=q: e.tensor_copy(out=Tsn[q][:], in_=Tsn[q][:]), r=["l2sin"], w=[("Ts", q)])
        S.op("dve", lambda e, q=q: e.tensor_copy(out=Tc[q][:], in_=Tc[q][:]), r=["l2cos"], w=[("Tc", q)])
        S.op("act", lambda e, q=q: e.activation(out=rho[q][:], in_=lamB, func=AF.Exp, scale=16.0), r=[tg + "lam"], w=[("rho", q)])
    suv = su[:].rearrange("p (m s) -> p s m", s=16)
    zi_ = 0
    for q in range(2):
        for ri in range(2):
            for s in range(16):
                pb = zi_ % 2
                zi_ += 1
                S.op("pe", lambda e, q=q, ri=ri, s=s, pb=pb: e.matmul(ps_z[pb][:, 0:NB], W1tab[q][ri][:, s, :], suv[:, s, :], start=True, stop=True),
                     r=["su", ("W1tab", q, ri, 0), ("W1tab", q, ri, 1)], w=[("ps_z", pb)])
                dst = W15[q][ri] if s == 15 else roll[s % 2]
                dn = ("W15", q, ri) if s == 15 else ("roll", s % 2)
                if s == 0:
                    S.op("dve", lambda e, pb=pb, dst=dst: e.tensor_copy(out=dst[:, 0:NB], in_=ps_z[pb][:, 0:NB]), w=[dn, ("ps_z", pb)])
                else:
                    S.op("dve", lambda e, pb=pb, dst=dst, s=s: e.tensor_tensor(out=dst[:, 0:NB], in0=ps_z[pb][:, 0:NB],
                                                                          in1=roll[(s - 1) % 2][:, 0:NB], op=ALU.add),
                         r=[("roll", (s - 1) % 2)], w=[dn, ("ps_z", pb)])
                S.op("act", lambda e, q=q, ri=ri, s=s, dst=dst: e.activation(out=W1bf[q][ri][:, s, 0:NB], in_=dst[:, 0:NB], func=AF.Copy),
                     r=[dn], w=[("W1bf", q, ri, s)])
    for q in range(2):
        ur, ui, t0, t1, vr, vi = [t[:, 0:NB] for t in l2]
        tc, tsn = Tc[q][:, 0:NB], Tsn[q][:, 0:NB]
        wre, wim = W15[q][0][:, 0:NB], W15[q][1][:, 0:NB]
        cmul(S, "dve", ur, ui, tc, tsn, wre, wim, t0, t1, [("Tc", q), ("Ts", q), ("W15", q, 0), ("W15", q, 1), "l2v"], "l2u", conj_a=True)
        rb = rho[q][:, 0:1].broadcast_to([128, NB])
        S.op("dve", lambda e, rb=rb: e.tensor_tensor_scan(out=vr, data0=rb, data1=ur, initial=0.0, op0=ALU.mult, op1=ALU.add),
             r=["l2ure", ("rho", q)], w=["l2vr"])
        S.op("dve", lambda e, rb=rb: e.tensor_tensor_scan(out=vi, data0=rb, data1=ui, initial=0.0, op0=ALU.mult, op1=ALU.add),
             r=["l2uim", ("rho", q)], w=["l2vi"])
        cmul(S, "dve", ur, ui, tc, tsn, vr, vi, t0, t1, [("Tc", q), ("Ts", q), "l2vr", "l2vi"], "l2x")
        for ri, src in ((0, ur), (1, ui)):
            S.op("pool", lambda e, q=q, ri=ri: e.memset(Xbf[q][ri][:, 0:1], 0.0), w=[("Xbf", q, ri)])
            if NB > 1:
                S.op("act", lambda e, q=q, ri=ri, src=src: e.activation(out=Xbf[q][ri][:, 1:NB], in_=src[:, 0:NB - 1], func=AF.Copy),
                     r=["l2xre", "l2xim"], w=[("Xbf", q, ri)])
        S.op("dve", lambda e: e.tensor_copy(out=l2[0][:, 0:1], in_=l2[0][:, 0:1]), r=[("Xbf", q, 0), ("Xbf", q, 1)], w=["l2v", "l2ure", "l2uim"])
    outv = outsb[:].rearrange("p (m s) -> p s m", s=16)
    for s in range(16):
        pb = s % 2
        k = 0
        for q in range(2):
            for ri in range(2):
                S.op("pe", lambda e, q=q, ri=ri, s=s, pb=pb, k=k: e.matmul(ps_y[pb][0:64, 0:NB], Gpad[q][ri][:, s, :], W1bf[q][ri][:, s, 0:NB],
                                                                     start=(k == 0), stop=False),
                     r=[("Gpad", q, ri), ("W1bf", q, ri, s)], w=[("ps_y", pb)])
                k += 1
        for q in range(2):
            for ri in range(2):
                S.op("pe", lambda e, q=q, ri=ri, s=s, pb=pb, k=k: e.matmul(ps_y[pb][0:64, 0:NB], Gpad[q][ri][:, s + 16, :], Xbf[q][ri][:, 0:NB],
                                                                     start=False, stop=(k == 7)),
                     r=[("Gpad", q, ri), ("Xbf", q, ri)], w=[("ps_y", pb)])
                k += 1
        S.op("dve", lambda e, s=s, pb=pb: e.scalar_tensor_tensor(out=ytmp[pb][:, 0:NB], in0=suv[:, s, :], scalar=dcol[:, 0:1],
                                                            in1=ps_y[pb][0:64, 0:NB], op0=ALU.mult, op1=ALU.add),
             r=["su", "dcol"], w=[("ytmp", pb), ("ps_y", pb)])
        S.op("act", lambda e, s=s, pb=pb: e.activation(out=outv[:, s, :], in_=ytmp[pb][:, 0:NB], func=AF.Gelu),
             r=[("ytmp", pb)], w=[("outsb", s)])
    S.op("sp", lambda e: e.dma_start(out=yg[:, :], in_=outsb[:]), r=[("outsb", s) for s in range(16)], dma=True)


def build_s5(T=SEQ):
    nc = bass.Bass("TRN2", target_bir_lowering=False)
    D = lambda n, s, d=F32, k="ExternalInput": nc.dram_tensor(n, s, d, kind=k).ap()
    fm = D("fm", [NFM, T], BF16)
    pA = D("s5p_pA", [64, 257]); pB = D("s5p_pB", [2, 128, 3]); cre = D("s5p_cre", [2, 128, 64]); cim = D("s5p_cim", [2, 128, 64])
    dcol = D("s5p_d", [64, 1]); negsig = D("negsig", [1, 1024]); kidx = D("kidx", [1, 32]); midx = D("midx", [1, 512])
    rowmask = D("rowmask", [64, 4])
    yg = D("yg", [64, T], BF16, "ExternalOutput")
    with contextlib.ExitStack() as st:
        S = Sched(nc)
        phase_s5(nc, S, st, fm, pA, pB, cre, cim, dcol, negsig, kidx, midx, rowmask, yg, T)
        S.emit()
    return nc


def phase_out(nc, S, st, mixin, ssg_d, hT, wout_d, gluw_d, glub_d, fnw_d, hout, final, NTOK=TQ):
    TS = lambda n, s, d: st.enter_context(nc.sbuf_tensor(n, s, d))
    PS = lambda n: st.enter_context(nc.psum_tensor(n, [128, 512], F32))
    wst = [TS(f"po_wst{i}", [128, 1024], F32) for i in range(2)]
    wout = TS("po_wout", [128, 8, 1024], BF16)
    gst = TS("po_gst", [128, 2, 256], F32)
    gluw = TS("po_gluw", [128, 2, 256], BF16)
    glub = TS("po_glub", [128, 2], F32)
    fnw = TS("po_fnw", [128, 8], F32)
    ones = TS("po_ones", [128, 128], BF16)
    mix = [TS(f"po_mix{i}", [128, 8, 512], BF16) for i in range(2)]
    ssg = [TS(f"po_ssg{i}", [128, 2, 512], BF16) for i in range(2)]
    hin = [TS(f"po_hin{i}", [128, 8, 512], F32) for i in range(2)]
    sg = TS("po_sg", [128, 512], F32)
    osb = TS("po_osb", [128, 2, 512], BF16)
    hn = TS("po_hn", [128, 8, 512], F32)
    hsq = TS("po_hsq", [128, 8, 512], BF16)
    nsq = TS("po_nsq", [128, 512], F32)
    ps_g = PS("po_ps_g")
    ps_o = [PS(f"po_ps_o{i}") for i in range(3)]
    ps_n = PS("po_ps_n")

    S.op("pool", lambda e: e.memset(ones[:], 1.0), w=["ones"])
    S.op("sp", lambda e: e.dma_start(out=gst[:], in_=gluw_d.rearrange("(k p) o -> p k o", p=128)), w=["gst"], dma=True)
    S.op("sp", lambda e: e.dma_start(out=glub[:], in_=glub_d[:, :]), w=["glub"], dma=True)
    S.op("sp", lambda e: e.dma_start(out=fnw[:], in_=fnw_d[:, :]), w=["fnw"], dma=True)
    S.op("dve", lambda e: e.tensor_copy(out=gluw[:], in_=gst[:]), r=["gst"], w=["gluw"])
    for k in range(8):
        S.op("sp", lambda e, k=k: e.dma_start(out=wst[k % 2][:], in_=wout_d[k * 128:(k + 1) * 128, :]), w=[("wst", k % 2)], dma=True)
        S.op("pool" if k % 2 else "dve", lambda e, k=k: e.tensor_copy(out=wout[:, k, :], in_=wst[k % 2][:]), r=[("wst", k % 2)], w=[("wout", k)])
    wr = [("wout", k) for k in range(8)]
    mv = mixin.rearrange("(k p) t -> p k t", p=128)
    sv = ssg_d.rearrange("(k p) t -> p k t", p=128)
    hv = hT.rearrange("(k p) t -> p k t", p=128)
    ov = hout.rearrange("(k p) t -> p k t", p=128)
    oi = 0
    for ti in range(NTOK // 512):
        b = ti % 2
        ts = slice(ti * 512, (ti + 1) * 512)
        S.op("sp", lambda e, b=b, ts=ts: e.dma_start(out=mix[b][:], in_=mv[:, :, ts]), w=[("mix", b)], dma=True)
        S.op("sp", lambda e, b=b, ts=ts: e.dma_start(out=ssg[b][:], in_=sv[:, :, ts]), w=[("ssg", b)], dma=True)
        S.op("pool", lambda e, b=b, ts=ts: e.dma_start(out=hin[b][:], in_=hv[:, :, ts]), w=[("hin", b)], dma=True)
        for oc in range(2):
            for kc in range(2):
                S.op("pe", lambda e, b=b, oc=oc, kc=kc: e.matmul(ps_g[:], gluw[:, kc, oc * 128:(oc + 1) * 128], mix[b][:, 2 + kc, :],
                                                             start=(kc == 0), stop=(kc == 1)), r=[("mix", b), "gluw"], w=["ps_g"])
            S.op("act", lambda e, oc=oc: e.activation(out=sg[:], in_=ps_g[:], func=AF.Sigmoid, bias=glub[:, oc:oc + 1]),
                 r=["glub"], w=["sg", "ps_g"])
            S.op("dve", lambda e, b=b, oc=oc: e.tensor_tensor(out=sg[:], in0=sg[:], in1=mix[b][:, 2 + oc, :], op=ALU.mult),
                 r=[("mix", b)], w=["sg"])
            S.op("dve", lambda e, b=b, oc=oc: e.tensor_tensor(out=osb[:, oc, :], in0=sg[:], in1=ssg[b][:, oc, :], op=ALU.mult),
                 r=[("ssg", b), "sg"], w=[("osb", oc)])
        for dc in range(8):
            pb = oi % 3
            oi += 1
            for kc in range(8):
                rhs = (lambda b=b, kc=kc: osb[:, kc - 2, :]) if kc in (2, 3) else (lambda b=b, kc=kc: mix[b][:, kc, :])
                S.op("pe", lambda e, dc=dc, kc=kc, pb=pb, rhs=rhs: e.matmul(ps_o[pb][:], wout[:, kc, dc * 128:(dc + 1) * 128], rhs(),
                                                                      start=(kc == 0), stop=(kc == 7)),
                     r=wr + [("mix", b), ("osb", 0), ("osb", 1)], w=[("ps_o", pb)])
            S.op("dve", lambda e, b=b, dc=dc, pb=pb: e.tensor_tensor(out=hn[:, dc, :], in0=ps_o[pb][:], in1=hin[b][:, dc, :], op=ALU.add),
                 r=[("hin", b)], w=[("hn", dc), ("ps_o", pb)])
            if not final:
                S.op("sp", lambda e, dc=dc, ts=ts: e.dma_start(out=ov[:, dc, ts], in_=hn[:, dc, :]), r=[("hn", dc)], dma=True)
        if final:
            hr = [("hn", dc) for dc in range(8)]
            S.op("act", lambda e: e.activation(out=hsq[:], in_=hn[:], func=AF.Square), r=hr, w=["hsq"])
            for k in range(8):
                S.op("pe", lambda e, k=k: e.matmul(ps_n[:], ones[:], hsq[:, k, :], start=(k == 0), stop=(k == 7)), r=["hsq", "ones"], w=["ps_n"])
            S.op("act", lambda e: e.activation(out=nsq[:], in_=ps_n[:], func=AF.Sqrt, scale=1.0 / D_MODEL, bias=EPS), w=["nsq", "ps_n"])
            S.op("dve", lambda e: e.reciprocal(out=nsq[:], in_=nsq[:]), w=["nsq"])
            for dc in range(8):
                S.op("pool" if dc % 2 else "dve", lambda e, dc=dc: e.scalar_tensor_tensor(
                    out=hn[:, dc, :], in0=hn[:, dc, :], scalar=fnw[:, dc:dc + 1], in1=nsq[:], op0=ALU.mult, op1=ALU.mult) if dc % 2 == 0 else
                    e.tensor_tensor(out=hn[:, dc, :], in0=hn[:, dc, :], in1=nsq[:], op=ALU.mult),
                    r=["nsq", "fnw"], w=[("hn", dc)])
                if dc % 2:
                    S.op("pool", lambda e, dc=dc: e.tensor_scalar(out=hn[:, dc, :], in0=hn[:, dc, :], scalar1=fnw[:, dc:dc + 1], scalar2=None,
                                                                  op0=ALU.mult), r=["fnw"], w=[("hn", dc)])
                S.op("sp", lambda e, dc=dc, ts=ts: e.dma_start(out=ov[:, dc, ts], in_=hn[:, dc, :]), r=[("hn", dc)], dma=True)


def build_out(final, NTOK=TQ):
    nc = bass.Bass("TRN2", target_bir_lowering=False)
    D = lambda n, s, d=F32, k="ExternalInput": nc.dram_tensor(n, s, d, kind=k).ap()
    mixin = D("mixin", [1024, NTOK], BF16)
    ssg = D("ssg", [256, NTOK], BF16)
    hT = D("hT", [D_MODEL, NTOK])
    wout = D("wout", [1024, 1024]); gluw = D("gluw", [256, 256]); glub = D("glub", [128, 2]); fnw = D("fnw", [128, 8])
    hout = D("hout", [D_MODEL, NTOK], F32, "ExternalOutput")
    with contextlib.ExitStack() as st:
        S = Sched(nc)
        phase_out(nc, S, st, mixin, ssg, hT, wout, gluw, glub, fnw, hout, final, NTOK)
        S.emit()
    return nc


_CACHE = {}


def _prog(key, fn):
    if key not in _CACHE:
        _CACHE[key] = fn()
    return _CACHE[key]


def build_mixers(l, T=SEQ, which=("ip", "at", "hg", "s5")):
    lambda_init = 0.8 - 0.6 * math.exp(-0.3 * l)
    nc = bass.Bass("TRN2", target_bir_lowering=False)
    D = lambda n, s, d=F32, k="ExternalInput": nc.dram_tensor(n, s, d, kind=k).ap()
    hT = D("hT", [D_MODEL, T]); wcat = D("wcat", [D_MODEL, NFM + NTM]); nw = D("nw", [128, 8])
    lqk = D("lqk", [1, 256]); subln = D("subln", [128, 1]); ropef = D("ropef", [128, 1]); rmat = D("rmat", [128, 128])
    cmask = D("cmask", [4, 128, 512], BF16)
    lbl_bc = D("lbl_bc", [1, 128]); lbl_col = D("lbl_col", [64, 2]); gw = D("gw", [64, 1]); mcat = D("mcat", [128, 136]); mrev = D("mrev", [128, 128])
    pA = D("s5p_pA", [64, 257]); pB = D("s5p_pB", [2, 128, 3]); cre = D("s5p_cre", [2, 128, 64]); cim = D("s5p_cim", [2, 128, 64])
    dcol = D("s5p_d", [64, 1]); negsig = D("negsig", [1, 1024]); kidx = D("kidx", [1, 32]); midx = D("midx", [1, 512]); rowmask = D("rowmask", [64, 4])
    fm = D("fm", [NFM, T], BF16, "Internal")
    tm_sf = D("tm_sf", [T, 64], F32, "Internal")
    tm_v = D("tm_v", [T, 192], BF16, "Internal")
    mo = D("mo", [320, T], BF16, "ExternalOutput")
    if "ip" in which:
        with contextlib.ExitStack() as st:
            S = Sched(nc)
            phase_inproj(nc, S, st, hT, wcat, nw, fm, tm_sf, tm_v, T)
            S.op("sp", lambda e: e.dma_start(out=mo[128:192, :], in_=fm[192:256, :]), r=[], dma=True)
            S.emit()
    if "at" in which:
        with contextlib.ExitStack() as st:
            S = Sched(nc)
            phase_attn(nc, S, st, fm, tm_v, lqk, subln, ropef, rmat, cmask, mo[192:320, :], lambda_init, T)
            S.emit()
    if "hg" in which:
        with contextlib.ExitStack() as st:
            S = Sched(nc)
            phase_hgrn(nc, S, st, fm, tm_sf, tm_v, lbl_bc, lbl_col, gw, mcat, mrev, mo[0:64, :], float(l), T)
            S.emit()
    if "s5" in which:
        with contextlib.ExitStack() as st:
            S = Sched(nc)
            phase_s5(nc, S, st, fm, pA, pB, cre, cim, dcol, negsig, kidx, midx, rowmask, mo[64:128, :], T)
            S.emit()
    return nc


def mixer_inputs(inp, l, c, hT_b):
    f = np.float32
    j = c % 4
    ropef, rmat, cmask = attn_consts()
    mcat, mrev = hgrn_consts()
    negsig, kidx, midx, rowmask = s5_consts()
    lbl = np.asarray(inp["hgrn_lb_logits"], f)[:, 64 * j:64 * j + 64]
    d = {"hT": hT_b, "wcat": np.ascontiguousarray(np.asarray(inp["w_in"][l], f)[:, core_cols(j)]),
         "nw": np.ascontiguousarray(np.asarray(inp["norm_w"][l], f).reshape(8, 128).T),
         "lqk": np.concatenate([inp["diff_lq1"][l], inp["diff_lq2"][l], inp["diff_lk1"][l], inp["diff_lk2"][l]])[None, :].astype(f),
         "subln": np.asarray(inp["diff_subln_w"][l], f)[:, None], "ropef": ropef, "rmat": rmat, "cmask": cmask,
         "lbl_bc": np.ascontiguousarray(lbl.reshape(1, 128)), "lbl_col": np.ascontiguousarray(lbl.T),
         "gw": np.asarray(inp["hgrn_norm_w"][l], f)[:, None], "mcat": mcat, "mrev": mrev,
         "negsig": negsig, "kidx": kidx, "midx": midx, "rowmask": rowmask}
    d.update(s5_params(inp, l, j))
    return d


def kernel(**inp):
    f = np.float32
    x = np.asarray(inp["x"], f)
    cores = list(range(NCORES))
    hT = [np.ascontiguousarray(x[b].T) for b in range(BATCH)]
    for l in range(DEPTH):
        nc = _prog(("mix", l), lambda: build_mixers(l))
        ims = [mixer_inputs(inp, l, c, hT[c // 4]) for c in cores]
        rm = run_bass_kernel_spmd(nc, ims, core_ids=cores).results
        final = (l == DEPTH - 1)
        nc = _prog(("out", final), lambda: build_out(final))
        ims = []
        for c in cores:
            b, tq = c // 4, c % 4
            ts = slice(tq * TQ, (tq + 1) * TQ)
            src = [4 * b + j for j in range(4)]
            mixin = np.concatenate([rm[s]["mo"][0:64, ts] for s in src] + [rm[s]["mo"][64:128, ts] for s in src]
                                   + [rm[s]["mo"][192:320, ts] for s in src])
            ssg = np.concatenate([rm[s]["mo"][128:192, ts] for s in src])
            ims.append({"mixin": np.ascontiguousarray(mixin), "ssg": np.ascontiguousarray(ssg), "hT": np.ascontiguousarray(hT[b][:, ts]),
                        "wout": np.asarray(inp["w_out"][l], f), "gluw": np.asarray(inp["s5_glu_w"][l], f),
                        "glub": np.ascontiguousarray(np.asarray(inp["s5_glu_b"][l], f).reshape(2, 128).T),
                        "fnw": np.ascontiguousarray(np.asarray(inp["final_norm_w"], f).reshape(8, 128).T)})
        ro = run_bass_kernel_spmd(nc, ims, core_ids=cores).results
        hT = [np.concatenate([ro[4 * b + tq]["hout"] for tq in range(4)], axis=1) for b in range(BATCH)]
    out = np.stack([hT[b].T for b in range(BATCH)]).astype(f)
    return np.ascontiguousarray(out)
```

```python
import contextlib
import math
import numpy as np
import ml_dtypes
import concourse.bass as bass
import concourse.mybir as mybir
from concourse.bass_utils import run_bass_kernel_spmd

F32 = mybir.dt.float32
BF16 = mybir.dt.bfloat16
I32 = mybir.dt.int32
AF = mybir.ActivationFunctionType
ALU = mybir.AluOpType
AX = mybir.AxisListType

D_MODEL = 1024
SEQ = 8192
BATCH = 2
DEPTH = 2
EPS = 1e-6
NCORES = 8
TQ = SEQ // 4
ROPE_THETA = 500000.0
import os
DBG = set(os.environ.get("KDBG", "").split(","))


class Sched:
    ENGS = ["pe", "act", "dve", "pool", "sp"]

    def __init__(self, nc):
        self.nc = nc
        self.ops = []
        self.last_w = {}
        self.readers = {}
        self.cnt = {e: 0 for e in self.ENGS}
        self.dma_cnt = {}
        self.base = set()

    def op(self, eng, fn, r=(), w=(), dma=False):
        deps = set(self.base)
        for x in r:
            if x in self.last_w:
                deps.add(self.last_w[x])
        for x in w:
            if x in self.last_w:
                deps.add(self.last_w[x])
            for d in self.readers.get(x, ()):
                deps.add(d)
        if dma:
            q = self.dma_cnt.get(eng, 0)
            self.dma_cnt[eng] = q + 1
            tok = ("dma", eng, q)
        else:
            self.cnt[eng] += 1
            tok = ("eng", eng, self.cnt[eng])
        self.ops.append((eng, fn, deps, tok))
        for x in w:
            self.last_w[x] = tok
            self.readers[x] = []
        for x in r:
            self.readers.setdefault(x, []).append(tok)
        return tok

    def barrier(self):
        b = set()
        for e in self.ENGS:
            if self.cnt[e] > 0:
                b.add(("eng", e, self.cnt[e]))
        for e, n in self.dma_cnt.items():
            for q in range(max(0, n - self.NSLOT), n):
                b.add(("dma", e, q))
        self.base = b
        self.last_w = {}
        self.readers = {}

    NSLOT = 8

    def emit(self):
        nc = self.nc
        NSLOT = self.NSLOT
        needed = set()
        for (eng, fn, deps, tok) in self.ops:
            for d in deps:
                if d[0] == "eng" and not (d[1] == "pe" and eng == "pe"):
                    needed.add(d)
        sig = {}
        run = {e: 0 for e in self.ENGS}
        for (eng, fn, deps, tok) in self.ops:
            if tok[0] == "eng":
                if tok in needed:
                    run[eng] += 1
                sig[tok] = run[eng]
        with contextlib.ExitStack() as st:
            esem = {e: st.enter_context(nc.semaphore("s_" + e)) for e in self.ENGS}
            dsem = {}
            for e in self.dma_cnt:
                dsem[e] = [st.enter_context(nc.semaphore(f"d_{e}_{i}")) for i in range(NSLOT)]
            block = st.enter_context(nc.Block())
            per = {e: [o for o in self.ops if o[0] == e] for e in self.ENGS}

            def mk(ename):
                def body(eng):
                    seen = {}

                    def wait(tok):
                        if tok[0] == "eng":
                            _, e2, n = tok
                            if e2 == "pe" and ename == "pe":
                                return
                            v = sig[tok]
                            key = ("eng", e2)
                            if seen.get(key, 0) >= v:
                                return
                            seen[key] = v
                            eng.wait_ge(esem[e2], v)
                        else:
                            _, e2, q = tok
                            slot = q % NSLOT
                            val = 16 * (q // NSLOT + 1)
                            key = ("dma", e2, slot)
                            if seen.get(key, 0) >= val:
                                return
                            seen[key] = val
                            eng.wait_ge(dsem[e2][slot], val)
                    for (_, fn, deps, tok) in per[ename]:
                        for d in sorted(deps):
                            wait(d)
                        if tok[0] == "dma":
                            q = tok[2]
                            if q >= NSLOT:
                                wait(("dma", ename, q - NSLOT))
                            ins = fn(eng)
                            ins.then_inc(dsem[ename][q % NSLOT], 16)
                        else:
                            ins = fn(eng)
                            if tok in needed:
                                ins.then_inc(esem[ename], 1)
                    n = self.dma_cnt.get(ename, 0)
                    for q in range(max(0, n - NSLOT), n):
                        wait(("dma", ename, q))
                return body
            block.tensor(mk("pe"))
            block.scalar(mk("act"))
            block.vector(mk("dve"))
            block.gpsimd(mk("pool"))
            block.sync(mk("sp"))


NFM = 704
NTM = 256
FM_CH = [(0, 128), (128, 128), (256, 64), (320, 128), (448, 128), (576, 128)]


def phase_inproj(nc, S, st, hT, wcat, nw, fm, tm_sf, tm_v, T=SEQ):
    TS = lambda n, s, d: st.enter_context(nc.sbuf_tensor(n, s, d))
    PS = lambda n: st.enter_context(nc.psum_tensor(n, [128, 512], F32))
    nw_sb = TS("ip_nw", [128, 8], F32)
    wst = [TS(f"ip_wst{i}", [128, NFM + NTM], F32) for i in range(2)]
    wall = TS("ip_wall", [128, 8, NFM + NTM], BF16)
    ones = TS("ip_ones", [128, 128], BF16)
    xin = [TS(f"ip_xin{i}", [128, 8, 512], F32) for i in range(2)]
    xsq = TS("ip_xsq", [128, 8, 512], BF16)
    sq = TS("ip_sq", [128, 512], F32)
    rstd = TS("ip_rstd", [128, 512], F32)
    xn = [TS(f"ip_xn{i}", [128, 8, 512], BF16) for i in range(2)]
    fmo = [TS(f"ip_fmo{i}", [128, 512], BF16) for i in range(6)]
    tsf = [TS(f"ip_tsf{i}", [128, 4, 64], F32) for i in range(2)]
    tv = [TS(f"ip_tv{i}", [128, 4, 192], BF16) for i in range(2)]
    ps_ss = PS("ip_ps_ss")
    ps_fm = [PS(f"ip_ps_fm{i}") for i in range(4)]
    ps_tm = [PS(f"ip_ps_tm{i}") for i in range(2)]

    S.op("sp", lambda e: e.dma_start(out=nw_sb[:], in_=nw[:, :]), w=["nw"], dma=True)
    S.op("pool", lambda e: e.memset(ones[:], 1.0), w=["ones"])
    for k in range(8):
        S.op("sp", lambda e, k=k: e.dma_start(out=wst[k % 2][:], in_=wcat[k * 128:(k + 1) * 128, :]),
             w=[("wst", k % 2)], dma=True)
        S.op("dve", lambda e, k=k: e.tensor_scalar(out=wall[:, k, :], in0=wst[k % 2][:], scalar1=nw_sb[:, k:k + 1],
                                                  scalar2=None, op0=ALU.mult),
             r=[("wst", k % 2), "nw"], w=[("wall", k)])
    wall_r = [("wall", k) for k in range(8)]
    hT_v = hT.rearrange("(k p) t -> p k t", p=128)
    fmi = 0
    NTI = T // 512

    def load(ti):
        b = ti % 2
        t0 = ti * 512
        S.op("pool", lambda e, b=b, t0=t0: e.dma_start(out=xin[b][:, 0:4, :], in_=hT_v[:, 0:4, t0:t0 + 512]),
             w=[("xin", b, 0)], dma=True)
        S.op("pool", lambda e, b=b, t0=t0: e.dma_start(out=xin[b][:, 4:8, :], in_=hT_v[:, 4:8, t0:t0 + 512]),
             w=[("xin", b, 1)], dma=True)
    def front_sq(ti):
        b = ti % 2
        xr = [("xin", b, 0), ("xin", b, 1)]
        S.op("act", lambda e, b=b: e.activation(out=xsq[:], in_=xin[b][:], func=AF.Square), r=xr, w=["xsq"])

    def front_ss(ti):
        b = ti % 2
        for k in range(8):
            S.op("pe", lambda e, k=k: e.matmul(ps_ss[:], ones[:], xsq[:, k, :], start=(k == 0), stop=(k == 7)),
                 r=["xsq", "ones"], w=["ps_ss"])
        S.op("act", lambda e: e.activation(out=sq[:], in_=ps_ss[:], func=AF.Sqrt, scale=1.0 / D_MODEL, bias=EPS),
             w=["ps_ss", "sq"])
        S.op("dve", lambda e: e.reciprocal(out=rstd[:], in_=sq[:]), r=["sq"], w=["rstd"])
        for hh in range(2):
            S.op("dve", lambda e, b=b, hh=hh: e.tensor_tensor(out=xn[b][:, 4 * hh:4 * hh + 4, :], in0=xin[b][:, 4 * hh:4 * hh + 4, :],
                                                         in1=rstd[:].unsqueeze(1).broadcast_to([128, 4, 512]), op=ALU.mult),
                 r=[("xin", b, hh), "rstd"], w=[("xn", b, hh)])
    load(0)
    if NTI > 1:
        load(1)
    front_sq(0)
    front_ss(0)
    for ti in range(NTI):
        b = ti % 2
        t0 = ti * 512
        if ti + 1 < NTI:
            front_sq(ti + 1)
        xnr = [("xn", b, 0), ("xn", b, 1)]
        for ci, (c0, cw) in enumerate(FM_CH):
            if "NOFM" in DBG or ("FM%d" % ci) in DBG:
                continue
            if ci == 3:
                if ti + 1 < NTI:
                    front_ss(ti + 1)
                if ti + 2 < NTI:
                    load(ti + 2)
            pb = fmi % 4
            for k in range(8):
                S.op("pe", lambda e, k=k, c0=c0, cw=cw, pb=pb, b=b: e.matmul(
                    ps_fm[pb][0:cw, :], wall[:, k, c0:c0 + cw], xn[b][:, k, :], start=(k == 0), stop=(k == 7)),
                    r=xnr + wall_r, w=[("ps_fm", pb)])
            fb = fmi % 6
            fmi += 1
            if ci == 0:
                S.op("act", lambda e, pb=pb, fb=fb: e.activation(out=fmo[fb][0:64, :], in_=ps_fm[pb][0:64, :], func=AF.Silu),
                     r=[("ps_fm", pb)], w=[("fmo", fb, 0)])
                S.op("act", lambda e, pb=pb, fb=fb: e.activation(out=fmo[fb][64:128, :], in_=ps_fm[pb][64:128, :],
                                                               func=AF.Sigmoid, scale=-1.0),
                     r=[("ps_fm", pb)], w=[("fmo", fb, 1)])
                wl = [("fmo", fb, 0), ("fmo", fb, 1)]
            elif ci in (1, 5):
                S.op("act", lambda e, pb=pb, fb=fb: e.activation(out=fmo[fb][:], in_=ps_fm[pb][:], func=AF.Silu),
                     r=[("ps_fm", pb)], w=[("fmo", fb, 0), ("fmo", fb, 1)])
                wl = [("fmo", fb, 0), ("fmo", fb, 1)]
            else:
                S.op("dve", lambda e, pb=pb, fb=fb, cw=cw: e.tensor_copy(out=fmo[fb][0:cw, :], in_=ps_fm[pb][0:cw, :]),
                     r=[("ps_fm", pb)], w=[("fmo", fb, 0), ("fmo", fb, 1)])
                wl = [("fmo", fb, 0), ("fmo", fb, 1)]
            S.op("sp", lambda e, fb=fb, c0=c0, cw=cw, t0=t0: e.dma_start(out=fm[c0:c0 + cw, t0:t0 + 512], in_=fmo[fb][0:cw, :]),
                 r=wl, dma=True)
        for pb in range(0 if "NOTM" in DBG else 2):
            for t4 in (2 * pb, 2 * pb + 1):
                off = (t4 % 2) * 256
                for k in range(8):
                    S.op("pe", lambda e, k=k, t4=t4, pb=pb, off=off, b=b: e.matmul(
                        ps_tm[pb][:, off:off + 256], xn[b][:, k, t4 * 128:(t4 + 1) * 128], wall[:, k, NFM:NFM + NTM],
                        start=(k == 0), stop=(k == 7)),
                        r=xnr + wall_r, w=[("ps_tm", pb)])
            if "TMNOEVAC" in DBG:
                continue
            for t4 in (2 * pb, 2 * pb + 1):
                off = (t4 % 2) * 256
                if "TMNOACT" not in DBG:
                  S.op("act", lambda e, pb=pb, b=b, t4=t4, off=off: e.activation(
                    out=tsf[b][:, t4, :], in_=ps_tm[pb][:, off:off + 64], func=AF.Sigmoid),
                    w=[("ps_tm", pb), ("tsf", b, t4)])
                if "TMNODVE" not in DBG:
                  S.op("dve", lambda e, pb=pb, b=b, t4=t4, off=off: e.tensor_copy(
                    out=tv[b][:, t4, :], in_=ps_tm[pb][:, off + 64:off + 256]),
                    w=[("ps_tm", pb), ("tv", b, t4)])
        if "TMNODMA" in DBG:
            continue
        S.op("sp", lambda e, b=b, t0=t0: e.dma_start(
            out=tm_sf[t0:t0 + 512, :].rearrange("(a p) c -> p a c", p=128), in_=tsf[b][:]),
            r=[("tsf", b, t4) for t4 in range(4)], dma=True)
        S.op("sp", lambda e, b=b, t0=t0: e.dma_start(
            out=tm_v[t0:t0 + 512, :].rearrange("(a p) c -> p a c", p=128), in_=tv[b][:]),
            r=[("tv", b, t4) for t4 in range(4)], dma=True)


def core_cols(j):
    r = lambda s, n: list(range(s, s + n))
    fmc = (r(0 + 64 * j, 64) + r(256 + 64 * j, 64) + r(768 + 64 * j, 64) + r(1280 + 64 * j, 64) + r(1024 + 64 * j, 64)
           + r(1536 + 128 * j, 128) + r(2048 + 128 * j, 128) + r(3072 + 128 * j, 128))
    tmc = r(256 + 64 * j, 64) + r(512 + 64 * j, 64) + r(2560 + 128 * j, 128)
    return np.array(fmc + tmc)


def build_inproj(T=SEQ):
    nc = bass.Bass("TRN2", target_bir_lowering=False)
    hT = nc.dram_tensor("hT", [D_MODEL, T], F32, kind="ExternalInput").ap()
    wcat = nc.dram_tensor("wcat", [D_MODEL, NFM + NTM], F32, kind="ExternalInput").ap()
    nw = nc.dram_tensor("nw", [128, 8], F32, kind="ExternalInput").ap()
    fm = nc.dram_tensor("fm", [NFM, T], BF16, kind="ExternalOutput").ap()
    tm_sf = nc.dram_tensor("tm_sf", [T, 64], F32, kind="ExternalOutput").ap()
    tm_v = nc.dram_tensor("tm_v", [T, 192], BF16, kind="ExternalOutput").ap()
    with contextlib.ExitStack() as st:
        S = Sched(nc)
        phase_inproj(nc, S, st, hT, wcat, nw, fm, tm_sf, tm_v, T)
        S.emit()
    return nc


C1_2PI = 6.28125
C2_2PI = 2.0 * math.pi - 6.28125


def sincos(S, ang, kf, ki, hs, sin_out, cos_out, tag, eng="dve"):
    a, k, h = tag + "ang", tag + "kf", tag + "hs"
    S.op(eng, lambda e: e.tensor_scalar(out=kf, in0=ang, scalar1=1.0 / (2.0 * math.pi), scalar2=None, op0=ALU.mult), r=[a], w=[k])
    S.op(eng, lambda e: e.tensor_copy(out=ki, in_=kf), r=[k], w=[tag + "ki"])
    S.op(eng, lambda e: e.tensor_copy(out=kf, in_=ki), r=[tag + "ki"], w=[k])
    S.op("dve", lambda e: e.scalar_tensor_tensor(out=ang, in0=kf, scalar=-C1_2PI, in1=ang, op0=ALU.mult, op1=ALU.add), r=[k], w=[a])
    S.op("dve", lambda e: e.scalar_tensor_tensor(out=ang, in0=kf, scalar=-C2_2PI, in1=ang, op0=ALU.mult, op1=ALU.add), r=[k], w=[a])
    S.op(eng, lambda e: e.tensor_scalar(out=ang, in0=ang, scalar1=math.pi, scalar2=-math.pi, op0=ALU.min, op1=ALU.max), w=[a])
    S.op("act", lambda e: e.activation(out=sin_out, in_=ang, func=AF.Sin), r=[a], w=[tag + "sin"])
    S.op("act", lambda e: e.activation(out=hs, in_=ang, func=AF.Sin, scale=0.5), r=[a], w=[h])
    S.op(eng, lambda e: e.tensor_tensor(out=hs, in0=hs, in1=hs, op=ALU.mult), w=[h])
    S.op(eng, lambda e: e.tensor_scalar(out=cos_out, in0=hs, scalar1=-2.0, scalar2=1.0, op0=ALU.mult, op1=ALU.add), r=[h], w=[tag + "cos"])


def rope_tables(nc, S, st, ropef, sinT, cosT, T, tag):
    TS = lambda n, s, d: st.enter_context(nc.sbuf_tensor(n, s, d))
    CH = min(512, T)
    NCH = T // CH
    pi_ = TS(tag + "_pi", [128, CH], I32)
    ang = TS(tag + "_ang", [128, CH], F32)
    kf = TS(tag + "_kf", [128, CH], F32)
    ki = TS(tag + "_ki", [128, CH], I32)
    hs = TS(tag + "_hs", [128, CH], F32)
    s1 = TS(tag + "_s1", [128, CH], F32)
    c1 = TS(tag + "_c1", [128, CH], F32)
    pj = TS(tag + "_pj", [128, NCH], I32)
    ang_b = TS(tag + "_angb", [128, NCH], F32)
    kf_b = TS(tag + "_kfb", [128, NCH], F32)
    ki_b = TS(tag + "_kib", [128, NCH], I32)
    hs_b = TS(tag + "_hsb", [128, NCH], F32)
    s2 = TS(tag + "_s2", [128, NCH], F32)
    c2 = TS(tag + "_c2", [128, NCH], F32)
    ns2 = TS(tag + "_ns2", [128, NCH], F32)
    tmp = [TS(tag + f"_tmp{i}", [128, CH], F32) for i in range(4)]
    S.op("pool", lambda e: e.iota(pi_[:], pattern=[[1, CH]], base=0, channel_multiplier=0), w=[tag + "pi"])
    S.op("pool", lambda e: e.iota(pj[:], pattern=[[CH, NCH]], base=0, channel_multiplier=0), w=[tag + "pj"])
    S.op("dve", lambda e: e.tensor_copy(out=ang[:], in_=pi_[:]), r=[tag + "pi"], w=[tag + "aang"])
    S.op("dve", lambda e: e.tensor_scalar(out=ang[:], in0=ang[:], scalar1=ropef[:, 0:1], scalar2=None, op0=ALU.mult), r=["ropef"], w=[tag + "aang"])
    sincos(S, ang[:], kf[:], ki[:], hs[:], s1[:], c1[:], tag + "a")
    S.op("dve", lambda e: e.tensor_copy(out=ang_b[:], in_=pj[:]), r=[tag + "pj"], w=[tag + "bang"])
    S.op("dve", lambda e: e.tensor_scalar(out=ang_b[:], in0=ang_b[:], scalar1=ropef[:, 0:1], scalar2=None, op0=ALU.mult), r=["ropef"], w=[tag + "bang"])
    sincos(S, ang_b[:], kf_b[:], ki_b[:], hs_b[:], s2[:], c2[:], tag + "b")
    S.op("dve", lambda e: e.tensor_scalar(out=ns2[:], in0=s2[:], scalar1=-1.0, scalar2=None, op0=ALU.mult), r=[tag + "bsin"], w=[tag + "ns2"])
    rd = [tag + "asin", tag + "acos", tag + "bsin", tag + "bcos", tag + "ns2"]
    for c in range(NCH):
        sl = slice(c * CH, (c + 1) * CH)
        ta, tb = tmp[(2 * c) % 4], tmp[(2 * c + 1) % 4]
        na, nb = (tag + "tmp", (2 * c) % 4), (tag + "tmp", (2 * c + 1) % 4)
        S.op("dve", lambda e, c=c, ta=ta: e.tensor_scalar(out=ta[:], in0=c1[:], scalar1=s2[:, c:c + 1], scalar2=None, op0=ALU.mult), r=rd, w=[na])
        S.op("dve", lambda e, c=c, ta=ta, sl=sl: e.scalar_tensor_tensor(out=sinT[:, sl], in0=s1[:], scalar=c2[:, c:c + 1], in1=ta[:],
                                                                        op0=ALU.mult, op1=ALU.add), r=rd + [na], w=[(tag + "sin", c)])
        S.op("dve", lambda e, c=c, tb=tb: e.tensor_scalar(out=tb[:], in0=s1[:], scalar1=ns2[:, c:c + 1], scalar2=None, op0=ALU.mult), r=rd, w=[nb])
        S.op("dve", lambda e, c=c, tb=tb, sl=sl: e.scalar_tensor_tensor(out=cosT[:, sl], in0=c1[:], scalar=c2[:, c:c + 1], in1=tb[:],
                                                                        op0=ALU.mult, op1=ALU.add), r=rd + [nb], w=[(tag + "cos", c)])
    return [(tag + "sin", c) for c in range(NCH)] + [(tag + "cos", c) for c in range(NCH)]


def phase_attn(nc, S, st, fm, tm_v, lqk, subln, ropef_d, rmat_d, cmask_d, oa, lambda_init, T=SEQ):
    TS = lambda n, s, d: st.enter_context(nc.sbuf_tensor(n, s, d))
    PS = lambda n: st.enter_context(nc.psum_tensor(n, [128, 512], F32))
    NQ = T // 512
    NK = T // 128
    ropef = TS("at_ropef", [128, 1], F32)
    rm32 = TS("at_rm32", [128, 128], F32)
    rm = TS("at_rm", [128, 128], BF16)
    cmask = TS("at_cmask", [128, 4, 512], BF16)
    ones = TS("at_ones", [128, 128], BF16)
    sinT = TS("at_sin", [128, T], BF16)
    cosT = TS("at_cos", [128, T], BF16)
    qraw = TS("at_qraw", [128, T], BF16)
    kraw = TS("at_kraw", [128, T], BF16)
    qr = TS("at_qr", [128, T], BF16)
    kr = TS("at_kr", [128, T], BF16)
    sag = TS("at_sag", [128, T], BF16)
    vsb = TS("at_v", [128, NK, 128], BF16)
    lq = TS("at_lq", [128, 256], F32)
    lp = TS("at_lp", [128, 128], F32)
    le = TS("at_le", [128, 2], F32)
    neglam = TS("at_neglam", [128, 1], F32)
    sw = TS("at_sw", [128, 1], F32)
    t1 = [TS(f"at_t1_{i}", [128, 512], BF16) for i in range(2)]
    t2 = [TS(f"at_t2_{i}", [128, 512], BF16) for i in range(2)]
    P = [[TS(f"at_P{i}_{m}", [128, 512], BF16) for m in range(2)] for i in range(3)]
    r0 = [TS(f"at_r0_{i}", [128, 512], F32) for i in range(2)]
    r1 = [TS(f"at_r1_{i}", [128, 512], F32) for i in range(2)]
    o0 = [TS(f"at_o0_{i}", [128, 512], F32) for i in range(2)]
    o1 = [TS(f"at_o1_{i}", [128, 512], F32) for i in range(2)]
    osq = TS("at_osq", [128, 512], BF16)
    nsq = TS("at_nsq", [128, 512], F32)
    ob = [TS(f"at_ob{i}", [128, 512], BF16) for i in range(2)]
    epsc = TS("at_epsc", [128, 1], F32)
    accD = TS("at_accD", [128, 512], F32)
    accP = TS("at_accP", [128, 512], F32)
    ones32 = TS("at_ones32", [128, 128], F32)
    ps_s = [[PS(f"at_ps_s{i}_{m}") for m in range(2)] for i in range(2)]
    ps_o = [PS(f"at_ps_o{m}") for m in range(2)]
    ps_l = [PS(f"at_ps_l{m}") for m in range(2)]
    ps_n = ps_s[0][0]

    S.op("sp", lambda e: e.dma_start(out=ropef[:], in_=ropef_d[:, :]), w=["ropef"], dma=True)
    S.op("sp", lambda e: e.dma_start(out=rm32[:], in_=rmat_d[:, :]), w=["rm32"], dma=True)
    S.op("sp", lambda e: e.dma_start(out=cmask[:], in_=cmask_d.rearrange("d p q -> p d q")), w=["cmask"], dma=True)
    S.op("sp", lambda e: e.dma_start(out=lq[:], in_=lqk.partition_broadcast(128)), w=["lq"], dma=True)
    S.op("sp", lambda e: e.dma_start(out=sw[:], in_=subln[:, :]), w=["sw"], dma=True)
    S.op("sp", lambda e: e.dma_start(out=qraw[:], in_=fm[320:448, :]), w=["qraw"], dma=True)
    S.op("sp", lambda e: e.dma_start(out=kraw[:], in_=fm[448:576, :]), w=["kraw"], dma=True)
    S.op("sp", lambda e: e.dma_start(out=sag[:], in_=fm[576:704, :]), w=["sag"], dma=True)
    S.op("pool", lambda e: e.dma_start(out=vsb[:], in_=tm_v[:, 64:192].rearrange("(a p) c -> p a c", p=128)), w=["vsb"], dma=True)
    S.op("pool", lambda e: e.memset(ones[:], 1.0), w=["ones"])
    S.op("pool", lambda e: e.memset(epsc[:], EPS), w=["epsc"])
    S.op("pool", lambda e: e.memset(ones32[:], 1.0), w=["ones32"])
    S.op("dve", lambda e: e.tensor_copy(out=rm[:], in_=rm32[:]), r=["rm32"], w=["rm"])
    S.op("dve", lambda e: e.tensor_tensor(out=lp[:], in0=lq[:, 0:128], in1=lq[:, 128:256], op=ALU.mult), r=["lq"], w=["lp"])
    S.op("dve", lambda e: e.tensor_reduce(out=le[:], in_=lp[:].rearrange("p (a c) -> p a c", a=2), axis=AX.X, op=ALU.add),
         r=["lp"], w=["le"])
    S.op("act", lambda e: e.activation(out=le[:], in_=le[:], func=AF.Exp), w=["le"])
    S.op("dve", lambda e: e.tensor_tensor(out=neglam[:], in0=le[:, 1:2], in1=le[:, 0:1], op=ALU.subtract), r=["le"], w=["neglam"])
    S.op("dve", lambda e: e.tensor_scalar(out=neglam[:], in0=neglam[:], scalar1=-lambda_init, scalar2=None, op0=ALU.add), w=["neglam"])
    S.op("dve", lambda e: e.tensor_scalar(out=sw[:], in0=sw[:], scalar1=1.0 - lambda_init, scalar2=None, op0=ALU.mult), w=["sw"])
    tabs = rope_tables(nc, S, st, ropef, sinT, cosT, T, "at_rp")
    ri = 0
    for src, dst, sn, dn in ((qraw, qr, "qraw", "qr"), (kraw, kr, "kraw", "kr")):
        for ti in range(NQ):
            sl = slice(ti * 512, (ti + 1) * 512)
            b = ri % 2
            ri += 1
            S.op("pe", lambda e, src=src, sl=sl, b=b: e.matmul(ps_s[b][0][:], rm[:], src[:, sl], start=True, stop=True),
                 r=[sn, "rm"], w=[f"ps_s{b}0"])
            S.op("dve", lambda e, src=src, sl=sl, b=b: e.tensor_tensor(out=t1[b][:], in0=src[:, sl], in1=cosT[:, sl], op=ALU.mult),
                 r=[sn] + tabs, w=[("t1", b)])
            S.op("dve", lambda e, sl=sl, b=b: e.tensor_tensor(out=t2[b][:], in0=ps_s[b][0][:], in1=sinT[:, sl], op=ALU.mult),
                 r=tabs, w=[("t2", b), f"ps_s{b}0"])
            S.op("pool", lambda e, dst=dst, sl=sl, b=b: e.tensor_tensor(out=dst[:, sl], in0=t1[b][:], in1=t2[b][:], op=ALU.add),
                 r=[("t1", b), ("t2", b)], w=[(dn, ti)])
    qr_all = [("qr", ti) for ti in range(NQ)]
    kr_all = [("kr", ti) for ti in range(NQ)]
    psn = lambda i, m: f"ps_s{i}{m}"
    pending = None
    pending_a = None
    for qi in range(NQ):
        qs = slice(qi * 512, (qi + 1) * 512)
        nk = 4 * (qi + 1)

        def QK(n):
            i = n % 2
            for m in range(2):
                S.op("pe", lambda e, n=n, i=i, m=m, qs=qs: e.matmul(ps_s[i][m][:], kr[64 * m:64 * m + 64, n * 128:(n + 1) * 128],
                                                         qr[64 * m:64 * m + 64, qs], start=True, stop=True),
                     r=[("qr", qi), ("kr", n // 4)], w=[psn(i, m)])

        def EXP(n):
            i = n % 2
            j = n % 3
            for m in range(2):
                S.op("act", lambda e, i=i, j=j, m=m: e.activation(out=P[j][m][:], in_=ps_s[i][m][:], func=AF.Exp, scale=0.125),
                     w=[psn(i, m), ("P", j, m)])
                d = n - 4 * qi
                if d >= 0:
                    S.op("dve", lambda e, j=j, m=m, d=d: e.tensor_tensor(out=P[j][m][:], in0=P[j][m][:], in1=cmask[:, d, :], op=ALU.mult),
                         r=["cmask"], w=[("P", j, m)])

        def PV(n, nk=nk):
            j = n % 3
            for m in range(2):
                S.op("pe", lambda e, n=n, j=j, m=m, nk=nk: e.matmul(ps_o[m][:], vsb[:, n, :], P[j][m][:], start=(n == 0), stop=(n == nk - 1)),
                     r=[("P", j, m), "vsb"], w=[f"ps_o{m}"])
            S.op("pe", lambda e, n=n, j=j, nk=nk: e.matmul(ps_l[0][:], ones[:], P[j][0][:], start=(n == 0), stop=(n == nk - 1)),
                 r=[("P", j, 0), "ones"], w=["ps_l0"])
            eng, acc, an = ("dve", accD, "accD") if n % 2 == 0 else ("pool", accP, "accP")
            if n < 2:
                S.op(eng, lambda e, j=j, acc=acc: e.tensor_copy(out=acc[:], in_=P[j][1][:]), r=[("P", j, 1)], w=[an])
            else:
                S.op(eng, lambda e, j=j, acc=acc: e.tensor_tensor(out=acc[:], in0=acc[:], in1=P[j][1][:], op=ALU.add), r=[("P", j, 1)], w=[an])
        QK(0)
        for n in range(nk):
            if n == 1 and pending_a is not None:
                pending_a()
                pending_a = None
            if n == min(7, nk - 3) and pending is not None:
                pending()
                pending = None
            if n + 1 < nk:
                QK(n + 1)
            EXP(n)
            PV(n)
        eb = qi % 2
        S.op("dve", lambda e, eb=eb: e.tensor_copy(out=o0[eb][:], in_=ps_o[0][:]), w=[("o0", eb), "ps_o0"])
        S.op("dve", lambda e, eb=eb: e.tensor_copy(out=o1[eb][:], in_=ps_o[1][:]), w=[("o1", eb), "ps_o1"])
        S.op("dve", lambda e, eb=eb: e.tensor_copy(out=r0[eb][:], in_=ps_l[0][:]), w=[("r0", eb), "ps_l0"])
        S.op("pe", lambda e: e.matmul(ps_l[1][:], ones32[:], accD[:], start=True, stop=False), r=["accD", "ones32"], w=["ps_l1"])
        S.op("pe", lambda e: e.matmul(ps_l[1][:], ones32[:], accP[:], start=False, stop=True), r=["accP", "ones32"], w=["ps_l1"])
        S.op("dve", lambda e, eb=eb: e.tensor_copy(out=r1[eb][:], in_=ps_l[1][:]), w=[("r1", eb), "ps_l1"])

        def E2a(eb=eb):
            for rr, rn in ((r0, "r0"), (r1, "r1")):
                S.op("act", lambda e, eb=eb, rr=rr: e.activation(out=rr[eb][:], in_=rr[eb][:], func=AF.Ln), w=[(rn, eb)])
                S.op("act", lambda e, eb=eb, rr=rr: e.activation(out=rr[eb][:], in_=rr[eb][:], func=AF.Exp, scale=-1.0), w=[(rn, eb)])
            S.op("dve", lambda e, eb=eb: e.tensor_tensor(out=o0[eb][:], in0=o0[eb][:], in1=r0[eb][:], op=ALU.mult), r=[("r0", eb)], w=[("o0", eb)])
            S.op("dve", lambda e, eb=eb: e.tensor_tensor(out=o1[eb][:], in0=o1[eb][:], in1=r1[eb][:], op=ALU.mult), r=[("r1", eb)], w=[("o1", eb)])
            S.op("dve", lambda e, eb=eb: e.scalar_tensor_tensor(out=o0[eb][:], in0=o1[eb][:], scalar=neglam[:, 0:1], in1=o0[eb][:],
                                                                op0=ALU.mult, op1=ALU.add), r=[("o1", eb), "neglam"], w=[("o0", eb)])
            S.op("act", lambda e, eb=eb: e.activation(out=osq[:], in_=o0[eb][:], func=AF.Square), r=[("o0", eb)], w=["osq"])

        def E2(eb=eb, qi=qi, qs=qs):
            S.op("pe", lambda e: e.matmul(ps_n[:], ones[:], osq[:], start=True, stop=True), r=["osq", "ones"], w=[psn(0, 0)])
            S.op("act", lambda e: e.activation(out=nsq[:], in_=ps_n[:], func=AF.Ln, scale=1.0 / 128.0, bias=epsc[:, 0:1]), r=["epsc"], w=["nsq", psn(0, 0)])
            S.op("act", lambda e: e.activation(out=nsq[:], in_=nsq[:], func=AF.Exp, scale=-0.5), w=["nsq"])
            S.op("dve", lambda e: e.scalar_tensor_tensor(out=o0[eb][:], in0=o0[eb][:], scalar=sw[:, 0:1], in1=nsq[:], op0=ALU.mult, op1=ALU.mult),
                 r=["nsq", "sw"], w=[("o0", eb)])
            S.op("pool", lambda e: e.tensor_tensor(out=ob[eb][:], in0=o0[eb][:], in1=sag[:, qs], op=ALU.mult),
                 r=[("o0", eb), "sag"], w=[("ob", eb)])
            S.op("sp", lambda e: e.dma_start(out=oa[:, qs], in_=ob[eb][:]), r=[("ob", eb)], dma=True)
        pending_a = E2a
        pending = E2
    if pending_a is not None:
        pending_a()
    if pending is not None:
        pending()


def attn_consts():
    ropef = np.zeros((128, 1), np.float32)
    inv = (ROPE_THETA ** (-np.arange(0, 16, 2, dtype=np.float32) / 16.0)).astype(np.float32)
    rmat = np.zeros((128, 128), np.float32)
    for base in (0, 64):
        for i in range(8):
            ropef[base + i, 0] = -inv[i]
            ropef[base + 8 + i, 0] = inv[i]
            rmat[base + 8 + i, base + i] = 1.0
            rmat[base + i, base + 8 + i] = 1.0
    k = np.arange(128)[:, None]
    q = np.arange(512)[None, :]
    cmask = np.stack([(128 * d + k <= q) for d in range(4)]).astype(ml_dtypes.bfloat16)
    return ropef, rmat, cmask


def build_attn(lambda_init, T=SEQ):
    nc = bass.Bass("TRN2", target_bir_lowering=False)
    fm = nc.dram_tensor("fm", [NFM, T], BF16, kind="ExternalInput").ap()
    tm_v = nc.dram_tensor("tm_v", [T, 192], BF16, kind="ExternalInput").ap()
    lqk = nc.dram_tensor("lqk", [1, 256], F32, kind="ExternalInput").ap()
    subln = nc.dram_tensor("subln", [128, 1], F32, kind="ExternalInput").ap()
    ropef = nc.dram_tensor("ropef", [128, 1], F32, kind="ExternalInput").ap()
    rmat = nc.dram_tensor("rmat", [128, 128], F32, kind="ExternalInput").ap()
    cmask = nc.dram_tensor("cmask", [4, 128, 512], BF16, kind="ExternalInput").ap()
    oa = nc.dram_tensor("oa", [128, T], BF16, kind="ExternalOutput").ap()
    with contextlib.ExitStack() as st:
        S = Sched(nc)
        phase_attn(nc, S, st, fm, tm_v, lqk, subln, ropef, rmat, cmask, oa, lambda_init, T)
        S.emit()
    return nc


def hgrn_consts():
    s = np.arange(128)[:, None]
    t = np.arange(128)[None, :]
    same = (s // 16) == (t // 16)
    m_incl = (same & (s <= t)).astype(np.float32)
    m_rev = (same & (s > t)).astype(np.float32)
    m_tot8 = ((s // 16) == np.arange(8)[None, :]).astype(np.float32)
    mcat = np.concatenate([m_incl, m_tot8], axis=1)
    return mcat, m_rev


def phase_hgrn(nc, S, st, fm, tm_sf, tm_v, lbl_bc_d, lbl_col_d, gw_d, mcat_d, mrev_d, oh, lb_coef, T=SEQ):
    TS = lambda n, s, d: st.enter_context(nc.sbuf_tensor(n, s, d))
    PS = lambda n: st.enter_context(nc.psum_tensor(n, [128, 512], F32))
    NT = T // 128
    sq = TS("hg_sq", [64, T], BF16)
    snf = TS("hg_snf", [64, T], BF16)
    shg = TS("hg_shg", [64, T], BF16)
    sf = TS("hg_sf", [128, NT, 64], F32)
    omf = TS("hg_omf", [128, NT, 64], F32)
    logf = TS("hg_logf", [128, NT, 64], F32)
    vall = TS("hg_v", [128, NT, 64], BF16)
    lbl = TS("hg_lbl", [128, 128], F32)
    lb_bc = TS("hg_lb_bc", [128, 64], F32)
    oml_bc = TS("hg_oml_bc", [128, 64], F32)
    lblc = TS("hg_lblc", [64, 2], F32)
    oml_col = TS("hg_oml_col", [64, 1], F32)
    gw = TS("hg_gw", [64, 1], F32)
    mcat = TS("hg_mcat", [128, 136], F32)
    mrev = TS("hg_mrev", [128, 128], F32)
    mincl_bf = TS("hg_mincl", [128, 128], BF16)
    mtot_bf = TS("hg_mtot", [128, 8], BF16)
    ones64 = TS("hg_ones", [64, 64], BF16)
    eq = TS("hg_eq", [64, 128], F32)
    ekn = TS("hg_ekn", [64, 128], F32)
    dec = [TS(f"hg_dec{i}", [64, 8], F32) for i in range(2)]
    ehat = TS("hg_ehat", [128, 64], F32)
    qt = [TS(f"hg_qt{i}", [64, 128], BF16) for i in range(2)]
    kt = TS("hg_kt", [64, 128], BF16)
    khat = TS("hg_khat", [128, 64], BF16)
    vblk = TS("hg_vblk", [128, 8, 64], BF16)
    scm = TS("hg_scm", [128, 128], BF16)
    Sall = [TS(f"hg_S{i}", [64, 9, 64], F32) for i in range(2)]
    Sbf = [TS(f"hg_Sbf{i}", [64, 8, 64], BF16) for i in range(2)]
    osq = TS("hg_osq", [64, 512], BF16)
    o32 = TS("hg_o32", [64, 512], F32)
    nsq = TS("hg_nsq", [64, 512], F32)
    ohb = [TS(f"hg_ohb{i}", [64, 512], BF16) for i in range(2)]
    ps_c = PS("hg_ps_c")
    ps_r = PS("hg_ps_r")
    ps_sc = PS("hg_ps_sc")
    ps_u = [PS(f"hg_ps_u{i}") for i in range(2)]
    ps_oh = [PS(f"hg_ps_oh{i}") for i in range(2)]
    ps_n = PS("hg_ps_n")

    S.op("sp", lambda e: e.dma_start(out=sq[:], in_=fm[0:64, :]), w=["sq"], dma=True)
    S.op("sp", lambda e: e.dma_start(out=snf[:], in_=fm[64:128, :]), w=["snf"], dma=True)
    S.op("sp", lambda e: e.dma_start(out=shg[:], in_=fm[128:192, :]), w=["shg"], dma=True)
    S.op("sp", lambda e: e.dma_start(out=sf[:], in_=tm_sf.rearrange("(a p) c -> p a c", p=128)), w=["sf"], dma=True)
    S.op("pool", lambda e: e.dma_start(out=vall[:], in_=tm_v[:, 0:64].rearrange("(a p) c -> p a c", p=128)), w=["vall"], dma=True)
    S.op("sp", lambda e: e.dma_start(out=lbl[:], in_=lbl_bc_d.partition_broadcast(128)), w=["lbl"], dma=True)
    S.op("sp", lambda e: e.dma_start(out=lblc[:], in_=lbl_col_d[:, :]), w=["lblc"], dma=True)
    S.op("sp", lambda e: e.dma_start(out=gw[:], in_=gw_d[:, :]), w=["gw"], dma=True)
    S.op("sp", lambda e: e.dma_start(out=mcat[:], in_=mcat_d[:, :]), w=["mcat"], dma=True)
    S.op("sp", lambda e: e.dma_start(out=mrev[:], in_=mrev_d[:, :]), w=["mrev"], dma=True)
    S.op("pool", lambda e: e.memset(ones64[:], 1.0), w=["ones64"])
    S.op("pool", lambda e: e.memset(Sall[0][:, 0, :], 0.0), w=[("S", 0)])
    S.op("dve", lambda e: e.tensor_copy(out=mincl_bf[:], in_=mcat[:, 0:128]), r=["mcat"], w=["mincl_bf"])
    S.op("dve", lambda e: e.tensor_copy(out=mtot_bf[:], in_=mcat[:, 128:136]), r=["mcat"], w=["mtot_bf"])
    S.op("dve", lambda e: e.tensor_tensor(out=lb_bc[:], in0=lbl[:, 64:128], in1=lbl[:, 0:64], op=ALU.subtract), r=["lbl"], w=["lb_bc"])
    S.op("act", lambda e: e.activation(out=lb_bc[:], in_=lb_bc[:], func=AF.Sigmoid), w=["lb_bc"])
    S.op("dve", lambda e: e.tensor_scalar(out=lb_bc[:], in0=lb_bc[:], scalar1=float(lb_coef), scalar2=None, op0=ALU.mult), w=["lb_bc"])
    S.op("dve", lambda e: e.tensor_scalar(out=oml_bc[:], in0=lb_bc[:], scalar1=-1.0, scalar2=1.0, op0=ALU.mult, op1=ALU.add),
         r=["lb_bc"], w=["oml_bc"])
    S.op("dve", lambda e: e.tensor_tensor(out=oml_col[:], in0=lblc[:, 1:2], in1=lblc[:, 0:1], op=ALU.subtract), r=["lblc"], w=["oml_col"])
    S.op("act", lambda e: e.activation(out=oml_col[:], in_=oml_col[:], func=AF.Sigmoid), w=["oml_col"])
    S.op("dve", lambda e: e.tensor_scalar(out=oml_col[:], in0=oml_col[:], scalar1=-float(lb_coef), scalar2=1.0, op0=ALU.mult, op1=ALU.add),
         w=["oml_col"])
    for g in range(NT // 8 if NT >= 8 else 1):
        nt = min(8, NT)
        sl = slice(g * 8, g * 8 + nt)
        S.op("dve", lambda e, sl=sl, nt=nt: e.tensor_tensor(out=sf[:, sl, :], in0=sf[:, sl, :],
                                                        in1=oml_bc[:].unsqueeze(1).broadcast_to([128, nt, 64]), op=ALU.mult),
             r=["oml_bc"], w=["sf"])
        S.op("dve", lambda e, sl=sl, nt=nt: e.tensor_tensor(out=sf[:, sl, :], in0=sf[:, sl, :],
                                                        in1=lb_bc[:].unsqueeze(1).broadcast_to([128, nt, 64]), op=ALU.add),
             r=["lb_bc"], w=["sf"])
        S.op("act", lambda e, sl=sl: e.activation(out=logf[:, sl, :], in_=sf[:, sl, :], func=AF.Ln), r=["sf"], w=[("logf", g)])
        S.op("pool", lambda e, sl=sl: e.tensor_scalar(out=omf[:, sl, :], in0=sf[:, sl, :], scalar1=-1.0, scalar2=1.0,
                                                     op0=ALU.mult, op1=ALU.add), r=["sf"], w=[("omf", g)])
    for i in range(NT):
        g = i // 8
        b = i % 2
        ts = slice(i * 128, (i + 1) * 128)
        ob = (i // 4) % 2
        S.op("pe", lambda e, i=i: e.matmul(ps_c[0:64, 0:136], logf[:, i, :], mcat[:], start=True, stop=True),
             r=[("logf", g), "mcat"], w=["ps_c"])
        S.op("pe", lambda e, i=i: e.matmul(ps_r[:, 0:64], mrev[:], logf[:, i, :], start=True, stop=True),
             r=[("logf", g), "mrev"], w=["ps_r"])
        S.op("act", lambda e: e.activation(out=eq[:], in_=ps_c[0:64, 0:128], func=AF.Exp), w=["eq", "ps_c"])
        S.op("act", lambda e: e.activation(out=ekn[:], in_=ps_c[0:64, 0:128], func=AF.Exp, scale=-1.0), w=["ekn", "ps_c"])
        S.op("act", lambda e, b=b: e.activation(out=dec[b][:], in_=ps_c[0:64, 128:136], func=AF.Exp), w=[("dec", b), "ps_c"])
        S.op("act", lambda e: e.activation(out=ehat[:], in_=ps_r[:, 0:64], func=AF.Exp), w=["ehat", "ps_r"])
        S.op("dve", lambda e, b=b, ts=ts: e.tensor_tensor(out=qt[b][:], in0=sq[:, ts], in1=eq[:], op=ALU.mult),
             r=["sq", "eq"], w=[("qt", b)])
        S.op("dve", lambda e, ts=ts: e.scalar_tensor_tensor(out=kt[:], in0=snf[:, ts], scalar=oml_col[:, 0:1], in1=ekn[:],
                                                           op0=ALU.mult, op1=ALU.mult), r=["snf", "ekn", "oml_col"], w=["kt"])
        S.op("dve", lambda e, i=i: e.tensor_tensor(out=khat[:], in0=omf[:, i, :], in1=ehat[:], op=ALU.mult),
             r=[("omf", g), "ehat"], w=["khat"])
        S.op("pool", lambda e, i=i: e.tensor_tensor(out=vblk[:], in0=vall[:, i, :].unsqueeze(1).broadcast_to([128, 8, 64]),
                                                   in1=mtot_bf[:].unsqueeze(2).broadcast_to([128, 8, 64]), op=ALU.mult),
             r=["vall", "mtot_bf"], w=["vblk"])
        S.op("pe", lambda e, b=b: e.matmul(ps_sc[:, 0:128], kt[:], qt[b][:], start=True, stop=True), r=["kt", ("qt", b)], w=["ps_sc"])
        S.op("dve", lambda e: e.tensor_tensor(out=scm[:], in0=ps_sc[:, 0:128], in1=mincl_bf[:], op=ALU.mult),
             r=["mincl_bf"], w=["scm", "ps_sc"])
        S.op("pe", lambda e, b=b: e.matmul(ps_u[b][0:64, :], khat[:], vblk[:].rearrange("p a c -> p (a c)"), start=True, stop=True),
             r=["khat", "vblk"], w=[("ps_u", b)])
        if i > 0:
            S.op("dve", lambda e, b=b: e.tensor_copy(out=Sall[b][:, 0, :], in_=Sall[1 - b][:, 8, :]), r=[("S", 1 - b)], w=[("S", b)])
        for n in range(8):
            S.op("dve", lambda e, b=b, n=n: e.scalar_tensor_tensor(
                out=Sall[b][:, n + 1, :], in0=Sall[b][:, n, :], scalar=dec[b][:, n:n + 1], in1=ps_u[b][0:64, n * 64:(n + 1) * 64],
                op0=ALU.mult, op1=ALU.add), r=[("dec", b)], w=[("S", b), ("ps_u", b)])
        S.op("act", lambda e, b=b: e.activation(out=Sbf[b][:], in_=Sall[b][:, 0:8, :], func=AF.Copy), r=[("S", b)], w=[("Sbf", b)])
        c0 = (i % 4) * 128
        S.op("pe", lambda e, i=i, ob=ob, c0=c0: e.matmul(ps_oh[ob][0:64, c0:c0 + 128], vall[:, i, :], scm[:], start=True, stop=False),
             r=["vall", "scm"], w=[("ps_oh", ob)])
        for n in range(8):
            S.op("pe", lambda e, b=b, ob=ob, c0=c0, n=n: e.matmul(
                ps_oh[ob][0:64, c0 + 16 * n:c0 + 16 * n + 16], Sbf[b][:, n, :], qt[b][:, 16 * n:16 * n + 16],
                start=False, stop=(n == 7)), r=[("Sbf", b), ("qt", b)], w=[("ps_oh", ob)])
        if i % 4 == 3 or i == NT - 1:
            qs = slice((i // 4) * 512, (i // 4) * 512 + 512)
            S.op("act", lambda e, ob=ob: e.activation(out=osq[:], in_=ps_oh[ob][0:64, :], func=AF.Square), w=["osq", ("ps_oh", ob)])
            S.op("act", lambda e, ob=ob: e.activation(out=o32[:], in_=ps_oh[ob][0:64, :], func=AF.Copy), w=["o32", ("ps_oh", ob)])
            S.op("pe", lambda e: e.matmul(ps_n[0:64, :], ones64[:], osq[:], start=True, stop=True), r=["osq", "ones64"], w=["ps_n"])
            S.op("act", lambda e: e.activation(out=nsq[:], in_=ps_n[0:64, :], func=AF.Sqrt, scale=1.0 / 64.0, bias=EPS), w=["nsq", "ps_n"])
            S.op("dve", lambda e: e.reciprocal(out=nsq[:], in_=nsq[:]), w=["nsq"])
            S.op("dve", lambda e: e.scalar_tensor_tensor(out=o32[:], in0=o32[:], scalar=gw[:, 0:1], in1=nsq[:], op0=ALU.mult, op1=ALU.mult),
                 r=["nsq", "gw"], w=["o32"])
            S.op("pool", lambda e, ob=ob, qs=qs: e.tensor_tensor(out=ohb[ob][:], in0=o32[:], in1=shg[:, qs], op=ALU.mult),
                 r=["o32", "shg"], w=[("ohb", ob)])
            S.op("sp", lambda e, ob=ob, qs=qs: e.dma_start(out=oh[:, qs], in_=ohb[ob][:]), r=[("ohb", ob)], dma=True)


def build_hgrn(lb_coef, T=SEQ):
    nc = bass.Bass("TRN2", target_bir_lowering=False)
    fm = nc.dram_tensor("fm", [NFM, T], BF16, kind="ExternalInput").ap()
    tm_sf = nc.dram_tensor("tm_sf", [T, 64], F32, kind="ExternalInput").ap()
    tm_v = nc.dram_tensor("tm_v", [T, 192], BF16, kind="ExternalInput").ap()
    lbl_bc = nc.dram_tensor("lbl_bc", [1, 128], F32, kind="ExternalInput").ap()
    lbl_col = nc.dram_tensor("lbl_col", [64, 2], F32, kind="ExternalInput").ap()
    gw = nc.dram_tensor("gw", [64, 1], F32, kind="ExternalInput").ap()
    mcat = nc.dram_tensor("mcat", [128, 136], F32, kind="ExternalInput").ap()
    mrev = nc.dram_tensor("mrev", [128, 128], F32, kind="ExternalInput").ap()
    oh = nc.dram_tensor("oh", [64, T], BF16, kind="ExternalOutput").ap()
    with contextlib.ExitStack() as st:
        S = Sched(nc)
        phase_hgrn(nc, S, st, fm, tm_sf, tm_v, lbl_bc, lbl_col, gw, mcat, mrev, oh, lb_coef, T)
        S.emit()
    return nc


def cmul(S, eng, o_re, o_im, a_re, a_im, b_re, b_im, t0, t1, rd, wr, conj_a=False):
    sg = -1.0 if conj_a else 1.0
    S.op(eng, lambda e: e.tensor_tensor(out=t0, in0=a_im, in1=b_im, op=ALU.mult), r=rd, w=[wr + "t0"])
    S.op(eng, lambda e: e.tensor_tensor(out=t1, in0=a_re, in1=b_re, op=ALU.mult), r=rd, w=[wr + "t1"])
    S.op("dve", lambda e: e.scalar_tensor_tensor(out=o_re, in0=t0, scalar=-sg, in1=t1, op0=ALU.mult, op1=ALU.add),
         r=[wr + "t0", wr + "t1"], w=[wr + "re"])
    S.op(eng, lambda e: e.tensor_tensor(out=t0, in0=a_im, in1=b_re, op=ALU.mult), r=rd + [wr + "re"], w=[wr + "t0"])
    S.op(eng, lambda e: e.tensor_tensor(out=t1, in0=a_re, in1=b_im, op=ALU.mult), r=rd + [wr + "re"], w=[wr + "t1"])
    S.op("dve", lambda e: e.scalar_tensor_tensor(out=o_im, in0=t0, scalar=sg, in1=t1, op0=ALU.mult, op1=ALU.add),
         r=[wr + "t0", wr + "t1"], w=[wr + "im"])


def s5_consts():
    negsig = np.repeat(-np.arange(16, dtype=np.float32), 64)[None, :]
    kidx = np.arange(32, dtype=np.float32)[None, :]
    midx = np.arange(1, 513, dtype=np.float32)[None, :]
    rowmask = (np.arange(64)[:, None] // 16 == np.arange(4)[None, :]).astype(np.float32)
    return negsig, kidx, midx, rowmask


def s5_params(z, l, j):
    gs = [4 * j + gl for gl in range(4)]
    f = np.float32
    pA_are = np.concatenate([np.repeat(z["s5_a_re"][l][g][None, :], 16, 0) for g in gs]).astype(f)
    pA_aim = np.concatenate([np.repeat(z["s5_a_im"][l][g][None, :], 16, 0) for g in gs]).astype(f)
    pA_ldt = np.concatenate([np.full((16, 1), z["s5_log_dt"][l][g]) for g in gs]).astype(f)
    pA_bre = np.concatenate([z["s5_b_re"][l][g].T for g in gs]).astype(f)
    pA_bim = np.concatenate([z["s5_b_im"][l][g].T for g in gs]).astype(f)
    pB = np.zeros((2, 128, 3), f)
    pB_cre = np.zeros((2, 128, 64), f)
    pB_cim = np.zeros((2, 128, 64), f)
    for q in range(2):
        for h in range(2):
            gl = 2 * q + h
            g = gs[gl]
            rows = slice(64 * h, 64 * h + 64)
            pB[q, rows, 0] = z["s5_a_re"][l][g]
            pB[q, rows, 1] = z["s5_a_im"][l][g]
            pB[q, rows, 2] = z["s5_log_dt"][l][g]
            pB_cre[q, rows, 16 * gl:16 * gl + 16] = z["s5_c_re"][l][g].T
            pB_cim[q, rows, 16 * gl:16 * gl + 16] = z["s5_c_im"][l][g].T
    dcol = z["s5_d"][l][64 * j:64 * j + 64][:, None].astype(f)
    pA = np.concatenate([pA_are, pA_aim, pA_bre, pA_bim, pA_ldt], axis=1)
    return {"s5p_pA": np.ascontiguousarray(pA), "s5p_pB": pB, "s5p_cre": pB_cre, "s5p_cim": pB_cim, "s5p_d": dcol}


def phase_s5(nc, S, st, fm, pA_d, pB_d, cre_d, cim_d, dcol_d, negsig_d, kidx_d, midx_d, rowmask_d, yg, T=SEQ):
    TS = lambda n, s, d: st.enter_context(nc.sbuf_tensor(n, s, d))
    PS = lambda n: st.enter_context(nc.psum_tensor(n, [128, 512], F32))
    NB = T // 16
    su = TS("s5_su", [64, T], BF16)
    outsb = TS("s5_out", [64, T], BF16)
    pA = TS("s5_pA", [64, 257], F32)
    dcol = TS("s5_dcol", [64, 1], F32)
    rowmask = TS("s5_rowmask", [64, 4], F32)
    negsig = TS("s5_negsig", [64, 1024], F32)
    SCR = TS("s5_scr", [128, 8192], F32)
    tA = [SCR[0:64, 1024 * i:1024 * (i + 1)] for i in range(8)]
    tAi = TS("s5_tAi", [64, 1024], I32)
    sA = [TS(f"s5_sA{i}", [64, 64], F32) for i in range(10)]
    sAi = TS("s5_sAi", [64, 64], I32)
    dtA = TS("s5_dtA", [64, 1], F32)
    W1tab = [[TS(f"s5_W1tab{q}{ri}", [64, 16, 128], BF16) for ri in range(2)] for q in range(2)]
    pB = [TS(f"s5_pB{q}", [128, 3], F32) for q in range(2)]
    crep = [TS(f"s5_crep{q}", [128, 64], F32) for q in range(2)]
    cimp = [TS(f"s5_cimp{q}", [128, 64], F32) for q in range(2)]
    kidx = TS("s5_kidx", [128, 32], F32)
    midx = TS("s5_midx", [128, 512], F32)
    tB = [TS(f"s5_tB{i}", [128, 32], F32) for i in range(7)]
    tBi = TS("s5_tBi", [128, 32], I32)
    cB = [TS(f"s5_cB{i}", [128, 1], F32) for i in range(6)]
    cBi = TS("s5_cBi", [128, 1], I32)
    gt = [SCR[:, 2048 * i:2048 * (i + 1)].rearrange("p (k c) -> p k c", k=32) for i in range(2)]
    Gpad = [[TS(f"s5_G{q}{ri}", [128, 32, 64], BF16) for ri in range(2)] for q in range(2)]
    Tc = [TS(f"s5_Tc{q}", [128, 512], F32) for q in range(2)]
    Tsn = [TS(f"s5_Ts{q}", [128, 512], F32) for q in range(2)]
    rho = [TS(f"s5_rho{q}", [128, 1], F32) for q in range(2)]
    l2 = [SCR[:, 4096 + 512 * i:4096 + 512 * (i + 1)] for i in range(6)]
    l2i = TS("s5_l2i", [128, 512], I32)
    roll = [TS(f"s5_roll{i}", [128, 512], F32) for i in range(2)]
    W15 = [[TS(f"s5_W15{q}{ri}", [128, 512], F32) for ri in range(2)] for q in range(2)]
    W1bf = [[TS(f"s5_W1bf{q}{ri}", [128, 16, 512], BF16) for ri in range(2)] for q in range(2)]
    Xbf = [[TS(f"s5_Xbf{q}{ri}", [128, 512], BF16) for ri in range(2)] for q in range(2)]
    ytmp = [TS(f"s5_ytmp{i}", [64, 512], F32) for i in range(2)]
    ps_z = [PS(f"s5_ps_z{i}") for i in range(2)]
    ps_y = [PS(f"s5_ps_y{i}") for i in range(2)]

    ld = lambda eng, dst, src, name: S.op(eng, lambda e: e.dma_start(out=dst, in_=src), w=[name], dma=True)
    ld("sp", su[:], fm[256:320, :], "su")
    ld("sp", pA[:], pA_d[:, :], "pA")
    ld("sp", dcol[:], dcol_d[:, :], "dcol")
    ld("sp", rowmask[:], rowmask_d[:, :], "rowmask")
    ld("sp", negsig[:], negsig_d.partition_broadcast(64), "negsig")
    ld("sp", kidx[:], kidx_d.partition_broadcast(128), "kidx")
    ld("sp", midx[:], midx_d.partition_broadcast(128), "midx")
    for q in range(2):
        ld("sp", pB[q][:], pB_d[q], ("pB", q))
        ld("sp", crep[q][:], cre_d[q], ("crep", q))
        ld("sp", cimp[q][:], cim_d[q], ("cimp", q))
    are, aim, bre, bim, ldt = pA[:, 0:64], pA[:, 64:128], pA[:, 128:192], pA[:, 192:256], pA[:, 256:257]
    lam, th, abr, abi, mg, zr, zi, den, u0, u1 = [t[:] for t in sA]
    S.op("act", lambda e: e.activation(out=dtA[:], in_=ldt, func=AF.Exp), r=["pA"], w=["dtA"])
    S.op("dve", lambda e: e.tensor_scalar(out=lam, in0=are, scalar1=dtA[:, 0:1], scalar2=None, op0=ALU.mult), r=["pA", "dtA"], w=["lamA"])
    S.op("dve", lambda e: e.tensor_scalar(out=th, in0=aim, scalar1=dtA[:, 0:1], scalar2=None, op0=ALU.mult), r=["pA", "dtA"], w=["thA"])
    S.op("dve", lambda e: e.tensor_copy(out=u0, in_=th), r=["thA"], w=["sAang"])
    sincos(S, u0, u1, sAi[:], den, abi, abr, "sA")
    S.op("act", lambda e: e.activation(out=mg, in_=lam, func=AF.Exp), r=["lamA"], w=["mgA"])
    S.op("dve", lambda e: e.tensor_tensor(out=abr, in0=abr, in1=mg, op=ALU.mult), r=["mgA", "sAcos"], w=["abr"])
    S.op("dve", lambda e: e.tensor_tensor(out=abi, in0=abi, in1=mg, op=ALU.mult), r=["mgA", "sAsin"], w=["abi"])
    S.op("dve", lambda e: e.tensor_scalar(out=abr, in0=abr, scalar1=-1.0, scalar2=None, op0=ALU.add), w=["abr"])
    S.op("dve", lambda e: e.tensor_tensor(out=den, in0=are, in1=are, op=ALU.mult), r=["pA", "sAcos", "sAsin"], w=["den"])
    S.op("dve", lambda e: e.tensor_tensor(out=u0, in0=aim, in1=aim, op=ALU.mult), r=["pA", "sAsin"], w=["u0"])
    S.op("dve", lambda e: e.tensor_tensor(out=den, in0=den, in1=u0, op=ALU.add), r=["u0"], w=["den"])
    S.op("dve", lambda e: e.reciprocal(out=den, in_=den), w=["den"])
    S.op("dve", lambda e: e.tensor_tensor(out=u0, in0=abr, in1=are, op=ALU.mult), r=["abr"], w=["u0"])
    S.op("dve", lambda e: e.tensor_tensor(out=u1, in0=abi, in1=aim, op=ALU.mult), r=["abi"], w=["u1"])
    S.op("dve", lambda e: e.tensor_tensor(out=zr, in0=u0, in1=u1, op=ALU.add), r=["u0", "u1"], w=["zr"])
    S.op("dve", lambda e: e.tensor_tensor(out=zr, in0=zr, in1=den, op=ALU.mult), r=["den"], w=["zr"])
    S.op("dve", lambda e: e.tensor_tensor(out=u0, in0=abi, in1=are, op=ALU.mult), r=["abi", "zr"], w=["u0"])
    S.op("dve", lambda e: e.tensor_tensor(out=u1, in0=abr, in1=aim, op=ALU.mult), r=["abr", "zr"], w=["u1"])
    S.op("dve", lambda e: e.tensor_tensor(out=zi, in0=u0, in1=u1, op=ALU.subtract), r=["u0", "u1"], w=["zi"])
    S.op("dve", lambda e: e.tensor_tensor(out=zi, in0=zi, in1=den, op=ALU.mult), r=["den"], w=["zi"])
    A3 = lambda t: t[:].rearrange("p (s m) -> p s m", s=16)
    bc3 = lambda ap: ap.unsqueeze(1).broadcast_to([64, 16, 64])
    ang3, kf3, hs3, sn3, cs3, mg3, w_r, w_i = tA
    S.op("dve", lambda e: e.tensor_tensor(out=A3(ang3), in0=A3(negsig), in1=bc3(th), op=ALU.mult), r=["negsig", "thA"], w=["tAang"])
    sincos(S, ang3[:], kf3[:], tAi[:], hs3[:], sn3[:], cs3[:], "tA")
    S.op("dve", lambda e: e.tensor_tensor(out=A3(mg3), in0=A3(negsig), in1=bc3(lam), op=ALU.mult), r=["negsig", "lamA"], w=["mg3"])
    S.op("act", lambda e: e.activation(out=mg3[:], in_=mg3[:], func=AF.Exp), w=["mg3"])
    S.op("dve", lambda e: e.tensor_tensor(out=cs3[:], in0=cs3[:], in1=mg3[:], op=ALU.mult), r=["mg3"], w=["tAcos"])
    S.op("dve", lambda e: e.tensor_tensor(out=sn3[:], in0=sn3[:], in1=mg3[:], op=ALU.mult), r=["mg3"], w=["tAsin"])
    cmul(S, "dve", A3(w_r), A3(w_i), A3(cs3), A3(sn3), bc3(zr), bc3(zi), A3(ang3), A3(kf3),
         ["tAcos", "tAsin", "zr", "zi", "tAang", "tAkf"], "wz")
    cmul(S, "dve", A3(cs3), A3(sn3), A3(w_r), A3(w_i), bc3(bre), bc3(bim), A3(ang3), A3(kf3),
         ["wzre", "wzim", "pA", "tAcos", "tAsin"], "Bs")
    for q in range(2):
        for ri, src in ((0, cs3), (1, sn3)):
            for h in range(2):
                gl = 2 * q + h
                S.op("dve", lambda e, q=q, ri=ri, h=h, gl=gl, src=src: e.tensor_scalar(
                    out=W1tab[q][ri][:, :, 64 * h:64 * h + 64], in0=A3(src), scalar1=rowmask[:, gl:gl + 1], scalar2=None, op0=ALU.mult),
                    r=["Bsre", "Bsim", "rowmask"], w=[("W1tab", q, ri, h)])
    S.barrier()
    for q in range(2):
        lamB, thB, dtB, phi, th15, junk = [t[:] for t in cB]
        angk, kfk, hsk, snk, csk, mgk, nsk = [t[:] for t in tB]
        pq = [("pB", q)]
        tg = f"B{q}"
        S.op("act", lambda e, q=q: e.activation(out=dtB, in_=pB[q][:, 2:3], func=AF.Exp), r=pq, w=[tg + "dt"])
        S.op("dve", lambda e, q=q: e.tensor_tensor(out=lamB, in0=pB[q][:, 0:1], in1=dtB, op=ALU.mult), r=pq + [tg + "dt"], w=[tg + "lam"])
        S.op("dve", lambda e, q=q: e.tensor_tensor(out=thB, in0=pB[q][:, 1:2], in1=dtB, op=ALU.mult), r=pq + [tg + "dt"], w=[tg + "th"])
        S.op("dve", lambda e: e.tensor_scalar(out=angk, in0=kidx[:], scalar1=thB[:, 0:1], scalar2=None, op0=ALU.mult),
             r=["kidx", tg + "th"], w=[tg + "kang"])
        sincos(S, angk, kfk, tBi[:], hsk, snk, csk, tg + "k")
        S.op("dve", lambda e: e.tensor_scalar(out=mgk, in0=kidx[:], scalar1=lamB[:, 0:1], scalar2=None, op0=ALU.mult),
             r=["kidx", tg + "lam"], w=[tg + "mgk"])
        S.op("act", lambda e: e.activation(out=mgk, in_=mgk, func=AF.Exp), w=[tg + "mgk"])
        S.op("dve", lambda e: e.tensor_tensor(out=csk, in0=csk, in1=mgk, op=ALU.mult), r=[tg + "mgk"], w=[tg + "kcos"])
        S.op("dve", lambda e: e.tensor_tensor(out=snk, in0=snk, in1=mgk, op=ALU.mult), r=[tg + "mgk"], w=[tg + "ksin"])
        S.op("dve", lambda e: e.tensor_scalar(out=nsk, in0=snk, scalar1=-1.0, scalar2=None, op0=ALU.mult), r=[tg + "ksin"], w=[tg + "nsk"])
        S.op("dve", lambda e: e.tensor_scalar(out=kfk, in0=csk, scalar1=-1.0, scalar2=None, op0=ALU.mult), r=[tg + "kcos"], w=[tg + "kkf"])
        kb = lambda ap: ap.unsqueeze(2).broadcast_to([128, 32, 64])
        cb = lambda t: t[:].unsqueeze(1).broadcast_to([128, 32, 64])
        for ri, (f1, f2) in enumerate(((csk, nsk), (nsk, kfk))):
            S.op("dve", lambda e, q=q, f1=f1: e.tensor_tensor(out=gt[0][:], in0=cb(crep[q]), in1=kb(f1), op=ALU.mult),
                 r=[("crep", q), tg + "kcos", tg + "nsk", tg + "kkf"], w=["gt0"])
            S.op("dve", lambda e, q=q, f2=f2: e.tensor_tensor(out=gt[1][:], in0=cb(cimp[q]), in1=kb(f2), op=ALU.mult),
                 r=[("cimp", q), tg + "kcos", tg + "nsk", tg + "kkf"], w=["gt1"])
            S.op("dve", lambda e, q=q, ri=ri: e.tensor_tensor(out=Gpad[q][ri][:], in0=gt[0][:], in1=gt[1][:], op=ALU.add),
                 r=["gt0", "gt1"], w=[("Gpad", q, ri)])
        S.op("dve", lambda e: e.tensor_scalar(out=phi, in0=thB, scalar1=16.0, scalar2=None, op0=ALU.mult), r=[tg + "th"], w=[tg + "phi"])
        S.op("dve", lambda e: e.tensor_scalar(out=th15, in0=phi, scalar1=1.0 / (2.0 * math.pi), scalar2=None, op0=ALU.mult),
             r=[tg + "phi"], w=[tg + "th15"])
        S.op("dve", lambda e: e.tensor_copy(out=cBi[:], in_=th15), r=[tg + "th15"], w=[tg + "cBi"])
        S.op("dve", lambda e: e.tensor_copy(out=th15, in_=cBi[:]), r=[tg + "cBi"], w=[tg + "th15"])
        S.op("dve", lambda e: e.scalar_tensor_tensor(out=phi, in0=th15, scalar=-C1_2PI, in1=phi, op0=ALU.mult, op1=ALU.add),
             r=[tg + "th15"], w=[tg + "phi"])
        S.op("dve", lambda e: e.scalar_tensor_tensor(out=phi, in0=th15, scalar=-C2_2PI, in1=phi, op0=ALU.mult, op1=ALU.add),
             r=[tg + "th15"], w=[tg + "phi"])
        S.op("dve", lambda e: e.tensor_scalar(out=l2[0][:], in0=midx[:], scalar1=phi[:, 0:1], scalar2=None, op0=ALU.mult),
             r=["midx", tg + "phi"], w=["l2ang"])
        sincos(S, l2[0][:], l2[1][:], l2i[:], l2[2][:], Tsn[q][:], Tc[q][:], "l2")
        S.op("dve", lambda e, q=q: e.tensor_copy(out=Tsn[q][:], in_=Tsn[q][:]), r=["l2sin"], w=[("Ts", q)])
        S.op("dve", lambda e, q=q: e.tensor_copy(out=Tc[q][:], in_=Tc[q][:]), r=["l2cos"], w=[("Tc", q)])
        S.op("act", lambda e, q=q: e.activation(out=rho[q][:], in_=lamB, func=AF.Exp, scale=16.0), r=[tg + "lam"], w=[("rho", q)])
    suv = su[:].rearrange("p (m s) -> p s m", s=16)
    zi_ = 0
    for q in range(2):
        for ri in range(2):
            for s in range(16):
                pb = zi_ % 2
                zi_ += 1
                S.op("pe", lambda e, q=q, ri=ri, s=s, pb=pb: e.matmul(ps_z[pb][:, 0:NB], W1tab[q][ri][:, s, :], suv[:, s, :], start=True, stop=True),
                     r=["su", ("W1tab", q, ri, 0), ("W1tab", q, ri, 1)], w=[("ps_z", pb)])
                dst = W15[q][ri] if s == 15 else roll[s % 2]
                dn = ("W15", q, ri) if s == 15 else ("roll", s % 2)
                if s == 0:
                    S.op("dve", lambda e, pb=pb, dst=dst: e.tensor_copy(out=dst[:, 0:NB], in_=ps_z[pb][:, 0:NB]), w=[dn, ("ps_z", pb)])
                else:
                    S.op("dve", lambda e, pb=pb, dst=dst, s=s: e.tensor_tensor(out=dst[:, 0:NB], in0=ps_z[pb][:, 0:NB],
                                                                          in1=roll[(s - 1) % 2][:, 0:NB], op=ALU.add),
                         r=[("roll", (s - 1) % 2)], w=[dn, ("ps_z", pb)])
                S.op("act", lambda e, q=q, ri=ri, s=s, dst=dst: e.activation(out=W1bf[q][ri][:, s, 0:NB], in_=dst[:, 0:NB], func=AF.Copy),
                     r=[dn], w=[("W1bf", q, ri, s)])
    for q in range(2):
        ur, ui, t0, t1, vr, vi = [t[:, 0:NB] for t in l2]
        tc, tsn = Tc[q][:, 0:NB], Tsn[q][:, 0:NB]
        wre, wim = W15[q][0][:, 0:NB], W15[q][1][:, 0:NB]
        cmul(S, "dve", ur, ui, tc, tsn, wre, wim, t0, t1, [("Tc", q), ("Ts", q), ("W15", q, 0), ("W15", q, 1), "l2v"], "l2u", conj_a=True)
        rb = rho[q][:, 0:1].broadcast_to([128, NB])
        S.op("dve", lambda e, rb=rb: e.tensor_tensor_scan(out=vr, data0=rb, data1=ur, initial=0.0, op0=ALU.mult, op1=ALU.add),
             r=["l2ure", ("rho", q)], w=["l2vr"])
        S.op("dve", lambda e, rb=rb: e.tensor_tensor_scan(out=vi, data0=rb, data1=ui, initial=0.0, op0=ALU.mult, op1=ALU.add),
             r=["l2uim", ("rho", q)], w=["l2vi"])
        cmul(S, "dve", ur, ui, tc, tsn, vr, vi, t0, t1, [("Tc", q), ("Ts", q), "l2vr", "l2vi"], "l2x")
        for ri, src in ((0, ur), (1, ui)):
            S.op("pool", lambda e, q=q, ri=ri: e.memset(Xbf[q][ri][:, 0:1], 0.0), w=[("Xbf", q, ri)])
            if NB > 1:
                S.op("act", lambda e, q=q, ri=ri, src=src: e.activation(out=Xbf[q][ri][:, 1:NB], in_=src[:, 0:NB - 1], func=AF.Copy),
                     r=["l2xre", "l2xim"], w=[("Xbf", q, ri)])
        S.op("dve", lambda e: e.tensor_copy(out=l2[0][:, 0:1], in_=l2[0][:, 0:1]), r=[("Xbf", q, 0), ("Xbf", q, 1)], w=["l2v", "l2ure", "l2uim"])
    outv = outsb[:].rearrange("p (m s) -> p s m", s=16)
    for s in range(16):
        pb = s % 2
        k = 0
        for q in range(2):
            for ri in range(2):
                S.op("pe", lambda e, q=q, ri=ri, s=s, pb=pb, k=k: e.matmul(ps_y[pb][0:64, 0:NB], Gpad[q][ri][:, s, :], W1bf[q][ri][:, s, 0:NB],
                                                                     start=(k == 0), stop=False),
                     r=[("Gpad", q, ri), ("W1bf", q, ri, s)], w=[("ps_y", pb)])
                k += 1
        for q in range(2):
            for ri in range(2):
                S.op("pe", lambda e, q=q, ri=ri, s=s, pb=pb, k=k: e.matmul(ps_y[pb][0:64, 0:NB], Gpad[q][ri][:, s + 16, :], Xbf[q][ri][:, 0:NB],
                                                                     start=False, stop=(k == 7)),
                     r=[("Gpad", q, ri), ("Xbf", q, ri)], w=[("ps_y", pb)])
                k += 1
        S.op("dve", lambda e, s=s, pb=pb: e.scalar_tensor_tensor(out=ytmp[pb][:, 0:NB], in0=suv[:, s, :], scalar=dcol[:, 0:1],
                                                            in1=ps_y[pb][0:64, 0:NB], op0=ALU.mult, op1=ALU.add),
             r=["su", "dcol"], w=[("ytmp", pb), ("ps_y", pb)])
        S.op("act", lambda e, s=s, pb=pb: e.activation(out=outv[:, s, :], in_=ytmp[pb][:, 0:NB], func=AF.Gelu),
             r=[("ytmp", pb)], w=[("outsb", s)])
    S.op("sp", lambda e: e.dma_start(out=yg[:, :], in_=outsb[:]), r=[("outsb", s) for s in range(16)], dma=True)


def build_s5(T=SEQ):
    nc = bass.Bass("TRN2", target_bir_lowering=False)
    D = lambda n, s, d=F32, k="ExternalInput": nc.dram_tensor(n, s, d, kind=k).ap()
    fm = D("fm", [NFM, T], BF16)
    pA = D("s5p_pA", [64, 257]); pB = D("s5p_pB", [2, 128, 3]); cre = D("s5p_cre", [2, 128, 64]); cim = D("s5p_cim", [2, 128, 64])
    dcol = D("s5p_d", [64, 1]); negsig = D("negsig", [1, 1024]); kidx = D("kidx", [1, 32]); midx = D("midx", [1, 512])
    rowmask = D("rowmask", [64, 4])
    yg = D("yg", [64, T], BF16, "ExternalOutput")
    with contextlib.ExitStack() as st:
        S = Sched(nc)
        phase_s5(nc, S, st, fm, pA, pB, cre, cim, dcol, negsig, kidx, midx, rowmask, yg, T)
        S.emit()
    return nc


def phase_out(nc, S, st, mixin, ssg_d, hT, wout_d, gluw_d, glub_d, fnw_d, hout, final, NTOK=TQ):
    TS = lambda n, s, d: st.enter_context(nc.sbuf_tensor(n, s, d))
    PS = lambda n: st.enter_context(nc.psum_tensor(n, [128, 512], F32))
    wst = [TS(f"po_wst{i}", [128, 1024], F32) for i in range(2)]
    wout = TS("po_wout", [128, 8, 1024], BF16)
    gst = TS("po_gst", [128, 2, 256], F32)
    gluw = TS("po_gluw", [128, 2, 256], BF16)
    glub = TS("po_glub", [128, 2], F32)
    fnw = TS("po_fnw", [128, 8], F32)
    ones = TS("po_ones", [128, 128], BF16)
    mix = [TS(f"po_mix{i}", [128, 8, 512], BF16) for i in range(2)]
    ssg = [TS(f"po_ssg{i}", [128, 2, 512], BF16) for i in range(2)]
    hin = [TS(f"po_hin{i}", [128, 8, 512], F32) for i in range(2)]
    sg = TS("po_sg", [128, 512], F32)
    osb = TS("po_osb", [128, 2, 512], BF16)
    hn = TS("po_hn", [128, 8, 512], F32)
    hsq = TS("po_hsq", [128, 8, 512], BF16)
    nsq = TS("po_nsq", [128, 512], F32)
    ps_g = PS("po_ps_g")
    ps_o = [PS(f"po_ps_o{i}") for i in range(3)]
    ps_n = PS("po_ps_n")

    S.op("pool", lambda e: e.memset(ones[:], 1.0), w=["ones"])
    S.op("sp", lambda e: e.dma_start(out=gst[:], in_=gluw_d.rearrange("(k p) o -> p k o", p=128)), w=["gst"], dma=True)
    S.op("sp", lambda e: e.dma_start(out=glub[:], in_=glub_d[:, :]), w=["glub"], dma=True)
    S.op("sp", lambda e: e.dma_start(out=fnw[:], in_=fnw_d[:, :]), w=["fnw"], dma=True)
    S.op("dve", lambda e: e.tensor_copy(out=gluw[:], in_=gst[:]), r=["gst"], w=["gluw"])
    for k in range(8):
        S.op("sp", lambda e, k=k: e.dma_start(out=wst[k % 2][:], in_=wout_d[k * 128:(k + 1) * 128, :]), w=[("wst", k % 2)], dma=True)
        S.op("pool" if k % 2 else "dve", lambda e, k=k: e.tensor_copy(out=wout[:, k, :], in_=wst[k % 2][:]), r=[("wst", k % 2)], w=[("wout", k)])
    wr = [("wout", k) for k in range(8)]
    mv = mixin.rearrange("(k p) t -> p k t", p=128)
    sv = ssg_d.rearrange("(k p) t -> p k t", p=128)
    hv = hT.rearrange("(k p) t -> p k t", p=128)
    ov = hout.rearrange("(k p) t -> p k t", p=128)
    oi = 0
    for ti in range(NTOK // 512):
        b = ti % 2
        ts = slice(ti * 512, (ti + 1) * 512)
        S.op("sp", lambda e, b=b, ts=ts: e.dma_start(out=mix[b][:], in_=mv[:, :, ts]), w=[("mix", b)], dma=True)
        S.op("sp", lambda e, b=b, ts=ts: e.dma_start(out=ssg[b][:], in_=sv[:, :, ts]), w=[("ssg", b)], dma=True)
        S.op("pool", lambda e, b=b, ts=ts: e.dma_start(out=hin[b][:], in_=hv[:, :, ts]), w=[("hin", b)], dma=True)
        for oc in range(2):
            for kc in range(2):
                S.op("pe", lambda e, b=b, oc=oc, kc=kc: e.matmul(ps_g[:], gluw[:, kc, oc * 128:(oc + 1) * 128], mix[b][:, 2 + kc, :],
                                                             start=(kc == 0), stop=(kc == 1)), r=[("mix", b), "gluw"], w=["ps_g"])
            S.op("act", lambda e, oc=oc: e.activation(out=sg[:], in_=ps_g[:], func=AF.Sigmoid, bias=glub[:, oc:oc + 1]),
                 r=["glub"], w=["sg", "ps_g"])
            S.op("dve", lambda e, b=b, oc=oc: e.tensor_tensor(out=sg[:], in0=sg[:], in1=mix[b][:, 2 + oc, :], op=ALU.mult),
                 r=[("mix", b)], w=["sg"])
            S.op("dve", lambda e, b=b, oc=oc: e.tensor_tensor(out=osb[:, oc, :], in0=sg[:], in1=ssg[b][:, oc, :], op=ALU.mult),
                 r=[("ssg", b), "sg"], w=[("osb", oc)])
        for dc in range(8):
            pb = oi % 3
            oi += 1
            for kc in range(8):
                rhs = (lambda b=b, kc=kc: osb[:, kc - 2, :]) if kc in (2, 3) else (lambda b=b, kc=kc: mix[b][:, kc, :])
                S.op("pe", lambda e, dc=dc, kc=kc, pb=pb, rhs=rhs: e.matmul(ps_o[pb][:], wout[:, kc, dc * 128:(dc + 1) * 128], rhs(),
                                                                      start=(kc == 0), stop=(kc == 7)),
                     r=wr + [("mix", b), ("osb", 0), ("osb", 1)], w=[("ps_o", pb)])
            S.op("dve", lambda e, b=b, dc=dc, pb=pb: e.tensor_tensor(out=hn[:, dc, :], in0=ps_o[pb][:], in1=hin[b][:, dc, :], op=ALU.add),
                 r=[("hin", b)], w=[("hn", dc), ("ps_o", pb)])
            if not final:
                S.op("sp", lambda e, dc=dc, ts=ts: e.dma_start(out=ov[:, dc, ts], in_=hn[:, dc, :]), r=[("hn", dc)], dma=True)
        if final:
            hr = [("hn", dc) for dc in range(8)]
            S.op("act", lambda e: e.activation(out=hsq[:], in_=hn[:], func=AF.Square), r=hr, w=["hsq"])
            for k in range(8):
                S.op("pe", lambda e, k=k: e.matmul(ps_n[:], ones[:], hsq[:, k, :], start=(k == 0), stop=(k == 7)), r=["hsq", "ones"], w=["ps_n"])
            S.op("act", lambda e: e.activation(out=nsq[:], in_=ps_n[:], func=AF.Sqrt, scale=1.0 / D_MODEL, bias=EPS), w=["nsq", "ps_n"])
            S.op("dve", lambda e: e.reciprocal(out=nsq[:], in_=nsq[:]), w=["nsq"])
            for dc in range(8):
                S.op("pool" if dc % 2 else "dve", lambda e, dc=dc: e.scalar_tensor_tensor(
                    out=hn[:, dc, :], in0=hn[:, dc, :], scalar=fnw[:, dc:dc + 1], in1=nsq[:], op0=ALU.mult, op1=ALU.mult) if dc % 2 == 0 else
                    e.tensor_tensor(out=hn[:, dc, :], in0=hn[:, dc, :], in1=nsq[:], op=ALU.mult),
                    r=["nsq", "fnw"], w=[("hn", dc)])
                if dc % 2:
                    S.op("pool", lambda e, dc=dc: e.tensor_scalar(out=hn[:, dc, :], in0=hn[:, dc, :], scalar1=fnw[:, dc:dc + 1], scalar2=None,
                                                                  op0=ALU.mult), r=["fnw"], w=[("hn", dc)])
                S.op("sp", lambda e, dc=dc, ts=ts: e.dma_start(out=ov[:, dc, ts], in_=hn[:, dc, :]), r=[("hn", dc)], dma=True)


def build_out(final, NTOK=TQ):
    nc = bass.Bass("TRN2", target_bir_lowering=False)
    D = lambda n, s, d=F32, k="ExternalInput": nc.dram_tensor(n, s, d, kind=k).ap()
    mixin = D("mixin", [1024, NTOK], BF16)
    ssg = D("ssg", [256, NTOK], BF16)
    hT = D("hT", [D_MODEL, NTOK])
    wout = D("wout", [1024, 1024]); gluw = D("gluw", [256, 256]); glub = D("glub", [128, 2]); fnw = D("fnw", [128, 8])
    hout = D("hout", [D_MODEL, NTOK], F32, "ExternalOutput")
    with contextlib.ExitStack() as st:
        S = Sched(nc)
        phase_out(nc, S, st, mixin, ssg, hT, wout, gluw, glub, fnw, hout, final, NTOK)
        S.emit()
    return nc


_CACHE = {}


def _prog(key, fn):
    if key not in _CACHE:
        _CACHE[key] = fn()
    return _CACHE[key]


def build_mixers(l, T=SEQ, which=("ip", "at", "hg", "s5")):
    lambda_init = 0.8 - 0.6 * math.exp(-0.3 * l)
    nc = bass.Bass("TRN2", target_bir_lowering=False)
    D = lambda n, s, d=F32, k="ExternalInput": nc.dram_tensor(n, s, d, kind=k).ap()
    hT = D("hT", [D_MODEL, T]); wcat = D("wcat", [D_MODEL, NFM + NTM]); nw = D("nw", [128, 8])
    lqk = D("lqk", [1, 256]); subln = D("subln", [128, 1]); ropef = D("ropef", [128, 1]); rmat = D("rmat", [128, 128])
    cmask = D("cmask", [4, 128, 512], BF16)
    lbl_bc = D("lbl_bc", [1, 128]); lbl_col = D("lbl_col", [64, 2]); gw = D("gw", [64, 1]); mcat = D("mcat", [128, 136]); mrev = D("mrev", [128, 128])
    pA = D("s5p_pA", [64, 257]); pB = D("s5p_pB", [2, 128, 3]); cre = D("s5p_cre", [2, 128, 64]); cim = D("s5p_cim", [2, 128, 64])
    dcol = D("s5p_d", [64, 1]); negsig = D("negsig", [1, 1024]); kidx = D("kidx", [1, 32]); midx = D("midx", [1, 512]); rowmask = D("rowmask", [64, 4])
    fm = D("fm", [NFM, T], BF16, "Internal")
    tm_sf = D("tm_sf", [T, 64], F32, "Internal")
    tm_v = D("tm_v", [T, 192], BF16, "Internal")
    mo = D("mo", [320, T], BF16, "ExternalOutput")
    if "ip" in which:
        with contextlib.ExitStack() as st:
            S = Sched(nc)
            phase_inproj(nc, S, st, hT, wcat, nw, fm, tm_sf, tm_v, T)
            S.op("sp", lambda e: e.dma_start(out=mo[128:192, :], in_=fm[192:256, :]), r=[], dma=True)
            S.emit()
    if "at" in which:
        with contextlib.ExitStack() as st:
            S = Sched(nc)
            phase_attn(nc, S, st, fm, tm_v, lqk, subln, ropef, rmat, cmask, mo[192:320, :], lambda_init, T)
            S.emit()
    if "hg" in which:
        with contextlib.ExitStack() as st:
            S = Sched(nc)
            phase_hgrn(nc, S, st, fm, tm_sf, tm_v, lbl_bc, lbl_col, gw, mcat, mrev, mo[0:64, :], float(l), T)
            S.emit()
    if "s5" in which:
        with contextlib.ExitStack() as st:
            S = Sched(nc)
            phase_s5(nc, S, st, fm, pA, pB, cre, cim, dcol, negsig, kidx, midx, rowmask, mo[64:128, :], T)
            S.emit()
    return nc


def mixer_inputs(inp, l, c, hT_b):
    f = np.float32
    j = c % 4
    ropef, rmat, cmask = attn_consts()
    mcat, mrev = hgrn_consts()
    negsig, kidx, midx, rowmask = s5_consts()
    lbl = np.asarray(inp["hgrn_lb_logits"], f)[:, 64 * j:64 * j + 64]
    d = {"hT": hT_b, "wcat": np.ascontiguousarray(np.asarray(inp["w_in"][l], f)[:, core_cols(j)]),
         "nw": np.ascontiguousarray(np.asarray(inp["norm_w"][l], f).reshape(8, 128).T),
         "lqk": np.concatenate([inp["diff_lq1"][l], inp["diff_lq2"][l], inp["diff_lk1"][l], inp["diff_lk2"][l]])[None, :].astype(f),
         "subln": np.asarray(inp["diff_subln_w"][l], f)[:, None], "ropef": ropef, "rmat": rmat, "cmask": cmask,
         "lbl_bc": np.ascontiguousarray(lbl.reshape(1, 128)), "lbl_col": np.ascontiguousarray(lbl.T),
         "gw": np.asarray(inp["hgrn_norm_w"][l], f)[:, None], "mcat": mcat, "mrev": mrev,
         "negsig": negsig, "kidx": kidx, "midx": midx, "rowmask": rowmask}
    d.update(s5_params(inp, l, j))
    return d


def kernel(**inp):
    f = np.float32
    x = np.asarray(inp["x"], f)
    cores = list(range(NCORES))
    hT = [np.ascontiguousarray(x[b].T) for b in range(BATCH)]
    for l in range(DEPTH):
        nc = _prog(("mix", l), lambda: build_mixers(l))
        ims = [mixer_inputs(inp, l, c, hT[c // 4]) for c in cores]
        rm = run_bass_kernel_spmd(nc, ims, core_ids=cores).results
        final = (l == DEPTH - 1)
        nc = _prog(("out", final), lambda: build_out(final))
        ims = []
        for c in cores:
            b, tq = c // 4, c % 4
            ts = slice(tq * TQ, (tq + 1) * TQ)
            src = [4 * b + j for j in range(4)]
            mixin = np.concatenate([rm[s]["mo"][0:64, ts] for s in src] + [rm[s]["mo"][64:128, ts] for s in src]
                                   + [rm[s]["mo"][192:320, ts] for s in src])
            ssg = np.concatenate([rm[s]["mo"][128:192, ts] for s in src])
            ims.append({"mixin": np.ascontiguousarray(mixin), "ssg": np.ascontiguousarray(ssg), "hT": np.ascontiguousarray(hT[b][:, ts]),
                        "wout": np.asarray(inp["w_out"][l], f), "gluw": np.asarray(inp["s5_glu_w"][l], f),
                        "glub": np.ascontiguousarray(np.asarray(inp["s5_glu_b"][l], f).reshape(2, 128).T),
                        "fnw": np.ascontiguousarray(np.asarray(inp["final_norm_w"], f).reshape(8, 128).T)})
        ro = run_bass_kernel_spmd(nc, ims, core_ids=cores).results
        hT = [np.concatenate([ro[4 * b + tq]["hout"] for tq in range(4)], axis=1) for b in range(BATCH)]
    out = np.stack([hT[b].T for b in range(BATCH)]).astype(f)
    return np.ascontiguousarray(out)
```

```python
import contextlib
import math
import numpy as np
import ml_dtypes
import concourse.bass as bass
import concourse.mybir as mybir
from concourse.bass_utils import run_bass_kernel_spmd

F32 = mybir.dt.float32
BF16 = mybir.dt.bfloat16
I32 = mybir.dt.int32
AF = mybir.ActivationFunctionType
ALU = mybir.AluOpType
AX = mybir.AxisListType

D_MODEL = 1024
SEQ = 8192
BATCH = 2
DEPTH = 2
EPS = 1e-6
NCORES = 8
TQ = SEQ // 4
ROPE_THETA = 500000.0
import os
DBG = set(os.environ.get("KDBG", "").split(","))


class Sched:
    ENGS = ["pe", "act", "dve", "pool", "sp"]

    def __init__(self, nc):
        self.nc = nc
        self.ops = []
        self.last_w = {}
        self.readers = {}
        self.cnt = {e: 0 for e in self.ENGS}
        self.dma_cnt = {}
        self.base = set()

    def op(self, eng, fn, r=(), w=(), dma=False):
        deps = set(self.base)
        for x in r:
            if x in self.last_w:
                deps.add(self.last_w[x])
        for x in w:
            if x in self.last_w:
                deps.add(self.last_w[x])
            for d in self.readers.get(x, ()):
                deps.add(d)
        if dma:
            q = self.dma_cnt.get(eng, 0)
            self.dma_cnt[eng] = q + 1
            tok = ("dma", eng, q)
        else:
            self.cnt[eng] += 1
            tok = ("eng", eng, self.cnt[eng])
        self.ops.append((eng, fn, deps, tok))
        for x in w:
            self.last_w[x] = tok
            self.readers[x] = []
        for x in r:
            self.readers.setdefault(x, []).append(tok)
        return tok

    def barrier(self):
        b = set()
        for e in self.ENGS:
            if self.cnt[e] > 0:
                b.add(("eng", e, self.cnt[e]))
        for e, n in self.dma_cnt.items():
            for q in range(max(0, n - self.NSLOT), n):
                b.add(("dma", e, q))
        self.base = b
        self.last_w = {}
        self.readers = {}

    NSLOT = 8

    def emit(self):
        nc = self.nc
        NSLOT = self.NSLOT
        needed = set()
        for (eng, fn, deps, tok) in self.ops:
            for d in deps:
                if d[0] == "eng" and not (d[1] == "pe" and eng == "pe"):
                    needed.add(d)
        sig = {}
        run = {e: 0 for e in self.ENGS}
        for (eng, fn, deps, tok) in self.ops:
            if tok[0] == "eng":
                if tok in needed:
                    run[eng] += 1
                sig[tok] = run[eng]
        with contextlib.ExitStack() as st:
            esem = {e: st.enter_context(nc.semaphore("s_" + e)) for e in self.ENGS}
            dsem = {}
            for e in self.dma_cnt:
                dsem[e] = [st.enter_context(nc.semaphore(f"d_{e}_{i}")) for i in range(NSLOT)]
            block = st.enter_context(nc.Block())
            per = {e: [o for o in self.ops if o[0] == e] for e in self.ENGS}

            def mk(ename):
                def body(eng):
                    seen = {}

                    def wait(tok):
                        if tok[0] == "eng":
                            _, e2, n = tok
                            if e2 == "pe" and ename == "pe":
                                return
                            v = sig[tok]
                            key = ("eng", e2)
                            if seen.get(key, 0) >= v:
                                return
                            seen[key] = v
                            eng.wait_ge(esem[e2], v)
                        else:
                            _, e2, q = tok
                            slot = q % NSLOT
                            val = 16 * (q // NSLOT + 1)
                            key = ("dma", e2, slot)
                            if seen.get(key, 0) >= val:
                                return
                            seen[key] = val
                            eng.wait_ge(dsem[e2][slot], val)
                    for (_, fn, deps, tok) in per[ename]:
                        for d in sorted(deps):
                            wait(d)
                        if tok[0] == "dma":
                            q = tok[2]
                            if q >= NSLOT:
                                wait(("dma", ename, q - NSLOT))
                            ins = fn(eng)
                            ins.then_inc(dsem[ename][q % NSLOT], 16)
                        else:
                            ins = fn(eng)
                            if tok in needed:
                                ins.then_inc(esem[ename], 1)
                    n = self.dma_cnt.get(ename, 0)
                    for q in range(max(0, n - NSLOT), n):
                        wait(("dma", ename, q))
                return body
            block.tensor(mk("pe"))
            block.scalar(mk("act"))
            block.vector(mk("dve"))
            block.gpsimd(mk("pool"))
            block.sync(mk("sp"))


NFM = 704
NTM = 256
FM_CH = [(0, 128), (128, 128), (256, 64), (320, 128), (448, 128), (576, 128)]


def phase_inproj(nc, S, st, hT, wcat, nw, fm, tm_sf, tm_v, T=SEQ):
    TS = lambda n, s, d: st.enter_context(nc.sbuf_tensor(n, s, d))
    PS = lambda n: st.enter_context(nc.psum_tensor(n, [128, 512], F32))
    nw_sb = TS("ip_nw", [128, 8], F32)
    wst = [TS(f"ip_wst{i}", [128, NFM + NTM], F32) for i in range(2)]
    wall = TS("ip_wall", [128, 8, NFM + NTM], BF16)
    ones = TS("ip_ones", [128, 128], BF16)
    xin = [TS(f"ip_xin{i}", [128, 8, 512], F32) for i in range(2)]
    xsq = TS("ip_xsq", [128, 8, 512], BF16)
    sq = TS("ip_sq", [128, 512], F32)
    rstd = TS("ip_rstd", [128, 512], F32)
    xn = [TS(f"ip_xn{i}", [128, 8, 512], BF16) for i in range(2)]
    fmo = [TS(f"ip_fmo{i}", [128, 512], BF16) for i in range(6)]
    tsf = [TS(f"ip_tsf{i}", [128, 4, 64], F32) for i in range(2)]
    tv = [TS(f"ip_tv{i}", [128, 4, 192], BF16) for i in range(2)]
    ps_ss = PS("ip_ps_ss")
    ps_fm = [PS(f"ip_ps_fm{i}") for i in range(4)]
    ps_tm = [PS(f"ip_ps_tm{i}") for i in range(2)]

    S.op("sp", lambda e: e.dma_start(out=nw_sb[:], in_=nw[:, :]), w=["nw"], dma=True)
    S.op("pool", lambda e: e.memset(ones[:], 1.0), w=["ones"])
    for k in range(8):
        S.op("sp", lambda e, k=k: e.dma_start(out=wst[k % 2][:], in_=wcat[k * 128:(k + 1) * 128, :]),
             w=[("wst", k % 2)], dma=True)
        S.op("dve", lambda e, k=k: e.tensor_scalar(out=wall[:, k, :], in0=wst[k % 2][:], scalar1=nw_sb[:, k:k + 1],
                                                  scalar2=None, op0=ALU.mult),
             r=[("wst", k % 2), "nw"], w=[("wall", k)])
    wall_r = [("wall", k) for k in range(8)]
    hT_v = hT.rearrange("(k p) t -> p k t", p=128)
    fmi = 0
    NTI = T // 512

    def load(ti):
        b = ti % 2
        t0 = ti * 512
        S.op("pool", lambda e, b=b, t0=t0: e.dma_start(out=xin[b][:, 0:4, :], in_=hT_v[:, 0:4, t0:t0 + 512]),
             w=[("xin", b, 0)], dma=True)
        S.op("pool", lambda e, b=b, t0=t0: e.dma_start(out=xin[b][:, 4:8, :], in_=hT_v[:, 4:8, t0:t0 + 512]),
             w=[("xin", b, 1)], dma=True)
    def front_sq(ti):
        b = ti % 2
        xr = [("xin", b, 0), ("xin", b, 1)]
        S.op("act", lambda e, b=b: e.activation(out=xsq[:], in_=xin[b][:], func=AF.Square), r=xr, w=["xsq"])

    def front_ss(ti):
        b = ti % 2
        for k in range(8):
            S.op("pe", lambda e, k=k: e.matmul(ps_ss[:], ones[:], xsq[:, k, :], start=(k == 0), stop=(k == 7)),
                 r=["xsq", "ones"], w=["ps_ss"])
        S.op("act", lambda e: e.activation(out=sq[:], in_=ps_ss[:], func=AF.Sqrt, scale=1.0 / D_MODEL, bias=EPS),
             w=["ps_ss", "sq"])
        S.op("dve", lambda e: e.reciprocal(out=rstd[:], in_=sq[:]), r=["sq"], w=["rstd"])
        for hh in range(2):
            S.op("dve", lambda e, b=b, hh=hh: e.tensor_tensor(out=xn[b][:, 4 * hh:4 * hh + 4, :], in0=xin[b][:, 4 * hh:4 * hh + 4, :],
                                                         in1=rstd[:].unsqueeze(1).broadcast_to([128, 4, 512]), op=ALU.mult),
                 r=[("xin", b, hh), "rstd"], w=[("xn", b, hh)])
    load(0)
    if NTI > 1:
        load(1)
    front_sq(0)
    front_ss(0)
    for ti in range(NTI):
        b = ti % 2
        t0 = ti * 512
        if ti + 1 < NTI:
            front_sq(ti + 1)
        xnr = [("xn", b, 0), ("xn", b, 1)]
        for ci, (c0, cw) in enumerate(FM_CH):
            if "NOFM" in DBG or ("FM%d" % ci) in DBG:
                continue
            if ci == 3:
                if ti + 1 < NTI:
                    front_ss(ti + 1)
                if ti + 2 < NTI:
                    load(ti + 2)
            pb = fmi % 4
            for k in range(8):
                S.op("pe", lambda e, k=k, c0=c0, cw=cw, pb=pb, b=b: e.matmul(
                    ps_fm[pb][0:cw, :], wall[:, k, c0:c0 + cw], xn[b][:, k, :], start=(k == 0), stop=(k == 7)),
                    r=xnr + wall_r, w=[("ps_fm", pb)])
            fb = fmi % 6
            fmi += 1
            if ci == 0:
                S.op("act", lambda e, pb=pb, fb=fb: e.activation(out=fmo[fb][0:64, :], in_=ps_fm[pb][0:64, :], func=AF.Silu),
                     r=[("ps_fm", pb)], w=[("fmo", fb, 0)])
                S.op("act", lambda e, pb=pb, fb=fb: e.activation(out=fmo[fb][64:128, :], in_=ps_fm[pb][64:128, :],
                                                               func=AF.Sigmoid, scale=-1.0),
                     r=[("ps_fm", pb)], w=[("fmo", fb, 1)])
                wl = [("fmo", fb, 0), ("fmo", fb, 1)]
            elif ci in (1, 5):
                S.op("act", lambda e, pb=pb, fb=fb: e.activation(out=fmo[fb][:], in_=ps_fm[pb][:], func=AF.Silu),
                     r=[("ps_fm", pb)], w=[("fmo", fb, 0), ("fmo", fb, 1)])
                wl = [("fmo", fb, 0), ("fmo", fb, 1)]
            else:
                S.op("dve", lambda e, pb=pb, fb=fb, cw=cw: e.tensor_copy(out=fmo[fb][0:cw, :], in_=ps_fm[pb][0:cw, :]),
                     r=[("ps_fm", pb)], w=[("fmo", fb, 0), ("fmo", fb, 1)])
                wl = [("fmo", fb, 0), ("fmo", fb, 1)]
            S.op("sp", lambda e, fb=fb, c0=c0, cw=cw, t0=t0: e.dma_start(out=fm[c0:c0 + cw, t0:t0 + 512], in_=fmo[fb][0:cw, :]),
                 r=wl, dma=True)
        for pb in range(0 if "NOTM" in DBG else 2):
            for t4 in (2 * pb, 2 * pb + 1):
                off = (t4 % 2) * 256
                for k in range(8):
                    S.op("pe", lambda e, k=k, t4=t4, pb=pb, off=off, b=b: e.matmul(
                        ps_tm[pb][:, off:off + 256], xn[b][:, k, t4 * 128:(t4 + 1) * 128], wall[:, k, NFM:NFM + NTM],
                        start=(k == 0), stop=(k == 7)),
                        r=xnr + wall_r, w=[("ps_tm", pb)])
            if "TMNOEVAC" in DBG:
                continue
            for t4 in (2 * pb, 2 * pb + 1):
                off = (t4 % 2) * 256
                if "TMNOACT" not in DBG:
                  S.op("act", lambda e, pb=pb, b=b, t4=t4, off=off: e.activation(
                    out=tsf[b][:, t4, :], in_=ps_tm[pb][:, off:off + 64], func=AF.Sigmoid),
                    w=[("ps_tm", pb), ("tsf", b, t4)])
                if "TMNODVE" not in DBG:
                  S.op("dve", lambda e, pb=pb, b=b, t4=t4, off=off: e.tensor_copy(
                    out=tv[b][:, t4, :], in_=ps_tm[pb][:, off + 64:off + 256]),
                    w=[("ps_tm", pb), ("tv", b, t4)])
        if "TMNODMA" in DBG:
            continue
        S.op("sp", lambda e, b=b, t0=t0: e.dma_start(
            out=tm_sf[t0:t0 + 512, :].rearrange("(a p) c -> p a c", p=128), in_=tsf[b][:]),
            r=[("tsf", b, t4) for t4 in range(4)], dma=True)
        S.op("sp", lambda e, b=b, t0=t0: e.dma_start(
            out=tm_v[t0:t0 + 512, :].rearrange("(a p) c -> p a c", p=128), in_=tv[b][:]),
            r=[("tv", b, t4) for t4 in range(4)], dma=True)


def core_cols(j):
    r = lambda s, n: list(range(s, s + n))
    fmc = (r(0 + 64 * j, 64) + r(256 + 64 * j, 64) + r(768 + 64 * j, 64) + r(1280 + 64 * j, 64) + r(1024 + 64 * j, 64)
           + r(1536 + 128 * j, 128) + r(2048 + 128 * j, 128) + r(3072 + 128 * j, 128))
    tmc = r(256 + 64 * j, 64) + r(512 + 64 * j, 64) + r(2560 + 128 * j, 128)
    return np.array(fmc + tmc)


def build_inproj(T=SEQ):
    nc = bass.Bass("TRN2", target_bir_lowering=False)
    hT = nc.dram_tensor("hT", [D_MODEL, T], F32, kind="ExternalInput").ap()
    wcat = nc.dram_tensor("wcat", [D_MODEL, NFM + NTM], F32, kind="ExternalInput").ap()
    nw = nc.dram_tensor("nw", [128, 8], F32, kind="ExternalInput").ap()
    fm = nc.dram_tensor("fm", [NFM, T], BF16, kind="ExternalOutput").ap()
    tm_sf = nc.dram_tensor("tm_sf", [T, 64], F32, kind="ExternalOutput").ap()
    tm_v = nc.dram_tensor("tm_v", [T, 192], BF16, kind="ExternalOutput").ap()
    with contextlib.ExitStack() as st:
        S = Sched(nc)
        phase_inproj(nc, S, st, hT, wcat, nw, fm, tm_sf, tm_v, T)
        S.emit()
    return nc


C1_2PI = 6.28125
C2_2PI = 2.0 * math.pi - 6.28125


def sincos(S, ang, kf, ki, hs, sin_out, cos_out, tag, eng="dve"):
    a, k, h = tag + "ang", tag + "kf", tag + "hs"
    S.op(eng, lambda e: e.tensor_scalar(out=kf, in0=ang, scalar1=1.0 / (2.0 * math.pi), scalar2=None, op0=ALU.mult), r=[a], w=[k])
    S.op(eng, lambda e: e.tensor_copy(out=ki, in_=kf), r=[k], w=[tag + "ki"])
    S.op(eng, lambda e: e.tensor_copy(out=kf, in_=ki), r=[tag + "ki"], w=[k])
    S.op("dve", lambda e: e.scalar_tensor_tensor(out=ang, in0=kf, scalar=-C1_2PI, in1=ang, op0=ALU.mult, op1=ALU.add), r=[k], w=[a])
    S.op("dve", lambda e: e.scalar_tensor_tensor(out=ang, in0=kf, scalar=-C2_2PI, in1=ang, op0=ALU.mult, op1=ALU.add), r=[k], w=[a])
    S.op(eng, lambda e: e.tensor_scalar(out=ang, in0=ang, scalar1=math.pi, scalar2=-math.pi, op0=ALU.min, op1=ALU.max), w=[a])
    S.op("act", lambda e: e.activation(out=sin_out, in_=ang, func=AF.Sin), r=[a], w=[tag + "sin"])
    S.op("act", lambda e: e.activation(out=hs, in_=ang, func=AF.Sin, scale=0.5), r=[a], w=[h])
    S.op(eng, lambda e: e.tensor_tensor(out=hs, in0=hs, in1=hs, op=ALU.mult), w=[h])
    S.op(eng, lambda e: e.tensor_scalar(out=cos_out, in0=hs, scalar1=-2.0, scalar2=1.0, op0=ALU.mult, op1=ALU.add), r=[h], w=[tag + "cos"])


def rope_tables(nc, S, st, ropef, sinT, cosT, T, tag):
    TS = lambda n, s, d: st.enter_context(nc.sbuf_tensor(n, s, d))
    CH = min(512, T)
    NCH = T // CH
    pi_ = TS(tag + "_pi", [128, CH], I32)
    ang = TS(tag + "_ang", [128, CH], F32)
    kf = TS(tag + "_kf", [128, CH], F32)
    ki = TS(tag + "_ki", [128, CH], I32)
    hs = TS(tag + "_hs", [128, CH], F32)
    s1 = TS(tag + "_s1", [128, CH], F32)
    c1 = TS(tag + "_c1", [128, CH], F32)
    pj = TS(tag + "_pj", [128, NCH], I32)
    ang_b = TS(tag + "_angb", [128, NCH], F32)
    kf_b = TS(tag + "_kfb", [128, NCH], F32)
    ki_b = TS(tag + "_kib", [128, NCH], I32)
    hs_b = TS(tag + "_hsb", [128, NCH], F32)
    s2 = TS(tag + "_s2", [128, NCH], F32)
    c2 = TS(tag + "_c2", [128, NCH], F32)
    ns2 = TS(tag + "_ns2", [128, NCH], F32)
    tmp = [TS(tag + f"_tmp{i}", [128, CH], F32) for i in range(4)]
    S.op("pool", lambda e: e.iota(pi_[:], pattern=[[1, CH]], base=0, channel_multiplier=0), w=[tag + "pi"])
    S.op("pool", lambda e: e.iota(pj[:], pattern=[[CH, NCH]], base=0, channel_multiplier=0), w=[tag + "pj"])
    S.op("dve", lambda e: e.tensor_copy(out=ang[:], in_=pi_[:]), r=[tag + "pi"], w=[tag + "aang"])
    S.op("dve", lambda e: e.tensor_scalar(out=ang[:], in0=ang[:], scalar1=ropef[:, 0:1], scalar2=None, op0=ALU.mult), r=["ropef"], w=[tag + "aang"])
    sincos(S, ang[:], kf[:], ki[:], hs[:], s1[:], c1[:], tag + "a")
    S.op("dve", lambda e: e.tensor_copy(out=ang_b[:], in_=pj[:]), r=[tag + "pj"], w=[tag + "bang"])
    S.op("dve", lambda e: e.tensor_scalar(out=ang_b[:], in0=ang_b[:], scalar1=ropef[:, 0:1], scalar2=None, op0=ALU.mult), r=["ropef"], w=[tag + "bang"])
    sincos(S, ang_b[:], kf_b[:], ki_b[:], hs_b[:], s2[:], c2[:], tag + "b")
    S.op("dve", lambda e: e.tensor_scalar(out=ns2[:], in0=s2[:], scalar1=-1.0, scalar2=None, op0=ALU.mult), r=[tag + "bsin"], w=[tag + "ns2"])
    rd = [tag + "asin", tag + "acos", tag + "bsin", tag + "bcos", tag + "ns2"]
    for c in range(NCH):
        sl = slice(c * CH, (c + 1) * CH)
        ta, tb = tmp[(2 * c) % 4], tmp[(2 * c + 1) % 4]
        na, nb = (tag + "tmp", (2 * c) % 4), (tag + "tmp", (2 * c + 1) % 4)
        S.op("dve", lambda e, c=c, ta=ta: e.tensor_scalar(out=ta[:], in0=c1[:], scalar1=s2[:, c:c + 1], scalar2=None, op0=ALU.mult), r=rd, w=[na])
        S.op("dve", lambda e, c=c, ta=ta, sl=sl: e.scalar_tensor_tensor(out=sinT[:, sl], in0=s1[:], scalar=c2[:, c:c + 1], in1=ta[:],
                                                                        op0=ALU.mult, op1=ALU.add), r=rd + [na], w=[(tag + "sin", c)])
        S.op("dve", lambda e, c=c, tb=tb: e.tensor_scalar(out=tb[:], in0=s1[:], scalar1=ns2[:, c:c + 1], scalar2=None, op0=ALU.mult), r=rd, w=[nb])
        S.op("dve", lambda e, c=c, tb=tb, sl=sl: e.scalar_tensor_tensor(out=cosT[:, sl], in0=c1[:], scalar=c2[:, c:c + 1], in1=tb[:],
                                                                        op0=ALU.mult, op1=ALU.add), r=rd + [nb], w=[(tag + "cos", c)])
    return [(tag + "sin", c) for c in range(NCH)] + [(tag + "cos", c) for c in range(NCH)]


def phase_attn(nc, S, st, fm, tm_v, lqk, subln, ropef_d, rmat_d, cmask_d, oa, lambda_init, T=SEQ):
    TS = lambda n, s, d: st.enter_context(nc.sbuf_tensor(n, s, d))
    PS = lambda n: st.enter_context(nc.psum_tensor(n, [128, 512], F32))
    NQ = T // 512
    NK = T // 128
    ropef = TS("at_ropef", [128, 1], F32)
    rm32 = TS("at_rm32", [128, 128], F32)
    rm = TS("at_rm", [128, 128], BF16)
    cmask = TS("at_cmask", [128, 4, 512], BF16)
    ones = TS("at_ones", [128, 128], BF16)
    sinT = TS("at_sin", [128, T], BF16)
    cosT = TS("at_cos", [128, T], BF16)
    qraw = TS("at_qraw", [128, T], BF16)
    kraw = TS("at_kraw", [128, T], BF16)
    qr = TS("at_qr", [128, T], BF16)
    kr = TS("at_kr", [128, T], BF16)
    sag = TS("at_sag", [128, T], BF16)
    vsb = TS("at_v", [128, NK, 128], BF16)
    lq = TS("at_lq", [128, 256], F32)
    lp = TS("at_lp", [128, 128], F32)
    le = TS("at_le", [128, 2], F32)
    neglam = TS("at_neglam", [128, 1], F32)
    sw = TS("at_sw", [128, 1], F32)
    t1 = [TS(f"at_t1_{i}", [128, 512], BF16) for i in range(2)]
    t2 = [TS(f"at_t2_{i}", [128, 512], BF16) for i in range(2)]
    P = [[TS(f"at_P{i}_{m}", [128, 512], BF16) for m in range(2)] for i in range(3)]
    r0 = [TS(f"at_r0_{i}", [128, 512], F32) for i in range(2)]
    r1 = [TS(f"at_r1_{i}", [128, 512], F32) for i in range(2)]
    o0 = [TS(f"at_o0_{i}", [128, 512], F32) for i in range(2)]
    o1 = [TS(f"at_o1_{i}", [128, 512], F32) for i in range(2)]
    osq = TS("at_osq", [128, 512], BF16)
    nsq = TS("at_nsq", [128, 512], F32)
    ob = [TS(f"at_ob{i}", [128, 512], BF16) for i in range(2)]
    epsc = TS("at_epsc", [128, 1], F32)
    accD = [TS(f"at_accD{i}", [128, 512], F32) for i in range(2)]
    accP = [TS(f"at_accP{i}", [128, 512], F32) for i in range(2)]
    ones32 = TS("at_ones32", [128, 128], F32)
    ps_s = [[PS(f"at_ps_s{i}_{m}") for m in range(2)] for i in range(2)]
    ps_o = [PS(f"at_ps_o{m}") for m in range(2)]
    ps_l = [PS(f"at_ps_l{m}") for m in range(2)]
    ps_n = ps_s[0][0]

    S.op("sp", lambda e: e.dma_start(out=ropef[:], in_=ropef_d[:, :]), w=["ropef"], dma=True)
    S.op("sp", lambda e: e.dma_start(out=rm32[:], in_=rmat_d[:, :]), w=["rm32"], dma=True)
    S.op("sp", lambda e: e.dma_start(out=cmask[:], in_=cmask_d.rearrange("d p q -> p d q")), w=["cmask"], dma=True)
    S.op("sp", lambda e: e.dma_start(out=lq[:], in_=lqk.partition_broadcast(128)), w=["lq"], dma=True)
    S.op("sp", lambda e: e.dma_start(out=sw[:], in_=subln[:, :]), w=["sw"], dma=True)
    S.op("sp", lambda e: e.dma_start(out=qraw[:], in_=fm[320:448, :]), w=["qraw"], dma=True)
    S.op("sp", lambda e: e.dma_start(out=kraw[:], in_=fm[448:576, :]), w=["kraw"], dma=True)
    S.op("sp", lambda e: e.dma_start(out=sag[:], in_=fm[576:704, :]), w=["sag"], dma=True)
    S.op("pool", lambda e: e.dma_start(out=vsb[:], in_=tm_v[:, 64:192].rearrange("(a p) c -> p a c", p=128)), w=["vsb"], dma=True)
    S.op("pool", lambda e: e.memset(ones[:], 1.0), w=["ones"])
    S.op("pool", lambda e: e.memset(epsc[:], EPS), w=["epsc"])
    S.op("pool", lambda e: e.memset(ones32[:], 1.0), w=["ones32"])
    S.op("dve", lambda e: e.tensor_copy(out=rm[:], in_=rm32[:]), r=["rm32"], w=["rm"])
    S.op("dve", lambda e: e.tensor_tensor(out=lp[:], in0=lq[:, 0:128], in1=lq[:, 128:256], op=ALU.mult), r=["lq"], w=["lp"])
    S.op("dve", lambda e: e.tensor_reduce(out=le[:], in_=lp[:].rearrange("p (a c) -> p a c", a=2), axis=AX.X, op=ALU.add),
         r=["lp"], w=["le"])
    S.op("act", lambda e: e.activation(out=le[:], in_=le[:], func=AF.Exp), w=["le"])
    S.op("dve", lambda e: e.tensor_tensor(out=neglam[:], in0=le[:, 1:2], in1=le[:, 0:1], op=ALU.subtract), r=["le"], w=["neglam"])
    S.op("dve", lambda e: e.tensor_scalar(out=neglam[:], in0=neglam[:], scalar1=-lambda_init, scalar2=None, op0=ALU.add), w=["neglam"])
    S.op("dve", lambda e: e.tensor_scalar(out=sw[:], in0=sw[:], scalar1=1.0 - lambda_init, scalar2=None, op0=ALU.mult), w=["sw"])
    tabs = rope_tables(nc, S, st, ropef, sinT, cosT, T, "at_rp")
    ri = 0
    for src, dst, sn, dn in ((qraw, qr, "qraw", "qr"), (kraw, kr, "kraw", "kr")):
        for ti in range(NQ):
            sl = slice(ti * 512, (ti + 1) * 512)
            b = ri % 2
            ri += 1
            S.op("pe", lambda e, src=src, sl=sl, b=b: e.matmul(ps_s[b][0][:], rm[:], src[:, sl], start=True, stop=True),
                 r=[sn, "rm"], w=[f"ps_s{b}0"])
            S.op("dve", lambda e, src=src, sl=sl, b=b: e.tensor_tensor(out=t1[b][:], in0=src[:, sl], in1=cosT[:, sl], op=ALU.mult),
                 r=[sn] + tabs, w=[("t1", b)])
            S.op("dve", lambda e, sl=sl, b=b: e.tensor_tensor(out=t2[b][:], in0=ps_s[b][0][:], in1=sinT[:, sl], op=ALU.mult),
                 r=tabs, w=[("t2", b), f"ps_s{b}0"])
            S.op("pool", lambda e, dst=dst, sl=sl, b=b: e.tensor_tensor(out=dst[:, sl], in0=t1[b][:], in1=t2[b][:], op=ALU.add),
                 r=[("t1", b), ("t2", b)], w=[(dn, ti)])
    qr_all = [("qr", ti) for ti in range(NQ)]
    kr_all = [("kr", ti) for ti in range(NQ)]
    psn = lambda i, m: f"ps_s{i}{m}"
    dq = []

    def pop_deferred(n, limit=2):
        k = 0
        while dq and k < limit:
            fn, need_odd = dq[0]
            if need_odd and n % 2 == 0:
                break
            dq.pop(0)
            fn()
            k += 1
    for qi in range(NQ):
        qs = slice(qi * 512, (qi + 1) * 512)
        nk = 4 * (qi + 1)
        ab = qi % 2

        def QK(n, qs=qs, qi=qi):
            i = n % 2
            for m in range(2):
                S.op("pe", lambda e, n=n, i=i, m=m, qs=qs: e.matmul(ps_s[i][m][:], kr[64 * m:64 * m + 64, n * 128:(n + 1) * 128],
                                                                qr[64 * m:64 * m + 64, qs], start=True, stop=True),
                     r=[("qr", qi), ("kr", n // 4)], w=[psn(i, m)])

        def EXP(n, qi=qi):
            i = n % 2
            j = n % 3
            for m in range(2):
                S.op("act", lambda e, i=i, j=j, m=m: e.activation(out=P[j][m][:], in_=ps_s[i][m][:], func=AF.Exp, scale=0.125),
                     w=[psn(i, m), ("P", j, m)])
                d = n - 4 * qi
                if d >= 0:
                    S.op("dve", lambda e, j=j, m=m, d=d: e.tensor_tensor(out=P[j][m][:], in0=P[j][m][:], in1=cmask[:, d, :], op=ALU.mult),
                         r=["cmask"], w=[("P", j, m)])

        def PV(n, nk=nk, ab=ab):
            j = n % 3
            for m in range(2):
                S.op("pe", lambda e, n=n, j=j, m=m, nk=nk: e.matmul(ps_o[m][:], vsb[:, n, :], P[j][m][:], start=(n == 0), stop=(n == nk - 1)),
                     r=[("P", j, m), "vsb"], w=[f"ps_o{m}"])
            S.op("pe", lambda e, n=n, j=j, nk=nk: e.matmul(ps_l[0][:], ones[:], P[j][0][:], start=(n == 0), stop=(n == nk - 1)),
                 r=[("P", j, 0), "ones"], w=["ps_l0"])
            eng, acc, an = ("dve", accD[ab], ("accD", ab)) if n % 2 == 0 else ("pool", accP[ab], ("accP", ab))
            if n < 2:
                S.op(eng, lambda e, j=j, acc=acc: e.tensor_copy(out=acc[:], in_=P[j][1][:]), r=[("P", j, 1)], w=[an])
            else:
                S.op(eng, lambda e, j=j, acc=acc: e.tensor_tensor(out=acc[:], in0=acc[:], in1=P[j][1][:], op=ALU.add), r=[("P", j, 1)], w=[an])
        QK(0)
        for n in range(nk):
            if n >= 1:
                pop_deferred(n)
            if n + 1 < nk:
                QK(n + 1)
            EXP(n)
            PV(n)
        while dq:
            fn, need_odd = dq.pop(0)
            fn()
        eb = qi % 2
        S.op("dve", lambda e, eb=eb: e.tensor_copy(out=o0[eb][:], in_=ps_o[0][:]), w=[("o0", eb), "ps_o0"])
        S.op("dve", lambda e, eb=eb: e.tensor_copy(out=o1[eb][:], in_=ps_o[1][:]), w=[("o1", eb), "ps_o1"])
        S.op("dve", lambda e, eb=eb: e.tensor_copy(out=r0[eb][:], in_=ps_l[0][:]), w=[("r0", eb), "ps_l0"])
        D = lambda fn, odd=False: dq.append((fn, odd))
        D(lambda ab=ab: S.op("pe", lambda e: e.matmul(ps_l[1][:], ones32[:], accD[ab][:], start=True, stop=False),
                             r=[("accD", ab), "ones32"], w=["ps_l1"]))
        D(lambda ab=ab: S.op("pe", lambda e: e.matmul(ps_l[1][:], ones32[:], accP[ab][:], start=False, stop=True),
                             r=[("accP", ab), "ones32"], w=["ps_l1"]))
        D(lambda eb=eb: S.op("dve", lambda e: e.tensor_copy(out=r1[eb][:], in_=ps_l[1][:]), w=[("r1", eb), "ps_l1"]))
        for rr, rn in ((r0, "r0"), (r1, "r1")):
            D(lambda eb=eb, rr=rr, rn=rn: S.op("act", lambda e: e.activation(out=rr[eb][:], in_=rr[eb][:], func=AF.Ln), w=[(rn, eb)]))
            D(lambda eb=eb, rr=rr, rn=rn: S.op("act", lambda e: e.activation(out=rr[eb][:], in_=rr[eb][:], func=AF.Exp, scale=-1.0), w=[(rn, eb)]))
        D(lambda eb=eb: S.op("dve", lambda e: e.tensor_tensor(out=o0[eb][:], in0=o0[eb][:], in1=r0[eb][:], op=ALU.mult), r=[("r0", eb)], w=[("o0", eb)]))
        D(lambda eb=eb: S.op("dve", lambda e: e.tensor_tensor(out=o1[eb][:], in0=o1[eb][:], in1=r1[eb][:], op=ALU.mult), r=[("r1", eb)], w=[("o1", eb)]))
        D(lambda eb=eb: S.op("dve", lambda e: e.scalar_tensor_tensor(out=o0[eb][:], in0=o1[eb][:], scalar=neglam[:, 0:1], in1=o0[eb][:],
                                                                     op0=ALU.mult, op1=ALU.add), r=[("o1", eb), "neglam"], w=[("o0", eb)]))
        D(lambda eb=eb: S.op("act", lambda e: e.activation(out=osq[:], in_=o0[eb][:], func=AF.Square), r=[("o0", eb)], w=["osq"]))
        D(lambda: S.op("pe", lambda e: e.matmul(ps_n[:], ones[:], osq[:], start=True, stop=True), r=["osq", "ones"], w=[psn(0, 0)]), True)
        D(lambda: S.op("act", lambda e: e.activation(out=nsq[:], in_=ps_n[:], func=AF.Ln, scale=1.0 / 128.0, bias=epsc[:, 0:1]),
                       r=["epsc"], w=["nsq", psn(0, 0)]))
        D(lambda: S.op("act", lambda e: e.activation(out=nsq[:], in_=nsq[:], func=AF.Exp, scale=-0.5), w=["nsq"]))
        D(lambda eb=eb: S.op("dve", lambda e: e.scalar_tensor_tensor(out=o0[eb][:], in0=o0[eb][:], scalar=sw[:, 0:1], in1=nsq[:],
                                                                     op0=ALU.mult, op1=ALU.mult), r=["nsq", "sw"], w=[("o0", eb)]))
        D(lambda eb=eb, qs=qs: S.op("pool", lambda e: e.tensor_tensor(out=ob[eb][:], in0=o0[eb][:], in1=sag[:, qs], op=ALU.mult),
                                    r=[("o0", eb), "sag"], w=[("ob", eb)]))
        D(lambda eb=eb, qs=qs: S.op("sp", lambda e: e.dma_start(out=oa[:, qs], in_=ob[eb][:]), r=[("ob", eb)], dma=True))
    while dq:
        fn, need_odd = dq.pop(0)
        fn()


def attn_consts():
    ropef = np.zeros((128, 1), np.float32)
    inv = (ROPE_THETA ** (-np.arange(0, 16, 2, dtype=np.float32) / 16.0)).astype(np.float32)
    rmat = np.zeros((128, 128), np.float32)
    for base in (0, 64):
        for i in range(8):
            ropef[base + i, 0] = -inv[i]
            ropef[base + 8 + i, 0] = inv[i]
            rmat[base + 8 + i, base + i] = 1.0
            rmat[base + i, base + 8 + i] = 1.0
    k = np.arange(128)[:, None]
    q = np.arange(512)[None, :]
    cmask = np.stack([(128 * d + k <= q) for d in range(4)]).astype(ml_dtypes.bfloat16)
    return ropef, rmat, cmask


def build_attn(lambda_init, T=SEQ):
    nc = bass.Bass("TRN2", target_bir_lowering=False)
    fm = nc.dram_tensor("fm", [NFM, T], BF16, kind="ExternalInput").ap()
    tm_v = nc.dram_tensor("tm_v", [T, 192], BF16, kind="ExternalInput").ap()
    lqk = nc.dram_tensor("lqk", [1, 256], F32, kind="ExternalInput").ap()
    subln = nc.dram_tensor("subln", [128, 1], F32, kind="ExternalInput").ap()
    ropef = nc.dram_tensor("ropef", [128, 1], F32, kind="ExternalInput").ap()
    rmat = nc.dram_tensor("rmat", [128, 128], F32, kind="ExternalInput").ap()
    cmask = nc.dram_tensor("cmask", [4, 128, 512], BF16, kind="ExternalInput").ap()
    oa = nc.dram_tensor("oa", [128, T], BF16, kind="ExternalOutput").ap()
    with contextlib.ExitStack() as st:
        S = Sched(nc)
        phase_attn(nc, S, st, fm, tm_v, lqk, subln, ropef, rmat, cmask, oa, lambda_init, T)
        S.emit()
    return nc


def hgrn_consts():
    s = np.arange(128)[:, None]
    t = np.arange(128)[None, :]
    same = (s // 16) == (t // 16)
    m_incl = (same & (s <= t)).astype(np.float32)
    m_rev = (same & (s > t)).astype(np.float32)
    m_tot8 = ((s // 16) == np.arange(8)[None, :]).astype(np.float32)
    mcat = np.concatenate([m_incl, m_tot8], axis=1)
    return mcat, m_rev


def phase_hgrn(nc, S, st, fm, tm_sf, tm_v, lbl_bc_d, lbl_col_d, gw_d, mcat_d, mrev_d, oh, lb_coef, T=SEQ):
    TS = lambda n, s, d: st.enter_context(nc.sbuf_tensor(n, s, d))
    PS = lambda n: st.enter_context(nc.psum_tensor(n, [128, 512], F32))
    NT = T // 128
    sq = TS("hg_sq", [64, T], BF16)
    snf = TS("hg_snf", [64, T], BF16)
    shg = TS("hg_shg", [64, T], BF16)
    sf = TS("hg_sf", [128, NT, 64], F32)
    omf = TS("hg_omf", [128, NT, 64], F32)
    logf = TS("hg_logf", [128, NT, 64], F32)
    vall = TS("hg_v", [128, NT, 64], BF16)
    lbl = TS("hg_lbl", [128, 128], F32)
    lb_bc = TS("hg_lb_bc", [128, 64], F32)
    oml_bc = TS("hg_oml_bc", [128, 64], F32)
    lblc = TS("hg_lblc", [64, 2], F32)
    oml_col = TS("hg_oml_col", [64, 1], F32)
    gw = TS("hg_gw", [64, 1], F32)
    mcat = TS("hg_mcat", [128, 136], F32)
    mrev = TS("hg_mrev", [128, 128], F32)
    mincl_bf = TS("hg_mincl", [128, 128], BF16)
    mtot_bf = TS("hg_mtot", [128, 8], BF16)
    ones64 = TS("hg_ones", [64, 64], BF16)
    epsc = TS("hg_epsc", [64, 1], F32)
    eq = TS("hg_eq", [64, 128], F32)
    ekn = TS("hg_ekn", [64, 128], F32)
    dec = [TS(f"hg_dec{i}", [64, 8], F32) for i in range(3)]
    ehat = TS("hg_ehat", [128, 64], F32)
    qt = [TS(f"hg_qt{i}", [64, 128], BF16) for i in range(3)]
    kt = TS("hg_kt", [64, 128], BF16)
    khat = TS("hg_khat", [128, 64], BF16)
    vblk = TS("hg_vblk", [128, 8, 64], BF16)
    scm = [TS(f"hg_scm{i}", [128, 128], BF16) for i in range(3)]
    Sall = [TS(f"hg_S{i}", [64, 9, 64], F32) for i in range(2)]
    Sbf = [TS(f"hg_Sbf{i}", [64, 8, 64], BF16) for i in range(2)]
    osq = TS("hg_osq", [64, 512], BF16)
    o32 = TS("hg_o32", [64, 512], F32)
    nsq = TS("hg_nsq", [64, 512], F32)
    ohb = [TS(f"hg_ohb{i}", [64, 512], BF16) for i in range(2)]
    ps_c = PS("hg_ps_c")
    ps_r = PS("hg_ps_r")
    ps_sc = PS("hg_ps_sc")
    ps_u = [PS(f"hg_ps_u{i}") for i in range(3)]
    ps_oh = [PS(f"hg_ps_oh{i}") for i in range(2)]

    S.op("sp", lambda e: e.dma_start(out=sq[:], in_=fm[0:64, :]), w=["sq"], dma=True)
    S.op("sp", lambda e: e.dma_start(out=snf[:], in_=fm[64:128, :]), w=["snf"], dma=True)
    S.op("sp", lambda e: e.dma_start(out=shg[:], in_=fm[128:192, :]), w=["shg"], dma=True)
    S.op("sp", lambda e: e.dma_start(out=sf[:], in_=tm_sf.rearrange("(a p) c -> p a c", p=128)), w=["sf"], dma=True)
    S.op("pool", lambda e: e.dma_start(out=vall[:], in_=tm_v[:, 0:64].rearrange("(a p) c -> p a c", p=128)), w=["vall"], dma=True)
    S.op("sp", lambda e: e.dma_start(out=lbl[:], in_=lbl_bc_d.partition_broadcast(128)), w=["lbl"], dma=True)
    S.op("sp", lambda e: e.dma_start(out=lblc[:], in_=lbl_col_d[:, :]), w=["lblc"], dma=True)
    S.op("sp", lambda e: e.dma_start(out=gw[:], in_=gw_d[:, :]), w=["gw"], dma=True)
    S.op("sp", lambda e: e.dma_start(out=mcat[:], in_=mcat_d[:, :]), w=["mcat"], dma=True)
    S.op("sp", lambda e: e.dma_start(out=mrev[:], in_=mrev_d[:, :]), w=["mrev"], dma=True)
    S.op("pool", lambda e: e.memset(ones64[:], 1.0), w=["ones64"])
    S.op("pool", lambda e: e.memset(epsc[:], EPS), w=["epsc"])
    S.op("pool", lambda e: e.memset(Sall[0][:, 0, :], 0.0), w=[("S", 0)])
    S.op("dve", lambda e: e.tensor_copy(out=mincl_bf[:], in_=mcat[:, 0:128]), r=["mcat"], w=["mincl_bf"])
    S.op("dve", lambda e: e.tensor_copy(out=mtot_bf[:], in_=mcat[:, 128:136]), r=["mcat"], w=["mtot_bf"])
    S.op("dve", lambda e: e.tensor_tensor(out=lb_bc[:], in0=lbl[:, 64:128], in1=lbl[:, 0:64], op=ALU.subtract), r=["lbl"], w=["lb_bc"])
    S.op("act", lambda e: e.activation(out=lb_bc[:], in_=lb_bc[:], func=AF.Sigmoid), w=["lb_bc"])
    S.op("dve", lambda e: e.tensor_scalar(out=lb_bc[:], in0=lb_bc[:], scalar1=float(lb_coef), scalar2=None, op0=ALU.mult), w=["lb_bc"])
    S.op("dve", lambda e: e.tensor_scalar(out=oml_bc[:], in0=lb_bc[:], scalar1=-1.0, scalar2=1.0, op0=ALU.mult, op1=ALU.add),
         r=["lb_bc"], w=["oml_bc"])
    S.op("dve", lambda e: e.tensor_tensor(out=oml_col[:], in0=lblc[:, 1:2], in1=lblc[:, 0:1], op=ALU.subtract), r=["lblc"], w=["oml_col"])
    S.op("act", lambda e: e.activation(out=oml_col[:], in_=oml_col[:], func=AF.Sigmoid), w=["oml_col"])
    S.op("dve", lambda e: e.tensor_scalar(out=oml_col[:], in0=oml_col[:], scalar1=-float(lb_coef), scalar2=1.0, op0=ALU.mult, op1=ALU.add),
         w=["oml_col"])
    for g in range(NT // 8 if NT >= 8 else 1):
        nt = min(8, NT)
        sl = slice(g * 8, g * 8 + nt)
        S.op("dve", lambda e, sl=sl, nt=nt: e.tensor_tensor(out=sf[:, sl, :], in0=sf[:, sl, :],
                                                        in1=oml_bc[:].unsqueeze(1).broadcast_to([128, nt, 64]), op=ALU.mult),
             r=["oml_bc"], w=["sf"])
        S.op("dve", lambda e, sl=sl, nt=nt: e.tensor_tensor(out=sf[:, sl, :], in0=sf[:, sl, :],
                                                        in1=lb_bc[:].unsqueeze(1).broadcast_to([128, nt, 64]), op=ALU.add),
             r=["lb_bc"], w=["sf"])
        S.op("act", lambda e, sl=sl: e.activation(out=logf[:, sl, :], in_=sf[:, sl, :], func=AF.Ln), r=["sf"], w=[("logf", g)])
        S.op("pool", lambda e, sl=sl: e.tensor_scalar(out=omf[:, sl, :], in0=sf[:, sl, :], scalar1=-1.0, scalar2=1.0,
                                                     op0=ALU.mult, op1=ALU.add), r=["sf"], w=[("omf", g)])
    def stageA1(i):
        g = i // 8
        b = i % 3
        S.op("pe", lambda e, i=i: e.matmul(ps_c[0:64, 0:136], logf[:, i, :], mcat[:], start=True, stop=True),
             r=[("logf", g), "mcat"], w=["ps_c"])
        S.op("pe", lambda e, i=i: e.matmul(ps_r[:, 0:64], mrev[:], logf[:, i, :], start=True, stop=True),
             r=[("logf", g), "mrev"], w=["ps_r"])
        S.op("act", lambda e: e.activation(out=eq[:], in_=ps_c[0:64, 0:128], func=AF.Exp), w=["eq", "ps_c"])
        S.op("act", lambda e: e.activation(out=ekn[:], in_=ps_c[0:64, 0:128], func=AF.Exp, scale=-1.0), w=["ekn", "ps_c"])
        S.op("act", lambda e, b=b: e.activation(out=dec[b][:], in_=ps_c[0:64, 128:136], func=AF.Exp), w=[("dec", b), "ps_c"])
        S.op("act", lambda e: e.activation(out=ehat[:], in_=ps_r[:, 0:64], func=AF.Exp), w=["ehat", "ps_r"])

    def stageA2(i):
        g = i // 8
        b = i % 3
        ts = slice(i * 128, (i + 1) * 128)
        S.op("dve", lambda e, b=b, ts=ts: e.tensor_tensor(out=qt[b][:], in0=sq[:, ts], in1=eq[:], op=ALU.mult),
             r=["sq", "eq"], w=[("qt", b)])
        S.op("dve", lambda e, ts=ts: e.scalar_tensor_tensor(out=kt[:], in0=snf[:, ts], scalar=oml_col[:, 0:1], in1=ekn[:],
                                                           op0=ALU.mult, op1=ALU.mult), r=["snf", "ekn", "oml_col"], w=["kt"])
        S.op("dve", lambda e, i=i: e.tensor_tensor(out=khat[:], in0=omf[:, i, :], in1=ehat[:], op=ALU.mult),
             r=[("omf", g), "ehat"], w=["khat"])
        S.op("pool", lambda e, i=i: e.tensor_tensor(out=vblk[:], in0=vall[:, i, :].unsqueeze(1).broadcast_to([128, 8, 64]),
                                                   in1=mtot_bf[:].unsqueeze(2).broadcast_to([128, 8, 64]), op=ALU.mult),
             r=["vall", "mtot_bf"], w=["vblk"])
        S.op("pe", lambda e, b=b: e.matmul(ps_sc[:, 0:128], kt[:], qt[b][:], start=True, stop=True), r=["kt", ("qt", b)], w=["ps_sc"])
        S.op("dve", lambda e, b=b: e.tensor_tensor(out=scm[b][:], in0=ps_sc[:, 0:128], in1=mincl_bf[:], op=ALU.mult),
             r=["mincl_bf"], w=[("scm", b), "ps_sc"])
        S.op("pe", lambda e, b=b: e.matmul(ps_u[b][0:64, :], khat[:], vblk[:].rearrange("p a c -> p (a c)"), start=True, stop=True),
             r=["khat", "vblk"], w=[("ps_u", b)])

    def stageB(i):
        b = i % 3
        sb = i % 2
        if i > 0:
            S.op("dve", lambda e, sb=sb: e.tensor_copy(out=Sall[sb][:, 0, :], in_=Sall[1 - sb][:, 8, :]), r=[("S", 1 - sb)], w=[("S", sb)])
        for n in range(8):
            S.op("dve", lambda e, b=b, sb=sb, n=n: e.scalar_tensor_tensor(
                out=Sall[sb][:, n + 1, :], in0=Sall[sb][:, n, :], scalar=dec[b][:, n:n + 1], in1=ps_u[b][0:64, n * 64:(n + 1) * 64],
                op0=ALU.mult, op1=ALU.add), r=[("dec", b)], w=[("S", sb), ("ps_u", b)])
        S.op("act", lambda e, sb=sb: e.activation(out=Sbf[sb][:], in_=Sall[sb][:, 0:8, :], func=AF.Copy), r=[("S", sb)], w=[("Sbf", sb)])

    def stageC(i):
        b = i % 3
        sb = i % 2
        ob = (i // 4) % 2
        c0 = (i % 4) * 128
        S.op("pe", lambda e, i=i, ob=ob, c0=c0, b=b: e.matmul(ps_oh[ob][0:64, c0:c0 + 128], vall[:, i, :], scm[b][:], start=True, stop=False),
             r=["vall", ("scm", b)], w=[("ps_oh", ob)])
        for n in range(8):
            S.op("pe", lambda e, b=b, sb=sb, ob=ob, c0=c0, n=n: e.matmul(
                ps_oh[ob][0:64, c0 + 16 * n:c0 + 16 * n + 16], Sbf[sb][:, n, :], qt[b][:, 16 * n:16 * n + 16],
                start=False, stop=(n == 7)), r=[("Sbf", sb), ("qt", b)], w=[("ps_oh", ob)])
        if i % 4 == 3 or i == NT - 1:
            qs = slice((i // 4) * 512, (i // 4) * 512 + 512)
            S.op("act", lambda e, ob=ob: e.activation(out=osq[:], in_=ps_oh[ob][0:64, :], func=AF.Square), w=["osq", ("ps_oh", ob)])
            S.op("act", lambda e, ob=ob: e.activation(out=o32[:], in_=ps_oh[ob][0:64, :], func=AF.Copy), w=["o32", ("ps_oh", ob)])
            S.op("pe", lambda e: e.matmul(ps_sc[0:64, :], ones64[:], osq[:], start=True, stop=True), r=["osq", "ones64"], w=["ps_sc"])
            S.op("act", lambda e: e.activation(out=nsq[:], in_=ps_sc[0:64, :], func=AF.Ln, scale=1.0 / 64.0, bias=epsc[:, 0:1]), r=["epsc"], w=["nsq", "ps_sc"])
            S.op("act", lambda e: e.activation(out=nsq[:], in_=nsq[:], func=AF.Exp, scale=-0.5), w=["nsq"])
            S.op("dve", lambda e: e.scalar_tensor_tensor(out=o32[:], in0=o32[:], scalar=gw[:, 0:1], in1=nsq[:], op0=ALU.mult, op1=ALU.mult),
                 r=["nsq", "gw"], w=["o32"])
            S.op("pool", lambda e, ob=ob, qs=qs: e.tensor_tensor(out=ohb[ob][:], in0=o32[:], in1=shg[:, qs], op=ALU.mult),
                 r=["o32", "shg"], w=[("ohb", ob)])
            S.op("sp", lambda e, ob=ob, qs=qs: e.dma_start(out=oh[:, qs], in_=ohb[ob][:]), r=[("ohb", ob)], dma=True)
    for i0_ in range(min(2, NT)):
        stageA1(i0_)
        stageA2(i0_)
    for i in range(NT):
        if i + 2 < NT:
            stageA1(i + 2)
        stageB(i)
        if i + 2 < NT:
            stageA2(i + 2)
        stageC(i)


def build_hgrn(lb_coef, T=SEQ):
    nc = bass.Bass("TRN2", target_bir_lowering=False)
    fm = nc.dram_tensor("fm", [NFM, T], BF16, kind="ExternalInput").ap()
    tm_sf = nc.dram_tensor("tm_sf", [T, 64], F32, kind="ExternalInput").ap()
    tm_v = nc.dram_tensor("tm_v", [T, 192], BF16, kind="ExternalInput").ap()
    lbl_bc = nc.dram_tensor("lbl_bc", [1, 128], F32, kind="ExternalInput").ap()
    lbl_col = nc.dram_tensor("lbl_col", [64, 2], F32, kind="ExternalInput").ap()
    gw = nc.dram_tensor("gw", [64, 1], F32, kind="ExternalInput").ap()
    mcat = nc.dram_tensor("mcat", [128, 136], F32, kind="ExternalInput").ap()
    mrev = nc.dram_tensor("mrev", [128, 128], F32, kind="ExternalInput").ap()
    oh = nc.dram_tensor("oh", [64, T], BF16, kind="ExternalOutput").ap()
    with contextlib.ExitStack() as st:
        S = Sched(nc)
        phase_hgrn(nc, S, st, fm, tm_sf, tm_v, lbl_bc, lbl_col, gw, mcat, mrev, oh, lb_coef, T)
        S.emit()
    return nc


def cmul(S, eng, o_re, o_im, a_re, a_im, b_re, b_im, t0, t1, rd, wr, conj_a=False):
    sg = -1.0 if conj_a else 1.0
    S.op(eng, lambda e: e.tensor_tensor(out=t0, in0=a_im, in1=b_im, op=ALU.mult), r=rd, w=[wr + "t0"])
    S.op(eng, lambda e: e.tensor_tensor(out=t1, in0=a_re, in1=b_re, op=ALU.mult), r=rd, w=[wr + "t1"])
    S.op("dve", lambda e: e.scalar_tensor_tensor(out=o_re, in0=t0, scalar=-sg, in1=t1, op0=ALU.mult, op1=ALU.add),
         r=[wr + "t0", wr + "t1"], w=[wr + "re"])
    S.op(eng, lambda e: e.tensor_tensor(out=t0, in0=a_im, in1=b_re, op=ALU.mult), r=rd + [wr + "re"], w=[wr + "t0"])
    S.op(eng, lambda e: e.tensor_tensor(out=t1, in0=a_re, in1=b_im, op=ALU.mult), r=rd + [wr + "re"], w=[wr + "t1"])
    S.op("dve", lambda e: e.scalar_tensor_tensor(out=o_im, in0=t0, scalar=sg, in1=t1, op0=ALU.mult, op1=ALU.add),
         r=[wr + "t0", wr + "t1"], w=[wr + "im"])


def s5_consts():
    negsig = np.repeat(-np.arange(16, dtype=np.float32), 64)[None, :]
    kidx = np.arange(32, dtype=np.float32)[None, :]
    midx = np.arange(1, 513, dtype=np.float32)[None, :]
    rowmask = (np.arange(64)[:, None] // 16 == np.arange(4)[None, :]).astype(np.float32)
    return negsig, kidx, midx, rowmask


def s5_params(z, l, j):
    gs = [4 * j + gl for gl in range(4)]
    f = np.float32
    pA_are = np.concatenate([np.repeat(z["s5_a_re"][l][g][None, :], 16, 0) for g in gs]).astype(f)
    pA_aim = np.concatenate([np.repeat(z["s5_a_im"][l][g][None, :], 16, 0) for g in gs]).astype(f)
    pA_ldt = np.concatenate([np.full((16, 1), z["s5_log_dt"][l][g]) for g in gs]).astype(f)
    pA_bre = np.concatenate([z["s5_b_re"][l][g].T for g in gs]).astype(f)
    pA_bim = np.concatenate([z["s5_b_im"][l][g].T for g in gs]).astype(f)
    pB = np.zeros((2, 128, 3), f)
    pB_cre = np.zeros((2, 128, 64), f)
    pB_cim = np.zeros((2, 128, 64), f)
    for q in range(2):
        for h in range(2):
            gl = 2 * q + h
            g = gs[gl]
            rows = slice(64 * h, 64 * h + 64)
            pB[q, rows, 0] = z["s5_a_re"][l][g]
            pB[q, rows, 1] = z["s5_a_im"][l][g]
            pB[q, rows, 2] = z["s5_log_dt"][l][g]
            pB_cre[q, rows, 16 * gl:16 * gl + 16] = z["s5_c_re"][l][g].T
            pB_cim[q, rows, 16 * gl:16 * gl + 16] = z["s5_c_im"][l][g].T
    dcol = z["s5_d"][l][64 * j:64 * j + 64][:, None].astype(f)
    pA = np.concatenate([pA_are, pA_aim, pA_bre, pA_bim, pA_ldt], axis=1)
    return {"s5p_pA": np.ascontiguousarray(pA), "s5p_pB": pB, "s5p_cre": pB_cre, "s5p_cim": pB_cim, "s5p_d": dcol}


def phase_s5(nc, S, st, fm, pA_d, pB_d, cre_d, cim_d, dcol_d, negsig_d, kidx_d, midx_d, rowmask_d, yg, T=SEQ):
    TS = lambda n, s, d: st.enter_context(nc.sbuf_tensor(n, s, d))
    PS = lambda n: st.enter_context(nc.psum_tensor(n, [128, 512], F32))
    NB = T // 16
    su = TS("s5_su", [64, T], BF16)
    outsb = TS("s5_out", [64, T], BF16)
    pA = TS("s5_pA", [64, 257], F32)
    dcol = TS("s5_dcol", [64, 1], F32)
    rowmask = TS("s5_rowmask", [64, 4], F32)
    negsig = TS("s5_negsig", [64, 1024], F32)
    SCR = TS("s5_scr", [128, 8192], F32)
    tA = [SCR[0:64, 1024 * i:1024 * (i + 1)] for i in range(8)]
    tAi = TS("s5_tAi", [64, 1024], I32)
    sA = [TS(f"s5_sA{i}", [64, 64], F32) for i in range(10)]
    sAi = TS("s5_sAi", [64, 64], I32)
    dtA = TS("s5_dtA", [64, 1], F32)
    W1tab = [[TS(f"s5_W1tab{q}{ri}", [64, 16, 128], BF16) for ri in range(2)] for q in range(2)]
    pB = [TS(f"s5_pB{q}", [128, 3], F32) for q in range(2)]
    crep = [TS(f"s5_crep{q}", [128, 64], F32) for q in range(2)]
    cimp = [TS(f"s5_cimp{q}", [128, 64], F32) for q in range(2)]
    kidx = TS("s5_kidx", [128, 32], F32)
    midx = TS("s5_midx", [128, 512], F32)
    tB = [TS(f"s5_tB{i}", [128, 32], F32) for i in range(7)]
    tBi = TS("s5_tBi", [128, 32], I32)
    cB = [TS(f"s5_cB{i}", [128, 1], F32) for i in range(6)]
    cBi = TS("s5_cBi", [128, 1], I32)
    gt = [SCR[:, 2048 * i:2048 * (i + 1)].rearrange("p (k c) -> p k c", k=32) for i in range(2)]
    Gpad = [[TS(f"s5_G{q}{ri}", [128, 32, 64], BF16) for ri in range(2)] for q in range(2)]
    Tc = [TS(f"s5_Tc{q}", [128, 512], F32) for q in range(2)]
    Tsn = [TS(f"s5_Ts{q}", [128, 512], F32) for q in range(2)]
    rho = [TS(f"s5_rho{q}", [128, 1], F32) for q in range(2)]
    l2 = [SCR[:, 4096 + 512 * i:4096 + 512 * (i + 1)] for i in range(6)]
    l2i = TS("s5_l2i", [128, 512], I32)
    roll = [TS(f"s5_roll{i}", [128, 512], F32) for i in range(2)]
    W15 = [[TS(f"s5_W15{q}{ri}", [128, 512], F32) for ri in range(2)] for q in range(2)]
    W1bf = [[TS(f"s5_W1bf{q}{ri}", [128, 16, 512], BF16) for ri in range(2)] for q in range(2)]
    Xbf = [[TS(f"s5_Xbf{q}{ri}", [128, 512], BF16) for ri in range(2)] for q in range(2)]
    ytmp = [TS(f"s5_ytmp{i}", [64, 512], F32) for i in range(2)]
    ps_z = [PS(f"s5_ps_z{i}") for i in range(2)]
    ps_y = [PS(f"s5_ps_y{i}") for i in range(2)]

    ld = lambda eng, dst, src, name: S.op(eng, lambda e: e.dma_start(out=dst, in_=src), w=[name], dma=True)
    ld("sp", su[:], fm[256:320, :], "su")
    ld("sp", pA[:], pA_d[:, :], "pA")
    ld("sp", dcol[:], dcol_d[:, :], "dcol")
    ld("sp", rowmask[:], rowmask_d[:, :], "rowmask")
    ld("sp", negsig[:], negsig_d.partition_broadcast(64), "negsig")
    ld("sp", kidx[:], kidx_d.partition_broadcast(128), "kidx")
    ld("sp", midx[:], midx_d.partition_broadcast(128), "midx")
    for q in range(2):
        ld("sp", pB[q][:], pB_d[q], ("pB", q))
        ld("sp", crep[q][:], cre_d[q], ("crep", q))
        ld("sp", cimp[q][:], cim_d[q], ("cimp", q))
    are, aim, bre, bim, ldt = pA[:, 0:64], pA[:, 64:128], pA[:, 128:192], pA[:, 192:256], pA[:, 256:257]
    lam, th, abr, abi, mg, zr, zi, den, u0, u1 = [t[:] for t in sA]
    S.op("act", lambda e: e.activation(out=dtA[:], in_=ldt, func=AF.Exp), r=["pA"], w=["dtA"])
    S.op("dve", lambda e: e.tensor_scalar(out=lam, in0=are, scalar1=dtA[:, 0:1], scalar2=None, op0=ALU.mult), r=["pA", "dtA"], w=["lamA"])
    S.op("dve", lambda e: e.tensor_scalar(out=th, in0=aim, scalar1=dtA[:, 0:1], scalar2=None, op0=ALU.mult), r=["pA", "dtA"], w=["thA"])
    S.op("dve", lambda e: e.tensor_copy(out=u0, in_=th), r=["thA"], w=["sAang"])
    sincos(S, u0, u1, sAi[:], den, abi, abr, "sA")
    S.op("act", lambda e: e.activation(out=mg, in_=lam, func=AF.Exp), r=["lamA"], w=["mgA"])
    S.op("dve", lambda e: e.tensor_tensor(out=abr, in0=abr, in1=mg, op=ALU.mult), r=["mgA", "sAcos"], w=["abr"])
    S.op("dve", lambda e: e.tensor_tensor(out=abi, in0=abi, in1=mg, op=ALU.mult), r=["mgA", "sAsin"], w=["abi"])
    S.op("dve", lambda e: e.tensor_scalar(out=abr, in0=abr, scalar1=-1.0, scalar2=None, op0=ALU.add), w=["abr"])
    S.op("dve", lambda e: e.tensor_tensor(out=den, in0=are, in1=are, op=ALU.mult), r=["pA", "sAcos", "sAsin"], w=["den"])
    S.op("dve", lambda e: e.tensor_tensor(out=u0, in0=aim, in1=aim, op=ALU.mult), r=["pA", "sAsin"], w=["u0"])
    S.op("dve", lambda e: e.tensor_tensor(out=den, in0=den, in1=u0, op=ALU.add), r=["u0"], w=["den"])
    S.op("dve", lambda e: e.reciprocal(out=den, in_=den), w=["den"])
    S.op("dve", lambda e: e.tensor_tensor(out=u0, in0=abr, in1=are, op=ALU.mult), r=["abr"], w=["u0"])
    S.op("dve", lambda e: e.tensor_tensor(out=u1, in0=abi, in1=aim, op=ALU.mult), r=["abi"], w=["u1"])
    S.op("dve", lambda e: e.tensor_tensor(out=zr, in0=u0, in1=u1, op=ALU.add), r=["u0", "u1"], w=["zr"])
    S.op("dve", lambda e: e.tensor_tensor(out=zr, in0=zr, in1=den, op=ALU.mult), r=["den"], w=["zr"])
    S.op("dve", lambda e: e.tensor_tensor(out=u0, in0=abi, in1=are, op=ALU.mult), r=["abi", "zr"], w=["u0"])
    S.op("dve", lambda e: e.tensor_tensor(out=u1, in0=abr, in1=aim, op=ALU.mult), r=["abr", "zr"], w=["u1"])
    S.op("dve", lambda e: e.tensor_tensor(out=zi, in0=u0, in1=u1, op=ALU.subtract), r=["u0", "u1"], w=["zi"])
    S.op("dve", lambda e: e.tensor_tensor(out=zi, in0=zi, in1=den, op=ALU.mult), r=["den"], w=["zi"])
    A3 = lambda t: t[:].rearrange("p (s m) -> p s m", s=16)
    bc3 = lambda ap: ap.unsqueeze(1).broadcast_to([64, 16, 64])
    ang3, kf3, hs3, sn3, cs3, mg3, w_r, w_i = tA
    S.op("dve", lambda e: e.tensor_tensor(out=A3(ang3), in0=A3(negsig), in1=bc3(th), op=ALU.mult), r=["negsig", "thA"], w=["tAang"])
    sincos(S, ang3[:], kf3[:], tAi[:], hs3[:], sn3[:], cs3[:], "tA")
    S.op("dve", lambda e: e.tensor_tensor(out=A3(mg3), in0=A3(negsig), in1=bc3(lam), op=ALU.mult), r=["negsig", "lamA"], w=["mg3"])
    S.op("act", lambda e: e.activation(out=mg3[:], in_=mg3[:], func=AF.Exp), w=["mg3"])
    S.op("dve", lambda e: e.tensor_tensor(out=cs3[:], in0=cs3[:], in1=mg3[:], op=ALU.mult), r=["mg3"], w=["tAcos"])
    S.op("dve", lambda e: e.tensor_tensor(out=sn3[:], in0=sn3[:], in1=mg3[:], op=ALU.mult), r=["mg3"], w=["tAsin"])
    cmul(S, "dve", A3(w_r), A3(w_i), A3(cs3), A3(sn3), bc3(zr), bc3(zi), A3(ang3), A3(kf3),
         ["tAcos", "tAsin", "zr", "zi", "tAang", "tAkf"], "wz")
    cmul(S, "dve", A3(cs3), A3(sn3), A3(w_r), A3(w_i), bc3(bre), bc3(bim), A3(ang3), A3(kf3),
         ["wzre", "wzim", "pA", "tAcos", "tAsin"], "Bs")
    for q in range(2):
        for ri, src in ((0, cs3), (1, sn3)):
            for h in range(2):
                gl = 2 * q + h
                S.op("dve", lambda e, q=q, ri=ri, h=h, gl=gl, src=src: e.tensor_scalar(
                    out=W1tab[q][ri][:, :, 64 * h:64 * h + 64], in0=A3(src), scalar1=rowmask[:, gl:gl + 1], scalar2=None, op0=ALU.mult),
                    r=["Bsre", "Bsim", "rowmask"], w=[("W1tab", q, ri, h)])
    S.barrier()
    for q in range(2):
        lamB, thB, dtB, phi, th15, junk = [t[:] for t in cB]
        angk, kfk, hsk, snk, csk, mgk, nsk = [t[:] for t in tB]
        pq = [("pB", q)]
        tg = f"B{q}"
        S.op("act", lambda e, q=q: e.activation(out=dtB, in_=pB[q][:, 2:3], func=AF.Exp), r=pq, w=[tg + "dt"])
        S.op("dve", lambda e, q=q: e.tensor_tensor(out=lamB, in0=pB[q][:, 0:1], in1=dtB, op=ALU.mult), r=pq + [tg + "dt"], w=[tg + "lam"])
        S.op("dve", lambda e, q=q: e.tensor_tensor(out=thB, in0=pB[q][:, 1:2], in1=dtB, op=ALU.mult), r=pq + [tg + "dt"], w=[tg + "th"])
        S.op("dve", lambda e: e.tensor_scalar(out=angk, in0=kidx[:], scalar1=thB[:, 0:1], scalar2=None, op0=ALU.mult),
             r=["kidx", tg + "th"], w=[tg + "kang"])
        sincos(S, angk, kfk, tBi[:], hsk, snk, csk, tg + "k")
        S.op("dve", lambda e: e.tensor_scalar(out=mgk, in0=kidx[:], scalar1=lamB[:, 0:1], scalar2=None, op0=ALU.mult),
             r=["kidx", tg + "lam"], w=[tg + "mgk"])
        S.op("act", lambda e: e.activation(out=mgk, in_=mgk, func=AF.Exp), w=[tg + "mgk"])
        S.op("dve", lambda e: e.tensor_tensor(out=csk, in0=csk, in1=mgk, op=ALU.mult), r=[tg + "mgk"], w=[tg + "kcos"])
        S.op("dve", lambda e: e.tensor_tensor(out=snk, in0=snk, in1=mgk, op=ALU.mult), r=[tg + "mgk"], w=[tg + "ksin"])
        S.op("dve", lambda e: e.tensor_scalar(out=nsk, in0=snk, scalar1=-1.0, scalar2=None, op0=ALU.mult), r=[tg + "ksin"], w=[tg + "nsk"])
        S.op("dve", lambda e: e.tensor_scalar(out=kfk, in0=csk, scalar1=-1.0, scalar2=None, op0=ALU.mult), r=[tg + "kcos"], w=[tg + "kkf"])
        kb = lambda ap: ap.unsqueeze(2).broadcast_to([128, 32, 64])
        cb = lambda t: t[:].unsqueeze(1).broadcast_to([128, 32, 64])
        for ri, (f1, f2) in enumerate(((csk, nsk), (nsk, kfk))):
            S.op("dve", lambda e, q=q, f1=f1: e.tensor_tensor(out=gt[0][:], in0=cb(crep[q]), in1=kb(f1), op=ALU.mult),
                 r=[("crep", q), tg + "kcos", tg + "nsk", tg + "kkf"], w=["gt0"])
            S.op("dve", lambda e, q=q, f2=f2: e.tensor_tensor(out=gt[1][:], in0=cb(cimp[q]), in1=kb(f2), op=ALU.mult),
                 r=[("cimp", q), tg + "kcos", tg + "nsk", tg + "kkf"], w=["gt1"])
            S.op("dve", lambda e, q=q, ri=ri: e.tensor_tensor(out=Gpad[q][ri][:], in0=gt[0][:], in1=gt[1][:], op=ALU.add),
                 r=["gt0", "gt1"], w=[("Gpad", q, ri)])
        S.op("dve", lambda e: e.tensor_scalar(out=phi, in0=thB, scalar1=16.0, scalar2=None, op0=ALU.mult), r=[tg + "th"], w=[tg + "phi"])
        S.op("dve", lambda e: e.tensor_scalar(out=th15, in0=phi, scalar1=1.0 / (2.0 * math.pi), scalar2=None, op0=ALU.mult),
             r=[tg + "phi"], w=[tg + "th15"])
        S.op("dve", lambda e: e.tensor_copy(out=cBi[:], in_=th15), r=[tg + "th15"], w=[tg + "cBi"])
        S.op("dve", lambda e: e.tensor_copy(out=th15, in_=cBi[:]), r=[tg + "cBi"], w=[tg + "th15"])
        S.op("dve", lambda e: e.scalar_tensor_tensor(out=phi, in0=th15, scalar=-C1_2PI, in1=phi, op0=ALU.mult, op1=ALU.add),
             r=[tg + "th15"], w=[tg + "phi"])
        S.op("dve", lambda e: e.scalar_tensor_tensor(out=phi, in0=th15, scalar=-C2_2PI, in1=phi, op0=ALU.mult, op1=ALU.add),
             r=[tg + "th15"], w=[tg + "phi"])
        S.op("dve", lambda e: e.tensor_scalar(out=l2[0][:], in0=midx[:], scalar1=phi[:, 0:1], scalar2=None, op0=ALU.mult),
             r=["midx", tg + "phi"], w=["l2ang"])
        sincos(S, l2[0][:], l2[1][:], l2i[:], l2[2][:], Tsn[q][:], Tc[q][:], "l2")
        S.op("dve", lambda e, q=q: e.tensor_copy(out=Tsn[q][:], in_=Tsn[q][:]), r=["l2sin"], w=[("Ts", q)])
        S.op("dve", lambda e, q=q: e.tensor_copy(out=Tc[q][:], in_=Tc[q][:]), r=["l2cos"], w=[("Tc", q)])
        S.op("act", lambda e, q=q: e.activation(out=rho[q][:], in_=lamB, func=AF.Exp, scale=16.0), r=[tg + "lam"], w=[("rho", q)])
    suv = su[:].rearrange("p (m s) -> p s m", s=16)
    zi_ = 0
    for q in range(2):
        for ri in range(2):
            for s in range(16):
                pb = zi_ % 2
                zi_ += 1
                S.op("pe", lambda e, q=q, ri=ri, s=s, pb=pb: e.matmul(ps_z[pb][:, 0:NB], W1tab[q][ri][:, s, :], suv[:, s, :], start=True, stop=True),
                     r=["su", ("W1tab", q, ri, 0), ("W1tab", q, ri, 1)], w=[("ps_z", pb)])
                dst = W15[q][ri] if s == 15 else roll[s % 2]
                dn = ("W15", q, ri) if s == 15 else ("roll", s % 2)
                if s == 0:
                    S.op("dve", lambda e, pb=pb, dst=dst: e.tensor_copy(out=dst[:, 0:NB], in_=ps_z[pb][:, 0:NB]), w=[dn, ("ps_z", pb)])
                else:
                    S.op("dve", lambda e, pb=pb, dst=dst, s=s: e.tensor_tensor(out=dst[:, 0:NB], in0=ps_z[pb][:, 0:NB],
                                                                          in1=roll[(s - 1) % 2][:, 0:NB], op=ALU.add),
                         r=[("roll", (s - 1) % 2)], w=[dn, ("ps_z", pb)])
                S.op("act", lambda e, q=q, ri=ri, s=s, dst=dst: e.activation(out=W1bf[q][ri][:, s, 0:NB], in_=dst[:, 0:NB], func=AF.Copy),
                     r=[dn], w=[("W1bf", q, ri, s)])
    for q in range(2):
        ur, ui, t0, t1, vr, vi = [t[:, 0:NB] for t in l2]
        tc, tsn = Tc[q][:, 0:NB], Tsn[q][:, 0:NB]
        wre, wim = W15[q][0][:, 0:NB], W15[q][1][:, 0:NB]
        cmul(S, "dve", ur, ui, tc, tsn, wre, wim, t0, t1, [("Tc", q), ("Ts", q), ("W15", q, 0), ("W15", q, 1), "l2v"], "l2u", conj_a=True)
        rb = rho[q][:, 0:1].broadcast_to([128, NB])
        S.op("dve", lambda e, rb=rb: e.tensor_tensor_scan(out=vr, data0=rb, data1=ur, initial=0.0, op0=ALU.mult, op1=ALU.add),
             r=["l2ure", ("rho", q)], w=["l2vr"])
        S.op("dve", lambda e, rb=rb: e.tensor_tensor_scan(out=vi, data0=rb, data1=ui, initial=0.0, op0=ALU.mult, op1=ALU.add),
             r=["l2uim", ("rho", q)], w=["l2vi"])
        cmul(S, "dve", ur, ui, tc, tsn, vr, vi, t0, t1, [("Tc", q), ("Ts", q), "l2vr", "l2vi"], "l2x")
        for ri, src in ((0, ur), (1, ui)):
            S.op("pool", lambda e, q=q, ri=ri: e.memset(Xbf[q][ri][:, 0:1], 0.0), w=[("Xbf", q, ri)])
            if NB > 1:
                S.op("act", lambda e, q=q, ri=ri, src=src: e.activation(out=Xbf[q][ri][:, 1:NB], in_=src[:, 0:NB - 1], func=AF.Copy),
                     r=["l2xre", "l2xim"], w=[("Xbf", q, ri)])
        S.op("dve", lambda e: e.tensor_copy(out=l2[0][:, 0:1], in_=l2[0][:, 0:1]), r=[("Xbf", q, 0), ("Xbf", q, 1)], w=["l2v", "l2ure", "l2uim"])
    outv = outsb[:].rearrange("p (m s) -> p s m", s=16)
    for s in range(16):
        pb = s % 2
        k = 0
        for q in range(2):
            for ri in range(2):
                S.op("pe", lambda e, q=q, ri=ri, s=s, pb=pb, k=k: e.matmul(ps_y[pb][0:64, 0:NB], Gpad[q][ri][:, s, :], W1bf[q][ri][:, s, 0:NB],
                                                                     start=(k == 0), stop=False),
                     r=[("Gpad", q, ri), ("W1bf", q, ri, s)], w=[("ps_y", pb)])
                k += 1
        for q in range(2):
            for ri in range(2):
                S.op("pe", lambda e, q=q, ri=ri, s=s, pb=pb, k=k: e.matmul(ps_y[pb][0:64, 0:NB], Gpad[q][ri][:, s + 16, :], Xbf[q][ri][:, 0:NB],
                                                                     start=False, stop=(k == 7)),
                     r=[("Gpad", q, ri), ("Xbf", q, ri)], w=[("ps_y", pb)])
                k += 1
        S.op("dve", lambda e, s=s, pb=pb: e.scalar_tensor_tensor(out=ytmp[pb][:, 0:NB], in0=suv[:, s, :], scalar=dcol[:, 0:1],
                                                            in1=ps_y[pb][0:64, 0:NB], op0=ALU.mult, op1=ALU.add),
             r=["su", "dcol"], w=[("ytmp", pb), ("ps_y", pb)])
        S.op("act", lambda e, s=s, pb=pb: e.activation(out=outv[:, s, :], in_=ytmp[pb][:, 0:NB], func=AF.Gelu),
             r=[("ytmp", pb)], w=[("outsb", s)])
    S.op("sp", lambda e: e.dma_start(out=yg[:, :], in_=outsb[:]), r=[("outsb", s) for s in range(16)], dma=True)


def build_s5(T=SEQ):
    nc = bass.Bass("TRN2", target_bir_lowering=False)
    D = lambda n, s, d=F32, k="ExternalInput": nc.dram_tensor(n, s, d, kind=k).ap()
    fm = D("fm", [NFM, T], BF16)
    pA = D("s5p_pA", [64, 257]); pB = D("s5p_pB", [2, 128, 3]); cre = D("s5p_cre", [2, 128, 64]); cim = D("s5p_cim", [2, 128, 64])
    dcol = D("s5p_d", [64, 1]); negsig = D("negsig", [1, 1024]); kidx = D("kidx", [1, 32]); midx = D("midx", [1, 512])
    rowmask = D("rowmask", [64, 4])
    yg = D("yg", [64, T], BF16, "ExternalOutput")
    with contextlib.ExitStack() as st:
        S = Sched(nc)
        phase_s5(nc, S, st, fm, pA, pB, cre, cim, dcol, negsig, kidx, midx, rowmask, yg, T)
        S.emit()
    return nc


def phase_out(nc, S, st, mixin, ssg_d, hT, wout_d, gluw_d, glub_d, fnw_d, hout, final, NTOK=TQ):
    TS = lambda n, s, d: st.enter_context(nc.sbuf_tensor(n, s, d))
    PS = lambda n: st.enter_context(nc.psum_tensor(n, [128, 512], F32))
    wst = [TS(f"po_wst{i}", [128, 1024], F32) for i in range(2)]
    wout = TS("po_wout", [128, 8, 1024], BF16)
    gst = TS("po_gst", [128, 2, 256], F32)
    gluw = TS("po_gluw", [128, 2, 256], BF16)
    glub = TS("po_glub", [128, 2], F32)
    fnw = TS("po_fnw", [128, 8], F32)
    ones = TS("po_ones", [128, 128], BF16)
    mix = [TS(f"po_mix{i}", [128, 8, 512], BF16) for i in range(2)]
    ssg = [TS(f"po_ssg{i}", [128, 2, 512], BF16) for i in range(2)]
    hin = [TS(f"po_hin{i}", [128, 8, 512], F32) for i in range(2)]
    sg = TS("po_sg", [128, 512], F32)
    osb = TS("po_osb", [128, 2, 512], BF16)
    hn = TS("po_hn", [128, 8, 512], F32)
    hsq = TS("po_hsq", [128, 8, 512], BF16)
    nsq = TS("po_nsq", [128, 512], F32)
    ps_g = PS("po_ps_g")
    ps_o = [PS(f"po_ps_o{i}") for i in range(3)]
    ps_n = PS("po_ps_n")

    S.op("pool", lambda e: e.memset(ones[:], 1.0), w=["ones"])
    S.op("sp", lambda e: e.dma_start(out=gst[:], in_=gluw_d.rearrange("(k p) o -> p k o", p=128)), w=["gst"], dma=True)
    S.op("sp", lambda e: e.dma_start(out=glub[:], in_=glub_d[:, :]), w=["glub"], dma=True)
    S.op("sp", lambda e: e.dma_start(out=fnw[:], in_=fnw_d[:, :]), w=["fnw"], dma=True)
    S.op("dve", lambda e: e.tensor_copy(out=gluw[:], in_=gst[:]), r=["gst"], w=["gluw"])
    for k in range(8):
        S.op("sp", lambda e, k=k: e.dma_start(out=wst[k % 2][:], in_=wout_d[k * 128:(k + 1) * 128, :]), w=[("wst", k % 2)], dma=True)
        S.op("pool" if k % 2 else "dve", lambda e, k=k: e.tensor_copy(out=wout[:, k, :], in_=wst[k % 2][:]), r=[("wst", k % 2)], w=[("wout", k)])
    wr = [("wout", k) for k in range(8)]
    mv = mixin.rearrange("(k p) t -> p k t", p=128)
    sv = ssg_d.rearrange("(k p) t -> p k t", p=128)
    hv = hT.rearrange("(k p) t -> p k t", p=128)
    ov = hout.rearrange("(k p) t -> p k t", p=128)
    oi = 0
    for ti in range(NTOK // 512):
        b = ti % 2
        ts = slice(ti * 512, (ti + 1) * 512)
        S.op("sp", lambda e, b=b, ts=ts: e.dma_start(out=mix[b][:], in_=mv[:, :, ts]), w=[("mix", b)], dma=True)
        S.op("sp", lambda e, b=b, ts=ts: e.dma_start(out=ssg[b][:], in_=sv[:, :, ts]), w=[("ssg", b)], dma=True)
        S.op("pool", lambda e, b=b, ts=ts: e.dma_start(out=hin[b][:], in_=hv[:, :, ts]), w=[("hin", b)], dma=True)
        for oc in range(2):
            for kc in range(2):
                S.op("pe", lambda e, b=b, oc=oc, kc=kc: e.matmul(ps_g[:], gluw[:, kc, oc * 128:(oc + 1) * 128], mix[b][:, 2 + kc, :],
                                                             start=(kc == 0), stop=(kc == 1)), r=[("mix", b), "gluw"], w=["ps_g"])
            S.op("act", lambda e, oc=oc: e.activation(out=sg[:], in_=ps_g[:], func=AF.Sigmoid, bias=glub[:, oc:oc + 1]),
                 r=["glub"], w=["sg", "ps_g"])
            S.op("dve", lambda e, b=b, oc=oc: e.tensor_tensor(out=sg[:], in0=sg[:], in1=mix[b][:, 2 + oc, :], op=ALU.mult),
                 r=[("mix", b)], w=["sg"])
            S.op("dve", lambda e, b=b, oc=oc: e.tensor_tensor(out=osb[:, oc, :], in0=sg[:], in1=ssg[b][:, oc, :], op=ALU.mult),
                 r=[("ssg", b), "sg"], w=[("osb", oc)])
        for dc in range(8):
            pb = oi % 3
            oi += 1
            for kc in range(8):
                rhs = (lambda b=b, kc=kc: osb[:, kc - 2, :]) if kc in (2, 3) else (lambda b=b, kc=kc: mix[b][:, kc, :])
                S.op("pe", lambda e, dc=dc, kc=kc, pb=pb, rhs=rhs: e.matmul(ps_o[pb][:], wout[:, kc, dc * 128:(dc + 1) * 128], rhs(),
                                                                      start=(kc == 0), stop=(kc == 7)),
                     r=wr + [("mix", b), ("osb", 0), ("osb", 1)], w=[("ps_o", pb)])
            S.op("dve", lambda e, b=b, dc=dc, pb=pb: e.tensor_tensor(out=hn[:, dc, :], in0=ps_o[pb][:], in1=hin[b][:, dc, :], op=ALU.add),
                 r=[("hin", b)], w=[("hn", dc), ("ps_o", pb)])
            if not final:
                S.op("sp", lambda e, dc=dc, ts=ts: e.dma_start(out=ov[:, dc, ts], in_=hn[:, dc, :]), r=[("hn", dc)], dma=True)
        if final:
            hr = [("hn", dc) for dc in range(8)]
            S.op("act", lambda e: e.activation(out=hsq[:], in_=hn[:], func=AF.Square), r=hr, w=["hsq"])
            for k in range(8):
                S.op("pe", lambda e, k=k: e.matmul(ps_n[:], ones[:], hsq[:, k, :], start=(k == 0), stop=(k == 7)), r=["hsq", "ones"], w=["ps_n"])
            S.op("act", lambda e: e.activation(out=nsq[:], in_=ps_n[:], func=AF.Sqrt, scale=1.0 / D_MODEL, bias=EPS), w=["nsq", "ps_n"])
            S.op("dve", lambda e: e.reciprocal(out=nsq[:], in_=nsq[:]), w=["nsq"])
            for dc in range(8):
                S.op("pool" if dc % 2 else "dve", lambda e, dc=dc: e.scalar_tensor_tensor(
                    out=hn[:, dc, :], in0=hn[:, dc, :], scalar=fnw[:, dc:dc + 1], in1=nsq[:], op0=ALU.mult, op1=ALU.mult) if dc % 2 == 0 else
                    e.tensor_tensor(out=hn[:, dc, :], in0=hn[:, dc, :], in1=nsq[:], op=ALU.mult),
                    r=["nsq", "fnw"], w=[("hn", dc)])
                if dc % 2:
                    S.op("pool", lambda e, dc=dc: e.tensor_scalar(out=hn[:, dc, :], in0=hn[:, dc, :], scalar1=fnw[:, dc:dc + 1], scalar2=None,
                                                                  op0=ALU.mult), r=["fnw"], w=[("hn", dc)])
                S.op("sp", lambda e, dc=dc, ts=ts: e.dma_start(out=ov[:, dc, ts], in_=hn[:, dc, :]), r=[("hn", dc)], dma=True)


def build_out(final, NTOK=TQ):
    nc = bass.Bass("TRN2", target_bir_lowering=False)
    D = lambda n, s, d=F32, k="ExternalInput": nc.dram_tensor(n, s, d, kind=k).ap()
    mixin = D("mixin", [1024, NTOK], BF16)
    ssg = D("ssg", [256, NTOK], BF16)
    hT = D("hT", [D_MODEL, NTOK])
    wout = D("wout", [1024, 1024]); gluw = D("gluw", [256, 256]); glub = D("glub", [128, 2]); fnw = D("fnw", [128, 8])
    hout = D("hout", [D_MODEL, NTOK], F32, "ExternalOutput")
    with contextlib.ExitStack() as st:
        S = Sched(nc)
        phase_out(nc, S, st, mixin, ssg, hT, wout, gluw, glub, fnw, hout, final, NTOK)
        S.emit()
    return nc


_CACHE = {}


def _prog(key, fn):
    if key not in _CACHE:
        _CACHE[key] = fn()
    return _CACHE[key]


def build_mixers(l, T=SEQ, which=("ip", "at", "hg", "s5")):
    lambda_init = 0.8 - 0.6 * math.exp(-0.3 * l)
    nc = bass.Bass("TRN2", target_bir_lowering=False)
    D = lambda n, s, d=F32, k="ExternalInput": nc.dram_tensor(n, s, d, kind=k).ap()
    hT = D("hT", [D_MODEL, T]); wcat = D("wcat", [D_MODEL, NFM + NTM]); nw = D("nw", [128, 8])
    lqk = D("lqk", [1, 256]); subln = D("subln", [128, 1]); ropef = D("ropef", [128, 1]); rmat = D("rmat", [128, 128])
    cmask = D("cmask", [4, 128, 512], BF16)
    lbl_bc = D("lbl_bc", [1, 128]); lbl_col = D("lbl_col", [64, 2]); gw = D("gw", [64, 1]); mcat = D("mcat", [128, 136]); mrev = D("mrev", [128, 128])
    pA = D("s5p_pA", [64, 257]); pB = D("s5p_pB", [2, 128, 3]); cre = D("s5p_cre", [2, 128, 64]); cim = D("s5p_cim", [2, 128, 64])
    dcol = D("s5p_d", [64, 1]); negsig = D("negsig", [1, 1024]); kidx = D("kidx", [1, 32]); midx = D("midx", [1, 512]); rowmask = D("rowmask", [64, 4])
    fm = D("fm", [NFM, T], BF16, "Internal")
    tm_sf = D("tm_sf", [T, 64], F32, "Internal")
    tm_v = D("tm_v", [T, 192], BF16, "Internal")
    mo = D("mo", [320, T], BF16, "ExternalOutput")
    if "ip" in which:
        with contextlib.ExitStack() as st:
            S = Sched(nc)
            phase_inproj(nc, S, st, hT, wcat, nw, fm, tm_sf, tm_v, T)
            S.op("sp", lambda e: e.dma_start(out=mo[128:192, :], in_=fm[192:256, :]), r=[], dma=True)
            S.emit()
    if "at" in which:
        with contextlib.ExitStack() as st:
            S = Sched(nc)
            phase_attn(nc, S, st, fm, tm_v, lqk, subln, ropef, rmat, cmask, mo[192:320, :], lambda_init, T)
            S.emit()
    if "hg" in which:
        with contextlib.ExitStack() as st:
            S = Sched(nc)
            phase_hgrn(nc, S, st, fm, tm_sf, tm_v, lbl_bc, lbl_col, gw, mcat, mrev, mo[0:64, :], float(l), T)
            S.emit()
    if "s5" in which:
        with contextlib.ExitStack() as st:
            S = Sched(nc)
            phase_s5(nc, S, st, fm, pA, pB, cre, cim, dcol, negsig, kidx, midx, rowmask, mo[64:128, :], T)
            S.emit()
    return nc


def mixer_inputs(inp, l, c, hT_b):
    f = np.float32
    j = c % 4
    ropef, rmat, cmask = attn_consts()
    mcat, mrev = hgrn_consts()
    negsig, kidx, midx, rowmask = s5_consts()
    lbl = np.asarray(inp["hgrn_lb_logits"], f)[:, 64 * j:64 * j + 64]
    d = {"hT": hT_b, "wcat": np.ascontiguousarray(np.asarray(inp["w_in"][l], f)[:, core_cols(j)]),
         "nw": np.ascontiguousarray(np.asarray(inp["norm_w"][l], f).reshape(8, 128).T),
         "lqk": np.concatenate([inp["diff_lq1"][l], inp["diff_lq2"][l], inp["diff_lk1"][l], inp["diff_lk2"][l]])[None, :].astype(f),
         "subln": np.asarray(inp["diff_subln_w"][l], f)[:, None], "ropef": ropef, "rmat": rmat, "cmask": cmask,
         "lbl_bc": np.ascontiguousarray(lbl.reshape(1, 128)), "lbl_col": np.ascontiguousarray(lbl.T),
         "gw": np.asarray(inp["hgrn_norm_w"][l], f)[:, None], "mcat": mcat, "mrev": mrev,
         "negsig": negsig, "kidx": kidx, "midx": midx, "rowmask": rowmask}
    d.update(s5_params(inp, l, j))
    return d


def kernel(**inp):
    f = np.float32
    x = np.asarray(inp["x"], f)
    cores = list(range(NCORES))
    hT = [np.ascontiguousarray(x[b].T) for b in range(BATCH)]
    for l in range(DEPTH):
        nc = _prog(("mix", l), lambda: build_mixers(l))
        ims = [mixer_inputs(inp, l, c, hT[c // 4]) for c in cores]
        rm = run_bass_kernel_spmd(nc, ims, core_ids=cores).results
        final = (l == DEPTH - 1)
        nc = _prog(("out", final), lambda: build_out(final))
        ims = []
        for c in cores:
            b, tq = c // 4, c % 4
            ts = slice(tq * TQ, (tq + 1) * TQ)
            src = [4 * b + j for j in range(4)]
            mixin = np.concatenate([rm[s]["mo"][0:64, ts] for s in src] + [rm[s]["mo"][64:128, ts] for s in src]
                                   + [rm[s]["mo"][192:320, ts] for s in src])
            ssg = np.concatenate([rm[s]["mo"][128:192, ts] for s in src])
            ims.append({"mixin": np.ascontiguousarray(mixin), "ssg": np.ascontiguousarray(ssg), "hT": np.ascontiguousarray(hT[b][:, ts]),
                        "wout": np.asarray(inp["w_out"][l], f), "gluw": np.asarray(inp["s5_glu_w"][l], f),
                        "glub": np.ascontiguousarray(np.asarray(inp["s5_glu_b"][l], f).reshape(2, 128).T),
                        "fnw": np.ascontiguousarray(np.asarray(inp["final_norm_w"], f).reshape(8, 128).T)})
        ro = run_bass_kernel_spmd(nc, ims, core_ids=cores).results
        hT = [np.concatenate([ro[4 * b + tq]["hout"] for tq in range(4)], axis=1) for b in range(BATCH)]
    out = np.stack([hT[b].T for b in range(BATCH)]).astype(f)
    return np.ascontiguousarray(out)
```

```python
import contextlib
import math
import numpy as np
import ml_dtypes
import concourse.bass as bass
import concourse.mybir as mybir
from concourse.bass_utils import run_bass_kernel_spmd

F32 = mybir.dt.float32
BF16 = mybir.dt.bfloat16
I32 = mybir.dt.int32
AF = mybir.ActivationFunctionType
ALU = mybir.AluOpType
AX = mybir.AxisListType

D_MODEL = 1024
SEQ = 8192
BATCH = 2
DEPTH = 2
EPS = 1e-6
NCORES = 8
TQ = SEQ // 4
ROPE_THETA = 500000.0
import os
DBG = set(os.environ.get("KDBG", "").split(","))


class Sched:
    ENGS = ["pe", "act", "dve", "pool", "sp"]

    def __init__(self, nc):
        self.nc = nc
        self.ops = []
        self.last_w = {}
        self.readers = {}
        self.cnt = {e: 0 for e in self.ENGS}
        self.dma_cnt = {}
        self.base = set()

    def op(self, eng, fn, r=(), w=(), dma=False):
        deps = set(self.base)
        for x in r:
            if x in self.last_w:
                deps.add(self.last_w[x])
        for x in w:
            if x in self.last_w:
                deps.add(self.last_w[x])
            for d in self.readers.get(x, ()):
                deps.add(d)
        if dma:
            q = self.dma_cnt.get(eng, 0)
            self.dma_cnt[eng] = q + 1
            tok = ("dma", eng, q)
        else:
            self.cnt[eng] += 1
            tok = ("eng", eng, self.cnt[eng])
        self.ops.append((eng, fn, deps, tok))
        for x in w:
            self.last_w[x] = tok
            self.readers[x] = []
        for x in r:
            self.readers.setdefault(x, []).append(tok)
        return tok

    def barrier(self):
        b = set()
        for e in self.ENGS:
            if self.cnt[e] > 0:
                b.add(("eng", e, self.cnt[e]))
        for e, n in self.dma_cnt.items():
            for q in range(max(0, n - self.NSLOT), n):
                b.add(("dma", e, q))
        self.base = b
        self.last_w = {}
        self.readers = {}

    NSLOT = 8

    def emit(self):
        nc = self.nc
        NSLOT = self.NSLOT
        needed = set()
        for (eng, fn, deps, tok) in self.ops:
            for d in deps:
                if d[0] == "eng" and not (d[1] == "pe" and eng == "pe"):
                    needed.add(d)
        sig = {}
        run = {e: 0 for e in self.ENGS}
        for (eng, fn, deps, tok) in self.ops:
            if tok[0] == "eng":
                if tok in needed:
                    run[eng] += 1
                sig[tok] = run[eng]
        with contextlib.ExitStack() as st:
            esem = {e: st.enter_context(nc.semaphore("s_" + e)) for e in self.ENGS}
            dsem = {}
            for e in self.dma_cnt:
                dsem[e] = [st.enter_context(nc.semaphore(f"d_{e}_{i}")) for i in range(NSLOT)]
            block = st.enter_context(nc.Block())
            per = {e: [o for o in self.ops if o[0] == e] for e in self.ENGS}

            def mk(ename):
                def body(eng):
                    seen = {}

                    def wait(tok):
                        if tok[0] == "eng":
                            _, e2, n = tok
                            if e2 == "pe" and ename == "pe":
                                return
                            v = sig[tok]
                            key = ("eng", e2)
                            if seen.get(key, 0) >= v:
                                return
                            seen[key] = v
                            eng.wait_ge(esem[e2], v)
                        else:
                            _, e2, q = tok
                            slot = q % NSLOT
                            val = 16 * (q // NSLOT + 1)
                            key = ("dma", e2, slot)
                            if seen.get(key, 0) >= val:
                                return
                            seen[key] = val
                            eng.wait_ge(dsem[e2][slot], val)
                    for (_, fn, deps, tok) in per[ename]:
                        for d in sorted(deps):
                            wait(d)
                        if tok[0] == "dma":
                            q = tok[2]
                            if q >= NSLOT:
                                wait(("dma", ename, q - NSLOT))
                            ins = fn(eng)
                            ins.then_inc(dsem[ename][q % NSLOT], 16)
                        else:
                            ins = fn(eng)
                            if tok in needed:
                                ins.then_inc(esem[ename], 1)
                    n = self.dma_cnt.get(ename, 0)
                    for q in range(max(0, n - NSLOT), n):
                        wait(("dma", ename, q))
                return body
            block.tensor(mk("pe"))
            block.scalar(mk("act"))
            block.vector(mk("dve"))
            block.gpsimd(mk("pool"))
            block.sync(mk("sp"))


NFM = 704
NTM = 256
FM_CH = [(0, 128), (128, 128), (256, 64), (320, 128), (448, 128), (576, 128)]


def phase_inproj(nc, S, st, hT, wcat, nw, fm, tm_sf, tm_v, T=SEQ):
    TS = lambda n, s, d: st.enter_context(nc.sbuf_tensor(n, s, d))
    PS = lambda n: st.enter_context(nc.psum_tensor(n, [128, 512], F32))
    nw_sb = TS("ip_nw", [128, 8], F32)
    wst = [TS(f"ip_wst{i}", [128, NFM + NTM], F32) for i in range(2)]
    wall = TS("ip_wall", [128, 8, NFM + NTM], BF16)
    ones = TS("ip_ones", [128, 128], BF16)
    xin = [TS(f"ip_xin{i}", [128, 8, 512], F32) for i in range(2)]
    xsq = TS("ip_xsq", [128, 8, 512], BF16)
    sq = TS("ip_sq", [128, 512], F32)
    rstd = TS("ip_rstd", [128, 512], F32)
    xn = [TS(f"ip_xn{i}", [128, 8, 512], BF16) for i in range(2)]
    fmo = [TS(f"ip_fmo{i}", [128, 512], BF16) for i in range(6)]
    tsf = [TS(f"ip_tsf{i}", [128, 4, 64], F32) for i in range(2)]
    tv = [TS(f"ip_tv{i}", [128, 4, 192], BF16) for i in range(2)]
    ps_ss = PS("ip_ps_ss")
    ps_fm = [PS(f"ip_ps_fm{i}") for i in range(4)]
    ps_tm = [PS(f"ip_ps_tm{i}") for i in range(2)]

    S.op("sp", lambda e: e.dma_start(out=nw_sb[:], in_=nw[:, :]), w=["nw"], dma=True)
    S.op("pool", lambda e: e.memset(ones[:], 1.0), w=["ones"])
    for k in range(8):
        S.op("sp", lambda e, k=k: e.dma_start(out=wst[k % 2][:], in_=wcat[k * 128:(k + 1) * 128, :]),
             w=[("wst", k % 2)], dma=True)
        S.op("dve", lambda e, k=k: e.tensor_scalar(out=wall[:, k, :], in0=wst[k % 2][:], scalar1=nw_sb[:, k:k + 1],
                                                  scalar2=None, op0=ALU.mult),
             r=[("wst", k % 2), "nw"], w=[("wall", k)])
    wall_r = [("wall", k) for k in range(8)]
    hT_v = hT.rearrange("(k p) t -> p k t", p=128)
    fmi = 0
    NTI = T // 512

    def load(ti):
        b = ti % 2
        t0 = ti * 512
        S.op("pool", lambda e, b=b, t0=t0: e.dma_start(out=xin[b][:, 0:4, :], in_=hT_v[:, 0:4, t0:t0 + 512]),
             w=[("xin", b, 0)], dma=True)
        S.op("pool", lambda e, b=b, t0=t0: e.dma_start(out=xin[b][:, 4:8, :], in_=hT_v[:, 4:8, t0:t0 + 512]),
             w=[("xin", b, 1)], dma=True)
    def front_sq(ti):
        b = ti % 2
        xr = [("xin", b, 0), ("xin", b, 1)]
        S.op("act", lambda e, b=b: e.activation(out=xsq[:], in_=xin[b][:], func=AF.Square), r=xr, w=["xsq"])

    def front_ss(ti):
        b = ti % 2
        for k in range(8):
            S.op("pe", lambda e, k=k: e.matmul(ps_ss[:], ones[:], xsq[:, k, :], start=(k == 0), stop=(k == 7)),
                 r=["xsq", "ones"], w=["ps_ss"])
        S.op("act", lambda e: e.activation(out=sq[:], in_=ps_ss[:], func=AF.Sqrt, scale=1.0 / D_MODEL, bias=EPS),
             w=["ps_ss", "sq"])
        S.op("dve", lambda e: e.reciprocal(out=rstd[:], in_=sq[:]), r=["sq"], w=["rstd"])
        for hh in range(2):
            S.op("dve", lambda e, b=b, hh=hh: e.tensor_tensor(out=xn[b][:, 4 * hh:4 * hh + 4, :], in0=xin[b][:, 4 * hh:4 * hh + 4, :],
                                                         in1=rstd[:].unsqueeze(1).broadcast_to([128, 4, 512]), op=ALU.mult),
                 r=[("xin", b, hh), "rstd"], w=[("xn", b, hh)])
    load(0)
    if NTI > 1:
        load(1)
    front_sq(0)
    front_ss(0)
    for ti in range(NTI):
        b = ti % 2
        t0 = ti * 512
        if ti + 1 < NTI:
            front_sq(ti + 1)
        xnr = [("xn", b, 0), ("xn", b, 1)]
        for ci, (c0, cw) in enumerate(FM_CH):
            if "NOFM" in DBG or ("FM%d" % ci) in DBG:
                continue
            if ci == 3:
                if ti + 1 < NTI:
                    front_ss(ti + 1)
                if ti + 2 < NTI:
                    load(ti + 2)
            pb = fmi % 4
            for k in range(8):
                S.op("pe", lambda e, k=k, c0=c0, cw=cw, pb=pb, b=b: e.matmul(
                    ps_fm[pb][0:cw, :], wall[:, k, c0:c0 + cw], xn[b][:, k, :], start=(k == 0), stop=(k == 7)),
                    r=xnr + wall_r, w=[("ps_fm", pb)])
            fb = fmi % 6
            fmi += 1
            if ci == 0:
                S.op("act", lambda e, pb=pb, fb=fb: e.activation(out=fmo[fb][0:64, :], in_=ps_fm[pb][0:64, :], func=AF.Silu),
                     r=[("ps_fm", pb)], w=[("fmo", fb, 0)])
                S.op("act", lambda e, pb=pb, fb=fb: e.activation(out=fmo[fb][64:128, :], in_=ps_fm[pb][64:128, :],
                                                               func=AF.Sigmoid, scale=-1.0),
                     r=[("ps_fm", pb)], w=[("fmo", fb, 1)])
                wl = [("fmo", fb, 0), ("fmo", fb, 1)]
            elif ci in (1, 5):
                S.op("act", lambda e, pb=pb, fb=fb: e.activation(out=fmo[fb][:], in_=ps_fm[pb][:], func=AF.Silu),
                     r=[("ps_fm", pb)], w=[("fmo", fb, 0), ("fmo", fb, 1)])
                wl = [("fmo", fb, 0), ("fmo", fb, 1)]
            else:
                S.op("dve", lambda e, pb=pb, fb=fb, cw=cw: e.tensor_copy(out=fmo[fb][0:cw, :], in_=ps_fm[pb][0:cw, :]),
                     r=[("ps_fm", pb)], w=[("fmo", fb, 0), ("fmo", fb, 1)])
                wl = [("fmo", fb, 0), ("fmo", fb, 1)]
            S.op("sp", lambda e, fb=fb, c0=c0, cw=cw, t0=t0: e.dma_start(out=fm[c0:c0 + cw, t0:t0 + 512], in_=fmo[fb][0:cw, :]),
                 r=wl, dma=True)
        for pb in range(0 if "NOTM" in DBG else 2):
            for t4 in (2 * pb, 2 * pb + 1):
                off = (t4 % 2) * 256
                for k in range(8):
                    S.op("pe", lambda e, k=k, t4=t4, pb=pb, off=off, b=b: e.matmul(
                        ps_tm[pb][:, off:off + 256], xn[b][:, k, t4 * 128:(t4 + 1) * 128], wall[:, k, NFM:NFM + NTM],
                        start=(k == 0), stop=(k == 7)),
                        r=xnr + wall_r, w=[("ps_tm", pb)])
            if "TMNOEVAC" in DBG:
                continue
            for t4 in (2 * pb, 2 * pb + 1):
                off = (t4 % 2) * 256
                if "TMNOACT" not in DBG:
                  S.op("act", lambda e, pb=pb, b=b, t4=t4, off=off: e.activation(
                    out=tsf[b][:, t4, :], in_=ps_tm[pb][:, off:off + 64], func=AF.Sigmoid),
                    w=[("ps_tm", pb), ("tsf", b, t4)])
                if "TMNODVE" not in DBG:
                  S.op("dve", lambda e, pb=pb, b=b, t4=t4, off=off: e.tensor_copy(
                    out=tv[b][:, t4, :], in_=ps_tm[pb][:, off + 64:off + 256]),
                    w=[("ps_tm", pb), ("tv", b, t4)])
        if "TMNODMA" in DBG:
            continue
        S.op("sp", lambda e, b=b, t0=t0: e.dma_start(
            out=tm_sf[t0:t0 + 512, :].rearrange("(a p) c -> p a c", p=128), in_=tsf[b][:]),
            r=[("tsf", b, t4) for t4 in range(4)], dma=True)
        S.op("sp", lambda e, b=b, t0=t0: e.dma_start(
            out=tm_v[t0:t0 + 512, :].rearrange("(a p) c -> p a c", p=128), in_=tv[b][:]),
            r=[("tv", b, t4) for t4 in range(4)], dma=True)


def core_cols(j):
    r = lambda s, n: list(range(s, s + n))
    fmc = (r(0 + 64 * j, 64) + r(256 + 64 * j, 64) + r(768 + 64 * j, 64) + r(1280 + 64 * j, 64) + r(1024 + 64 * j, 64)
           + r(1536 + 128 * j, 128) + r(2048 + 128 * j, 128) + r(3072 + 128 * j, 128))
    tmc = r(256 + 64 * j, 64) + r(512 + 64 * j, 64) + r(2560 + 128 * j, 128)
    return np.array(fmc + tmc)


def build_inproj(T=SEQ):
    nc = bass.Bass("TRN2", target_bir_lowering=False)
    hT = nc.dram_tensor("hT", [D_MODEL, T], F32, kind="ExternalInput").ap()
    wcat = nc.dram_tensor("wcat", [D_MODEL, NFM + NTM], F32, kind="ExternalInput").ap()
    nw = nc.dram_tensor("nw", [128, 8], F32, kind="ExternalInput").ap()
    fm = nc.dram_tensor("fm", [NFM, T], BF16, kind="ExternalOutput").ap()
    tm_sf = nc.dram_tensor("tm_sf", [T, 64], F32, kind="ExternalOutput").ap()
    tm_v = nc.dram_tensor("tm_v", [T, 192], BF16, kind="ExternalOutput").ap()
    with contextlib.ExitStack() as st:
        S = Sched(nc)
        phase_inproj(nc, S, st, hT, wcat, nw, fm, tm_sf, tm_v, T)
        S.emit()
    return nc


C1_2PI = 6.28125
C2_2PI = 2.0 * math.pi - 6.28125


def sincos(S, ang, kf, ki, hs, sin_out, cos_out, tag, eng="dve"):
    a, k, h = tag + "ang", tag + "kf", tag + "hs"
    S.op(eng, lambda e: e.tensor_scalar(out=kf, in0=ang, scalar1=1.0 / (2.0 * math.pi), scalar2=None, op0=ALU.mult), r=[a], w=[k])
    S.op(eng, lambda e: e.tensor_copy(out=ki, in_=kf), r=[k], w=[tag + "ki"])
    S.op(eng, lambda e: e.tensor_copy(out=kf, in_=ki), r=[tag + "ki"], w=[k])
    S.op("dve", lambda e: e.scalar_tensor_tensor(out=ang, in0=kf, scalar=-C1_2PI, in1=ang, op0=ALU.mult, op1=ALU.add), r=[k], w=[a])
    S.op("dve", lambda e: e.scalar_tensor_tensor(out=ang, in0=kf, scalar=-C2_2PI, in1=ang, op0=ALU.mult, op1=ALU.add), r=[k], w=[a])
    S.op(eng, lambda e: e.tensor_scalar(out=ang, in0=ang, scalar1=math.pi, scalar2=-math.pi, op0=ALU.min, op1=ALU.max), w=[a])
    S.op("act", lambda e: e.activation(out=sin_out, in_=ang, func=AF.Sin), r=[a], w=[tag + "sin"])
    S.op("act", lambda e: e.activation(out=hs, in_=ang, func=AF.Sin, scale=0.5), r=[a], w=[h])
    S.op(eng, lambda e: e.tensor_tensor(out=hs, in0=hs, in1=hs, op=ALU.mult), w=[h])
    S.op(eng, lambda e: e.tensor_scalar(out=cos_out, in0=hs, scalar1=-2.0, scalar2=1.0, op0=ALU.mult, op1=ALU.add), r=[h], w=[tag + "cos"])


def rope_tables(nc, S, st, ropef, sinT, cosT, T, tag):
    TS = lambda n, s, d: st.enter_context(nc.sbuf_tensor(n, s, d))
    CH = min(512, T)
    NCH = T // CH
    pi_ = TS(tag + "_pi", [128, CH], I32)
    ang = TS(tag + "_ang", [128, CH], F32)
    kf = TS(tag + "_kf", [128, CH], F32)
    ki = TS(tag + "_ki", [128, CH], I32)
    hs = TS(tag + "_hs", [128, CH], F32)
    s1 = TS(tag + "_s1", [128, CH], F32)
    c1 = TS(tag + "_c1", [128, CH], F32)
    pj = TS(tag + "_pj", [128, NCH], I32)
    ang_b = TS(tag + "_angb", [128, NCH], F32)
    kf_b = TS(tag + "_kfb", [128, NCH], F32)
    ki_b = TS(tag + "_kib", [128, NCH], I32)
    hs_b = TS(tag + "_hsb", [128, NCH], F32)
    s2 = TS(tag + "_s2", [128, NCH], F32)
    c2 = TS(tag + "_c2", [128, NCH], F32)
    ns2 = TS(tag + "_ns2", [128, NCH], F32)
    tmp = [TS(tag + f"_tmp{i}", [128, CH], F32) for i in range(4)]
    S.op("pool", lambda e: e.iota(pi_[:], pattern=[[1, CH]], base=0, channel_multiplier=0), w=[tag + "pi"])
    S.op("pool", lambda e: e.iota(pj[:], pattern=[[CH, NCH]], base=0, channel_multiplier=0), w=[tag + "pj"])
    S.op("dve", lambda e: e.tensor_copy(out=ang[:], in_=pi_[:]), r=[tag + "pi"], w=[tag + "aang"])
    S.op("dve", lambda e: e.tensor_scalar(out=ang[:], in0=ang[:], scalar1=ropef[:, 0:1], scalar2=None, op0=ALU.mult), r=["ropef"], w=[tag + "aang"])
    sincos(S, ang[:], kf[:], ki[:], hs[:], s1[:], c1[:], tag + "a")
    S.op("dve", lambda e: e.tensor_copy(out=ang_b[:], in_=pj[:]), r=[tag + "pj"], w=[tag + "bang"])
    S.op("dve", lambda e: e.tensor_scalar(out=ang_b[:], in0=ang_b[:], scalar1=ropef[:, 0:1], scalar2=None, op0=ALU.mult), r=["ropef"], w=[tag + "bang"])
    sincos(S, ang_b[:], kf_b[:], ki_b[:], hs_b[:], s2[:], c2[:], tag + "b")
    S.op("dve", lambda e: e.tensor_scalar(out=ns2[:], in0=s2[:], scalar1=-1.0, scalar2=None, op0=ALU.mult), r=[tag + "bsin"], w=[tag + "ns2"])
    rd = [tag + "asin", tag + "acos", tag + "bsin", tag + "bcos", tag + "ns2"]
    for c in range(NCH):
        sl = slice(c * CH, (c + 1) * CH)
        ta, tb = tmp[(2 * c) % 4], tmp[(2 * c + 1) % 4]
        na, nb = (tag + "tmp", (2 * c) % 4), (tag + "tmp", (2 * c + 1) % 4)
        S.op("dve", lambda e, c=c, ta=ta: e.tensor_scalar(out=ta[:], in0=c1[:], scalar1=s2[:, c:c + 1], scalar2=None, op0=ALU.mult), r=rd, w=[na])
        S.op("dve", lambda e, c=c, ta=ta, sl=sl: e.scalar_tensor_tensor(out=sinT[:, sl], in0=s1[:], scalar=c2[:, c:c + 1], in1=ta[:],
                                                                        op0=ALU.mult, op1=ALU.add), r=rd + [na], w=[(tag + "sin", c)])
        S.op("dve", lambda e, c=c, tb=tb: e.tensor_scalar(out=tb[:], in0=s1[:], scalar1=ns2[:, c:c + 1], scalar2=None, op0=ALU.mult), r=rd, w=[nb])
        S.op("dve", lambda e, c=c, tb=tb, sl=sl: e.scalar_tensor_tensor(out=cosT[:, sl], in0=c1[:], scalar=c2[:, c:c + 1], in1=tb[:],
                                                                        op0=ALU.mult, op1=ALU.add), r=rd + [nb], w=[(tag + "cos", c)])
    return [(tag + "sin", c) for c in range(NCH)] + [(tag + "cos", c) for c in range(NCH)]


def phase_attn(nc, S, st, fm, tm_v, lqk, subln, ropef_d, rmat_d, cmask_d, oa, lambda_init, T=SEQ):
    TS = lambda n, s, d: st.enter_context(nc.sbuf_tensor(n, s, d))
    PS = lambda n: st.enter_context(nc.psum_tensor(n, [128, 512], F32))
    NQ = T // 512
    NK = T // 128
    ropef = TS("at_ropef", [128, 1], F32)
    rm32 = TS("at_rm32", [128, 128], F32)
    rm = TS("at_rm", [128, 128], BF16)
    cmask = TS("at_cmask", [128, 4, 512], BF16)
    ones = TS("at_ones", [128, 128], BF16)
    sinT = TS("at_sin", [128, T], BF16)
    cosT = TS("at_cos", [128, T], BF16)
    qraw = TS("at_qraw", [128, T], BF16)
    kraw = TS("at_kraw", [128, T], BF16)
    qr = TS("at_qr", [128, T], BF16)
    kr = TS("at_kr", [128, T], BF16)
    sag = TS("at_sag", [128, T], BF16)
    vsb = TS("at_v", [128, NK, 128], BF16)
    lq = TS("at_lq", [128, 256], F32)
    lp = TS("at_lp", [128, 128], F32)
    le = TS("at_le", [128, 2], F32)
    neglam = TS("at_neglam", [128, 1], F32)
    sw = TS("at_sw", [128, 1], F32)
    t1 = [TS(f"at_t1_{i}", [128, 512], BF16) for i in range(2)]
    t2 = [TS(f"at_t2_{i}", [128, 512], BF16) for i in range(2)]
    P = [[TS(f"at_P{i}_{m}", [128, 512], BF16) for m in range(2)] for i in range(3)]
    r0 = [TS(f"at_r0_{i}", [128, 512], F32) for i in range(2)]
    r1 = [TS(f"at_r1_{i}", [128, 512], F32) for i in range(2)]
    o0 = [TS(f"at_o0_{i}", [128, 512], F32) for i in range(2)]
    o1 = [TS(f"at_o1_{i}", [128, 512], F32) for i in range(2)]
    osq = TS("at_osq", [128, 512], BF16)
    nsq = TS("at_nsq", [128, 512], F32)
    ob = [TS(f"at_ob{i}", [128, 512], BF16) for i in range(2)]
    epsc = TS("at_epsc", [128, 1], F32)
    accD = [TS(f"at_accD{i}", [128, 512], F32) for i in range(2)]
    accP = [TS(f"at_accP{i}", [128, 512], F32) for i in range(2)]
    ones32 = TS("at_ones32", [128, 128], F32)
    ps_s = [[PS(f"at_ps_s{i}_{m}") for m in range(2)] for i in range(2)]
    ps_o = [PS(f"at_ps_o{m}") for m in range(2)]
    ps_l = [PS(f"at_ps_l{m}") for m in range(2)]
    ps_n = ps_s[0][0]

    S.op("sp", lambda e: e.dma_start(out=ropef[:], in_=ropef_d[:, :]), w=["ropef"], dma=True)
    S.op("sp", lambda e: e.dma_start(out=rm32[:], in_=rmat_d[:, :]), w=["rm32"], dma=True)
    S.op("sp", lambda e: e.dma_start(out=cmask[:], in_=cmask_d.rearrange("d p q -> p d q")), w=["cmask"], dma=True)
    S.op("sp", lambda e: e.dma_start(out=lq[:], in_=lqk.partition_broadcast(128)), w=["lq"], dma=True)
    S.op("sp", lambda e: e.dma_start(out=sw[:], in_=subln[:, :]), w=["sw"], dma=True)
    S.op("sp", lambda e: e.dma_start(out=qraw[:], in_=fm[320:448, :]), w=["qraw"], dma=True)
    S.op("sp", lambda e: e.dma_start(out=kraw[:], in_=fm[448:576, :]), w=["kraw"], dma=True)
    S.op("sp", lambda e: e.dma_start(out=sag[:], in_=fm[576:704, :]), w=["sag"], dma=True)
    S.op("pool", lambda e: e.dma_start(out=vsb[:], in_=tm_v[:, 64:192].rearrange("(a p) c -> p a c", p=128)), w=["vsb"], dma=True)
    S.op("pool", lambda e: e.memset(ones[:], 1.0), w=["ones"])
    S.op("pool", lambda e: e.memset(epsc[:], EPS), w=["epsc"])
    S.op("pool", lambda e: e.memset(ones32[:], 1.0), w=["ones32"])
    S.op("dve", lambda e: e.tensor_copy(out=rm[:], in_=rm32[:]), r=["rm32"], w=["rm"])
    S.op("dve", lambda e: e.tensor_tensor(out=lp[:], in0=lq[:, 0:128], in1=lq[:, 128:256], op=ALU.mult), r=["lq"], w=["lp"])
    S.op("dve", lambda e: e.tensor_reduce(out=le[:], in_=lp[:].rearrange("p (a c) -> p a c", a=2), axis=AX.X, op=ALU.add),
         r=["lp"], w=["le"])
    S.op("act", lambda e: e.activation(out=le[:], in_=le[:], func=AF.Exp), w=["le"])
    S.op("dve", lambda e: e.tensor_tensor(out=neglam[:], in0=le[:, 1:2], in1=le[:, 0:1], op=ALU.subtract), r=["le"], w=["neglam"])
    S.op("dve", lambda e: e.tensor_scalar(out=neglam[:], in0=neglam[:], scalar1=-lambda_init, scalar2=None, op0=ALU.add), w=["neglam"])
    S.op("dve", lambda e: e.tensor_scalar(out=sw[:], in0=sw[:], scalar1=1.0 - lambda_init, scalar2=None, op0=ALU.mult), w=["sw"])
    tabs = rope_tables(nc, S, st, ropef, sinT, cosT, T, "at_rp")
    ri = 0
    for src, dst, sn, dn in ((qraw, qr, "qraw", "qr"), (kraw, kr, "kraw", "kr")):
        for ti in range(NQ):
            sl = slice(ti * 512, (ti + 1) * 512)
            b = ri % 2
            ri += 1
            S.op("pe", lambda e, src=src, sl=sl, b=b: e.matmul(ps_s[b][0][:], rm[:], src[:, sl], start=True, stop=True),
                 r=[sn, "rm"], w=[f"ps_s{b}0"])
            S.op("dve", lambda e, src=src, sl=sl, b=b: e.tensor_tensor(out=t1[b][:], in0=src[:, sl], in1=cosT[:, sl], op=ALU.mult),
                 r=[sn] + tabs, w=[("t1", b)])
            S.op("dve", lambda e, sl=sl, b=b: e.tensor_tensor(out=t2[b][:], in0=ps_s[b][0][:], in1=sinT[:, sl], op=ALU.mult),
                 r=tabs, w=[("t2", b), f"ps_s{b}0"])
            S.op("pool", lambda e, dst=dst, sl=sl, b=b: e.tensor_tensor(out=dst[:, sl], in0=t1[b][:], in1=t2[b][:], op=ALU.add),
                 r=[("t1", b), ("t2", b)], w=[(dn, ti)])
    qr_all = [("qr", ti) for ti in range(NQ)]
    kr_all = [("kr", ti) for ti in range(NQ)]
    psn = lambda i, m: f"ps_s{i}{m}"
    dq = []

    def pop_deferred(n, limit=2):
        k = 0
        while dq and k < limit:
            fn, need_odd = dq[0]
            if need_odd and n % 2 == 0:
                break
            dq.pop(0)
            fn()
            k += 1
    for qi in range(NQ):
        qs = slice(qi * 512, (qi + 1) * 512)
        nk = 4 * (qi + 1)
        ab = qi % 2

        def QK(n, qs=qs, qi=qi):
            i = n % 2
            for m in range(2):
                S.op("pe", lambda e, n=n, i=i, m=m, qs=qs: e.matmul(ps_s[i][m][:], kr[64 * m:64 * m + 64, n * 128:(n + 1) * 128],
                                                                qr[64 * m:64 * m + 64, qs], start=True, stop=True),
                     r=[("qr", qi), ("kr", n // 4)], w=[psn(i, m)])

        def EXP(n, qi=qi):
            i = n % 2
            j = n % 3
            for m in range(2):
                S.op("act", lambda e, i=i, j=j, m=m: e.activation(out=P[j][m][:], in_=ps_s[i][m][:], func=AF.Exp, scale=0.125),
                     w=[psn(i, m), ("P", j, m)])
                d = n - 4 * qi
                if d >= 0:
                    S.op("dve", lambda e, j=j, m=m, d=d: e.tensor_tensor(out=P[j][m][:], in0=P[j][m][:], in1=cmask[:, d, :], op=ALU.mult),
                         r=["cmask"], w=[("P", j, m)])

        def PV(n, nk=nk, ab=ab):
            j = n % 3
            for m in range(2):
                S.op("pe", lambda e, n=n, j=j, m=m, nk=nk: e.matmul(ps_o[m][:], vsb[:, n, :], P[j][m][:], start=(n == 0), stop=(n == nk - 1)),
                     r=[("P", j, m), "vsb"], w=[f"ps_o{m}"])
            S.op("pe", lambda e, n=n, j=j, nk=nk: e.matmul(ps_l[0][:], ones[:], P[j][0][:], start=(n == 0), stop=(n == nk - 1)),
                 r=[("P", j, 0), "ones"], w=["ps_l0"])
            eng, acc, an = ("dve", accD[ab], ("accD", ab)) if n % 2 == 0 else ("pool", accP[ab], ("accP", ab))
            if n < 2:
                S.op(eng, lambda e, j=j, acc=acc: e.tensor_copy(out=acc[:], in_=P[j][1][:]), r=[("P", j, 1)], w=[an])
            else:
                S.op(eng, lambda e, j=j, acc=acc: e.tensor_tensor(out=acc[:], in0=acc[:], in1=P[j][1][:], op=ALU.add), r=[("P", j, 1)], w=[an])
        QK(0)
        for n in range(nk):
            if n >= 1:
                pop_deferred(n)
            if n + 1 < nk:
                QK(n + 1)
            EXP(n)
            PV(n)
        while dq:
            fn, need_odd = dq.pop(0)
            fn()
        eb = qi % 2
        S.op("dve", lambda e, eb=eb: e.tensor_copy(out=o0[eb][:], in_=ps_o[0][:]), w=[("o0", eb), "ps_o0"])
        S.op("dve", lambda e, eb=eb: e.tensor_copy(out=o1[eb][:], in_=ps_o[1][:]), w=[("o1", eb), "ps_o1"])
        S.op("dve", lambda e, eb=eb: e.tensor_copy(out=r0[eb][:], in_=ps_l[0][:]), w=[("r0", eb), "ps_l0"])
        D = lambda fn, odd=False: dq.append((fn, odd))
        D(lambda ab=ab: S.op("pe", lambda e: e.matmul(ps_l[1][:], ones32[:], accD[ab][:], start=True, stop=False),
                             r=[("accD", ab), "ones32"], w=["ps_l1"]))
        D(lambda ab=ab: S.op("pe", lambda e: e.matmul(ps_l[1][:], ones32[:], accP[ab][:], start=False, stop=True),
                             r=[("accP", ab), "ones32"], w=["ps_l1"]))
        D(lambda eb=eb: S.op("dve", lambda e: e.tensor_copy(out=r1[eb][:], in_=ps_l[1][:]), w=[("r1", eb), "ps_l1"]))
        for rr, rn in ((r0, "r0"), (r1, "r1")):
            D(lambda eb=eb, rr=rr, rn=rn: S.op("act", lambda e: e.activation(out=rr[eb][:], in_=rr[eb][:], func=AF.Ln), w=[(rn, eb)]))
            D(lambda eb=eb, rr=rr, rn=rn: S.op("act", lambda e: e.activation(out=rr[eb][:], in_=rr[eb][:], func=AF.Exp, scale=-1.0), w=[(rn, eb)]))
        D(lambda eb=eb: S.op("dve", lambda e: e.tensor_tensor(out=o0[eb][:], in0=o0[eb][:], in1=r0[eb][:], op=ALU.mult), r=[("r0", eb)], w=[("o0", eb)]))
        D(lambda eb=eb: S.op("dve", lambda e: e.tensor_tensor(out=o1[eb][:], in0=o1[eb][:], in1=r1[eb][:], op=ALU.mult), r=[("r1", eb)], w=[("o1", eb)]))
        D(lambda eb=eb: S.op("dve", lambda e: e.scalar_tensor_tensor(out=o0[eb][:], in0=o1[eb][:], scalar=neglam[:, 0:1], in1=o0[eb][:],
                                                                     op0=ALU.mult, op1=ALU.add), r=[("o1", eb), "neglam"], w=[("o0", eb)]))
        D(lambda eb=eb: S.op("act", lambda e: e.activation(out=osq[:], in_=o0[eb][:], func=AF.Square), r=[("o0", eb)], w=["osq"]))
        D(lambda: S.op("pe", lambda e: e.matmul(ps_n[:], ones[:], osq[:], start=True, stop=True), r=["osq", "ones"], w=[psn(0, 0)]), True)
        D(lambda: S.op("act", lambda e: e.activation(out=nsq[:], in_=ps_n[:], func=AF.Ln, scale=1.0 / 128.0, bias=epsc[:, 0:1]),
                       r=["epsc"], w=["nsq", psn(0, 0)]))
        D(lambda: S.op("act", lambda e: e.activation(out=nsq[:], in_=nsq[:], func=AF.Exp, scale=-0.5), w=["nsq"]))
        D(lambda eb=eb: S.op("dve", lambda e: e.scalar_tensor_tensor(out=o0[eb][:], in0=o0[eb][:], scalar=sw[:, 0:1], in1=nsq[:],
                                                                     op0=ALU.mult, op1=ALU.mult), r=["nsq", "sw"], w=[("o0", eb)]))
        D(lambda eb=eb, qs=qs: S.op("pool", lambda e: e.tensor_tensor(out=ob[eb][:], in0=o0[eb][:], in1=sag[:, qs], op=ALU.mult),
                                    r=[("o0", eb), "sag"], w=[("ob", eb)]))
        D(lambda eb=eb, qs=qs: S.op("sp", lambda e: e.dma_start(out=oa[:, qs], in_=ob[eb][:]), r=[("ob", eb)], dma=True))
    while dq:
        fn, need_odd = dq.pop(0)
        fn()


def attn_consts():
    ropef = np.zeros((128, 1), np.float32)
    inv = (ROPE_THETA ** (-np.arange(0, 16, 2, dtype=np.float32) / 16.0)).astype(np.float32)
    rmat = np.zeros((128, 128), np.float32)
    for base in (0, 64):
        for i in range(8):
            ropef[base + i, 0] = -inv[i]
            ropef[base + 8 + i, 0] = inv[i]
            rmat[base + 8 + i, base + i] = 1.0
            rmat[base + i, base + 8 + i] = 1.0
    k = np.arange(128)[:, None]
    q = np.arange(512)[None, :]
    cmask = np.stack([(128 * d + k <= q) for d in range(4)]).astype(ml_dtypes.bfloat16)
    return ropef, rmat, cmask


def build_attn(lambda_init, T=SEQ):
    nc = bass.Bass("TRN2", target_bir_lowering=False)
    fm = nc.dram_tensor("fm", [NFM, T], BF16, kind="ExternalInput").ap()
    tm_v = nc.dram_tensor("tm_v", [T, 192], BF16, kind="ExternalInput").ap()
    lqk = nc.dram_tensor("lqk", [1, 256], F32, kind="ExternalInput").ap()
    subln = nc.dram_tensor("subln", [128, 1], F32, kind="ExternalInput").ap()
    ropef = nc.dram_tensor("ropef", [128, 1], F32, kind="ExternalInput").ap()
    rmat = nc.dram_tensor("rmat", [128, 128], F32, kind="ExternalInput").ap()
    cmask = nc.dram_tensor("cmask", [4, 128, 512], BF16, kind="ExternalInput").ap()
    oa = nc.dram_tensor("oa", [128, T], BF16, kind="ExternalOutput").ap()
    with contextlib.ExitStack() as st:
        S = Sched(nc)
        phase_attn(nc, S, st, fm, tm_v, lqk, subln, ropef, rmat, cmask, oa, lambda_init, T)
        S.emit()
    return nc


def hgrn_consts():
    s = np.arange(128)[:, None]
    t = np.arange(128)[None, :]
    same = (s // 16) == (t // 16)
    m_incl = (same & (s <= t)).astype(np.float32)
    m_rev = (same & (s > t)).astype(np.float32)
    m_tot8 = ((s // 16) == np.arange(8)[None, :]).astype(np.float32)
    mcat = np.concatenate([m_incl, m_tot8], axis=1)
    return mcat, m_rev


def phase_hgrn(nc, S, st, fm, tm_sf, tm_v, lbl_bc_d, lbl_col_d, gw_d, mcat_d, mrev_d, oh, lb_coef, T=SEQ):
    TS = lambda n, s, d: st.enter_context(nc.sbuf_tensor(n, s, d))
    PS = lambda n: st.enter_context(nc.psum_tensor(n, [128, 512], F32))
    NT = T // 128
    sq = TS("hg_sq", [64, T], BF16)
    snf = TS("hg_snf", [64, T], BF16)
    shg = TS("hg_shg", [64, T], BF16)
    sf = TS("hg_sf", [128, NT, 64], F32)
    omf = TS("hg_omf", [128, NT, 64], F32)
    logf = TS("hg_logf", [128, NT, 64], F32)
    vall = TS("hg_v", [128, NT, 64], BF16)
    lbl = TS("hg_lbl", [128, 128], F32)
    lb_bc = TS("hg_lb_bc", [128, 64], F32)
    oml_bc = TS("hg_oml_bc", [128, 64], F32)
    lblc = TS("hg_lblc", [64, 2], F32)
    oml_col = TS("hg_oml_col", [64, 1], F32)
    gw = TS("hg_gw", [64, 1], F32)
    mcat = TS("hg_mcat", [128, 136], F32)
    mrev = TS("hg_mrev", [128, 128], F32)
    mincl_bf = TS("hg_mincl", [128, 128], BF16)
    mtot_bf = TS("hg_mtot", [128, 8], BF16)
    ones64 = TS("hg_ones", [64, 64], BF16)
    epsc = TS("hg_epsc", [64, 1], F32)
    eq = TS("hg_eq", [64, 128], F32)
    ekn = TS("hg_ekn", [64, 128], F32)
    dec = [TS(f"hg_dec{i}", [64, 8], F32) for i in range(3)]
    ehat = TS("hg_ehat", [128, 64], F32)
    qt = [TS(f"hg_qt{i}", [64, 128], BF16) for i in range(3)]
    kt = TS("hg_kt", [64, 128], BF16)
    khat = TS("hg_khat", [128, 64], BF16)
    vblk = TS("hg_vblk", [128, 8, 64], BF16)
    scm = [TS(f"hg_scm{i}", [128, 128], BF16) for i in range(3)]
    Sall = [TS(f"hg_S{i}", [64, 9, 64], F32) for i in range(2)]
    Sbf = [TS(f"hg_Sbf{i}", [64, 8, 64], BF16) for i in range(2)]
    osq = TS("hg_osq", [64, 512], BF16)
    o32 = TS("hg_o32", [64, 512], F32)
    nsq = TS("hg_nsq", [64, 512], F32)
    ohb = [TS(f"hg_ohb{i}", [64, 512], BF16) for i in range(2)]
    ps_c = PS("hg_ps_c")
    ps_r = PS("hg_ps_r")
    ps_sc = PS("hg_ps_sc")
    ps_u = [PS(f"hg_ps_u{i}") for i in range(3)]
    ps_oh = [PS(f"hg_ps_oh{i}") for i in range(2)]

    S.op("sp", lambda e: e.dma_start(out=sq[:], in_=fm[0:64, :]), w=["sq"], dma=True)
    S.op("sp", lambda e: e.dma_start(out=snf[:], in_=fm[64:128, :]), w=["snf"], dma=True)
    S.op("sp", lambda e: e.dma_start(out=shg[:], in_=fm[128:192, :]), w=["shg"], dma=True)
    S.op("sp", lambda e: e.dma_start(out=sf[:], in_=tm_sf.rearrange("(a p) c -> p a c", p=128)), w=["sf"], dma=True)
    S.op("pool", lambda e: e.dma_start(out=vall[:], in_=tm_v[:, 0:64].rearrange("(a p) c -> p a c", p=128)), w=["vall"], dma=True)
    S.op("sp", lambda e: e.dma_start(out=lbl[:], in_=lbl_bc_d.partition_broadcast(128)), w=["lbl"], dma=True)
    S.op("sp", lambda e: e.dma_start(out=lblc[:], in_=lbl_col_d[:, :]), w=["lblc"], dma=True)
    S.op("sp", lambda e: e.dma_start(out=gw[:], in_=gw_d[:, :]), w=["gw"], dma=True)
    S.op("sp", lambda e: e.dma_start(out=mcat[:], in_=mcat_d[:, :]), w=["mcat"], dma=True)
    S.op("sp", lambda e: e.dma_start(out=mrev[:], in_=mrev_d[:, :]), w=["mrev"], dma=True)
    S.op("pool", lambda e: e.memset(ones64[:], 1.0), w=["ones64"])
    S.op("pool", lambda e: e.memset(epsc[:], EPS), w=["epsc"])
    S.op("pool", lambda e: e.memset(Sall[0][:, 0, :], 0.0), w=[("S", 0)])
    S.op("dve", lambda e: e.tensor_copy(out=mincl_bf[:], in_=mcat[:, 0:128]), r=["mcat"], w=["mincl_bf"])
    S.op("dve", lambda e: e.tensor_copy(out=mtot_bf[:], in_=mcat[:, 128:136]), r=["mcat"], w=["mtot_bf"])
    S.op("dve", lambda e: e.tensor_tensor(out=lb_bc[:], in0=lbl[:, 64:128], in1=lbl[:, 0:64], op=ALU.subtract), r=["lbl"], w=["lb_bc"])
    S.op("act", lambda e: e.activation(out=lb_bc[:], in_=lb_bc[:], func=AF.Sigmoid), w=["lb_bc"])
    S.op("dve", lambda e: e.tensor_scalar(out=lb_bc[:], in0=lb_bc[:], scalar1=float(lb_coef), scalar2=None, op0=ALU.mult), w=["lb_bc"])
    S.op("dve", lambda e: e.tensor_scalar(out=oml_bc[:], in0=lb_bc[:], scalar1=-1.0, scalar2=1.0, op0=ALU.mult, op1=ALU.add),
         r=["lb_bc"], w=["oml_bc"])
    S.op("dve", lambda e: e.tensor_tensor(out=oml_col[:], in0=lblc[:, 1:2], in1=lblc[:, 0:1], op=ALU.subtract), r=["lblc"], w=["oml_col"])
    S.op("act", lambda e: e.activation(out=oml_col[:], in_=oml_col[:], func=AF.Sigmoid), w=["oml_col"])
    S.op("dve", lambda e: e.tensor_scalar(out=oml_col[:], in0=oml_col[:], scalar1=-float(lb_coef), scalar2=1.0, op0=ALU.mult, op1=ALU.add),
         w=["oml_col"])
    for g in range(NT // 8 if NT >= 8 else 1):
        nt = min(8, NT)
        sl = slice(g * 8, g * 8 + nt)
        S.op("dve", lambda e, sl=sl, nt=nt: e.tensor_tensor(out=sf[:, sl, :], in0=sf[:, sl, :],
                                                        in1=oml_bc[:].unsqueeze(1).broadcast_to([128, nt, 64]), op=ALU.mult),
             r=["oml_bc"], w=["sf"])
        S.op("dve", lambda e, sl=sl, nt=nt: e.tensor_tensor(out=sf[:, sl, :], in0=sf[:, sl, :],
                                                        in1=lb_bc[:].unsqueeze(1).broadcast_to([128, nt, 64]), op=ALU.add),
             r=["lb_bc"], w=["sf"])
        S.op("act", lambda e, sl=sl: e.activation(out=logf[:, sl, :], in_=sf[:, sl, :], func=AF.Ln), r=["sf"], w=[("logf", g)])
        S.op("pool", lambda e, sl=sl: e.tensor_scalar(out=omf[:, sl, :], in0=sf[:, sl, :], scalar1=-1.0, scalar2=1.0,
                                                     op0=ALU.mult, op1=ALU.add), r=["sf"], w=[("omf", g)])
    def stageA1(i):
        g = i // 8
        b = i % 3
        S.op("pe", lambda e, i=i: e.matmul(ps_c[0:64, 0:136], logf[:, i, :], mcat[:], start=True, stop=True),
             r=[("logf", g), "mcat"], w=["ps_c"])
        S.op("pe", lambda e, i=i: e.matmul(ps_r[:, 0:64], mrev[:], logf[:, i, :], start=True, stop=True),
             r=[("logf", g), "mrev"], w=["ps_r"])
        S.op("act", lambda e: e.activation(out=eq[:], in_=ps_c[0:64, 0:128], func=AF.Exp), w=["eq", "ps_c"])
        S.op("act", lambda e: e.activation(out=ekn[:], in_=ps_c[0:64, 0:128], func=AF.Exp, scale=-1.0), w=["ekn", "ps_c"])
        S.op("act", lambda e, b=b: e.activation(out=dec[b][:], in_=ps_c[0:64, 128:136], func=AF.Exp), w=[("dec", b), "ps_c"])
        S.op("act", lambda e: e.activation(out=ehat[:], in_=ps_r[:, 0:64], func=AF.Exp), w=["ehat", "ps_r"])

    def stageA2(i):
        g = i // 8
        b = i % 3
        ts = slice(i * 128, (i + 1) * 128)
        S.op("dve", lambda e, b=b, ts=ts: e.tensor_tensor(out=qt[b][:], in0=sq[:, ts], in1=eq[:], op=ALU.mult),
             r=["sq", "eq"], w=[("qt", b)])
        S.op("dve", lambda e, ts=ts: e.scalar_tensor_tensor(out=kt[:], in0=snf[:, ts], scalar=oml_col[:, 0:1], in1=ekn[:],
                                                           op0=ALU.mult, op1=ALU.mult), r=["snf", "ekn", "oml_col"], w=["kt"])
        S.op("dve", lambda e, i=i: e.tensor_tensor(out=khat[:], in0=omf[:, i, :], in1=ehat[:], op=ALU.mult),
             r=[("omf", g), "ehat"], w=["khat"])
        S.op("pool", lambda e, i=i: e.tensor_tensor(out=vblk[:], in0=vall[:, i, :].unsqueeze(1).broadcast_to([128, 8, 64]),
                                                   in1=mtot_bf[:].unsqueeze(2).broadcast_to([128, 8, 64]), op=ALU.mult),
             r=["vall", "mtot_bf"], w=["vblk"])
        S.op("pe", lambda e, b=b: e.matmul(ps_sc[:, 0:128], kt[:], qt[b][:], start=True, stop=True), r=["kt", ("qt", b)], w=["ps_sc"])
        S.op("dve", lambda e, b=b: e.tensor_tensor(out=scm[b][:], in0=ps_sc[:, 0:128], in1=mincl_bf[:], op=ALU.mult),
             r=["mincl_bf"], w=[("scm", b), "ps_sc"])
        S.op("pe", lambda e, b=b: e.matmul(ps_u[b][0:64, :], khat[:], vblk[:].rearrange("p a c -> p (a c)"), start=True, stop=True),
             r=["khat", "vblk"], w=[("ps_u", b)])

    def stageB(i):
        b = i % 3
        sb = i % 2
        if i > 0:
            S.op("dve", lambda e, sb=sb: e.tensor_copy(out=Sall[sb][:, 0, :], in_=Sall[1 - sb][:, 8, :]), r=[("S", 1 - sb)], w=[("S", sb)])
        for n in range(8):
            S.op("dve", lambda e, b=b, sb=sb, n=n: e.scalar_tensor_tensor(
                out=Sall[sb][:, n + 1, :], in0=Sall[sb][:, n, :], scalar=dec[b][:, n:n + 1], in1=ps_u[b][0:64, n * 64:(n + 1) * 64],
                op0=ALU.mult, op1=ALU.add), r=[("dec", b)], w=[("S", sb), ("ps_u", b)])
        S.op("act", lambda e, sb=sb: e.activation(out=Sbf[sb][:], in_=Sall[sb][:, 0:8, :], func=AF.Copy), r=[("S", sb)], w=[("Sbf", sb)])

    def stageC(i):
        b = i % 3
        sb = i % 2
        ob = (i // 4) % 2
        c0 = (i % 4) * 128
        S.op("pe", lambda e, i=i, ob=ob, c0=c0, b=b: e.matmul(ps_oh[ob][0:64, c0:c0 + 128], vall[:, i, :], scm[b][:], start=True, stop=False),
             r=["vall", ("scm", b)], w=[("ps_oh", ob)])
        for n in range(8):
            S.op("pe", lambda e, b=b, sb=sb, ob=ob, c0=c0, n=n: e.matmul(
                ps_oh[ob][0:64, c0 + 16 * n:c0 + 16 * n + 16], Sbf[sb][:, n, :], qt[b][:, 16 * n:16 * n + 16],
                start=False, stop=(n == 7)), r=[("Sbf", sb), ("qt", b)], w=[("ps_oh", ob)])
        if i % 4 == 3 or i == NT - 1:
            qs = slice((i // 4) * 512, (i // 4) * 512 + 512)
            S.op("act", lambda e, ob=ob: e.activation(out=osq[:], in_=ps_oh[ob][0:64, :], func=AF.Square), w=["osq", ("ps_oh", ob)])
            S.op("act", lambda e, ob=ob: e.activation(out=o32[:], in_=ps_oh[ob][0:64, :], func=AF.Copy), w=["o32", ("ps_oh", ob)])
            S.op("pe", lambda e: e.matmul(ps_sc[0:64, :], ones64[:], osq[:], start=True, stop=True), r=["osq", "ones64"], w=["ps_sc"])
            S.op("act", lambda e: e.activation(out=nsq[:], in_=ps_sc[0:64, :], func=AF.Ln, scale=1.0 / 64.0, bias=epsc[:, 0:1]), r=["epsc"], w=["nsq", "ps_sc"])
            S.op("act", lambda e: e.activation(out=nsq[:], in_=nsq[:], func=AF.Exp, scale=-0.5), w=["nsq"])
            S.op("dve", lambda e: e.scalar_tensor_tensor(out=o32[:], in0=o32[:], scalar=gw[:, 0:1], in1=nsq[:], op0=ALU.mult, op1=ALU.mult),
                 r=["nsq", "gw"], w=["o32"])
            S.op("pool", lambda e, ob=ob, qs=qs: e.tensor_tensor(out=ohb[ob][:], in0=o32[:], in1=shg[:, qs], op=ALU.mult),
                 r=["o32", "shg"], w=[("ohb", ob)])
            S.op("sp", lambda e, ob=ob, qs=qs: e.dma_start(out=oh[:, qs], in_=ohb[ob][:]), r=[("ohb", ob)], dma=True)
    for i0_ in range(min(2, NT)):
        stageA1(i0_)
        stageA2(i0_)
    for i in range(NT):
        if i + 2 < NT:
            stageA1(i + 2)
        stageB(i)
        if i + 2 < NT:
            stageA2(i + 2)
        stageC(i)


def build_hgrn(lb_coef, T=SEQ):
    nc = bass.Bass("TRN2", target_bir_lowering=False)
    fm = nc.dram_tensor("fm", [NFM, T], BF16, kind="ExternalInput").ap()
    tm_sf = nc.dram_tensor("tm_sf", [T, 64], F32, kind="ExternalInput").ap()
    tm_v = nc.dram_tensor("tm_v", [T, 192], BF16, kind="ExternalInput").ap()
    lbl_bc = nc.dram_tensor("lbl_bc", [1, 128], F32, kind="ExternalInput").ap()
    lbl_col = nc.dram_tensor("lbl_col", [64, 2], F32, kind="ExternalInput").ap()
    gw = nc.dram_tensor("gw", [64, 1], F32, kind="ExternalInput").ap()
    mcat = nc.dram_tensor("mcat", [128, 136], F32, kind="ExternalInput").ap()
    mrev = nc.dram_tensor("mrev", [128, 128], F32, kind="ExternalInput").ap()
    oh = nc.dram_tensor("oh", [64, T], BF16, kind="ExternalOutput").ap()
    with contextlib.ExitStack() as st:
        S = Sched(nc)
        phase_hgrn(nc, S, st, fm, tm_sf, tm_v, lbl_bc, lbl_col, gw, mcat, mrev, oh, lb_coef, T)
        S.emit()
    return nc


def cmul(S, eng, o_re, o_im, a_re, a_im, b_re, b_im, t0, t1, rd, wr, conj_a=False):
    sg = -1.0 if conj_a else 1.0
    S.op(eng, lambda e: e.tensor_tensor(out=t0, in0=a_im, in1=b_im, op=ALU.mult), r=rd, w=[wr + "t0"])
    S.op(eng, lambda e: e.tensor_tensor(out=t1, in0=a_re, in1=b_re, op=ALU.mult), r=rd, w=[wr + "t1"])
    S.op("dve", lambda e: e.scalar_tensor_tensor(out=o_re, in0=t0, scalar=-sg, in1=t1, op0=ALU.mult, op1=ALU.add),
         r=[wr + "t0", wr + "t1"], w=[wr + "re"])
    S.op(eng, lambda e: e.tensor_tensor(out=t0, in0=a_im, in1=b_re, op=ALU.mult), r=rd + [wr + "re"], w=[wr + "t0"])
    S.op(eng, lambda e: e.tensor_tensor(out=t1, in0=a_re, in1=b_im, op=ALU.mult), r=rd + [wr + "re"], w=[wr + "t1"])
    S.op("dve", lambda e: e.scalar_tensor_tensor(out=o_im, in0=t0, scalar=sg, in1=t1, op0=ALU.mult, op1=ALU.add),
         r=[wr + "t0", wr + "t1"], w=[wr + "im"])


def s5_consts():
    negsig = np.repeat(-np.arange(16, dtype=np.float32), 64)[None, :]
    kidx = np.arange(32, dtype=np.float32)[None, :]
    midx = np.arange(1, 513, dtype=np.float32)[None, :]
    rowmask = (np.arange(64)[:, None] // 16 == np.arange(4)[None, :]).astype(np.float32)
    return negsig, kidx, midx, rowmask


def s5_params(z, l, j):
    gs = [4 * j + gl for gl in range(4)]
    f = np.float32
    pA_are = np.concatenate([np.repeat(z["s5_a_re"][l][g][None, :], 16, 0) for g in gs]).astype(f)
    pA_aim = np.concatenate([np.repeat(z["s5_a_im"][l][g][None, :], 16, 0) for g in gs]).astype(f)
    pA_ldt = np.concatenate([np.full((16, 1), z["s5_log_dt"][l][g]) for g in gs]).astype(f)
    pA_bre = np.concatenate([z["s5_b_re"][l][g].T for g in gs]).astype(f)
    pA_bim = np.concatenate([z["s5_b_im"][l][g].T for g in gs]).astype(f)
    pB = np.zeros((2, 128, 3), f)
    pB_cre = np.zeros((2, 128, 64), f)
    pB_cim = np.zeros((2, 128, 64), f)
    for q in range(2):
        for h in range(2):
            gl = 2 * q + h
            g = gs[gl]
            rows = slice(64 * h, 64 * h + 64)
            pB[q, rows, 0] = z["s5_a_re"][l][g]
            pB[q, rows, 1] = z["s5_a_im"][l][g]
            pB[q, rows, 2] = z["s5_log_dt"][l][g]
            pB_cre[q, rows, 16 * gl:16 * gl + 16] = z["s5_c_re"][l][g].T
            pB_cim[q, rows, 16 * gl:16 * gl + 16] = z["s5_c_im"][l][g].T
    dcol = z["s5_d"][l][64 * j:64 * j + 64][:, None].astype(f)
    pA = np.concatenate([pA_are, pA_aim, pA_bre, pA_bim, pA_ldt], axis=1)
    return {"s5p_pA": np.ascontiguousarray(pA), "s5p_pB": pB, "s5p_cre": pB_cre, "s5p_cim": pB_cim, "s5p_d": dcol}


def phase_s5(nc, S, st, fm, pA_d, pB_d, cre_d, cim_d, dcol_d, negsig_d, kidx_d, midx_d, rowmask_d, yg, T=SEQ):
    TS = lambda n, s, d: st.enter_context(nc.sbuf_tensor(n, s, d))
    PS = lambda n: st.enter_context(nc.psum_tensor(n, [128, 512], F32))
    NB = T // 16
    su = TS("s5_su", [64, T], BF16)
    outsb = TS("s5_out", [64, T], BF16)
    pA = TS("s5_pA", [64, 257], F32)
    dcol = TS("s5_dcol", [64, 1], F32)
    rowmask = TS("s5_rowmask", [64, 4], F32)
    negsig = TS("s5_negsig", [64, 1024], F32)
    SCR = TS("s5_scr", [128, 8192], F32)
    tA = [SCR[0:64, 1024 * i:1024 * (i + 1)] for i in range(8)]
    tAi = TS("s5_tAi", [64, 1024], I32)
    sA = [TS(f"s5_sA{i}", [64, 64], F32) for i in range(10)]
    sAi = TS("s5_sAi", [64, 64], I32)
    dtA = TS("s5_dtA", [64, 1], F32)
    W1tab = [[TS(f"s5_W1tab{q}{ri}", [64, 16, 128], BF16) for ri in range(2)] for q in range(2)]
    pB = [TS(f"s5_pB{q}", [128, 3], F32) for q in range(2)]
    crep = [TS(f"s5_crep{q}", [128, 64], F32) for q in range(2)]
    cimp = [TS(f"s5_cimp{q}", [128, 64], F32) for q in range(2)]
    kidx = TS("s5_kidx", [128, 32], F32)
    midx = TS("s5_midx", [128, 512], F32)
    tB = [TS(f"s5_tB{i}", [128, 32], F32) for i in range(7)]
    tBi = TS("s5_tBi", [128, 32], I32)
    cB = [TS(f"s5_cB{i}", [128, 1], F32) for i in range(6)]
    cBi = TS("s5_cBi", [128, 1], I32)
    gt = [SCR[:, 2048 * i:2048 * (i + 1)].rearrange("p (k c) -> p k c", k=32) for i in range(2)]
    Gpad = [[TS(f"s5_G{q}{ri}", [128, 32, 64], BF16) for ri in range(2)] for q in range(2)]
    Tc = [TS(f"s5_Tc{q}", [128, 512], F32) for q in range(2)]
    Tsn = [TS(f"s5_Ts{q}", [128, 512], F32) for q in range(2)]
    rho = [TS(f"s5_rho{q}", [128, 1], F32) for q in range(2)]
    l2 = [SCR[:, 4096 + 512 * i:4096 + 512 * (i + 1)] for i in range(6)]
    l2i = TS("s5_l2i", [128, 512], I32)
    roll = [TS(f"s5_roll{i}", [128, 512], F32) for i in range(2)]
    W15 = [[TS(f"s5_W15{q}{ri}", [128, 512], F32) for ri in range(2)] for q in range(2)]
    W1bf = [[TS(f"s5_W1bf{q}{ri}", [128, 16, 512], BF16) for ri in range(2)] for q in range(2)]
    Xbf = [[TS(f"s5_Xbf{q}{ri}", [128, 512], BF16) for ri in range(2)] for q in range(2)]
    ytmp = [TS(f"s5_ytmp{i}", [64, 512], F32) for i in range(2)]
    ps_z = [PS(f"s5_ps_z{i}") for i in range(2)]
    ps_y = [PS(f"s5_ps_y{i}") for i in range(2)]

    ld = lambda eng, dst, src, name: S.op(eng, lambda e: e.dma_start(out=dst, in_=src), w=[name], dma=True)
    ld("sp", su[:], fm[256:320, :], "su")
    ld("sp", pA[:], pA_d[:, :], "pA")
    ld("sp", dcol[:], dcol_d[:, :], "dcol")
    ld("sp", rowmask[:], rowmask_d[:, :], "rowmask")
    ld("sp", negsig[:], negsig_d.partition_broadcast(64), "negsig")
    ld("sp", kidx[:], kidx_d.partition_broadcast(128), "kidx")
    ld("sp", midx[:], midx_d.partition_broadcast(128), "midx")
    for q in range(2):
        ld("sp", pB[q][:], pB_d[q], ("pB", q))
        ld("sp", crep[q][:], cre_d[q], ("crep", q))
        ld("sp", cimp[q][:], cim_d[q], ("cimp", q))
    are, aim, bre, bim, ldt = pA[:, 0:64], pA[:, 64:128], pA[:, 128:192], pA[:, 192:256], pA[:, 256:257]
    lam, th, abr, abi, mg, zr, zi, den, u0, u1 = [t[:] for t in sA]
    S.op("act", lambda e: e.activation(out=dtA[:], in_=ldt, func=AF.Exp), r=["pA"], w=["dtA"])
    S.op("dve", lambda e: e.tensor_scalar(out=lam, in0=are, scalar1=dtA[:, 0:1], scalar2=None, op0=ALU.mult), r=["pA", "dtA"], w=["lamA"])
    S.op("dve", lambda e: e.tensor_scalar(out=th, in0=aim, scalar1=dtA[:, 0:1], scalar2=None, op0=ALU.mult), r=["pA", "dtA"], w=["thA"])
    S.op("dve", lambda e: e.tensor_copy(out=u0, in_=th), r=["thA"], w=["sAang"])
    sincos(S, u0, u1, sAi[:], den, abi, abr, "sA")
    S.op("act", lambda e: e.activation(out=mg, in_=lam, func=AF.Exp), r=["lamA"], w=["mgA"])
    S.op("dve", lambda e: e.tensor_tensor(out=abr, in0=abr, in1=mg, op=ALU.mult), r=["mgA", "sAcos"], w=["abr"])
    S.op("dve", lambda e: e.tensor_tensor(out=abi, in0=abi, in1=mg, op=ALU.mult), r=["mgA", "sAsin"], w=["abi"])
    S.op("dve", lambda e: e.tensor_scalar(out=abr, in0=abr, scalar1=-1.0, scalar2=None, op0=ALU.add), w=["abr"])
    S.op("dve", lambda e: e.tensor_tensor(out=den, in0=are, in1=are, op=ALU.mult), r=["pA", "sAcos", "sAsin"], w=["den"])
    S.op("dve", lambda e: e.tensor_tensor(out=u0, in0=aim, in1=aim, op=ALU.mult), r=["pA", "sAsin"], w=["u0"])
    S.op("dve", lambda e: e.tensor_tensor(out=den, in0=den, in1=u0, op=ALU.add), r=["u0"], w=["den"])
    S.op("dve", lambda e: e.reciprocal(out=den, in_=den), w=["den"])
    S.op("dve", lambda e: e.tensor_tensor(out=u0, in0=abr, in1=are, op=ALU.mult), r=["abr"], w=["u0"])
    S.op("dve", lambda e: e.tensor_tensor(out=u1, in0=abi, in1=aim, op=ALU.mult), r=["abi"], w=["u1"])
    S.op("dve", lambda e: e.tensor_tensor(out=zr, in0=u0, in1=u1, op=ALU.add), r=["u0", "u1"], w=["zr"])
    S.op("dve", lambda e: e.tensor_tensor(out=zr, in0=zr, in1=den, op=ALU.mult), r=["den"], w=["zr"])
    S.op("dve", lambda e: e.tensor_tensor(out=u0, in0=abi, in1=are, op=ALU.mult), r=["abi", "zr"], w=["u0"])
    S.op("dve", lambda e: e.tensor_tensor(out=u1, in0=abr, in1=aim, op=ALU.mult), r=["abr", "zr"], w=["u1"])
    S.op("dve", lambda e: e.tensor_tensor(out=zi, in0=u0, in1=u1, op=ALU.subtract), r=["u0", "u1"], w=["zi"])
    S.op("dve", lambda e: e.tensor_tensor(out=zi, in0=zi, in1=den, op=ALU.mult), r=["den"], w=["zi"])
    A3 = lambda t: t[:].rearrange("p (s m) -> p s m", s=16)
    bc3 = lambda ap: ap.unsqueeze(1).broadcast_to([64, 16, 64])
    ang3, kf3, hs3, sn3, cs3, mg3, w_r, w_i = tA
    S.op("dve", lambda e: e.tensor_tensor(out=A3(ang3), in0=A3(negsig), in1=bc3(th), op=ALU.mult), r=["negsig", "thA"], w=["tAang"])
    sincos(S, ang3[:], kf3[:], tAi[:], hs3[:], sn3[:], cs3[:], "tA")
    S.op("dve", lambda e: e.tensor_tensor(out=A3(mg3), in0=A3(negsig), in1=bc3(lam), op=ALU.mult), r=["negsig", "lamA"], w=["mg3"])
    S.op("act", lambda e: e.activation(out=mg3[:], in_=mg3[:], func=AF.Exp), w=["mg3"])
    S.op("dve", lambda e: e.tensor_tensor(out=cs3[:], in0=cs3[:], in1=mg3[:], op=ALU.mult), r=["mg3"], w=["tAcos"])
    S.op("dve", lambda e: e.tensor_tensor(out=sn3[:], in0=sn3[:], in1=mg3[:], op=ALU.mult), r=["mg3"], w=["tAsin"])
    cmul(S, "dve", A3(w_r), A3(w_i), A3(cs3), A3(sn3), bc3(zr), bc3(zi), A3(ang3), A3(kf3),
         ["tAcos", "tAsin", "zr", "zi", "tAang", "tAkf"], "wz")
    cmul(S, "dve", A3(cs3), A3(sn3), A3(w_r), A3(w_i), bc3(bre), bc3(bim), A3(ang3), A3(kf3),
         ["wzre", "wzim", "pA", "tAcos", "tAsin"], "Bs")
    for q in range(2):
        for ri, src in ((0, cs3), (1, sn3)):
            for h in range(2):
                gl = 2 * q + h
                S.op("dve", lambda e, q=q, ri=ri, h=h, gl=gl, src=src: e.tensor_scalar(
                    out=W1tab[q][ri][:, :, 64 * h:64 * h + 64], in0=A3(src), scalar1=rowmask[:, gl:gl + 1], scalar2=None, op0=ALU.mult),
                    r=["Bsre", "Bsim", "rowmask"], w=[("W1tab", q, ri, h)])
    S.barrier()
    bq = []

    class _Defer:
        def op(self, *a, **k):
            bq.append((a, k))
    SB = _Defer()
    for q in range(2):
        lamB, thB, dtB, phi, th15, junk = [t[:] for t in cB]
        angk, kfk, hsk, snk, csk, mgk, nsk = [t[:] for t in tB]
        pq = [("pB", q)]
        tg = f"B{q}"
        SB.op("act", lambda e, q=q: e.activation(out=dtB, in_=pB[q][:, 2:3], func=AF.Exp), r=pq, w=[tg + "dt"])
        SB.op("dve", lambda e, q=q: e.tensor_tensor(out=lamB, in0=pB[q][:, 0:1], in1=dtB, op=ALU.mult), r=pq + [tg + "dt"], w=[tg + "lam"])
        SB.op("dve", lambda e, q=q: e.tensor_tensor(out=thB, in0=pB[q][:, 1:2], in1=dtB, op=ALU.mult), r=pq + [tg + "dt"], w=[tg + "th"])
        SB.op("dve", lambda e: e.tensor_scalar(out=angk, in0=kidx[:], scalar1=thB[:, 0:1], scalar2=None, op0=ALU.mult),
             r=["kidx", tg + "th"], w=[tg + "kang"])
        sincos(SB, angk, kfk, tBi[:], hsk, snk, csk, tg + "k")
        SB.op("dve", lambda e: e.tensor_scalar(out=mgk, in0=kidx[:], scalar1=lamB[:, 0:1], scalar2=None, op0=ALU.mult),
             r=["kidx", tg + "lam"], w=[tg + "mgk"])
        SB.op("act", lambda e: e.activation(out=mgk, in_=mgk, func=AF.Exp), w=[tg + "mgk"])
        SB.op("dve", lambda e: e.tensor_tensor(out=csk, in0=csk, in1=mgk, op=ALU.mult), r=[tg + "mgk"], w=[tg + "kcos"])
        SB.op("dve", lambda e: e.tensor_tensor(out=snk, in0=snk, in1=mgk, op=ALU.mult), r=[tg + "mgk"], w=[tg + "ksin"])
        SB.op("dve", lambda e: e.tensor_scalar(out=nsk, in0=snk, scalar1=-1.0, scalar2=None, op0=ALU.mult), r=[tg + "ksin"], w=[tg + "nsk"])
        SB.op("dve", lambda e: e.tensor_scalar(out=kfk, in0=csk, scalar1=-1.0, scalar2=None, op0=ALU.mult), r=[tg + "kcos"], w=[tg + "kkf"])
        kb = lambda ap: ap.unsqueeze(2).broadcast_to([128, 32, 64])
        cb = lambda t: t[:].unsqueeze(1).broadcast_to([128, 32, 64])
        for ri, (f1, f2) in enumerate(((csk, nsk), (nsk, kfk))):
            SB.op("dve", lambda e, q=q, f1=f1: e.tensor_tensor(out=gt[0][:], in0=cb(crep[q]), in1=kb(f1), op=ALU.mult),
                 r=[("crep", q), tg + "kcos", tg + "nsk", tg + "kkf"], w=["gt0"])
            SB.op("dve", lambda e, q=q, f2=f2: e.tensor_tensor(out=gt[1][:], in0=cb(cimp[q]), in1=kb(f2), op=ALU.mult),
                 r=[("cimp", q), tg + "kcos", tg + "nsk", tg + "kkf"], w=["gt1"])
            SB.op("dve", lambda e, q=q, ri=ri: e.tensor_tensor(out=Gpad[q][ri][:], in0=gt[0][:], in1=gt[1][:], op=ALU.add),
                 r=["gt0", "gt1"], w=[("Gpad", q, ri)])
        SB.op("dve", lambda e: e.tensor_scalar(out=phi, in0=thB, scalar1=16.0, scalar2=None, op0=ALU.mult), r=[tg + "th"], w=[tg + "phi"])
        SB.op("dve", lambda e: e.tensor_scalar(out=th15, in0=phi, scalar1=1.0 / (2.0 * math.pi), scalar2=None, op0=ALU.mult),
             r=[tg + "phi"], w=[tg + "th15"])
        SB.op("dve", lambda e: e.tensor_copy(out=cBi[:], in_=th15), r=[tg + "th15"], w=[tg + "cBi"])
        SB.op("dve", lambda e: e.tensor_copy(out=th15, in_=cBi[:]), r=[tg + "cBi"], w=[tg + "th15"])
        SB.op("dve", lambda e: e.scalar_tensor_tensor(out=phi, in0=th15, scalar=-C1_2PI, in1=phi, op0=ALU.mult, op1=ALU.add),
             r=[tg + "th15"], w=[tg + "phi"])
        SB.op("dve", lambda e: e.scalar_tensor_tensor(out=phi, in0=th15, scalar=-C2_2PI, in1=phi, op0=ALU.mult, op1=ALU.add),
             r=[tg + "th15"], w=[tg + "phi"])
        SB.op("dve", lambda e: e.tensor_scalar(out=l2[0][:], in0=midx[:], scalar1=phi[:, 0:1], scalar2=None, op0=ALU.mult),
             r=["midx", tg + "phi"], w=["l2ang"])
        sincos(SB, l2[0][:], l2[1][:], l2i[:], l2[2][:], Tsn[q][:], Tc[q][:], "l2")
        SB.op("dve", lambda e, q=q: e.tensor_copy(out=Tsn[q][:], in_=Tsn[q][:]), r=["l2sin"], w=[("Ts", q)])
        SB.op("dve", lambda e, q=q: e.tensor_copy(out=Tc[q][:], in_=Tc[q][:]), r=["l2cos"], w=[("Tc", q)])
        SB.op("act", lambda e, q=q: e.activation(out=rho[q][:], in_=lamB, func=AF.Exp, scale=16.0), r=[tg + "lam"], w=[("rho", q)])
    suv = su[:].rearrange("p (m s) -> p s m", s=16)
    zi_ = 0
    for q in range(2):
        for ri in range(2):
            for s in range(16):
                pb = zi_ % 2
                zi_ += 1
                S.op("pe", lambda e, q=q, ri=ri, s=s, pb=pb: e.matmul(ps_z[pb][:, 0:NB], W1tab[q][ri][:, s, :], suv[:, s, :], start=True, stop=True),
                     r=["su", ("W1tab", q, ri, 0), ("W1tab", q, ri, 1)], w=[("ps_z", pb)])
                dst = W15[q][ri] if s == 15 else roll[s % 2]
                dn = ("W15", q, ri) if s == 15 else ("roll", s % 2)
                if s == 0:
                    S.op("dve", lambda e, pb=pb, dst=dst: e.tensor_copy(out=dst[:, 0:NB], in_=ps_z[pb][:, 0:NB]), w=[dn, ("ps_z", pb)])
                else:
                    S.op("dve", lambda e, pb=pb, dst=dst, s=s: e.tensor_tensor(out=dst[:, 0:NB], in0=ps_z[pb][:, 0:NB],
                                                                          in1=roll[(s - 1) % 2][:, 0:NB], op=ALU.add),
                         r=[("roll", (s - 1) % 2)], w=[dn, ("ps_z", pb)])
                S.op("act", lambda e, q=q, ri=ri, s=s, dst=dst: e.activation(out=W1bf[q][ri][:, s, 0:NB], in_=dst[:, 0:NB], func=AF.Copy),
                     r=[dn], w=[("W1bf", q, ri, s)])
                for _ in range(3):
                    if bq:
                        a, k = bq.pop(0)
                        S.op(*a, **k)
    while bq:
        a, k = bq.pop(0)
        S.op(*a, **k)
    for q in range(2):
        ur, ui, t0, t1, vr, vi = [t[:, 0:NB] for t in l2]
        tc, tsn = Tc[q][:, 0:NB], Tsn[q][:, 0:NB]
        wre, wim = W15[q][0][:, 0:NB], W15[q][1][:, 0:NB]
        cmul(S, "dve", ur, ui, tc, tsn, wre, wim, t0, t1, [("Tc", q), ("Ts", q), ("W15", q, 0), ("W15", q, 1), "l2v"], "l2u", conj_a=True)
        rb = rho[q][:, 0:1].broadcast_to([128, NB])
        S.op("dve", lambda e, rb=rb: e.tensor_tensor_scan(out=vr, data0=rb, data1=ur, initial=0.0, op0=ALU.mult, op1=ALU.add),
             r=["l2ure", ("rho", q)], w=["l2vr"])
        S.op("dve", lambda e, rb=rb: e.tensor_tensor_scan(out=vi, data0=rb, data1=ui, initial=0.0, op0=ALU.mult, op1=ALU.add),
             r=["l2uim", ("rho", q)], w=["l2vi"])
        cmul(S, "dve", ur, ui, tc, tsn, vr, vi, t0, t1, [("Tc", q), ("Ts", q), "l2vr", "l2vi"], "l2x")
        for ri, src in ((0, ur), (1, ui)):
            S.op("pool", lambda e, q=q, ri=ri: e.memset(Xbf[q][ri][:, 0:1], 0.0), w=[("Xbf", q, ri)])
            if NB > 1:
                S.op("act", lambda e, q=q, ri=ri, src=src: e.activation(out=Xbf[q][ri][:, 1:NB], in_=src[:, 0:NB - 1], func=AF.Copy),
                     r=["l2xre", "l2xim"], w=[("Xbf", q, ri)])
        S.op("dve", lambda e: e.tensor_copy(out=l2[0][:, 0:1], in_=l2[0][:, 0:1]), r=[("Xbf", q, 0), ("Xbf", q, 1)], w=["l2v", "l2ure", "l2uim"])
    outv = outsb[:].rearrange("p (m s) -> p s m", s=16)
    for s in range(16):
        pb = s % 2
        k = 0
        for q in range(2):
            for ri in range(2):
                S.op("pe", lambda e, q=q, ri=ri, s=s, pb=pb, k=k: e.matmul(ps_y[pb][0:64, 0:NB], Gpad[q][ri][:, s, :], W1bf[q][ri][:, s, 0:NB],
                                                                     start=(k == 0), stop=False),
                     r=[("Gpad", q, ri), ("W1bf", q, ri, s)], w=[("ps_y", pb)])
                k += 1
        for q in range(2):
            for ri in range(2):
                S.op("pe", lambda e, q=q, ri=ri, s=s, pb=pb, k=k: e.matmul(ps_y[pb][0:64, 0:NB], Gpad[q][ri][:, s + 16, :], Xbf[q][ri][:, 0:NB],
                                                                     start=False, stop=(k == 7)),
                     r=[("Gpad", q, ri), ("Xbf", q, ri)], w=[("ps_y", pb)])
                k += 1
        S.op("dve", lambda e, s=s, pb=pb: e.scalar_tensor_tensor(out=ytmp[pb][:, 0:NB], in0=suv[:, s, :], scalar=dcol[:, 0:1],
                                                            in1=ps_y[pb][0:64, 0:NB], op0=ALU.mult, op1=ALU.add),
             r=["su", "dcol"], w=[("ytmp", pb), ("ps_y", pb)])
        S.op("act", lambda e, s=s, pb=pb: e.activation(out=outv[:, s, :], in_=ytmp[pb][:, 0:NB], func=AF.Gelu),
             r=[("ytmp", pb)], w=[("outsb", s)])
    S.op("sp", lambda e: e.dma_start(out=yg[:, :], in_=outsb[:]), r=[("outsb", s) for s in range(16)], dma=True)


def build_s5(T=SEQ):
    nc = bass.Bass("TRN2", target_bir_lowering=False)
    D = lambda n, s, d=F32, k="ExternalInput": nc.dram_tensor(n, s, d, kind=k).ap()
    fm = D("fm", [NFM, T], BF16)
    pA = D("s5p_pA", [64, 257]); pB = D("s5p_pB", [2, 128, 3]); cre = D("s5p_cre", [2, 128, 64]); cim = D("s5p_cim", [2, 128, 64])
    dcol = D("s5p_d", [64, 1]); negsig = D("negsig", [1, 1024]); kidx = D("kidx", [1, 32]); midx = D("midx", [1, 512])
    rowmask = D("rowmask", [64, 4])
    yg = D("yg", [64, T], BF16, "ExternalOutput")
    with contextlib.ExitStack() as st:
        S = Sched(nc)
        phase_s5(nc, S, st, fm, pA, pB, cre, cim, dcol, negsig, kidx, midx, rowmask, yg, T)
        S.emit()
    return nc


def phase_out(nc, S, st, mixin, ssg_d, hT, wout_d, gluw_d, glub_d, fnw_d, hout, final, NTOK=TQ):
    TS = lambda n, s, d: st.enter_context(nc.sbuf_tensor(n, s, d))
    PS = lambda n: st.enter_context(nc.psum_tensor(n, [128, 512], F32))
    wst = [TS(f"po_wst{i}", [128, 1024], F32) for i in range(2)]
    wout = TS("po_wout", [128, 8, 1024], BF16)
    gst = TS("po_gst", [128, 2, 256], F32)
    gluw = TS("po_gluw", [128, 2, 256], BF16)
    glub = TS("po_glub", [128, 2], F32)
    fnw = TS("po_fnw", [128, 8], F32)
    ones = TS("po_ones", [128, 128], BF16)
    mix = [TS(f"po_mix{i}", [128, 8, 512], BF16) for i in range(2)]
    ssg = [TS(f"po_ssg{i}", [128, 2, 512], BF16) for i in range(2)]
    hin = [TS(f"po_hin{i}", [128, 8, 512], F32) for i in range(2)]
    sg = TS("po_sg", [128, 512], F32)
    osb = TS("po_osb", [128, 2, 512], BF16)
    hn = TS("po_hn", [128, 8, 512], F32)
    hsq = TS("po_hsq", [128, 8, 512], BF16)
    nsq = TS("po_nsq", [128, 512], F32)
    ps_g = PS("po_ps_g")
    ps_o = [PS(f"po_ps_o{i}") for i in range(3)]
    ps_n = PS("po_ps_n")

    S.op("pool", lambda e: e.memset(ones[:], 1.0), w=["ones"])
    S.op("sp", lambda e: e.dma_start(out=gst[:], in_=gluw_d.rearrange("(k p) o -> p k o", p=128)), w=["gst"], dma=True)
    S.op("sp", lambda e: e.dma_start(out=glub[:], in_=glub_d[:, :]), w=["glub"], dma=True)
    S.op("sp", lambda e: e.dma_start(out=fnw[:], in_=fnw_d[:, :]), w=["fnw"], dma=True)
    S.op("dve", lambda e: e.tensor_copy(out=gluw[:], in_=gst[:]), r=["gst"], w=["gluw"])
    for k in range(8):
        S.op("sp", lambda e, k=k: e.dma_start(out=wst[k % 2][:], in_=wout_d[k * 128:(k + 1) * 128, :]), w=[("wst", k % 2)], dma=True)
        S.op("pool" if k % 2 else "dve", lambda e, k=k: e.tensor_copy(out=wout[:, k, :], in_=wst[k % 2][:]), r=[("wst", k % 2)], w=[("wout", k)])
    wr = [("wout", k) for k in range(8)]
    mv = mixin.rearrange("(k p) t -> p k t", p=128)
    sv = ssg_d.rearrange("(k p) t -> p k t", p=128)
    hv = hT.rearrange("(k p) t -> p k t", p=128)
    ov = hout.rearrange("(k p) t -> p k t", p=128)
    oi = 0
    for ti in range(NTOK // 512):
        b = ti % 2
        ts = slice(ti * 512, (ti + 1) * 512)
        S.op("sp", lambda e, b=b, ts=ts: e.dma_start(out=mix[b][:], in_=mv[:, :, ts]), w=[("mix", b)], dma=True)
        S.op("sp", lambda e, b=b, ts=ts: e.dma_start(out=ssg[b][:], in_=sv[:, :, ts]), w=[("ssg", b)], dma=True)
        S.op("pool", lambda e, b=b, ts=ts: e.dma_start(out=hin[b][:], in_=hv[:, :, ts]), w=[("hin", b)], dma=True)
        for oc in range(2):
            for kc in range(2):
                S.op("pe", lambda e, b=b, oc=oc, kc=kc: e.matmul(ps_g[:], gluw[:, kc, oc * 128:(oc + 1) * 128], mix[b][:, 2 + kc, :],
                                                             start=(kc == 0), stop=(kc == 1)), r=[("mix", b), "gluw"], w=["ps_g"])
            S.op("act", lambda e, oc=oc: e.activation(out=sg[:], in_=ps_g[:], func=AF.Sigmoid, bias=glub[:, oc:oc + 1]),
                 r=["glub"], w=["sg", "ps_g"])
            S.op("dve", lambda e, b=b, oc=oc: e.tensor_tensor(out=sg[:], in0=sg[:], in1=mix[b][:, 2 + oc, :], op=ALU.mult),
                 r=[("mix", b)], w=["sg"])
            S.op("dve", lambda e, b=b, oc=oc: e.tensor_tensor(out=osb[:, oc, :], in0=sg[:], in1=ssg[b][:, oc, :], op=ALU.mult),
                 r=[("ssg", b), "sg"], w=[("osb", oc)])
        for dc in range(8):
            pb = oi % 3
            oi += 1
            for kc in range(8):
                rhs = (lambda b=b, kc=kc: osb[:, kc - 2, :]) if kc in (2, 3) else (lambda b=b, kc=kc: mix[b][:, kc, :])
                S.op("pe", lambda e, dc=dc, kc=kc, pb=pb, rhs=rhs: e.matmul(ps_o[pb][:], wout[:, kc, dc * 128:(dc + 1) * 128], rhs(),
                                                                      start=(kc == 0), stop=(kc == 7)),
                     r=wr + [("mix", b), ("osb", 0), ("osb", 1)], w=[("ps_o", pb)])
            S.op("dve", lambda e, b=b, dc=dc, pb=pb: e.tensor_tensor(out=hn[:, dc, :], in0=ps_o[pb][:], in1=hin[b][:, dc, :], op=ALU.add),
                 r=[("hin", b)], w=[("hn", dc), ("ps_o", pb)])
            if not final:
                S.op("sp", lambda e, dc=dc, ts=ts: e.dma_start(out=ov[:, dc, ts], in_=hn[:, dc, :]), r=[("hn", dc)], dma=True)
        if final:
            hr = [("hn", dc) for dc in range(8)]
            S.op("act", lambda e: e.activation(out=hsq[:], in_=hn[:], func=AF.Square), r=hr, w=["hsq"])
            for k in range(8):
                S.op("pe", lambda e, k=k: e.matmul(ps_n[:], ones[:], hsq[:, k, :], start=(k == 0), stop=(k == 7)), r=["hsq", "ones"], w=["ps_n"])
            S.op("act", lambda e: e.activation(out=nsq[:], in_=ps_n[:], func=AF.Sqrt, scale=1.0 / D_MODEL, bias=EPS), w=["nsq", "ps_n"])
            S.op("dve", lambda e: e.reciprocal(out=nsq[:], in_=nsq[:]), w=["nsq"])
            for dc in range(8):
                S.op("pool" if dc % 2 else "dve", lambda e, dc=dc: e.scalar_tensor_tensor(
                    out=hn[:, dc, :], in0=hn[:, dc, :], scalar=fnw[:, dc:dc + 1], in1=nsq[:], op0=ALU.mult, op1=ALU.mult) if dc % 2 == 0 else
                    e.tensor_tensor(out=hn[:, dc, :], in0=hn[:, dc, :], in1=nsq[:], op=ALU.mult),
                    r=["nsq", "fnw"], w=[("hn", dc)])
                if dc % 2:
                    S.op("pool", lambda e, dc=dc: e.tensor_scalar(out=hn[:, dc, :], in0=hn[:, dc, :], scalar1=fnw[:, dc:dc + 1], scalar2=None,
                                                                  op0=ALU.mult), r=["fnw"], w=[("hn", dc)])
                S.op("sp", lambda e, dc=dc, ts=ts: e.dma_start(out=ov[:, dc, ts], in_=hn[:, dc, :]), r=[("hn", dc)], dma=True)


def build_out(final, NTOK=TQ):
    nc = bass.Bass("TRN2", target_bir_lowering=False)
    D = lambda n, s, d=F32, k="ExternalInput": nc.dram_tensor(n, s, d, kind=k).ap()
    mixin = D("mixin", [1024, NTOK], BF16)
    ssg = D("ssg", [256, NTOK], BF16)
    hT = D("hT", [D_MODEL, NTOK])
    wout = D("wout", [1024, 1024]); gluw = D("gluw", [256, 256]); glub = D("glub", [128, 2]); fnw = D("fnw", [128, 8])
    hout = D("hout", [D_MODEL, NTOK], F32, "ExternalOutput")
    with contextlib.ExitStack() as st:
        S = Sched(nc)
        phase_out(nc, S, st, mixin, ssg, hT, wout, gluw, glub, fnw, hout, final, NTOK)
        S.emit()
    return nc


_CACHE = {}


def _prog(key, fn):
    if key not in _CACHE:
        _CACHE[key] = fn()
    return _CACHE[key]


def build_mixers(l, T=SEQ, which=("ip", "at", "hg", "s5")):
    lambda_init = 0.8 - 0.6 * math.exp(-0.3 * l)
    nc = bass.Bass("TRN2", target_bir_lowering=False)
    D = lambda n, s, d=F32, k="ExternalInput": nc.dram_tensor(n, s, d, kind=k).ap()
    hT = D("hT", [D_MODEL, T]); wcat = D("wcat", [D_MODEL, NFM + NTM]); nw = D("nw", [128, 8])
    lqk = D("lqk", [1, 256]); subln = D("subln", [128, 1]); ropef = D("ropef", [128, 1]); rmat = D("rmat", [128, 128])
    cmask = D("cmask", [4, 128, 512], BF16)
    lbl_bc = D("lbl_bc", [1, 128]); lbl_col = D("lbl_col", [64, 2]); gw = D("gw", [64, 1]); mcat = D("mcat", [128, 136]); mrev = D("mrev", [128, 128])
    pA = D("s5p_pA", [64, 257]); pB = D("s5p_pB", [2, 128, 3]); cre = D("s5p_cre", [2, 128, 64]); cim = D("s5p_cim", [2, 128, 64])
    dcol = D("s5p_d", [64, 1]); negsig = D("negsig", [1, 1024]); kidx = D("kidx", [1, 32]); midx = D("midx", [1, 512]); rowmask = D("rowmask", [64, 4])
    fm = D("fm", [NFM, T], BF16, "Internal")
    tm_sf = D("tm_sf", [T, 64], F32, "Internal")
    tm_v = D("tm_v", [T, 192], BF16, "Internal")
    mo = D("mo", [320, T], BF16, "ExternalOutput")
    if "ip" in which:
        with contextlib.ExitStack() as st:
            S = Sched(nc)
            phase_inproj(nc, S, st, hT, wcat, nw, fm, tm_sf, tm_v, T)
            S.emit()
    if "at" in which:
        with contextlib.ExitStack() as st:
            S = Sched(nc)
            S.op("sp", lambda e: e.dma_start(out=mo[128:192, :], in_=fm[192:256, :]), dma=True)
            phase_attn(nc, S, st, fm, tm_v, lqk, subln, ropef, rmat, cmask, mo[192:320, :], lambda_init, T)
            S.emit()
    if "hg" in which:
        with contextlib.ExitStack() as st:
            S = Sched(nc)
            phase_hgrn(nc, S, st, fm, tm_sf, tm_v, lbl_bc, lbl_col, gw, mcat, mrev, mo[0:64, :], float(l), T)
            S.emit()
    if "s5" in which:
        with contextlib.ExitStack() as st:
            S = Sched(nc)
            phase_s5(nc, S, st, fm, pA, pB, cre, cim, dcol, negsig, kidx, midx, rowmask, mo[64:128, :], T)
            S.emit()
    return nc


def mixer_inputs(inp, l, c, hT_b):
    f = np.float32
    j = c % 4
    ropef, rmat, cmask = attn_consts()
    mcat, mrev = hgrn_consts()
    negsig, kidx, midx, rowmask = s5_consts()
    lbl = np.asarray(inp["hgrn_lb_logits"], f)[:, 64 * j:64 * j + 64]
    d = {"hT": hT_b, "wcat": np.ascontiguousarray(np.asarray(inp["w_in"][l], f)[:, core_cols(j)]),
         "nw": np.ascontiguousarray(np.asarray(inp["norm_w"][l], f).reshape(8, 128).T),
         "lqk": np.concatenate([inp["diff_lq1"][l], inp["diff_lq2"][l], inp["diff_lk1"][l], inp["diff_lk2"][l]])[None, :].astype(f),
         "subln": np.asarray(inp["diff_subln_w"][l], f)[:, None], "ropef": ropef, "rmat": rmat, "cmask": cmask,
         "lbl_bc": np.ascontiguousarray(lbl.reshape(1, 128)), "lbl_col": np.ascontiguousarray(lbl.T),
         "gw": np.asarray(inp["hgrn_norm_w"][l], f)[:, None], "mcat": mcat, "mrev": mrev,
         "negsig": negsig, "kidx": kidx, "midx": midx, "rowmask": rowmask}
    d.update(s5_params(inp, l, j))
    return d


def kernel(**inp):
    f = np.float32
    x = np.asarray(inp["x"], f)
    cores = list(range(NCORES))
    hT = [np.ascontiguousarray(x[b].T) for b in range(BATCH)]
    for l in range(DEPTH):
        nc = _prog(("mix", l), lambda: build_mixers(l))
        ims = [mixer_inputs(inp, l, c, hT[c // 4]) for c in cores]
        rm = run_bass_kernel_spmd(nc, ims, core_ids=cores).results
        final = (l == DEPTH - 1)
        nc = _prog(("out", final), lambda: build_out(final))
        ims = []
        for c in cores:
            b, tq = c // 4, c % 4
            ts = slice(tq * TQ, (tq + 1) * TQ)
            src = [4 * b + j for j in range(4)]
            mixin = np.concatenate([rm[s]["mo"][0:64, ts] for s in src] + [rm[s]["mo"][64:128, ts] for s in src]
                                   + [rm[s]["mo"][192:320, ts] for s in src])
            ssg = np.concatenate([rm[s]["mo"][128:192, ts] for s in src])
            ims.append({"mixin": np.ascontiguousarray(mixin), "ssg": np.ascontiguousarray(ssg), "hT": np.ascontiguousarray(hT[b][:, ts]),
                        "wout": np.asarray(inp["w_out"][l], f), "gluw": np.asarray(inp["s5_glu_w"][l], f),
                        "glub": np.ascontiguousarray(np.asarray(inp["s5_glu_b"][l], f).reshape(2, 128).T),
                        "fnw": np.ascontiguousarray(np.asarray(inp["final_norm_w"], f).reshape(8, 128).T)})
        ro = run_bass_kernel_spmd(nc, ims, core_ids=cores).results
        hT = [np.concatenate([ro[4 * b + tq]["hout"] for tq in range(4)], axis=1) for b in range(BATCH)]
    out = np.stack([hT[b].T for b in range(BATCH)]).astype(f)
    return np.ascontiguousarray(out)
```

```python
import contextlib
import math
import numpy as np
import ml_dtypes
import concourse.bass as bass
import concourse.mybir as mybir
from concourse.bass_utils import run_bass_kernel_spmd

F32 = mybir.dt.float32
BF16 = mybir.dt.bfloat16
I32 = mybir.dt.int32
AF = mybir.ActivationFunctionType
ALU = mybir.AluOpType
AX = mybir.AxisListType

D_MODEL = 1024
SEQ = 8192
BATCH = 2
DEPTH = 2
EPS = 1e-6
NCORES = 8
TQ = SEQ // 4
ROPE_THETA = 500000.0
import os
DBG = set(os.environ.get("KDBG", "").split(","))


class Sched:
    ENGS = ["pe", "act", "dve", "pool", "sp"]

    def __init__(self, nc):
        self.nc = nc
        self.ops = []
        self.last_w = {}
        self.readers = {}
        self.cnt = {e: 0 for e in self.ENGS}
        self.dma_cnt = {}
        self.base = set()

    def op(self, eng, fn, r=(), w=(), dma=False):
        deps = set(self.base)
        for x in r:
            if x in self.last_w:
                deps.add(self.last_w[x])
        for x in w:
            if x in self.last_w:
                deps.add(self.last_w[x])
            for d in self.readers.get(x, ()):
                deps.add(d)
        if dma:
            q = self.dma_cnt.get(eng, 0)
            self.dma_cnt[eng] = q + 1
            tok = ("dma", eng, q)
        else:
            self.cnt[eng] += 1
            tok = ("eng", eng, self.cnt[eng])
        self.ops.append((eng, fn, deps, tok))
        for x in w:
            self.last_w[x] = tok
            self.readers[x] = []
        for x in r:
            self.readers.setdefault(x, []).append(tok)
        return tok

    def barrier(self):
        b = set()
        for e in self.ENGS:
            if self.cnt[e] > 0:
                b.add(("eng", e, self.cnt[e]))
        for e, n in self.dma_cnt.items():
            for q in range(max(0, n - self.NSLOT), n):
                b.add(("dma", e, q))
        self.base = b
        self.last_w = {}
        self.readers = {}

    NSLOT = 8

    def emit(self):
        nc = self.nc
        NSLOT = self.NSLOT
        needed = set()
        for (eng, fn, deps, tok) in self.ops:
            for d in deps:
                if d[0] == "eng" and not (d[1] == "pe" and eng == "pe"):
                    needed.add(d)
        sig = {}
        run = {e: 0 for e in self.ENGS}
        for (eng, fn, deps, tok) in self.ops:
            if tok[0] == "eng":
                if tok in needed:
                    run[eng] += 1
                sig[tok] = run[eng]
        with contextlib.ExitStack() as st:
            esem = {e: st.enter_context(nc.semaphore("s_" + e)) for e in self.ENGS}
            dsem = {}
            for e in self.dma_cnt:
                dsem[e] = [st.enter_context(nc.semaphore(f"d_{e}_{i}")) for i in range(NSLOT)]
            block = st.enter_context(nc.Block())
            per = {e: [o for o in self.ops if o[0] == e] for e in self.ENGS}

            def mk(ename):
                def body(eng):
                    seen = {}

                    def wait(tok):
                        if tok[0] == "eng":
                            _, e2, n = tok
                            if e2 == "pe" and ename == "pe":
                                return
                            v = sig[tok]
                            key = ("eng", e2)
                            if seen.get(key, 0) >= v:
                                return
                            seen[key] = v
                            eng.wait_ge(esem[e2], v)
                        else:
                            _, e2, q = tok
                            slot = q % NSLOT
                            val = 16 * (q // NSLOT + 1)
                            key = ("dma", e2, slot)
                            if seen.get(key, 0) >= val:
                                return
                            seen[key] = val
                            eng.wait_ge(dsem[e2][slot], val)
                    for (_, fn, deps, tok) in per[ename]:
                        for d in sorted(deps):
                            wait(d)
                        if tok[0] == "dma":
                            q = tok[2]
                            if q >= NSLOT:
                                wait(("dma", ename, q - NSLOT))
                            ins = fn(eng)
                            ins.then_inc(dsem[ename][q % NSLOT], 16)
                        else:
                            ins = fn(eng)
                            if tok in needed:
                                ins.then_inc(esem[ename], 1)
                    n = self.dma_cnt.get(ename, 0)
                    for q in range(max(0, n - NSLOT), n):
                        wait(("dma", ename, q))
                return body
            block.tensor(mk("pe"))
            block.scalar(mk("act"))
            block.vector(mk("dve"))
            block.gpsimd(mk("pool"))
            block.sync(mk("sp"))


NFM = 704
NTM = 256
FM_CH = [(0, 128), (128, 128), (256, 64), (320, 128), (448, 128), (576, 128)]


def phase_inproj(nc, S, st, hT, wcat, nw, fm, tm_sf, tm_v, T=SEQ):
    TS = lambda n, s, d: st.enter_context(nc.sbuf_tensor(n, s, d))
    PS = lambda n: st.enter_context(nc.psum_tensor(n, [128, 512], F32))
    nw_sb = TS("ip_nw", [128, 8], F32)
    wst = [TS(f"ip_wst{i}", [128, NFM + NTM], F32) for i in range(2)]
    wall = TS("ip_wall", [128, 8, NFM + NTM], BF16)
    ones = TS("ip_ones", [128, 128], BF16)
    xin = [TS(f"ip_xin{i}", [128, 8, 512], F32) for i in range(2)]
    xsq = TS("ip_xsq", [128, 8, 512], BF16)
    sq = TS("ip_sq", [128, 512], F32)
    rstd = TS("ip_rstd", [128, 512], F32)
    xn = [TS(f"ip_xn{i}", [128, 8, 512], BF16) for i in range(2)]
    fmo = [TS(f"ip_fmo{i}", [128, 512], BF16) for i in range(6)]
    tsf = [TS(f"ip_tsf{i}", [128, 4, 64], F32) for i in range(2)]
    tv = [TS(f"ip_tv{i}", [128, 4, 192], BF16) for i in range(2)]
    ps_ss = PS("ip_ps_ss")
    ps_fm = [PS(f"ip_ps_fm{i}") for i in range(4)]
    ps_tm = [PS(f"ip_ps_tm{i}") for i in range(2)]

    S.op("sp", lambda e: e.dma_start(out=nw_sb[:], in_=nw[:, :]), w=["nw"], dma=True)
    S.op("pool", lambda e: e.memset(ones[:], 1.0), w=["ones"])
    for k in range(8):
        S.op("sp", lambda e, k=k: e.dma_start(out=wst[k % 2][:], in_=wcat[k * 128:(k + 1) * 128, :]),
             w=[("wst", k % 2)], dma=True)
        S.op("dve", lambda e, k=k: e.tensor_scalar(out=wall[:, k, :], in0=wst[k % 2][:], scalar1=nw_sb[:, k:k + 1],
                                                  scalar2=None, op0=ALU.mult),
             r=[("wst", k % 2), "nw"], w=[("wall", k)])
    wall_r = [("wall", k) for k in range(8)]
    hT_v = hT.rearrange("(k p) t -> p k t", p=128)
    fmi = 0
    NTI = T // 512

    def load(ti):
        b = ti % 2
        t0 = ti * 512
        S.op("pool", lambda e, b=b, t0=t0: e.dma_start(out=xin[b][:, 0:4, :], in_=hT_v[:, 0:4, t0:t0 + 512]),
             w=[("xin", b, 0)], dma=True)
        S.op("pool", lambda e, b=b, t0=t0: e.dma_start(out=xin[b][:, 4:8, :], in_=hT_v[:, 4:8, t0:t0 + 512]),
             w=[("xin", b, 1)], dma=True)
    def front_sq(ti):
        b = ti % 2
        xr = [("xin", b, 0), ("xin", b, 1)]
        S.op("act", lambda e, b=b: e.activation(out=xsq[:], in_=xin[b][:], func=AF.Square), r=xr, w=["xsq"])

    def front_ss(ti):
        b = ti % 2
        for k in range(8):
            S.op("pe", lambda e, k=k: e.matmul(ps_ss[:], ones[:], xsq[:, k, :], start=(k == 0), stop=(k == 7)),
                 r=["xsq", "ones"], w=["ps_ss"])
        S.op("act", lambda e: e.activation(out=sq[:], in_=ps_ss[:], func=AF.Sqrt, scale=1.0 / D_MODEL, bias=EPS),
             w=["ps_ss", "sq"])
        S.op("dve", lambda e: e.reciprocal(out=rstd[:], in_=sq[:]), r=["sq"], w=["rstd"])
        for hh in range(2):
            S.op("dve", lambda e, b=b, hh=hh: e.tensor_tensor(out=xn[b][:, 4 * hh:4 * hh + 4, :], in0=xin[b][:, 4 * hh:4 * hh + 4, :],
                                                         in1=rstd[:].unsqueeze(1).broadcast_to([128, 4, 512]), op=ALU.mult),
                 r=[("xin", b, hh), "rstd"], w=[("xn", b, hh)])
    load(0)
    if NTI > 1:
        load(1)
    front_sq(0)
    front_ss(0)
    for ti in range(NTI):
        b = ti % 2
        t0 = ti * 512
        if ti + 1 < NTI:
            front_sq(ti + 1)
        xnr = [("xn", b, 0), ("xn", b, 1)]
        for ci, (c0, cw) in enumerate(FM_CH):
            if "NOFM" in DBG or ("FM%d" % ci) in DBG:
                continue
            if ci == 3:
                if ti + 1 < NTI:
                    front_ss(ti + 1)
                if ti + 2 < NTI:
                    load(ti + 2)
            pb = fmi % 4
            for k in range(8):
                S.op("pe", lambda e, k=k, c0=c0, cw=cw, pb=pb, b=b: e.matmul(
                    ps_fm[pb][0:cw, :], wall[:, k, c0:c0 + cw], xn[b][:, k, :], start=(k == 0), stop=(k == 7)),
                    r=xnr + wall_r, w=[("ps_fm", pb)])
            fb = fmi % 6
            fmi += 1
            if ci == 0:
                S.op("act", lambda e, pb=pb, fb=fb: e.activation(out=fmo[fb][0:64, :], in_=ps_fm[pb][0:64, :], func=AF.Silu),
                     r=[("ps_fm", pb)], w=[("fmo", fb, 0)])
                S.op("act", lambda e, pb=pb, fb=fb: e.activation(out=fmo[fb][64:128, :], in_=ps_fm[pb][64:128, :],
                                                               func=AF.Sigmoid, scale=-1.0),
                     r=[("ps_fm", pb)], w=[("fmo", fb, 1)])
                wl = [("fmo", fb, 0), ("fmo", fb, 1)]
            elif ci in (1, 5):
                S.op("act", lambda e, pb=pb, fb=fb: e.activation(out=fmo[fb][:], in_=ps_fm[pb][:], func=AF.Silu),
                     r=[("ps_fm", pb)], w=[("fmo", fb, 0), ("fmo", fb, 1)])
                wl = [("fmo", fb, 0), ("fmo", fb, 1)]
            else:
                S.op("dve", lambda e, pb=pb, fb=fb, cw=cw: e.tensor_copy(out=fmo[fb][0:cw, :], in_=ps_fm[pb][0:cw, :]),
                     r=[("ps_fm", pb)], w=[("fmo", fb, 0), ("fmo", fb, 1)])
                wl = [("fmo", fb, 0), ("fmo", fb, 1)]
            S.op("sp", lambda e, fb=fb, c0=c0, cw=cw, t0=t0: e.dma_start(out=fm[c0:c0 + cw, t0:t0 + 512], in_=fmo[fb][0:cw, :]),
                 r=wl, dma=True)
        for pb in range(0 if "NOTM" in DBG else 2):
            for t4 in (2 * pb, 2 * pb + 1):
                off = (t4 % 2) * 256
                for k in range(8):
                    S.op("pe", lambda e, k=k, t4=t4, pb=pb, off=off, b=b: e.matmul(
                        ps_tm[pb][:, off:off + 256], xn[b][:, k, t4 * 128:(t4 + 1) * 128], wall[:, k, NFM:NFM + NTM],
                        start=(k == 0), stop=(k == 7)),
                        r=xnr + wall_r, w=[("ps_tm", pb)])
            if "TMNOEVAC" in DBG:
                continue
            for t4 in (2 * pb, 2 * pb + 1):
                off = (t4 % 2) * 256
                if "TMNOACT" not in DBG:
                  S.op("act", lambda e, pb=pb, b=b, t4=t4, off=off: e.activation(
                    out=tsf[b][:, t4, :], in_=ps_tm[pb][:, off:off + 64], func=AF.Sigmoid),
                    w=[("ps_tm", pb), ("tsf", b, t4)])
                if "TMNODVE" not in DBG:
                  S.op("dve", lambda e, pb=pb, b=b, t4=t4, off=off: e.tensor_copy(
                    out=tv[b][:, t4, :], in_=ps_tm[pb][:, off + 64:off + 256]),
                    w=[("ps_tm", pb), ("tv", b, t4)])
        if "TMNODMA" in DBG:
            continue
        S.op("sp", lambda e, b=b, t0=t0: e.dma_start(
            out=tm_sf[t0:t0 + 512, :].rearrange("(a p) c -> p a c", p=128), in_=tsf[b][:]),
            r=[("tsf", b, t4) for t4 in range(4)], dma=True)
        S.op("sp", lambda e, b=b, t0=t0: e.dma_start(
            out=tm_v[t0:t0 + 512, :].rearrange("(a p) c -> p a c", p=128), in_=tv[b][:]),
            r=[("tv", b, t4) for t4 in range(4)], dma=True)


def core_cols(j):
    r = lambda s, n: list(range(s, s + n))
    fmc = (r(0 + 64 * j, 64) + r(256 + 64 * j, 64) + r(768 + 64 * j, 64) + r(1280 + 64 * j, 64) + r(1024 + 64 * j, 64)
           + r(1536 + 128 * j, 128) + r(2048 + 128 * j, 128) + r(3072 + 128 * j, 128))
    tmc = r(256 + 64 * j, 64) + r(512 + 64 * j, 64) + r(2560 + 128 * j, 128)
    return np.array(fmc + tmc)


def build_inproj(T=SEQ):
    nc = bass.Bass("TRN2", target_bir_lowering=False)
    hT = nc.dram_tensor("hT", [D_MODEL, T], F32, kind="ExternalInput").ap()
    wcat = nc.dram_tensor("wcat", [D_MODEL, NFM + NTM], F32, kind="ExternalInput").ap()
    nw = nc.dram_tensor("nw", [128, 8], F32, kind="ExternalInput").ap()
    fm = nc.dram_tensor("fm", [NFM, T], BF16, kind="ExternalOutput").ap()
    tm_sf = nc.dram_tensor("tm_sf", [T, 64], F32, kind="ExternalOutput").ap()
    tm_v = nc.dram_tensor("tm_v", [T, 192], BF16, kind="ExternalOutput").ap()
    with contextlib.ExitStack() as st:
        S = Sched(nc)
        phase_inproj(nc, S, st, hT, wcat, nw, fm, tm_sf, tm_v, T)
        S.emit()
    return nc


C1_2PI = 6.28125
C2_2PI = 2.0 * math.pi - 6.28125


def sincos(S, ang, kf, ki, hs, sin_out, cos_out, tag, eng="dve"):
    a, k, h = tag + "ang", tag + "kf", tag + "hs"
    S.op(eng, lambda e: e.tensor_scalar(out=kf, in0=ang, scalar1=1.0 / (2.0 * math.pi), scalar2=None, op0=ALU.mult), r=[a], w=[k])
    S.op(eng, lambda e: e.tensor_copy(out=ki, in_=kf), r=[k], w=[tag + "ki"])
    S.op(eng, lambda e: e.tensor_copy(out=kf, in_=ki), r=[tag + "ki"], w=[k])
    S.op("dve", lambda e: e.scalar_tensor_tensor(out=ang, in0=kf, scalar=-C1_2PI, in1=ang, op0=ALU.mult, op1=ALU.add), r=[k], w=[a])
    S.op("dve", lambda e: e.scalar_tensor_tensor(out=ang, in0=kf, scalar=-C2_2PI, in1=ang, op0=ALU.mult, op1=ALU.add), r=[k], w=[a])
    S.op(eng, lambda e: e.tensor_scalar(out=ang, in0=ang, scalar1=math.pi, scalar2=-math.pi, op0=ALU.min, op1=ALU.max), w=[a])
    S.op("act", lambda e: e.activation(out=sin_out, in_=ang, func=AF.Sin), r=[a], w=[tag + "sin"])
    S.op("act", lambda e: e.activation(out=hs, in_=ang, func=AF.Sin, scale=0.5), r=[a], w=[h])
    S.op(eng, lambda e: e.tensor_tensor(out=hs, in0=hs, in1=hs, op=ALU.mult), w=[h])
    S.op(eng, lambda e: e.tensor_scalar(out=cos_out, in0=hs, scalar1=-2.0, scalar2=1.0, op0=ALU.mult, op1=ALU.add), r=[h], w=[tag + "cos"])


def rope_tables(nc, S, st, ropef, sinT, cosT, T, tag):
    TS = lambda n, s, d: st.enter_context(nc.sbuf_tensor(n, s, d))
    CH = min(512, T)
    NCH = T // CH
    pi_ = TS(tag + "_pi", [128, CH], I32)
    ang = TS(tag + "_ang", [128, CH], F32)
    kf = TS(tag + "_kf", [128, CH], F32)
    ki = TS(tag + "_ki", [128, CH], I32)
    hs = TS(tag + "_hs", [128, CH], F32)
    s1 = TS(tag + "_s1", [128, CH], F32)
    c1 = TS(tag + "_c1", [128, CH], F32)
    pj = TS(tag + "_pj", [128, NCH], I32)
    ang_b = TS(tag + "_angb", [128, NCH], F32)
    kf_b = TS(tag + "_kfb", [128, NCH], F32)
    ki_b = TS(tag + "_kib", [128, NCH], I32)
    hs_b = TS(tag + "_hsb", [128, NCH], F32)
    s2 = TS(tag + "_s2", [128, NCH], F32)
    c2 = TS(tag + "_c2", [128, NCH], F32)
    ns2 = TS(tag + "_ns2", [128, NCH], F32)
    tmp = [TS(tag + f"_tmp{i}", [128, CH], F32) for i in range(4)]
    S.op("pool", lambda e: e.iota(pi_[:], pattern=[[1, CH]], base=0, channel_multiplier=0), w=[tag + "pi"])
    S.op("pool", lambda e: e.iota(pj[:], pattern=[[CH, NCH]], base=0, channel_multiplier=0), w=[tag + "pj"])
    S.op("dve", lambda e: e.tensor_copy(out=ang[:], in_=pi_[:]), r=[tag + "pi"], w=[tag + "aang"])
    S.op("dve", lambda e: e.tensor_scalar(out=ang[:], in0=ang[:], scalar1=ropef[:, 0:1], scalar2=None, op0=ALU.mult), r=["ropef"], w=[tag + "aang"])
    sincos(S, ang[:], kf[:], ki[:], hs[:], s1[:], c1[:], tag + "a")
    S.op("dve", lambda e: e.tensor_copy(out=ang_b[:], in_=pj[:]), r=[tag + "pj"], w=[tag + "bang"])
    S.op("dve", lambda e: e.tensor_scalar(out=ang_b[:], in0=ang_b[:], scalar1=ropef[:, 0:1], scalar2=None, op0=ALU.mult), r=["ropef"], w=[tag + "bang"])
    sincos(S, ang_b[:], kf_b[:], ki_b[:], hs_b[:], s2[:], c2[:], tag + "b")
    S.op("dve", lambda e: e.tensor_scalar(out=ns2[:], in0=s2[:], scalar1=-1.0, scalar2=None, op0=ALU.mult), r=[tag + "bsin"], w=[tag + "ns2"])
    rd = [tag + "asin", tag + "acos", tag + "bsin", tag + "bcos", tag + "ns2"]
    for c in range(NCH):
        sl = slice(c * CH, (c + 1) * CH)
        ta, tb = tmp[(2 * c) % 4], tmp[(2 * c + 1) % 4]
        na, nb = (tag + "tmp", (2 * c) % 4), (tag + "tmp", (2 * c + 1) % 4)
        S.op("dve", lambda e, c=c, ta=ta: e.tensor_scalar(out=ta[:], in0=c1[:], scalar1=s2[:, c:c + 1], scalar2=None, op0=ALU.mult), r=rd, w=[na])
        S.op("dve", lambda e, c=c, ta=ta, sl=sl: e.scalar_tensor_tensor(out=sinT[:, sl], in0=s1[:], scalar=c2[:, c:c + 1], in1=ta[:],
                                                                        op0=ALU.mult, op1=ALU.add), r=rd + [na], w=[(tag + "sin", c)])
        S.op("dve", lambda e, c=c, tb=tb: e.tensor_scalar(out=tb[:], in0=s1[:], scalar1=ns2[:, c:c + 1], scalar2=None, op0=ALU.mult), r=rd, w=[nb])
        S.op("dve", lambda e, c=c, tb=tb, sl=sl: e.scalar_tensor_tensor(out=cosT[:, sl], in0=c1[:], scalar=c2[:, c:c + 1], in1=tb[:],
                                                                        op0=ALU.mult, op1=ALU.add), r=rd + [nb], w=[(tag + "cos", c)])
    return [(tag + "sin", c) for c in range(NCH)] + [(tag + "cos", c) for c in range(NCH)]


def phase_attn(nc, S, st, fm, tm_v, lqk, subln, ropef_d, rmat_d, cmask_d, oa, lambda_init, T=SEQ):
    TS = lambda n, s, d: st.enter_context(nc.sbuf_tensor(n, s, d))
    PS = lambda n: st.enter_context(nc.psum_tensor(n, [128, 512], F32))
    NQ = T // 512
    NK = T // 128
    ropef = TS("at_ropef", [128, 1], F32)
    rm32 = TS("at_rm32", [128, 128], F32)
    rm = TS("at_rm", [128, 128], BF16)
    cmask = TS("at_cmask", [128, 4, 512], BF16)
    ones = TS("at_ones", [128, 128], BF16)
    sinT = TS("at_sin", [128, T], BF16)
    cosT = TS("at_cos", [128, T], BF16)
    qraw = TS("at_qraw", [128, T], BF16)
    kraw = TS("at_kraw", [128, T], BF16)
    qr = TS("at_qr", [128, T], BF16)
    kr = TS("at_kr", [128, T], BF16)
    sag = TS("at_sag", [128, T], BF16)
    vsb = TS("at_v", [128, NK, 128], BF16)
    lq = TS("at_lq", [128, 256], F32)
    lp = TS("at_lp", [128, 128], F32)
    le = TS("at_le", [128, 2], F32)
    neglam = TS("at_neglam", [128, 1], F32)
    sw = TS("at_sw", [128, 1], F32)
    t1 = [TS(f"at_t1_{i}", [128, 512], BF16) for i in range(2)]
    t2 = [TS(f"at_t2_{i}", [128, 512], BF16) for i in range(2)]
    P = [[TS(f"at_P{i}_{m}", [128, 512], BF16) for m in range(2)] for i in range(3)]
    r0 = [TS(f"at_r0_{i}", [128, 512], F32) for i in range(2)]
    r1 = [TS(f"at_r1_{i}", [128, 512], F32) for i in range(2)]
    o0 = [TS(f"at_o0_{i}", [128, 512], F32) for i in range(2)]
    o1 = [TS(f"at_o1_{i}", [128, 512], F32) for i in range(2)]
    osq = TS("at_osq", [128, 512], BF16)
    nsq = TS("at_nsq", [128, 512], F32)
    ob = [TS(f"at_ob{i}", [128, 512], BF16) for i in range(2)]
    epsc = TS("at_epsc", [128, 1], F32)
    accD = [TS(f"at_accD{i}", [128, 512], F32) for i in range(2)]
    accP = [TS(f"at_accP{i}", [128, 512], F32) for i in range(2)]
    ones32 = TS("at_ones32", [128, 128], F32)
    ps_s = [[PS(f"at_ps_s{i}_{m}") for m in range(2)] for i in range(2)]
    ps_o = [PS(f"at_ps_o{m}") for m in range(2)]
    ps_l = [PS(f"at_ps_l{m}") for m in range(2)]
    ps_n = ps_s[0][0]

    S.op("sp", lambda e: e.dma_start(out=ropef[:], in_=ropef_d[:, :]), w=["ropef"], dma=True)
    S.op("sp", lambda e: e.dma_start(out=rm32[:], in_=rmat_d[:, :]), w=["rm32"], dma=True)
    S.op("sp", lambda e: e.dma_start(out=cmask[:], in_=cmask_d.rearrange("d p q -> p d q")), w=["cmask"], dma=True)
    S.op("sp", lambda e: e.dma_start(out=lq[:], in_=lqk.partition_broadcast(128)), w=["lq"], dma=True)
    S.op("sp", lambda e: e.dma_start(out=sw[:], in_=subln[:, :]), w=["sw"], dma=True)
    S.op("sp", lambda e: e.dma_start(out=qraw[:], in_=fm[320:448, :]), w=["qraw"], dma=True)
    S.op("sp", lambda e: e.dma_start(out=kraw[:], in_=fm[448:576, :]), w=["kraw"], dma=True)
    S.op("sp", lambda e: e.dma_start(out=sag[:], in_=fm[576:704, :]), w=["sag"], dma=True)
    S.op("pool", lambda e: e.dma_start(out=vsb[:], in_=tm_v[:, 64:192].rearrange("(a p) c -> p a c", p=128)), w=["vsb"], dma=True)
    S.op("pool", lambda e: e.memset(ones[:], 1.0), w=["ones"])
    S.op("pool", lambda e: e.memset(epsc[:], EPS), w=["epsc"])
    S.op("pool", lambda e: e.memset(ones32[:], 1.0), w=["ones32"])
    S.op("dve", lambda e: e.tensor_copy(out=rm[:], in_=rm32[:]), r=["rm32"], w=["rm"])
    S.op("dve", lambda e: e.tensor_tensor(out=lp[:], in0=lq[:, 0:128], in1=lq[:, 128:256], op=ALU.mult), r=["lq"], w=["lp"])
    S.op("dve", lambda e: e.tensor_reduce(out=le[:], in_=lp[:].rearrange("p (a c) -> p a c", a=2), axis=AX.X, op=ALU.add),
         r=["lp"], w=["le"])
    S.op("act", lambda e: e.activation(out=le[:], in_=le[:], func=AF.Exp), w=["le"])
    S.op("dve", lambda e: e.tensor_tensor(out=neglam[:], in0=le[:, 1:2], in1=le[:, 0:1], op=ALU.subtract), r=["le"], w=["neglam"])
    S.op("dve", lambda e: e.tensor_scalar(out=neglam[:], in0=neglam[:], scalar1=-lambda_init, scalar2=None, op0=ALU.add), w=["neglam"])
    S.op("dve", lambda e: e.tensor_scalar(out=sw[:], in0=sw[:], scalar1=1.0 - lambda_init, scalar2=None, op0=ALU.mult), w=["sw"])
    tabs = rope_tables(nc, S, st, ropef, sinT, cosT, T, "at_rp")
    ri = 0
    for src, dst, sn, dn in ((qraw, qr, "qraw", "qr"), (kraw, kr, "kraw", "kr")):
        for ti in range(NQ):
            sl = slice(ti * 512, (ti + 1) * 512)
            b = ri % 2
            ri += 1
            S.op("pe", lambda e, src=src, sl=sl, b=b: e.matmul(ps_s[b][0][:], rm[:], src[:, sl], start=True, stop=True),
                 r=[sn, "rm"], w=[f"ps_s{b}0"])
            S.op("dve", lambda e, src=src, sl=sl, b=b: e.tensor_tensor(out=t1[b][:], in0=src[:, sl], in1=cosT[:, sl], op=ALU.mult),
                 r=[sn] + tabs, w=[("t1", b)])
            S.op("dve", lambda e, sl=sl, b=b: e.tensor_tensor(out=t2[b][:], in0=ps_s[b][0][:], in1=sinT[:, sl], op=ALU.mult),
                 r=tabs, w=[("t2", b), f"ps_s{b}0"])
            S.op("pool", lambda e, dst=dst, sl=sl, b=b: e.tensor_tensor(out=dst[:, sl], in0=t1[b][:], in1=t2[b][:], op=ALU.add),
                 r=[("t1", b), ("t2", b)], w=[(dn, ti)])
    qr_all = [("qr", ti) for ti in range(NQ)]
    kr_all = [("kr", ti) for ti in range(NQ)]
    psn = lambda i, m: f"ps_s{i}{m}"
    dq = []

    def pop_deferred(n, limit=2):
        k = 0
        while dq and k < limit:
            fn, need_odd = dq[0]
            if need_odd and n % 2 == 0:
                break
            dq.pop(0)
            fn()
            k += 1
    for qi in range(NQ):
        qs = slice(qi * 512, (qi + 1) * 512)
        nk = 4 * (qi + 1)
        ab = qi % 2

        def c0_of(n, qi=qi):
            return 128 * max(0, n - 4 * qi)

        def QK(n, qs=qs, qi=qi):
            i = n % 2
            c0 = c0_of(n)
            for m in range(2):
                S.op("pe", lambda e, n=n, i=i, m=m, qs=qs, c0=c0: e.matmul(
                    ps_s[i][m][:, c0:512], kr[64 * m:64 * m + 64, n * 128:(n + 1) * 128],
                    qr[64 * m:64 * m + 64, qs.start + c0:qs.stop], start=True, stop=True),
                     r=[("qr", qi), ("kr", n // 4)], w=[psn(i, m)])

        def EXP(n, qi=qi):
            i = n % 2
            j = n % 3
            c0 = c0_of(n)
            for m in range(2):
                S.op("act", lambda e, i=i, j=j, m=m, c0=c0: e.activation(out=P[j][m][:, c0:512], in_=ps_s[i][m][:, c0:512], func=AF.Exp, scale=0.125),
                     w=[psn(i, m), ("P", j, m)])
                d = n - 4 * qi
                if d >= 0:
                    S.op("dve", lambda e, j=j, m=m, d=d, c0=c0: e.tensor_tensor(out=P[j][m][:, c0:512], in0=P[j][m][:, c0:512],
                                                                             in1=cmask[:, d, c0:512], op=ALU.mult),
                         r=["cmask"], w=[("P", j, m)])

        def PV(n, nk=nk, ab=ab):
            j = n % 3
            c0 = c0_of(n)
            for m in range(2):
                S.op("pe", lambda e, n=n, j=j, m=m, nk=nk, c0=c0: e.matmul(ps_o[m][:, c0:512], vsb[:, n, :], P[j][m][:, c0:512],
                                                                       start=(n == 0), stop=(n == nk - 1)),
                     r=[("P", j, m), "vsb"], w=[f"ps_o{m}"])
            S.op("pe", lambda e, n=n, j=j, nk=nk, c0=c0: e.matmul(ps_l[0][:, c0:512], ones[:], P[j][0][:, c0:512], start=(n == 0), stop=(n == nk - 1)),
                 r=[("P", j, 0), "ones"], w=["ps_l0"])
            eng, acc, an = ("dve", accD[ab], ("accD", ab)) if n % 2 == 0 else ("pool", accP[ab], ("accP", ab))
            if n < 2:
                if c0 > 0:
                    S.op(eng, lambda e, acc=acc, c0=c0: e.memset(acc[:, 0:c0], 0.0), w=[an])
                S.op(eng, lambda e, j=j, acc=acc, c0=c0: e.tensor_copy(out=acc[:, c0:512], in_=P[j][1][:, c0:512]), r=[("P", j, 1)], w=[an])
            else:
                S.op(eng, lambda e, j=j, acc=acc, c0=c0: e.tensor_tensor(out=acc[:, c0:512], in0=acc[:, c0:512], in1=P[j][1][:, c0:512], op=ALU.add),
                     r=[("P", j, 1)], w=[an])
        QK(0)
        for n in range(nk):
            if n >= 1:
                pop_deferred(n)
            if n + 1 < nk:
                QK(n + 1)
            EXP(n)
            PV(n)
        while dq:
            fn, need_odd = dq.pop(0)
            fn()
        eb = qi % 2
        S.op("dve", lambda e, eb=eb: e.tensor_copy(out=o0[eb][:], in_=ps_o[0][:]), w=[("o0", eb), "ps_o0"])
        S.op("dve", lambda e, eb=eb: e.tensor_copy(out=o1[eb][:], in_=ps_o[1][:]), w=[("o1", eb), "ps_o1"])
        S.op("dve", lambda e, eb=eb: e.tensor_copy(out=r0[eb][:], in_=ps_l[0][:]), w=[("r0", eb), "ps_l0"])
        D = lambda fn, odd=False: dq.append((fn, odd))
        D(lambda ab=ab: S.op("pe", lambda e: e.matmul(ps_l[1][:], ones32[:], accD[ab][:], start=True, stop=False),
                             r=[("accD", ab), "ones32"], w=["ps_l1"]))
        D(lambda ab=ab: S.op("pe", lambda e: e.matmul(ps_l[1][:], ones32[:], accP[ab][:], start=False, stop=True),
                             r=[("accP", ab), "ones32"], w=["ps_l1"]))
        D(lambda eb=eb: S.op("dve", lambda e: e.tensor_copy(out=r1[eb][:], in_=ps_l[1][:]), w=[("r1", eb), "ps_l1"]))
        for rr, rn in ((r0, "r0"), (r1, "r1")):
            D(lambda eb=eb, rr=rr, rn=rn: S.op("act", lambda e: e.activation(out=rr[eb][:], in_=rr[eb][:], func=AF.Ln), w=[(rn, eb)]))
            D(lambda eb=eb, rr=rr, rn=rn: S.op("act", lambda e: e.activation(out=rr[eb][:], in_=rr[eb][:], func=AF.Exp, scale=-1.0), w=[(rn, eb)]))
        D(lambda eb=eb: S.op("dve", lambda e: e.tensor_tensor(out=o0[eb][:], in0=o0[eb][:], in1=r0[eb][:], op=ALU.mult), r=[("r0", eb)], w=[("o0", eb)]))
        D(lambda eb=eb: S.op("dve", lambda e: e.tensor_tensor(out=o1[eb][:], in0=o1[eb][:], in1=r1[eb][:], op=ALU.mult), r=[("r1", eb)], w=[("o1", eb)]))
        D(lambda eb=eb: S.op("dve", lambda e: e.scalar_tensor_tensor(out=o0[eb][:], in0=o1[eb][:], scalar=neglam[:, 0:1], in1=o0[eb][:],
                                                                     op0=ALU.mult, op1=ALU.add), r=[("o1", eb), "neglam"], w=[("o0", eb)]))
        D(lambda eb=eb: S.op("act", lambda e: e.activation(out=osq[:], in_=o0[eb][:], func=AF.Square), r=[("o0", eb)], w=["osq"]))
        D(lambda: S.op("pe", lambda e: e.matmul(ps_n[:], ones[:], osq[:], start=True, stop=True), r=["osq", "ones"], w=[psn(0, 0)]), True)
        D(lambda: S.op("act", lambda e: e.activation(out=nsq[:], in_=ps_n[:], func=AF.Ln, scale=1.0 / 128.0, bias=epsc[:, 0:1]),
                       r=["epsc"], w=["nsq", psn(0, 0)]))
        D(lambda: S.op("act", lambda e: e.activation(out=nsq[:], in_=nsq[:], func=AF.Exp, scale=-0.5), w=["nsq"]))
        D(lambda eb=eb: S.op("dve", lambda e: e.scalar_tensor_tensor(out=o0[eb][:], in0=o0[eb][:], scalar=sw[:, 0:1], in1=nsq[:],
                                                                     op0=ALU.mult, op1=ALU.mult), r=["nsq", "sw"], w=[("o0", eb)]))
        D(lambda eb=eb, qs=qs: S.op("pool", lambda e: e.tensor_tensor(out=ob[eb][:], in0=o0[eb][:], in1=sag[:, qs], op=ALU.mult),
                                    r=[("o0", eb), "sag"], w=[("ob", eb)]))
        D(lambda eb=eb, qs=qs: S.op("sp", lambda e: e.dma_start(out=oa[:, qs], in_=ob[eb][:]), r=[("ob", eb)], dma=True))
    while dq:
        fn, need_odd = dq.pop(0)
        fn()


def attn_consts():
    ropef = np.zeros((128, 1), np.float32)
    inv = (ROPE_THETA ** (-np.arange(0, 16, 2, dtype=np.float32) / 16.0)).astype(np.float32)
    rmat = np.zeros((128, 128), np.float32)
    for base in (0, 64):
        for i in range(8):
            ropef[base + i, 0] = -inv[i]
            ropef[base + 8 + i, 0] = inv[i]
            rmat[base + 8 + i, base + i] = 1.0
            rmat[base + i, base + 8 + i] = 1.0
    k = np.arange(128)[:, None]
    q = np.arange(512)[None, :]
    cmask = np.stack([(128 * d + k <= q) for d in range(4)]).astype(ml_dtypes.bfloat16)
    return ropef, rmat, cmask


def build_attn(lambda_init, T=SEQ):
    nc = bass.Bass("TRN2", target_bir_lowering=False)
    fm = nc.dram_tensor("fm", [NFM, T], BF16, kind="ExternalInput").ap()
    tm_v = nc.dram_tensor("tm_v", [T, 192], BF16, kind="ExternalInput").ap()
    lqk = nc.dram_tensor("lqk", [1, 256], F32, kind="ExternalInput").ap()
    subln = nc.dram_tensor("subln", [128, 1], F32, kind="ExternalInput").ap()
    ropef = nc.dram_tensor("ropef", [128, 1], F32, kind="ExternalInput").ap()
    rmat = nc.dram_tensor("rmat", [128, 128], F32, kind="ExternalInput").ap()
    cmask = nc.dram_tensor("cmask", [4, 128, 512], BF16, kind="ExternalInput").ap()
    oa = nc.dram_tensor("oa", [128, T], BF16, kind="ExternalOutput").ap()
    with contextlib.ExitStack() as st:
        S = Sched(nc)
        phase_attn(nc, S, st, fm, tm_v, lqk, subln, ropef, rmat, cmask, oa, lambda_init, T)
        S.emit()
    return nc


def hgrn_consts():
    s = np.arange(128)[:, None]
    t = np.arange(128)[None, :]
    same = (s // 16) == (t // 16)
    m_incl = (same & (s <= t)).astype(np.float32)
    m_rev = (same & (s > t)).astype(np.float32)
    m_tot8 = ((s // 16) == np.arange(8)[None, :]).astype(np.float32)
    mcat = np.concatenate([m_incl, m_tot8], axis=1)
    return mcat, m_rev


def phase_hgrn(nc, S, st, fm, tm_sf, tm_v, lbl_bc_d, lbl_col_d, gw_d, mcat_d, mrev_d, oh, lb_coef, T=SEQ):
    TS = lambda n, s, d: st.enter_context(nc.sbuf_tensor(n, s, d))
    PS = lambda n: st.enter_context(nc.psum_tensor(n, [128, 512], F32))
    NT = T // 128
    sq = TS("hg_sq", [64, T], BF16)
    snf = TS("hg_snf", [64, T], BF16)
    shg = TS("hg_shg", [64, T], BF16)
    sf = TS("hg_sf", [128, NT, 64], F32)
    omf = TS("hg_omf", [128, NT, 64], F32)
    logf = TS("hg_logf", [128, NT, 64], F32)
    vall = TS("hg_v", [128, NT, 64], BF16)
    lbl = TS("hg_lbl", [128, 128], F32)
    lb_bc = TS("hg_lb_bc", [128, 64], F32)
    oml_bc = TS("hg_oml_bc", [128, 64], F32)
    lblc = TS("hg_lblc", [64, 2], F32)
    oml_col = TS("hg_oml_col", [64, 1], F32)
    gw = TS("hg_gw", [64, 1], F32)
    mcat = TS("hg_mcat", [128, 136], F32)
    mrev = TS("hg_mrev", [128, 128], F32)
    mincl_bf = TS("hg_mincl", [128, 128], BF16)
    mtot_bf = TS("hg_mtot", [128, 8], BF16)
    ones64 = TS("hg_ones", [64, 64], BF16)
    epsc = TS("hg_epsc", [64, 1], F32)
    eq = TS("hg_eq", [64, 128], F32)
    ekn = TS("hg_ekn", [64, 128], F32)
    dec = [TS(f"hg_dec{i}", [64, 8], F32) for i in range(3)]
    ehat = TS("hg_ehat", [128, 64], F32)
    qt = [TS(f"hg_qt{i}", [64, 128], BF16) for i in range(3)]
    kt = TS("hg_kt", [64, 128], BF16)
    khat = TS("hg_khat", [128, 64], BF16)
    vblk = TS("hg_vblk", [128, 8, 64], BF16)
    scm = [TS(f"hg_scm{i}", [128, 128], BF16) for i in range(3)]
    Sall = [TS(f"hg_S{i}", [64, 9, 64], F32) for i in range(2)]
    Sbf = [TS(f"hg_Sbf{i}", [64, 8, 64], BF16) for i in range(2)]
    osq = TS("hg_osq", [64, 512], BF16)
    o32 = TS("hg_o32", [64, 512], F32)
    nsq = TS("hg_nsq", [64, 512], F32)
    ohb = [TS(f"hg_ohb{i}", [64, 512], BF16) for i in range(2)]
    ps_c = PS("hg_ps_c")
    ps_r = PS("hg_ps_r")
    ps_sc = PS("hg_ps_sc")
    ps_u = [PS(f"hg_ps_u{i}") for i in range(3)]
    ps_oh = [PS(f"hg_ps_oh{i}") for i in range(2)]

    S.op("sp", lambda e: e.dma_start(out=sq[:], in_=fm[0:64, :]), w=["sq"], dma=True)
    S.op("sp", lambda e: e.dma_start(out=snf[:], in_=fm[64:128, :]), w=["snf"], dma=True)
    S.op("sp", lambda e: e.dma_start(out=shg[:], in_=fm[128:192, :]), w=["shg"], dma=True)
    S.op("sp", lambda e: e.dma_start(out=sf[:], in_=tm_sf.rearrange("(a p) c -> p a c", p=128)), w=["sf"], dma=True)
    S.op("pool", lambda e: e.dma_start(out=vall[:], in_=tm_v[:, 0:64].rearrange("(a p) c -> p a c", p=128)), w=["vall"], dma=True)
    S.op("sp", lambda e: e.dma_start(out=lbl[:], in_=lbl_bc_d.partition_broadcast(128)), w=["lbl"], dma=True)
    S.op("sp", lambda e: e.dma_start(out=lblc[:], in_=lbl_col_d[:, :]), w=["lblc"], dma=True)
    S.op("sp", lambda e: e.dma_start(out=gw[:], in_=gw_d[:, :]), w=["gw"], dma=True)
    S.op("sp", lambda e: e.dma_start(out=mcat[:], in_=mcat_d[:, :]), w=["mcat"], dma=True)
    S.op("sp", lambda e: e.dma_start(out=mrev[:], in_=mrev_d[:, :]), w=["mrev"], dma=True)
    S.op("pool", lambda e: e.memset(ones64[:], 1.0), w=["ones64"])
    S.op("pool", lambda e: e.memset(epsc[:], EPS), w=["epsc"])
    S.op("pool", lambda e: e.memset(Sall[0][:, 0, :], 0.0), w=[("S", 0)])
    S.op("dve", lambda e: e.tensor_copy(out=mincl_bf[:], in_=mcat[:, 0:128]), r=["mcat"], w=["mincl_bf"])
    S.op("dve", lambda e: e.tensor_copy(out=mtot_bf[:], in_=mcat[:, 128:136]), r=["mcat"], w=["mtot_bf"])
    S.op("dve", lambda e: e.tensor_tensor(out=lb_bc[:], in0=lbl[:, 64:128], in1=lbl[:, 0:64], op=ALU.subtract), r=["lbl"], w=["lb_bc"])
    S.op("act", lambda e: e.activation(out=lb_bc[:], in_=lb_bc[:], func=AF.Sigmoid), w=["lb_bc"])
    S.op("dve", lambda e: e.tensor_scalar(out=lb_bc[:], in0=lb_bc[:], scalar1=float(lb_coef), scalar2=None, op0=ALU.mult), w=["lb_bc"])
    S.op("dve", lambda e: e.tensor_scalar(out=oml_bc[:], in0=lb_bc[:], scalar1=-1.0, scalar2=1.0, op0=ALU.mult, op1=ALU.add),
         r=["lb_bc"], w=["oml_bc"])
    S.op("dve", lambda e: e.tensor_tensor(out=oml_col[:], in0=lblc[:, 1:2], in1=lblc[:, 0:1], op=ALU.subtract), r=["lblc"], w=["oml_col"])
    S.op("act", lambda e: e.activation(out=oml_col[:], in_=oml_col[:], func=AF.Sigmoid), w=["oml_col"])
    S.op("dve", lambda e: e.tensor_scalar(out=oml_col[:], in0=oml_col[:], scalar1=-float(lb_coef), scalar2=1.0, op0=ALU.mult, op1=ALU.add),
         w=["oml_col"])
    for g in range(NT // 8 if NT >= 8 else 1):
        nt = min(8, NT)
        sl = slice(g * 8, g * 8 + nt)
        S.op("dve", lambda e, sl=sl, nt=nt: e.tensor_tensor(out=sf[:, sl, :], in0=sf[:, sl, :],
                                                        in1=oml_bc[:].unsqueeze(1).broadcast_to([128, nt, 64]), op=ALU.mult),
             r=["oml_bc"], w=["sf"])
        S.op("dve", lambda e, sl=sl, nt=nt: e.tensor_tensor(out=sf[:, sl, :], in0=sf[:, sl, :],
                                                        in1=lb_bc[:].unsqueeze(1).broadcast_to([128, nt, 64]), op=ALU.add),
             r=["lb_bc"], w=["sf"])
        S.op("act", lambda e, sl=sl: e.activation(out=logf[:, sl, :], in_=sf[:, sl, :], func=AF.Ln), r=["sf"], w=[("logf", g)])
        S.op("pool", lambda e, sl=sl: e.tensor_scalar(out=omf[:, sl, :], in0=sf[:, sl, :], scalar1=-1.0, scalar2=1.0,
                                                     op0=ALU.mult, op1=ALU.add), r=["sf"], w=[("omf", g)])
    def stageA1(i):
        g = i // 8
        b = i % 3
        S.op("pe", lambda e, i=i: e.matmul(ps_c[0:64, 0:136], logf[:, i, :], mcat[:], start=True, stop=True),
             r=[("logf", g), "mcat"], w=["ps_c"])
        S.op("pe", lambda e, i=i: e.matmul(ps_r[:, 0:64], mrev[:], logf[:, i, :], start=True, stop=True),
             r=[("logf", g), "mrev"], w=["ps_r"])
        S.op("act", lambda e: e.activation(out=eq[:], in_=ps_c[0:64, 0:128], func=AF.Exp), w=["eq", "ps_c"])
        S.op("act", lambda e: e.activation(out=ekn[:], in_=ps_c[0:64, 0:128], func=AF.Exp, scale=-1.0), w=["ekn", "ps_c"])
        S.op("act", lambda e, b=b: e.activation(out=dec[b][:], in_=ps_c[0:64, 128:136], func=AF.Exp), w=[("dec", b), "ps_c"])
        S.op("act", lambda e: e.activation(out=ehat[:], in_=ps_r[:, 0:64], func=AF.Exp), w=["ehat", "ps_r"])

    def stageA2(i):
        g = i // 8
        b = i % 3
        ts = slice(i * 128, (i + 1) * 128)
        S.op("pool", lambda e, b=b, ts=ts: e.tensor_tensor(out=qt[b][:], in0=sq[:, ts], in1=eq[:], op=ALU.mult),
             r=["sq", "eq"], w=[("qt", b)])
        S.op("dve", lambda e, ts=ts: e.scalar_tensor_tensor(out=kt[:], in0=snf[:, ts], scalar=oml_col[:, 0:1], in1=ekn[:],
                                                           op0=ALU.mult, op1=ALU.mult), r=["snf", "ekn", "oml_col"], w=["kt"])
        S.op("pool", lambda e, i=i: e.tensor_tensor(out=khat[:], in0=omf[:, i, :], in1=ehat[:], op=ALU.mult),
             r=[("omf", g), "ehat"], w=["khat"])
        S.op("pool", lambda e, i=i: e.tensor_tensor(out=vblk[:], in0=vall[:, i, :].unsqueeze(1).broadcast_to([128, 8, 64]),
                                                   in1=mtot_bf[:].unsqueeze(2).broadcast_to([128, 8, 64]), op=ALU.mult),
             r=["vall", "mtot_bf"], w=["vblk"])
        S.op("pe", lambda e, b=b: e.matmul(ps_sc[:, 0:128], kt[:], qt[b][:], start=True, stop=True), r=["kt", ("qt", b)], w=["ps_sc"])
        S.op("dve", lambda e, b=b: e.tensor_tensor(out=scm[b][:], in0=ps_sc[:, 0:128], in1=mincl_bf[:], op=ALU.mult),
             r=["mincl_bf"], w=[("scm", b), "ps_sc"])
        S.op("pe", lambda e, b=b: e.matmul(ps_u[b][0:64, :], khat[:], vblk[:].rearrange("p a c -> p (a c)"), start=True, stop=True),
             r=["khat", "vblk"], w=[("ps_u", b)])

    def stageB(i):
        b = i % 3
        sb = i % 2
        if i > 0:
            S.op("dve", lambda e, sb=sb: e.tensor_copy(out=Sall[sb][:, 0, :], in_=Sall[1 - sb][:, 8, :]), r=[("S", 1 - sb)], w=[("S", sb)])
        for n in range(8):
            S.op("dve", lambda e, b=b, sb=sb, n=n: e.scalar_tensor_tensor(
                out=Sall[sb][:, n + 1, :], in0=Sall[sb][:, n, :], scalar=dec[b][:, n:n + 1], in1=ps_u[b][0:64, n * 64:(n + 1) * 64],
                op0=ALU.mult, op1=ALU.add), r=[("dec", b)], w=[("S", sb), ("ps_u", b)])
        S.op("act", lambda e, sb=sb: e.activation(out=Sbf[sb][:], in_=Sall[sb][:, 0:8, :], func=AF.Copy), r=[("S", sb)], w=[("Sbf", sb)])

    def stageC(i):
        b = i % 3
        sb = i % 2
        ob = (i // 4) % 2
        c0 = (i % 4) * 128
        S.op("pe", lambda e, i=i, ob=ob, c0=c0, b=b: e.matmul(ps_oh[ob][0:64, c0:c0 + 128], vall[:, i, :], scm[b][:], start=True, stop=False),
             r=["vall", ("scm", b)], w=[("ps_oh", ob)])
        for n in range(8):
            S.op("pe", lambda e, b=b, sb=sb, ob=ob, c0=c0, n=n: e.matmul(
                ps_oh[ob][0:64, c0 + 16 * n:c0 + 16 * n + 16], Sbf[sb][:, n, :], qt[b][:, 16 * n:16 * n + 16],
                start=False, stop=(n == 7)), r=[("Sbf", sb), ("qt", b)], w=[("ps_oh", ob)])
        if i % 4 == 3 or i == NT - 1:
            qs = slice((i // 4) * 512, (i // 4) * 512 + 512)
            S.op("act", lambda e, ob=ob: e.activation(out=osq[:], in_=ps_oh[ob][0:64, :], func=AF.Square), w=["osq", ("ps_oh", ob)])
            S.op("act", lambda e, ob=ob: e.activation(out=o32[:], in_=ps_oh[ob][0:64, :], func=AF.Copy), w=["o32", ("ps_oh", ob)])
            S.op("pe", lambda e: e.matmul(ps_sc[0:64, :], ones64[:], osq[:], start=True, stop=True), r=["osq", "ones64"], w=["ps_sc"])
            S.op("act", lambda e: e.activation(out=nsq[:], in_=ps_sc[0:64, :], func=AF.Ln, scale=1.0 / 64.0, bias=epsc[:, 0:1]), r=["epsc"], w=["nsq", "ps_sc"])
            S.op("act", lambda e: e.activation(out=nsq[:], in_=nsq[:], func=AF.Exp, scale=-0.5), w=["nsq"])
            S.op("dve", lambda e: e.scalar_tensor_tensor(out=o32[:], in0=o32[:], scalar=gw[:, 0:1], in1=nsq[:], op0=ALU.mult, op1=ALU.mult),
                 r=["nsq", "gw"], w=["o32"])
            S.op("pool", lambda e, ob=ob, qs=qs: e.tensor_tensor(out=ohb[ob][:], in0=o32[:], in1=shg[:, qs], op=ALU.mult),
                 r=["o32", "shg"], w=[("ohb", ob)])
            S.op("sp", lambda e, ob=ob, qs=qs: e.dma_start(out=oh[:, qs], in_=ohb[ob][:]), r=[("ohb", ob)], dma=True)
    for i0_ in range(min(2, NT)):
        stageA1(i0_)
        stageA2(i0_)
    for i in range(NT):
        if i + 2 < NT:
            stageA1(i + 2)
        stageB(i)
        if i + 2 < NT:
            stageA2(i + 2)
        stageC(i)


def build_hgrn(lb_coef, T=SEQ):
    nc = bass.Bass("TRN2", target_bir_lowering=False)
    fm = nc.dram_tensor("fm", [NFM, T], BF16, kind="ExternalInput").ap()
    tm_sf = nc.dram_tensor("tm_sf", [T, 64], F32, kind="ExternalInput").ap()
    tm_v = nc.dram_tensor("tm_v", [T, 192], BF16, kind="ExternalInput").ap()
    lbl_bc = nc.dram_tensor("lbl_bc", [1, 128], F32, kind="ExternalInput").ap()
    lbl_col = nc.dram_tensor("lbl_col", [64, 2], F32, kind="ExternalInput").ap()
    gw = nc.dram_tensor("gw", [64, 1], F32, kind="ExternalInput").ap()
    mcat = nc.dram_tensor("mcat", [128, 136], F32, kind="ExternalInput").ap()
    mrev = nc.dram_tensor("mrev", [128, 128], F32, kind="ExternalInput").ap()
    oh = nc.dram_tensor("oh", [64, T], BF16, kind="ExternalOutput").ap()
    with contextlib.ExitStack() as st:
        S = Sched(nc)
        phase_hgrn(nc, S, st, fm, tm_sf, tm_v, lbl_bc, lbl_col, gw, mcat, mrev, oh, lb_coef, T)
        S.emit()
    return nc


def cmul(S, eng, o_re, o_im, a_re, a_im, b_re, b_im, t0, t1, rd, wr, conj_a=False):
    sg = -1.0 if conj_a else 1.0
    S.op(eng, lambda e: e.tensor_tensor(out=t0, in0=a_im, in1=b_im, op=ALU.mult), r=rd, w=[wr + "t0"])
    S.op(eng, lambda e: e.tensor_tensor(out=t1, in0=a_re, in1=b_re, op=ALU.mult), r=rd, w=[wr + "t1"])
    S.op("dve", lambda e: e.scalar_tensor_tensor(out=o_re, in0=t0, scalar=-sg, in1=t1, op0=ALU.mult, op1=ALU.add),
         r=[wr + "t0", wr + "t1"], w=[wr + "re"])
    S.op(eng, lambda e: e.tensor_tensor(out=t0, in0=a_im, in1=b_re, op=ALU.mult), r=rd + [wr + "re"], w=[wr + "t0"])
    S.op(eng, lambda e: e.tensor_tensor(out=t1, in0=a_re, in1=b_im, op=ALU.mult), r=rd + [wr + "re"], w=[wr + "t1"])
    S.op("dve", lambda e: e.scalar_tensor_tensor(out=o_im, in0=t0, scalar=sg, in1=t1, op0=ALU.mult, op1=ALU.add),
         r=[wr + "t0", wr + "t1"], w=[wr + "im"])


def s5_consts():
    negsig = np.repeat(-np.arange(16, dtype=np.float32), 64)[None, :]
    kidx = np.arange(32, dtype=np.float32)[None, :]
    midx = np.arange(1, 513, dtype=np.float32)[None, :]
    rowmask = (np.arange(64)[:, None] // 16 == np.arange(4)[None, :]).astype(np.float32)
    return negsig, kidx, midx, rowmask


def s5_params(z, l, j):
    gs = [4 * j + gl for gl in range(4)]
    f = np.float32
    pA_are = np.concatenate([np.repeat(z["s5_a_re"][l][g][None, :], 16, 0) for g in gs]).astype(f)
    pA_aim = np.concatenate([np.repeat(z["s5_a_im"][l][g][None, :], 16, 0) for g in gs]).astype(f)
    pA_ldt = np.concatenate([np.full((16, 1), z["s5_log_dt"][l][g]) for g in gs]).astype(f)
    pA_bre = np.concatenate([z["s5_b_re"][l][g].T for g in gs]).astype(f)
    pA_bim = np.concatenate([z["s5_b_im"][l][g].T for g in gs]).astype(f)
    pB = np.zeros((2, 128, 3), f)
    pB_cre = np.zeros((2, 128, 64), f)
    pB_cim = np.zeros((2, 128, 64), f)
    for q in range(2):
        for h in range(2):
            gl = 2 * q + h
            g = gs[gl]
            rows = slice(64 * h, 64 * h + 64)
            pB[q, rows, 0] = z["s5_a_re"][l][g]
            pB[q, rows, 1] = z["s5_a_im"][l][g]
            pB[q, rows, 2] = z["s5_log_dt"][l][g]
            pB_cre[q, rows, 16 * gl:16 * gl + 16] = z["s5_c_re"][l][g].T
            pB_cim[q, rows, 16 * gl:16 * gl + 16] = z["s5_c_im"][l][g].T
    dcol = z["s5_d"][l][64 * j:64 * j + 64][:, None].astype(f)
    pA = np.concatenate([pA_are, pA_aim, pA_bre, pA_bim, pA_ldt], axis=1)
    return {"s5p_pA": np.ascontiguousarray(pA), "s5p_pB": pB, "s5p_cre": pB_cre, "s5p_cim": pB_cim, "s5p_d": dcol}


def phase_s5(nc, S, st, fm, pA_d, pB_d, cre_d, cim_d, dcol_d, negsig_d, kidx_d, midx_d, rowmask_d, yg, T=SEQ):
    TS = lambda n, s, d: st.enter_context(nc.sbuf_tensor(n, s, d))
    PS = lambda n: st.enter_context(nc.psum_tensor(n, [128, 512], F32))
    NB = T // 16
    su = TS("s5_su", [64, T], BF16)
    outsb = TS("s5_out", [64, T], BF16)
    pA = TS("s5_pA", [64, 257], F32)
    dcol = TS("s5_dcol", [64, 1], F32)
    rowmask = TS("s5_rowmask", [64, 4], F32)
    negsig = TS("s5_negsig", [64, 1024], F32)
    SCR = TS("s5_scr", [128, 8192], F32)
    tA = [SCR[0:64, 1024 * i:1024 * (i + 1)] for i in range(8)]
    tAi = TS("s5_tAi", [64, 1024], I32)
    sA = [TS(f"s5_sA{i}", [64, 64], F32) for i in range(10)]
    sAi = TS("s5_sAi", [64, 64], I32)
    dtA = TS("s5_dtA", [64, 1], F32)
    W1tab = [[TS(f"s5_W1tab{q}{ri}", [64, 16, 128], BF16) for ri in range(2)] for q in range(2)]
    pB = [TS(f"s5_pB{q}", [128, 3], F32) for q in range(2)]
    crep = [TS(f"s5_crep{q}", [128, 64], F32) for q in range(2)]
    cimp = [TS(f"s5_cimp{q}", [128, 64], F32) for q in range(2)]
    kidx = TS("s5_kidx", [128, 32], F32)
    midx = TS("s5_midx", [128, 512], F32)
    tB = [TS(f"s5_tB{i}", [128, 32], F32) for i in range(7)]
    tBi = TS("s5_tBi", [128, 32], I32)
    cB = [TS(f"s5_cB{i}", [128, 1], F32) for i in range(6)]
    cBi = TS("s5_cBi", [128, 1], I32)
    gt = [SCR[:, 2048 * i:2048 * (i + 1)].rearrange("p (k c) -> p k c", k=32) for i in range(2)]
    Gpad = [[TS(f"s5_G{q}{ri}", [128, 32, 64], BF16) for ri in range(2)] for q in range(2)]
    Tc = [TS(f"s5_Tc{q}", [128, 512], F32) for q in range(2)]
    Tsn = [TS(f"s5_Ts{q}", [128, 512], F32) for q in range(2)]
    rho = [TS(f"s5_rho{q}", [128, 1], F32) for q in range(2)]
    l2 = [SCR[:, 4096 + 512 * i:4096 + 512 * (i + 1)] for i in range(6)]
    l2i = TS("s5_l2i", [128, 512], I32)
    roll = [TS(f"s5_roll{i}", [128, 512], F32) for i in range(2)]
    W15 = [[TS(f"s5_W15{q}{ri}", [128, 512], F32) for ri in range(2)] for q in range(2)]
    W1bf = [[TS(f"s5_W1bf{q}{ri}", [128, 16, 512], BF16) for ri in range(2)] for q in range(2)]
    Xbf = [[TS(f"s5_Xbf{q}{ri}", [128, 512], BF16) for ri in range(2)] for q in range(2)]
    ytmp = [TS(f"s5_ytmp{i}", [64, 512], F32) for i in range(2)]
    ps_z = [PS(f"s5_ps_z{i}") for i in range(2)]
    ps_y = [PS(f"s5_ps_y{i}") for i in range(2)]

    ld = lambda eng, dst, src, name: S.op(eng, lambda e: e.dma_start(out=dst, in_=src), w=[name], dma=True)
    ld("sp", su[:], fm[256:320, :], "su")
    ld("sp", pA[:], pA_d[:, :], "pA")
    ld("sp", dcol[:], dcol_d[:, :], "dcol")
    ld("sp", rowmask[:], rowmask_d[:, :], "rowmask")
    ld("sp", negsig[:], negsig_d.partition_broadcast(64), "negsig")
    ld("sp", kidx[:], kidx_d.partition_broadcast(128), "kidx")
    ld("sp", midx[:], midx_d.partition_broadcast(128), "midx")
    for q in range(2):
        ld("sp", pB[q][:], pB_d[q], ("pB", q))
        ld("sp", crep[q][:], cre_d[q], ("crep", q))
        ld("sp", cimp[q][:], cim_d[q], ("cimp", q))
    are, aim, bre, bim, ldt = pA[:, 0:64], pA[:, 64:128], pA[:, 128:192], pA[:, 192:256], pA[:, 256:257]
    lam, th, abr, abi, mg, zr, zi, den, u0, u1 = [t[:] for t in sA]
    S.op("act", lambda e: e.activation(out=dtA[:], in_=ldt, func=AF.Exp), r=["pA"], w=["dtA"])
    S.op("dve", lambda e: e.tensor_scalar(out=lam, in0=are, scalar1=dtA[:, 0:1], scalar2=None, op0=ALU.mult), r=["pA", "dtA"], w=["lamA"])
    S.op("dve", lambda e: e.tensor_scalar(out=th, in0=aim, scalar1=dtA[:, 0:1], scalar2=None, op0=ALU.mult), r=["pA", "dtA"], w=["thA"])
    S.op("dve", lambda e: e.tensor_copy(out=u0, in_=th), r=["thA"], w=["sAang"])
    sincos(S, u0, u1, sAi[:], den, abi, abr, "sA")
    S.op("act", lambda e: e.activation(out=mg, in_=lam, func=AF.Exp), r=["lamA"], w=["mgA"])
    S.op("dve", lambda e: e.tensor_tensor(out=abr, in0=abr, in1=mg, op=ALU.mult), r=["mgA", "sAcos"], w=["abr"])
    S.op("dve", lambda e: e.tensor_tensor(out=abi, in0=abi, in1=mg, op=ALU.mult), r=["mgA", "sAsin"], w=["abi"])
    S.op("dve", lambda e: e.tensor_scalar(out=abr, in0=abr, scalar1=-1.0, scalar2=None, op0=ALU.add), w=["abr"])
    S.op("dve", lambda e: e.tensor_tensor(out=den, in0=are, in1=are, op=ALU.mult), r=["pA", "sAcos", "sAsin"], w=["den"])
    S.op("dve", lambda e: e.tensor_tensor(out=u0, in0=aim, in1=aim, op=ALU.mult), r=["pA", "sAsin"], w=["u0"])
    S.op("dve", lambda e: e.tensor_tensor(out=den, in0=den, in1=u0, op=ALU.add), r=["u0"], w=["den"])
    S.op("dve", lambda e: e.reciprocal(out=den, in_=den), w=["den"])
    S.op("dve", lambda e: e.tensor_tensor(out=u0, in0=abr, in1=are, op=ALU.mult), r=["abr"], w=["u0"])
    S.op("dve", lambda e: e.tensor_tensor(out=u1, in0=abi, in1=aim, op=ALU.mult), r=["abi"], w=["u1"])
    S.op("dve", lambda e: e.tensor_tensor(out=zr, in0=u0, in1=u1, op=ALU.add), r=["u0", "u1"], w=["zr"])
    S.op("dve", lambda e: e.tensor_tensor(out=zr, in0=zr, in1=den, op=ALU.mult), r=["den"], w=["zr"])
    S.op("dve", lambda e: e.tensor_tensor(out=u0, in0=abi, in1=are, op=ALU.mult), r=["abi", "zr"], w=["u0"])
    S.op("dve", lambda e: e.tensor_tensor(out=u1, in0=abr, in1=aim, op=ALU.mult), r=["abr", "zr"], w=["u1"])
    S.op("dve", lambda e: e.tensor_tensor(out=zi, in0=u0, in1=u1, op=ALU.subtract), r=["u0", "u1"], w=["zi"])
    S.op("dve", lambda e: e.tensor_tensor(out=zi, in0=zi, in1=den, op=ALU.mult), r=["den"], w=["zi"])
    A3 = lambda t: t[:].rearrange("p (s m) -> p s m", s=16)
    bc3 = lambda ap: ap.unsqueeze(1).broadcast_to([64, 16, 64])
    ang3, kf3, hs3, sn3, cs3, mg3, w_r, w_i = tA
    S.op("dve", lambda e: e.tensor_tensor(out=A3(ang3), in0=A3(negsig), in1=bc3(th), op=ALU.mult), r=["negsig", "thA"], w=["tAang"])
    sincos(S, ang3[:], kf3[:], tAi[:], hs3[:], sn3[:], cs3[:], "tA")
    S.op("dve", lambda e: e.tensor_tensor(out=A3(mg3), in0=A3(negsig), in1=bc3(lam), op=ALU.mult), r=["negsig", "lamA"], w=["mg3"])
    S.op("act", lambda e: e.activation(out=mg3[:], in_=mg3[:], func=AF.Exp), w=["mg3"])
    S.op("dve", lambda e: e.tensor_tensor(out=cs3[:], in0=cs3[:], in1=mg3[:], op=ALU.mult), r=["mg3"], w=["tAcos"])
    S.op("dve", lambda e: e.tensor_tensor(out=sn3[:], in0=sn3[:], in1=mg3[:], op=ALU.mult), r=["mg3"], w=["tAsin"])
    cmul(S, "dve", A3(w_r), A3(w_i), A3(cs3), A3(sn3), bc3(zr), bc3(zi), A3(ang3), A3(kf3),
         ["tAcos", "tAsin", "zr", "zi", "tAang", "tAkf"], "wz")
    cmul(S, "dve", A3(cs3), A3(sn3), A3(w_r), A3(w_i), bc3(bre), bc3(bim), A3(ang3), A3(kf3),
         ["wzre", "wzim", "pA", "tAcos", "tAsin"], "Bs")
    for q in range(2):
        for ri, src in ((0, cs3), (1, sn3)):
            for h in range(2):
                gl = 2 * q + h
                S.op("dve", lambda e, q=q, ri=ri, h=h, gl=gl, src=src: e.tensor_scalar(
                    out=W1tab[q][ri][:, :, 64 * h:64 * h + 64], in0=A3(src), scalar1=rowmask[:, gl:gl + 1], scalar2=None, op0=ALU.mult),
                    r=["Bsre", "Bsim", "rowmask"], w=[("W1tab", q, ri, h)])
    S.barrier()
    bq = []

    class _Defer:
        def op(self, *a, **k):
            bq.append((a, k))
    SB = _Defer()
    for q in range(2):
        lamB, thB, dtB, phi, th15, junk = [t[:] for t in cB]
        angk, kfk, hsk, snk, csk, mgk, nsk = [t[:] for t in tB]
        pq = [("pB", q)]
        tg = f"B{q}"
        SB.op("act", lambda e, q=q: e.activation(out=dtB, in_=pB[q][:, 2:3], func=AF.Exp), r=pq, w=[tg + "dt"])
        SB.op("dve", lambda e, q=q: e.tensor_tensor(out=lamB, in0=pB[q][:, 0:1], in1=dtB, op=ALU.mult), r=pq + [tg + "dt"], w=[tg + "lam"])
        SB.op("dve", lambda e, q=q: e.tensor_tensor(out=thB, in0=pB[q][:, 1:2], in1=dtB, op=ALU.mult), r=pq + [tg + "dt"], w=[tg + "th"])
        SB.op("dve", lambda e: e.tensor_scalar(out=angk, in0=kidx[:], scalar1=thB[:, 0:1], scalar2=None, op0=ALU.mult),
             r=["kidx", tg + "th"], w=[tg + "kang"])
        sincos(SB, angk, kfk, tBi[:], hsk, snk, csk, tg + "k")
        SB.op("dve", lambda e: e.tensor_scalar(out=mgk, in0=kidx[:], scalar1=lamB[:, 0:1], scalar2=None, op0=ALU.mult),
             r=["kidx", tg + "lam"], w=[tg + "mgk"])
        SB.op("act", lambda e: e.activation(out=mgk, in_=mgk, func=AF.Exp), w=[tg + "mgk"])
        SB.op("dve", lambda e: e.tensor_tensor(out=csk, in0=csk, in1=mgk, op=ALU.mult), r=[tg + "mgk"], w=[tg + "kcos"])
        SB.op("dve", lambda e: e.tensor_tensor(out=snk, in0=snk, in1=mgk, op=ALU.mult), r=[tg + "mgk"], w=[tg + "ksin"])
        SB.op("dve", lambda e: e.tensor_scalar(out=nsk, in0=snk, scalar1=-1.0, scalar2=None, op0=ALU.mult), r=[tg + "ksin"], w=[tg + "nsk"])
        SB.op("dve", lambda e: e.tensor_scalar(out=kfk, in0=csk, scalar1=-1.0, scalar2=None, op0=ALU.mult), r=[tg + "kcos"], w=[tg + "kkf"])
        kb = lambda ap: ap.unsqueeze(2).broadcast_to([128, 32, 64])
        cb = lambda t: t[:].unsqueeze(1).broadcast_to([128, 32, 64])
        for ri, (f1, f2) in enumerate(((csk, nsk), (nsk, kfk))):
            SB.op("dve", lambda e, q=q, f1=f1: e.tensor_tensor(out=gt[0][:], in0=cb(crep[q]), in1=kb(f1), op=ALU.mult),
                 r=[("crep", q), tg + "kcos", tg + "nsk", tg + "kkf"], w=["gt0"])
            SB.op("dve", lambda e, q=q, f2=f2: e.tensor_tensor(out=gt[1][:], in0=cb(cimp[q]), in1=kb(f2), op=ALU.mult),
                 r=[("cimp", q), tg + "kcos", tg + "nsk", tg + "kkf"], w=["gt1"])
            SB.op("dve", lambda e, q=q, ri=ri: e.tensor_tensor(out=Gpad[q][ri][:], in0=gt[0][:], in1=gt[1][:], op=ALU.add),
                 r=["gt0", "gt1"], w=[("Gpad", q, ri)])
        SB.op("dve", lambda e: e.tensor_scalar(out=phi, in0=thB, scalar1=16.0, scalar2=None, op0=ALU.mult), r=[tg + "th"], w=[tg + "phi"])
        SB.op("dve", lambda e: e.tensor_scalar(out=th15, in0=phi, scalar1=1.0 / (2.0 * math.pi), scalar2=None, op0=ALU.mult),
             r=[tg + "phi"], w=[tg + "th15"])
        SB.op("dve", lambda e: e.tensor_copy(out=cBi[:], in_=th15), r=[tg + "th15"], w=[tg + "cBi"])
        SB.op("dve", lambda e: e.tensor_copy(out=th15, in_=cBi[:]), r=[tg + "cBi"], w=[tg + "th15"])
        SB.op("dve", lambda e: e.scalar_tensor_tensor(out=phi, in0=th15, scalar=-C1_2PI, in1=phi, op0=ALU.mult, op1=ALU.add),
             r=[tg + "th15"], w=[tg + "phi"])
        SB.op("dve", lambda e: e.scalar_tensor_tensor(out=phi, in0=th15, scalar=-C2_2PI, in1=phi, op0=ALU.mult, op1=ALU.add),
             r=[tg + "th15"], w=[tg + "phi"])
        SB.op("dve", lambda e: e.tensor_scalar(out=l2[0][:], in0=midx[:], scalar1=phi[:, 0:1], scalar2=None, op0=ALU.mult),
             r=["midx", tg + "phi"], w=["l2ang"])
        sincos(SB, l2[0][:], l2[1][:], l2i[:], l2[2][:], Tsn[q][:], Tc[q][:], "l2")
        SB.op("dve", lambda e, q=q: e.tensor_copy(out=Tsn[q][:], in_=Tsn[q][:]), r=["l2sin"], w=[("Ts", q)])
        SB.op("dve", lambda e, q=q: e.tensor_copy(out=Tc[q][:], in_=Tc[q][:]), r=["l2cos"], w=[("Tc", q)])
        SB.op("act", lambda e, q=q: e.activation(out=rho[q][:], in_=lamB, func=AF.Exp, scale=16.0), r=[tg + "lam"], w=[("rho", q)])
    suv = su[:].rearrange("p (m s) -> p s m", s=16)
    zi_ = 0
    for q in range(2):
        for ri in range(2):
            for s in range(16):
                pb = zi_ % 2
                zi_ += 1
                S.op("pe", lambda e, q=q, ri=ri, s=s, pb=pb: e.matmul(ps_z[pb][:, 0:NB], W1tab[q][ri][:, s, :], suv[:, s, :], start=True, stop=True),
                     r=["su", ("W1tab", q, ri, 0), ("W1tab", q, ri, 1)], w=[("ps_z", pb)])
                dst = W15[q][ri] if s == 15 else roll[s % 2]
                dn = ("W15", q, ri) if s == 15 else ("roll", s % 2)
                if s == 0:
                    S.op("dve", lambda e, pb=pb, dst=dst: e.tensor_copy(out=dst[:, 0:NB], in_=ps_z[pb][:, 0:NB]), w=[dn, ("ps_z", pb)])
                else:
                    S.op("dve", lambda e, pb=pb, dst=dst, s=s: e.tensor_tensor(out=dst[:, 0:NB], in0=ps_z[pb][:, 0:NB],
                                                                          in1=roll[(s - 1) % 2][:, 0:NB], op=ALU.add),
                         r=[("roll", (s - 1) % 2)], w=[dn, ("ps_z", pb)])
                S.op("act", lambda e, q=q, ri=ri, s=s, dst=dst: e.activation(out=W1bf[q][ri][:, s, 0:NB], in_=dst[:, 0:NB], func=AF.Copy),
                     r=[dn], w=[("W1bf", q, ri, s)])
                for _ in range(3):
                    if bq:
                        a, k = bq.pop(0)
                        S.op(*a, **k)
    while bq:
        a, k = bq.pop(0)
        S.op(*a, **k)
    for q in range(2):
        ur, ui, t0, t1, vr, vi = [t[:, 0:NB] for t in l2]
        tc, tsn = Tc[q][:, 0:NB], Tsn[q][:, 0:NB]
        wre, wim = W15[q][0][:, 0:NB], W15[q][1][:, 0:NB]
        cmul(S, "dve", ur, ui, tc, tsn, wre, wim, t0, t1, [("Tc", q), ("Ts", q), ("W15", q, 0), ("W15", q, 1), "l2v"], "l2u", conj_a=True)
        rb = rho[q][:, 0:1].broadcast_to([128, NB])
        S.op("dve", lambda e, rb=rb: e.tensor_tensor_scan(out=vr, data0=rb, data1=ur, initial=0.0, op0=ALU.mult, op1=ALU.add),
             r=["l2ure", ("rho", q)], w=["l2vr"])
        S.op("dve", lambda e, rb=rb: e.tensor_tensor_scan(out=vi, data0=rb, data1=ui, initial=0.0, op0=ALU.mult, op1=ALU.add),
             r=["l2uim", ("rho", q)], w=["l2vi"])
        cmul(S, "dve", ur, ui, tc, tsn, vr, vi, t0, t1, [("Tc", q), ("Ts", q), "l2vr", "l2vi"], "l2x")
        for ri, src in ((0, ur), (1, ui)):
            S.op("pool", lambda e, q=q, ri=ri: e.memset(Xbf[q][ri][:, 0:1], 0.0), w=[("Xbf", q, ri)])
            if NB > 1:
                S.op("act", lambda e, q=q, ri=ri, src=src: e.activation(out=Xbf[q][ri][:, 1:NB], in_=src[:, 0:NB - 1], func=AF.Copy),
                     r=["l2xre", "l2xim"], w=[("Xbf", q, ri)])
        S.op("dve", lambda e: e.tensor_copy(out=l2[0][:, 0:1], in_=l2[0][:, 0:1]), r=[("Xbf", q, 0), ("Xbf", q, 1)], w=["l2v", "l2ure", "l2uim"])
    outv = outsb[:].rearrange("p (m s) -> p s m", s=16)
    for s in range(16):
        pb = s % 2
        k = 0
        for q in range(2):
            for ri in range(2):
                S.op("pe", lambda e, q=q, ri=ri, s=s, pb=pb, k=k: e.matmul(ps_y[pb][0:64, 0:NB], Gpad[q][ri][:, s, :], W1bf[q][ri][:, s, 0:NB],
                                                                     start=(k == 0), stop=False),
                     r=[("Gpad", q, ri), ("W1bf", q, ri, s)], w=[("ps_y", pb)])
                k += 1
        for q in range(2):
            for ri in range(2):
                S.op("pe", lambda e, q=q, ri=ri, s=s, pb=pb, k=k: e.matmul(ps_y[pb][0:64, 0:NB], Gpad[q][ri][:, s + 16, :], Xbf[q][ri][:, 0:NB],
                                                                     start=False, stop=(k == 7)),
                     r=[("Gpad", q, ri), ("Xbf", q, ri)], w=[("ps_y", pb)])
                k += 1
        S.op("dve", lambda e, s=s, pb=pb: e.scalar_tensor_tensor(out=ytmp[pb][:, 0:NB], in0=suv[:, s, :], scalar=dcol[:, 0:1],
                                                            in1=ps_y[pb][0:64, 0:NB], op0=ALU.mult, op1=ALU.add),
             r=["su", "dcol"], w=[("ytmp", pb), ("ps_y", pb)])
        S.op("act", lambda e, s=s, pb=pb: e.activation(out=outv[:, s, :], in_=ytmp[pb][:, 0:NB], func=AF.Gelu),
             r=[("ytmp", pb)], w=[("outsb", s)])
    S.op("sp", lambda e: e.dma_start(out=yg[:, :], in_=outsb[:]), r=[("outsb", s) for s in range(16)], dma=True)


def build_s5(T=SEQ):
    nc = bass.Bass("TRN2", target_bir_lowering=False)
    D = lambda n, s, d=F32, k="ExternalInput": nc.dram_tensor(n, s, d, kind=k).ap()
    fm = D("fm", [NFM, T], BF16)
    pA = D("s5p_pA", [64, 257]); pB = D("s5p_pB", [2, 128, 3]); cre = D("s5p_cre", [2, 128, 64]); cim = D("s5p_cim", [2, 128, 64])
    dcol = D("s5p_d", [64, 1]); negsig = D("negsig", [1, 1024]); kidx = D("kidx", [1, 32]); midx = D("midx", [1, 512])
    rowmask = D("rowmask", [64, 4])
    yg = D("yg", [64, T], BF16, "ExternalOutput")
    with contextlib.ExitStack() as st:
        S = Sched(nc)
        phase_s5(nc, S, st, fm, pA, pB, cre, cim, dcol, negsig, kidx, midx, rowmask, yg, T)
        S.emit()
    return nc


def phase_out(nc, S, st, mixin, ssg_d, hT, wout_d, gluw_d, glub_d, fnw_d, hout, final, NTOK=TQ):
    TS = lambda n, s, d: st.enter_context(nc.sbuf_tensor(n, s, d))
    PS = lambda n: st.enter_context(nc.psum_tensor(n, [128, 512], F32))
    wst = [TS(f"po_wst{i}", [128, 1024], F32) for i in range(2)]
    wout = TS("po_wout", [128, 8, 1024], BF16)
    gst = TS("po_gst", [128, 2, 256], F32)
    gluw = TS("po_gluw", [128, 2, 256], BF16)
    glub = TS("po_glub", [128, 2], F32)
    fnw = TS("po_fnw", [128, 8], F32)
    ones = TS("po_ones", [128, 128], BF16)
    mix = [TS(f"po_mix{i}", [128, 8, 512], BF16) for i in range(2)]
    ssg = [TS(f"po_ssg{i}", [128, 2, 512], BF16) for i in range(2)]
    hin = [TS(f"po_hin{i}", [128, 8, 512], F32) for i in range(2)]
    sg = TS("po_sg", [128, 512], F32)
    osb = TS("po_osb", [128, 2, 512], BF16)
    hn = TS("po_hn", [128, 8, 512], F32)
    hsq = TS("po_hsq", [128, 8, 512], BF16)
    nsq = TS("po_nsq", [128, 512], F32)
    ps_g = PS("po_ps_g")
    ps_o = [PS(f"po_ps_o{i}") for i in range(3)]
    ps_n = PS("po_ps_n")

    S.op("pool", lambda e: e.memset(ones[:], 1.0), w=["ones"])
    S.op("sp", lambda e: e.dma_start(out=gst[:], in_=gluw_d.rearrange("(k p) o -> p k o", p=128)), w=["gst"], dma=True)
    S.op("sp", lambda e: e.dma_start(out=glub[:], in_=glub_d[:, :]), w=["glub"], dma=True)
    S.op("sp", lambda e: e.dma_start(out=fnw[:], in_=fnw_d[:, :]), w=["fnw"], dma=True)
    S.op("dve", lambda e: e.tensor_copy(out=gluw[:], in_=gst[:]), r=["gst"], w=["gluw"])
    for k in range(8):
        S.op("sp", lambda e, k=k: e.dma_start(out=wst[k % 2][:], in_=wout_d[k * 128:(k + 1) * 128, :]), w=[("wst", k % 2)], dma=True)
        S.op("pool" if k % 2 else "dve", lambda e, k=k: e.tensor_copy(out=wout[:, k, :], in_=wst[k % 2][:]), r=[("wst", k % 2)], w=[("wout", k)])
    wr = [("wout", k) for k in range(8)]
    mv = mixin.rearrange("(k p) t -> p k t", p=128)
    sv = ssg_d.rearrange("(k p) t -> p k t", p=128)
    hv = hT.rearrange("(k p) t -> p k t", p=128)
    ov = hout.rearrange("(k p) t -> p k t", p=128)
    oi = 0
    for ti in range(NTOK // 512):
        b = ti % 2
        ts = slice(ti * 512, (ti + 1) * 512)
        S.op("sp", lambda e, b=b, ts=ts: e.dma_start(out=mix[b][:], in_=mv[:, :, ts]), w=[("mix", b)], dma=True)
        S.op("sp", lambda e, b=b, ts=ts: e.dma_start(out=ssg[b][:], in_=sv[:, :, ts]), w=[("ssg", b)], dma=True)
        S.op("pool", lambda e, b=b, ts=ts: e.dma_start(out=hin[b][:], in_=hv[:, :, ts]), w=[("hin", b)], dma=True)
        for oc in range(2):
            for kc in range(2):
                S.op("pe", lambda e, b=b, oc=oc, kc=kc: e.matmul(ps_g[:], gluw[:, kc, oc * 128:(oc + 1) * 128], mix[b][:, 2 + kc, :],
                                                             start=(kc == 0), stop=(kc == 1)), r=[("mix", b), "gluw"], w=["ps_g"])
            S.op("act", lambda e, oc=oc: e.activation(out=sg[:], in_=ps_g[:], func=AF.Sigmoid, bias=glub[:, oc:oc + 1]),
                 r=["glub"], w=["sg", "ps_g"])
            S.op("dve", lambda e, b=b, oc=oc: e.tensor_tensor(out=sg[:], in0=sg[:], in1=mix[b][:, 2 + oc, :], op=ALU.mult),
                 r=[("mix", b)], w=["sg"])
            S.op("dve", lambda e, b=b, oc=oc: e.tensor_tensor(out=osb[:, oc, :], in0=sg[:], in1=ssg[b][:, oc, :], op=ALU.mult),
                 r=[("ssg", b), "sg"], w=[("osb", oc)])
        for dc in range(8):
            pb = oi % 3
            oi += 1
            for kc in range(8):
                rhs = (lambda b=b, kc=kc: osb[:, kc - 2, :]) if kc in (2, 3) else (lambda b=b, kc=kc: mix[b][:, kc, :])
                S.op("pe", lambda e, dc=dc, kc=kc, pb=pb, rhs=rhs: e.matmul(ps_o[pb][:], wout[:, kc, dc * 128:(dc + 1) * 128], rhs(),
                                                                      start=(kc == 0), stop=(kc == 7)),
                     r=wr + [("mix", b), ("osb", 0), ("osb", 1)], w=[("ps_o", pb)])
            S.op("dve", lambda e, b=b, dc=dc, pb=pb: e.tensor_tensor(out=hn[:, dc, :], in0=ps_o[pb][:], in1=hin[b][:, dc, :], op=ALU.add),
                 r=[("hin", b)], w=[("hn", dc), ("ps_o", pb)])
            if not final:
                S.op("sp", lambda e, dc=dc, ts=ts: e.dma_start(out=ov[:, dc, ts], in_=hn[:, dc, :]), r=[("hn", dc)], dma=True)
        if final:
            hr = [("hn", dc) for dc in range(8)]
            S.op("act", lambda e: e.activation(out=hsq[:], in_=hn[:], func=AF.Square), r=hr, w=["hsq"])
            for k in range(8):
                S.op("pe", lambda e, k=k: e.matmul(ps_n[:], ones[:], hsq[:, k, :], start=(k == 0), stop=(k == 7)), r=["hsq", "ones"], w=["ps_n"])
            S.op("act", lambda e: e.activation(out=nsq[:], in_=ps_n[:], func=AF.Sqrt, scale=1.0 / D_MODEL, bias=EPS), w=["nsq", "ps_n"])
            S.op("dve", lambda e: e.reciprocal(out=nsq[:], in_=nsq[:]), w=["nsq"])
            for dc in range(8):
                S.op("pool" if dc % 2 else "dve", lambda e, dc=dc: e.scalar_tensor_tensor(
                    out=hn[:, dc, :], in0=hn[:, dc, :], scalar=fnw[:, dc:dc + 1], in1=nsq[:], op0=ALU.mult, op1=ALU.mult) if dc % 2 == 0 else
                    e.tensor_tensor(out=hn[:, dc, :], in0=hn[:, dc, :], in1=nsq[:], op=ALU.mult),
                    r=["nsq", "fnw"], w=[("hn", dc)])
                if dc % 2:
                    S.op("pool", lambda e, dc=dc: e.tensor_scalar(out=hn[:, dc, :], in0=hn[:, dc, :], scalar1=fnw[:, dc:dc + 1], scalar2=None,
                                                                  op0=ALU.mult), r=["fnw"], w=[("hn", dc)])
                S.op("sp", lambda e, dc=dc, ts=ts: e.dma_start(out=ov[:, dc, ts], in_=hn[:, dc, :]), r=[("hn", dc)], dma=True)


def build_out(final, NTOK=TQ):
    nc = bass.Bass("TRN2", target_bir_lowering=False)
    D = lambda n, s, d=F32, k="ExternalInput": nc.dram_tensor(n, s, d, kind=k).ap()
    mixin = D("mixin", [1024, NTOK], BF16)
    ssg = D("ssg", [256, NTOK], BF16)
    hT = D("hT", [D_MODEL, NTOK])
    wout = D("wout", [1024, 1024]); gluw = D("gluw", [256, 256]); glub = D("glub", [128, 2]); fnw = D("fnw", [128, 8])
    hout = D("hout", [D_MODEL, NTOK], F32, "ExternalOutput")
    with contextlib.ExitStack() as st:
        S = Sched(nc)
        phase_out(nc, S, st, mixin, ssg, hT, wout, gluw, glub, fnw, hout, final, NTOK)
        S.emit()
    return nc


_CACHE = {}


def _prog(key, fn):
    if key not in _CACHE:
        _CACHE[key] = fn()
    return _CACHE[key]


def build_mixers(l, T=SEQ, which=("ip", "at", "hg", "s5")):
    lambda_init = 0.8 - 0.6 * math.exp(-0.3 * l)
    nc = bass.Bass("TRN2", target_bir_lowering=False)
    D = lambda n, s, d=F32, k="ExternalInput": nc.dram_tensor(n, s, d, kind=k).ap()
    hT = D("hT", [D_MODEL, T]); wcat = D("wcat", [D_MODEL, NFM + NTM]); nw = D("nw", [128, 8])
    lqk = D("lqk", [1, 256]); subln = D("subln", [128, 1]); ropef = D("ropef", [128, 1]); rmat = D("rmat", [128, 128])
    cmask = D("cmask", [4, 128, 512], BF16)
    lbl_bc = D("lbl_bc", [1, 128]); lbl_col = D("lbl_col", [64, 2]); gw = D("gw", [64, 1]); mcat = D("mcat", [128, 136]); mrev = D("mrev", [128, 128])
    pA = D("s5p_pA", [64, 257]); pB = D("s5p_pB", [2, 128, 3]); cre = D("s5p_cre", [2, 128, 64]); cim = D("s5p_cim", [2, 128, 64])
    dcol = D("s5p_d", [64, 1]); negsig = D("negsig", [1, 1024]); kidx = D("kidx", [1, 32]); midx = D("midx", [1, 512]); rowmask = D("rowmask", [64, 4])
    fm = D("fm", [NFM, T], BF16, "Internal")
    tm_sf = D("tm_sf", [T, 64], F32, "Internal")
    tm_v = D("tm_v", [T, 192], BF16, "Internal")
    mo = D("mo", [320, T], BF16, "ExternalOutput")
    if "ip" in which:
        with contextlib.ExitStack() as st:
            S = Sched(nc)
            phase_inproj(nc, S, st, hT, wcat, nw, fm, tm_sf, tm_v, T)
            S.emit()
    if "at" in which:
        with contextlib.ExitStack() as st:
            S = Sched(nc)
            S.op("sp", lambda e: e.dma_start(out=mo[128:192, :], in_=fm[192:256, :]), dma=True)
            phase_attn(nc, S, st, fm, tm_v, lqk, subln, ropef, rmat, cmask, mo[192:320, :], lambda_init, T)
            S.emit()
    if "hg" in which:
        with contextlib.ExitStack() as st:
            S = Sched(nc)
            phase_hgrn(nc, S, st, fm, tm_sf, tm_v, lbl_bc, lbl_col, gw, mcat, mrev, mo[0:64, :], float(l), T)
            S.emit()
    if "s5" in which:
        with contextlib.ExitStack() as st:
            S = Sched(nc)
            phase_s5(nc, S, st, fm, pA, pB, cre, cim, dcol, negsig, kidx, midx, rowmask, mo[64:128, :], T)
            S.emit()
    return nc


def mixer_inputs(inp, l, c, hT_b):
    f = np.float32
    j = c % 4
    ropef, rmat, cmask = attn_consts()
    mcat, mrev = hgrn_consts()
    negsig, kidx, midx, rowmask = s5_consts()
    lbl = np.asarray(inp["hgrn_lb_logits"], f)[:, 64 * j:64 * j + 64]
    d = {"hT": hT_b, "wcat": np.ascontiguousarray(np.asarray(inp["w_in"][l], f)[:, core_cols(j)]),
         "nw": np.ascontiguousarray(np.asarray(inp["norm_w"][l], f).reshape(8, 128).T),
         "lqk": np.concatenate([inp["diff_lq1"][l], inp["diff_lq2"][l], inp["diff_lk1"][l], inp["diff_lk2"][l]])[None, :].astype(f),
         "subln": np.asarray(inp["diff_subln_w"][l], f)[:, None], "ropef": ropef, "rmat": rmat, "cmask": cmask,
         "lbl_bc": np.ascontiguousarray(lbl.reshape(1, 128)), "lbl_col": np.ascontiguousarray(lbl.T),
         "gw": np.asarray(inp["hgrn_norm_w"][l], f)[:, None], "mcat": mcat, "mrev": mrev,
         "negsig": negsig, "kidx": kidx, "midx": midx, "rowmask": rowmask}
    d.update(s5_params(inp, l, j))
    return d


def kernel(**inp):
    f = np.float32
    x = np.asarray(inp["x"], f)
    cores = list(range(NCORES))
    hT = [np.ascontiguousarray(x[b].T) for b in range(BATCH)]
    for l in range(DEPTH):
        nc = _prog(("mix", l), lambda: build_mixers(l))
        ims = [mixer_inputs(inp, l, c, hT[c // 4]) for c in cores]
        rm = run_bass_kernel_spmd(nc, ims, core_ids=cores).results
        final = (l == DEPTH - 1)
        nc = _prog(("out", final), lambda: build_out(final))
        ims = []
        for c in cores:
            b, tq = c // 4, c % 4
            ts = slice(tq * TQ, (tq + 1) * TQ)
            src = [4 * b + j for j in range(4)]
            mixin = np.concatenate([rm[s]["mo"][0:64, ts] for s in src] + [rm[s]["mo"][64:128, ts] for s in src]
                                   + [rm[s]["mo"][192:320, ts] for s in src])
            ssg = np.concatenate([rm[s]["mo"][128:192, ts] for s in src])
            ims.append({"mixin": np.ascontiguousarray(mixin), "ssg": np.ascontiguousarray(ssg), "hT": np.ascontiguousarray(hT[b][:, ts]),
                        "wout": np.asarray(inp["w_out"][l], f), "gluw": np.asarray(inp["s5_glu_w"][l], f),
                        "glub": np.ascontiguousarray(np.asarray(inp["s5_glu_b"][l], f).reshape(2, 128).T),
                        "fnw": np.ascontiguousarray(np.asarray(inp["final_norm_w"], f).reshape(8, 128).T)})
        ro = run_bass_kernel_spmd(nc, ims, core_ids=cores).results
        hT = [np.concatenate([ro[4 * b + tq]["hout"] for tq in range(4)], axis=1) for b in range(BATCH)]
    out = np.stack([hT[b].T for b in range(BATCH)]).astype(f)
    return np.ascontiguousarray(out)
```

```python
import contextlib
import math
import numpy as np
import ml_dtypes
import concourse.bass as bass
import concourse.mybir as mybir
from concourse.bass_utils import run_bass_kernel_spmd

F32 = mybir.dt.float32
BF16 = mybir.dt.bfloat16
I32 = mybir.dt.int32
AF = mybir.ActivationFunctionType
ALU = mybir.AluOpType
AX = mybir.AxisListType

D_MODEL = 1024
SEQ = 8192
BATCH = 2
DEPTH = 2
EPS = 1e-6
NCORES = 8
TQ = SEQ // 4
ROPE_THETA = 500000.0
import os
DBG = set(os.environ.get("KDBG", "").split(","))


class Sched:
    ENGS = ["pe", "act", "dve", "pool", "sp"]

    def __init__(self, nc):
        self.nc = nc
        self.ops = []
        self.last_w = {}
        self.readers = {}
        self.cnt = {e: 0 for e in self.ENGS}
        self.dma_cnt = {}
        self.base = set()

    def op(self, eng, fn, r=(), w=(), dma=False):
        deps = set(self.base)
        for x in r:
            if x in self.last_w:
                deps.add(self.last_w[x])
        for x in w:
            if x in self.last_w:
                deps.add(self.last_w[x])
            for d in self.readers.get(x, ()):
                deps.add(d)
        if dma:
            q = self.dma_cnt.get(eng, 0)
            self.dma_cnt[eng] = q + 1
            tok = ("dma", eng, q)
        else:
            self.cnt[eng] += 1
            tok = ("eng", eng, self.cnt[eng])
        self.ops.append((eng, fn, deps, tok))
        for x in w:
            self.last_w[x] = tok
            self.readers[x] = []
        for x in r:
            self.readers.setdefault(x, []).append(tok)
        return tok

    def barrier(self):
        b = set()
        for e in self.ENGS:
            if self.cnt[e] > 0:
                b.add(("eng", e, self.cnt[e]))
        for e, n in self.dma_cnt.items():
            for q in range(max(0, n - self.NSLOT), n):
                b.add(("dma", e, q))
        self.base = b
        self.last_w = {}
        self.readers = {}

    NSLOT = 8

    def emit(self):
        nc = self.nc
        NSLOT = self.NSLOT
        needed = set()
        for (eng, fn, deps, tok) in self.ops:
            for d in deps:
                if d[0] == "eng" and not (d[1] == "pe" and eng == "pe"):
                    needed.add(d)
        sig = {}
        run = {e: 0 for e in self.ENGS}
        for (eng, fn, deps, tok) in self.ops:
            if tok[0] == "eng":
                if tok in needed:
                    run[eng] += 1
                sig[tok] = run[eng]
        with contextlib.ExitStack() as st:
            esem = {e: st.enter_context(nc.semaphore("s_" + e)) for e in self.ENGS}
            dsem = {}
            for e in self.dma_cnt:
                dsem[e] = [st.enter_context(nc.semaphore(f"d_{e}_{i}")) for i in range(NSLOT)]
            block = st.enter_context(nc.Block())
            per = {e: [o for o in self.ops if o[0] == e] for e in self.ENGS}

            def mk(ename):
                def body(eng):
                    seen = {}

                    def wait(tok):
                        if tok[0] == "eng":
                            _, e2, n = tok
                            if e2 == "pe" and ename == "pe":
                                return
                            v = sig[tok]
                            key = ("eng", e2)
                            if seen.get(key, 0) >= v:
                                return
                            seen[key] = v
                            eng.wait_ge(esem[e2], v)
                        else:
                            _, e2, q = tok
                            slot = q % NSLOT
                            val = 16 * (q // NSLOT + 1)
                            key = ("dma", e2, slot)
                            if seen.get(key, 0) >= val:
                                return
                            seen[key] = val
                            eng.wait_ge(dsem[e2][slot], val)
                    for (_, fn, deps, tok) in per[ename]:
                        for d in sorted(deps):
                            wait(d)
                        if tok[0] == "dma":
                            q = tok[2]
                            if q >= NSLOT:
                                wait(("dma", ename, q - NSLOT))
                            ins = fn(eng)
                            ins.then_inc(dsem[ename][q % NSLOT], 16)
                        else:
                            ins = fn(eng)
                            if tok in needed:
                                ins.then_inc(esem[ename], 1)
                    n = self.dma_cnt.get(ename, 0)
                    for q in range(max(0, n - NSLOT), n):
                        wait(("dma", ename, q))
                return body
            block.tensor(mk("pe"))
            block.scalar(mk("act"))
            block.vector(mk("dve"))
            block.gpsimd(mk("pool"))
            block.sync(mk("sp"))


NFM = 704
NTM = 256
FM_CH = [(0, 128), (128, 128), (256, 64), (320, 128), (448, 128), (576, 128)]


def phase_inproj(nc, S, st, hT, wcat, nw, fm, tm_sf, tm_v, T=SEQ):
    TS = lambda n, s, d: st.enter_context(nc.sbuf_tensor(n, s, d))
    PS = lambda n: st.enter_context(nc.psum_tensor(n, [128, 512], F32))
    nw_sb = TS("ip_nw", [128, 8], F32)
    wst = [TS(f"ip_wst{i}", [128, NFM + NTM], F32) for i in range(2)]
    wall = TS("ip_wall", [128, 8, NFM + NTM], BF16)
    ones = TS("ip_ones", [128, 128], BF16)
    xin = [TS(f"ip_xin{i}", [128, 8, 512], F32) for i in range(2)]
    xsq = TS("ip_xsq", [128, 8, 512], BF16)
    sq = TS("ip_sq", [128, 512], F32)
    rstd = TS("ip_rstd", [128, 512], F32)
    xn = [TS(f"ip_xn{i}", [128, 8, 512], BF16) for i in range(2)]
    fmo = [TS(f"ip_fmo{i}", [128, 512], BF16) for i in range(6)]
    tsf = [TS(f"ip_tsf{i}", [128, 4, 64], F32) for i in range(2)]
    tv = [TS(f"ip_tv{i}", [128, 4, 192], BF16) for i in range(2)]
    ps_ss = PS("ip_ps_ss")
    ps_fm = [PS(f"ip_ps_fm{i}") for i in range(4)]
    ps_tm = [PS(f"ip_ps_tm{i}") for i in range(2)]

    S.op("sp", lambda e: e.dma_start(out=nw_sb[:], in_=nw[:, :]), w=["nw"], dma=True)
    S.op("pool", lambda e: e.memset(ones[:], 1.0), w=["ones"])
    for k in range(8):
        S.op("sp", lambda e, k=k: e.dma_start(out=wst[k % 2][:], in_=wcat[k * 128:(k + 1) * 128, :]),
             w=[("wst", k % 2)], dma=True)
        S.op("dve", lambda e, k=k: e.tensor_scalar(out=wall[:, k, :], in0=wst[k % 2][:], scalar1=nw_sb[:, k:k + 1],
                                                  scalar2=None, op0=ALU.mult),
             r=[("wst", k % 2), "nw"], w=[("wall", k)])
    wall_r = [("wall", k) for k in range(8)]
    hT_v = hT.rearrange("(k p) t -> p k t", p=128)
    fmi = 0
    NTI = T // 512

    def load(ti):
        b = ti % 2
        t0 = ti * 512
        S.op("pool", lambda e, b=b, t0=t0: e.dma_start(out=xin[b][:, 0:4, :], in_=hT_v[:, 0:4, t0:t0 + 512]),
             w=[("xin", b, 0)], dma=True)
        S.op("pool", lambda e, b=b, t0=t0: e.dma_start(out=xin[b][:, 4:8, :], in_=hT_v[:, 4:8, t0:t0 + 512]),
             w=[("xin", b, 1)], dma=True)
    def front_sq(ti):
        b = ti % 2
        xr = [("xin", b, 0), ("xin", b, 1)]
        S.op("act", lambda e, b=b: e.activation(out=xsq[:], in_=xin[b][:], func=AF.Square), r=xr, w=["xsq"])

    def front_ss(ti):
        b = ti % 2
        for k in range(8):
            S.op("pe", lambda e, k=k: e.matmul(ps_ss[:], ones[:], xsq[:, k, :], start=(k == 0), stop=(k == 7)),
                 r=["xsq", "ones"], w=["ps_ss"])
        S.op("act", lambda e: e.activation(out=sq[:], in_=ps_ss[:], func=AF.Sqrt, scale=1.0 / D_MODEL, bias=EPS),
             w=["ps_ss", "sq"])
        S.op("dve", lambda e: e.reciprocal(out=rstd[:], in_=sq[:]), r=["sq"], w=["rstd"])
        for hh in range(2):
            S.op("dve", lambda e, b=b, hh=hh: e.tensor_tensor(out=xn[b][:, 4 * hh:4 * hh + 4, :], in0=xin[b][:, 4 * hh:4 * hh + 4, :],
                                                         in1=rstd[:].unsqueeze(1).broadcast_to([128, 4, 512]), op=ALU.mult),
                 r=[("xin", b, hh), "rstd"], w=[("xn", b, hh)])
    load(0)
    if NTI > 1:
        load(1)
    front_sq(0)
    front_ss(0)
    for ti in range(NTI):
        b = ti % 2
        t0 = ti * 512
        if ti + 1 < NTI:
            front_sq(ti + 1)
        xnr = [("xn", b, 0), ("xn", b, 1)]
        for ci, (c0, cw) in enumerate(FM_CH):
            if "NOFM" in DBG or ("FM%d" % ci) in DBG:
                continue
            if ci == 3:
                if ti + 1 < NTI:
                    front_ss(ti + 1)
                if ti + 2 < NTI:
                    load(ti + 2)
            pb = fmi % 4
            for k in range(8):
                S.op("pe", lambda e, k=k, c0=c0, cw=cw, pb=pb, b=b: e.matmul(
                    ps_fm[pb][0:cw, :], wall[:, k, c0:c0 + cw], xn[b][:, k, :], start=(k == 0), stop=(k == 7)),
                    r=xnr + wall_r, w=[("ps_fm", pb)])
            fb = fmi % 6
            fmi += 1
            if ci == 0:
                S.op("act", lambda e, pb=pb, fb=fb: e.activation(out=fmo[fb][0:64, :], in_=ps_fm[pb][0:64, :], func=AF.Silu),
                     r=[("ps_fm", pb)], w=[("fmo", fb, 0)])
                S.op("act", lambda e, pb=pb, fb=fb: e.activation(out=fmo[fb][64:128, :], in_=ps_fm[pb][64:128, :],
                                                               func=AF.Sigmoid, scale=-1.0),
                     r=[("ps_fm", pb)], w=[("fmo", fb, 1)])
                wl = [("fmo", fb, 0), ("fmo", fb, 1)]
            elif ci in (1, 5):
                S.op("act", lambda e, pb=pb, fb=fb: e.activation(out=fmo[fb][:], in_=ps_fm[pb][:], func=AF.Silu),
                     r=[("ps_fm", pb)], w=[("fmo", fb, 0), ("fmo", fb, 1)])
                wl = [("fmo", fb, 0), ("fmo", fb, 1)]
            else:
                S.op("dve", lambda e, pb=pb, fb=fb, cw=cw: e.tensor_copy(out=fmo[fb][0:cw, :], in_=ps_fm[pb][0:cw, :]),
                     r=[("ps_fm", pb)], w=[("fmo", fb, 0), ("fmo", fb, 1)])
                wl = [("fmo", fb, 0), ("fmo", fb, 1)]
            S.op("sp", lambda e, fb=fb, c0=c0, cw=cw, t0=t0: e.dma_start(out=fm[c0:c0 + cw, t0:t0 + 512], in_=fmo[fb][0:cw, :]),
                 r=wl, dma=True)
        for pb in range(0 if "NOTM" in DBG else 2):
            for t4 in (2 * pb, 2 * pb + 1):
                off = (t4 % 2) * 256
                for k in range(8):
                    S.op("pe", lambda e, k=k, t4=t4, pb=pb, off=off, b=b: e.matmul(
                        ps_tm[pb][:, off:off + 256], xn[b][:, k, t4 * 128:(t4 + 1) * 128], wall[:, k, NFM:NFM + NTM],
                        start=(k == 0), stop=(k == 7)),
                        r=xnr + wall_r, w=[("ps_tm", pb)])
            if "TMNOEVAC" in DBG:
                continue
            for t4 in (2 * pb, 2 * pb + 1):
                off = (t4 % 2) * 256
                if "TMNOACT" not in DBG:
                  S.op("act", lambda e, pb=pb, b=b, t4=t4, off=off: e.activation(
                    out=tsf[b][:, t4, :], in_=ps_tm[pb][:, off:off + 64], func=AF.Sigmoid),
                    w=[("ps_tm", pb), ("tsf", b, t4)])
                if "TMNODVE" not in DBG:
                  S.op("dve", lambda e, pb=pb, b=b, t4=t4, off=off: e.tensor_copy(
                    out=tv[b][:, t4, :], in_=ps_tm[pb][:, off + 64:off + 256]),
                    w=[("ps_tm", pb), ("tv", b, t4)])
        if "TMNODMA" in DBG:
            continue
        S.op("sp", lambda e, b=b, t0=t0: e.dma_start(
            out=tm_sf[t0:t0 + 512, :].rearrange("(a p) c -> p a c", p=128), in_=tsf[b][:]),
            r=[("tsf", b, t4) for t4 in range(4)], dma=True)
        S.op("sp", lambda e, b=b, t0=t0: e.dma_start(
            out=tm_v[t0:t0 + 512, :].rearrange("(a p) c -> p a c", p=128), in_=tv[b][:]),
            r=[("tv", b, t4) for t4 in range(4)], dma=True)


def core_cols(j):
    r = lambda s, n: list(range(s, s + n))
    fmc = (r(0 + 64 * j, 64) + r(256 + 64 * j, 64) + r(768 + 64 * j, 64) + r(1280 + 64 * j, 64) + r(1024 + 64 * j, 64)
           + r(1536 + 128 * j, 128) + r(2048 + 128 * j, 128) + r(3072 + 128 * j, 128))
    tmc = r(256 + 64 * j, 64) + r(512 + 64 * j, 64) + r(2560 + 128 * j, 128)
    return np.array(fmc + tmc)


def build_inproj(T=SEQ):
    nc = bass.Bass("TRN2", target_bir_lowering=False)
    hT = nc.dram_tensor("hT", [D_MODEL, T], F32, kind="ExternalInput").ap()
    wcat = nc.dram_tensor("wcat", [D_MODEL, NFM + NTM], F32, kind="ExternalInput").ap()
    nw = nc.dram_tensor("nw", [128, 8], F32, kind="ExternalInput").ap()
    fm = nc.dram_tensor("fm", [NFM, T], BF16, kind="ExternalOutput").ap()
    tm_sf = nc.dram_tensor("tm_sf", [T, 64], F32, kind="ExternalOutput").ap()
    tm_v = nc.dram_tensor("tm_v", [T, 192], BF16, kind="ExternalOutput").ap()
    with contextlib.ExitStack() as st:
        S = Sched(nc)
        phase_inproj(nc, S, st, hT, wcat, nw, fm, tm_sf, tm_v, T)
        S.emit()
    return nc


C1_2PI = 6.28125
C2_2PI = 2.0 * math.pi - 6.28125


def sincos(S, ang, kf, ki, hs, sin_out, cos_out, tag, eng="dve"):
    a, k, h = tag + "ang", tag + "kf", tag + "hs"
    S.op(eng, lambda e: e.tensor_scalar(out=kf, in0=ang, scalar1=1.0 / (2.0 * math.pi), scalar2=None, op0=ALU.mult), r=[a], w=[k])
    S.op(eng, lambda e: e.tensor_copy(out=ki, in_=kf), r=[k], w=[tag + "ki"])
    S.op(eng, lambda e: e.tensor_copy(out=kf, in_=ki), r=[tag + "ki"], w=[k])
    S.op("dve", lambda e: e.scalar_tensor_tensor(out=ang, in0=kf, scalar=-C1_2PI, in1=ang, op0=ALU.mult, op1=ALU.add), r=[k], w=[a])
    S.op("dve", lambda e: e.scalar_tensor_tensor(out=ang, in0=kf, scalar=-C2_2PI, in1=ang, op0=ALU.mult, op1=ALU.add), r=[k], w=[a])
    S.op(eng, lambda e: e.tensor_scalar(out=ang, in0=ang, scalar1=math.pi, scalar2=-math.pi, op0=ALU.min, op1=ALU.max), w=[a])
    S.op("act", lambda e: e.activation(out=sin_out, in_=ang, func=AF.Sin), r=[a], w=[tag + "sin"])
    S.op("act", lambda e: e.activation(out=hs, in_=ang, func=AF.Sin, scale=0.5), r=[a], w=[h])
    S.op(eng, lambda e: e.tensor_tensor(out=hs, in0=hs, in1=hs, op=ALU.mult), w=[h])
    S.op(eng, lambda e: e.tensor_scalar(out=cos_out, in0=hs, scalar1=-2.0, scalar2=1.0, op0=ALU.mult, op1=ALU.add), r=[h], w=[tag + "cos"])


def rope_tables(nc, S, st, ropef, sinT, cosT, T, tag):
    TS = lambda n, s, d: st.enter_context(nc.sbuf_tensor(n, s, d))
    CH = min(512, T)
    NCH = T // CH
    pi_ = TS(tag + "_pi", [128, CH], I32)
    ang = TS(tag + "_ang", [128, CH], F32)
    kf = TS(tag + "_kf", [128, CH], F32)
    ki = TS(tag + "_ki", [128, CH], I32)
    hs = TS(tag + "_hs", [128, CH], F32)
    s1 = TS(tag + "_s1", [128, CH], F32)
    c1 = TS(tag + "_c1", [128, CH], F32)
    pj = TS(tag + "_pj", [128, NCH], I32)
    ang_b = TS(tag + "_angb", [128, NCH], F32)
    kf_b = TS(tag + "_kfb", [128, NCH], F32)
    ki_b = TS(tag + "_kib", [128, NCH], I32)
    hs_b = TS(tag + "_hsb", [128, NCH], F32)
    s2 = TS(tag + "_s2", [128, NCH], F32)
    c2 = TS(tag + "_c2", [128, NCH], F32)
    ns2 = TS(tag + "_ns2", [128, NCH], F32)
    tmp = [TS(tag + f"_tmp{i}", [128, CH], F32) for i in range(4)]
    S.op("pool", lambda e: e.iota(pi_[:], pattern=[[1, CH]], base=0, channel_multiplier=0), w=[tag + "pi"])
    S.op("pool", lambda e: e.iota(pj[:], pattern=[[CH, NCH]], base=0, channel_multiplier=0), w=[tag + "pj"])
    S.op("dve", lambda e: e.tensor_copy(out=ang[:], in_=pi_[:]), r=[tag + "pi"], w=[tag + "aang"])
    S.op("dve", lambda e: e.tensor_scalar(out=ang[:], in0=ang[:], scalar1=ropef[:, 0:1], scalar2=None, op0=ALU.mult), r=["ropef"], w=[tag + "aang"])
    sincos(S, ang[:], kf[:], ki[:], hs[:], s1[:], c1[:], tag + "a")
    S.op("dve", lambda e: e.tensor_copy(out=ang_b[:], in_=pj[:]), r=[tag + "pj"], w=[tag + "bang"])
    S.op("dve", lambda e: e.tensor_scalar(out=ang_b[:], in0=ang_b[:], scalar1=ropef[:, 0:1], scalar2=None, op0=ALU.mult), r=["ropef"], w=[tag + "bang"])
    sincos(S, ang_b[:], kf_b[:], ki_b[:], hs_b[:], s2[:], c2[:], tag + "b")
    S.op("dve", lambda e: e.tensor_scalar(out=ns2[:], in0=s2[:], scalar1=-1.0, scalar2=None, op0=ALU.mult), r=[tag + "bsin"], w=[tag + "ns2"])
    rd = [tag + "asin", tag + "acos", tag + "bsin", tag + "bcos", tag + "ns2"]
    for c in range(NCH):
        sl = slice(c * CH, (c + 1) * CH)
        ta, tb = tmp[(2 * c) % 4], tmp[(2 * c + 1) % 4]
        na, nb = (tag + "tmp", (2 * c) % 4), (tag + "tmp", (2 * c + 1) % 4)
        S.op("dve", lambda e, c=c, ta=ta: e.tensor_scalar(out=ta[:], in0=c1[:], scalar1=s2[:, c:c + 1], scalar2=None, op0=ALU.mult), r=rd, w=[na])
        S.op("dve", lambda e, c=c, ta=ta, sl=sl: e.scalar_tensor_tensor(out=sinT[:, sl], in0=s1[:], scalar=c2[:, c:c + 1], in1=ta[:],
                                                                        op0=ALU.mult, op1=ALU.add), r=rd + [na], w=[(tag + "sin", c)])
        S.op("dve", lambda e, c=c, tb=tb: e.tensor_scalar(out=tb[:], in0=s1[:], scalar1=ns2[:, c:c + 1], scalar2=None, op0=ALU.mult), r=rd, w=[nb])
        S.op("dve", lambda e, c=c, tb=tb, sl=sl: e.scalar_tensor_tensor(out=cosT[:, sl], in0=c1[:], scalar=c2[:, c:c + 1], in1=tb[:],
                                                                        op0=ALU.mult, op1=ALU.add), r=rd + [nb], w=[(tag + "cos", c)])
    return [(tag + "sin", c) for c in range(NCH)] + [(tag + "cos", c) for c in range(NCH)]


def phase_attn(nc, S, st, fm, tm_v, lqk, subln, ropef_d, rmat_d, cmask_d, oa, lambda_init, T=SEQ):
    TS = lambda n, s, d: st.enter_context(nc.sbuf_tensor(n, s, d))
    PS = lambda n: st.enter_context(nc.psum_tensor(n, [128, 512], F32))
    NQ = T // 512
    NK = T // 128
    ropef = TS("at_ropef", [128, 1], F32)
    rm32 = TS("at_rm32", [128, 128], F32)
    rm = TS("at_rm", [128, 128], BF16)
    cmask = TS("at_cmask", [128, 4, 512], BF16)
    ones = TS("at_ones", [128, 128], BF16)
    sinT = TS("at_sin", [128, T], BF16)
    cosT = TS("at_cos", [128, T], BF16)
    qraw = TS("at_qraw", [128, T], BF16)
    kraw = TS("at_kraw", [128, T], BF16)
    qr = TS("at_qr", [128, T], BF16)
    kr = TS("at_kr", [128, T], BF16)
    sag = TS("at_sag", [128, T], BF16)
    vsb = TS("at_v", [128, NK, 128], BF16)
    lq = TS("at_lq", [128, 256], F32)
    lp = TS("at_lp", [128, 128], F32)
    le = TS("at_le", [128, 2], F32)
    neglam = TS("at_neglam", [128, 1], F32)
    sw = TS("at_sw", [128, 1], F32)
    t1 = [TS(f"at_t1_{i}", [128, 512], BF16) for i in range(2)]
    t2 = [TS(f"at_t2_{i}", [128, 512], BF16) for i in range(2)]
    P = [[TS(f"at_P{i}_{m}", [128, 512], BF16) for m in range(2)] for i in range(3)]
    r0 = [TS(f"at_r0_{i}", [128, 512], F32) for i in range(2)]
    r1 = [TS(f"at_r1_{i}", [128, 512], F32) for i in range(2)]
    o0 = [TS(f"at_o0_{i}", [128, 512], F32) for i in range(2)]
    o1 = [TS(f"at_o1_{i}", [128, 512], F32) for i in range(2)]
    osq = TS("at_osq", [128, 512], BF16)
    nsq = TS("at_nsq", [128, 512], F32)
    ob = [TS(f"at_ob{i}", [128, 512], BF16) for i in range(2)]
    epsc = TS("at_epsc", [128, 1], F32)
    accD = [TS(f"at_accD{i}", [128, 512], F32) for i in range(2)]
    accP = [TS(f"at_accP{i}", [128, 512], F32) for i in range(2)]
    ones32 = TS("at_ones32", [128, 128], F32)
    ps_s = [[PS(f"at_ps_s{i}_{m}") for m in range(2)] for i in range(2)]
    ps_o = [PS(f"at_ps_o{m}") for m in range(2)]
    ps_l = [PS(f"at_ps_l{m}") for m in range(2)]
    ps_n = ps_s[0][0]

    S.op("sp", lambda e: e.dma_start(out=ropef[:], in_=ropef_d[:, :]), w=["ropef"], dma=True)
    S.op("sp", lambda e: e.dma_start(out=rm32[:], in_=rmat_d[:, :]), w=["rm32"], dma=True)
    S.op("sp", lambda e: e.dma_start(out=cmask[:], in_=cmask_d.rearrange("d p q -> p d q")), w=["cmask"], dma=True)
    S.op("sp", lambda e: e.dma_start(out=lq[:], in_=lqk.partition_broadcast(128)), w=["lq"], dma=True)
    S.op("sp", lambda e: e.dma_start(out=sw[:], in_=subln[:, :]), w=["sw"], dma=True)
    S.op("sp", lambda e: e.dma_start(out=qraw[:], in_=fm[320:448, :]), w=["qraw"], dma=True)
    S.op("sp", lambda e: e.dma_start(out=kraw[:], in_=fm[448:576, :]), w=["kraw"], dma=True)
    S.op("sp", lambda e: e.dma_start(out=sag[:], in_=fm[576:704, :]), w=["sag"], dma=True)
    S.op("pool", lambda e: e.dma_start(out=vsb[:], in_=tm_v[:, 64:192].rearrange("(a p) c -> p a c", p=128)), w=["vsb"], dma=True)
    S.op("pool", lambda e: e.memset(ones[:], 1.0), w=["ones"])
    S.op("pool", lambda e: e.memset(epsc[:], EPS), w=["epsc"])
    S.op("pool", lambda e: e.memset(ones32[:], 1.0), w=["ones32"])
    S.op("dve", lambda e: e.tensor_copy(out=rm[:], in_=rm32[:]), r=["rm32"], w=["rm"])
    S.op("dve", lambda e: e.tensor_tensor(out=lp[:], in0=lq[:, 0:128], in1=lq[:, 128:256], op=ALU.mult), r=["lq"], w=["lp"])
    S.op("dve", lambda e: e.tensor_reduce(out=le[:], in_=lp[:].rearrange("p (a c) -> p a c", a=2), axis=AX.X, op=ALU.add),
         r=["lp"], w=["le"])
    S.op("act", lambda e: e.activation(out=le[:], in_=le[:], func=AF.Exp), w=["le"])
    S.op("dve", lambda e: e.tensor_tensor(out=neglam[:], in0=le[:, 1:2], in1=le[:, 0:1], op=ALU.subtract), r=["le"], w=["neglam"])
    S.op("dve", lambda e: e.tensor_scalar(out=neglam[:], in0=neglam[:], scalar1=-lambda_init, scalar2=None, op0=ALU.add), w=["neglam"])
    S.op("dve", lambda e: e.tensor_scalar(out=sw[:], in0=sw[:], scalar1=1.0 - lambda_init, scalar2=None, op0=ALU.mult), w=["sw"])
    tabs = rope_tables(nc, S, st, ropef, sinT, cosT, T, "at_rp")
    def rot_ops(ti):
        sl = slice(ti * 512, (ti + 1) * 512)
        ops = []
        for k_, (src, dst, sn, dn) in enumerate(((qraw, qr, "qraw", "qr"), (kraw, kr, "kraw", "kr"))):
            b = k_
            ops.append(lambda src=src, sl=sl, sn=sn, b=b: S.op("dve", lambda e: e.tensor_tensor(out=t1[b][:], in0=src[:, sl], in1=cosT[:, sl], op=ALU.mult),
                                                             r=[sn] + tabs, w=[("t1", b)]))

            def mm_unit(src=src, sl=sl, sn=sn, b=b):
                S.op("pe", lambda e: e.matmul(ps_l[1][:], rm[:], src[:, sl], start=True, stop=True), r=[sn, "rm"], w=["ps_l1"])
                S.op("dve", lambda e: e.tensor_tensor(out=t2[b][:], in0=ps_l[1][:], in1=sinT[:, sl], op=ALU.mult), r=tabs, w=[("t2", b), "ps_l1"])
            ops.append(mm_unit)
            ops.append(lambda dst=dst, sl=sl, b=b, dn=dn, ti=ti: S.op("pool", lambda e: e.tensor_tensor(out=dst[:, sl], in0=t1[b][:], in1=t2[b][:], op=ALU.add),
                                                                    r=[("t1", b), ("t2", b)], w=[(dn, ti)]))
        return ops
    for f_ in rot_ops(0):
        f_()
    qr_all = [("qr", ti) for ti in range(NQ)]
    kr_all = [("kr", ti) for ti in range(NQ)]
    psn = lambda i, m: f"ps_s{i}{m}"
    dq = []

    def pop_deferred(n, limit=2):
        k = 0
        while dq and k < limit:
            fn, need_odd = dq[0]
            if need_odd and n % 2 == 0:
                break
            dq.pop(0)
            fn()
            k += 1
    for qi in range(NQ):
        qs = slice(qi * 512, (qi + 1) * 512)
        nk = 4 * (qi + 1)
        ab = qi % 2

        def c0_of(n, qi=qi):
            return 128 * max(0, n - 4 * qi)

        def QK(n, qs=qs, qi=qi):
            i = n % 2
            c0 = c0_of(n)
            for m in range(2):
                S.op("pe", lambda e, n=n, i=i, m=m, qs=qs, c0=c0: e.matmul(
                    ps_s[i][m][:, c0:512], kr[64 * m:64 * m + 64, n * 128:(n + 1) * 128],
                    qr[64 * m:64 * m + 64, qs.start + c0:qs.stop], start=True, stop=True),
                     r=[("qr", qi), ("kr", n // 4)], w=[psn(i, m)])

        def EXP(n, qi=qi):
            i = n % 2
            j = n % 3
            c0 = c0_of(n)
            for m in range(2):
                S.op("act", lambda e, i=i, j=j, m=m, c0=c0: e.activation(out=P[j][m][:, c0:512], in_=ps_s[i][m][:, c0:512], func=AF.Exp, scale=0.125),
                     w=[psn(i, m), ("P", j, m)])
                d = n - 4 * qi
                if d >= 0:
                    S.op("dve", lambda e, j=j, m=m, d=d, c0=c0: e.tensor_tensor(out=P[j][m][:, c0:512], in0=P[j][m][:, c0:512],
                                                                             in1=cmask[:, d, c0:512], op=ALU.mult),
                         r=["cmask"], w=[("P", j, m)])

        def PV(n, nk=nk, ab=ab):
            j = n % 3
            c0 = c0_of(n)
            for m in range(2):
                S.op("pe", lambda e, n=n, j=j, m=m, nk=nk, c0=c0: e.matmul(ps_o[m][:, c0:512], vsb[:, n, :], P[j][m][:, c0:512],
                                                                       start=(n == 0), stop=(n == nk - 1)),
                     r=[("P", j, m), "vsb"], w=[f"ps_o{m}"])
            S.op("pe", lambda e, n=n, j=j, nk=nk, c0=c0: e.matmul(ps_l[0][:, c0:512], ones[:], P[j][0][:, c0:512], start=(n == 0), stop=(n == nk - 1)),
                 r=[("P", j, 0), "ones"], w=["ps_l0"])
            eng, acc, an = ("dve", accD[ab], ("accD", ab)) if n % 2 == 0 else ("pool", accP[ab], ("accP", ab))
            if n < 2:
                if c0 > 0:
                    S.op(eng, lambda e, acc=acc, c0=c0: e.memset(acc[:, 0:c0], 0.0), w=[an])
                S.op(eng, lambda e, j=j, acc=acc, c0=c0: e.tensor_copy(out=acc[:, c0:512], in_=P[j][1][:, c0:512]), r=[("P", j, 1)], w=[an])
            else:
                S.op(eng, lambda e, j=j, acc=acc, c0=c0: e.tensor_tensor(out=acc[:, c0:512], in0=acc[:, c0:512], in1=P[j][1][:, c0:512], op=ALU.add),
                     r=[("P", j, 1)], w=[an])
        rq = rot_ops(qi + 1) if qi + 1 < NQ else []
        QK(0)
        for n in range(nk):
            if n >= 1:
                pop_deferred(n)
                for _ in range(2):
                    if rq:
                        rq.pop(0)()
            if n + 1 < nk:
                QK(n + 1)
            EXP(n)
            PV(n)
        while dq:
            fn, need_odd = dq.pop(0)
            fn()
        while rq:
            rq.pop(0)()
        eb = qi % 2
        S.op("dve", lambda e, eb=eb: e.tensor_copy(out=o0[eb][:], in_=ps_o[0][:]), w=[("o0", eb), "ps_o0"])
        S.op("dve", lambda e, eb=eb: e.tensor_copy(out=o1[eb][:], in_=ps_o[1][:]), w=[("o1", eb), "ps_o1"])
        S.op("dve", lambda e, eb=eb: e.tensor_copy(out=r0[eb][:], in_=ps_l[0][:]), w=[("r0", eb), "ps_l0"])
        D = lambda fn, odd=False: dq.append((fn, odd))
        def l1_unit(ab=ab, eb=eb):
            S.op("pe", lambda e: e.matmul(ps_l[1][:], ones32[:], accD[ab][:], start=True, stop=False), r=[("accD", ab), "ones32"], w=["ps_l1"])
            S.op("pe", lambda e: e.matmul(ps_l[1][:], ones32[:], accP[ab][:], start=False, stop=True), r=[("accP", ab), "ones32"], w=["ps_l1"])
            S.op("dve", lambda e: e.tensor_copy(out=r1[eb][:], in_=ps_l[1][:]), w=[("r1", eb), "ps_l1"])
        D(l1_unit)
        for rr, rn in ((r0, "r0"), (r1, "r1")):
            D(lambda eb=eb, rr=rr, rn=rn: S.op("act", lambda e: e.activation(out=rr[eb][:], in_=rr[eb][:], func=AF.Ln), w=[(rn, eb)]))
            D(lambda eb=eb, rr=rr, rn=rn: S.op("act", lambda e: e.activation(out=rr[eb][:], in_=rr[eb][:], func=AF.Exp, scale=-1.0), w=[(rn, eb)]))
        D(lambda eb=eb: S.op("dve", lambda e: e.tensor_tensor(out=o0[eb][:], in0=o0[eb][:], in1=r0[eb][:], op=ALU.mult), r=[("r0", eb)], w=[("o0", eb)]))
        D(lambda eb=eb: S.op("dve", lambda e: e.tensor_tensor(out=o1[eb][:], in0=o1[eb][:], in1=r1[eb][:], op=ALU.mult), r=[("r1", eb)], w=[("o1", eb)]))
        D(lambda eb=eb: S.op("dve", lambda e: e.scalar_tensor_tensor(out=o0[eb][:], in0=o1[eb][:], scalar=neglam[:, 0:1], in1=o0[eb][:],
                                                                     op0=ALU.mult, op1=ALU.add), r=[("o1", eb), "neglam"], w=[("o0", eb)]))
        D(lambda eb=eb: S.op("act", lambda e: e.activation(out=osq[:], in_=o0[eb][:], func=AF.Square), r=[("o0", eb)], w=["osq"]))
        def norm_unit():
            S.op("pe", lambda e: e.matmul(ps_n[:], ones[:], osq[:], start=True, stop=True), r=["osq", "ones"], w=[psn(0, 0)])
            S.op("act", lambda e: e.activation(out=nsq[:], in_=ps_n[:], func=AF.Ln, scale=1.0 / 128.0, bias=epsc[:, 0:1]),
                 r=["epsc"], w=["nsq", psn(0, 0)])
        D(norm_unit, True)
        D(lambda: S.op("act", lambda e: e.activation(out=nsq[:], in_=nsq[:], func=AF.Exp, scale=-0.5), w=["nsq"]))
        D(lambda eb=eb: S.op("dve", lambda e: e.scalar_tensor_tensor(out=o0[eb][:], in0=o0[eb][:], scalar=sw[:, 0:1], in1=nsq[:],
                                                                     op0=ALU.mult, op1=ALU.mult), r=["nsq", "sw"], w=[("o0", eb)]))
        D(lambda eb=eb, qs=qs: S.op("pool", lambda e: e.tensor_tensor(out=ob[eb][:], in0=o0[eb][:], in1=sag[:, qs], op=ALU.mult),
                                    r=[("o0", eb), "sag"], w=[("ob", eb)]))
        D(lambda eb=eb, qs=qs: S.op("sp", lambda e: e.dma_start(out=oa[:, qs], in_=ob[eb][:]), r=[("ob", eb)], dma=True))
    while dq:
        fn, need_odd = dq.pop(0)
        fn()


def attn_consts():
    ropef = np.zeros((128, 1), np.float32)
    inv = (ROPE_THETA ** (-np.arange(0, 16, 2, dtype=np.float32) / 16.0)).astype(np.float32)
    rmat = np.zeros((128, 128), np.float32)
    for base in (0, 64):
        for i in range(8):
            ropef[base + i, 0] = -inv[i]
            ropef[base + 8 + i, 0] = inv[i]
            rmat[base + 8 + i, base + i] = 1.0
            rmat[base + i, base + 8 + i] = 1.0
    k = np.arange(128)[:, None]
    q = np.arange(512)[None, :]
    cmask = np.stack([(128 * d + k <= q) for d in range(4)]).astype(ml_dtypes.bfloat16)
    return ropef, rmat, cmask


def build_attn(lambda_init, T=SEQ):
    nc = bass.Bass("TRN2", target_bir_lowering=False)
    fm = nc.dram_tensor("fm", [NFM, T], BF16, kind="ExternalInput").ap()
    tm_v = nc.dram_tensor("tm_v", [T, 192], BF16, kind="ExternalInput").ap()
    lqk = nc.dram_tensor("lqk", [1, 256], F32, kind="ExternalInput").ap()
    subln = nc.dram_tensor("subln", [128, 1], F32, kind="ExternalInput").ap()
    ropef = nc.dram_tensor("ropef", [128, 1], F32, kind="ExternalInput").ap()
    rmat = nc.dram_tensor("rmat", [128, 128], F32, kind="ExternalInput").ap()
    cmask = nc.dram_tensor("cmask", [4, 128, 512], BF16, kind="ExternalInput").ap()
    oa = nc.dram_tensor("oa", [128, T], BF16, kind="ExternalOutput").ap()
    with contextlib.ExitStack() as st:
        S = Sched(nc)
        phase_attn(nc, S, st, fm, tm_v, lqk, subln, ropef, rmat, cmask, oa, lambda_init, T)
        S.emit()
    return nc


def hgrn_consts():
    s = np.arange(128)[:, None]
    t = np.arange(128)[None, :]
    same = (s // 16) == (t // 16)
    m_incl = (same & (s <= t)).astype(np.float32)
    m_rev = (same & (s > t)).astype(np.float32)
    m_tot8 = ((s // 16) == np.arange(8)[None, :]).astype(np.float32)
    mcat = np.concatenate([m_incl, m_tot8], axis=1)
    return mcat, m_rev


def phase_hgrn(nc, S, st, fm, tm_sf, tm_v, lbl_bc_d, lbl_col_d, gw_d, mcat_d, mrev_d, oh, lb_coef, T=SEQ):
    TS = lambda n, s, d: st.enter_context(nc.sbuf_tensor(n, s, d))
    PS = lambda n: st.enter_context(nc.psum_tensor(n, [128, 512], F32))
    NT = T // 128
    sq = TS("hg_sq", [64, T], BF16)
    snf = TS("hg_snf", [64, T], BF16)
    shg = TS("hg_shg", [64, T], BF16)
    sf = TS("hg_sf", [128, NT, 64], F32)
    omf = TS("hg_omf", [128, NT, 64], F32)
    logf = TS("hg_logf", [128, NT, 64], F32)
    vall = TS("hg_v", [128, NT, 64], BF16)
    lbl = TS("hg_lbl", [128, 128], F32)
    lb_bc = TS("hg_lb_bc", [128, 64], F32)
    oml_bc = TS("hg_oml_bc", [128, 64], F32)
    lblc = TS("hg_lblc", [64, 2], F32)
    oml_col = TS("hg_oml_col", [64, 1], F32)
    gw = TS("hg_gw", [64, 1], F32)
    mcat = TS("hg_mcat", [128, 136], F32)
    mrev = TS("hg_mrev", [128, 128], F32)
    mincl_bf = TS("hg_mincl", [128, 128], BF16)
    mtot_bf = TS("hg_mtot", [128, 8], BF16)
    ones64 = TS("hg_ones", [64, 64], BF16)
    epsc = TS("hg_epsc", [64, 1], F32)
    eq = TS("hg_eq", [64, 128], F32)
    ekn = TS("hg_ekn", [64, 128], F32)
    dec = [TS(f"hg_dec{i}", [64, 8], F32) for i in range(3)]
    ehat = TS("hg_ehat", [128, 64], F32)
    qt = [TS(f"hg_qt{i}", [64, 128], BF16) for i in range(3)]
    kt = TS("hg_kt", [64, 128], BF16)
    khat = TS("hg_khat", [128, 64], BF16)
    vblk = TS("hg_vblk", [128, 8, 64], BF16)
    scm = [TS(f"hg_scm{i}", [128, 128], BF16) for i in range(3)]
    Sall = [TS(f"hg_S{i}", [64, 9, 64], F32) for i in range(2)]
    Sbf = [TS(f"hg_Sbf{i}", [64, 8, 64], BF16) for i in range(2)]
    osq = TS("hg_osq", [64, 512], BF16)
    o32 = TS("hg_o32", [64, 512], F32)
    nsq = TS("hg_nsq", [64, 512], F32)
    ohb = [TS(f"hg_ohb{i}", [64, 512], BF16) for i in range(2)]
    ps_c = PS("hg_ps_c")
    ps_r = PS("hg_ps_r")
    ps_sc = PS("hg_ps_sc")
    ps_u = [PS(f"hg_ps_u{i}") for i in range(3)]
    ps_oh = [PS(f"hg_ps_oh{i}") for i in range(2)]

    S.op("sp", lambda e: e.dma_start(out=sq[:], in_=fm[0:64, :]), w=["sq"], dma=True)
    S.op("sp", lambda e: e.dma_start(out=snf[:], in_=fm[64:128, :]), w=["snf"], dma=True)
    S.op("sp", lambda e: e.dma_start(out=shg[:], in_=fm[128:192, :]), w=["shg"], dma=True)
    S.op("sp", lambda e: e.dma_start(out=sf[:], in_=tm_sf.rearrange("(a p) c -> p a c", p=128)), w=["sf"], dma=True)
    S.op("pool", lambda e: e.dma_start(out=vall[:], in_=tm_v[:, 0:64].rearrange("(a p) c -> p a c", p=128)), w=["vall"], dma=True)
    S.op("sp", lambda e: e.dma_start(out=lbl[:], in_=lbl_bc_d.partition_broadcast(128)), w=["lbl"], dma=True)
    S.op("sp", lambda e: e.dma_start(out=lblc[:], in_=lbl_col_d[:, :]), w=["lblc"], dma=True)
    S.op("sp", lambda e: e.dma_start(out=gw[:], in_=gw_d[:, :]), w=["gw"], dma=True)
    S.op("sp", lambda e: e.dma_start(out=mcat[:], in_=mcat_d[:, :]), w=["mcat"], dma=True)
    S.op("sp", lambda e: e.dma_start(out=mrev[:], in_=mrev_d[:, :]), w=["mrev"], dma=True)
    S.op("pool", lambda e: e.memset(ones64[:], 1.0), w=["ones64"])
    S.op("pool", lambda e: e.memset(epsc[:], EPS), w=["epsc"])
    S.op("pool", lambda e: e.memset(Sall[0][:, 0, :], 0.0), w=[("S", 0)])
    S.op("dve", lambda e: e.tensor_copy(out=mincl_bf[:], in_=mcat[:, 0:128]), r=["mcat"], w=["mincl_bf"])
    S.op("dve", lambda e: e.tensor_copy(out=mtot_bf[:], in_=mcat[:, 128:136]), r=["mcat"], w=["mtot_bf"])
    S.op("dve", lambda e: e.tensor_tensor(out=lb_bc[:], in0=lbl[:, 64:128], in1=lbl[:, 0:64], op=ALU.subtract), r=["lbl"], w=["lb_bc"])
    S.op("act", lambda e: e.activation(out=lb_bc[:], in_=lb_bc[:], func=AF.Sigmoid), w=["lb_bc"])
    S.op("dve", lambda e: e.tensor_scalar(out=lb_bc[:], in0=lb_bc[:], scalar1=float(lb_coef), scalar2=None, op0=ALU.mult), w=["lb_bc"])
    S.op("dve", lambda e: e.tensor_scalar(out=oml_bc[:], in0=lb_bc[:], scalar1=-1.0, scalar2=1.0, op0=ALU.mult, op1=ALU.add),
         r=["lb_bc"], w=["oml_bc"])
    S.op("dve", lambda e: e.tensor_tensor(out=oml_col[:], in0=lblc[:, 1:2], in1=lblc[:, 0:1], op=ALU.subtract), r=["lblc"], w=["oml_col"])
    S.op("act", lambda e: e.activation(out=oml_col[:], in_=oml_col[:], func=AF.Sigmoid), w=["oml_col"])
    S.op("dve", lambda e: e.tensor_scalar(out=oml_col[:], in0=oml_col[:], scalar1=-float(lb_coef), scalar2=1.0, op0=ALU.mult, op1=ALU.add),
         w=["oml_col"])
    for g in range(NT // 8 if NT >= 8 else 1):
        nt = min(8, NT)
        sl = slice(g * 8, g * 8 + nt)
        S.op("dve", lambda e, sl=sl, nt=nt: e.tensor_tensor(out=sf[:, sl, :], in0=sf[:, sl, :],
                                                        in1=oml_bc[:].unsqueeze(1).broadcast_to([128, nt, 64]), op=ALU.mult),
             r=["oml_bc"], w=["sf"])
        S.op("dve", lambda e, sl=sl, nt=nt: e.tensor_tensor(out=sf[:, sl, :], in0=sf[:, sl, :],
                                                        in1=lb_bc[:].unsqueeze(1).broadcast_to([128, nt, 64]), op=ALU.add),
             r=["lb_bc"], w=["sf"])
        S.op("act", lambda e, sl=sl: e.activation(out=logf[:, sl, :], in_=sf[:, sl, :], func=AF.Ln), r=["sf"], w=[("logf", g)])
        S.op("pool", lambda e, sl=sl: e.tensor_scalar(out=omf[:, sl, :], in0=sf[:, sl, :], scalar1=-1.0, scalar2=1.0,
                                                     op0=ALU.mult, op1=ALU.add), r=["sf"], w=[("omf", g)])
    def stageA1(i):
        g = i // 8
        b = i % 3
        S.op("pe", lambda e, i=i: e.matmul(ps_c[0:64, 0:136], logf[:, i, :], mcat[:], start=True, stop=True),
             r=[("logf", g), "mcat"], w=["ps_c"])
        S.op("pe", lambda e, i=i: e.matmul(ps_r[:, 0:64], mrev[:], logf[:, i, :], start=True, stop=True),
             r=[("logf", g), "mrev"], w=["ps_r"])
        S.op("act", lambda e: e.activation(out=eq[:], in_=ps_c[0:64, 0:128], func=AF.Exp), w=["eq", "ps_c"])
        S.op("act", lambda e: e.activation(out=ekn[:], in_=ps_c[0:64, 0:128], func=AF.Exp, scale=-1.0), w=["ekn", "ps_c"])
        S.op("act", lambda e, b=b: e.activation(out=dec[b][:], in_=ps_c[0:64, 128:136], func=AF.Exp), w=[("dec", b), "ps_c"])
        S.op("act", lambda e: e.activation(out=ehat[:], in_=ps_r[:, 0:64], func=AF.Exp), w=["ehat", "ps_r"])

    def stageA2(i):
        g = i // 8
        b = i % 3
        ts = slice(i * 128, (i + 1) * 128)
        S.op("pool", lambda e, b=b, ts=ts: e.tensor_tensor(out=qt[b][:], in0=sq[:, ts], in1=eq[:], op=ALU.mult),
             r=["sq", "eq"], w=[("qt", b)])
        S.op("dve", lambda e, ts=ts: e.scalar_tensor_tensor(out=kt[:], in0=snf[:, ts], scalar=oml_col[:, 0:1], in1=ekn[:],
                                                           op0=ALU.mult, op1=ALU.mult), r=["snf", "ekn", "oml_col"], w=["kt"])
        S.op("pool", lambda e, i=i: e.tensor_tensor(out=khat[:], in0=omf[:, i, :], in1=ehat[:], op=ALU.mult),
             r=[("omf", g), "ehat"], w=["khat"])
        S.op("pool", lambda e, i=i: e.tensor_tensor(out=vblk[:], in0=vall[:, i, :].unsqueeze(1).broadcast_to([128, 8, 64]),
                                                   in1=mtot_bf[:].unsqueeze(2).broadcast_to([128, 8, 64]), op=ALU.mult),
             r=["vall", "mtot_bf"], w=["vblk"])
        S.op("pe", lambda e, b=b: e.matmul(ps_sc[:, 0:128], kt[:], qt[b][:], start=True, stop=True), r=["kt", ("qt", b)], w=["ps_sc"])
        S.op("dve", lambda e, b=b: e.tensor_tensor(out=scm[b][:], in0=ps_sc[:, 0:128], in1=mincl_bf[:], op=ALU.mult),
             r=["mincl_bf"], w=[("scm", b), "ps_sc"])
        S.op("pe", lambda e, b=b: e.matmul(ps_u[b][0:64, :], khat[:], vblk[:].rearrange("p a c -> p (a c)"), start=True, stop=True),
             r=["khat", "vblk"], w=[("ps_u", b)])

    def stageB(i):
        b = i % 3
        sb = i % 2
        if i > 0:
            S.op("dve", lambda e, sb=sb: e.tensor_copy(out=Sall[sb][:, 0, :], in_=Sall[1 - sb][:, 8, :]), r=[("S", 1 - sb)], w=[("S", sb)])
        for n in range(8):
            S.op("dve", lambda e, b=b, sb=sb, n=n: e.scalar_tensor_tensor(
                out=Sall[sb][:, n + 1, :], in0=Sall[sb][:, n, :], scalar=dec[b][:, n:n + 1], in1=ps_u[b][0:64, n * 64:(n + 1) * 64],
                op0=ALU.mult, op1=ALU.add), r=[("dec", b)], w=[("S", sb), ("ps_u", b)])
        S.op("act", lambda e, sb=sb: e.activation(out=Sbf[sb][:], in_=Sall[sb][:, 0:8, :], func=AF.Copy), r=[("S", sb)], w=[("Sbf", sb)])

    def stageC(i):
        b = i % 3
        sb = i % 2
        ob = (i // 4) % 2
        c0 = (i % 4) * 128
        S.op("pe", lambda e, i=i, ob=ob, c0=c0, b=b: e.matmul(ps_oh[ob][0:64, c0:c0 + 128], vall[:, i, :], scm[b][:], start=True, stop=False),
             r=["vall", ("scm", b)], w=[("ps_oh", ob)])
        for n in range(8):
            S.op("pe", lambda e, b=b, sb=sb, ob=ob, c0=c0, n=n: e.matmul(
                ps_oh[ob][0:64, c0 + 16 * n:c0 + 16 * n + 16], Sbf[sb][:, n, :], qt[b][:, 16 * n:16 * n + 16],
                start=False, stop=(n == 7)), r=[("Sbf", sb), ("qt", b)], w=[("ps_oh", ob)])
        if i % 4 == 3 or i == NT - 1:
            qs = slice((i // 4) * 512, (i // 4) * 512 + 512)
            S.op("act", lambda e, ob=ob: e.activation(out=osq[:], in_=ps_oh[ob][0:64, :], func=AF.Square), w=["osq", ("ps_oh", ob)])
            S.op("act", lambda e, ob=ob: e.activation(out=o32[:], in_=ps_oh[ob][0:64, :], func=AF.Copy), w=["o32", ("ps_oh", ob)])
            S.op("pe", lambda e: e.matmul(ps_sc[0:64, :], ones64[:], osq[:], start=True, stop=True), r=["osq", "ones64"], w=["ps_sc"])
            S.op("act", lambda e: e.activation(out=nsq[:], in_=ps_sc[0:64, :], func=AF.Ln, scale=1.0 / 64.0, bias=epsc[:, 0:1]), r=["epsc"], w=["nsq", "ps_sc"])
            S.op("act", lambda e: e.activation(out=nsq[:], in_=nsq[:], func=AF.Exp, scale=-0.5), w=["nsq"])
            S.op("dve", lambda e: e.scalar_tensor_tensor(out=o32[:], in0=o32[:], scalar=gw[:, 0:1], in1=nsq[:], op0=ALU.mult, op1=ALU.mult),
                 r=["nsq", "gw"], w=["o32"])
            S.op("pool", lambda e, ob=ob, qs=qs: e.tensor_tensor(out=ohb[ob][:], in0=o32[:], in1=shg[:, qs], op=ALU.mult),
                 r=["o32", "shg"], w=[("ohb", ob)])
            S.op("sp", lambda e, ob=ob, qs=qs: e.dma_start(out=oh[:, qs], in_=ohb[ob][:]), r=[("ohb", ob)], dma=True)
    for i0_ in range(min(2, NT)):
        stageA1(i0_)
        stageA2(i0_)
    for i in range(NT):
        if i + 2 < NT:
            stageA1(i + 2)
        stageB(i)
        if i + 2 < NT:
            stageA2(i + 2)
        stageC(i)


def build_hgrn(lb_coef, T=SEQ):
    nc = bass.Bass("TRN2", target_bir_lowering=False)
    fm = nc.dram_tensor("fm", [NFM, T], BF16, kind="ExternalInput").ap()
    tm_sf = nc.dram_tensor("tm_sf", [T, 64], F32, kind="ExternalInput").ap()
    tm_v = nc.dram_tensor("tm_v", [T, 192], BF16, kind="ExternalInput").ap()
    lbl_bc = nc.dram_tensor("lbl_bc", [1, 128], F32, kind="ExternalInput").ap()
    lbl_col = nc.dram_tensor("lbl_col", [64, 2], F32, kind="ExternalInput").ap()
    gw = nc.dram_tensor("gw", [64, 1], F32, kind="ExternalInput").ap()
    mcat = nc.dram_tensor("mcat", [128, 136], F32, kind="ExternalInput").ap()
    mrev = nc.dram_tensor("mrev", [128, 128], F32, kind="ExternalInput").ap()
    oh = nc.dram_tensor("oh", [64, T], BF16, kind="ExternalOutput").ap()
    with contextlib.ExitStack() as st:
        S = Sched(nc)
        phase_hgrn(nc, S, st, fm, tm_sf, tm_v, lbl_bc, lbl_col, gw, mcat, mrev, oh, lb_coef, T)
        S.emit()
    return nc


def cmul(S, eng, o_re, o_im, a_re, a_im, b_re, b_im, t0, t1, rd, wr, conj_a=False):
    sg = -1.0 if conj_a else 1.0
    S.op(eng, lambda e: e.tensor_tensor(out=t0, in0=a_im, in1=b_im, op=ALU.mult), r=rd, w=[wr + "t0"])
    S.op(eng, lambda e: e.tensor_tensor(out=t1, in0=a_re, in1=b_re, op=ALU.mult), r=rd, w=[wr + "t1"])
    S.op("dve", lambda e: e.scalar_tensor_tensor(out=o_re, in0=t0, scalar=-sg, in1=t1, op0=ALU.mult, op1=ALU.add),
         r=[wr + "t0", wr + "t1"], w=[wr + "re"])
    S.op(eng, lambda e: e.tensor_tensor(out=t0, in0=a_im, in1=b_re, op=ALU.mult), r=rd + [wr + "re"], w=[wr + "t0"])
    S.op(eng, lambda e: e.tensor_tensor(out=t1, in0=a_re, in1=b_im, op=ALU.mult), r=rd + [wr + "re"], w=[wr + "t1"])
    S.op("dve", lambda e: e.scalar_tensor_tensor(out=o_im, in0=t0, scalar=sg, in1=t1, op0=ALU.mult, op1=ALU.add),
         r=[wr + "t0", wr + "t1"], w=[wr + "im"])


def s5_consts():
    negsig = np.repeat(-np.arange(16, dtype=np.float32), 64)[None, :]
    kidx = np.arange(32, dtype=np.float32)[None, :]
    midx = np.arange(1, 513, dtype=np.float32)[None, :]
    rowmask = (np.arange(64)[:, None] // 16 == np.arange(4)[None, :]).astype(np.float32)
    return negsig, kidx, midx, rowmask


def s5_params(z, l, j):
    gs = [4 * j + gl for gl in range(4)]
    f = np.float32
    pA_are = np.concatenate([np.repeat(z["s5_a_re"][l][g][None, :], 16, 0) for g in gs]).astype(f)
    pA_aim = np.concatenate([np.repeat(z["s5_a_im"][l][g][None, :], 16, 0) for g in gs]).astype(f)
    pA_ldt = np.concatenate([np.full((16, 1), z["s5_log_dt"][l][g]) for g in gs]).astype(f)
    pA_bre = np.concatenate([z["s5_b_re"][l][g].T for g in gs]).astype(f)
    pA_bim = np.concatenate([z["s5_b_im"][l][g].T for g in gs]).astype(f)
    pB = np.zeros((2, 128, 3), f)
    pB_cre = np.zeros((2, 128, 64), f)
    pB_cim = np.zeros((2, 128, 64), f)
    for q in range(2):
        for h in range(2):
            gl = 2 * q + h
            g = gs[gl]
            rows = slice(64 * h, 64 * h + 64)
            pB[q, rows, 0] = z["s5_a_re"][l][g]
            pB[q, rows, 1] = z["s5_a_im"][l][g]
            pB[q, rows, 2] = z["s5_log_dt"][l][g]
            pB_cre[q, rows, 16 * gl:16 * gl + 16] = z["s5_c_re"][l][g].T
            pB_cim[q, rows, 16 * gl:16 * gl + 16] = z["s5_c_im"][l][g].T
    dcol = z["s5_d"][l][64 * j:64 * j + 64][:, None].astype(f)
    pA = np.concatenate([pA_are, pA_aim, pA_bre, pA_bim, pA_ldt], axis=1)
    return {"s5p_pA": np.ascontiguousarray(pA), "s5p_pB": pB, "s5p_cre": pB_cre, "s5p_cim": pB_cim, "s5p_d": dcol}


def phase_s5(nc, S, st, fm, pA_d, pB_d, cre_d, cim_d, dcol_d, negsig_d, kidx_d, midx_d, rowmask_d, yg, T=SEQ):
    TS = lambda n, s, d: st.enter_context(nc.sbuf_tensor(n, s, d))
    PS = lambda n: st.enter_context(nc.psum_tensor(n, [128, 512], F32))
    NB = T // 16
    su = TS("s5_su", [64, T], BF16)
    outsb = TS("s5_out", [64, T], BF16)
    pA = TS("s5_pA", [64, 257], F32)
    dcol = TS("s5_dcol", [64, 1], F32)
    rowmask = TS("s5_rowmask", [64, 4], F32)
    negsig = TS("s5_negsig", [64, 1024], F32)
    SCR = TS("s5_scr", [128, 8192], F32)
    tA = [SCR[0:64, 1024 * i:1024 * (i + 1)] for i in range(8)]
    tAi = TS("s5_tAi", [64, 1024], I32)
    sA = [TS(f"s5_sA{i}", [64, 64], F32) for i in range(10)]
    sAi = TS("s5_sAi", [64, 64], I32)
    dtA = TS("s5_dtA", [64, 1], F32)
    W1tab = [[TS(f"s5_W1tab{q}{ri}", [64, 16, 128], BF16) for ri in range(2)] for q in range(2)]
    pB = [TS(f"s5_pB{q}", [128, 3], F32) for q in range(2)]
    crep = [TS(f"s5_crep{q}", [128, 64], F32) for q in range(2)]
    cimp = [TS(f"s5_cimp{q}", [128, 64], F32) for q in range(2)]
    kidx = TS("s5_kidx", [128, 32], F32)
    midx = TS("s5_midx", [128, 512], F32)
    tB = [TS(f"s5_tB{i}", [128, 32], F32) for i in range(7)]
    tBi = TS("s5_tBi", [128, 32], I32)
    cB = [TS(f"s5_cB{i}", [128, 1], F32) for i in range(6)]
    cBi = TS("s5_cBi", [128, 1], I32)
    gt = [SCR[:, 2048 * i:2048 * (i + 1)].rearrange("p (k c) -> p k c", k=32) for i in range(2)]
    Gpad = [[TS(f"s5_G{q}{ri}", [128, 32, 64], BF16) for ri in range(2)] for q in range(2)]
    Tc = [TS(f"s5_Tc{q}", [128, 512], F32) for q in range(2)]
    Tsn = [TS(f"s5_Ts{q}", [128, 512], F32) for q in range(2)]
    rho = [TS(f"s5_rho{q}", [128, 1], F32) for q in range(2)]
    l2 = [SCR[:, 4096 + 512 * i:4096 + 512 * (i + 1)] for i in range(6)]
    l2i = TS("s5_l2i", [128, 512], I32)
    roll = [TS(f"s5_roll{i}", [128, 512], F32) for i in range(2)]
    W15 = [[TS(f"s5_W15{q}{ri}", [128, 512], F32) for ri in range(2)] for q in range(2)]
    W1bf = [[TS(f"s5_W1bf{q}{ri}", [128, 16, 512], BF16) for ri in range(2)] for q in range(2)]
    Xbf = [[TS(f"s5_Xbf{q}{ri}", [128, 512], BF16) for ri in range(2)] for q in range(2)]
    ytmp = [TS(f"s5_ytmp{i}", [64, 512], F32) for i in range(2)]
    ps_z = [PS(f"s5_ps_z{i}") for i in range(2)]
    ps_y = [PS(f"s5_ps_y{i}") for i in range(2)]

    ld = lambda eng, dst, src, name: S.op(eng, lambda e: e.dma_start(out=dst, in_=src), w=[name], dma=True)
    ld("sp", su[:], fm[256:320, :], "su")
    ld("sp", pA[:], pA_d[:, :], "pA")
    ld("sp", dcol[:], dcol_d[:, :], "dcol")
    ld("sp", rowmask[:], rowmask_d[:, :], "rowmask")
    ld("sp", negsig[:], negsig_d.partition_broadcast(64), "negsig")
    ld("sp", kidx[:], kidx_d.partition_broadcast(128), "kidx")
    ld("sp", midx[:], midx_d.partition_broadcast(128), "midx")
    for q in range(2):
        ld("sp", pB[q][:], pB_d[q], ("pB", q))
        ld("sp", crep[q][:], cre_d[q], ("crep", q))
        ld("sp", cimp[q][:], cim_d[q], ("cimp", q))
    are, aim, bre, bim, ldt = pA[:, 0:64], pA[:, 64:128], pA[:, 128:192], pA[:, 192:256], pA[:, 256:257]
    lam, th, abr, abi, mg, zr, zi, den, u0, u1 = [t[:] for t in sA]
    S.op("act", lambda e: e.activation(out=dtA[:], in_=ldt, func=AF.Exp), r=["pA"], w=["dtA"])
    S.op("dve", lambda e: e.tensor_scalar(out=lam, in0=are, scalar1=dtA[:, 0:1], scalar2=None, op0=ALU.mult), r=["pA", "dtA"], w=["lamA"])
    S.op("dve", lambda e: e.tensor_scalar(out=th, in0=aim, scalar1=dtA[:, 0:1], scalar2=None, op0=ALU.mult), r=["pA", "dtA"], w=["thA"])
    S.op("dve", lambda e: e.tensor_copy(out=u0, in_=th), r=["thA"], w=["sAang"])
    sincos(S, u0, u1, sAi[:], den, abi, abr, "sA")
    S.op("act", lambda e: e.activation(out=mg, in_=lam, func=AF.Exp), r=["lamA"], w=["mgA"])
    S.op("dve", lambda e: e.tensor_tensor(out=abr, in0=abr, in1=mg, op=ALU.mult), r=["mgA", "sAcos"], w=["abr"])
    S.op("dve", lambda e: e.tensor_tensor(out=abi, in0=abi, in1=mg, op=ALU.mult), r=["mgA", "sAsin"], w=["abi"])
    S.op("dve", lambda e: e.tensor_scalar(out=abr, in0=abr, scalar1=-1.0, scalar2=None, op0=ALU.add), w=["abr"])
    S.op("dve", lambda e: e.tensor_tensor(out=den, in0=are, in1=are, op=ALU.mult), r=["pA", "sAcos", "sAsin"], w=["den"])
    S.op("dve", lambda e: e.tensor_tensor(out=u0, in0=aim, in1=aim, op=ALU.mult), r=["pA", "sAsin"], w=["u0"])
    S.op("dve", lambda e: e.tensor_tensor(out=den, in0=den, in1=u0, op=ALU.add), r=["u0"], w=["den"])
    S.op("dve", lambda e: e.reciprocal(out=den, in_=den), w=["den"])
    S.op("dve", lambda e: e.tensor_tensor(out=u0, in0=abr, in1=are, op=ALU.mult), r=["abr"], w=["u0"])
    S.op("dve", lambda e: e.tensor_tensor(out=u1, in0=abi, in1=aim, op=ALU.mult), r=["abi"], w=["u1"])
    S.op("dve", lambda e: e.tensor_tensor(out=zr, in0=u0, in1=u1, op=ALU.add), r=["u0", "u1"], w=["zr"])
    S.op("dve", lambda e: e.tensor_tensor(out=zr, in0=zr, in1=den, op=ALU.mult), r=["den"], w=["zr"])
    S.op("dve", lambda e: e.tensor_tensor(out=u0, in0=abi, in1=are, op=ALU.mult), r=["abi", "zr"], w=["u0"])
    S.op("dve", lambda e: e.tensor_tensor(out=u1, in0=abr, in1=aim, op=ALU.mult), r=["abr", "zr"], w=["u1"])
    S.op("dve", lambda e: e.tensor_tensor(out=zi, in0=u0, in1=u1, op=ALU.subtract), r=["u0", "u1"], w=["zi"])
    S.op("dve", lambda e: e.tensor_tensor(out=zi, in0=zi, in1=den, op=ALU.mult), r=["den"], w=["zi"])
    A3 = lambda t: t[:].rearrange("p (s m) -> p s m", s=16)
    bc3 = lambda ap: ap.unsqueeze(1).broadcast_to([64, 16, 64])
    ang3, kf3, hs3, sn3, cs3, mg3, w_r, w_i = tA
    S.op("dve", lambda e: e.tensor_tensor(out=A3(ang3), in0=A3(negsig), in1=bc3(th), op=ALU.mult), r=["negsig", "thA"], w=["tAang"])
    sincos(S, ang3[:], kf3[:], tAi[:], hs3[:], sn3[:], cs3[:], "tA")
    S.op("dve", lambda e: e.tensor_tensor(out=A3(mg3), in0=A3(negsig), in1=bc3(lam), op=ALU.mult), r=["negsig", "lamA"], w=["mg3"])
    S.op("act", lambda e: e.activation(out=mg3[:], in_=mg3[:], func=AF.Exp), w=["mg3"])
    S.op("dve", lambda e: e.tensor_tensor(out=cs3[:], in0=cs3[:], in1=mg3[:], op=ALU.mult), r=["mg3"], w=["tAcos"])
    S.op("dve", lambda e: e.tensor_tensor(out=sn3[:], in0=sn3[:], in1=mg3[:], op=ALU.mult), r=["mg3"], w=["tAsin"])
    cmul(S, "dve", A3(w_r), A3(w_i), A3(cs3), A3(sn3), bc3(zr), bc3(zi), A3(ang3), A3(kf3),
         ["tAcos", "tAsin", "zr", "zi", "tAang", "tAkf"], "wz")
    cmul(S, "dve", A3(cs3), A3(sn3), A3(w_r), A3(w_i), bc3(bre), bc3(bim), A3(ang3), A3(kf3),
         ["wzre", "wzim", "pA", "tAcos", "tAsin"], "Bs")
    for q in range(2):
        for ri, src in ((0, cs3), (1, sn3)):
            for h in range(2):
                gl = 2 * q + h
                S.op("dve", lambda e, q=q, ri=ri, h=h, gl=gl, src=src: e.tensor_scalar(
                    out=W1tab[q][ri][:, :, 64 * h:64 * h + 64], in0=A3(src), scalar1=rowmask[:, gl:gl + 1], scalar2=None, op0=ALU.mult),
                    r=["Bsre", "Bsim", "rowmask"], w=[("W1tab", q, ri, h)])
    S.barrier()
    bq = []

    class _Defer:
        def op(self, *a, **k):
            bq.append((a, k))
    SB = _Defer()
    for q in range(2):
        lamB, thB, dtB, phi, th15, junk = [t[:] for t in cB]
        angk, kfk, hsk, snk, csk, mgk, nsk = [t[:] for t in tB]
        pq = [("pB", q)]
        tg = f"B{q}"
        SB.op("act", lambda e, q=q: e.activation(out=dtB, in_=pB[q][:, 2:3], func=AF.Exp), r=pq, w=[tg + "dt"])
        SB.op("dve", lambda e, q=q: e.tensor_tensor(out=lamB, in0=pB[q][:, 0:1], in1=dtB, op=ALU.mult), r=pq + [tg + "dt"], w=[tg + "lam"])
        SB.op("dve", lambda e, q=q: e.tensor_tensor(out=thB, in0=pB[q][:, 1:2], in1=dtB, op=ALU.mult), r=pq + [tg + "dt"], w=[tg + "th"])
        SB.op("dve", lambda e: e.tensor_scalar(out=angk, in0=kidx[:], scalar1=thB[:, 0:1], scalar2=None, op0=ALU.mult),
             r=["kidx", tg + "th"], w=[tg + "kang"])
        sincos(SB, angk, kfk, tBi[:], hsk, snk, csk, tg + "k")
        SB.op("dve", lambda e: e.tensor_scalar(out=mgk, in0=kidx[:], scalar1=lamB[:, 0:1], scalar2=None, op0=ALU.mult),
             r=["kidx", tg + "lam"], w=[tg + "mgk"])
        SB.op("act", lambda e: e.activation(out=mgk, in_=mgk, func=AF.Exp), w=[tg + "mgk"])
        SB.op("dve", lambda e: e.tensor_tensor(out=csk, in0=csk, in1=mgk, op=ALU.mult), r=[tg + "mgk"], w=[tg + "kcos"])
        SB.op("dve", lambda e: e.tensor_tensor(out=snk, in0=snk, in1=mgk, op=ALU.mult), r=[tg + "mgk"], w=[tg + "ksin"])
        SB.op("dve", lambda e: e.tensor_scalar(out=nsk, in0=snk, scalar1=-1.0, scalar2=None, op0=ALU.mult), r=[tg + "ksin"], w=[tg + "nsk"])
        SB.op("dve", lambda e: e.tensor_scalar(out=kfk, in0=csk, scalar1=-1.0, scalar2=None, op0=ALU.mult), r=[tg + "kcos"], w=[tg + "kkf"])
        kb = lambda ap: ap.unsqueeze(2).broadcast_to([128, 32, 64])
        cb = lambda t: t[:].unsqueeze(1).broadcast_to([128, 32, 64])
        for ri, (f1, f2) in enumerate(((csk, nsk), (nsk, kfk))):
            SB.op("dve", lambda e, q=q, f1=f1: e.tensor_tensor(out=gt[0][:], in0=cb(crep[q]), in1=kb(f1), op=ALU.mult),
                 r=[("crep", q), tg + "kcos", tg + "nsk", tg + "kkf"], w=["gt0"])
            SB.op("dve", lambda e, q=q, f2=f2: e.tensor_tensor(out=gt[1][:], in0=cb(cimp[q]), in1=kb(f2), op=ALU.mult),
                 r=[("cimp", q), tg + "kcos", tg + "nsk", tg + "kkf"], w=["gt1"])
            SB.op("dve", lambda e, q=q, ri=ri: e.tensor_tensor(out=Gpad[q][ri][:], in0=gt[0][:], in1=gt[1][:], op=ALU.add),
                 r=["gt0", "gt1"], w=[("Gpad", q, ri)])
        SB.op("dve", lambda e: e.tensor_scalar(out=phi, in0=thB, scalar1=16.0, scalar2=None, op0=ALU.mult), r=[tg + "th"], w=[tg + "phi"])
        SB.op("dve", lambda e: e.tensor_scalar(out=th15, in0=phi, scalar1=1.0 / (2.0 * math.pi), scalar2=None, op0=ALU.mult),
             r=[tg + "phi"], w=[tg + "th15"])
        SB.op("dve", lambda e: e.tensor_copy(out=cBi[:], in_=th15), r=[tg + "th15"], w=[tg + "cBi"])
        SB.op("dve", lambda e: e.tensor_copy(out=th15, in_=cBi[:]), r=[tg + "cBi"], w=[tg + "th15"])
        SB.op("dve", lambda e: e.scalar_tensor_tensor(out=phi, in0=th15, scalar=-C1_2PI, in1=phi, op0=ALU.mult, op1=ALU.add),
             r=[tg + "th15"], w=[tg + "phi"])
        SB.op("dve", lambda e: e.scalar_tensor_tensor(out=phi, in0=th15, scalar=-C2_2PI, in1=phi, op0=ALU.mult, op1=ALU.add),
             r=[tg + "th15"], w=[tg + "phi"])
        SB.op("dve", lambda e: e.tensor_scalar(out=l2[0][:], in0=midx[:], scalar1=phi[:, 0:1], scalar2=None, op0=ALU.mult),
             r=["midx", tg + "phi"], w=["l2ang"])
        sincos(SB, l2[0][:], l2[1][:], l2i[:], l2[2][:], Tsn[q][:], Tc[q][:], "l2")
        SB.op("dve", lambda e, q=q: e.tensor_copy(out=Tsn[q][:], in_=Tsn[q][:]), r=["l2sin"], w=[("Ts", q)])
        SB.op("dve", lambda e, q=q: e.tensor_copy(out=Tc[q][:], in_=Tc[q][:]), r=["l2cos"], w=[("Tc", q)])
        SB.op("act", lambda e, q=q: e.activation(out=rho[q][:], in_=lamB, func=AF.Exp, scale=16.0), r=[tg + "lam"], w=[("rho", q)])
    suv = su[:].rearrange("p (m s) -> p s m", s=16)
    zi_ = 0
    for q in range(2):
        for ri in range(2):
            for s in range(16):
                pb = zi_ % 2
                zi_ += 1
                S.op("pe", lambda e, q=q, ri=ri, s=s, pb=pb: e.matmul(ps_z[pb][:, 0:NB], W1tab[q][ri][:, s, :], suv[:, s, :], start=True, stop=True),
                     r=["su", ("W1tab", q, ri, 0), ("W1tab", q, ri, 1)], w=[("ps_z", pb)])
                dst = W15[q][ri] if s == 15 else roll[s % 2]
                dn = ("W15", q, ri) if s == 15 else ("roll", s % 2)
                if s == 0:
                    S.op("dve", lambda e, pb=pb, dst=dst: e.tensor_copy(out=dst[:, 0:NB], in_=ps_z[pb][:, 0:NB]), w=[dn, ("ps_z", pb)])
                else:
                    S.op("dve", lambda e, pb=pb, dst=dst, s=s: e.tensor_tensor(out=dst[:, 0:NB], in0=ps_z[pb][:, 0:NB],
                                                                          in1=roll[(s - 1) % 2][:, 0:NB], op=ALU.add),
                         r=[("roll", (s - 1) % 2)], w=[dn, ("ps_z", pb)])
                S.op("act", lambda e, q=q, ri=ri, s=s, dst=dst: e.activation(out=W1bf[q][ri][:, s, 0:NB], in_=dst[:, 0:NB], func=AF.Copy),
                     r=[dn], w=[("W1bf", q, ri, s)])
                for _ in range(3):
                    if bq:
                        a, k = bq.pop(0)
                        S.op(*a, **k)
    while bq:
        a, k = bq.pop(0)
        S.op(*a, **k)
    for q in range(2):
        ur, ui, t0, t1, vr, vi = [t[:, 0:NB] for t in l2]
        tc, tsn = Tc[q][:, 0:NB], Tsn[q][:, 0:NB]
        wre, wim = W15[q][0][:, 0:NB], W15[q][1][:, 0:NB]
        cmul(S, "dve", ur, ui, tc, tsn, wre, wim, t0, t1, [("Tc", q), ("Ts", q), ("W15", q, 0), ("W15", q, 1), "l2v"], "l2u", conj_a=True)
        rb = rho[q][:, 0:1].broadcast_to([128, NB])
        S.op("dve", lambda e, rb=rb: e.tensor_tensor_scan(out=vr, data0=rb, data1=ur, initial=0.0, op0=ALU.mult, op1=ALU.add),
             r=["l2ure", ("rho", q)], w=["l2vr"])
        S.op("dve", lambda e, rb=rb: e.tensor_tensor_scan(out=vi, data0=rb, data1=ui, initial=0.0, op0=ALU.mult, op1=ALU.add),
             r=["l2uim", ("rho", q)], w=["l2vi"])
        cmul(S, "dve", ur, ui, tc, tsn, vr, vi, t0, t1, [("Tc", q), ("Ts", q), "l2vr", "l2vi"], "l2x")
        for ri, src in ((0, ur), (1, ui)):
            S.op("pool", lambda e, q=q, ri=ri: e.memset(Xbf[q][ri][:, 0:1], 0.0), w=[("Xbf", q, ri)])
            if NB > 1:
                S.op("act", lambda e, q=q, ri=ri, src=src: e.activation(out=Xbf[q][ri][:, 1:NB], in_=src[:, 0:NB - 1], func=AF.Copy),
                     r=["l2xre", "l2xim"], w=[("Xbf", q, ri)])
        S.op("dve", lambda e: e.tensor_copy(out=l2[0][:, 0:1], in_=l2[0][:, 0:1]), r=[("Xbf", q, 0), ("Xbf", q, 1)], w=["l2v", "l2ure", "l2uim"])
    outv = outsb[:].rearrange("p (m s) -> p s m", s=16)
    for s in range(16):
        pb = s % 2
        k = 0
        for q in range(2):
            for ri in range(2):
                S.op("pe", lambda e, q=q, ri=ri, s=s, pb=pb, k=k: e.matmul(ps_y[pb][0:64, 0:NB], Gpad[q][ri][:, s, :], W1bf[q][ri][:, s, 0:NB],
                                                                     start=(k == 0), stop=False),
                     r=[("Gpad", q, ri), ("W1bf", q, ri, s)], w=[("ps_y", pb)])
                k += 1
        for q in range(2):
            for ri in range(2):
                S.op("pe", lambda e, q=q, ri=ri, s=s, pb=pb, k=k: e.matmul(ps_y[pb][0:64, 0:NB], Gpad[q][ri][:, s + 16, :], Xbf[q][ri][:, 0:NB],
                                                                     start=False, stop=(k == 7)),
                     r=[("Gpad", q, ri), ("Xbf", q, ri)], w=[("ps_y", pb)])
                k += 1
        S.op("dve", lambda e, s=s, pb=pb: e.scalar_tensor_tensor(out=ytmp[pb][:, 0:NB], in0=suv[:, s, :], scalar=dcol[:, 0:1],
                                                            in1=ps_y[pb][0:64, 0:NB], op0=ALU.mult, op1=ALU.add),
             r=["su", "dcol"], w=[("ytmp", pb), ("ps_y", pb)])
        S.op("act", lambda e, s=s, pb=pb: e.activation(out=outv[:, s, :], in_=ytmp[pb][:, 0:NB], func=AF.Gelu),
             r=[("ytmp", pb)], w=[("outsb", s)])
    S.op("sp", lambda e: e.dma_start(out=yg[:, :], in_=outsb[:]), r=[("outsb", s) for s in range(16)], dma=True)


def build_s5(T=SEQ):
    nc = bass.Bass("TRN2", target_bir_lowering=False)
    D = lambda n, s, d=F32, k="ExternalInput": nc.dram_tensor(n, s, d, kind=k).ap()
    fm = D("fm", [NFM, T], BF16)
    pA = D("s5p_pA", [64, 257]); pB = D("s5p_pB", [2, 128, 3]); cre = D("s5p_cre", [2, 128, 64]); cim = D("s5p_cim", [2, 128, 64])
    dcol = D("s5p_d", [64, 1]); negsig = D("negsig", [1, 1024]); kidx = D("kidx", [1, 32]); midx = D("midx", [1, 512])
    rowmask = D("rowmask", [64, 4])
    yg = D("yg", [64, T], BF16, "ExternalOutput")
    with contextlib.ExitStack() as st:
        S = Sched(nc)
        phase_s5(nc, S, st, fm, pA, pB, cre, cim, dcol, negsig, kidx, midx, rowmask, yg, T)
        S.emit()
    return nc


def phase_out(nc, S, st, mixin, ssg_d, hT, wout_d, gluw_d, glub_d, fnw_d, hout, final, NTOK=TQ):
    TS = lambda n, s, d: st.enter_context(nc.sbuf_tensor(n, s, d))
    PS = lambda n: st.enter_context(nc.psum_tensor(n, [128, 512], F32))
    wst = [TS(f"po_wst{i}", [128, 1024], F32) for i in range(2)]
    wout = TS("po_wout", [128, 8, 1024], BF16)
    gst = TS("po_gst", [128, 2, 256], F32)
    gluw = TS("po_gluw", [128, 2, 256], BF16)
    glub = TS("po_glub", [128, 2], F32)
    fnw = TS("po_fnw", [128, 8], F32)
    ones = TS("po_ones", [128, 128], BF16)
    mix = [TS(f"po_mix{i}", [128, 8, 512], BF16) for i in range(2)]
    ssg = [TS(f"po_ssg{i}", [128, 2, 512], BF16) for i in range(2)]
    hin = [TS(f"po_hin{i}", [128, 8, 512], F32) for i in range(2)]
    sg = TS("po_sg", [128, 512], F32)
    osb = TS("po_osb", [128, 2, 512], BF16)
    hn = TS("po_hn", [128, 8, 512], F32)
    hsq = TS("po_hsq", [128, 8, 512], BF16)
    nsq = TS("po_nsq", [128, 512], F32)
    ps_g = PS("po_ps_g")
    ps_o = [PS(f"po_ps_o{i}") for i in range(3)]
    ps_n = PS("po_ps_n")

    S.op("pool", lambda e: e.memset(ones[:], 1.0), w=["ones"])
    S.op("sp", lambda e: e.dma_start(out=gst[:], in_=gluw_d.rearrange("(k p) o -> p k o", p=128)), w=["gst"], dma=True)
    S.op("sp", lambda e: e.dma_start(out=glub[:], in_=glub_d[:, :]), w=["glub"], dma=True)
    S.op("sp", lambda e: e.dma_start(out=fnw[:], in_=fnw_d[:, :]), w=["fnw"], dma=True)
    S.op("dve", lambda e: e.tensor_copy(out=gluw[:], in_=gst[:]), r=["gst"], w=["gluw"])
    for k in range(8):
        S.op("sp", lambda e, k=k: e.dma_start(out=wst[k % 2][:], in_=wout_d[k * 128:(k + 1) * 128, :]), w=[("wst", k % 2)], dma=True)
        S.op("pool" if k % 2 else "dve", lambda e, k=k: e.tensor_copy(out=wout[:, k, :], in_=wst[k % 2][:]), r=[("wst", k % 2)], w=[("wout", k)])
    wr = [("wout", k) for k in range(8)]
    mv = mixin.rearrange("(k p) t -> p k t", p=128)
    sv = ssg_d.rearrange("(k p) t -> p k t", p=128)
    hv = hT.rearrange("(k p) t -> p k t", p=128)
    ov = hout.rearrange("(k p) t -> p k t", p=128)
    oi = 0
    for ti in range(NTOK // 512):
        b = ti % 2
        ts = slice(ti * 512, (ti + 1) * 512)
        S.op("sp", lambda e, b=b, ts=ts: e.dma_start(out=mix[b][:], in_=mv[:, :, ts]), w=[("mix", b)], dma=True)
        S.op("sp", lambda e, b=b, ts=ts: e.dma_start(out=ssg[b][:], in_=sv[:, :, ts]), w=[("ssg", b)], dma=True)
        S.op("pool", lambda e, b=b, ts=ts: e.dma_start(out=hin[b][:], in_=hv[:, :, ts]), w=[("hin", b)], dma=True)
        for oc in range(2):
            for kc in range(2):
                S.op("pe", lambda e, b=b, oc=oc, kc=kc: e.matmul(ps_g[:], gluw[:, kc, oc * 128:(oc + 1) * 128], mix[b][:, 2 + kc, :],
                                                             start=(kc == 0), stop=(kc == 1)), r=[("mix", b), "gluw"], w=["ps_g"])
            S.op("act", lambda e, oc=oc: e.activation(out=sg[:], in_=ps_g[:], func=AF.Sigmoid, bias=glub[:, oc:oc + 1]),
                 r=["glub"], w=["sg", "ps_g"])
            S.op("dve", lambda e, b=b, oc=oc: e.tensor_tensor(out=sg[:], in0=sg[:], in1=mix[b][:, 2 + oc, :], op=ALU.mult),
                 r=[("mix", b)], w=["sg"])
            S.op("dve", lambda e, b=b, oc=oc: e.tensor_tensor(out=osb[:, oc, :], in0=sg[:], in1=ssg[b][:, oc, :], op=ALU.mult),
                 r=[("ssg", b), "sg"], w=[("osb", oc)])
        for dc in range(8):
            pb = oi % 3
            oi += 1
            for kc in range(8):
                rhs = (lambda b=b, kc=kc: osb[:, kc - 2, :]) if kc in (2, 3) else (lambda b=b, kc=kc: mix[b][:, kc, :])
                S.op("pe", lambda e, dc=dc, kc=kc, pb=pb, rhs=rhs: e.matmul(ps_o[pb][:], wout[:, kc, dc * 128:(dc + 1) * 128], rhs(),
                                                                      start=(kc == 0), stop=(kc == 7)),
                     r=wr + [("mix", b), ("osb", 0), ("osb", 1)], w=[("ps_o", pb)])
            S.op("dve", lambda e, b=b, dc=dc, pb=pb: e.tensor_tensor(out=hn[:, dc, :], in0=ps_o[pb][:], in1=hin[b][:, dc, :], op=ALU.add),
                 r=[("hin", b)], w=[("hn", dc), ("ps_o", pb)])
            if not final:
                S.op("sp", lambda e, dc=dc, ts=ts: e.dma_start(out=ov[:, dc, ts], in_=hn[:, dc, :]), r=[("hn", dc)], dma=True)
        if final:
            hr = [("hn", dc) for dc in range(8)]
            S.op("act", lambda e: e.activation(out=hsq[:], in_=hn[:], func=AF.Square), r=hr, w=["hsq"])
            for k in range(8):
                S.op("pe", lambda e, k=k: e.matmul(ps_n[:], ones[:], hsq[:, k, :], start=(k == 0), stop=(k == 7)), r=["hsq", "ones"], w=["ps_n"])
            S.op("act", lambda e: e.activation(out=nsq[:], in_=ps_n[:], func=AF.Sqrt, scale=1.0 / D_MODEL, bias=EPS), w=["nsq", "ps_n"])
            S.op("dve", lambda e: e.reciprocal(out=nsq[:], in_=nsq[:]), w=["nsq"])
            for dc in range(8):
                S.op("pool" if dc % 2 else "dve", lambda e, dc=dc: e.scalar_tensor_tensor(
                    out=hn[:, dc, :], in0=hn[:, dc, :], scalar=fnw[:, dc:dc + 1], in1=nsq[:], op0=ALU.mult, op1=ALU.mult) if dc % 2 == 0 else
                    e.tensor_tensor(out=hn[:, dc, :], in0=hn[:, dc, :], in1=nsq[:], op=ALU.mult),
                    r=["nsq", "fnw"], w=[("hn", dc)])
                if dc % 2:
                    S.op("pool", lambda e, dc=dc: e.tensor_scalar(out=hn[:, dc, :], in0=hn[:, dc, :], scalar1=fnw[:, dc:dc + 1], scalar2=None,
                                                                  op0=ALU.mult), r=["fnw"], w=[("hn", dc)])
                S.op("sp", lambda e, dc=dc, ts=ts: e.dma_start(out=ov[:, dc, ts], in_=hn[:, dc, :]), r=[("hn", dc)], dma=True)


def build_out(final, NTOK=TQ):
    nc = bass.Bass("TRN2", target_bir_lowering=False)
    D = lambda n, s, d=F32, k="ExternalInput": nc.dram_tensor(n, s, d, kind=k).ap()
    mixin = D("mixin", [1024, NTOK], BF16)
    ssg = D("ssg", [256, NTOK], BF16)
    hT = D("hT", [D_MODEL, NTOK])
    wout = D("wout", [1024, 1024]); gluw = D("gluw", [256, 256]); glub = D("glub", [128, 2]); fnw = D("fnw", [128, 8])
    hout = D("hout", [D_MODEL, NTOK], F32, "ExternalOutput")
    with contextlib.ExitStack() as st:
        S = Sched(nc)
        phase_out(nc, S, st, mixin, ssg, hT, wout, gluw, glub, fnw, hout, final, NTOK)
        S.emit()
    return nc


_CACHE = {}


def _prog(key, fn):
    if key not in _CACHE:
        _CACHE[key] = fn()
    return _CACHE[key]


def build_mixers(l, T=SEQ, which=("ip", "at", "hg", "s5")):
    lambda_init = 0.8 - 0.6 * math.exp(-0.3 * l)
    nc = bass.Bass("TRN2", target_bir_lowering=False)
    D = lambda n, s, d=F32, k="ExternalInput": nc.dram_tensor(n, s, d, kind=k).ap()
    hT = D("hT", [D_MODEL, T]); wcat = D("wcat", [D_MODEL, NFM + NTM]); nw = D("nw", [128, 8])
    lqk = D("lqk", [1, 256]); subln = D("subln", [128, 1]); ropef = D("ropef", [128, 1]); rmat = D("rmat", [128, 128])
    cmask = D("cmask", [4, 128, 512], BF16)
    lbl_bc = D("lbl_bc", [1, 128]); lbl_col = D("lbl_col", [64, 2]); gw = D("gw", [64, 1]); mcat = D("mcat", [128, 136]); mrev = D("mrev", [128, 128])
    pA = D("s5p_pA", [64, 257]); pB = D("s5p_pB", [2, 128, 3]); cre = D("s5p_cre", [2, 128, 64]); cim = D("s5p_cim", [2, 128, 64])
    dcol = D("s5p_d", [64, 1]); negsig = D("negsig", [1, 1024]); kidx = D("kidx", [1, 32]); midx = D("midx", [1, 512]); rowmask = D("rowmask", [64, 4])
    fm = D("fm", [NFM, T], BF16, "Internal")
    tm_sf = D("tm_sf", [T, 64], F32, "Internal")
    tm_v = D("tm_v", [T, 192], BF16, "Internal")
    mo = D("mo", [320, T], BF16, "ExternalOutput")
    if "ip" in which:
        with contextlib.ExitStack() as st:
            S = Sched(nc)
            phase_inproj(nc, S, st, hT, wcat, nw, fm, tm_sf, tm_v, T)
            S.emit()
    if "at" in which:
        with contextlib.ExitStack() as st:
            S = Sched(nc)
            S.op("sp", lambda e: e.dma_start(out=mo[128:192, :], in_=fm[192:256, :]), dma=True)
            phase_attn(nc, S, st, fm, tm_v, lqk, subln, ropef, rmat, cmask, mo[192:320, :], lambda_init, T)
            S.emit()
    if "hg" in which:
        with contextlib.ExitStack() as st:
            S = Sched(nc)
            phase_hgrn(nc, S, st, fm, tm_sf, tm_v, lbl_bc, lbl_col, gw, mcat, mrev, mo[0:64, :], float(l), T)
            S.emit()
    if "s5" in which:
        with contextlib.ExitStack() as st:
            S = Sched(nc)
            phase_s5(nc, S, st, fm, pA, pB, cre, cim, dcol, negsig, kidx, midx, rowmask, mo[64:128, :], T)
            S.emit()
    return nc


def mixer_inputs(inp, l, c, hT_b):
    f = np.float32
    j = c % 4
    ropef, rmat, cmask = attn_consts()
    mcat, mrev = hgrn_consts()
    negsig, kidx, midx, rowmask = s5_consts()
    lbl = np.asarray(inp["hgrn_lb_logits"], f)[:, 64 * j:64 * j + 64]
    d = {"hT": hT_b, "wcat": np.ascontiguousarray(np.asarray(inp["w_in"][l], f)[:, core_cols(j)]),
         "nw": np.ascontiguousarray(np.asarray(inp["norm_w"][l], f).reshape(8, 128).T),
         "lqk": np.concatenate([inp["diff_lq1"][l], inp["diff_lq2"][l], inp["diff_lk1"][l], inp["diff_lk2"][l]])[None, :].astype(f),
         "subln": np.asarray(inp["diff_subln_w"][l], f)[:, None], "ropef": ropef, "rmat": rmat, "cmask": cmask,
         "lbl_bc": np.ascontiguousarray(lbl.reshape(1, 128)), "lbl_col": np.ascontiguousarray(lbl.T),
         "gw": np.asarray(inp["hgrn_norm_w"][l], f)[:, None], "mcat": mcat, "mrev": mrev,
         "negsig": negsig, "kidx": kidx, "midx": midx, "rowmask": rowmask}
    d.update(s5_params(inp, l, j))
    return d


def kernel(**inp):
    f = np.float32
    x = np.asarray(inp["x"], f)
    cores = list(range(NCORES))
    hT = [np.ascontiguousarray(x[b].T) for b in range(BATCH)]
    for l in range(DEPTH):
        nc = _prog(("mix", l), lambda: build_mixers(l))
        ims = [mixer_inputs(inp, l, c, hT[c // 4]) for c in cores]
        rm = run_bass_kernel_spmd(nc, ims, core_ids=cores).results
        final = (l == DEPTH - 1)
        nc = _prog(("out", final), lambda: build_out(final))
        ims = []
        for c in cores:
            b, tq = c // 4, c % 4
            ts = slice(tq * TQ, (tq + 1) * TQ)
            src = [4 * b + j for j in range(4)]
            mixin = np.concatenate([rm[s]["mo"][0:64, ts] for s in src] + [rm[s]["mo"][64:128, ts] for s in src]
                                   + [rm[s]["mo"][192:320, ts] for s in src])
            ssg = np.concatenate([rm[s]["mo"][128:192, ts] for s in src])
            ims.append({"mixin": np.ascontiguousarray(mixin), "ssg": np.ascontiguousarray(ssg), "hT": np.ascontiguousarray(hT[b][:, ts]),
                        "wout": np.asarray(inp["w_out"][l], f), "gluw": np.asarray(inp["s5_glu_w"][l], f),
                        "glub": np.ascontiguousarray(np.asarray(inp["s5_glu_b"][l], f).reshape(2, 128).T),
                        "fnw": np.ascontiguousarray(np.asarray(inp["final_norm_w"], f).reshape(8, 128).T)})
        ro = run_bass_kernel_spmd(nc, ims, core_ids=cores).results
        hT = [np.concatenate([ro[4 * b + tq]["hout"] for tq in range(4)], axis=1) for b in range(BATCH)]
    out = np.stack([hT[b].T for b in range(BATCH)]).astype(f)
    return np.ascontiguousarray(out)
```

```python
import contextlib
import math
import numpy as np
import ml_dtypes
import concourse.bass as bass
import concourse.mybir as mybir
from concourse.bass_utils import run_bass_kernel_spmd

F32 = mybir.dt.float32
BF16 = mybir.dt.bfloat16
I32 = mybir.dt.int32
AF = mybir.ActivationFunctionType
ALU = mybir.AluOpType
AX = mybir.AxisListType

D_MODEL = 1024
SEQ = 8192
BATCH = 2
DEPTH = 2
EPS = 1e-6
NCORES = 8
TQ = SEQ // 4
ROPE_THETA = 500000.0
import os
DBG = set(os.environ.get("KDBG", "").split(","))


class Sched:
    ENGS = ["pe", "act", "dve", "pool", "sp"]

    def __init__(self, nc):
        self.nc = nc
        self.ops = []
        self.last_w = {}
        self.readers = {}
        self.cnt = {e: 0 for e in self.ENGS}
        self.dma_cnt = {}
        self.base = set()

    def op(self, eng, fn, r=(), w=(), dma=False):
        deps = set(self.base)
        for x in r:
            if x in self.last_w:
                deps.add(self.last_w[x])
        for x in w:
            if x in self.last_w:
                deps.add(self.last_w[x])
            for d in self.readers.get(x, ()):
                deps.add(d)
        if dma:
            q = self.dma_cnt.get(eng, 0)
            self.dma_cnt[eng] = q + 1
            tok = ("dma", eng, q)
        else:
            self.cnt[eng] += 1
            tok = ("eng", eng, self.cnt[eng])
        self.ops.append((eng, fn, deps, tok))
        for x in w:
            self.last_w[x] = tok
            self.readers[x] = []
        for x in r:
            self.readers.setdefault(x, []).append(tok)
        return tok

    def barrier(self):
        b = set()
        for e in self.ENGS:
            if self.cnt[e] > 0:
                b.add(("eng", e, self.cnt[e]))
        for e, n in self.dma_cnt.items():
            for q in range(max(0, n - self.NSLOT), n):
                b.add(("dma", e, q))
        self.base = b
        self.last_w = {}
        self.readers = {}

    NSLOT = 8

    def emit(self):
        nc = self.nc
        NSLOT = self.NSLOT
        needed = set()
        for (eng, fn, deps, tok) in self.ops:
            for d in deps:
                if d[0] == "eng" and not (d[1] == "pe" and eng == "pe"):
                    needed.add(d)
        sig = {}
        run = {e: 0 for e in self.ENGS}
        for (eng, fn, deps, tok) in self.ops:
            if tok[0] == "eng":
                if tok in needed:
                    run[eng] += 1
                sig[tok] = run[eng]
        with contextlib.ExitStack() as st:
            esem = {e: st.enter_context(nc.semaphore("s_" + e)) for e in self.ENGS}
            dsem = {}
            for e in self.dma_cnt:
                dsem[e] = [st.enter_context(nc.semaphore(f"d_{e}_{i}")) for i in range(NSLOT)]
            block = st.enter_context(nc.Block())
            per = {e: [o for o in self.ops if o[0] == e] for e in self.ENGS}

            def mk(ename):
                def body(eng):
                    seen = {}

                    def wait(tok):
                        if tok[0] == "eng":
                            _, e2, n = tok
                            if e2 == "pe" and ename == "pe":
                                return
                            v = sig[tok]
                            key = ("eng", e2)
                            if seen.get(key, 0) >= v:
                                return
                            seen[key] = v
                            eng.wait_ge(esem[e2], v)
                        else:
                            _, e2, q = tok
                            slot = q % NSLOT
                            val = 16 * (q // NSLOT + 1)
                            key = ("dma", e2, slot)
                            if seen.get(key, 0) >= val:
                                return
                            seen[key] = val
                            eng.wait_ge(dsem[e2][slot], val)
                    for (_, fn, deps, tok) in per[ename]:
                        for d in sorted(deps):
                            wait(d)
                        if tok[0] == "dma":
                            q = tok[2]
                            if q >= NSLOT:
                                wait(("dma", ename, q - NSLOT))
                            ins = fn(eng)
                            ins.then_inc(dsem[ename][q % NSLOT], 16)
                        else:
                            ins = fn(eng)
                            if tok in needed:
                                ins.then_inc(esem[ename], 1)
                    n = self.dma_cnt.get(ename, 0)
                    for q in range(max(0, n - NSLOT), n):
                        wait(("dma", ename, q))
                return body
            block.tensor(mk("pe"))
            block.scalar(mk("act"))
            block.vector(mk("dve"))
            block.gpsimd(mk("pool"))
            block.sync(mk("sp"))


NFM = 704
NTM = 256
FM_CH = [(0, 128), (128, 128), (256, 64), (320, 128), (448, 128), (576, 128)]


def phase_inproj(nc, S, st, hT, wcat, nw, fm, tm_sf, tm_v, T=SEQ):
    TS = lambda n, s, d: st.enter_context(nc.sbuf_tensor(n, s, d))
    PS = lambda n: st.enter_context(nc.psum_tensor(n, [128, 512], F32))
    nw_sb = TS("ip_nw", [128, 8], F32)
    wst = [TS(f"ip_wst{i}", [128, NFM + NTM], F32) for i in range(2)]
    wall = TS("ip_wall", [128, 8, NFM + NTM], BF16)
    ones = TS("ip_ones", [128, 128], BF16)
    xin = [TS(f"ip_xin{i}", [128, 8, 512], F32) for i in range(2)]
    xsq = TS("ip_xsq", [128, 8, 512], BF16)
    sq = TS("ip_sq", [128, 512], F32)
    rstd = TS("ip_rstd", [128, 512], F32)
    xn = [TS(f"ip_xn{i}", [128, 8, 512], BF16) for i in range(2)]
    fmo = [TS(f"ip_fmo{i}", [128, 512], BF16) for i in range(6)]
    tsf = [TS(f"ip_tsf{i}", [128, 4, 64], F32) for i in range(2)]
    tv = [TS(f"ip_tv{i}", [128, 4, 192], BF16) for i in range(2)]
    ps_ss = PS("ip_ps_ss")
    ps_fm = [PS(f"ip_ps_fm{i}") for i in range(4)]
    ps_tm = [PS(f"ip_ps_tm{i}") for i in range(2)]

    S.op("sp", lambda e: e.dma_start(out=nw_sb[:], in_=nw[:, :]), w=["nw"], dma=True)
    S.op("pool", lambda e: e.memset(ones[:], 1.0), w=["ones"])
    for k in range(8):
        S.op("sp", lambda e, k=k: e.dma_start(out=wst[k % 2][:], in_=wcat[k * 128:(k + 1) * 128, :]),
             w=[("wst", k % 2)], dma=True)
        S.op("dve", lambda e, k=k: e.tensor_scalar(out=wall[:, k, :], in0=wst[k % 2][:], scalar1=nw_sb[:, k:k + 1],
                                                  scalar2=None, op0=ALU.mult),
             r=[("wst", k % 2), "nw"], w=[("wall", k)])
    wall_r = [("wall", k) for k in range(8)]
    hT_v = hT.rearrange("(k p) t -> p k t", p=128)
    fmi = 0
    NTI = T // 512

    def load(ti):
        b = ti % 2
        t0 = ti * 512
        S.op("pool", lambda e, b=b, t0=t0: e.dma_start(out=xin[b][:, 0:4, :], in_=hT_v[:, 0:4, t0:t0 + 512]),
             w=[("xin", b, 0)], dma=True)
        S.op("pool", lambda e, b=b, t0=t0: e.dma_start(out=xin[b][:, 4:8, :], in_=hT_v[:, 4:8, t0:t0 + 512]),
             w=[("xin", b, 1)], dma=True)
    def front_sq(ti):
        b = ti % 2
        xr = [("xin", b, 0), ("xin", b, 1)]
        S.op("act", lambda e, b=b: e.activation(out=xsq[:], in_=xin[b][:], func=AF.Square), r=xr, w=["xsq"])

    def front_ss(ti):
        b = ti % 2
        for k in range(8):
            S.op("pe", lambda e, k=k: e.matmul(ps_ss[:], ones[:], xsq[:, k, :], start=(k == 0), stop=(k == 7)),
                 r=["xsq", "ones"], w=["ps_ss"])
        S.op("act", lambda e: e.activation(out=sq[:], in_=ps_ss[:], func=AF.Sqrt, scale=1.0 / D_MODEL, bias=EPS),
             w=["ps_ss", "sq"])
        S.op("dve", lambda e: e.reciprocal(out=rstd[:], in_=sq[:]), r=["sq"], w=["rstd"])
        for hh in range(2):
            S.op("dve", lambda e, b=b, hh=hh: e.tensor_tensor(out=xn[b][:, 4 * hh:4 * hh + 4, :], in0=xin[b][:, 4 * hh:4 * hh + 4, :],
                                                         in1=rstd[:].unsqueeze(1).broadcast_to([128, 4, 512]), op=ALU.mult),
                 r=[("xin", b, hh), "rstd"], w=[("xn", b, hh)])
    load(0)
    if NTI > 1:
        load(1)
    front_sq(0)
    front_ss(0)
    for ti in range(NTI):
        b = ti % 2
        t0 = ti * 512
        if ti + 1 < NTI:
            front_sq(ti + 1)
        xnr = [("xn", b, 0), ("xn", b, 1)]
        for ci, (c0, cw) in enumerate(FM_CH):
            if "NOFM" in DBG or ("FM%d" % ci) in DBG:
                continue
            if ci == 3:
                if ti + 1 < NTI:
                    front_ss(ti + 1)
                if ti + 2 < NTI:
                    load(ti + 2)
            pb = fmi % 4
            for k in range(8):
                S.op("pe", lambda e, k=k, c0=c0, cw=cw, pb=pb, b=b: e.matmul(
                    ps_fm[pb][0:cw, :], wall[:, k, c0:c0 + cw], xn[b][:, k, :], start=(k == 0), stop=(k == 7)),
                    r=xnr + wall_r, w=[("ps_fm", pb)])
            fb = fmi % 6
            fmi += 1
            if ci == 0:
                S.op("act", lambda e, pb=pb, fb=fb: e.activation(out=fmo[fb][0:64, :], in_=ps_fm[pb][0:64, :], func=AF.Silu),
                     r=[("ps_fm", pb)], w=[("fmo", fb, 0)])
                S.op("act", lambda e, pb=pb, fb=fb: e.activation(out=fmo[fb][64:128, :], in_=ps_fm[pb][64:128, :],
                                                               func=AF.Sigmoid, scale=-1.0),
                     r=[("ps_fm", pb)], w=[("fmo", fb, 1)])
                wl = [("fmo", fb, 0), ("fmo", fb, 1)]
            elif ci in (1, 5):
                S.op("act", lambda e, pb=pb, fb=fb: e.activation(out=fmo[fb][:], in_=ps_fm[pb][:], func=AF.Silu),
                     r=[("ps_fm", pb)], w=[("fmo", fb, 0), ("fmo", fb, 1)])
                wl = [("fmo", fb, 0), ("fmo", fb, 1)]
            else:
                S.op("dve", lambda e, pb=pb, fb=fb, cw=cw: e.tensor_copy(out=fmo[fb][0:cw, :], in_=ps_fm[pb][0:cw, :]),
                     r=[("ps_fm", pb)], w=[("fmo", fb, 0), ("fmo", fb, 1)])
                wl = [("fmo", fb, 0), ("fmo", fb, 1)]
            S.op("sp", lambda e, fb=fb, c0=c0, cw=cw, t0=t0: e.dma_start(out=fm[c0:c0 + cw, t0:t0 + 512], in_=fmo[fb][0:cw, :]),
                 r=wl, dma=True)
        for pb in range(0 if "NOTM" in DBG else 2):
            for t4 in (2 * pb, 2 * pb + 1):
                off = (t4 % 2) * 256
                for k in range(8):
                    S.op("pe", lambda e, k=k, t4=t4, pb=pb, off=off, b=b: e.matmul(
                        ps_tm[pb][:, off:off + 256], xn[b][:, k, t4 * 128:(t4 + 1) * 128], wall[:, k, NFM:NFM + NTM],
                        start=(k == 0), stop=(k == 7)),
                        r=xnr + wall_r, w=[("ps_tm", pb)])
            if "TMNOEVAC" in DBG:
                continue
            for t4 in (2 * pb, 2 * pb + 1):
                off = (t4 % 2) * 256
                if "TMNOACT" not in DBG:
                  S.op("act", lambda e, pb=pb, b=b, t4=t4, off=off: e.activation(
                    out=tsf[b][:, t4, :], in_=ps_tm[pb][:, off:off + 64], func=AF.Sigmoid),
                    w=[("ps_tm", pb), ("tsf", b, t4)])
                if "TMNODVE" not in DBG:
                  S.op("dve", lambda e, pb=pb, b=b, t4=t4, off=off: e.tensor_copy(
                    out=tv[b][:, t4, :], in_=ps_tm[pb][:, off + 64:off + 256]),
                    w=[("ps_tm", pb), ("tv", b, t4)])
        if "TMNODMA" in DBG:
            continue
        S.op("sp", lambda e, b=b, t0=t0: e.dma_start(
            out=tm_sf[t0:t0 + 512, :].rearrange("(a p) c -> p a c", p=128), in_=tsf[b][:]),
            r=[("tsf", b, t4) for t4 in range(4)], dma=True)
        S.op("sp", lambda e, b=b, t0=t0: e.dma_start(
            out=tm_v[t0:t0 + 512, :].rearrange("(a p) c -> p a c", p=128), in_=tv[b][:]),
            r=[("tv", b, t4) for t4 in range(4)], dma=True)


def core_cols(j):
    r = lambda s, n: list(range(s, s + n))
    fmc = (r(0 + 64 * j, 64) + r(256 + 64 * j, 64) + r(768 + 64 * j, 64) + r(1280 + 64 * j, 64) + r(1024 + 64 * j, 64)
           + r(1536 + 128 * j, 128) + r(2048 + 128 * j, 128) + r(3072 + 128 * j, 128))
    tmc = r(256 + 64 * j, 64) + r(512 + 64 * j, 64) + r(2560 + 128 * j, 128)
    return np.array(fmc + tmc)


def build_inproj(T=SEQ):
    nc = bass.Bass("TRN2", target_bir_lowering=False)
    hT = nc.dram_tensor("hT", [D_MODEL, T], F32, kind="ExternalInput").ap()
    wcat = nc.dram_tensor("wcat", [D_MODEL, NFM + NTM], F32, kind="ExternalInput").ap()
    nw = nc.dram_tensor("nw", [128, 8], F32, kind="ExternalInput").ap()
    fm = nc.dram_tensor("fm", [NFM, T], BF16, kind="ExternalOutput").ap()
    tm_sf = nc.dram_tensor("tm_sf", [T, 64], F32, kind="ExternalOutput").ap()
    tm_v = nc.dram_tensor("tm_v", [T, 192], BF16, kind="ExternalOutput").ap()
    with contextlib.ExitStack() as st:
        S = Sched(nc)
        phase_inproj(nc, S, st, hT, wcat, nw, fm, tm_sf, tm_v, T)
        S.emit()
    return nc


C1_2PI = 6.28125
C2_2PI = 2.0 * math.pi - 6.28125


def sincos(S, ang, kf, ki, hs, sin_out, cos_out, tag, eng="dve"):
    a, k, h = tag + "ang", tag + "kf", tag + "hs"
    S.op(eng, lambda e: e.tensor_scalar(out=kf, in0=ang, scalar1=1.0 / (2.0 * math.pi), scalar2=None, op0=ALU.mult), r=[a], w=[k])
    S.op(eng, lambda e: e.tensor_copy(out=ki, in_=kf), r=[k], w=[tag + "ki"])
    S.op(eng, lambda e: e.tensor_copy(out=kf, in_=ki), r=[tag + "ki"], w=[k])
    S.op("dve", lambda e: e.scalar_tensor_tensor(out=ang, in0=kf, scalar=-C1_2PI, in1=ang, op0=ALU.mult, op1=ALU.add), r=[k], w=[a])
    S.op("dve", lambda e: e.scalar_tensor_tensor(out=ang, in0=kf, scalar=-C2_2PI, in1=ang, op0=ALU.mult, op1=ALU.add), r=[k], w=[a])
    S.op(eng, lambda e: e.tensor_scalar(out=ang, in0=ang, scalar1=math.pi, scalar2=-math.pi, op0=ALU.min, op1=ALU.max), w=[a])
    S.op("act", lambda e: e.activation(out=sin_out, in_=ang, func=AF.Sin), r=[a], w=[tag + "sin"])
    S.op("act", lambda e: e.activation(out=hs, in_=ang, func=AF.Sin, scale=0.5), r=[a], w=[h])
    S.op(eng, lambda e: e.tensor_tensor(out=hs, in0=hs, in1=hs, op=ALU.mult), w=[h])
    S.op(eng, lambda e: e.tensor_scalar(out=cos_out, in0=hs, scalar1=-2.0, scalar2=1.0, op0=ALU.mult, op1=ALU.add), r=[h], w=[tag + "cos"])


def rope_tables(nc, S, st, ropef, sinT, cosT, T, tag):
    TS = lambda n, s, d: st.enter_context(nc.sbuf_tensor(n, s, d))
    CH = min(512, T)
    NCH = T // CH
    pi_ = TS(tag + "_pi", [128, CH], I32)
    ang = TS(tag + "_ang", [128, CH], F32)
    kf = TS(tag + "_kf", [128, CH], F32)
    ki = TS(tag + "_ki", [128, CH], I32)
    hs = TS(tag + "_hs", [128, CH], F32)
    s1 = TS(tag + "_s1", [128, CH], F32)
    c1 = TS(tag + "_c1", [128, CH], F32)
    pj = TS(tag + "_pj", [128, NCH], I32)
    ang_b = TS(tag + "_angb", [128, NCH], F32)
    kf_b = TS(tag + "_kfb", [128, NCH], F32)
    ki_b = TS(tag + "_kib", [128, NCH], I32)
    hs_b = TS(tag + "_hsb", [128, NCH], F32)
    s2 = TS(tag + "_s2", [128, NCH], F32)
    c2 = TS(tag + "_c2", [128, NCH], F32)
    ns2 = TS(tag + "_ns2", [128, NCH], F32)
    tmp = [TS(tag + f"_tmp{i}", [128, CH], F32) for i in range(4)]
    S.op("pool", lambda e: e.iota(pi_[:], pattern=[[1, CH]], base=0, channel_multiplier=0), w=[tag + "pi"])
    S.op("pool", lambda e: e.iota(pj[:], pattern=[[CH, NCH]], base=0, channel_multiplier=0), w=[tag + "pj"])
    S.op("dve", lambda e: e.tensor_copy(out=ang[:], in_=pi_[:]), r=[tag + "pi"], w=[tag + "aang"])
    S.op("dve", lambda e: e.tensor_scalar(out=ang[:], in0=ang[:], scalar1=ropef[:, 0:1], scalar2=None, op0=ALU.mult), r=["ropef"], w=[tag + "aang"])
    sincos(S, ang[:], kf[:], ki[:], hs[:], s1[:], c1[:], tag + "a")
    S.op("dve", lambda e: e.tensor_copy(out=ang_b[:], in_=pj[:]), r=[tag + "pj"], w=[tag + "bang"])
    S.op("dve", lambda e: e.tensor_scalar(out=ang_b[:], in0=ang_b[:], scalar1=ropef[:, 0:1], scalar2=None, op0=ALU.mult), r=["ropef"], w=[tag + "bang"])
    sincos(S, ang_b[:], kf_b[:], ki_b[:], hs_b[:], s2[:], c2[:], tag + "b")
    S.op("dve", lambda e: e.tensor_scalar(out=ns2[:], in0=s2[:], scalar1=-1.0, scalar2=None, op0=ALU.mult), r=[tag + "bsin"], w=[tag + "ns2"])
    rd = [tag + "asin", tag + "acos", tag + "bsin", tag + "bcos", tag + "ns2"]
    for c in range(NCH):
        sl = slice(c * CH, (c + 1) * CH)
        ta, tb = tmp[(2 * c) % 4], tmp[(2 * c + 1) % 4]
        na, nb = (tag + "tmp", (2 * c) % 4), (tag + "tmp", (2 * c + 1) % 4)
        S.op("dve", lambda e, c=c, ta=ta: e.tensor_scalar(out=ta[:], in0=c1[:], scalar1=s2[:, c:c + 1], scalar2=None, op0=ALU.mult), r=rd, w=[na])
        S.op("dve", lambda e, c=c, ta=ta, sl=sl: e.scalar_tensor_tensor(out=sinT[:, sl], in0=s1[:], scalar=c2[:, c:c + 1], in1=ta[:],
                                                                        op0=ALU.mult, op1=ALU.add), r=rd + [na], w=[(tag + "sin", c)])
        S.op("dve", lambda e, c=c, tb=tb: e.tensor_scalar(out=tb[:], in0=s1[:], scalar1=ns2[:, c:c + 1], scalar2=None, op0=ALU.mult), r=rd, w=[nb])
        S.op("dve", lambda e, c=c, tb=tb, sl=sl: e.scalar_tensor_tensor(out=cosT[:, sl], in0=c1[:], scalar=c2[:, c:c + 1], in1=tb[:],
                                                                        op0=ALU.mult, op1=ALU.add), r=rd + [nb], w=[(tag + "cos", c)])
    return [(tag + "sin", c) for c in range(NCH)] + [(tag + "cos", c) for c in range(NCH)]


def phase_attn(nc, S, st, fm, tm_v, lqk, subln, ropef_d, rmat_d, cmask_d, oa, lambda_init, T=SEQ):
    TS = lambda n, s, d: st.enter_context(nc.sbuf_tensor(n, s, d))
    PS = lambda n: st.enter_context(nc.psum_tensor(n, [128, 512], F32))
    NQ = T // 512
    NK = T // 128
    ropef = TS("at_ropef", [128, 1], F32)
    rm32 = TS("at_rm32", [128, 128], F32)
    rm = TS("at_rm", [128, 128], BF16)
    cmask = TS("at_cmask", [128, 4, 512], BF16)
    ones = TS("at_ones", [128, 128], BF16)
    sinT = TS("at_sin", [128, T], BF16)
    cosT = TS("at_cos", [128, T], BF16)
    qraw = TS("at_qraw", [128, T], BF16)
    kraw = TS("at_kraw", [128, T], BF16)
    qr = TS("at_qr", [128, T], BF16)
    kr = TS("at_kr", [128, T], BF16)
    sag = TS("at_sag", [128, T], BF16)
    vsb = TS("at_v", [128, NK, 128], BF16)
    lq = TS("at_lq", [128, 256], F32)
    lp = TS("at_lp", [128, 128], F32)
    le = TS("at_le", [128, 2], F32)
    neglam = TS("at_neglam", [128, 1], F32)
    sw = TS("at_sw", [128, 1], F32)
    t1 = [TS(f"at_t1_{i}", [128, 512], BF16) for i in range(2)]
    t2 = [TS(f"at_t2_{i}", [128, 512], BF16) for i in range(2)]
    P = [[TS(f"at_P{i}_{m}", [128, 512], BF16) for m in range(2)] for i in range(4)]
    r0 = [TS(f"at_r0_{i}", [128, 512], F32) for i in range(2)]
    r1 = [TS(f"at_r1_{i}", [128, 512], F32) for i in range(2)]
    o0 = [TS(f"at_o0_{i}", [128, 512], F32) for i in range(2)]
    o1 = [TS(f"at_o1_{i}", [128, 512], F32) for i in range(2)]
    osq = TS("at_osq", [128, 512], BF16)
    nsq = TS("at_nsq", [128, 512], F32)
    ob = [TS(f"at_ob{i}", [128, 512], BF16) for i in range(2)]
    epsc = TS("at_epsc", [128, 1], F32)
    accD = [TS(f"at_accD{i}", [128, 512], F32) for i in range(2)]
    accP = [TS(f"at_accP{i}", [128, 512], F32) for i in range(2)]
    ones32 = TS("at_ones32", [128, 128], F32)
    ps_s = [[PS(f"at_ps_s{i}_{m}") for m in range(2)] for i in range(2)]
    ps_o = [PS(f"at_ps_o{m}") for m in range(2)]
    ps_l = [PS(f"at_ps_l{m}") for m in range(2)]
    ps_n = ps_s[0][0]

    S.op("sp", lambda e: e.dma_start(out=ropef[:], in_=ropef_d[:, :]), w=["ropef"], dma=True)
    S.op("sp", lambda e: e.dma_start(out=rm32[:], in_=rmat_d[:, :]), w=["rm32"], dma=True)
    S.op("sp", lambda e: e.dma_start(out=cmask[:], in_=cmask_d.rearrange("d p q -> p d q")), w=["cmask"], dma=True)
    S.op("sp", lambda e: e.dma_start(out=lq[:], in_=lqk.partition_broadcast(128)), w=["lq"], dma=True)
    S.op("sp", lambda e: e.dma_start(out=sw[:], in_=subln[:, :]), w=["sw"], dma=True)
    S.op("sp", lambda e: e.dma_start(out=qraw[:], in_=fm[320:448, :]), w=["qraw"], dma=True)
    S.op("sp", lambda e: e.dma_start(out=kraw[:], in_=fm[448:576, :]), w=["kraw"], dma=True)
    S.op("sp", lambda e: e.dma_start(out=sag[:], in_=fm[576:704, :]), w=["sag"], dma=True)
    S.op("pool", lambda e: e.dma_start(out=vsb[:], in_=tm_v[:, 64:192].rearrange("(a p) c -> p a c", p=128)), w=["vsb"], dma=True)
    S.op("pool", lambda e: e.memset(ones[:], 1.0), w=["ones"])
    S.op("pool", lambda e: e.memset(epsc[:], EPS), w=["epsc"])
    S.op("pool", lambda e: e.memset(ones32[:], 1.0), w=["ones32"])
    S.op("dve", lambda e: e.tensor_copy(out=rm[:], in_=rm32[:]), r=["rm32"], w=["rm"])
    S.op("dve", lambda e: e.tensor_tensor(out=lp[:], in0=lq[:, 0:128], in1=lq[:, 128:256], op=ALU.mult), r=["lq"], w=["lp"])
    S.op("dve", lambda e: e.tensor_reduce(out=le[:], in_=lp[:].rearrange("p (a c) -> p a c", a=2), axis=AX.X, op=ALU.add),
         r=["lp"], w=["le"])
    S.op("act", lambda e: e.activation(out=le[:], in_=le[:], func=AF.Exp), w=["le"])
    S.op("dve", lambda e: e.tensor_tensor(out=neglam[:], in0=le[:, 1:2], in1=le[:, 0:1], op=ALU.subtract), r=["le"], w=["neglam"])
    S.op("dve", lambda e: e.tensor_scalar(out=neglam[:], in0=neglam[:], scalar1=-lambda_init, scalar2=None, op0=ALU.add), w=["neglam"])
    S.op("dve", lambda e: e.tensor_scalar(out=sw[:], in0=sw[:], scalar1=1.0 - lambda_init, scalar2=None, op0=ALU.mult), w=["sw"])
    tabs = rope_tables(nc, S, st, ropef, sinT, cosT, T, "at_rp")
    def rot_ops(ti):
        sl = slice(ti * 512, (ti + 1) * 512)
        ops = []
        for k_, (src, dst, sn, dn) in enumerate(((qraw, qr, "qraw", "qr"), (kraw, kr, "kraw", "kr"))):
            b = k_
            ops.append(lambda src=src, sl=sl, sn=sn, b=b: S.op("dve", lambda e: e.tensor_tensor(out=t1[b][:], in0=src[:, sl], in1=cosT[:, sl], op=ALU.mult),
                                                             r=[sn] + tabs, w=[("t1", b)]))

            def mm_unit(src=src, sl=sl, sn=sn, b=b):
                S.op("pe", lambda e: e.matmul(ps_l[1][:], rm[:], src[:, sl], start=True, stop=True), r=[sn, "rm"], w=["ps_l1"])
                S.op("dve", lambda e: e.tensor_tensor(out=t2[b][:], in0=ps_l[1][:], in1=sinT[:, sl], op=ALU.mult), r=tabs, w=[("t2", b), "ps_l1"])
            ops.append(mm_unit)
            ops.append(lambda dst=dst, sl=sl, b=b, dn=dn, ti=ti: S.op("pool", lambda e: e.tensor_tensor(out=dst[:, sl], in0=t1[b][:], in1=t2[b][:], op=ALU.add),
                                                                    r=[("t1", b), ("t2", b)], w=[(dn, ti)]))
        return ops
    for f_ in rot_ops(0):
        f_()
    qr_all = [("qr", ti) for ti in range(NQ)]
    kr_all = [("kr", ti) for ti in range(NQ)]
    psn = lambda i, m: f"ps_s{i}{m}"
    dq = []

    def pop_deferred(n, limit=2):
        k = 0
        while dq and k < limit:
            fn, need_odd = dq[0]
            if need_odd and n % 2 == 0:
                break
            dq.pop(0)
            fn()
            k += 1
    for qi in range(NQ):
        qs = slice(qi * 512, (qi + 1) * 512)
        nk = 4 * (qi + 1)
        ab = qi % 2

        def c0_of(n, qi=qi):
            return 128 * max(0, n - 4 * qi)

        def QK(n, qs=qs, qi=qi):
            i = n % 2
            c0 = c0_of(n)
            for m in range(2):
                S.op("pe", lambda e, n=n, i=i, m=m, qs=qs, c0=c0: e.matmul(
                    ps_s[i][m][:, c0:512], kr[64 * m:64 * m + 64, n * 128:(n + 1) * 128],
                    qr[64 * m:64 * m + 64, qs.start + c0:qs.stop], start=True, stop=True),
                     r=[("qr", qi), ("kr", n // 4)], w=[psn(i, m)])

        def EXP(n, qi=qi):
            i = n % 2
            j = n % 4
            c0 = c0_of(n)
            for m in range(2):
                S.op("act", lambda e, i=i, j=j, m=m, c0=c0: e.activation(out=P[j][m][:, c0:512], in_=ps_s[i][m][:, c0:512], func=AF.Exp, scale=0.125),
                     w=[psn(i, m), ("P", j, m)])
                d = n - 4 * qi
                if d >= 0:
                    S.op("dve", lambda e, j=j, m=m, d=d, c0=c0: e.tensor_tensor(out=P[j][m][:, c0:512], in0=P[j][m][:, c0:512],
                                                                             in1=cmask[:, d, c0:512], op=ALU.mult),
                         r=["cmask"], w=[("P", j, m)])

        def PV(n, nk=nk, ab=ab):
            j = n % 4
            c0 = c0_of(n)
            for m in range(2):
                S.op("pe", lambda e, n=n, j=j, m=m, nk=nk, c0=c0: e.matmul(ps_o[m][:, c0:512], vsb[:, n, :], P[j][m][:, c0:512],
                                                                       start=(n == 0), stop=(n == nk - 1)),
                     r=[("P", j, m), "vsb"], w=[f"ps_o{m}"])
            S.op("pe", lambda e, n=n, j=j, nk=nk, c0=c0: e.matmul(ps_l[0][:, c0:512], ones[:], P[j][0][:, c0:512], start=(n == 0), stop=(n == nk - 1)),
                 r=[("P", j, 0), "ones"], w=["ps_l0"])
            eng, acc, an = ("dve", accD[ab], ("accD", ab)) if n % 2 == 0 else ("pool", accP[ab], ("accP", ab))
            if n < 2:
                if c0 > 0:
                    S.op(eng, lambda e, acc=acc, c0=c0: e.memset(acc[:, 0:c0], 0.0), w=[an])
                S.op(eng, lambda e, j=j, acc=acc, c0=c0: e.tensor_copy(out=acc[:, c0:512], in_=P[j][1][:, c0:512]), r=[("P", j, 1)], w=[an])
            else:
                S.op(eng, lambda e, j=j, acc=acc, c0=c0: e.tensor_tensor(out=acc[:, c0:512], in0=acc[:, c0:512], in1=P[j][1][:, c0:512], op=ALU.add),
                     r=[("P", j, 1)], w=[an])
        rq = rot_ops(qi + 1) if qi + 1 < NQ else []
        QK(0)
        for n in range(nk):
            if n >= 1:
                pop_deferred(n)
                for _ in range(2):
                    if rq:
                        rq.pop(0)()
            if n + 1 < nk:
                QK(n + 1)
            EXP(n)
            if n >= 2:
                PV(n - 2)
        PV(nk - 2)
        PV(nk - 1)
        while dq:
            fn, need_odd = dq.pop(0)
            fn()
        while rq:
            rq.pop(0)()
        eb = qi % 2
        S.op("dve", lambda e, eb=eb: e.tensor_copy(out=o0[eb][:], in_=ps_o[0][:]), w=[("o0", eb), "ps_o0"])
        S.op("dve", lambda e, eb=eb: e.tensor_copy(out=o1[eb][:], in_=ps_o[1][:]), w=[("o1", eb), "ps_o1"])
        S.op("dve", lambda e, eb=eb: e.tensor_copy(out=r0[eb][:], in_=ps_l[0][:]), w=[("r0", eb), "ps_l0"])
        D = lambda fn, odd=False: dq.append((fn, odd))
        def l1_unit(ab=ab, eb=eb):
            S.op("pe", lambda e: e.matmul(ps_l[1][:], ones32[:], accD[ab][:], start=True, stop=False), r=[("accD", ab), "ones32"], w=["ps_l1"])
            S.op("pe", lambda e: e.matmul(ps_l[1][:], ones32[:], accP[ab][:], start=False, stop=True), r=[("accP", ab), "ones32"], w=["ps_l1"])
            S.op("dve", lambda e: e.tensor_copy(out=r1[eb][:], in_=ps_l[1][:]), w=[("r1", eb), "ps_l1"])
        D(l1_unit)
        for rr, rn in ((r0, "r0"), (r1, "r1")):
            D(lambda eb=eb, rr=rr, rn=rn: S.op("act", lambda e: e.activation(out=rr[eb][:], in_=rr[eb][:], func=AF.Ln), w=[(rn, eb)]))
            D(lambda eb=eb, rr=rr, rn=rn: S.op("act", lambda e: e.activation(out=rr[eb][:], in_=rr[eb][:], func=AF.Exp, scale=-1.0), w=[(rn, eb)]))
        D(lambda eb=eb: S.op("dve", lambda e: e.tensor_tensor(out=o0[eb][:], in0=o0[eb][:], in1=r0[eb][:], op=ALU.mult), r=[("r0", eb)], w=[("o0", eb)]))
        D(lambda eb=eb: S.op("dve", lambda e: e.tensor_tensor(out=o1[eb][:], in0=o1[eb][:], in1=r1[eb][:], op=ALU.mult), r=[("r1", eb)], w=[("o1", eb)]))
        D(lambda eb=eb: S.op("dve", lambda e: e.scalar_tensor_tensor(out=o0[eb][:], in0=o1[eb][:], scalar=neglam[:, 0:1], in1=o0[eb][:],
                                                                     op0=ALU.mult, op1=ALU.add), r=[("o1", eb), "neglam"], w=[("o0", eb)]))
        D(lambda eb=eb: S.op("act", lambda e: e.activation(out=osq[:], in_=o0[eb][:], func=AF.Square), r=[("o0", eb)], w=["osq"]))
        def norm_unit():
            S.op("pe", lambda e: e.matmul(ps_n[:], ones[:], osq[:], start=True, stop=True), r=["osq", "ones"], w=[psn(0, 0)])
            S.op("act", lambda e: e.activation(out=nsq[:], in_=ps_n[:], func=AF.Ln, scale=1.0 / 128.0, bias=epsc[:, 0:1]),
                 r=["epsc"], w=["nsq", psn(0, 0)])
        D(norm_unit, True)
        D(lambda: S.op("act", lambda e: e.activation(out=nsq[:], in_=nsq[:], func=AF.Exp, scale=-0.5), w=["nsq"]))
        D(lambda eb=eb: S.op("dve", lambda e: e.scalar_tensor_tensor(out=o0[eb][:], in0=o0[eb][:], scalar=sw[:, 0:1], in1=nsq[:],
                                                                     op0=ALU.mult, op1=ALU.mult), r=["nsq", "sw"], w=[("o0", eb)]))
        D(lambda eb=eb, qs=qs: S.op("pool", lambda e: e.tensor_tensor(out=ob[eb][:], in0=o0[eb][:], in1=sag[:, qs], op=ALU.mult),
                                    r=[("o0", eb), "sag"], w=[("ob", eb)]))
        D(lambda eb=eb, qs=qs: S.op("sp", lambda e: e.dma_start(out=oa[:, qs], in_=ob[eb][:]), r=[("ob", eb)], dma=True))
    while dq:
        fn, need_odd = dq.pop(0)
        fn()


def attn_consts():
    ropef = np.zeros((128, 1), np.float32)
    inv = (ROPE_THETA ** (-np.arange(0, 16, 2, dtype=np.float32) / 16.0)).astype(np.float32)
    rmat = np.zeros((128, 128), np.float32)
    for base in (0, 64):
        for i in range(8):
            ropef[base + i, 0] = -inv[i]
            ropef[base + 8 + i, 0] = inv[i]
            rmat[base + 8 + i, base + i] = 1.0
            rmat[base + i, base + 8 + i] = 1.0
    k = np.arange(128)[:, None]
    q = np.arange(512)[None, :]
    cmask = np.stack([(128 * d + k <= q) for d in range(4)]).astype(ml_dtypes.bfloat16)
    return ropef, rmat, cmask


def build_attn(lambda_init, T=SEQ):
    nc = bass.Bass("TRN2", target_bir_lowering=False)
    fm = nc.dram_tensor("fm", [NFM, T], BF16, kind="ExternalInput").ap()
    tm_v = nc.dram_tensor("tm_v", [T, 192], BF16, kind="ExternalInput").ap()
    lqk = nc.dram_tensor("lqk", [1, 256], F32, kind="ExternalInput").ap()
    subln = nc.dram_tensor("subln", [128, 1], F32, kind="ExternalInput").ap()
    ropef = nc.dram_tensor("ropef", [128, 1], F32, kind="ExternalInput").ap()
    rmat = nc.dram_tensor("rmat", [128, 128], F32, kind="ExternalInput").ap()
    cmask = nc.dram_tensor("cmask", [4, 128, 512], BF16, kind="ExternalInput").ap()
    oa = nc.dram_tensor("oa", [128, T], BF16, kind="ExternalOutput").ap()
    with contextlib.ExitStack() as st:
        S = Sched(nc)
        phase_attn(nc, S, st, fm, tm_v, lqk, subln, ropef, rmat, cmask, oa, lambda_init, T)
        S.emit()
    return nc


def hgrn_consts():
    s = np.arange(128)[:, None]
    t = np.arange(128)[None, :]
    same = (s // 16) == (t // 16)
    m_incl = (same & (s <= t)).astype(np.float32)
    m_rev = (same & (s > t)).astype(np.float32)
    m_tot8 = ((s // 16) == np.arange(8)[None, :]).astype(np.float32)
    mcat = np.concatenate([m_incl, m_tot8], axis=1)
    return mcat, m_rev


def phase_hgrn(nc, S, st, fm, tm_sf, tm_v, lbl_bc_d, lbl_col_d, gw_d, mcat_d, mrev_d, oh, lb_coef, T=SEQ):
    TS = lambda n, s, d: st.enter_context(nc.sbuf_tensor(n, s, d))
    PS = lambda n: st.enter_context(nc.psum_tensor(n, [128, 512], F32))
    NT = T // 128
    sq = TS("hg_sq", [64, T], BF16)
    snf = TS("hg_snf", [64, T], BF16)
    shg = TS("hg_shg", [64, T], BF16)
    sf = TS("hg_sf", [128, NT, 64], F32)
    omf = TS("hg_omf", [128, NT, 64], F32)
    logf = TS("hg_logf", [128, NT, 64], F32)
    vall = TS("hg_v", [128, NT, 64], BF16)
    lbl = TS("hg_lbl", [128, 128], F32)
    lb_bc = TS("hg_lb_bc", [128, 64], F32)
    oml_bc = TS("hg_oml_bc", [128, 64], F32)
    lblc = TS("hg_lblc", [64, 2], F32)
    oml_col = TS("hg_oml_col", [64, 1], F32)
    gw = TS("hg_gw", [64, 1], F32)
    mcat = TS("hg_mcat", [128, 136], F32)
    mrev = TS("hg_mrev", [128, 128], F32)
    mincl_bf = TS("hg_mincl", [128, 128], BF16)
    mtot_bf = TS("hg_mtot", [128, 8], BF16)
    ones64 = TS("hg_ones", [64, 64], BF16)
    epsc = TS("hg_epsc", [64, 1], F32)
    eq = TS("hg_eq", [64, 128], F32)
    ekn = TS("hg_ekn", [64, 128], F32)
    dec = [TS(f"hg_dec{i}", [64, 8], F32) for i in range(3)]
    ehat = TS("hg_ehat", [128, 64], F32)
    qt = [TS(f"hg_qt{i}", [64, 128], BF16) for i in range(3)]
    kt = TS("hg_kt", [64, 128], BF16)
    khat = TS("hg_khat", [128, 64], BF16)
    vblk = TS("hg_vblk", [128, 8, 64], BF16)
    scm = [TS(f"hg_scm{i}", [128, 128], BF16) for i in range(3)]
    Sall = [TS(f"hg_S{i}", [64, 9, 64], F32) for i in range(2)]
    Sbf = [TS(f"hg_Sbf{i}", [64, 8, 64], BF16) for i in range(2)]
    osq = TS("hg_osq", [64, 512], BF16)
    o32 = TS("hg_o32", [64, 512], F32)
    nsq = TS("hg_nsq", [64, 512], F32)
    ohb = [TS(f"hg_ohb{i}", [64, 512], BF16) for i in range(2)]
    ps_c = PS("hg_ps_c")
    ps_r = PS("hg_ps_r")
    ps_sc = PS("hg_ps_sc")
    ps_u = [PS(f"hg_ps_u{i}") for i in range(3)]
    ps_oh = [PS(f"hg_ps_oh{i}") for i in range(2)]

    S.op("sp", lambda e: e.dma_start(out=sq[:], in_=fm[0:64, :]), w=["sq"], dma=True)
    S.op("sp", lambda e: e.dma_start(out=snf[:], in_=fm[64:128, :]), w=["snf"], dma=True)
    S.op("sp", lambda e: e.dma_start(out=shg[:], in_=fm[128:192, :]), w=["shg"], dma=True)
    S.op("sp", lambda e: e.dma_start(out=sf[:], in_=tm_sf.rearrange("(a p) c -> p a c", p=128)), w=["sf"], dma=True)
    S.op("pool", lambda e: e.dma_start(out=vall[:], in_=tm_v[:, 0:64].rearrange("(a p) c -> p a c", p=128)), w=["vall"], dma=True)
    S.op("sp", lambda e: e.dma_start(out=lbl[:], in_=lbl_bc_d.partition_broadcast(128)), w=["lbl"], dma=True)
    S.op("sp", lambda e: e.dma_start(out=lblc[:], in_=lbl_col_d[:, :]), w=["lblc"], dma=True)
    S.op("sp", lambda e: e.dma_start(out=gw[:], in_=gw_d[:, :]), w=["gw"], dma=True)
    S.op("sp", lambda e: e.dma_start(out=mcat[:], in_=mcat_d[:, :]), w=["mcat"], dma=True)
    S.op("sp", lambda e: e.dma_start(out=mrev[:], in_=mrev_d[:, :]), w=["mrev"], dma=True)
    S.op("pool", lambda e: e.memset(ones64[:], 1.0), w=["ones64"])
    S.op("pool", lambda e: e.memset(epsc[:], EPS), w=["epsc"])
    S.op("pool", lambda e: e.memset(Sall[0][:, 0, :], 0.0), w=[("S", 0)])
    S.op("dve", lambda e: e.tensor_copy(out=mincl_bf[:], in_=mcat[:, 0:128]), r=["mcat"], w=["mincl_bf"])
    S.op("dve", lambda e: e.tensor_copy(out=mtot_bf[:], in_=mcat[:, 128:136]), r=["mcat"], w=["mtot_bf"])
    S.op("dve", lambda e: e.tensor_tensor(out=lb_bc[:], in0=lbl[:, 64:128], in1=lbl[:, 0:64], op=ALU.subtract), r=["lbl"], w=["lb_bc"])
    S.op("act", lambda e: e.activation(out=lb_bc[:], in_=lb_bc[:], func=AF.Sigmoid), w=["lb_bc"])
    S.op("dve", lambda e: e.tensor_scalar(out=lb_bc[:], in0=lb_bc[:], scalar1=float(lb_coef), scalar2=None, op0=ALU.mult), w=["lb_bc"])
    S.op("dve", lambda e: e.tensor_scalar(out=oml_bc[:], in0=lb_bc[:], scalar1=-1.0, scalar2=1.0, op0=ALU.mult, op1=ALU.add),
         r=["lb_bc"], w=["oml_bc"])
    S.op("dve", lambda e: e.tensor_tensor(out=oml_col[:], in0=lblc[:, 1:2], in1=lblc[:, 0:1], op=ALU.subtract), r=["lblc"], w=["oml_col"])
    S.op("act", lambda e: e.activation(out=oml_col[:], in_=oml_col[:], func=AF.Sigmoid), w=["oml_col"])
    S.op("dve", lambda e: e.tensor_scalar(out=oml_col[:], in0=oml_col[:], scalar1=-float(lb_coef), scalar2=1.0, op0=ALU.mult, op1=ALU.add),
         w=["oml_col"])
    for g in range(NT // 8 if NT >= 8 else 1):
        nt = min(8, NT)
        sl = slice(g * 8, g * 8 + nt)
        S.op("dve", lambda e, sl=sl, nt=nt: e.tensor_tensor(out=sf[:, sl, :], in0=sf[:, sl, :],
                                                        in1=oml_bc[:].unsqueeze(1).broadcast_to([128, nt, 64]), op=ALU.mult),
             r=["oml_bc"], w=["sf"])
        S.op("dve", lambda e, sl=sl, nt=nt: e.tensor_tensor(out=sf[:, sl, :], in0=sf[:, sl, :],
                                                        in1=lb_bc[:].unsqueeze(1).broadcast_to([128, nt, 64]), op=ALU.add),
             r=["lb_bc"], w=["sf"])
        S.op("act", lambda e, sl=sl: e.activation(out=logf[:, sl, :], in_=sf[:, sl, :], func=AF.Ln), r=["sf"], w=[("logf", g)])
        S.op("pool", lambda e, sl=sl: e.tensor_scalar(out=omf[:, sl, :], in0=sf[:, sl, :], scalar1=-1.0, scalar2=1.0,
                                                     op0=ALU.mult, op1=ALU.add), r=["sf"], w=[("omf", g)])
    def stageA1(i):
        g = i // 8
        b = i % 3
        S.op("pe", lambda e, i=i: e.matmul(ps_c[0:64, 0:136], logf[:, i, :], mcat[:], start=True, stop=True),
             r=[("logf", g), "mcat"], w=["ps_c"])
        S.op("pe", lambda e, i=i: e.matmul(ps_r[:, 0:64], mrev[:], logf[:, i, :], start=True, stop=True),
             r=[("logf", g), "mrev"], w=["ps_r"])
        S.op("act", lambda e: e.activation(out=eq[:], in_=ps_c[0:64, 0:128], func=AF.Exp), w=["eq", "ps_c"])
        S.op("act", lambda e: e.activation(out=ekn[:], in_=ps_c[0:64, 0:128], func=AF.Exp, scale=-1.0), w=["ekn", "ps_c"])
        S.op("act", lambda e, b=b: e.activation(out=dec[b][:], in_=ps_c[0:64, 128:136], func=AF.Exp), w=[("dec", b), "ps_c"])
        S.op("act", lambda e: e.activation(out=ehat[:], in_=ps_r[:, 0:64], func=AF.Exp), w=["ehat", "ps_r"])

    def stageA2(i):
        g = i // 8
        b = i % 3
        ts = slice(i * 128, (i + 1) * 128)
        S.op("pool", lambda e, b=b, ts=ts: e.tensor_tensor(out=qt[b][:], in0=sq[:, ts], in1=eq[:], op=ALU.mult),
             r=["sq", "eq"], w=[("qt", b)])
        S.op("dve", lambda e, ts=ts: e.scalar_tensor_tensor(out=kt[:], in0=snf[:, ts], scalar=oml_col[:, 0:1], in1=ekn[:],
                                                           op0=ALU.mult, op1=ALU.mult), r=["snf", "ekn", "oml_col"], w=["kt"])
        S.op("pool", lambda e, i=i: e.tensor_tensor(out=khat[:], in0=omf[:, i, :], in1=ehat[:], op=ALU.mult),
             r=[("omf", g), "ehat"], w=["khat"])
        S.op("pool", lambda e, i=i: e.tensor_tensor(out=vblk[:], in0=vall[:, i, :].unsqueeze(1).broadcast_to([128, 8, 64]),
                                                   in1=mtot_bf[:].unsqueeze(2).broadcast_to([128, 8, 64]), op=ALU.mult),
             r=["vall", "mtot_bf"], w=["vblk"])
        S.op("pe", lambda e, b=b: e.matmul(ps_sc[:, 0:128], kt[:], qt[b][:], start=True, stop=True), r=["kt", ("qt", b)], w=["ps_sc"])
        S.op("dve", lambda e, b=b: e.tensor_tensor(out=scm[b][:], in0=ps_sc[:, 0:128], in1=mincl_bf[:], op=ALU.mult),
             r=["mincl_bf"], w=[("scm", b), "ps_sc"])
        S.op("pe", lambda e, b=b: e.matmul(ps_u[b][0:64, :], khat[:], vblk[:].rearrange("p a c -> p (a c)"), start=True, stop=True),
             r=["khat", "vblk"], w=[("ps_u", b)])

    def stageB(i):
        b = i % 3
        sb = i % 2
        if i > 0:
            S.op("dve", lambda e, sb=sb: e.tensor_copy(out=Sall[sb][:, 0, :], in_=Sall[1 - sb][:, 8, :]), r=[("S", 1 - sb)], w=[("S", sb)])
        for n in range(8):
            S.op("dve", lambda e, b=b, sb=sb, n=n: e.scalar_tensor_tensor(
                out=Sall[sb][:, n + 1, :], in0=Sall[sb][:, n, :], scalar=dec[b][:, n:n + 1], in1=ps_u[b][0:64, n * 64:(n + 1) * 64],
                op0=ALU.mult, op1=ALU.add), r=[("dec", b)], w=[("S", sb), ("ps_u", b)])
        S.op("act", lambda e, sb=sb: e.activation(out=Sbf[sb][:], in_=Sall[sb][:, 0:8, :], func=AF.Copy), r=[("S", sb)], w=[("Sbf", sb)])

    def stageC(i):
        b = i % 3
        sb = i % 2
        ob = (i // 4) % 2
        c0 = (i % 4) * 128
        S.op("pe", lambda e, i=i, ob=ob, c0=c0, b=b: e.matmul(ps_oh[ob][0:64, c0:c0 + 128], vall[:, i, :], scm[b][:], start=True, stop=False),
             r=["vall", ("scm", b)], w=[("ps_oh", ob)])
        for n in range(8):
            S.op("pe", lambda e, b=b, sb=sb, ob=ob, c0=c0, n=n: e.matmul(
                ps_oh[ob][0:64, c0 + 16 * n:c0 + 16 * n + 16], Sbf[sb][:, n, :], qt[b][:, 16 * n:16 * n + 16],
                start=False, stop=(n == 7)), r=[("Sbf", sb), ("qt", b)], w=[("ps_oh", ob)])
        if i % 4 == 3 or i == NT - 1:
            qs = slice((i // 4) * 512, (i // 4) * 512 + 512)
            S.op("act", lambda e, ob=ob: e.activation(out=osq[:], in_=ps_oh[ob][0:64, :], func=AF.Square), w=["osq", ("ps_oh", ob)])
            S.op("act", lambda e, ob=ob: e.activation(out=o32[:], in_=ps_oh[ob][0:64, :], func=AF.Copy), w=["o32", ("ps_oh", ob)])
            S.op("pe", lambda e: e.matmul(ps_sc[0:64, :], ones64[:], osq[:], start=True, stop=True), r=["osq", "ones64"], w=["ps_sc"])
            S.op("act", lambda e: e.activation(out=nsq[:], in_=ps_sc[0:64, :], func=AF.Ln, scale=1.0 / 64.0, bias=epsc[:, 0:1]), r=["epsc"], w=["nsq", "ps_sc"])
            S.op("act", lambda e: e.activation(out=nsq[:], in_=nsq[:], func=AF.Exp, scale=-0.5), w=["nsq"])
            S.op("dve", lambda e: e.scalar_tensor_tensor(out=o32[:], in0=o32[:], scalar=gw[:, 0:1], in1=nsq[:], op0=ALU.mult, op1=ALU.mult),
                 r=["nsq", "gw"], w=["o32"])
            S.op("pool", lambda e, ob=ob, qs=qs: e.tensor_tensor(out=ohb[ob][:], in0=o32[:], in1=shg[:, qs], op=ALU.mult),
                 r=["o32", "shg"], w=[("ohb", ob)])
            S.op("sp", lambda e, ob=ob, qs=qs: e.dma_start(out=oh[:, qs], in_=ohb[ob][:]), r=[("ohb", ob)], dma=True)
    for i0_ in range(min(2, NT)):
        stageA1(i0_)
        stageA2(i0_)
    for i in range(NT):
        if i + 2 < NT:
            stageA1(i + 2)
        stageB(i)
        if i + 2 < NT:
            stageA2(i + 2)
        stageC(i)


def build_hgrn(lb_coef, T=SEQ):
    nc = bass.Bass("TRN2", target_bir_lowering=False)
    fm = nc.dram_tensor("fm", [NFM, T], BF16, kind="ExternalInput").ap()
    tm_sf = nc.dram_tensor("tm_sf", [T, 64], F32, kind="ExternalInput").ap()
    tm_v = nc.dram_tensor("tm_v", [T, 192], BF16, kind="ExternalInput").ap()
    lbl_bc = nc.dram_tensor("lbl_bc", [1, 128], F32, kind="ExternalInput").ap()
    lbl_col = nc.dram_tensor("lbl_col", [64, 2], F32, kind="ExternalInput").ap()
    gw = nc.dram_tensor("gw", [64, 1], F32, kind="ExternalInput").ap()
    mcat = nc.dram_tensor("mcat", [128, 136], F32, kind="ExternalInput").ap()
    mrev = nc.dram_tensor("mrev", [128, 128], F32, kind="ExternalInput").ap()
    oh = nc.dram_tensor("oh", [64, T], BF16, kind="ExternalOutput").ap()
    with contextlib.ExitStack() as st:
        S = Sched(nc)
        phase_hgrn(nc, S, st, fm, tm_sf, tm_v, lbl_bc, lbl_col, gw, mcat, mrev, oh, lb_coef, T)
        S.emit()
    return nc


def cmul(S, eng, o_re, o_im, a_re, a_im, b_re, b_im, t0, t1, rd, wr, conj_a=False):
    sg = -1.0 if conj_a else 1.0
    S.op(eng, lambda e: e.tensor_tensor(out=t0, in0=a_im, in1=b_im, op=ALU.mult), r=rd, w=[wr + "t0"])
    S.op(eng, lambda e: e.tensor_tensor(out=t1, in0=a_re, in1=b_re, op=ALU.mult), r=rd, w=[wr + "t1"])
    S.op("dve", lambda e: e.scalar_tensor_tensor(out=o_re, in0=t0, scalar=-sg, in1=t1, op0=ALU.mult, op1=ALU.add),
         r=[wr + "t0", wr + "t1"], w=[wr + "re"])
    S.op(eng, lambda e: e.tensor_tensor(out=t0, in0=a_im, in1=b_re, op=ALU.mult), r=rd + [wr + "re"], w=[wr + "t0"])
    S.op(eng, lambda e: e.tensor_tensor(out=t1, in0=a_re, in1=b_im, op=ALU.mult), r=rd + [wr + "re"], w=[wr + "t1"])
    S.op("dve", lambda e: e.scalar_tensor_tensor(out=o_im, in0=t0, scalar=sg, in1=t1, op0=ALU.mult, op1=ALU.add),
         r=[wr + "t0", wr + "t1"], w=[wr + "im"])


def s5_consts():
    negsig = np.repeat(-np.arange(16, dtype=np.float32), 64)[None, :]
    kidx = np.arange(32, dtype=np.float32)[None, :]
    midx = np.arange(1, 513, dtype=np.float32)[None, :]
    rowmask = (np.arange(64)[:, None] // 16 == np.arange(4)[None, :]).astype(np.float32)
    return negsig, kidx, midx, rowmask


def s5_params(z, l, j):
    gs = [4 * j + gl for gl in range(4)]
    f = np.float32
    pA_are = np.concatenate([np.repeat(z["s5_a_re"][l][g][None, :], 16, 0) for g in gs]).astype(f)
    pA_aim = np.concatenate([np.repeat(z["s5_a_im"][l][g][None, :], 16, 0) for g in gs]).astype(f)
    pA_ldt = np.concatenate([np.full((16, 1), z["s5_log_dt"][l][g]) for g in gs]).astype(f)
    pA_bre = np.concatenate([z["s5_b_re"][l][g].T for g in gs]).astype(f)
    pA_bim = np.concatenate([z["s5_b_im"][l][g].T for g in gs]).astype(f)
    pB = np.zeros((2, 128, 3), f)
    pB_cre = np.zeros((2, 128, 64), f)
    pB_cim = np.zeros((2, 128, 64), f)
    for q in range(2):
        for h in range(2):
            gl = 2 * q + h
            g = gs[gl]
            rows = slice(64 * h, 64 * h + 64)
            pB[q, rows, 0] = z["s5_a_re"][l][g]
            pB[q, rows, 1] = z["s5_a_im"][l][g]
            pB[q, rows, 2] = z["s5_log_dt"][l][g]
            pB_cre[q, rows, 16 * gl:16 * gl + 16] = z["s5_c_re"][l][g].T
            pB_cim[q, rows, 16 * gl:16 * gl + 16] = z["s5_c_im"][l][g].T
    dcol = z["s5_d"][l][64 * j:64 * j + 64][:, None].astype(f)
    pA = np.concatenate([pA_are, pA_aim, pA_bre, pA_bim, pA_ldt], axis=1)
    return {"s5p_pA": np.ascontiguousarray(pA), "s5p_pB": pB, "s5p_cre": pB_cre, "s5p_cim": pB_cim, "s5p_d": dcol}


def phase_s5(nc, S, st, fm, pA_d, pB_d, cre_d, cim_d, dcol_d, negsig_d, kidx_d, midx_d, rowmask_d, yg, T=SEQ):
    TS = lambda n, s, d: st.enter_context(nc.sbuf_tensor(n, s, d))
    PS = lambda n: st.enter_context(nc.psum_tensor(n, [128, 512], F32))
    NB = T // 16
    su = TS("s5_su", [64, T], BF16)
    outsb = TS("s5_out", [64, T], BF16)
    pA = TS("s5_pA", [64, 257], F32)
    dcol = TS("s5_dcol", [64, 1], F32)
    rowmask = TS("s5_rowmask", [64, 4], F32)
    negsig = TS("s5_negsig", [64, 1024], F32)
    SCR = TS("s5_scr", [128, 8192], F32)
    tA = [SCR[0:64, 1024 * i:1024 * (i + 1)] for i in range(8)]
    tAi = TS("s5_tAi", [64, 1024], I32)
    sA = [TS(f"s5_sA{i}", [64, 64], F32) for i in range(10)]
    sAi = TS("s5_sAi", [64, 64], I32)
    dtA = TS("s5_dtA", [64, 1], F32)
    W1tab = [[TS(f"s5_W1tab{q}{ri}", [64, 16, 128], BF16) for ri in range(2)] for q in range(2)]
    pB = [TS(f"s5_pB{q}", [128, 3], F32) for q in range(2)]
    crep = [TS(f"s5_crep{q}", [128, 64], F32) for q in range(2)]
    cimp = [TS(f"s5_cimp{q}", [128, 64], F32) for q in range(2)]
    kidx = TS("s5_kidx", [128, 32], F32)
    midx = TS("s5_midx", [128, 512], F32)
    tB = [TS(f"s5_tB{i}", [128, 32], F32) for i in range(7)]
    tBi = TS("s5_tBi", [128, 32], I32)
    cB = [TS(f"s5_cB{i}", [128, 1], F32) for i in range(6)]
    cBi = TS("s5_cBi", [128, 1], I32)
    gt = [SCR[:, 2048 * i:2048 * (i + 1)].rearrange("p (k c) -> p k c", k=32) for i in range(2)]
    Gpad = [[TS(f"s5_G{q}{ri}", [128, 32, 64], BF16) for ri in range(2)] for q in range(2)]
    Tc = [TS(f"s5_Tc{q}", [128, 512], F32) for q in range(2)]
    Tsn = [TS(f"s5_Ts{q}", [128, 512], F32) for q in range(2)]
    rho = [TS(f"s5_rho{q}", [128, 1], F32) for q in range(2)]
    l2 = [SCR[:, 4096 + 512 * i:4096 + 512 * (i + 1)] for i in range(6)]
    l2i = TS("s5_l2i", [128, 512], I32)
    roll = [TS(f"s5_roll{i}", [128, 512], F32) for i in range(2)]
    W15 = [[TS(f"s5_W15{q}{ri}", [128, 512], F32) for ri in range(2)] for q in range(2)]
    W1bf = [[TS(f"s5_W1bf{q}{ri}", [128, 16, 512], BF16) for ri in range(2)] for q in range(2)]
    Xbf = [[TS(f"s5_Xbf{q}{ri}", [128, 512], BF16) for ri in range(2)] for q in range(2)]
    ytmp = [TS(f"s5_ytmp{i}", [64, 512], F32) for i in range(2)]
    ps_z = [PS(f"s5_ps_z{i}") for i in range(2)]
    ps_y = [PS(f"s5_ps_y{i}") for i in range(2)]

    ld = lambda eng, dst, src, name: S.op(eng, lambda e: e.dma_start(out=dst, in_=src), w=[name], dma=True)
    ld("sp", su[:], fm[256:320, :], "su")
    ld("sp", pA[:], pA_d[:, :], "pA")
    ld("sp", dcol[:], dcol_d[:, :], "dcol")
    ld("sp", rowmask[:], rowmask_d[:, :], "rowmask")
    ld("sp", negsig[:], negsig_d.partition_broadcast(64), "negsig")
    ld("sp", kidx[:], kidx_d.partition_broadcast(128), "kidx")
    ld("sp", midx[:], midx_d.partition_broadcast(128), "midx")
    for q in range(2):
        ld("sp", pB[q][:], pB_d[q], ("pB", q))
        ld("sp", crep[q][:], cre_d[q], ("crep", q))
        ld("sp", cimp[q][:], cim_d[q], ("cimp", q))
    are, aim, bre, bim, ldt = pA[:, 0:64], pA[:, 64:128], pA[:, 128:192], pA[:, 192:256], pA[:, 256:257]
    lam, th, abr, abi, mg, zr, zi, den, u0, u1 = [t[:] for t in sA]
    S.op("act", lambda e: e.activation(out=dtA[:], in_=ldt, func=AF.Exp), r=["pA"], w=["dtA"])
    S.op("dve", lambda e: e.tensor_scalar(out=lam, in0=are, scalar1=dtA[:, 0:1], scalar2=None, op0=ALU.mult), r=["pA", "dtA"], w=["lamA"])
    S.op("dve", lambda e: e.tensor_scalar(out=th, in0=aim, scalar1=dtA[:, 0:1], scalar2=None, op0=ALU.mult), r=["pA", "dtA"], w=["thA"])
    S.op("dve", lambda e: e.tensor_copy(out=u0, in_=th), r=["thA"], w=["sAang"])
    sincos(S, u0, u1, sAi[:], den, abi, abr, "sA")
    S.op("act", lambda e: e.activation(out=mg, in_=lam, func=AF.Exp), r=["lamA"], w=["mgA"])
    S.op("dve", lambda e: e.tensor_tensor(out=abr, in0=abr, in1=mg, op=ALU.mult), r=["mgA", "sAcos"], w=["abr"])
    S.op("dve", lambda e: e.tensor_tensor(out=abi, in0=abi, in1=mg, op=ALU.mult), r=["mgA", "sAsin"], w=["abi"])
    S.op("dve", lambda e: e.tensor_scalar(out=abr, in0=abr, scalar1=-1.0, scalar2=None, op0=ALU.add), w=["abr"])
    S.op("dve", lambda e: e.tensor_tensor(out=den, in0=are, in1=are, op=ALU.mult), r=["pA", "sAcos", "sAsin"], w=["den"])
    S.op("dve", lambda e: e.tensor_tensor(out=u0, in0=aim, in1=aim, op=ALU.mult), r=["pA", "sAsin"], w=["u0"])
    S.op("dve", lambda e: e.tensor_tensor(out=den, in0=den, in1=u0, op=ALU.add), r=["u0"], w=["den"])
    S.op("dve", lambda e: e.reciprocal(out=den, in_=den), w=["den"])
    S.op("dve", lambda e: e.tensor_tensor(out=u0, in0=abr, in1=are, op=ALU.mult), r=["abr"], w=["u0"])
    S.op("dve", lambda e: e.tensor_tensor(out=u1, in0=abi, in1=aim, op=ALU.mult), r=["abi"], w=["u1"])
    S.op("dve", lambda e: e.tensor_tensor(out=zr, in0=u0, in1=u1, op=ALU.add), r=["u0", "u1"], w=["zr"])
    S.op("dve", lambda e: e.tensor_tensor(out=zr, in0=zr, in1=den, op=ALU.mult), r=["den"], w=["zr"])
    S.op("dve", lambda e: e.tensor_tensor(out=u0, in0=abi, in1=are, op=ALU.mult), r=["abi", "zr"], w=["u0"])
    S.op("dve", lambda e: e.tensor_tensor(out=u1, in0=abr, in1=aim, op=ALU.mult), r=["abr", "zr"], w=["u1"])
    S.op("dve", lambda e: e.tensor_tensor(out=zi, in0=u0, in1=u1, op=ALU.subtract), r=["u0", "u1"], w=["zi"])
    S.op("dve", lambda e: e.tensor_tensor(out=zi, in0=zi, in1=den, op=ALU.mult), r=["den"], w=["zi"])
    A3 = lambda t: t[:].rearrange("p (s m) -> p s m", s=16)
    bc3 = lambda ap: ap.unsqueeze(1).broadcast_to([64, 16, 64])
    ang3, kf3, hs3, sn3, cs3, mg3, w_r, w_i = tA
    S.op("dve", lambda e: e.tensor_tensor(out=A3(ang3), in0=A3(negsig), in1=bc3(th), op=ALU.mult), r=["negsig", "thA"], w=["tAang"])
    sincos(S, ang3[:], kf3[:], tAi[:], hs3[:], sn3[:], cs3[:], "tA")
    S.op("dve", lambda e: e.tensor_tensor(out=A3(mg3), in0=A3(negsig), in1=bc3(lam), op=ALU.mult), r=["negsig", "lamA"], w=["mg3"])
    S.op("act", lambda e: e.activation(out=mg3[:], in_=mg3[:], func=AF.Exp), w=["mg3"])
    S.op("dve", lambda e: e.tensor_tensor(out=cs3[:], in0=cs3[:], in1=mg3[:], op=ALU.mult), r=["mg3"], w=["tAcos"])
    S.op("dve", lambda e: e.tensor_tensor(out=sn3[:], in0=sn3[:], in1=mg3[:], op=ALU.mult), r=["mg3"], w=["tAsin"])
    cmul(S, "dve", A3(w_r), A3(w_i), A3(cs3), A3(sn3), bc3(zr), bc3(zi), A3(ang3), A3(kf3),
         ["tAcos", "tAsin", "zr", "zi", "tAang", "tAkf"], "wz")
    cmul(S, "dve", A3(cs3), A3(sn3), A3(w_r), A3(w_i), bc3(bre), bc3(bim), A3(ang3), A3(kf3),
         ["wzre", "wzim", "pA", "tAcos", "tAsin"], "Bs")
    for q in range(2):
        for ri, src in ((0, cs3), (1, sn3)):
            for h in range(2):
                gl = 2 * q + h
                S.op("dve", lambda e, q=q, ri=ri, h=h, gl=gl, src=src: e.tensor_scalar(
                    out=W1tab[q][ri][:, :, 64 * h:64 * h + 64], in0=A3(src), scalar1=rowmask[:, gl:gl + 1], scalar2=None, op0=ALU.mult),
                    r=["Bsre", "Bsim", "rowmask"], w=[("W1tab", q, ri, h)])
    S.barrier()
    bq = []

    class _Defer:
        def op(self, *a, **k):
            bq.append((a, k))
    SB = _Defer()
    for q in range(2):
        lamB, thB, dtB, phi, th15, junk = [t[:] for t in cB]
        angk, kfk, hsk, snk, csk, mgk, nsk = [t[:] for t in tB]
        pq = [("pB", q)]
        tg = f"B{q}"
        SB.op("act", lambda e, q=q: e.activation(out=dtB, in_=pB[q][:, 2:3], func=AF.Exp), r=pq, w=[tg + "dt"])
        SB.op("dve", lambda e, q=q: e.tensor_tensor(out=lamB, in0=pB[q][:, 0:1], in1=dtB, op=ALU.mult), r=pq + [tg + "dt"], w=[tg + "lam"])
        SB.op("dve", lambda e, q=q: e.tensor_tensor(out=thB, in0=pB[q][:, 1:2], in1=dtB, op=ALU.mult), r=pq + [tg + "dt"], w=[tg + "th"])
        SB.op("dve", lambda e: e.tensor_scalar(out=angk, in0=kidx[:], scalar1=thB[:, 0:1], scalar2=None, op0=ALU.mult),
             r=["kidx", tg + "th"], w=[tg + "kang"])
        sincos(SB, angk, kfk, tBi[:], hsk, snk, csk, tg + "k")
        SB.op("dve", lambda e: e.tensor_scalar(out=mgk, in0=kidx[:], scalar1=lamB[:, 0:1], scalar2=None, op0=ALU.mult),
             r=["kidx", tg + "lam"], w=[tg + "mgk"])
        SB.op("act", lambda e: e.activation(out=mgk, in_=mgk, func=AF.Exp), w=[tg + "mgk"])
        SB.op("dve", lambda e: e.tensor_tensor(out=csk, in0=csk, in1=mgk, op=ALU.mult), r=[tg + "mgk"], w=[tg + "kcos"])
        SB.op("dve", lambda e: e.tensor_tensor(out=snk, in0=snk, in1=mgk, op=ALU.mult), r=[tg + "mgk"], w=[tg + "ksin"])
        SB.op("dve", lambda e: e.tensor_scalar(out=nsk, in0=snk, scalar1=-1.0, scalar2=None, op0=ALU.mult), r=[tg + "ksin"], w=[tg + "nsk"])
        SB.op("dve", lambda e: e.tensor_scalar(out=kfk, in0=csk, scalar1=-1.0, scalar2=None, op0=ALU.mult), r=[tg + "kcos"], w=[tg + "kkf"])
        kb = lambda ap: ap.unsqueeze(2).broadcast_to([128, 32, 64])
        cb = lambda t: t[:].unsqueeze(1).broadcast_to([128, 32, 64])
        for ri, (f1, f2) in enumerate(((csk, nsk), (nsk, kfk))):
            SB.op("dve", lambda e, q=q, f1=f1: e.tensor_tensor(out=gt[0][:], in0=cb(crep[q]), in1=kb(f1), op=ALU.mult),
                 r=[("crep", q), tg + "kcos", tg + "nsk", tg + "kkf"], w=["gt0"])
            SB.op("dve", lambda e, q=q, f2=f2: e.tensor_tensor(out=gt[1][:], in0=cb(cimp[q]), in1=kb(f2), op=ALU.mult),
                 r=[("cimp", q), tg + "kcos", tg + "nsk", tg + "kkf"], w=["gt1"])
            SB.op("dve", lambda e, q=q, ri=ri: e.tensor_tensor(out=Gpad[q][ri][:], in0=gt[0][:], in1=gt[1][:], op=ALU.add),
                 r=["gt0", "gt1"], w=[("Gpad", q, ri)])
        SB.op("dve", lambda e: e.tensor_scalar(out=phi, in0=thB, scalar1=16.0, scalar2=None, op0=ALU.mult), r=[tg + "th"], w=[tg + "phi"])
        SB.op("dve", lambda e: e.tensor_scalar(out=th15, in0=phi, scalar1=1.0 / (2.0 * math.pi), scalar2=None, op0=ALU.mult),
             r=[tg + "phi"], w=[tg + "th15"])
        SB.op("dve", lambda e: e.tensor_copy(out=cBi[:], in_=th15), r=[tg + "th15"], w=[tg + "cBi"])
        SB.op("dve", lambda e: e.tensor_copy(out=th15, in_=cBi[:]), r=[tg + "cBi"], w=[tg + "th15"])
        SB.op("dve", lambda e: e.scalar_tensor_tensor(out=phi, in0=th15, scalar=-C1_2PI, in1=phi, op0=ALU.mult, op1=ALU.add),
             r=[tg + "th15"], w=[tg + "phi"])
        SB.op("dve", lambda e: e.scalar_tensor_tensor(out=phi, in0=th15, scalar=-C2_2PI, in1=phi, op0=ALU.mult, op1=ALU.add),
             r=[tg + "th15"], w=[tg + "phi"])
        SB.op("dve", lambda e: e.tensor_scalar(out=l2[0][:], in0=midx[:], scalar1=phi[:, 0:1], scalar2=None, op0=ALU.mult),
             r=["midx", tg + "phi"], w=["l2ang"])
        sincos(SB, l2[0][:], l2[1][:], l2i[:], l2[2][:], Tsn[q][:], Tc[q][:], "l2")
        SB.op("dve", lambda e, q=q: e.tensor_copy(out=Tsn[q][:], in_=Tsn[q][:]), r=["l2sin"], w=[("Ts", q)])
        SB.op("dve", lambda e, q=q: e.tensor_copy(out=Tc[q][:], in_=Tc[q][:]), r=["l2cos"], w=[("Tc", q)])
        SB.op("act", lambda e, q=q: e.activation(out=rho[q][:], in_=lamB, func=AF.Exp, scale=16.0), r=[tg + "lam"], w=[("rho", q)])
    suv = su[:].rearrange("p (m s) -> p s m", s=16)
    zi_ = 0
    for q in range(2):
        for ri in range(2):
            for s in range(16):
                pb = zi_ % 2
                zi_ += 1
                S.op("pe", lambda e, q=q, ri=ri, s=s, pb=pb: e.matmul(ps_z[pb][:, 0:NB], W1tab[q][ri][:, s, :], suv[:, s, :], start=True, stop=True),
                     r=["su", ("W1tab", q, ri, 0), ("W1tab", q, ri, 1)], w=[("ps_z", pb)])
                dst = W15[q][ri] if s == 15 else roll[s % 2]
                dn = ("W15", q, ri) if s == 15 else ("roll", s % 2)
                if s == 0:
                    S.op("dve", lambda e, pb=pb, dst=dst: e.tensor_copy(out=dst[:, 0:NB], in_=ps_z[pb][:, 0:NB]), w=[dn, ("ps_z", pb)])
                else:
                    S.op("dve", lambda e, pb=pb, dst=dst, s=s: e.tensor_tensor(out=dst[:, 0:NB], in0=ps_z[pb][:, 0:NB],
                                                                          in1=roll[(s - 1) % 2][:, 0:NB], op=ALU.add),
                         r=[("roll", (s - 1) % 2)], w=[dn, ("ps_z", pb)])
                S.op("act", lambda e, q=q, ri=ri, s=s, dst=dst: e.activation(out=W1bf[q][ri][:, s, 0:NB], in_=dst[:, 0:NB], func=AF.Copy),
                     r=[dn], w=[("W1bf", q, ri, s)])
                for _ in range(3):
                    if bq:
                        a, k = bq.pop(0)
                        S.op(*a, **k)
    while bq:
        a, k = bq.pop(0)
        S.op(*a, **k)
    for q in range(2):
        ur, ui, t0, t1, vr, vi = [t[:, 0:NB] for t in l2]
        tc, tsn = Tc[q][:, 0:NB], Tsn[q][:, 0:NB]
        wre, wim = W15[q][0][:, 0:NB], W15[q][1][:, 0:NB]
        cmul(S, "dve", ur, ui, tc, tsn, wre, wim, t0, t1, [("Tc", q), ("Ts", q), ("W15", q, 0), ("W15", q, 1), "l2v"], "l2u", conj_a=True)
        rb = rho[q][:, 0:1].broadcast_to([128, NB])
        S.op("dve", lambda e, rb=rb: e.tensor_tensor_scan(out=vr, data0=rb, data1=ur, initial=0.0, op0=ALU.mult, op1=ALU.add),
             r=["l2ure", ("rho", q)], w=["l2vr"])
        S.op("dve", lambda e, rb=rb: e.tensor_tensor_scan(out=vi, data0=rb, data1=ui, initial=0.0, op0=ALU.mult, op1=ALU.add),
             r=["l2uim", ("rho", q)], w=["l2vi"])
        cmul(S, "dve", ur, ui, tc, tsn, vr, vi, t0, t1, [("Tc", q), ("Ts", q), "l2vr", "l2vi"], "l2x")
        for ri, src in ((0, ur), (1, ui)):
            S.op("pool", lambda e, q=q, ri=ri: e.memset(Xbf[q][ri][:, 0:1], 0.0), w=[("Xbf", q, ri)])
            if NB > 1:
                S.op("act", lambda e, q=q, ri=ri, src=src: e.activation(out=Xbf[q][ri][:, 1:NB], in_=src[:, 0:NB - 1], func=AF.Copy),
                     r=["l2xre", "l2xim"], w=[("Xbf", q, ri)])
        S.op("dve", lambda e: e.tensor_copy(out=l2[0][:, 0:1], in_=l2[0][:, 0:1]), r=[("Xbf", q, 0), ("Xbf", q, 1)], w=["l2v", "l2ure", "l2uim"])
    outv = outsb[:].rearrange("p (m s) -> p s m", s=16)
    for s in range(16):
        pb = s % 2
        k = 0
        for q in range(2):
            for ri in range(2):
                S.op("pe", lambda e, q=q, ri=ri, s=s, pb=pb, k=k: e.matmul(ps_y[pb][0:64, 0:NB], Gpad[q][ri][:, s, :], W1bf[q][ri][:, s, 0:NB],
                                                                     start=(k == 0), stop=False),
                     r=[("Gpad", q, ri), ("W1bf", q, ri, s)], w=[("ps_y", pb)])
                k += 1
        for q in range(2):
            for ri in range(2):
                S.op("pe", lambda e, q=q, ri=ri, s=s, pb=pb, k=k: e.matmul(ps_y[pb][0:64, 0:NB], Gpad[q][ri][:, s + 16, :], Xbf[q][ri][:, 0:NB],
                                                                     start=False, stop=(k == 7)),
                     r=[("Gpad", q, ri), ("Xbf", q, ri)], w=[("ps_y", pb)])
                k += 1
        S.op("dve", lambda e, s=s, pb=pb: e.scalar_tensor_tensor(out=ytmp[pb][:, 0:NB], in0=suv[:, s, :], scalar=dcol[:, 0:1],
                                                            in1=ps_y[pb][0:64, 0:NB], op0=ALU.mult, op1=ALU.add),
             r=["su", "dcol"], w=[("ytmp", pb), ("ps_y", pb)])
        S.op("act", lambda e, s=s, pb=pb: e.activation(out=outv[:, s, :], in_=ytmp[pb][:, 0:NB], func=AF.Gelu),
             r=[("ytmp", pb)], w=[("outsb", s)])
    S.op("sp", lambda e: e.dma_start(out=yg[:, :], in_=outsb[:]), r=[("outsb", s) for s in range(16)], dma=True)


def build_s5(T=SEQ):
    nc = bass.Bass("TRN2", target_bir_lowering=False)
    D = lambda n, s, d=F32, k="ExternalInput": nc.dram_tensor(n, s, d, kind=k).ap()
    fm = D("fm", [NFM, T], BF16)
    pA = D("s5p_pA", [64, 257]); pB = D("s5p_pB", [2, 128, 3]); cre = D("s5p_cre", [2, 128, 64]); cim = D("s5p_cim", [2, 128, 64])
    dcol = D("s5p_d", [64, 1]); negsig = D("negsig", [1, 1024]); kidx = D("kidx", [1, 32]); midx = D("midx", [1, 512])
    rowmask = D("rowmask", [64, 4])
    yg = D("yg", [64, T], BF16, "ExternalOutput")
    with contextlib.ExitStack() as st:
        S = Sched(nc)
        phase_s5(nc, S, st, fm, pA, pB, cre, cim, dcol, negsig, kidx, midx, rowmask, yg, T)
        S.emit()
    return nc


def phase_out(nc, S, st, mixin, ssg_d, hT, wout_d, gluw_d, glub_d, fnw_d, hout, final, NTOK=TQ):
    TS = lambda n, s, d: st.enter_context(nc.sbuf_tensor(n, s, d))
    PS = lambda n: st.enter_context(nc.psum_tensor(n, [128, 512], F32))
    wst = [TS(f"po_wst{i}", [128, 1024], F32) for i in range(2)]
    wout = TS("po_wout", [128, 8, 1024], BF16)
    gst = TS("po_gst", [128, 2, 256], F32)
    gluw = TS("po_gluw", [128, 2, 256], BF16)
    glub = TS("po_glub", [128, 2], F32)
    fnw = TS("po_fnw", [128, 8], F32)
    ones = TS("po_ones", [128, 128], BF16)
    mix = [TS(f"po_mix{i}", [128, 8, 512], BF16) for i in range(2)]
    ssg = [TS(f"po_ssg{i}", [128, 2, 512], BF16) for i in range(2)]
    hin = [TS(f"po_hin{i}", [128, 8, 512], F32) for i in range(2)]
    sg = TS("po_sg", [128, 512], F32)
    osb = TS("po_osb", [128, 2, 512], BF16)
    hn = TS("po_hn", [128, 8, 512], F32)
    hsq = TS("po_hsq", [128, 8, 512], BF16)
    nsq = TS("po_nsq", [128, 512], F32)
    ps_g = PS("po_ps_g")
    ps_o = [PS(f"po_ps_o{i}") for i in range(3)]
    ps_n = PS("po_ps_n")

    S.op("pool", lambda e: e.memset(ones[:], 1.0), w=["ones"])
    S.op("sp", lambda e: e.dma_start(out=gst[:], in_=gluw_d.rearrange("(k p) o -> p k o", p=128)), w=["gst"], dma=True)
    S.op("sp", lambda e: e.dma_start(out=glub[:], in_=glub_d[:, :]), w=["glub"], dma=True)
    S.op("sp", lambda e: e.dma_start(out=fnw[:], in_=fnw_d[:, :]), w=["fnw"], dma=True)
    S.op("dve", lambda e: e.tensor_copy(out=gluw[:], in_=gst[:]), r=["gst"], w=["gluw"])
    for k in range(8):
        S.op("sp", lambda e, k=k: e.dma_start(out=wst[k % 2][:], in_=wout_d[k * 128:(k + 1) * 128, :]), w=[("wst", k % 2)], dma=True)
        S.op("pool" if k % 2 else "dve", lambda e, k=k: e.tensor_copy(out=wout[:, k, :], in_=wst[k % 2][:]), r=[("wst", k % 2)], w=[("wout", k)])
    wr = [("wout", k) for k in range(8)]
    mv = mixin.rearrange("(k p) t -> p k t", p=128)
    sv = ssg_d.rearrange("(k p) t -> p k t", p=128)
    hv = hT.rearrange("(k p) t -> p k t", p=128)
    ov = hout.rearrange("(k p) t -> p k t", p=128)
    oi = 0
    for ti in range(NTOK // 512):
        b = ti % 2
        ts = slice(ti * 512, (ti + 1) * 512)
        S.op("sp", lambda e, b=b, ts=ts: e.dma_start(out=mix[b][:], in_=mv[:, :, ts]), w=[("mix", b)], dma=True)
        S.op("sp", lambda e, b=b, ts=ts: e.dma_start(out=ssg[b][:], in_=sv[:, :, ts]), w=[("ssg", b)], dma=True)
        S.op("pool", lambda e, b=b, ts=ts: e.dma_start(out=hin[b][:], in_=hv[:, :, ts]), w=[("hin", b)], dma=True)
        for oc in range(2):
            for kc in range(2):
                S.op("pe", lambda e, b=b, oc=oc, kc=kc: e.matmul(ps_g[:], gluw[:, kc, oc * 128:(oc + 1) * 128], mix[b][:, 2 + kc, :],
                                                             start=(kc == 0), stop=(kc == 1)), r=[("mix", b), "gluw"], w=["ps_g"])
            S.op("act", lambda e, oc=oc: e.activation(out=sg[:], in_=ps_g[:], func=AF.Sigmoid, bias=glub[:, oc:oc + 1]),
                 r=["glub"], w=["sg", "ps_g"])
            S.op("dve", lambda e, b=b, oc=oc: e.tensor_tensor(out=sg[:], in0=sg[:], in1=mix[b][:, 2 + oc, :], op=ALU.mult),
                 r=[("mix", b)], w=["sg"])
            S.op("dve", lambda e, b=b, oc=oc: e.tensor_tensor(out=osb[:, oc, :], in0=sg[:], in1=ssg[b][:, oc, :], op=ALU.mult),
                 r=[("ssg", b), "sg"], w=[("osb", oc)])
        for dc in range(8):
            pb = oi % 3
            oi += 1
            for kc in range(8):
                rhs = (lambda b=b, kc=kc: osb[:, kc - 2, :]) if kc in (2, 3) else (lambda b=b, kc=kc: mix[b][:, kc, :])
                S.op("pe", lambda e, dc=dc, kc=kc, pb=pb, rhs=rhs: e.matmul(ps_o[pb][:], wout[:, kc, dc * 128:(dc + 1) * 128], rhs(),
                                                                      start=(kc == 0), stop=(kc == 7)),
                     r=wr + [("mix", b), ("osb", 0), ("osb", 1)], w=[("ps_o", pb)])
            S.op("dve", lambda e, b=b, dc=dc, pb=pb: e.tensor_tensor(out=hn[:, dc, :], in0=ps_o[pb][:], in1=hin[b][:, dc, :], op=ALU.add),
                 r=[("hin", b)], w=[("hn", dc), ("ps_o", pb)])
            if not final:
                S.op("sp", lambda e, dc=dc, ts=ts: e.dma_start(out=ov[:, dc, ts], in_=hn[:, dc, :]), r=[("hn", dc)], dma=True)
        if final:
            hr = [("hn", dc) for dc in range(8)]
            S.op("act", lambda e: e.activation(out=hsq[:], in_=hn[:], func=AF.Square), r=hr, w=["hsq"])
            for k in range(8):
                S.op("pe", lambda e, k=k: e.matmul(ps_n[:], ones[:], hsq[:, k, :], start=(k == 0), stop=(k == 7)), r=["hsq", "ones"], w=["ps_n"])
            S.op("act", lambda e: e.activation(out=nsq[:], in_=ps_n[:], func=AF.Sqrt, scale=1.0 / D_MODEL, bias=EPS), w=["nsq", "ps_n"])
            S.op("dve", lambda e: e.reciprocal(out=nsq[:], in_=nsq[:]), w=["nsq"])
            for dc in range(8):
                S.op("pool" if dc % 2 else "dve", lambda e, dc=dc: e.scalar_tensor_tensor(
                    out=hn[:, dc, :], in0=hn[:, dc, :], scalar=fnw[:, dc:dc + 1], in1=nsq[:], op0=ALU.mult, op1=ALU.mult) if dc % 2 == 0 else
                    e.tensor_tensor(out=hn[:, dc, :], in0=hn[:, dc, :], in1=nsq[:], op=ALU.mult),
                    r=["nsq", "fnw"], w=[("hn", dc)])
                if dc % 2:
                    S.op("pool", lambda e, dc=dc: e.tensor_scalar(out=hn[:, dc, :], in0=hn[:, dc, :], scalar1=fnw[:, dc:dc + 1], scalar2=None,
                                                                  op0=ALU.mult), r=["fnw"], w=[("hn", dc)])
                S.op("sp", lambda e, dc=dc, ts=ts: e.dma_start(out=ov[:, dc, ts], in_=hn[:, dc, :]), r=[("hn", dc)], dma=True)


def build_out(final, NTOK=TQ):
    nc = bass.Bass("TRN2", target_bir_lowering=False)
    D = lambda n, s, d=F32, k="ExternalInput": nc.dram_tensor(n, s, d, kind=k).ap()
    mixin = D("mixin", [1024, NTOK], BF16)
    ssg = D("ssg", [256, NTOK], BF16)
    hT = D("hT", [D_MODEL, NTOK])
    wout = D("wout", [1024, 1024]); gluw = D("gluw", [256, 256]); glub = D("glub", [128, 2]); fnw = D("fnw", [128, 8])
    hout = D("hout", [D_MODEL, NTOK], F32, "ExternalOutput")
    with contextlib.ExitStack() as st:
        S = Sched(nc)
        phase_out(nc, S, st, mixin, ssg, hT, wout, gluw, glub, fnw, hout, final, NTOK)
        S.emit()
    return nc


_CACHE = {}


def _prog(key, fn):
    if key not in _CACHE:
        _CACHE[key] = fn()
    return _CACHE[key]


def build_mixers(l, T=SEQ, which=("ip", "at", "hg", "s5")):
    lambda_init = 0.8 - 0.6 * math.exp(-0.3 * l)
    nc = bass.Bass("TRN2", target_bir_lowering=False)
    D = lambda n, s, d=F32, k="ExternalInput": nc.dram_tensor(n, s, d, kind=k).ap()
    hT = D("hT", [D_MODEL, T]); wcat = D("wcat", [D_MODEL, NFM + NTM]); nw = D("nw", [128, 8])
    lqk = D("lqk", [1, 256]); subln = D("subln", [128, 1]); ropef = D("ropef", [128, 1]); rmat = D("rmat", [128, 128])
    cmask = D("cmask", [4, 128, 512], BF16)
    lbl_bc = D("lbl_bc", [1, 128]); lbl_col = D("lbl_col", [64, 2]); gw = D("gw", [64, 1]); mcat = D("mcat", [128, 136]); mrev = D("mrev", [128, 128])
    pA = D("s5p_pA", [64, 257]); pB = D("s5p_pB", [2, 128, 3]); cre = D("s5p_cre", [2, 128, 64]); cim = D("s5p_cim", [2, 128, 64])
    dcol = D("s5p_d", [64, 1]); negsig = D("negsig", [1, 1024]); kidx = D("kidx", [1, 32]); midx = D("midx", [1, 512]); rowmask = D("rowmask", [64, 4])
    fm = D("fm", [NFM, T], BF16, "Internal")
    tm_sf = D("tm_sf", [T, 64], F32, "Internal")
    tm_v = D("tm_v", [T, 192], BF16, "Internal")
    mo = D("mo", [320, T], BF16, "ExternalOutput")
    if "ip" in which:
        with contextlib.ExitStack() as st:
            S = Sched(nc)
            phase_inproj(nc, S, st, hT, wcat, nw, fm, tm_sf, tm_v, T)
            S.emit()
    if "at" in which:
        with contextlib.ExitStack() as st:
            S = Sched(nc)
            S.op("sp", lambda e: e.dma_start(out=mo[128:192, :], in_=fm[192:256, :]), dma=True)
            phase_attn(nc, S, st, fm, tm_v, lqk, subln, ropef, rmat, cmask, mo[192:320, :], lambda_init, T)
            S.emit()
    if "hg" in which:
        with contextlib.ExitStack() as st:
            S = Sched(nc)
            phase_hgrn(nc, S, st, fm, tm_sf, tm_v, lbl_bc, lbl_col, gw, mcat, mrev, mo[0:64, :], float(l), T)
            S.emit()
    if "s5" in which:
        with contextlib.ExitStack() as st:
            S = Sched(nc)
            phase_s5(nc, S, st, fm, pA, pB, cre, cim, dcol, negsig, kidx, midx, rowmask, mo[64:128, :], T)
            S.emit()
    return nc


def mixer_inputs(inp, l, c, hT_b):
    f = np.float32
    j = c % 4
    ropef, rmat, cmask = attn_consts()
    mcat, mrev = hgrn_consts()
    negsig, kidx, midx, rowmask = s5_consts()
    lbl = np.asarray(inp["hgrn_lb_logits"], f)[:, 64 * j:64 * j + 64]
    d = {"hT": hT_b, "wcat": np.ascontiguousarray(np.asarray(inp["w_in"][l], f)[:, core_cols(j)]),
         "nw": np.ascontiguousarray(np.asarray(inp["norm_w"][l], f).reshape(8, 128).T),
         "lqk": np.concatenate([inp["diff_lq1"][l], inp["diff_lq2"][l], inp["diff_lk1"][l], inp["diff_lk2"][l]])[None, :].astype(f),
         "subln": np.asarray(inp["diff_subln_w"][l], f)[:, None], "ropef": ropef, "rmat": rmat, "cmask": cmask,
         "lbl_bc": np.ascontiguousarray(lbl.reshape(1, 128)), "lbl_col": np.ascontiguousarray(lbl.T),
         "gw": np.asarray(inp["hgrn_norm_w"][l], f)[:, None], "mcat": mcat, "mrev": mrev,
         "negsig": negsig, "kidx": kidx, "midx": midx, "rowmask": rowmask}
    d.update(s5_params(inp, l, j))
    return d


def kernel(**inp):
    f = np.float32
    x = np.asarray(inp["x"], f)
    cores = list(range(NCORES))
    hT = [np.ascontiguousarray(x[b].T) for b in range(BATCH)]
    for l in range(DEPTH):
        nc = _prog(("mix", l), lambda: build_mixers(l))
        ims = [mixer_inputs(inp, l, c, hT[c // 4]) for c in cores]
        rm = run_bass_kernel_spmd(nc, ims, core_ids=cores).results
        final = (l == DEPTH - 1)
        nc = _prog(("out", final), lambda: build_out(final))
        ims = []
        for c in cores:
            b, tq = c // 4, c % 4
            ts = slice(tq * TQ, (tq + 1) * TQ)
            src = [4 * b + j for j in range(4)]
            mixin = np.concatenate([rm[s]["mo"][0:64, ts] for s in src] + [rm[s]["mo"][64:128, ts] for s in src]
                                   + [rm[s]["mo"][192:320, ts] for s in src])
            ssg = np.concatenate([rm[s]["mo"][128:192, ts] for s in src])
            ims.append({"mixin": np.ascontiguousarray(mixin), "ssg": np.ascontiguousarray(ssg), "hT": np.ascontiguousarray(hT[b][:, ts]),
                        "wout": np.asarray(inp["w_out"][l], f), "gluw": np.asarray(inp["s5_glu_w"][l], f),
                        "glub": np.ascontiguousarray(np.asarray(inp["s5_glu_b"][l], f).reshape(2, 128).T),
                        "fnw": np.ascontiguousarray(np.asarray(inp["final_norm_w"], f).reshape(8, 128).T)})
        ro = run_bass_kernel_spmd(nc, ims, core_ids=cores).results
        hT = [np.concatenate([ro[4 * b + tq]["hout"] for tq in range(4)], axis=1) for b in range(BATCH)]
    out = np.stack([hT[b].T for b in range(BATCH)]).astype(f)
    return np.ascontiguousarray(out)
```

```python
import contextlib
import math
import numpy as np
import ml_dtypes
import concourse.bass as bass
import concourse.mybir as mybir
from concourse.bass_utils import run_bass_kernel_spmd

F32 = mybir.dt.float32
BF16 = mybir.dt.bfloat16
I32 = mybir.dt.int32
AF = mybir.ActivationFunctionType
ALU = mybir.AluOpType
AX = mybir.AxisListType

D_MODEL = 1024
SEQ = 8192
BATCH = 2
DEPTH = 2
EPS = 1e-6
NCORES = 8
TQ = SEQ // 4
ROPE_THETA = 500000.0
import os
DBG = set(os.environ.get("KDBG", "").split(","))


class Sched:
    ENGS = ["pe", "act", "dve", "pool", "sp"]

    def __init__(self, nc):
        self.nc = nc
        self.ops = []
        self.last_w = {}
        self.readers = {}
        self.cnt = {e: 0 for e in self.ENGS}
        self.dma_cnt = {}
        self.base = set()

    def op(self, eng, fn, r=(), w=(), dma=False):
        deps = set(self.base)
        for x in r:
            if x in self.last_w:
                deps.add(self.last_w[x])
        for x in w:
            if x in self.last_w:
                deps.add(self.last_w[x])
            for d in self.readers.get(x, ()):
                deps.add(d)
        if dma:
            q = self.dma_cnt.get(eng, 0)
            self.dma_cnt[eng] = q + 1
            tok = ("dma", eng, q)
        else:
            self.cnt[eng] += 1
            tok = ("eng", eng, self.cnt[eng])
        self.ops.append((eng, fn, deps, tok))
        for x in w:
            self.last_w[x] = tok
            self.readers[x] = []
        for x in r:
            self.readers.setdefault(x, []).append(tok)
        return tok

    def barrier(self):
        b = set()
        for e in self.ENGS:
            if self.cnt[e] > 0:
                b.add(("eng", e, self.cnt[e]))
        for e, n in self.dma_cnt.items():
            for q in range(max(0, n - self.NSLOT), n):
                b.add(("dma", e, q))
        self.base = b
        self.last_w = {}
        self.readers = {}

    NSLOT = 8

    def emit(self):
        nc = self.nc
        NSLOT = self.NSLOT
        needed = set()
        for (eng, fn, deps, tok) in self.ops:
            for d in deps:
                if d[0] == "eng" and not (d[1] == "pe" and eng == "pe"):
                    needed.add(d)
        sig = {}
        run = {e: 0 for e in self.ENGS}
        for (eng, fn, deps, tok) in self.ops:
            if tok[0] == "eng":
                if tok in needed:
                    run[eng] += 1
                sig[tok] = run[eng]
        with contextlib.ExitStack() as st:
            esem = {e: st.enter_context(nc.semaphore("s_" + e)) for e in self.ENGS}
            dsem = {}
            for e in self.dma_cnt:
                dsem[e] = [st.enter_context(nc.semaphore(f"d_{e}_{i}")) for i in range(NSLOT)]
            block = st.enter_context(nc.Block())
            per = {e: [o for o in self.ops if o[0] == e] for e in self.ENGS}

            def mk(ename):
                def body(eng):
                    seen = {}

                    def wait(tok):
                        if tok[0] == "eng":
                            _, e2, n = tok
                            if e2 == "pe" and ename == "pe":
                                return
                            v = sig[tok]
                            key = ("eng", e2)
                            if seen.get(key, 0) >= v:
                                return
                            seen[key] = v
                            eng.wait_ge(esem[e2], v)
                        else:
                            _, e2, q = tok
                            slot = q % NSLOT
                            val = 16 * (q // NSLOT + 1)
                            key = ("dma", e2, slot)
                            if seen.get(key, 0) >= val:
                                return
                            seen[key] = val
                            eng.wait_ge(dsem[e2][slot], val)
                    for (_, fn, deps, tok) in per[ename]:
                        for d in sorted(deps):
                            wait(d)
                        if tok[0] == "dma":
                            q = tok[2]
                            if q >= NSLOT:
                                wait(("dma", ename, q - NSLOT))
                            ins = fn(eng)
                            ins.then_inc(dsem[ename][q % NSLOT], 16)
                        else:
                            ins = fn(eng)
                            if tok in needed:
                                ins.then_inc(esem[ename], 1)
                    n = self.dma_cnt.get(ename, 0)
                    for q in range(max(0, n - NSLOT), n):
                        wait(("dma", ename, q))
                return body
            block.tensor(mk("pe"))
            block.scalar(mk("act"))
            block.vector(mk("dve"))
            block.gpsimd(mk("pool"))
            block.sync(mk("sp"))


NFM = 704
NTM = 256
FM_CH = [(0, 128), (128, 128), (256, 64), (320, 128), (448, 128), (576, 128)]


def phase_inproj(nc, S, st, hT, wcat, nw, fm, tm_sf, tm_v, T=SEQ):
    TS = lambda n, s, d: st.enter_context(nc.sbuf_tensor(n, s, d))
    PS = lambda n: st.enter_context(nc.psum_tensor(n, [128, 512], F32))
    nw_sb = TS("ip_nw", [128, 8], F32)
    wst = [TS(f"ip_wst{i}", [128, NFM + NTM], F32) for i in range(2)]
    wall = TS("ip_wall", [128, 8, NFM + NTM], BF16)
    ones = TS("ip_ones", [128, 128], BF16)
    epsc = TS("ip_epsc", [128, 1], F32)
    xin = [TS(f"ip_xin{i}", [128, 8, 512], F32) for i in range(2)]
    xsq = TS("ip_xsq", [128, 8, 512], BF16)
    sq = TS("ip_sq", [128, 512], F32)
    rstd = TS("ip_rstd", [128, 512], F32)
    xn = [TS(f"ip_xn{i}", [128, 8, 512], BF16) for i in range(2)]
    fmo = [TS(f"ip_fmo{i}", [128, 512], BF16) for i in range(6)]
    tsf = [TS(f"ip_tsf{i}", [128, 4, 64], F32) for i in range(2)]
    tv = [TS(f"ip_tv{i}", [128, 4, 192], BF16) for i in range(2)]
    ps_ss = PS("ip_ps_ss")
    ps_fm = [PS(f"ip_ps_fm{i}") for i in range(4)]
    ps_tm = [PS(f"ip_ps_tm{i}") for i in range(2)]

    S.op("sp", lambda e: e.dma_start(out=nw_sb[:], in_=nw[:, :]), w=["nw"], dma=True)
    S.op("pool", lambda e: e.memset(ones[:], 1.0), w=["ones"])
    S.op("pool", lambda e: e.memset(epsc[:], EPS), w=["epsc"])
    for k in range(8):
        S.op("sp", lambda e, k=k: e.dma_start(out=wst[k % 2][:], in_=wcat[k * 128:(k + 1) * 128, :]),
             w=[("wst", k % 2)], dma=True)
        S.op("dve", lambda e, k=k: e.tensor_scalar(out=wall[:, k, :], in0=wst[k % 2][:], scalar1=nw_sb[:, k:k + 1],
                                                  scalar2=None, op0=ALU.mult),
             r=[("wst", k % 2), "nw"], w=[("wall", k)])
    wall_r = [("wall", k) for k in range(8)]
    hT_v = hT.rearrange("(k p) t -> p k t", p=128)
    fmi = 0
    NTI = T // 512

    def load(ti):
        b = ti % 2
        t0 = ti * 512
        S.op("pool", lambda e, b=b, t0=t0: e.dma_start(out=xin[b][:, 0:4, :], in_=hT_v[:, 0:4, t0:t0 + 512]),
             w=[("xin", b, 0)], dma=True)
        S.op("pool", lambda e, b=b, t0=t0: e.dma_start(out=xin[b][:, 4:8, :], in_=hT_v[:, 4:8, t0:t0 + 512]),
             w=[("xin", b, 1)], dma=True)
    def front_sq(ti):
        b = ti % 2
        xr = [("xin", b, 0), ("xin", b, 1)]
        S.op("act", lambda e, b=b: e.activation(out=xsq[:], in_=xin[b][:], func=AF.Square), r=xr, w=["xsq"])

    def front_ss(ti):
        b = ti % 2
        for k in range(8):
            S.op("pe", lambda e, k=k: e.matmul(ps_ss[:], ones[:], xsq[:, k, :], start=(k == 0), stop=(k == 7)),
                 r=["xsq", "ones"], w=["ps_ss"])
        S.op("act", lambda e: e.activation(out=sq[:], in_=ps_ss[:], func=AF.Ln, scale=1.0 / D_MODEL, bias=epsc[:, 0:1]),
             r=["epsc"], w=["ps_ss", "sq"])
        S.op("act", lambda e: e.activation(out=rstd[:], in_=sq[:], func=AF.Exp, scale=-0.5), r=["sq"], w=["rstd"])
        for hh in range(2):
            S.op("dve", lambda e, b=b, hh=hh: e.tensor_tensor(out=xn[b][:, 4 * hh:4 * hh + 4, :], in0=xin[b][:, 4 * hh:4 * hh + 4, :],
                                                         in1=rstd[:].unsqueeze(1).broadcast_to([128, 4, 512]), op=ALU.mult),
                 r=[("xin", b, hh), "rstd"], w=[("xn", b, hh)])
    load(0)
    if NTI > 1:
        load(1)
    front_sq(0)
    front_ss(0)
    for ti in range(NTI):
        b = ti % 2
        t0 = ti * 512
        if ti + 1 < NTI:
            front_sq(ti + 1)
        xnr = [("xn", b, 0), ("xn", b, 1)]
        for ci, (c0, cw) in enumerate(FM_CH):
            if "NOFM" in DBG or ("FM%d" % ci) in DBG:
                continue
            if ci == 3:
                if ti + 1 < NTI:
                    front_ss(ti + 1)
                if ti + 2 < NTI:
                    load(ti + 2)
            pb = fmi % 4
            for k in range(8):
                S.op("pe", lambda e, k=k, c0=c0, cw=cw, pb=pb, b=b: e.matmul(
                    ps_fm[pb][0:cw, :], wall[:, k, c0:c0 + cw], xn[b][:, k, :], start=(k == 0), stop=(k == 7)),
                    r=xnr + wall_r, w=[("ps_fm", pb)])
            fb = fmi % 6
            fmi += 1
            if ci == 0:
                S.op("act", lambda e, pb=pb, fb=fb: e.activation(out=fmo[fb][0:64, :], in_=ps_fm[pb][0:64, :], func=AF.Silu),
                     r=[("ps_fm", pb)], w=[("fmo", fb, 0)])
                S.op("act", lambda e, pb=pb, fb=fb: e.activation(out=fmo[fb][64:128, :], in_=ps_fm[pb][64:128, :],
                                                               func=AF.Sigmoid, scale=-1.0),
                     r=[("ps_fm", pb)], w=[("fmo", fb, 1)])
                wl = [("fmo", fb, 0), ("fmo", fb, 1)]
            elif ci in (1, 5):
                S.op("act", lambda e, pb=pb, fb=fb: e.activation(out=fmo[fb][:], in_=ps_fm[pb][:], func=AF.Silu),
                     r=[("ps_fm", pb)], w=[("fmo", fb, 0), ("fmo", fb, 1)])
                wl = [("fmo", fb, 0), ("fmo", fb, 1)]
            else:
                S.op("dve", lambda e, pb=pb, fb=fb, cw=cw: e.tensor_copy(out=fmo[fb][0:cw, :], in_=ps_fm[pb][0:cw, :]),
                     r=[("ps_fm", pb)], w=[("fmo", fb, 0), ("fmo", fb, 1)])
                wl = [("fmo", fb, 0), ("fmo", fb, 1)]
            S.op("sp", lambda e, fb=fb, c0=c0, cw=cw, t0=t0: e.dma_start(out=fm[c0:c0 + cw, t0:t0 + 512], in_=fmo[fb][0:cw, :]),
                 r=wl, dma=True)
        for pb in range(0 if "NOTM" in DBG else 2):
            for t4 in (2 * pb, 2 * pb + 1):
                off = (t4 % 2) * 256
                for k in range(8):
                    S.op("pe", lambda e, k=k, t4=t4, pb=pb, off=off, b=b: e.matmul(
                        ps_tm[pb][:, off:off + 256], xn[b][:, k, t4 * 128:(t4 + 1) * 128], wall[:, k, NFM:NFM + NTM],
                        start=(k == 0), stop=(k == 7)),
                        r=xnr + wall_r, w=[("ps_tm", pb)])
            if "TMNOEVAC" in DBG:
                continue
            for t4 in (2 * pb, 2 * pb + 1):
                off = (t4 % 2) * 256
                if "TMNOACT" not in DBG:
                  S.op("act", lambda e, pb=pb, b=b, t4=t4, off=off: e.activation(
                    out=tsf[b][:, t4, :], in_=ps_tm[pb][:, off:off + 64], func=AF.Sigmoid),
                    w=[("ps_tm", pb), ("tsf", b, t4)])
                if "TMNODVE" not in DBG:
                  S.op("dve", lambda e, pb=pb, b=b, t4=t4, off=off: e.tensor_copy(
                    out=tv[b][:, t4, :], in_=ps_tm[pb][:, off + 64:off + 256]),
                    w=[("ps_tm", pb), ("tv", b, t4)])
        if "TMNODMA" in DBG:
            continue
        S.op("sp", lambda e, b=b, t0=t0: e.dma_start(
            out=tm_sf[t0:t0 + 512, :].rearrange("(a p) c -> p a c", p=128), in_=tsf[b][:]),
            r=[("tsf", b, t4) for t4 in range(4)], dma=True)
        S.op("sp", lambda e, b=b, t0=t0: e.dma_start(
            out=tm_v[t0:t0 + 512, :].rearrange("(a p) c -> p a c", p=128), in_=tv[b][:]),
            r=[("tv", b, t4) for t4 in range(4)], dma=True)


def core_cols(j):
    r = lambda s, n: list(range(s, s + n))
    fmc = (r(0 + 64 * j, 64) + r(256 + 64 * j, 64) + r(768 + 64 * j, 64) + r(1280 + 64 * j, 64) + r(1024 + 64 * j, 64)
           + r(1536 + 128 * j, 128) + r(2048 + 128 * j, 128) + r(3072 + 128 * j, 128))
    tmc = r(256 + 64 * j, 64) + r(512 + 64 * j, 64) + r(2560 + 128 * j, 128)
    return np.array(fmc + tmc)


def build_inproj(T=SEQ):
    nc = bass.Bass("TRN2", target_bir_lowering=False)
    hT = nc.dram_tensor("hT", [D_MODEL, T], F32, kind="ExternalInput").ap()
    wcat = nc.dram_tensor("wcat", [D_MODEL, NFM + NTM], F32, kind="ExternalInput").ap()
    nw = nc.dram_tensor("nw", [128, 8], F32, kind="ExternalInput").ap()
    fm = nc.dram_tensor("fm", [NFM, T], BF16, kind="ExternalOutput").ap()
    tm_sf = nc.dram_tensor("tm_sf", [T, 64], F32, kind="ExternalOutput").ap()
    tm_v = nc.dram_tensor("tm_v", [T, 192], BF16, kind="ExternalOutput").ap()
    with contextlib.ExitStack() as st:
        S = Sched(nc)
        phase_inproj(nc, S, st, hT, wcat, nw, fm, tm_sf, tm_v, T)
        S.emit()
    return nc


C1_2PI = 6.28125
C2_2PI = 2.0 * math.pi - 6.28125


def sincos(S, ang, kf, ki, hs, sin_out, cos_out, tag, eng="dve"):
    a, k, h = tag + "ang", tag + "kf", tag + "hs"
    S.op(eng, lambda e: e.tensor_scalar(out=kf, in0=ang, scalar1=1.0 / (2.0 * math.pi), scalar2=None, op0=ALU.mult), r=[a], w=[k])
    S.op(eng, lambda e: e.tensor_copy(out=ki, in_=kf), r=[k], w=[tag + "ki"])
    S.op(eng, lambda e: e.tensor_copy(out=kf, in_=ki), r=[tag + "ki"], w=[k])
    S.op("dve", lambda e: e.scalar_tensor_tensor(out=ang, in0=kf, scalar=-C1_2PI, in1=ang, op0=ALU.mult, op1=ALU.add), r=[k], w=[a])
    S.op("dve", lambda e: e.scalar_tensor_tensor(out=ang, in0=kf, scalar=-C2_2PI, in1=ang, op0=ALU.mult, op1=ALU.add), r=[k], w=[a])
    S.op(eng, lambda e: e.tensor_scalar(out=ang, in0=ang, scalar1=math.pi, scalar2=-math.pi, op0=ALU.min, op1=ALU.max), w=[a])
    S.op("act", lambda e: e.activation(out=sin_out, in_=ang, func=AF.Sin), r=[a], w=[tag + "sin"])
    S.op("act", lambda e: e.activation(out=hs, in_=ang, func=AF.Sin, scale=0.5), r=[a], w=[h])
    S.op(eng, lambda e: e.tensor_tensor(out=hs, in0=hs, in1=hs, op=ALU.mult), w=[h])
    S.op(eng, lambda e: e.tensor_scalar(out=cos_out, in0=hs, scalar1=-2.0, scalar2=1.0, op0=ALU.mult, op1=ALU.add), r=[h], w=[tag + "cos"])


def rope_tables(nc, S, st, ropef, sinT, cosT, T, tag):
    TS = lambda n, s, d: st.enter_context(nc.sbuf_tensor(n, s, d))
    CH = min(512, T)
    NCH = T // CH
    pi_ = TS(tag + "_pi", [128, CH], I32)
    ang = TS(tag + "_ang", [128, CH], F32)
    kf = TS(tag + "_kf", [128, CH], F32)
    ki = TS(tag + "_ki", [128, CH], I32)
    hs = TS(tag + "_hs", [128, CH], F32)
    s1 = TS(tag + "_s1", [128, CH], F32)
    c1 = TS(tag + "_c1", [128, CH], F32)
    pj = TS(tag + "_pj", [128, NCH], I32)
    ang_b = TS(tag + "_angb", [128, NCH], F32)
    kf_b = TS(tag + "_kfb", [128, NCH], F32)
    ki_b = TS(tag + "_kib", [128, NCH], I32)
    hs_b = TS(tag + "_hsb", [128, NCH], F32)
    s2 = TS(tag + "_s2", [128, NCH], F32)
    c2 = TS(tag + "_c2", [128, NCH], F32)
    ns2 = TS(tag + "_ns2", [128, NCH], F32)
    tmp = [TS(tag + f"_tmp{i}", [128, CH], F32) for i in range(4)]
    S.op("pool", lambda e: e.iota(pi_[:], pattern=[[1, CH]], base=0, channel_multiplier=0), w=[tag + "pi"])
    S.op("pool", lambda e: e.iota(pj[:], pattern=[[CH, NCH]], base=0, channel_multiplier=0), w=[tag + "pj"])
    S.op("dve", lambda e: e.tensor_copy(out=ang[:], in_=pi_[:]), r=[tag + "pi"], w=[tag + "aang"])
    S.op("dve", lambda e: e.tensor_scalar(out=ang[:], in0=ang[:], scalar1=ropef[:, 0:1], scalar2=None, op0=ALU.mult), r=["ropef"], w=[tag + "aang"])
    sincos(S, ang[:], kf[:], ki[:], hs[:], s1[:], c1[:], tag + "a")
    S.op("dve", lambda e: e.tensor_copy(out=ang_b[:], in_=pj[:]), r=[tag + "pj"], w=[tag + "bang"])
    S.op("dve", lambda e: e.tensor_scalar(out=ang_b[:], in0=ang_b[:], scalar1=ropef[:, 0:1], scalar2=None, op0=ALU.mult), r=["ropef"], w=[tag + "bang"])
    sincos(S, ang_b[:], kf_b[:], ki_b[:], hs_b[:], s2[:], c2[:], tag + "b")
    S.op("dve", lambda e: e.tensor_scalar(out=ns2[:], in0=s2[:], scalar1=-1.0, scalar2=None, op0=ALU.mult), r=[tag + "bsin"], w=[tag + "ns2"])
    rd = [tag + "asin", tag + "acos", tag + "bsin", tag + "bcos", tag + "ns2"]
    for c in range(NCH):
        sl = slice(c * CH, (c + 1) * CH)
        ta, tb = tmp[(2 * c) % 4], tmp[(2 * c + 1) % 4]
        na, nb = (tag + "tmp", (2 * c) % 4), (tag + "tmp", (2 * c + 1) % 4)
        S.op("dve", lambda e, c=c, ta=ta: e.tensor_scalar(out=ta[:], in0=c1[:], scalar1=s2[:, c:c + 1], scalar2=None, op0=ALU.mult), r=rd, w=[na])
        S.op("dve", lambda e, c=c, ta=ta, sl=sl: e.scalar_tensor_tensor(out=sinT[:, sl], in0=s1[:], scalar=c2[:, c:c + 1], in1=ta[:],
                                                                        op0=ALU.mult, op1=ALU.add), r=rd + [na], w=[(tag + "sin", c)])
        S.op("dve", lambda e, c=c, tb=tb: e.tensor_scalar(out=tb[:], in0=s1[:], scalar1=ns2[:, c:c + 1], scalar2=None, op0=ALU.mult), r=rd, w=[nb])
        S.op("dve", lambda e, c=c, tb=tb, sl=sl: e.scalar_tensor_tensor(out=cosT[:, sl], in0=c1[:], scalar=c2[:, c:c + 1], in1=tb[:],
                                                                        op0=ALU.mult, op1=ALU.add), r=rd + [nb], w=[(tag + "cos", c)])
    return [(tag + "sin", c) for c in range(NCH)] + [(tag + "cos", c) for c in range(NCH)]


def phase_attn(nc, S, st, fm, tm_v, lqk, subln, ropef_d, rmat_d, cmask_d, oa, lambda_init, T=SEQ):
    TS = lambda n, s, d: st.enter_context(nc.sbuf_tensor(n, s, d))
    PS = lambda n: st.enter_context(nc.psum_tensor(n, [128, 512], F32))
    NQ = T // 512
    NK = T // 128
    ropef = TS("at_ropef", [128, 1], F32)
    rm32 = TS("at_rm32", [128, 128], F32)
    rm = TS("at_rm", [128, 128], BF16)
    cmask = TS("at_cmask", [128, 4, 512], BF16)
    ones = TS("at_ones", [128, 128], BF16)
    sinT = TS("at_sin", [128, T], BF16)
    cosT = TS("at_cos", [128, T], BF16)
    qraw = TS("at_qraw", [128, T], BF16)
    kraw = TS("at_kraw", [128, T], BF16)
    qr = TS("at_qr", [128, T], BF16)
    kr = TS("at_kr", [128, T], BF16)
    sag = TS("at_sag", [128, T], BF16)
    vsb = TS("at_v", [128, NK, 128], BF16)
    lq = TS("at_lq", [128, 256], F32)
    lp = TS("at_lp", [128, 128], F32)
    le = TS("at_le", [128, 2], F32)
    neglam = TS("at_neglam", [128, 1], F32)
    sw = TS("at_sw", [128, 1], F32)
    t1 = [TS(f"at_t1_{i}", [128, 512], BF16) for i in range(2)]
    t2 = [TS(f"at_t2_{i}", [128, 512], BF16) for i in range(2)]
    P = [[TS(f"at_P{i}_{m}", [128, 512], BF16) for m in range(2)] for i in range(4)]
    r0 = [TS(f"at_r0_{i}", [128, 512], F32) for i in range(2)]
    r1 = [TS(f"at_r1_{i}", [128, 512], F32) for i in range(2)]
    o0 = [TS(f"at_o0_{i}", [128, 512], F32) for i in range(2)]
    o1 = [TS(f"at_o1_{i}", [128, 512], F32) for i in range(2)]
    osq = TS("at_osq", [128, 512], BF16)
    nsq = TS("at_nsq", [128, 512], F32)
    ob = [TS(f"at_ob{i}", [128, 512], BF16) for i in range(2)]
    epsc = TS("at_epsc", [128, 1], F32)
    accD = [TS(f"at_accD{i}", [128, 512], F32) for i in range(2)]
    accP = [TS(f"at_accP{i}", [128, 512], F32) for i in range(2)]
    ones32 = TS("at_ones32", [128, 128], F32)
    ps_s = [[PS(f"at_ps_s{i}_{m}") for m in range(2)] for i in range(2)]
    ps_o = [PS(f"at_ps_o{m}") for m in range(2)]
    ps_l = [PS(f"at_ps_l{m}") for m in range(2)]
    ps_n = ps_s[0][0]

    S.op("sp", lambda e: e.dma_start(out=ropef[:], in_=ropef_d[:, :]), w=["ropef"], dma=True)
    S.op("sp", lambda e: e.dma_start(out=rm32[:], in_=rmat_d[:, :]), w=["rm32"], dma=True)
    S.op("sp", lambda e: e.dma_start(out=cmask[:], in_=cmask_d.rearrange("d p q -> p d q")), w=["cmask"], dma=True)
    S.op("sp", lambda e: e.dma_start(out=lq[:], in_=lqk.partition_broadcast(128)), w=["lq"], dma=True)
    S.op("sp", lambda e: e.dma_start(out=sw[:], in_=subln[:, :]), w=["sw"], dma=True)
    S.op("sp", lambda e: e.dma_start(out=qraw[:], in_=fm[320:448, :]), w=["qraw"], dma=True)
    S.op("sp", lambda e: e.dma_start(out=kraw[:], in_=fm[448:576, :]), w=["kraw"], dma=True)
    S.op("sp", lambda e: e.dma_start(out=sag[:], in_=fm[576:704, :]), w=["sag"], dma=True)
    S.op("pool", lambda e: e.dma_start(out=vsb[:], in_=tm_v[:, 64:192].rearrange("(a p) c -> p a c", p=128)), w=["vsb"], dma=True)
    S.op("pool", lambda e: e.memset(ones[:], 1.0), w=["ones"])
    S.op("pool", lambda e: e.memset(epsc[:], EPS), w=["epsc"])
    S.op("pool", lambda e: e.memset(ones32[:], 1.0), w=["ones32"])
    S.op("dve", lambda e: e.tensor_copy(out=rm[:], in_=rm32[:]), r=["rm32"], w=["rm"])
    S.op("dve", lambda e: e.tensor_tensor(out=lp[:], in0=lq[:, 0:128], in1=lq[:, 128:256], op=ALU.mult), r=["lq"], w=["lp"])
    S.op("dve", lambda e: e.tensor_reduce(out=le[:], in_=lp[:].rearrange("p (a c) -> p a c", a=2), axis=AX.X, op=ALU.add),
         r=["lp"], w=["le"])
    S.op("act", lambda e: e.activation(out=le[:], in_=le[:], func=AF.Exp), w=["le"])
    S.op("dve", lambda e: e.tensor_tensor(out=neglam[:], in0=le[:, 1:2], in1=le[:, 0:1], op=ALU.subtract), r=["le"], w=["neglam"])
    S.op("dve", lambda e: e.tensor_scalar(out=neglam[:], in0=neglam[:], scalar1=-lambda_init, scalar2=None, op0=ALU.add), w=["neglam"])
    S.op("dve", lambda e: e.tensor_scalar(out=sw[:], in0=sw[:], scalar1=1.0 - lambda_init, scalar2=None, op0=ALU.mult), w=["sw"])
    tabs = rope_tables(nc, S, st, ropef, sinT, cosT, T, "at_rp")
    def rot_ops(ti):
        sl = slice(ti * 512, (ti + 1) * 512)
        ops = []
        for k_, (src, dst, sn, dn) in enumerate(((qraw, qr, "qraw", "qr"), (kraw, kr, "kraw", "kr"))):
            b = k_
            ops.append(lambda src=src, sl=sl, sn=sn, b=b: S.op("dve", lambda e: e.tensor_tensor(out=t1[b][:], in0=src[:, sl], in1=cosT[:, sl], op=ALU.mult),
                                                             r=[sn] + tabs, w=[("t1", b)]))

            def mm_unit(src=src, sl=sl, sn=sn, b=b):
                S.op("pe", lambda e: e.matmul(ps_l[1][:], rm[:], src[:, sl], start=True, stop=True), r=[sn, "rm"], w=["ps_l1"])
                S.op("dve", lambda e: e.tensor_tensor(out=t2[b][:], in0=ps_l[1][:], in1=sinT[:, sl], op=ALU.mult), r=tabs, w=[("t2", b), "ps_l1"])
            ops.append(mm_unit)
            ops.append(lambda dst=dst, sl=sl, b=b, dn=dn, ti=ti: S.op("pool", lambda e: e.tensor_tensor(out=dst[:, sl], in0=t1[b][:], in1=t2[b][:], op=ALU.add),
                                                                    r=[("t1", b), ("t2", b)], w=[(dn, ti)]))
        return ops
    for f_ in rot_ops(0):
        f_()
    qr_all = [("qr", ti) for ti in range(NQ)]
    kr_all = [("kr", ti) for ti in range(NQ)]
    psn = lambda i, m: f"ps_s{i}{m}"
    dq = []

    def pop_deferred(n, limit=2):
        k = 0
        while dq and k < limit:
            fn, need_odd = dq[0]
            if need_odd and n % 2 == 0:
                break
            dq.pop(0)
            fn()
            k += 1
    for qi in range(NQ):
        qs = slice(qi * 512, (qi + 1) * 512)
        nk = 4 * (qi + 1)
        ab = qi % 2

        def c0_of(n, qi=qi):
            return 128 * max(0, n - 4 * qi)

        def QK(n, qs=qs, qi=qi):
            i = n % 2
            c0 = c0_of(n)
            for m in range(2):
                S.op("pe", lambda e, n=n, i=i, m=m, qs=qs, c0=c0: e.matmul(
                    ps_s[i][m][:, c0:512], kr[64 * m:64 * m + 64, n * 128:(n + 1) * 128],
                    qr[64 * m:64 * m + 64, qs.start + c0:qs.stop], start=True, stop=True),
                     r=[("qr", qi), ("kr", n // 4)], w=[psn(i, m)])

        def EXP(n, qi=qi):
            i = n % 2
            j = n % 4
            c0 = c0_of(n)
            for m in range(2):
                S.op("act", lambda e, i=i, j=j, m=m, c0=c0: e.activation(out=P[j][m][:, c0:512], in_=ps_s[i][m][:, c0:512], func=AF.Exp, scale=0.125),
                     w=[psn(i, m), ("P", j, m)])
                d = n - 4 * qi
                if d >= 0:
                    S.op("dve", lambda e, j=j, m=m, d=d, c0=c0: e.tensor_tensor(out=P[j][m][:, c0:512], in0=P[j][m][:, c0:512],
                                                                             in1=cmask[:, d, c0:512], op=ALU.mult),
                         r=["cmask"], w=[("P", j, m)])

        def PV(n, nk=nk, ab=ab):
            j = n % 4
            c0 = c0_of(n)
            for m in range(2):
                S.op("pe", lambda e, n=n, j=j, m=m, nk=nk, c0=c0: e.matmul(ps_o[m][:, c0:512], vsb[:, n, :], P[j][m][:, c0:512],
                                                                       start=(n == 0), stop=(n == nk - 1)),
                     r=[("P", j, m), "vsb"], w=[f"ps_o{m}"])
            S.op("pe", lambda e, n=n, j=j, nk=nk, c0=c0: e.matmul(ps_l[0][:, c0:512], ones[:], P[j][0][:, c0:512], start=(n == 0), stop=(n == nk - 1)),
                 r=[("P", j, 0), "ones"], w=["ps_l0"])
            eng, acc, an = ("dve", accD[ab], ("accD", ab)) if n % 2 == 0 else ("pool", accP[ab], ("accP", ab))
            if n < 2:
                if c0 > 0:
                    S.op(eng, lambda e, acc=acc, c0=c0: e.memset(acc[:, 0:c0], 0.0), w=[an])
                S.op(eng, lambda e, j=j, acc=acc, c0=c0: e.tensor_copy(out=acc[:, c0:512], in_=P[j][1][:, c0:512]), r=[("P", j, 1)], w=[an])
            else:
                S.op(eng, lambda e, j=j, acc=acc, c0=c0: e.tensor_tensor(out=acc[:, c0:512], in0=acc[:, c0:512], in1=P[j][1][:, c0:512], op=ALU.add),
                     r=[("P", j, 1)], w=[an])
        rq = rot_ops(qi + 1) if qi + 1 < NQ else []
        QK(0)
        for n in range(nk):
            if n >= 1:
                pop_deferred(n)
                for _ in range(2):
                    if rq:
                        rq.pop(0)()
            if n + 1 < nk:
                QK(n + 1)
            EXP(n)
            if n >= 2:
                PV(n - 2)
        PV(nk - 2)
        PV(nk - 1)
        while dq:
            fn, need_odd = dq.pop(0)
            fn()
        while rq:
            rq.pop(0)()
        eb = qi % 2
        S.op("dve", lambda e, eb=eb: e.tensor_copy(out=o0[eb][:], in_=ps_o[0][:]), w=[("o0", eb), "ps_o0"])
        S.op("dve", lambda e, eb=eb: e.tensor_copy(out=o1[eb][:], in_=ps_o[1][:]), w=[("o1", eb), "ps_o1"])
        S.op("dve", lambda e, eb=eb: e.tensor_copy(out=r0[eb][:], in_=ps_l[0][:]), w=[("r0", eb), "ps_l0"])
        D = lambda fn, odd=False: dq.append((fn, odd))
        def l1_unit(ab=ab, eb=eb):
            S.op("pe", lambda e: e.matmul(ps_l[1][:], ones32[:], accD[ab][:], start=True, stop=False), r=[("accD", ab), "ones32"], w=["ps_l1"])
            S.op("pe", lambda e: e.matmul(ps_l[1][:], ones32[:], accP[ab][:], start=False, stop=True), r=[("accP", ab), "ones32"], w=["ps_l1"])
            S.op("dve", lambda e: e.tensor_copy(out=r1[eb][:], in_=ps_l[1][:]), w=[("r1", eb), "ps_l1"])
        D(l1_unit)
        for rr, rn in ((r0, "r0"), (r1, "r1")):
            D(lambda eb=eb, rr=rr, rn=rn: S.op("act", lambda e: e.activation(out=rr[eb][:], in_=rr[eb][:], func=AF.Ln), w=[(rn, eb)]))
            D(lambda eb=eb, rr=rr, rn=rn: S.op("act", lambda e: e.activation(out=rr[eb][:], in_=rr[eb][:], func=AF.Exp, scale=-1.0), w=[(rn, eb)]))
        D(lambda eb=eb: S.op("dve", lambda e: e.tensor_tensor(out=o0[eb][:], in0=o0[eb][:], in1=r0[eb][:], op=ALU.mult), r=[("r0", eb)], w=[("o0", eb)]))
        D(lambda eb=eb: S.op("dve", lambda e: e.tensor_tensor(out=o1[eb][:], in0=o1[eb][:], in1=r1[eb][:], op=ALU.mult), r=[("r1", eb)], w=[("o1", eb)]))
        D(lambda eb=eb: S.op("dve", lambda e: e.scalar_tensor_tensor(out=o0[eb][:], in0=o1[eb][:], scalar=neglam[:, 0:1], in1=o0[eb][:],
                                                                     op0=ALU.mult, op1=ALU.add), r=[("o1", eb), "neglam"], w=[("o0", eb)]))
        D(lambda eb=eb: S.op("act", lambda e: e.activation(out=osq[:], in_=o0[eb][:], func=AF.Square), r=[("o0", eb)], w=["osq"]))
        def norm_unit():
            S.op("pe", lambda e: e.matmul(ps_n[:], ones[:], osq[:], start=True, stop=True), r=["osq", "ones"], w=[psn(0, 0)])
            S.op("act", lambda e: e.activation(out=nsq[:], in_=ps_n[:], func=AF.Ln, scale=1.0 / 128.0, bias=epsc[:, 0:1]),
                 r=["epsc"], w=["nsq", psn(0, 0)])
        D(norm_unit, True)
        D(lambda: S.op("act", lambda e: e.activation(out=nsq[:], in_=nsq[:], func=AF.Exp, scale=-0.5), w=["nsq"]))
        D(lambda eb=eb: S.op("dve", lambda e: e.scalar_tensor_tensor(out=o0[eb][:], in0=o0[eb][:], scalar=sw[:, 0:1], in1=nsq[:],
                                                                     op0=ALU.mult, op1=ALU.mult), r=["nsq", "sw"], w=[("o0", eb)]))
        D(lambda eb=eb, qs=qs: S.op("pool", lambda e: e.tensor_tensor(out=ob[eb][:], in0=o0[eb][:], in1=sag[:, qs], op=ALU.mult),
                                    r=[("o0", eb), "sag"], w=[("ob", eb)]))
        D(lambda eb=eb, qs=qs: S.op("sp", lambda e: e.dma_start(out=oa[:, qs], in_=ob[eb][:]), r=[("ob", eb)], dma=True))
    while dq:
        fn, need_odd = dq.pop(0)
        fn()


def attn_consts():
    ropef = np.zeros((128, 1), np.float32)
    inv = (ROPE_THETA ** (-np.arange(0, 16, 2, dtype=np.float32) / 16.0)).astype(np.float32)
    rmat = np.zeros((128, 128), np.float32)
    for base in (0, 64):
        for i in range(8):
            ropef[base + i, 0] = -inv[i]
            ropef[base + 8 + i, 0] = inv[i]
            rmat[base + 8 + i, base + i] = 1.0
            rmat[base + i, base + 8 + i] = 1.0
    k = np.arange(128)[:, None]
    q = np.arange(512)[None, :]
    cmask = np.stack([(128 * d + k <= q) for d in range(4)]).astype(ml_dtypes.bfloat16)
    return ropef, rmat, cmask


def build_attn(lambda_init, T=SEQ):
    nc = bass.Bass("TRN2", target_bir_lowering=False)
    fm = nc.dram_tensor("fm", [NFM, T], BF16, kind="ExternalInput").ap()
    tm_v = nc.dram_tensor("tm_v", [T, 192], BF16, kind="ExternalInput").ap()
    lqk = nc.dram_tensor("lqk", [1, 256], F32, kind="ExternalInput").ap()
    subln = nc.dram_tensor("subln", [128, 1], F32, kind="ExternalInput").ap()
    ropef = nc.dram_tensor("ropef", [128, 1], F32, kind="ExternalInput").ap()
    rmat = nc.dram_tensor("rmat", [128, 128], F32, kind="ExternalInput").ap()
    cmask = nc.dram_tensor("cmask", [4, 128, 512], BF16, kind="ExternalInput").ap()
    oa = nc.dram_tensor("oa", [128, T], BF16, kind="ExternalOutput").ap()
    with contextlib.ExitStack() as st:
        S = Sched(nc)
        phase_attn(nc, S, st, fm, tm_v, lqk, subln, ropef, rmat, cmask, oa, lambda_init, T)
        S.emit()
    return nc


def hgrn_consts():
    s = np.arange(128)[:, None]
    t = np.arange(128)[None, :]
    same = (s // 16) == (t // 16)
    m_incl = (same & (s <= t)).astype(np.float32)
    m_rev = (same & (s > t)).astype(np.float32)
    m_tot8 = ((s // 16) == np.arange(8)[None, :]).astype(np.float32)
    mcat = np.concatenate([m_incl, m_tot8], axis=1)
    return mcat, m_rev


def phase_hgrn(nc, S, st, fm, tm_sf, tm_v, lbl_bc_d, lbl_col_d, gw_d, mcat_d, mrev_d, oh, lb_coef, T=SEQ):
    TS = lambda n, s, d: st.enter_context(nc.sbuf_tensor(n, s, d))
    PS = lambda n: st.enter_context(nc.psum_tensor(n, [128, 512], F32))
    NT = T // 128
    sq = TS("hg_sq", [64, T], BF16)
    snf = TS("hg_snf", [64, T], BF16)
    shg = TS("hg_shg", [64, T], BF16)
    sf = TS("hg_sf", [128, NT, 64], F32)
    omf = TS("hg_omf", [128, NT, 64], F32)
    logf = TS("hg_logf", [128, NT, 64], F32)
    vall = TS("hg_v", [128, NT, 64], BF16)
    lbl = TS("hg_lbl", [128, 128], F32)
    lb_bc = TS("hg_lb_bc", [128, 64], F32)
    oml_bc = TS("hg_oml_bc", [128, 64], F32)
    lblc = TS("hg_lblc", [64, 2], F32)
    oml_col = TS("hg_oml_col", [64, 1], F32)
    gw = TS("hg_gw", [64, 1], F32)
    mcat = TS("hg_mcat", [128, 136], F32)
    mrev = TS("hg_mrev", [128, 128], F32)
    mincl_bf = TS("hg_mincl", [128, 128], BF16)
    mtot_bf = TS("hg_mtot", [128, 8], BF16)
    ones64 = TS("hg_ones", [64, 64], BF16)
    epsc = TS("hg_epsc", [64, 1], F32)
    eq = TS("hg_eq", [64, 128], F32)
    ekn = TS("hg_ekn", [64, 128], F32)
    dec = [TS(f"hg_dec{i}", [64, 8], F32) for i in range(3)]
    ehat = TS("hg_ehat", [128, 64], F32)
    qt = [TS(f"hg_qt{i}", [64, 128], BF16) for i in range(3)]
    kt = TS("hg_kt", [64, 128], BF16)
    khat = TS("hg_khat", [128, 64], BF16)
    vblk = TS("hg_vblk", [128, 8, 64], BF16)
    scm = [TS(f"hg_scm{i}", [128, 128], BF16) for i in range(3)]
    Sall = [TS(f"hg_S{i}", [64, 9, 64], F32) for i in range(2)]
    Sbf = [TS(f"hg_Sbf{i}", [64, 8, 64], BF16) for i in range(2)]
    osq = TS("hg_osq", [64, 512], BF16)
    o32 = TS("hg_o32", [64, 512], F32)
    nsq = TS("hg_nsq", [64, 512], F32)
    ohb = [TS(f"hg_ohb{i}", [64, 512], BF16) for i in range(2)]
    ps_c = PS("hg_ps_c")
    ps_r = PS("hg_ps_r")
    ps_sc = PS("hg_ps_sc")
    ps_u = [PS(f"hg_ps_u{i}") for i in range(3)]
    ps_oh = [PS(f"hg_ps_oh{i}") for i in range(2)]

    S.op("sp", lambda e: e.dma_start(out=sq[:], in_=fm[0:64, :]), w=["sq"], dma=True)
    S.op("sp", lambda e: e.dma_start(out=snf[:], in_=fm[64:128, :]), w=["snf"], dma=True)
    S.op("sp", lambda e: e.dma_start(out=shg[:], in_=fm[128:192, :]), w=["shg"], dma=True)
    S.op("sp", lambda e: e.dma_start(out=sf[:], in_=tm_sf.rearrange("(a p) c -> p a c", p=128)), w=["sf"], dma=True)
    S.op("pool", lambda e: e.dma_start(out=vall[:], in_=tm_v[:, 0:64].rearrange("(a p) c -> p a c", p=128)), w=["vall"], dma=True)
    S.op("sp", lambda e: e.dma_start(out=lbl[:], in_=lbl_bc_d.partition_broadcast(128)), w=["lbl"], dma=True)
    S.op("sp", lambda e: e.dma_start(out=lblc[:], in_=lbl_col_d[:, :]), w=["lblc"], dma=True)
    S.op("sp", lambda e: e.dma_start(out=gw[:], in_=gw_d[:, :]), w=["gw"], dma=True)
    S.op("sp", lambda e: e.dma_start(out=mcat[:], in_=mcat_d[:, :]), w=["mcat"], dma=True)
    S.op("sp", lambda e: e.dma_start(out=mrev[:], in_=mrev_d[:, :]), w=["mrev"], dma=True)
    S.op("pool", lambda e: e.memset(ones64[:], 1.0), w=["ones64"])
    S.op("pool", lambda e: e.memset(epsc[:], EPS), w=["epsc"])
    S.op("pool", lambda e: e.memset(Sall[0][:, 0, :], 0.0), w=[("S", 0)])
    S.op("dve", lambda e: e.tensor_copy(out=mincl_bf[:], in_=mcat[:, 0:128]), r=["mcat"], w=["mincl_bf"])
    S.op("dve", lambda e: e.tensor_copy(out=mtot_bf[:], in_=mcat[:, 128:136]), r=["mcat"], w=["mtot_bf"])
    S.op("dve", lambda e: e.tensor_tensor(out=lb_bc[:], in0=lbl[:, 64:128], in1=lbl[:, 0:64], op=ALU.subtract), r=["lbl"], w=["lb_bc"])
    S.op("act", lambda e: e.activation(out=lb_bc[:], in_=lb_bc[:], func=AF.Sigmoid), w=["lb_bc"])
    S.op("dve", lambda e: e.tensor_scalar(out=lb_bc[:], in0=lb_bc[:], scalar1=float(lb_coef), scalar2=None, op0=ALU.mult), w=["lb_bc"])
    S.op("dve", lambda e: e.tensor_scalar(out=oml_bc[:], in0=lb_bc[:], scalar1=-1.0, scalar2=1.0, op0=ALU.mult, op1=ALU.add),
         r=["lb_bc"], w=["oml_bc"])
    S.op("dve", lambda e: e.tensor_tensor(out=oml_col[:], in0=lblc[:, 1:2], in1=lblc[:, 0:1], op=ALU.subtract), r=["lblc"], w=["oml_col"])
    S.op("act", lambda e: e.activation(out=oml_col[:], in_=oml_col[:], func=AF.Sigmoid), w=["oml_col"])
    S.op("dve", lambda e: e.tensor_scalar(out=oml_col[:], in0=oml_col[:], scalar1=-float(lb_coef), scalar2=1.0, op0=ALU.mult, op1=ALU.add),
         w=["oml_col"])
    for g in range(NT // 8 if NT >= 8 else 1):
        nt = min(8, NT)
        sl = slice(g * 8, g * 8 + nt)
        S.op("dve", lambda e, sl=sl, nt=nt: e.tensor_tensor(out=sf[:, sl, :], in0=sf[:, sl, :],
                                                        in1=oml_bc[:].unsqueeze(1).broadcast_to([128, nt, 64]), op=ALU.mult),
             r=["oml_bc"], w=["sf"])
        S.op("dve", lambda e, sl=sl, nt=nt: e.tensor_tensor(out=sf[:, sl, :], in0=sf[:, sl, :],
                                                        in1=lb_bc[:].unsqueeze(1).broadcast_to([128, nt, 64]), op=ALU.add),
             r=["lb_bc"], w=["sf"])
        S.op("act", lambda e, sl=sl: e.activation(out=logf[:, sl, :], in_=sf[:, sl, :], func=AF.Ln), r=["sf"], w=[("logf", g)])
        S.op("pool", lambda e, sl=sl: e.tensor_scalar(out=omf[:, sl, :], in0=sf[:, sl, :], scalar1=-1.0, scalar2=1.0,
                                                     op0=ALU.mult, op1=ALU.add), r=["sf"], w=[("omf", g)])
    def stageA1(i):
        g = i // 8
        b = i % 3
        S.op("pe", lambda e, i=i: e.matmul(ps_c[0:64, 0:136], logf[:, i, :], mcat[:], start=True, stop=True),
             r=[("logf", g), "mcat"], w=["ps_c"])
        S.op("pe", lambda e, i=i: e.matmul(ps_r[:, 0:64], mrev[:], logf[:, i, :], start=True, stop=True),
             r=[("logf", g), "mrev"], w=["ps_r"])
        S.op("act", lambda e: e.activation(out=eq[:], in_=ps_c[0:64, 0:128], func=AF.Exp), w=["eq", "ps_c"])
        S.op("act", lambda e: e.activation(out=ekn[:], in_=ps_c[0:64, 0:128], func=AF.Exp, scale=-1.0), w=["ekn", "ps_c"])
        S.op("act", lambda e, b=b: e.activation(out=dec[b][:], in_=ps_c[0:64, 128:136], func=AF.Exp), w=[("dec", b), "ps_c"])
        S.op("act", lambda e: e.activation(out=ehat[:], in_=ps_r[:, 0:64], func=AF.Exp), w=["ehat", "ps_r"])

    def stageA2(i):
        g = i // 8
        b = i % 3
        ts = slice(i * 128, (i + 1) * 128)
        S.op("pool", lambda e, b=b, ts=ts: e.tensor_tensor(out=qt[b][:], in0=sq[:, ts], in1=eq[:], op=ALU.mult),
             r=["sq", "eq"], w=[("qt", b)])
        S.op("dve", lambda e, ts=ts: e.scalar_tensor_tensor(out=kt[:], in0=snf[:, ts], scalar=oml_col[:, 0:1], in1=ekn[:],
                                                           op0=ALU.mult, op1=ALU.mult), r=["snf", "ekn", "oml_col"], w=["kt"])
        S.op("pool", lambda e, i=i: e.tensor_tensor(out=khat[:], in0=omf[:, i, :], in1=ehat[:], op=ALU.mult),
             r=[("omf", g), "ehat"], w=["khat"])
        S.op("pool", lambda e, i=i: e.tensor_tensor(out=vblk[:], in0=vall[:, i, :].unsqueeze(1).broadcast_to([128, 8, 64]),
                                                   in1=mtot_bf[:].unsqueeze(2).broadcast_to([128, 8, 64]), op=ALU.mult),
             r=["vall", "mtot_bf"], w=["vblk"])
        S.op("pe", lambda e, b=b: e.matmul(ps_sc[:, 0:128], kt[:], qt[b][:], start=True, stop=True), r=["kt", ("qt", b)], w=["ps_sc"])
        S.op("dve", lambda e, b=b: e.tensor_tensor(out=scm[b][:], in0=ps_sc[:, 0:128], in1=mincl_bf[:], op=ALU.mult),
             r=["mincl_bf"], w=[("scm", b), "ps_sc"])
        S.op("pe", lambda e, b=b: e.matmul(ps_u[b][0:64, :], khat[:], vblk[:].rearrange("p a c -> p (a c)"), start=True, stop=True),
             r=["khat", "vblk"], w=[("ps_u", b)])

    def stageB(i):
        b = i % 3
        sb = i % 2
        if i > 0:
            S.op("dve", lambda e, sb=sb: e.tensor_copy(out=Sall[sb][:, 0, :], in_=Sall[1 - sb][:, 8, :]), r=[("S", 1 - sb)], w=[("S", sb)])
        for n in range(8):
            S.op("dve", lambda e, b=b, sb=sb, n=n: e.scalar_tensor_tensor(
                out=Sall[sb][:, n + 1, :], in0=Sall[sb][:, n, :], scalar=dec[b][:, n:n + 1], in1=ps_u[b][0:64, n * 64:(n + 1) * 64],
                op0=ALU.mult, op1=ALU.add), r=[("dec", b)], w=[("S", sb), ("ps_u", b)])
        S.op("act", lambda e, sb=sb: e.activation(out=Sbf[sb][:], in_=Sall[sb][:, 0:8, :], func=AF.Copy), r=[("S", sb)], w=[("Sbf", sb)])

    def stageC(i):
        b = i % 3
        sb = i % 2
        ob = (i // 4) % 2
        c0 = (i % 4) * 128
        S.op("pe", lambda e, i=i, ob=ob, c0=c0, b=b: e.matmul(ps_oh[ob][0:64, c0:c0 + 128], vall[:, i, :], scm[b][:], start=True, stop=False),
             r=["vall", ("scm", b)], w=[("ps_oh", ob)])
        for n in range(8):
            S.op("pe", lambda e, b=b, sb=sb, ob=ob, c0=c0, n=n: e.matmul(
                ps_oh[ob][0:64, c0 + 16 * n:c0 + 16 * n + 16], Sbf[sb][:, n, :], qt[b][:, 16 * n:16 * n + 16],
                start=False, stop=(n == 7)), r=[("Sbf", sb), ("qt", b)], w=[("ps_oh", ob)])
        if i % 4 == 3 or i == NT - 1:
            qs = slice((i // 4) * 512, (i // 4) * 512 + 512)
            S.op("act", lambda e, ob=ob: e.activation(out=osq[:], in_=ps_oh[ob][0:64, :], func=AF.Square), w=["osq", ("ps_oh", ob)])
            S.op("act", lambda e, ob=ob: e.activation(out=o32[:], in_=ps_oh[ob][0:64, :], func=AF.Copy), w=["o32", ("ps_oh", ob)])
            S.op("pe", lambda e: e.matmul(ps_sc[0:64, :], ones64[:], osq[:], start=True, stop=True), r=["osq", "ones64"], w=["ps_sc"])
            S.op("act", lambda e: e.activation(out=nsq[:], in_=ps_sc[0:64, :], func=AF.Ln, scale=1.0 / 64.0, bias=epsc[:, 0:1]), r=["epsc"], w=["nsq", "ps_sc"])
            S.op("act", lambda e: e.activation(out=nsq[:], in_=nsq[:], func=AF.Exp, scale=-0.5), w=["nsq"])
            S.op("dve", lambda e: e.scalar_tensor_tensor(out=o32[:], in0=o32[:], scalar=gw[:, 0:1], in1=nsq[:], op0=ALU.mult, op1=ALU.mult),
                 r=["nsq", "gw"], w=["o32"])
            S.op("pool", lambda e, ob=ob, qs=qs: e.tensor_tensor(out=ohb[ob][:], in0=o32[:], in1=shg[:, qs], op=ALU.mult),
                 r=["o32", "shg"], w=[("ohb", ob)])
            S.op("sp", lambda e, ob=ob, qs=qs: e.dma_start(out=oh[:, qs], in_=ohb[ob][:]), r=[("ohb", ob)], dma=True)
    for i0_ in range(min(2, NT)):
        stageA1(i0_)
        stageA2(i0_)
    for i in range(NT):
        if i + 2 < NT:
            stageA1(i + 2)
        stageB(i)
        if i + 2 < NT:
            stageA2(i + 2)
        stageC(i)


def build_hgrn(lb_coef, T=SEQ):
    nc = bass.Bass("TRN2", target_bir_lowering=False)
    fm = nc.dram_tensor("fm", [NFM, T], BF16, kind="ExternalInput").ap()
    tm_sf = nc.dram_tensor("tm_sf", [T, 64], F32, kind="ExternalInput").ap()
    tm_v = nc.dram_tensor("tm_v", [T, 192], BF16, kind="ExternalInput").ap()
    lbl_bc = nc.dram_tensor("lbl_bc", [1, 128], F32, kind="ExternalInput").ap()
    lbl_col = nc.dram_tensor("lbl_col", [64, 2], F32, kind="ExternalInput").ap()
    gw = nc.dram_tensor("gw", [64, 1], F32, kind="ExternalInput").ap()
    mcat = nc.dram_tensor("mcat", [128, 136], F32, kind="ExternalInput").ap()
    mrev = nc.dram_tensor("mrev", [128, 128], F32, kind="ExternalInput").ap()
    oh = nc.dram_tensor("oh", [64, T], BF16, kind="ExternalOutput").ap()
    with contextlib.ExitStack() as st:
        S = Sched(nc)
        phase_hgrn(nc, S, st, fm, tm_sf, tm_v, lbl_bc, lbl_col, gw, mcat, mrev, oh, lb_coef, T)
        S.emit()
    return nc


def cmul(S, eng, o_re, o_im, a_re, a_im, b_re, b_im, t0, t1, rd, wr, conj_a=False):
    sg = -1.0 if conj_a else 1.0
    S.op(eng, lambda e: e.tensor_tensor(out=t0, in0=a_im, in1=b_im, op=ALU.mult), r=rd, w=[wr + "t0"])
    S.op(eng, lambda e: e.tensor_tensor(out=t1, in0=a_re, in1=b_re, op=ALU.mult), r=rd, w=[wr + "t1"])
    S.op("dve", lambda e: e.scalar_tensor_tensor(out=o_re, in0=t0, scalar=-sg, in1=t1, op0=ALU.mult, op1=ALU.add),
         r=[wr + "t0", wr + "t1"], w=[wr + "re"])
    S.op(eng, lambda e: e.tensor_tensor(out=t0, in0=a_im, in1=b_re, op=ALU.mult), r=rd + [wr + "re"], w=[wr + "t0"])
    S.op(eng, lambda e: e.tensor_tensor(out=t1, in0=a_re, in1=b_im, op=ALU.mult), r=rd + [wr + "re"], w=[wr + "t1"])
    S.op("dve", lambda e: e.scalar_tensor_tensor(out=o_im, in0=t0, scalar=sg, in1=t1, op0=ALU.mult, op1=ALU.add),
         r=[wr + "t0", wr + "t1"], w=[wr + "im"])


def s5_consts():
    negsig = np.repeat(-np.arange(16, dtype=np.float32), 64)[None, :]
    kidx = np.arange(32, dtype=np.float32)[None, :]
    midx = np.arange(1, 513, dtype=np.float32)[None, :]
    rowmask = (np.arange(64)[:, None] // 16 == np.arange(4)[None, :]).astype(np.float32)
    return negsig, kidx, midx, rowmask


def s5_params(z, l, j):
    gs = [4 * j + gl for gl in range(4)]
    f = np.float32
    pA_are = np.concatenate([np.repeat(z["s5_a_re"][l][g][None, :], 16, 0) for g in gs]).astype(f)
    pA_aim = np.concatenate([np.repeat(z["s5_a_im"][l][g][None, :], 16, 0) for g in gs]).astype(f)
    pA_ldt = np.concatenate([np.full((16, 1), z["s5_log_dt"][l][g]) for g in gs]).astype(f)
    pA_bre = np.concatenate([z["s5_b_re"][l][g].T for g in gs]).astype(f)
    pA_bim = np.concatenate([z["s5_b_im"][l][g].T for g in gs]).astype(f)
    pB = np.zeros((2, 128, 3), f)
    pB_cre = np.zeros((2, 128, 64), f)
    pB_cim = np.zeros((2, 128, 64), f)
    for q in range(2):
        for h in range(2):
            gl = 2 * q + h
            g = gs[gl]
            rows = slice(64 * h, 64 * h + 64)
            pB[q, rows, 0] = z["s5_a_re"][l][g]
            pB[q, rows, 1] = z["s5_a_im"][l][g]
            pB[q, rows, 2] = z["s5_log_dt"][l][g]
            pB_cre[q, rows, 16 * gl:16 * gl + 16] = z["s5_c_re"][l][g].T
            pB_cim[q, rows, 16 * gl:16 * gl + 16] = z["s5_c_im"][l][g].T
    dcol = z["s5_d"][l][64 * j:64 * j + 64][:, None].astype(f)
    pA = np.concatenate([pA_are, pA_aim, pA_bre, pA_bim, pA_ldt], axis=1)
    return {"s5p_pA": np.ascontiguousarray(pA), "s5p_pB": pB, "s5p_cre": pB_cre, "s5p_cim": pB_cim, "s5p_d": dcol}


def phase_s5(nc, S, st, fm, pA_d, pB_d, cre_d, cim_d, dcol_d, negsig_d, kidx_d, midx_d, rowmask_d, yg, T=SEQ):
    TS = lambda n, s, d: st.enter_context(nc.sbuf_tensor(n, s, d))
    PS = lambda n: st.enter_context(nc.psum_tensor(n, [128, 512], F32))
    NB = T // 16
    su = TS("s5_su", [64, T], BF16)
    outsb = TS("s5_out", [64, T], BF16)
    pA = TS("s5_pA", [64, 257], F32)
    dcol = TS("s5_dcol", [64, 1], F32)
    rowmask = TS("s5_rowmask", [64, 4], F32)
    negsig = TS("s5_negsig", [64, 1024], F32)
    SCR = TS("s5_scr", [128, 8192], F32)
    tA = [SCR[0:64, 1024 * i:1024 * (i + 1)] for i in range(8)]
    tAi = TS("s5_tAi", [64, 1024], I32)
    sA = [TS(f"s5_sA{i}", [64, 64], F32) for i in range(10)]
    sAi = TS("s5_sAi", [64, 64], I32)
    dtA = TS("s5_dtA", [64, 1], F32)
    W1tab = [[TS(f"s5_W1tab{q}{ri}", [64, 16, 128], BF16) for ri in range(2)] for q in range(2)]
    pB = [TS(f"s5_pB{q}", [128, 3], F32) for q in range(2)]
    crep = [TS(f"s5_crep{q}", [128, 64], F32) for q in range(2)]
    cimp = [TS(f"s5_cimp{q}", [128, 64], F32) for q in range(2)]
    kidx = TS("s5_kidx", [128, 32], F32)
    midx = TS("s5_midx", [128, 512], F32)
    tB = [TS(f"s5_tB{i}", [128, 32], F32) for i in range(7)]
    tBi = TS("s5_tBi", [128, 32], I32)
    cB = [TS(f"s5_cB{i}", [128, 1], F32) for i in range(6)]
    cBi = TS("s5_cBi", [128, 1], I32)
    gt = [SCR[:, 2048 * i:2048 * (i + 1)].rearrange("p (k c) -> p k c", k=32) for i in range(2)]
    Gpad = [[TS(f"s5_G{q}{ri}", [128, 32, 64], BF16) for ri in range(2)] for q in range(2)]
    Tc = [TS(f"s5_Tc{q}", [128, 512], F32) for q in range(2)]
    Tsn = [TS(f"s5_Ts{q}", [128, 512], F32) for q in range(2)]
    rho = [TS(f"s5_rho{q}", [128, 1], F32) for q in range(2)]
    l2 = [SCR[:, 4096 + 512 * i:4096 + 512 * (i + 1)] for i in range(6)]
    l2i = TS("s5_l2i", [128, 512], I32)
    roll = [TS(f"s5_roll{i}", [128, 512], F32) for i in range(2)]
    W15 = [[TS(f"s5_W15{q}{ri}", [128, 512], F32) for ri in range(2)] for q in range(2)]
    W1bf = [[TS(f"s5_W1bf{q}{ri}", [128, 16, 512], BF16) for ri in range(2)] for q in range(2)]
    Xbf = [[TS(f"s5_Xbf{q}{ri}", [128, 512], BF16) for ri in range(2)] for q in range(2)]
    ytmp = [TS(f"s5_ytmp{i}", [64, 512], F32) for i in range(2)]
    ps_z = [PS(f"s5_ps_z{i}") for i in range(2)]
    ps_y = [PS(f"s5_ps_y{i}") for i in range(2)]

    ld = lambda eng, dst, src, name: S.op(eng, lambda e: e.dma_start(out=dst, in_=src), w=[name], dma=True)
    ld("sp", su[:], fm[256:320, :], "su")
    ld("sp", pA[:], pA_d[:, :], "pA")
    ld("sp", dcol[:], dcol_d[:, :], "dcol")
    ld("sp", rowmask[:], rowmask_d[:, :], "rowmask")
    ld("sp", negsig[:], negsig_d.partition_broadcast(64), "negsig")
    ld("sp", kidx[:], kidx_d.partition_broadcast(128), "kidx")
    ld("sp", midx[:], midx_d.partition_broadcast(128), "midx")
    for q in range(2):
        ld("sp", pB[q][:], pB_d[q], ("pB", q))
        ld("sp", crep[q][:], cre_d[q], ("crep", q))
        ld("sp", cimp[q][:], cim_d[q], ("cimp", q))
    are, aim, bre, bim, ldt = pA[:, 0:64], pA[:, 64:128], pA[:, 128:192], pA[:, 192:256], pA[:, 256:257]
    lam, th, abr, abi, mg, zr, zi, den, u0, u1 = [t[:] for t in sA]
    S.op("act", lambda e: e.activation(out=dtA[:], in_=ldt, func=AF.Exp), r=["pA"], w=["dtA"])
    S.op("dve", lambda e: e.tensor_scalar(out=lam, in0=are, scalar1=dtA[:, 0:1], scalar2=None, op0=ALU.mult), r=["pA", "dtA"], w=["lamA"])
    S.op("dve", lambda e: e.tensor_scalar(out=th, in0=aim, scalar1=dtA[:, 0:1], scalar2=None, op0=ALU.mult), r=["pA", "dtA"], w=["thA"])
    S.op("dve", lambda e: e.tensor_copy(out=u0, in_=th), r=["thA"], w=["sAang"])
    sincos(S, u0, u1, sAi[:], den, abi, abr, "sA")
    S.op("act", lambda e: e.activation(out=mg, in_=lam, func=AF.Exp), r=["lamA"], w=["mgA"])
    S.op("dve", lambda e: e.tensor_tensor(out=abr, in0=abr, in1=mg, op=ALU.mult), r=["mgA", "sAcos"], w=["abr"])
    S.op("dve", lambda e: e.tensor_tensor(out=abi, in0=abi, in1=mg, op=ALU.mult), r=["mgA", "sAsin"], w=["abi"])
    S.op("dve", lambda e: e.tensor_scalar(out=abr, in0=abr, scalar1=-1.0, scalar2=None, op0=ALU.add), w=["abr"])
    S.op("dve", lambda e: e.tensor_tensor(out=den, in0=are, in1=are, op=ALU.mult), r=["pA", "sAcos", "sAsin"], w=["den"])
    S.op("dve", lambda e: e.tensor_tensor(out=u0, in0=aim, in1=aim, op=ALU.mult), r=["pA", "sAsin"], w=["u0"])
    S.op("dve", lambda e: e.tensor_tensor(out=den, in0=den, in1=u0, op=ALU.add), r=["u0"], w=["den"])
    S.op("dve", lambda e: e.reciprocal(out=den, in_=den), w=["den"])
    S.op("dve", lambda e: e.tensor_tensor(out=u0, in0=abr, in1=are, op=ALU.mult), r=["abr"], w=["u0"])
    S.op("dve", lambda e: e.tensor_tensor(out=u1, in0=abi, in1=aim, op=ALU.mult), r=["abi"], w=["u1"])
    S.op("dve", lambda e: e.tensor_tensor(out=zr, in0=u0, in1=u1, op=ALU.add), r=["u0", "u1"], w=["zr"])
    S.op("dve", lambda e: e.tensor_tensor(out=zr, in0=zr, in1=den, op=ALU.mult), r=["den"], w=["zr"])
    S.op("dve", lambda e: e.tensor_tensor(out=u0, in0=abi, in1=are, op=ALU.mult), r=["abi", "zr"], w=["u0"])
    S.op("dve", lambda e: e.tensor_tensor(out=u1, in0=abr, in1=aim, op=ALU.mult), r=["abr", "zr"], w=["u1"])
    S.op("dve", lambda e: e.tensor_tensor(out=zi, in0=u0, in1=u1, op=ALU.subtract), r=["u0", "u1"], w=["zi"])
    S.op("dve", lambda e: e.tensor_tensor(out=zi, in0=zi, in1=den, op=ALU.mult), r=["den"], w=["zi"])
    A3 = lambda t: t[:].rearrange("p (s m) -> p s m", s=16)
    bc3 = lambda ap: ap.unsqueeze(1).broadcast_to([64, 16, 64])
    ang3, kf3, hs3, sn3, cs3, mg3, w_r, w_i = tA
    S.op("dve", lambda e: e.tensor_tensor(out=A3(ang3), in0=A3(negsig), in1=bc3(th), op=ALU.mult), r=["negsig", "thA"], w=["tAang"])
    sincos(S, ang3[:], kf3[:], tAi[:], hs3[:], sn3[:], cs3[:], "tA")
    S.op("dve", lambda e: e.tensor_tensor(out=A3(mg3), in0=A3(negsig), in1=bc3(lam), op=ALU.mult), r=["negsig", "lamA"], w=["mg3"])
    S.op("act", lambda e: e.activation(out=mg3[:], in_=mg3[:], func=AF.Exp), w=["mg3"])
    S.op("dve", lambda e: e.tensor_tensor(out=cs3[:], in0=cs3[:], in1=mg3[:], op=ALU.mult), r=["mg3"], w=["tAcos"])
    S.op("dve", lambda e: e.tensor_tensor(out=sn3[:], in0=sn3[:], in1=mg3[:], op=ALU.mult), r=["mg3"], w=["tAsin"])
    cmul(S, "dve", A3(w_r), A3(w_i), A3(cs3), A3(sn3), bc3(zr), bc3(zi), A3(ang3), A3(kf3),
         ["tAcos", "tAsin", "zr", "zi", "tAang", "tAkf"], "wz")
    cmul(S, "dve", A3(cs3), A3(sn3), A3(w_r), A3(w_i), bc3(bre), bc3(bim), A3(ang3), A3(kf3),
         ["wzre", "wzim", "pA", "tAcos", "tAsin"], "Bs")
    for q in range(2):
        for ri, src in ((0, cs3), (1, sn3)):
            for h in range(2):
                gl = 2 * q + h
                S.op("dve", lambda e, q=q, ri=ri, h=h, gl=gl, src=src: e.tensor_scalar(
                    out=W1tab[q][ri][:, :, 64 * h:64 * h + 64], in0=A3(src), scalar1=rowmask[:, gl:gl + 1], scalar2=None, op0=ALU.mult),
                    r=["Bsre", "Bsim", "rowmask"], w=[("W1tab", q, ri, h)])
    S.barrier()
    bq = []

    class _Defer:
        def op(self, *a, **k):
            bq.append((a, k))
    SB = _Defer()
    for q in range(2):
        lamB, thB, dtB, phi, th15, junk = [t[:] for t in cB]
        angk, kfk, hsk, snk, csk, mgk, nsk = [t[:] for t in tB]
        pq = [("pB", q)]
        tg = f"B{q}"
        SB.op("act", lambda e, q=q: e.activation(out=dtB, in_=pB[q][:, 2:3], func=AF.Exp), r=pq, w=[tg + "dt"])
        SB.op("dve", lambda e, q=q: e.tensor_tensor(out=lamB, in0=pB[q][:, 0:1], in1=dtB, op=ALU.mult), r=pq + [tg + "dt"], w=[tg + "lam"])
        SB.op("dve", lambda e, q=q: e.tensor_tensor(out=thB, in0=pB[q][:, 1:2], in1=dtB, op=ALU.mult), r=pq + [tg + "dt"], w=[tg + "th"])
        SB.op("dve", lambda e: e.tensor_scalar(out=angk, in0=kidx[:], scalar1=thB[:, 0:1], scalar2=None, op0=ALU.mult),
             r=["kidx", tg + "th"], w=[tg + "kang"])
        sincos(SB, angk, kfk, tBi[:], hsk, snk, csk, tg + "k")
        SB.op("dve", lambda e: e.tensor_scalar(out=mgk, in0=kidx[:], scalar1=lamB[:, 0:1], scalar2=None, op0=ALU.mult),
             r=["kidx", tg + "lam"], w=[tg + "mgk"])
        SB.op("act", lambda e: e.activation(out=mgk, in_=mgk, func=AF.Exp), w=[tg + "mgk"])
        SB.op("dve", lambda e: e.tensor_tensor(out=csk, in0=csk, in1=mgk, op=ALU.mult), r=[tg + "mgk"], w=[tg + "kcos"])
        SB.op("dve", lambda e: e.tensor_tensor(out=snk, in0=snk, in1=mgk, op=ALU.mult), r=[tg + "mgk"], w=[tg + "ksin"])
        SB.op("dve", lambda e: e.tensor_scalar(out=nsk, in0=snk, scalar1=-1.0, scalar2=None, op0=ALU.mult), r=[tg + "ksin"], w=[tg + "nsk"])
        SB.op("dve", lambda e: e.tensor_scalar(out=kfk, in0=csk, scalar1=-1.0, scalar2=None, op0=ALU.mult), r=[tg + "kcos"], w=[tg + "kkf"])
        kb = lambda ap: ap.unsqueeze(2).broadcast_to([128, 32, 64])
        cb = lambda t: t[:].unsqueeze(1).broadcast_to([128, 32, 64])
        for ri, (f1, f2) in enumerate(((csk, nsk), (nsk, kfk))):
            SB.op("dve", lambda e, q=q, f1=f1: e.tensor_tensor(out=gt[0][:], in0=cb(crep[q]), in1=kb(f1), op=ALU.mult),
                 r=[("crep", q), tg + "kcos", tg + "nsk", tg + "kkf"], w=["gt0"])
            SB.op("dve", lambda e, q=q, f2=f2: e.tensor_tensor(out=gt[1][:], in0=cb(cimp[q]), in1=kb(f2), op=ALU.mult),
                 r=[("cimp", q), tg + "kcos", tg + "nsk", tg + "kkf"], w=["gt1"])
            SB.op("dve", lambda e, q=q, ri=ri: e.tensor_tensor(out=Gpad[q][ri][:], in0=gt[0][:], in1=gt[1][:], op=ALU.add),
                 r=["gt0", "gt1"], w=[("Gpad", q, ri)])
        SB.op("dve", lambda e: e.tensor_scalar(out=phi, in0=thB, scalar1=16.0, scalar2=None, op0=ALU.mult), r=[tg + "th"], w=[tg + "phi"])
        SB.op("dve", lambda e: e.tensor_scalar(out=th15, in0=phi, scalar1=1.0 / (2.0 * math.pi), scalar2=None, op0=ALU.mult),
             r=[tg + "phi"], w=[tg + "th15"])
        SB.op("dve", lambda e: e.tensor_copy(out=cBi[:], in_=th15), r=[tg + "th15"], w=[tg + "cBi"])
        SB.op("dve", lambda e: e.tensor_copy(out=th15, in_=cBi[:]), r=[tg + "cBi"], w=[tg + "th15"])
        SB.op("dve", lambda e: e.scalar_tensor_tensor(out=phi, in0=th15, scalar=-C1_2PI, in1=phi, op0=ALU.mult, op1=ALU.add),
             r=[tg + "th15"], w=[tg + "phi"])
        SB.op("dve", lambda e: e.scalar_tensor_tensor(out=phi, in0=th15, scalar=-C2_2PI, in1=phi, op0=ALU.mult, op1=ALU.add),
             r=[tg + "th15"], w=[tg + "phi"])
        SB.op("dve", lambda e: e.tensor_scalar(out=l2[0][:], in0=midx[:], scalar1=phi[:, 0:1], scalar2=None, op0=ALU.mult),
             r=["midx", tg + "phi"], w=["l2ang"])
        sincos(SB, l2[0][:], l2[1][:], l2i[:], l2[2][:], Tsn[q][:], Tc[q][:], "l2")
        SB.op("dve", lambda e, q=q: e.tensor_copy(out=Tsn[q][:], in_=Tsn[q][:]), r=["l2sin"], w=[("Ts", q)])
        SB.op("dve", lambda e, q=q: e.tensor_copy(out=Tc[q][:], in_=Tc[q][:]), r=["l2cos"], w=[("Tc", q)])
        SB.op("act", lambda e, q=q: e.activation(out=rho[q][:], in_=lamB, func=AF.Exp, scale=16.0), r=[tg + "lam"], w=[("rho", q)])
    suv = su[:].rearrange("p (m s) -> p s m", s=16)
    zi_ = 0
    for q in range(2):
        for ri in range(2):
            for s in range(16):
                pb = zi_ % 2
                zi_ += 1
                S.op("pe", lambda e, q=q, ri=ri, s=s, pb=pb: e.matmul(ps_z[pb][:, 0:NB], W1tab[q][ri][:, s, :], suv[:, s, :], start=True, stop=True),
                     r=["su", ("W1tab", q, ri, 0), ("W1tab", q, ri, 1)], w=[("ps_z", pb)])
                dst = W15[q][ri] if s == 15 else roll[s % 2]
                dn = ("W15", q, ri) if s == 15 else ("roll", s % 2)
                if s == 0:
                    S.op("dve", lambda e, pb=pb, dst=dst: e.tensor_copy(out=dst[:, 0:NB], in_=ps_z[pb][:, 0:NB]), w=[dn, ("ps_z", pb)])
                else:
                    S.op("dve", lambda e, pb=pb, dst=dst, s=s: e.tensor_tensor(out=dst[:, 0:NB], in0=ps_z[pb][:, 0:NB],
                                                                          in1=roll[(s - 1) % 2][:, 0:NB], op=ALU.add),
                         r=[("roll", (s - 1) % 2)], w=[dn, ("ps_z", pb)])
                S.op("act", lambda e, q=q, ri=ri, s=s, dst=dst: e.activation(out=W1bf[q][ri][:, s, 0:NB], in_=dst[:, 0:NB], func=AF.Copy),
                     r=[dn], w=[("W1bf", q, ri, s)])
                for _ in range(3):
                    if bq:
                        a, k = bq.pop(0)
                        S.op(*a, **k)
    while bq:
        a, k = bq.pop(0)
        S.op(*a, **k)
    for q in range(2):
        ur, ui, t0, t1, vr, vi = [t[:, 0:NB] for t in l2]
        tc, tsn = Tc[q][:, 0:NB], Tsn[q][:, 0:NB]
        wre, wim = W15[q][0][:, 0:NB], W15[q][1][:, 0:NB]
        cmul(S, "dve", ur, ui, tc, tsn, wre, wim, t0, t1, [("Tc", q), ("Ts", q), ("W15", q, 0), ("W15", q, 1), "l2v"], "l2u", conj_a=True)
        rb = rho[q][:, 0:1].broadcast_to([128, NB])
        S.op("dve", lambda e, rb=rb: e.tensor_tensor_scan(out=vr, data0=rb, data1=ur, initial=0.0, op0=ALU.mult, op1=ALU.add),
             r=["l2ure", ("rho", q)], w=["l2vr"])
        S.op("dve", lambda e, rb=rb: e.tensor_tensor_scan(out=vi, data0=rb, data1=ui, initial=0.0, op0=ALU.mult, op1=ALU.add),
             r=["l2uim", ("rho", q)], w=["l2vi"])
        cmul(S, "dve", ur, ui, tc, tsn, vr, vi, t0, t1, [("Tc", q), ("Ts", q), "l2vr", "l2vi"], "l2x")
        for ri, src in ((0, ur), (1, ui)):
            S.op("pool", lambda e, q=q, ri=ri: e.memset(Xbf[q][ri][:, 0:1], 0.0), w=[("Xbf", q, ri)])
            if NB > 1:
                S.op("act", lambda e, q=q, ri=ri, src=src: e.activation(out=Xbf[q][ri][:, 1:NB], in_=src[:, 0:NB - 1], func=AF.Copy),
                     r=["l2xre", "l2xim"], w=[("Xbf", q, ri)])
        S.op("dve", lambda e: e.tensor_copy(out=l2[0][:, 0:1], in_=l2[0][:, 0:1]), r=[("Xbf", q, 0), ("Xbf", q, 1)], w=["l2v", "l2ure", "l2uim"])
    outv = outsb[:].rearrange("p (m s) -> p s m", s=16)
    for s in range(16):
        pb = s % 2
        k = 0
        for q in range(2):
            for ri in range(2):
                S.op("pe", lambda e, q=q, ri=ri, s=s, pb=pb, k=k: e.matmul(ps_y[pb][0:64, 0:NB], Gpad[q][ri][:, s, :], W1bf[q][ri][:, s, 0:NB],
                                                                     start=(k == 0), stop=False),
                     r=[("Gpad", q, ri), ("W1bf", q, ri, s)], w=[("ps_y", pb)])
                k += 1
        for q in range(2):
            for ri in range(2):
                S.op("pe", lambda e, q=q, ri=ri, s=s, pb=pb, k=k: e.matmul(ps_y[pb][0:64, 0:NB], Gpad[q][ri][:, s + 16, :], Xbf[q][ri][:, 0:NB],
                                                                     start=False, stop=(k == 7)),
                     r=[("Gpad", q, ri), ("Xbf", q, ri)], w=[("ps_y", pb)])
                k += 1
        S.op("dve", lambda e, s=s, pb=pb: e.scalar_tensor_tensor(out=ytmp[pb][:, 0:NB], in0=suv[:, s, :], scalar=dcol[:, 0:1],
                                                            in1=ps_y[pb][0:64, 0:NB], op0=ALU.mult, op1=ALU.add),
             r=["su", "dcol"], w=[("ytmp", pb), ("ps_y", pb)])
        S.op("act", lambda e, s=s, pb=pb: e.activation(out=outv[:, s, :], in_=ytmp[pb][:, 0:NB], func=AF.Gelu),
             r=[("ytmp", pb)], w=[("outsb", s)])
    S.op("sp", lambda e: e.dma_start(out=yg[:, :], in_=outsb[:]), r=[("outsb", s) for s in range(16)], dma=True)


def build_s5(T=SEQ):
    nc = bass.Bass("TRN2", target_bir_lowering=False)
    D = lambda n, s, d=F32, k="ExternalInput": nc.dram_tensor(n, s, d, kind=k).ap()
    fm = D("fm", [NFM, T], BF16)
    pA = D("s5p_pA", [64, 257]); pB = D("s5p_pB", [2, 128, 3]); cre = D("s5p_cre", [2, 128, 64]); cim = D("s5p_cim", [2, 128, 64])
    dcol = D("s5p_d", [64, 1]); negsig = D("negsig", [1, 1024]); kidx = D("kidx", [1, 32]); midx = D("midx", [1, 512])
    rowmask = D("rowmask", [64, 4])
    yg = D("yg", [64, T], BF16, "ExternalOutput")
    with contextlib.ExitStack() as st:
        S = Sched(nc)
        phase_s5(nc, S, st, fm, pA, pB, cre, cim, dcol, negsig, kidx, midx, rowmask, yg, T)
        S.emit()
    return nc


def phase_out(nc, S, st, mixin, ssg_d, hT, wout_d, gluw_d, glub_d, fnw_d, hout, final, NTOK=TQ):
    TS = lambda n, s, d: st.enter_context(nc.sbuf_tensor(n, s, d))
    PS = lambda n: st.enter_context(nc.psum_tensor(n, [128, 512], F32))
    wst = [TS(f"po_wst{i}", [128, 1024], F32) for i in range(2)]
    wout = TS("po_wout", [128, 8, 1024], BF16)
    gst = TS("po_gst", [128, 2, 256], F32)
    gluw = TS("po_gluw", [128, 2, 256], BF16)
    glub = TS("po_glub", [128, 2], F32)
    fnw = TS("po_fnw", [128, 8], F32)
    ones = TS("po_ones", [128, 128], BF16)
    mix = [TS(f"po_mix{i}", [128, 8, 512], BF16) for i in range(2)]
    ssg = [TS(f"po_ssg{i}", [128, 2, 512], BF16) for i in range(2)]
    hin = [TS(f"po_hin{i}", [128, 8, 512], F32) for i in range(2)]
    sg = TS("po_sg", [128, 512], F32)
    osb = TS("po_osb", [128, 2, 512], BF16)
    hn = TS("po_hn", [128, 8, 512], F32)
    hsq = TS("po_hsq", [128, 8, 512], BF16)
    nsq = TS("po_nsq", [128, 512], F32)
    ps_g = PS("po_ps_g")
    ps_o = [PS(f"po_ps_o{i}") for i in range(3)]
    ps_n = PS("po_ps_n")

    S.op("pool", lambda e: e.memset(ones[:], 1.0), w=["ones"])
    S.op("sp", lambda e: e.dma_start(out=gst[:], in_=gluw_d.rearrange("(k p) o -> p k o", p=128)), w=["gst"], dma=True)
    S.op("sp", lambda e: e.dma_start(out=glub[:], in_=glub_d[:, :]), w=["glub"], dma=True)
    S.op("sp", lambda e: e.dma_start(out=fnw[:], in_=fnw_d[:, :]), w=["fnw"], dma=True)
    S.op("dve", lambda e: e.tensor_copy(out=gluw[:], in_=gst[:]), r=["gst"], w=["gluw"])
    for k in range(8):
        S.op("sp", lambda e, k=k: e.dma_start(out=wst[k % 2][:], in_=wout_d[k * 128:(k + 1) * 128, :]), w=[("wst", k % 2)], dma=True)
        S.op("pool" if k % 2 else "dve", lambda e, k=k: e.tensor_copy(out=wout[:, k, :], in_=wst[k % 2][:]), r=[("wst", k % 2)], w=[("wout", k)])
    wr = [("wout", k) for k in range(8)]
    mv = mixin.rearrange("(k p) t -> p k t", p=128)
    sv = ssg_d.rearrange("(k p) t -> p k t", p=128)
    hv = hT.rearrange("(k p) t -> p k t", p=128)
    ov = hout.rearrange("(k p) t -> p k t", p=128)
    oi = 0
    for ti in range(NTOK // 512):
        b = ti % 2
        ts = slice(ti * 512, (ti + 1) * 512)
        S.op("sp", lambda e, b=b, ts=ts: e.dma_start(out=mix[b][:], in_=mv[:, :, ts]), w=[("mix", b)], dma=True)
        S.op("sp", lambda e, b=b, ts=ts: e.dma_start(out=ssg[b][:], in_=sv[:, :, ts]), w=[("ssg", b)], dma=True)
        S.op("pool", lambda e, b=b, ts=ts: e.dma_start(out=hin[b][:], in_=hv[:, :, ts]), w=[("hin", b)], dma=True)
        for oc in range(2):
            for kc in range(2):
                S.op("pe", lambda e, b=b, oc=oc, kc=kc: e.matmul(ps_g[:], gluw[:, kc, oc * 128:(oc + 1) * 128], mix[b][:, 2 + kc, :],
                                                             start=(kc == 0), stop=(kc == 1)), r=[("mix", b), "gluw"], w=["ps_g"])
            S.op("act", lambda e, oc=oc: e.activation(out=sg[:], in_=ps_g[:], func=AF.Sigmoid, bias=glub[:, oc:oc + 1]),
                 r=["glub"], w=["sg", "ps_g"])
            S.op("dve", lambda e, b=b, oc=oc: e.tensor_tensor(out=sg[:], in0=sg[:], in1=mix[b][:, 2 + oc, :], op=ALU.mult),
                 r=[("mix", b)], w=["sg"])
            S.op("dve", lambda e, b=b, oc=oc: e.tensor_tensor(out=osb[:, oc, :], in0=sg[:], in1=ssg[b][:, oc, :], op=ALU.mult),
                 r=[("ssg", b), "sg"], w=[("osb", oc)])
        for dc in range(8):
            pb = oi % 3
            oi += 1
            for kc in range(8):
                rhs = (lambda b=b, kc=kc: osb[:, kc - 2, :]) if kc in (2, 3) else (lambda b=b, kc=kc: mix[b][:, kc, :])
                S.op("pe", lambda e, dc=dc, kc=kc, pb=pb, rhs=rhs: e.matmul(ps_o[pb][:], wout[:, kc, dc * 128:(dc + 1) * 128], rhs(),
                                                                      start=(kc == 0), stop=(kc == 7)),
                     r=wr + [("mix", b), ("osb", 0), ("osb", 1)], w=[("ps_o", pb)])
            S.op("dve", lambda e, b=b, dc=dc, pb=pb: e.tensor_tensor(out=hn[:, dc, :], in0=ps_o[pb][:], in1=hin[b][:, dc, :], op=ALU.add),
                 r=[("hin", b)], w=[("hn", dc), ("ps_o", pb)])
            if not final:
                S.op("sp", lambda e, dc=dc, ts=ts: e.dma_start(out=ov[:, dc, ts], in_=hn[:, dc, :]), r=[("hn", dc)], dma=True)
        if final:
            hr = [("hn", dc) for dc in range(8)]
            S.op("act", lambda e: e.activation(out=hsq[:], in_=hn[:], func=AF.Square), r=hr, w=["hsq"])
            for k in range(8):
                S.op("pe", lambda e, k=k: e.matmul(ps_n[:], ones[:], hsq[:, k, :], start=(k == 0), stop=(k == 7)), r=["hsq", "ones"], w=["ps_n"])
            S.op("act", lambda e: e.activation(out=nsq[:], in_=ps_n[:], func=AF.Sqrt, scale=1.0 / D_MODEL, bias=EPS), w=["nsq", "ps_n"])
            S.op("dve", lambda e: e.reciprocal(out=nsq[:], in_=nsq[:]), w=["nsq"])
            for dc in range(8):
                S.op("pool" if dc % 2 else "dve", lambda e, dc=dc: e.scalar_tensor_tensor(
                    out=hn[:, dc, :], in0=hn[:, dc, :], scalar=fnw[:, dc:dc + 1], in1=nsq[:], op0=ALU.mult, op1=ALU.mult) if dc % 2 == 0 else
                    e.tensor_tensor(out=hn[:, dc, :], in0=hn[:, dc, :], in1=nsq[:], op=ALU.mult),
                    r=["nsq", "fnw"], w=[("hn", dc)])
                if dc % 2:
                    S.op("pool", lambda e, dc=dc: e.tensor_scalar(out=hn[:, dc, :], in0=hn[:, dc, :], scalar1=fnw[:, dc:dc + 1], scalar2=None,
                                                                  op0=ALU.mult), r=["fnw"], w=[("hn", dc)])
                S.op("sp", lambda e, dc=dc, ts=ts: e.dma_start(out=ov[:, dc, ts], in_=hn[:, dc, :]), r=[("hn", dc)], dma=True)


def build_out(final, NTOK=TQ):
    nc = bass.Bass("TRN2", target_bir_lowering=False)
    D = lambda n, s, d=F32, k="ExternalInput": nc.dram_tensor(n, s, d, kind=k).ap()
    mixin = D("mixin", [1024, NTOK], BF16)
    ssg = D("ssg", [256, NTOK], BF16)
    hT = D("hT", [D_MODEL, NTOK])
    wout = D("wout", [1024, 1024]); gluw = D("gluw", [256, 256]); glub = D("glub", [128, 2]); fnw = D("fnw", [128, 8])
    hout = D("hout", [D_MODEL, NTOK], F32, "ExternalOutput")
    with contextlib.ExitStack() as st:
        S = Sched(nc)
        phase_out(nc, S, st, mixin, ssg, hT, wout, gluw, glub, fnw, hout, final, NTOK)
        S.emit()
    return nc


_CACHE = {}


def _prog(key, fn):
    if key not in _CACHE:
        _CACHE[key] = fn()
    return _CACHE[key]


def build_mixers(l, T=SEQ, which=("ip", "at", "hg", "s5")):
    lambda_init = 0.8 - 0.6 * math.exp(-0.3 * l)
    nc = bass.Bass("TRN2", target_bir_lowering=False)
    D = lambda n, s, d=F32, k="ExternalInput": nc.dram_tensor(n, s, d, kind=k).ap()
    hT = D("hT", [D_MODEL, T]); wcat = D("wcat", [D_MODEL, NFM + NTM]); nw = D("nw", [128, 8])
    lqk = D("lqk", [1, 256]); subln = D("subln", [128, 1]); ropef = D("ropef", [128, 1]); rmat = D("rmat", [128, 128])
    cmask = D("cmask", [4, 128, 512], BF16)
    lbl_bc = D("lbl_bc", [1, 128]); lbl_col = D("lbl_col", [64, 2]); gw = D("gw", [64, 1]); mcat = D("mcat", [128, 136]); mrev = D("mrev", [128, 128])
    pA = D("s5p_pA", [64, 257]); pB = D("s5p_pB", [2, 128, 3]); cre = D("s5p_cre", [2, 128, 64]); cim = D("s5p_cim", [2, 128, 64])
    dcol = D("s5p_d", [64, 1]); negsig = D("negsig", [1, 1024]); kidx = D("kidx", [1, 32]); midx = D("midx", [1, 512]); rowmask = D("rowmask", [64, 4])
    fm = D("fm", [NFM, T], BF16, "Internal")
    tm_sf = D("tm_sf", [T, 64], F32, "Internal")
    tm_v = D("tm_v", [T, 192], BF16, "Internal")
    mo = D("mo", [320, T], BF16, "ExternalOutput")
    if "ip" in which:
        with contextlib.ExitStack() as st:
            S = Sched(nc)
            phase_inproj(nc, S, st, hT, wcat, nw, fm, tm_sf, tm_v, T)
            S.emit()
    if "at" in which:
        with contextlib.ExitStack() as st:
            S = Sched(nc)
            S.op("sp", lambda e: e.dma_start(out=mo[128:192, :], in_=fm[192:256, :]), dma=True)
            phase_attn(nc, S, st, fm, tm_v, lqk, subln, ropef, rmat, cmask, mo[192:320, :], lambda_init, T)
            S.emit()
    if "hg" in which:
        with contextlib.ExitStack() as st:
            S = Sched(nc)
            phase_hgrn(nc, S, st, fm, tm_sf, tm_v, lbl_bc, lbl_col, gw, mcat, mrev, mo[0:64, :], float(l), T)
            S.emit()
    if "s5" in which:
        with contextlib.ExitStack() as st:
            S = Sched(nc)
            phase_s5(nc, S, st, fm, pA, pB, cre, cim, dcol, negsig, kidx, midx, rowmask, mo[64:128, :], T)
            S.emit()
    return nc


def mixer_inputs(inp, l, c, hT_b):
    f = np.float32
    j = c % 4
    ropef, rmat, cmask = attn_consts()
    mcat, mrev = hgrn_consts()
    negsig, kidx, midx, rowmask = s5_consts()
    lbl = np.asarray(inp["hgrn_lb_logits"], f)[:, 64 * j:64 * j + 64]
    d = {"hT": hT_b, "wcat": np.ascontiguousarray(np.asarray(inp["w_in"][l], f)[:, core_cols(j)]),
         "nw": np.ascontiguousarray(np.asarray(inp["norm_w"][l], f).reshape(8, 128).T),
         "lqk": np.concatenate([inp["diff_lq1"][l], inp["diff_lq2"][l], inp["diff_lk1"][l], inp["diff_lk2"][l]])[None, :].astype(f),
         "subln": np.asarray(inp["diff_subln_w"][l], f)[:, None], "ropef": ropef, "rmat": rmat, "cmask": cmask,
         "lbl_bc": np.ascontiguousarray(lbl.reshape(1, 128)), "lbl_col": np.ascontiguousarray(lbl.T),
         "gw": np.asarray(inp["hgrn_norm_w"][l], f)[:, None], "mcat": mcat, "mrev": mrev,
         "negsig": negsig, "kidx": kidx, "midx": midx, "rowmask": rowmask}
    d.update(s5_params(inp, l, j))
    return d


def kernel(**inp):
    f = np.float32
    x = np.asarray(inp["x"], f)
    cores = list(range(NCORES))
    hT = [np.ascontiguousarray(x[b].T) for b in range(BATCH)]
    for l in range(DEPTH):
        nc = _prog(("mix", l), lambda: build_mixers(l))
        ims = [mixer_inputs(inp, l, c, hT[c // 4]) for c in cores]
        rm = run_bass_kernel_spmd(nc, ims, core_ids=cores).results
        final = (l == DEPTH - 1)
        nc = _prog(("out", final), lambda: build_out(final))
        ims = []
        for c in cores:
            b, tq = c // 4, c % 4
            ts = slice(tq * TQ, (tq + 1) * TQ)
            src = [4 * b + j for j in range(4)]
            mixin = np.concatenate([rm[s]["mo"][0:64, ts] for s in src] + [rm[s]["mo"][64:128, ts] for s in src]
                                   + [rm[s]["mo"][192:320, ts] for s in src])
            ssg = np.concatenate([rm[s]["mo"][128:192, ts] for s in src])
            ims.append({"mixin": np.ascontiguousarray(mixin), "ssg": np.ascontiguousarray(ssg), "hT": np.ascontiguousarray(hT[b][:, ts]),
                        "wout": np.asarray(inp["w_out"][l], f), "gluw": np.asarray(inp["s5_glu_w"][l], f),
                        "glub": np.ascontiguousarray(np.asarray(inp["s5_glu_b"][l], f).reshape(2, 128).T),
                        "fnw": np.ascontiguousarray(np.asarray(inp["final_norm_w"], f).reshape(8, 128).T)})
        ro = run_bass_kernel_spmd(nc, ims, core_ids=cores).results
        hT = [np.concatenate([ro[4 * b + tq]["hout"] for tq in range(4)], axis=1) for b in range(BATCH)]
    out = np.stack([hT[b].T for b in range(BATCH)]).astype(f)
    return np.ascontiguousarray(out)
```
